# Optimizing a Trainium2 kernel written in Bass

```python
import math
import jax, jax.numpy as jnp
from jax import lax
import numpy as np

D_MODEL = 1024
BATCH = 8
SEQ = 2048
DEPTH = 1

NORM_EPS = 1e-6
D_FF = 2816
RWKV_HEADS = 8
RWKV_HEAD_DIM = 64
RWKV_DIM = RWKV_HEADS * RWKV_HEAD_DIM
DECAY_LORA = 64
AAA_LORA = 64
GATE_LORA = 128
RWKV_GN_EPS = 64e-5
RWKV_PROJ = 3 * RWKV_DIM + DECAY_LORA + AAA_LORA + GATE_LORA
NSA_HEADS = 8
NSA_KV_GROUPS = 2
NSA_HEAD_DIM = 64
NSA_DIM = NSA_HEADS * NSA_HEAD_DIM
NSA_KV_DIM = NSA_KV_GROUPS * NSA_HEAD_DIM
CMP_LEN = 32
CMP_STRIDE = 16
CMP_HIDDEN = 256
SEL_BLOCK = 64
SEL_TOP_N = 16
WINDOW = 512
Q_BLOCK = 128
SEL_Q_CHUNK = 64
REL_BUCKETS = 32
REL_MAX_DIST = 128
MEM_TOKENS = 256
MEM_HEADS = 4
MEM_HEAD_DIM = 128
MEM_DIM = MEM_HEADS * MEM_HEAD_DIM
N_BRANCH = 3
IN_SPLITS = (RWKV_PROJ, NSA_DIM, 6 * NSA_KV_DIM, 3 * NSA_HEADS, MEM_DIM, N_BRANCH * D_MODEL)
IN_DIM = RWKV_PROJ + NSA_DIM + 6 * NSA_KV_DIM + 3 * NSA_HEADS + MEM_DIM + N_BRANCH * D_MODEL

kernel_name = 'hybrid_rwkv7_nsa_memxattn_macaron'


def _split(a, sizes):
    offs = np.cumsum(sizes)[:-1].tolist()
    return jnp.split(a, offs, axis=-1)


def _rmsnorm(x, g):
    xf = x.astype(jnp.float32)
    y = xf * lax.rsqrt(jnp.mean(xf * xf, axis=-1, keepdims=True) + NORM_EPS)
    return (y * g.astype(jnp.float32)).astype(x.dtype)


def _swiglu(h, w_gate, w_up, w_down):
    return (jax.nn.silu(h @ w_gate) * (h @ w_up)) @ w_down


def _token_shift(p):
    return jnp.pad(p[:, :-1], ((0, 0), (1, 0), (0, 0)))


def _masked_softmax(s, mask):
    s = jnp.where(mask, s.astype(jnp.float32), -1e30)
    p = jax.nn.softmax(s, axis=-1)
    return jnp.where(mask, p, 0.0)


def _t5_bucket(dist):
    n = jnp.maximum(dist, 0)
    exact = REL_BUCKETS // 2
    nf = jnp.maximum(n, 1).astype(jnp.float32)
    large = exact + (jnp.log(nf / exact) / math.log(REL_MAX_DIST / exact)
                     * (REL_BUCKETS - exact)).astype(jnp.int32)
    large = jnp.minimum(large, REL_BUCKETS - 1)
    return jnp.where(n < exact, n, large)


def _rel_bias(dist, tab):
    b = tab[_t5_bucket(dist)]
    return jnp.moveaxis(b, (-2, -1), (0, 1))


def _rwkv7_scan(r, decay, k, v, kk, a):
    B, S, H, N = r.shape

    def step(state, inp):
        r_t, w_t, k_t, v_t, kk_t, a_t = inp
        sa = jnp.einsum('bhvk,bhk->bhv', state, -kk_t)
        state = (state * w_t[:, :, None, :]
                 + sa[..., None] * (kk_t * a_t)[:, :, None, :]
                 + v_t[..., None] * k_t[:, :, None, :])
        y = jnp.einsum('bhvk,bhk->bhv', state, r_t)
        return state, y

    xs = tuple(jnp.moveaxis(t.astype(jnp.float32), 1, 0) for t in (r, decay, k, v, kk, a))
    state0 = jnp.zeros((B, H, N, N), jnp.float32)
    _, ys = lax.scan(step, state0, xs)
    return jnp.moveaxis(ys, 0, 1)


def _rwkv7_branch(p, mu, w0, w2, a0, a2, g2, k_k, k_a, r_k, gn_gain, gn_bias):
    B, S, _ = p.shape
    H, N = RWKV_HEADS, RWKV_HEAD_DIM
    p = p + (_token_shift(p) - p) * mu
    r, k, v, xw, xa, xg = _split(p, (RWKV_DIM, RWKV_DIM, RWKV_DIM, DECAY_LORA, AAA_LORA, GATE_LORA))
    w = -jax.nn.softplus(-(w0 + jnp.tanh(xw) @ w2)) - 0.5
    decay = jnp.exp(-jnp.exp(w.astype(jnp.float32)))
    a = jax.nn.sigmoid(a0 + xa @ a2)
    g = jax.nn.sigmoid(xg) @ g2
    heads = lambda t: t.reshape(B, S, H, N)
    kk = heads(k * k_k).astype(jnp.float32)
    kk = kk / jnp.maximum(jnp.sqrt(jnp.sum(kk * kk, axis=-1, keepdims=True)), 1e-12)
    k = k * (1.0 + (a - 1.0) * k_a)
    r, k, v, a, decay = heads(r), heads(k), heads(v), heads(a), heads(decay)
    y = _rwkv7_scan(r, decay, k, v, kk, a)
    mean = jnp.mean(y, axis=-1, keepdims=True)
    var = jnp.mean(jnp.square(y - mean), axis=-1, keepdims=True)
    y = ((y - mean) * lax.rsqrt(var + RWKV_GN_EPS)).reshape(B, S, RWKV_DIM)
    y = (y * gn_gain + gn_bias).astype(p.dtype)
    bonus = jnp.sum(r * k * r_k, axis=-1, keepdims=True) * v
    y = y + bonus.reshape(B, S, RWKV_DIM)
    return y * g


def _nsa_branch(q, kv, gate_logits, pe_k, cmp_k_w1, cmp_k_w2, pe_v, cmp_v_w1, cmp_v_w2, rel_bias):
    B, S, _ = q.shape
    G, HPG, DH = NSA_KV_GROUPS, NSA_HEADS // NSA_KV_GROUPS, NSA_HEAD_DIM
    qg = q.reshape(B, S, G, HPG, DH).transpose(0, 2, 3, 1, 4) * (DH ** -0.5)
    k_cmp, v_cmp, k_sel, v_sel, k_win, v_win = [
        t.reshape(B, S, G, DH).transpose(0, 2, 1, 3) for t in _split(kv, (NSA_KV_DIM,) * 6)]
    t_pos = jnp.arange(S)
    bias_tab = rel_bias.reshape(REL_BUCKETS, G, HPG)

    n_cmp = (S - CMP_LEN) // CMP_STRIDE + 1
    blk_idx = jnp.arange(n_cmp)[:, None] * CMP_STRIDE + jnp.arange(CMP_LEN)[None, :]

    def compress(t, pe, w1, w2):
        blocks = t[:, :, blk_idx] + pe
        return jax.nn.gelu(blocks.reshape(B, G, n_cmp, CMP_LEN * DH) @ w1) @ w2

    kc = compress(k_cmp, pe_k, cmp_k_w1, cmp_k_w2)
    vc = compress(v_cmp, pe_v, cmp_v_w1, cmp_v_w2)
    cmp_end = jnp.arange(n_cmp) * CMP_STRIDE + CMP_LEN - 1
    dist_c = t_pos[:, None] - cmp_end[None, :]
    s_c = jnp.einsum('bghsd,bgnd->bghsn', qg, kc) + _rel_bias(dist_c, bias_tab)
    p_c = _masked_softmax(s_c, dist_c >= 0)
    o_cmp = jnp.einsum('bghsn,bgnd->bghsd', p_c.astype(vc.dtype), vc)

    n_blk = S // SEL_BLOCK
    n_sel = min(SEL_TOP_N, n_blk)
    jb = jnp.arange(n_blk)
    ic = jnp.arange(n_cmp)
    overlap = ((ic[:, None] * CMP_STRIDE <= jb[None, :] * SEL_BLOCK + SEL_BLOCK - 1)
               & (ic[:, None] * CMP_STRIDE + CMP_LEN - 1 >= jb[None, :] * SEL_BLOCK)).astype(jnp.float32)
    imp = jnp.einsum('bghsn,nj->bgsj', p_c, overlap)
    cur = t_pos[:, None] // SEL_BLOCK
    forced = (jb[None, :] == 0) | (jb[None, :] == cur) | (jb[None, :] == cur - 1)
    imp = jnp.where(jb[None, :] > cur, -1e6, jnp.where(forced, 1e6, imp))
    _, sel_idx = lax.top_k(imp, n_sel)

    kb = k_sel.reshape(B, G, n_blk, SEL_BLOCK, DH)
    vb = v_sel.reshape(B, G, n_blk, SEL_BLOCK, DH)
    gather = jax.vmap(jax.vmap(lambda blocks, idx: blocks[idx]))
    n_keys = n_sel * SEL_BLOCK
    tab_lookup = jax.vmap(lambda bk, tb: tb[bk], in_axes=(1, 1), out_axes=1)

    def sel_chunk(c):
        t0 = c * SEL_Q_CHUNK
        q_c = lax.dynamic_slice_in_dim(qg, t0, SEL_Q_CHUNK, axis=3)
        idx_c = lax.dynamic_slice_in_dim(sel_idx, t0, SEL_Q_CHUNK, axis=2)
        k_c = gather(kb, idx_c).reshape(B, G, SEL_Q_CHUNK, n_keys, DH)
        v_c = gather(vb, idx_c).reshape(B, G, SEL_Q_CHUNK, n_keys, DH)
        key_pos = (idx_c[..., None] * SEL_BLOCK + jnp.arange(SEL_BLOCK)).reshape(B, G, SEL_Q_CHUNK, n_keys)
        dist = (t0 + jnp.arange(SEL_Q_CHUNK))[:, None] - key_pos
        bias = jnp.moveaxis(tab_lookup(_t5_bucket(dist), bias_tab), -1, 2)
        s = jnp.einsum('bghqd,bgqkd->bghqk', q_c, k_c) + bias
        p = _masked_softmax(s, (dist >= 0)[:, :, None])
        return jnp.einsum('bghqk,bgqkd->bghqd', p.astype(v_c.dtype), v_c)

    o_sel = lax.map(sel_chunk, jnp.arange(S // SEL_Q_CHUNK))
    o_sel = jnp.moveaxis(o_sel, 0, 3).reshape(B, G, HPG, S, DH)

    nqb = S // Q_BLOCK
    nwb = WINDOW // Q_BLOCK

    def band(t):
        tp = jnp.pad(t, ((0, 0), (0, 0), (WINDOW, 0), (0, 0))).reshape(B, G, nqb + nwb, Q_BLOCK, DH)
        return jnp.concatenate([tp[:, :, o:o + nqb] for o in range(nwb + 1)], axis=3)

    kw, vw = band(k_win), band(v_win)
    kj = jnp.arange((nwb + 1) * Q_BLOCK)[None, :]
    dist_w = WINDOW + jnp.arange(Q_BLOCK)[:, None] - kj
    key_pos_w = jnp.arange(nqb)[:, None, None] * Q_BLOCK - WINDOW + kj[None]
    mask_w = (dist_w >= 0) & (dist_w < WINDOW) & (key_pos_w >= 0)
    bias_w = _rel_bias(dist_w, bias_tab)[:, :, None]
    qw = qg.reshape(B, G, HPG, nqb, Q_BLOCK, DH)
    s_w = jnp.einsum('bghnqd,bgnkd->bghnqk', qw, kw) + bias_w
    p_w = _masked_softmax(s_w, mask_w)
    o_win = jnp.einsum('bghnqk,bgnkd->bghnqd', p_w.astype(vw.dtype), vw).reshape(B, G, HPG, S, DH)

    gates = jax.nn.sigmoid(gate_logits).reshape(B, S, 3, G, HPG).transpose(2, 0, 3, 4, 1)[..., None]
    o = gates[0] * o_cmp + gates[1] * o_sel + gates[2] * o_win
    return o.transpose(0, 3, 1, 2, 4).reshape(B, S, NSA_DIM)


def _memory_branch(q, mem_n, w_k, w_v):
    B, S, _ = q.shape
    M = mem_n.shape[1]
    qh = q.reshape(B, S, MEM_HEADS, MEM_HEAD_DIM) * (MEM_HEAD_DIM ** -0.5)
    kh = (mem_n @ w_k).reshape(B, M, MEM_HEADS, MEM_HEAD_DIM)
    vh = (mem_n @ w_v).reshape(B, M, MEM_HEADS, MEM_HEAD_DIM)
    s = jnp.einsum('bshd,bmhd->bhsm', qh, kh)
    p = jax.nn.softmax(s.astype(jnp.float32), axis=-1).astype(vh.dtype)
    return jnp.einsum('bhsm,bmhd->bshd', p, vh).reshape(B, S, MEM_DIM)


def setup_inputs(seed: int = 0) -> dict:
    key = jax.random.key(seed)
    ks = iter(jax.random.split(key, 40))
    L, D = DEPTH, D_MODEL

    def nrm(shape, scale):
        return scale * jax.random.normal(next(ks), shape, jnp.float32)

    def gain(shape):
        return 1.0 + nrm(shape, 0.02)

    return {
        'x': nrm((BATCH, SEQ, D), 1.0),
        'mem': nrm((BATCH, MEM_TOKENS, D), 1.0),
        'ffn1_norm': gain((L, D)),
        'ffn1_w_gate': nrm((L, D, D_FF), D ** -0.5),
        'ffn1_w_up': nrm((L, D, D_FF), D ** -0.5),
        'ffn1_w_down': nrm((L, D_FF, D), D_FF ** -0.5),
        'mix_norm': gain((L, D)),
        'w_in': nrm((L, D, IN_DIM), D ** -0.5),
        'rwkv_mu': jax.random.uniform(next(ks), (L, RWKV_PROJ), jnp.float32),
        'rwkv_w0': jax.random.uniform(next(ks), (L, RWKV_DIM), jnp.float32, -6.0, -1.0),
        'rwkv_w2': nrm((L, DECAY_LORA, RWKV_DIM), DECAY_LORA ** -0.5),
        'rwkv_a0': nrm((L, RWKV_DIM), 0.1),
        'rwkv_a2': nrm((L, AAA_LORA, RWKV_DIM), AAA_LORA ** -0.5),
        'rwkv_g2': nrm((L, GATE_LORA, RWKV_DIM), GATE_LORA ** -0.5),
        'rwkv_k_k': 0.85 + nrm((L, RWKV_DIM), 0.05),
        'rwkv_k_a': 1.0 + nrm((L, RWKV_DIM), 0.05),
        'rwkv_r_k': nrm((L, RWKV_HEADS, RWKV_HEAD_DIM), 0.1),
        'rwkv_gn_gain': gain((L, RWKV_DIM)),
        'rwkv_gn_bias': nrm((L, RWKV_DIM), 0.02),
        'cmp_pe_k': nrm((L, CMP_LEN, NSA_HEAD_DIM), 0.02),
        'cmp_k_w1': nrm((L, CMP_LEN * NSA_HEAD_DIM, CMP_HIDDEN), (CMP_LEN * NSA_HEAD_DIM) ** -0.5),
        'cmp_k_w2': nrm((L, CMP_HIDDEN, NSA_HEAD_DIM), CMP_HIDDEN ** -0.5),
        'cmp_pe_v': nrm((L, CMP_LEN, NSA_HEAD_DIM), 0.02),
        'cmp_v_w1': nrm((L, CMP_LEN * NSA_HEAD_DIM, CMP_HIDDEN), (CMP_LEN * NSA_HEAD_DIM) ** -0.5),
        'cmp_v_w2': nrm((L, CMP_HIDDEN, NSA_HEAD_DIM), CMP_HIDDEN ** -0.5),
        'rel_bias': nrm((REL_BUCKETS, NSA_HEADS), 0.1),
        'mem_norm': gain((L, D)),
        'mem_w_k': nrm((L, D, MEM_DIM), D ** -0.5),
        'mem_w_v': nrm((L, D, MEM_DIM), D ** -0.5),
        'w_br_rwkv': nrm((L, RWKV_DIM, D), RWKV_DIM ** -0.5),
        'w_br_nsa': nrm((L, NSA_DIM, D), NSA_DIM ** -0.5),
        'w_br_mem': nrm((L, MEM_DIM, D), MEM_DIM ** -0.5),
        'w_out': nrm((L, D, D), D ** -0.5),
        'ffn2_norm': gain((L, D)),
        'ffn2_w_gate': nrm((L, D, D_FF), D ** -0.5),
        'ffn2_w_up': nrm((L, D, D_FF), D ** -0.5),
        'ffn2_w_down': nrm((L, D_FF, D), D_FF ** -0.5),
        'final_norm': gain((D,)),
    }


def reference(x, mem, ffn1_norm, ffn1_w_gate, ffn1_w_up, ffn1_w_down, mix_norm, w_in,
              rwkv_mu, rwkv_w0, rwkv_w2, rwkv_a0, rwkv_a2, rwkv_g2, rwkv_k_k, rwkv_k_a, rwkv_r_k,
              rwkv_gn_gain, rwkv_gn_bias, cmp_pe_k, cmp_k_w1, cmp_k_w2, cmp_pe_v, cmp_v_w1, cmp_v_w2,
              rel_bias, mem_norm, mem_w_k, mem_w_v, w_br_rwkv, w_br_nsa, w_br_mem, w_out,
              ffn2_norm, ffn2_w_gate, ffn2_w_up, ffn2_w_down, final_norm):
    B, S, _ = x.shape
    for l in range(DEPTH):
        x = x + 0.5 * _swiglu(_rmsnorm(x, ffn1_norm[l]), ffn1_w_gate[l], ffn1_w_up[l], ffn1_w_down[l])
        h = _rmsnorm(x, mix_norm[l])
        p_rwkv, q_nsa, kv_nsa, g_nsa, q_mem, g_branch = _split(h @ w_in[l], IN_SPLITS)
        y_rwkv = _rwkv7_branch(p_rwkv, rwkv_mu[l], rwkv_w0[l], rwkv_w2[l], rwkv_a0[l], rwkv_a2[l],
                               rwkv_g2[l], rwkv_k_k[l], rwkv_k_a[l], rwkv_r_k[l],
                               rwkv_gn_gain[l], rwkv_gn_bias[l])
        y_nsa = _nsa_branch(q_nsa, kv_nsa, g_nsa, cmp_pe_k[l], cmp_k_w1[l], cmp_k_w2[l],
                            cmp_pe_v[l], cmp_v_w1[l], cmp_v_w2[l], rel_bias)
        y_mem = _memory_branch(q_mem, _rmsnorm(mem, mem_norm[l]), mem_w_k[l], mem_w_v[l])
        gb = jax.nn.sigmoid(g_branch).reshape(B, S, N_BRANCH, D_MODEL)
        merged = (gb[:, :, 0] * (y_rwkv @ w_br_rwkv[l])
                  + gb[:, :, 1] * (y_nsa @ w_br_nsa[l])
                  + gb[:, :, 2] * (y_mem @ w_br_mem[l]))
        x = x + merged @ w_out[l]
        x = x + 0.5 * _swiglu(_rmsnorm(x, ffn2_norm[l]), ffn2_w_gate[l], ffn2_w_up[l], ffn2_w_down[l])
    return _rmsnorm(x, final_norm)
```

```python
import math
from contextlib import ExitStack
import numpy as np
import concourse.bass as bass
import concourse.mybir as mybir
from concourse.bass_utils import run_bass_kernel_spmd

F32 = mybir.dt.float32
BF16 = mybir.dt.bfloat16
AF = mybir.ActivationFunctionType
ALU = mybir.AluOpType
AX = mybir.AxisListType

D = 1024
S_LEN = 2048
DFF = 2816
NCH = 8
TC = 512
NTC = S_LEN // TC
EPS = 1e-6


class Buf:
    __slots__ = ("name", "last_w", "readers")

    def __init__(self, name=""):
        self.name = name
        self.last_w = None
        self.readers = []


class Sched:
    ENG = ("pe", "act", "dve", "pool", "sp")

    def __init__(self, nc, stack, n_dma_sems=16):
        self.nc = nc
        self.eng = {"pe": nc.tensor, "act": nc.scalar, "dve": nc.vector,
                    "pool": nc.gpsimd, "sp": nc.sync}
        self.sem = {}
        for e in ("pe", "act", "dve", "pool"):
            self.sem[e] = stack.enter_context(nc.semaphore("s_" + e))
        self.cnt = {e: 0 for e in ("pe", "act", "dve", "pool")}
        nq = {"sp": 28, "pool": 28, "act": 8}
        self.dsem = []
        self.qsems = {}
        for q, n in nq.items():
            self.qsems[q] = list(range(len(self.dsem), len(self.dsem) + n))
            for i in range(n):
                self.dsem.append(stack.enter_context(nc.semaphore("d%s%d" % (q, i))))
        self.dcnt = [0] * len(self.dsem)
        self.dnext = {q: 0 for q in nq}
        self.waited = {e: {} for e in self.ENG}
        self.n_ops = 0
        self.n_waits = 0

    def _semobj(self, key):
        return self.sem[key] if isinstance(key, str) else self.dsem[key]

    def _need(self, engine, toks):
        best = {}
        for t in toks:
            if t is None:
                continue
            key, val = t
            if best.get(key, 0) < val:
                best[key] = val
        w = self.waited[engine]
        for key, val in best.items():
            if w.get(key, 0) >= val:
                continue
            self.eng[engine].wait_ge(self._semobj(key), val)
            w[key] = val
            self.n_waits += 1

    @staticmethod
    def _deps(reads, writes):
        toks = []
        for b in reads:
            toks.append(b.last_w)
        for b in writes:
            toks.append(b.last_w)
            toks.extend(b.readers)
        return toks

    @staticmethod
    def _commit(tok, reads, writes):
        for b in reads:
            b.readers.append(tok)
            if len(b.readers) > 48:
                best = {}
                for k, v in b.readers:
                    if best.get(k, 0) < v:
                        best[k] = v
                b.readers = list(best.items())
        for b in writes:
            b.last_w = tok
            b.readers = []

    def op(self, engine, fn, reads=(), writes=()):
        self._need(engine, self._deps(reads, writes))
        ins = fn(self.eng[engine])
        self.cnt[engine] += 1
        ins.then_inc(self.sem[engine], 1)
        tok = (engine, self.cnt[engine])
        self._commit(tok, reads, writes)
        self.n_ops += 1
        return tok

    def dma(self, out_ap, in_ap, reads=(), writes=(), queue="sp", **kw):
        pool = self.qsems[queue]
        i = pool[self.dnext[queue]]
        self.dnext[queue] = (self.dnext[queue] + 1) % len(pool)
        prev = [(i, self.dcnt[i])] if self.dcnt[i] else []
        self._need(queue, self._deps(reads, writes) + prev)
        ins = self.eng[queue].dma_start(out=out_ap, in_=in_ap, **kw)
        self.dcnt[i] += 16
        ins.then_inc(self.dsem[i], 16)
        tok = (i, self.dcnt[i])
        self._commit(tok, reads, writes)
        self.n_ops += 1
        return tok

    def barrier(self, bufs):
        toks = []
        for b in bufs:
            toks.append(b.last_w)
            toks.extend(b.readers)
        for e in self.ENG:
            self._need(e, toks)

    def full_barrier(self):
        toks = [(e, self.cnt[e]) for e in ("pe", "act", "dve", "pool") if self.cnt[e]]
        toks += [(i, self.dcnt[i]) for i in range(len(self.dsem)) if self.dcnt[i]]
        for e in self.ENG:
            self._need(e, toks)

    def wait_all_dma(self, engine="sp"):
        for i in range(len(self.dsem)):
            if self.dcnt[i]:
                self.eng[engine].wait_ge(self.dsem[i], self.dcnt[i])


class Ring:
    def __init__(self, tiles, bufs=None):
        self.tiles = tiles
        self.bufs = bufs if bufs is not None else [Buf() for _ in tiles]
        self.i = 0

    def get(self):
        t, b = self.tiles[self.i], self.bufs[self.i]
        self.i = (self.i + 1) % len(self.tiles)
        return t, b


COLS = {}
_c = 0
for _n, _k in (("ffn1_norm", 8), ("mix_norm", 8), ("ffn2_norm", 8), ("final_norm", 8),
               ("mem_norm", 8), ("mu", 14), ("w0", 4), ("a0", 4), ("k_k", 4), ("k_a", 4), ("r_k", 4)):
    COLS[_n] = (_c, _k)
    _c += _k
NCOLS = _c


def _colpack(v):
    v = np.asarray(v, np.float32).reshape(-1, 128)
    return np.ascontiguousarray(v.T)


class Builder:
    def __init__(self, debug=()):
        self.debug = set(debug)
        self.nc = bass.Bass("TRN2", target_bir_lowering=False)
        self.dram_in = {}
        self.dram_out = {}

    def din(self, name, shape, dt=F32):
        t = self.nc.dram_tensor(name, list(shape), dt, kind="ExternalInput").ap()
        self.dram_in[name] = t
        return t

    def dout(self, name, shape, dt=F32):
        t = self.nc.dram_tensor(name, list(shape), dt, kind="ExternalOutput").ap()
        self.dram_out[name] = t
        return t

    def sb(self, name, shape, dt):
        self._uid = getattr(self, "_uid", 0) + 1
        return self.st.enter_context(self.nc.sbuf_tensor("sb%d_%s" % (self._uid, name), list(shape), dt))

    def ps(self, name, shape, dt=F32):
        return self.st.enter_context(self.nc.psum_tensor("ps_" + name, list(shape), dt))

    def rmsnorm_to_hn(self, gname):
        S = self.S
        g0, _ = COLS[gname]
        for tc in range(NTC):
            ts = slice(tc * TC, (tc + 1) * TC)
            pt, pb = self.psum.get()
            for c in range(NCH):
                sq, sqb = self.sq_ring.get()
                S.op("act", lambda e: e.activation(sq[:], self.X[:, c, ts], AF.Square),
                     reads=[self.XB[c][tc]], writes=[sqb])
                S.op("pe", lambda e: e.matmul(pt[:], self.ones_f[:], sq[:], start=(c == 0), stop=(c == NCH - 1)),
                     reads=[sqb, self.constb], writes=[pb])
            rs, rsb = self.rstd_ring.get()
            S.op("act", lambda e: e.activation(rs[:], pt[:], AF.Sqrt, bias=self.eps_t[:], scale=1.0 / D),
                 reads=[pb, self.constb], writes=[rsb])
            S.op("dve", lambda e: e.reciprocal(rs[:], rs[:]), reads=[rsb], writes=[rsb])
            for c in range(NCH):
                S.op("dve", lambda e: e.scalar_tensor_tensor(
                    self.HN[:, c, ts], self.X[:, c, ts], self.cols[:, g0 + c:g0 + c + 1], rs[:],
                    ALU.mult, ALU.mult),
                    reads=[self.XB[c][tc], rsb, self.constb], writes=[self.HNB[c][tc]])

    def ffn(self, wg, wu, wd, gname):
        S = self.S
        self.rmsnorm_to_hn(gname)
        groups = [(i, min(4, 22 - i)) for i in range(0, 22, 4)]

        def load(gi):
            f0, nf = groups[gi]
            slot = gi % 2
            S.dma(self.WG[slot][:, :, 0:nf * 128],
                  wg[:, f0 * 128:(f0 + nf) * 128].rearrange("(k p) n -> p k n", p=128),
                  writes=[self.WGB[slot]], queue="pool")
            S.dma(self.WU[slot][:, :, 0:nf * 128],
                  wu[:, f0 * 128:(f0 + nf) * 128].rearrange("(k p) n -> p k n", p=128),
                  writes=[self.WUB[slot]], queue="pool")
            S.dma(self.WD[slot][:, 0:nf, :],
                  wd[f0 * 128:(f0 + nf) * 128, :].rearrange("(f p) n -> p f n", p=128),
                  writes=[self.WDB[slot]], queue="pool")

        load(0)
        for gi, (f0, nf) in enumerate(groups):
            if gi + 1 < len(groups):
                load(gi + 1)
            slot = gi % 2
            WG, WU, WD = self.WG[slot], self.WU[slot], self.WD[slot]
            for tc in range(NTC):
                ts = slice(tc * TC, (tc + 1) * TC)
                hreads = [self.HNB[c][tc] for c in range(NCH)]
                a_t, a_b = self.a_ring.get()
                for f in range(nf):
                    pg, pgb = self.psum.get()
                    pu, pub = self.psum.get()

                    def mm_g(e):
                        for k in range(NCH):
                            ins = e.matmul(pg[:], WG[:, k, f * 128:(f + 1) * 128], self.HN[:, k, ts],
                                           start=(k == 0), stop=(k == NCH - 1))
                        return ins

                    def mm_u(e):
                        for k in range(NCH):
                            ins = e.matmul(pu[:], WU[:, k, f * 128:(f + 1) * 128], self.HN[:, k, ts],
                                           start=(k == 0), stop=(k == NCH - 1))
                        return ins
                    S.op("pe", mm_g, reads=hreads + [self.WGB[slot]], writes=[pgb])
                    S.op("pe", mm_u, reads=hreads + [self.WUB[slot]], writes=[pub])
                    sg, sgb = self.sg_ring.get()
                    S.op("act", lambda e: e.activation(sg[:], pg[:], AF.Silu), reads=[pgb], writes=[sgb])
                    S.op("dve", lambda e: e.tensor_tensor(a_t[:, f, :], sg[:], pu[:], ALU.mult),
                         reads=[sgb, pub], writes=[a_b])
                for dc in range(NCH):
                    po, pob = self.psum.get()

                    def mm_d(e):
                        for f in range(nf):
                            ins = e.matmul(po[:], WD[:, f, dc * 128:(dc + 1) * 128], a_t[:, f, :],
                                           start=(f == 0), stop=(f == nf - 1))
                        return ins
                    S.op("pe", mm_d, reads=[a_b, self.WDB[slot]], writes=[pob])
                    S.op("dve", lambda e: e.scalar_tensor_tensor(
                        self.X[:, dc, ts], po[:], 0.5, self.X[:, dc, ts], ALU.mult, ALU.add),
                        reads=[pob, self.XB[dc][tc]], writes=[self.XB[dc][tc]])


    def load_w(self, tile_ap, dram_ap, buf):
        self.S.dma(tile_ap, dram_ap.rearrange("(k p) n -> p k n", p=128), writes=[buf], queue="pool")

    def dump_feat(self, name, tile, nchunks, buf_list):
        o = self.dout(name, [nchunks * 128, S_LEN])
        for c in range(nchunks):
            self.S.dma(o[c * 128:(c + 1) * 128, :], tile[:, c, :], reads=buf_list, queue="pool")


    def rwkv_branch(self, w_rwkv, w2_d, a2_d, g2_d, gng_d, gnb_d):
        S = self.S
        CN = COLS
        NT = S_LEN // 128
        with ExitStack() as st4:
            old, self.st = self.st, st4
            WR = self.sb("WR", [128, NCH, 1792], BF16); WRB = Buf()
            W2 = self.sb("W2A2", [128, 512], F32); A2 = W2; G2 = self.sb("G2", [128, 512], F32)
            GNG = self.sb("GNG", [128, 512], F32); GNB = self.sb("GNB", [128, 512], F32)
            BO = self.sb("BO", [128, 128], F32); BOb = self.sb("BOb", [128, 128], BF16)
            ID2 = self.sb("ID2", [128, 64], BF16)
            OMK = self.sb("OMK", [128, 4], F32)
            cb = Buf()
            self.load_w(WR[:], w_rwkv, WRB)
            S.dma(W2[0:64, :], w2_d, writes=[cb]); S.dma(A2[64:128, :], a2_d, writes=[cb]); S.dma(G2[:], g2_d, writes=[cb])
            S.dma(GNG[:], gng_d, writes=[cb]); S.dma(GNB[:], gnb_d, writes=[cb])
            S.op("dve", lambda e: e.memset(BO[:], 0.0), reads=[cb], writes=[cb])
            S.op("dve", lambda e: e.memset(BO[0:64, 0:64], 1.0), reads=[cb], writes=[cb])
            S.op("dve", lambda e: e.memset(BO[64:128, 64:128], 1.0), reads=[cb], writes=[cb])
            S.op("dve", lambda e: e.tensor_copy(BOb[:], BO[:]), reads=[cb], writes=[cb])
            S.op("dve", lambda e: e.tensor_copy(ID2[0:64, :], self.ident_f[0:64, 0:64]), reads=[cb, self.constb], writes=[cb])
            S.op("dve", lambda e: e.tensor_copy(ID2[64:128, :], self.ident_f[64:128, 64:128]), reads=[cb, self.constb], writes=[cb])
            ka0 = CN["k_a"][0]
            S.op("dve", lambda e: e.tensor_scalar(OMK[:], self.cols[:, ka0:ka0 + 4], -1.0, 1.0, ALU.mult, ALU.add),
                 reads=[cb, self.constb], writes=[cb])
            P32 = self.sb("P32", [128, 14, 129], F32); P32B = Buf()
            DD = self.sb("DD", [128, 128], F32); DDB = Buf()
            CAR = self.sb("CAR", [128, 14, 1], F32)
            PL = P32[:, :, 1:129]; PLB = P32B
            TW = self.sb("TW", [64, 128], F32); SGg = self.sb("SGg", [128, 128], F32)
            WD = self.sb("WD", [128, 4, 128], F32); SIG = WD
            A32 = self.sb("A32", [128, 4, 128], F32)
            KK = self.sb("KK", [128, 4, 128], F32); SQ = self.sb("SQ", [128, 4, 128], F32)
            KKN = self.sb("KKN", [128, 4, 128], F32); NB = self.sb("NB", [128, 4, 128], F32)
            KM = self.sb("KM", [128, 4, 128], F32); BON = self.sb("BON", [128, 4, 128], F32)
            RM = self.sb("RM", [128, 4, 128, 2], BF16)
            VDr = Ring([self.sb("VD%d" % i, [128, 4, 64], BF16) for i in range(2)])
            H = self.sb("H", [128, 4, 64], F32); Hb = self.sb("Hb", [128, 4, 64], BF16); HK = self.sb("HK", [128, 4, 64], BF16)
            T1 = self.sb("T1", [128, 4, 64], F32); T2r = Ring([self.sb("T2_%d" % i, [128, 4, 64], F32) for i in range(2)])
            YST = [self.sb("YST%d" % i, [2, 4, 256], F32) for i in range(2)]; YSTB = [Buf(), Buf()]
            YTOK = A32[:].rearrange("p c t -> p (c t)").rearrange("p (c h v) -> p c h v", c=4, h=2); YTOKB = Buf()
            YC = KKN[:].rearrange("p c t -> p (c t)").rearrange("p (a v) -> p a v", a=8)
            ST8 = self.sb("ST8", [128, 8], F32); ST8b = self.sb("ST8b", [128, 8], F32)
            YF = SQ
            db = Buf(); hb = Buf(); hbb = Buf(); hkb = Buf(); t1b = Buf(); vrb = Buf(); vtb = Buf(); rmb = Buf(); yb = Buf()
            S.op("pool", lambda e: e.memset(P32[:], 0.0), writes=[P32B])
            S.op("pool", lambda e: e.memset(RM[:], 0.0), writes=[rmb])
            S.op("pool", lambda e: e.memset(H[:], 0.0), writes=[hb])
            mu0 = CN["mu"][0]; w00 = CN["w0"][0]; a00 = CN["a0"][0]; kk0 = CN["k_k"][0]; rk0 = CN["r_k"][0]
            ident = self.ident_f
            for i in range(NT):
                t0 = i * 128
                tcix = t0 // TC
                tsl = slice(t0, t0 + 128)
                hreads = [self.HNB[c][tcix] for c in range(NCH)]
                for cg in range(4):
                    c0 = cg * 4
                    n = min(4, 14 - c0)
                    p, pb = self.psum.get()

                    def mm(e):
                        for cc in range(n):
                            for k in range(NCH):
                                ins = e.matmul(p[:, cc * 128:(cc + 1) * 128], WR[:, k, (c0 + cc) * 128:(c0 + cc + 1) * 128],
                                               self.HN[:, k, tsl], start=(k == 0), stop=(k == NCH - 1))
                        return ins
                    S.op("pe", mm, reads=hreads + [WRB], writes=[pb])
                    S.op("act", lambda e: e.copy(P32[:, c0:c0 + n, 1:129], p[:, 0:n * 128].rearrange("p (c t) -> p c t", c=n)),
                         reads=[pb], writes=[P32B])
                S.op("dve", lambda e: e.tensor_copy(CAR[:], P32[:, :, 128:129]), reads=[P32B], writes=[DDB])
                for c in range(14):
                    S.op("dve", lambda e: e.tensor_tensor(DD[:], P32[:, c, 0:128], P32[:, c, 1:129], ALU.subtract), reads=[P32B, DDB], writes=[DDB])
                    S.op("dve", lambda e: e.scalar_tensor_tensor(P32[:, c, 1:129], DD[:], self.cols[:, mu0 + c:mu0 + c + 1], P32[:, c, 1:129],
                                                                 ALU.mult, ALU.add), reads=[DDB, P32B, self.constb], writes=[P32B])
                S.op("dve", lambda e: e.tensor_copy(P32[:, :, 0:1], CAR[:]), reads=[P32B, DDB], writes=[P32B])
                S.op("act", lambda e: e.activation(TW[:], PL[0:64, 12, :], AF.Tanh), reads=[PLB], writes=[db])
                S.op("act", lambda e: e.activation(SGg[:], PL[:, 13, :], AF.Sigmoid), reads=[PLB], writes=[db])
                pz, pzb = self.psum.get(); pa, pab = self.psum.get()

                def mmz(e):
                    for fc in range(4):
                        ins = e.matmul(pz[:, fc * 128:(fc + 1) * 128], W2[0:64, fc * 128:(fc + 1) * 128], TW[:], start=True, stop=True)
                    return ins

                def mma(e):
                    for fc in range(4):
                        ins = e.matmul(pa[:, fc * 128:(fc + 1) * 128], A2[64:128, fc * 128:(fc + 1) * 128], PL[64:128, 12, :], start=True, stop=True)
                    return ins

                S.op("pe", mmz, reads=[db, cb], writes=[pzb])
                S.op("pe", mma, reads=[PLB, cb], writes=[pab])
                for fc in range(4):
                    S.op("act", lambda e: e.activation(SIG[:, fc, :], pz[:, fc * 128:(fc + 1) * 128], AF.Sigmoid,
                                                       bias=self.cols[:, w00 + fc:w00 + fc + 1]), reads=[pzb, self.constb], writes=[db])
                    S.op("act", lambda e: e.activation(A32[:, fc, :], pa[:, fc * 128:(fc + 1) * 128], AF.Sigmoid,
                                                       bias=self.cols[:, a00 + fc:a00 + fc + 1]), reads=[pab, self.constb], writes=[db, YTOKB])
                S.op("act", lambda e: e.activation(WD[:], SIG[:], AF.Exp, scale=-0.6065306597126334), reads=[db], writes=[db])
                for fc in range(4):
                    S.op("dve", lambda e: e.tensor_scalar(KK[:, fc, :], PL[:, 4 + fc, :], self.cols[:, kk0 + fc:kk0 + fc + 1], None, ALU.mult),
                         reads=[PLB, self.constb], writes=[db])
                S.op("dve", lambda e: e.tensor_tensor(SQ[:], KK[:], KK[:], ALU.mult), reads=[db], writes=[db])
                pss, pssb = self.psum.get()
                S.op("pe", lambda e: e.matmul(pss[:], BO[:], SQ[:].rearrange("p c t -> p (c t)"), start=True, stop=True), reads=[db, cb], writes=[pssb])
                S.op("act", lambda e: e.activation(SQ[:], pss[:].rearrange("p (c t) -> p c t", c=4), AF.Sqrt), reads=[pssb, db], writes=[db])
                S.op("dve", lambda e: e.tensor_scalar(SQ[:], SQ[:], 1e-12, None, ALU.max), reads=[db], writes=[db])
                S.op("dve", lambda e: e.reciprocal(SQ[:], SQ[:]), reads=[db], writes=[db])
                S.op("dve", lambda e: e.tensor_tensor(KKN[:], KK[:], SQ[:], ALU.mult), reads=[db], writes=[db, yb])
                S.op("dve", lambda e: e.scalar_tensor_tensor(NB[:], KKN[:], -1.0, A32[:], ALU.mult, ALU.mult), reads=[db], writes=[db])
                for fc in range(4):
                    S.op("dve", lambda e: e.tensor_scalar(KK[:, fc, :], A32[:, fc, :], self.cols[:, ka0 + fc:ka0 + fc + 1], OMK[:, fc:fc + 1],
                                                          ALU.mult, ALU.add), reads=[db, cb, self.constb], writes=[db])
                S.op("dve", lambda e: e.tensor_tensor(KM[:], PL[:, 4:8, :], KK[:], ALU.mult), reads=[db, PLB], writes=[db])
                S.op("dve", lambda e: e.tensor_tensor(SQ[:], PL[:, 0:4, :], KM[:], ALU.mult), reads=[db, PLB], writes=[db])
                for fc in range(4):
                    S.op("dve", lambda e: e.tensor_scalar(SQ[:, fc, :], SQ[:, fc, :], self.cols[:, rk0 + fc:rk0 + fc + 1], None, ALU.mult),
                         reads=[db, self.constb], writes=[db])
                pbn, pbnb = self.psum.get()
                S.op("pe", lambda e: e.matmul(pbn[:], BO[:], SQ[:].rearrange("p c t -> p (c t)"), start=True, stop=True), reads=[db, cb], writes=[pbnb])
                S.op("dve", lambda e: e.tensor_tensor(BON[:], pbn[:].rearrange("p (c t) -> p c t", c=4), PL[:, 8:12, :], ALU.mult),
                     reads=[pbnb, PLB], writes=[db])
                S.op("dve", lambda e: e.tensor_copy(RM[0:64, :, :, 0], PL[0:64, 0:4, :]), reads=[PLB, rmb], writes=[rmb])
                S.op("dve", lambda e: e.tensor_copy(RM[64:128, :, :, 1], PL[64:128, 0:4, :]), reads=[PLB, rmb], writes=[rmb])
                for tt in range(128):
                    pvb_t, pvbb = self.psum.get()

                    VD, vdb = VDr.get()
                    S.op("pool", lambda e: e.tensor_tensor(VD[:], ID2[:].unsqueeze(1).to_broadcast([128, 4, 64]),
                                                           PL[:, 8:12, tt:tt + 1].to_broadcast([128, 4, 64]), ALU.mult),
                         reads=[PLB, cb], writes=[vdb])
                    S.op("pe", lambda e: e.matmul(pvb_t[:, 0:256], BOb[:], VD[:].rearrange("p c v -> p (c v)"), start=True, stop=True),
                         reads=[vdb, cb], writes=[pvbb])
                    T2, t2b = T2r.get()
                    S.op("pool" if False else "dve", lambda e: e.tensor_tensor(
                        T2[:], pvb_t[:, 0:256].rearrange("p (c v) -> p c v", c=4), KM[:, :, tt:tt + 1].to_broadcast([128, 4, 64]), ALU.mult),
                        reads=[pvbb, db], writes=[t2b])
                    S.op("dve", lambda e: e.tensor_tensor(HK[:], H[:], KKN[:, :, tt:tt + 1].to_broadcast([128, 4, 64]), ALU.mult),
                         reads=[hb, db], writes=[hkb])
                    psa, psab = self.psum.get()
                    S.op("pe", lambda e: e.matmul(psa[:, 0:256], BOb[:], HK[:].rearrange("p c v -> p (c v)"), start=True, stop=True),
                         reads=[hkb, cb], writes=[psab])
                    S.op("dve", lambda e: e.tensor_tensor(T1[:], psa[:, 0:256].rearrange("p (c v) -> p c v", c=4),
                                                          NB[:, :, tt:tt + 1].to_broadcast([128, 4, 64]), ALU.mult),
                         reads=[psab, db], writes=[t1b])
                    S.op("dve", lambda e: e.tensor_tensor(H[:], H[:], WD[:, :, tt:tt + 1].to_broadcast([128, 4, 64]), ALU.mult),
                         reads=[hb, db], writes=[hb])
                    S.op("dve", lambda e: e.tensor_tensor(T1[:], T1[:], T2[:], ALU.add), reads=[t1b, t2b], writes=[t1b])
                    S.op("dve", lambda e: e.tensor_tensor(H[:], H[:], T1[:], ALU.add), reads=[hb, t1b], writes=[hb])
                    S.op("act", lambda e: e.copy(Hb[:], H[:]), reads=[hb], writes=[hbb])
                    py, pyb = self.psum.get()

                    def mmy(e):
                        for fc in range(4):
                            ins = e.matmul(py[0:2, fc * 64:(fc + 1) * 64], RM[:, fc, tt, :], Hb[:, fc, :], start=True, stop=True)
                        return ins
                    S.op("pe", mmy, reads=[hbb, rmb], writes=[pyb])
                    slot = tt % 2
                    S.op("act", lambda e: e.copy(YST[slot][0:2, 0, :], py[0:2, 0:256]), reads=[pyb], writes=[YSTB[slot]])
                    for hp in range(2):
                        S.dma(YTOK[tt:tt + 1, :, hp, :], YST[slot][hp:hp + 1, 0, :].rearrange("p (c v) -> p c v", c=4),
                              reads=[YSTB[slot], db], writes=[YTOKB])
                YT8 = YTOK.rearrange("t c h v -> t (c h) v")
                S.op("dve", lambda e: e.tensor_reduce(ST8[:], YT8, AX.X, ALU.add), reads=[YTOKB], writes=[yb])
                S.op("dve", lambda e: e.tensor_scalar(ST8[:], ST8[:], 1.0 / 64, None, ALU.mult), reads=[yb], writes=[yb])
                S.op("dve", lambda e: e.tensor_tensor(YC, YT8, ST8[:].unsqueeze(2).to_broadcast([128, 8, 64]), ALU.subtract),
                     reads=[YTOKB, yb], writes=[yb, db])
                S.op("dve", lambda e: e.tensor_tensor(YTOK.rearrange("t c h v -> t (c h) v"), YC, YC, ALU.mult), reads=[yb, YTOKB], writes=[YTOKB])
                S.op("dve", lambda e: e.tensor_reduce(ST8b[:], YT8, AX.X, ALU.add), reads=[YTOKB], writes=[yb])
                S.op("act", lambda e: e.activation(ST8b[:], ST8b[:], AF.Sqrt, bias=self.gneps_t[:], scale=1.0 / 64), reads=[yb, self.constb], writes=[yb])
                S.op("dve", lambda e: e.reciprocal(ST8b[:], ST8b[:]), reads=[yb], writes=[yb])
                S.op("dve", lambda e: e.tensor_tensor(YC, YC, ST8b[:].unsqueeze(2).to_broadcast([128, 8, 64]), ALU.mult), reads=[yb], writes=[yb])
                YCf = YC.rearrange("t a v -> t (a v)")
                S.op("dve", lambda e: e.tensor_tensor(YCf, YCf, GNG[:], ALU.mult), reads=[yb, cb], writes=[yb])
                S.op("dve", lambda e: e.tensor_tensor(YCf, YCf, GNB[:], ALU.add), reads=[yb, cb], writes=[yb])
                pyt, pytb = self.psum.get(); pg, pgb = self.psum.get()

                def mmt2(e):
                    for fc in range(4):
                        ins = e.transpose(pyt[:, fc * 128:(fc + 1) * 128], YC[:, 2 * fc:2 * fc + 2, :].rearrange("t a v -> t (a v)"), ident[:])
                    for fc in range(4):
                        ins = e.matmul(pg[:, fc * 128:(fc + 1) * 128], G2[:, fc * 128:(fc + 1) * 128], SGg[:], start=True, stop=True)
                    return ins
                S.op("pe", mmt2, reads=[yb, self.constb, db, cb], writes=[pytb, pgb])
                S.op("dve", lambda e: e.tensor_tensor(YF[:], pyt[:].rearrange("p (c t) -> p c t", c=4), BON[:], ALU.add), reads=[pytb, db], writes=[yb, db])
                S.op("dve", lambda e: e.tensor_tensor(self.Y[0][:, :, tsl], YF[:], pg[:].rearrange("p (c t) -> p c t", c=4), ALU.mult),
                     reads=[yb, db, pgb], writes=[self.YB[0][tcix]])
            S.full_barrier()
            self.st = old


    def nsa_branch(self, d):
        S = self.S
        NT = S_LEN // 128
        with ExitStack() as st4:
            old, self.st = self.st, st4
            cb = Buf()
            KT = self.sb("KT", [128, 2, S_LEN], BF16); KTB = Buf()
            VT = self.sb("VT", [128, NT, 256], BF16); VTB = Buf()
            KC = self.sb("KC", [128, 127], BF16); VC = self.sb("VC", [128, 128], BF16); kcb = Buf()
            BM = self.sb("BM", [128, 3, 2, 512], BF16)
            BVC = self.sb("BVC", [32, 2, 512], BF16)
            stA = ExitStack(); self.st = stA
            G1 = self.sb("G1", [128, 2, 512], F32); G2_ = self.sb("G2b", [128, 2, 512], F32); MK = self.sb("MK", [128, 128], F32)
            gb = Buf()
            S.dma(G2_[:], d["t31"], writes=[gb])
            for kind in range(3):
                S.dma(G1[:], d["bmg"][kind], reads=[gb], writes=[gb])
                S.dma(MK[:], d["msk"][kind], reads=[gb], writes=[gb])
                S.op("dve", lambda e: e.tensor_tensor(G1[:], G1[:], G2_[:], ALU.subtract), reads=[gb], writes=[gb])
                S.op("dve", lambda e: e.tensor_tensor(BM[:, kind, :, :].rearrange("p g (j q) -> p (g j) q", j=4),
                                                      G1[:].rearrange("p g (j q) -> p (g j) q", j=4),
                                                      MK[:].unsqueeze(1).to_broadcast([128, 8, 128]), ALU.add), reads=[gb], writes=[cb, gb])
            S.dma(G1[0:32, :, :], d["bvcg"], reads=[gb], writes=[gb])
            S.dma(MK[0:32, :], d["mskc"], reads=[gb], writes=[gb])
            S.op("dve", lambda e: e.tensor_tensor(G1[0:32], G1[0:32], G2_[0:32], ALU.subtract), reads=[gb], writes=[gb])
            S.op("dve", lambda e: e.tensor_tensor(BVC[:].rearrange("p g (j q) -> p (g j) q", j=4),
                                                  G1[0:32].rearrange("p g (j q) -> p (g j) q", j=4),
                                                  MK[0:32, :].unsqueeze(1).to_broadcast([32, 8, 128]), ALU.add), reads=[gb], writes=[cb, gb])
            S.full_barrier()
            stA.close()
            stB = ExitStack(); self.st = stB
            KCMP = self.sb("KCMP", [128, S_LEN], BF16); VCT = self.sb("VCT", [128, S_LEN], BF16)
            stB1 = ExitStack(); self.st = stB1
            WKV = self.sb("WKV", [128, NCH, 768], BF16); wkvb = Buf()
            self.load_w(WKV[:], d["w_kvn"], wkvb)
            for tc in range(NTC):
                ts = slice(tc * TC, (tc + 1) * TC)
                hreads = [self.HNB[c][tc] for c in range(NCH)]
                for dst, col in ((KCMP[:, ts], 0), (VCT[:, ts], 128), (KT[:, 0, ts], 256), (KT[:, 1, ts], 512)):
                    p, pb = self.psum.get()

                    def mm(e):
                        for k in range(NCH):
                            ins = e.matmul(p[:], WKV[:, k, col:col + 128], self.HN[:, k, ts], start=(k == 0), stop=(k == NCH - 1))
                        return ins
                    S.op("pe", mm, reads=hreads + [wkvb], writes=[pb])
                    S.op("act", lambda e: e.copy(dst, p[:]), reads=[pb], writes=[KTB])
                for tl in range(4):
                    tile = tc * 4 + tl
                    tq = slice(tile * 128, (tile + 1) * 128)
                    p, pb = self.psum.get()

                    def mm(e):
                        for k in range(NCH):
                            e.matmul(p[:, 0:128], self.HN[:, k, tq], WKV[:, k, 384:512], start=(k == 0), stop=(k == NCH - 1))
                        for k in range(NCH):
                            ins = e.matmul(p[:, 128:256], self.HN[:, k, tq], WKV[:, k, 640:768], start=(k == 0), stop=(k == NCH - 1))
                        return ins
                    S.op("pe", mm, reads=hreads + [wkvb], writes=[pb])
                    S.op("dve", lambda e: e.tensor_copy(VT[:, tile, :], p[:, 0:256]), reads=[pb], writes=[VTB])
            S.full_barrier()
            stB1.close()
            stB2 = ExitStack(); self.st = stB2
            W1 = self.sb("W1", [128, 32, 256], BF16); PET = self.sb("PET", [128, 32], BF16)
            W2D = self.sb("W2D", [128, 2, 128], BF16); HID = self.sb("HID", [128, 2, 127], BF16)
            ZZ = self.sb("ZZ", [128, 127], F32); Z2 = self.sb("Z2", [128, 127], F32); BC = self.sb("BCc", [128, 1], F32)
            wb = Buf(); zb = Buf(); hb_ = Buf()
            for kv in range(2):
                w1d = d["cmp_w1"][kv].rearrange("(l dd) m -> dd l m", dd=64)
                S.dma(W1[0:64], w1d, writes=[wb], queue="pool"); S.dma(W1[64:128], w1d, writes=[wb], queue="pool")
                S.dma(PET[0:64], d["cmp_peT"][kv], writes=[wb], queue="pool"); S.dma(PET[64:128], d["cmp_peT"][kv], writes=[wb], queue="pool")
                w2v = d["cmp_w2"][kv].rearrange("(c p) n -> p c n", p=128)
                S.dma(W2D[:, :, 0:64], w2v, writes=[wb], queue="pool"); S.dma(W2D[:, :, 64:128], w2v, writes=[wb], queue="pool")
                SRC = KCMP if kv == 0 else VCT
                for g in range(2):
                    gs = slice(g * 64, (g + 1) * 64)
                    for mc in range(2):
                        ph, phb = self.psum.get(); pbias, pbb = self.psum.get()

                        def mm(e):
                            for l in range(32):
                                ins = e.matmul(ph[:, 0:127], W1[gs, l, mc * 128:(mc + 1) * 128], SRC[gs, l:l + 16 * 126 + 1:16],
                                               start=(l == 0), stop=(l == 31))
                            return ins

                        def mmb(e):
                            for l in range(32):
                                ins = e.matmul(pbias[:, 0:1], W1[gs, l, mc * 128:(mc + 1) * 128], PET[gs, l:l + 1], start=(l == 0), stop=(l == 31))
                            return ins
                        S.op("pe", mm, reads=[wb, KTB], writes=[phb])
                        S.op("pe", mmb, reads=[wb], writes=[pbb])
                        S.op("act", lambda e: e.copy(BC[:], pbias[:, 0:1]), reads=[pbb, zb], writes=[zb])
                        S.op("dve", lambda e: e.tensor_scalar(ZZ[:], ph[:, 0:127], BC[:, 0:1], None, ALU.add), reads=[phb, zb], writes=[zb])
                        S.op("dve", lambda e: e.tensor_tensor(Z2[:], ZZ[:], ZZ[:], ALU.mult), reads=[zb], writes=[zb])
                        S.op("dve", lambda e: e.tensor_scalar(Z2[:], Z2[:], 0.044715, 1.0, ALU.mult, ALU.add), reads=[zb], writes=[zb])
                        S.op("dve", lambda e: e.tensor_tensor(Z2[:], Z2[:], ZZ[:], ALU.mult), reads=[zb], writes=[zb])
                        S.op("act", lambda e: e.activation(Z2[:], Z2[:], AF.Sigmoid, scale=1.5957691216057308), reads=[zb], writes=[zb])
                        S.op("dve", lambda e: e.tensor_tensor(HID[:, mc, :], ZZ[:], Z2[:], ALU.mult), reads=[zb, hb_], writes=[hb_])
                    po, pob = self.psum.get()
                    if kv == 0:
                        def mm2(e):
                            for mc in range(2):
                                ins = e.matmul(po[:, 0:127], W2D[:, mc, :], HID[:, mc, :], start=(mc == 0), stop=(mc == 1))
                            return ins
                        S.op("pe", mm2, reads=[hb_, wb], writes=[pob])
                        S.op("act", lambda e: e.copy(KC[gs, :], po[gs, 0:127]), reads=[pob], writes=[kcb])
                    else:
                        def mm2(e):
                            for mc in range(2):
                                ins = e.matmul(po[0:127, 0:64], HID[:, mc, :], W2D[:, mc, 0:64], start=(mc == 0), stop=(mc == 1))
                            return ins
                        S.op("pe", mm2, reads=[hb_, wb], writes=[pob])
                        S.op("act", lambda e: e.copy(VC[0:127, gs], po[0:127, 0:64]), reads=[pob], writes=[kcb])
            S.full_barrier()
            stB2.close(); stB.close(); self.st = st4
            if "kcvc" in self.debug:
                okc = self.dout("dbg_kc", [128, 127]); ovc = self.dout("dbg_vc", [127, 128])
                S.dma(okc, KC[:], reads=[kcb], queue="pool"); S.dma(ovc, VC[0:127, :], reads=[kcb], queue="pool")
            WQ = self.sb("WQN", [128, NCH, 512], BF16); WGN = self.sb("WGN", [128, NCH, 24], BF16)
            SHCF = self.sb("SHCF", [32, 247], BF16); EF = self.sb("EF", [32, S_LEN], BF16)
            OV = self.sb("OV", [128, 32], BF16); AB = self.sb("ABF", [128, 2, 64], F32)
            SELG = self.sb("SELG", [24, 12, 128], BF16); IDb = self.sb("IDb", [128, 128], BF16)
            self.load_w(WQ[:], d["w_qn"], cb)
            self.load_w(WGN[:], d["w_gn"], cb)
            S.dma(SHCF[:], d["shcf"], writes=[cb], queue="pool"); S.dma(EF[:], d["efull"], writes=[cb], queue="pool")
            S.dma(OV[0:127, :], d["ov"], writes=[cb], queue="pool"); S.dma(AB[:], d["abf"], writes=[cb])
            S.dma(SELG[:], d["selg"], writes=[cb], queue="pool")
            S.op("dve", lambda e: e.tensor_copy(IDb[:], self.ident_f[:]), reads=[self.constb, cb], writes=[cb])
            QS = self.sb("QS", [128, 4, 128], BF16); qsb = Buf()
            GS = self.sb("GS", [24, 128], BF16); gsb = Buf()
            pt_ring = Ring([self.sb("PT%d" % i, [128, 512], BF16) for i in range(3)])
            RR = self.sb("RR", [128, 512], F32); rrb = Buf()
            YA = self.sb("YA", [128, 512], F32); yab = Buf()
            PN = self.sb("PN", [128, 512], BF16); pnb = Buf()
            IMP = self.sb("IMP", [128, 32], F32); IM2 = self.sb("IM2", [128, 32], F32); MX = self.sb("MX8", [128, 8], F32); ib = Buf()
            NMT = [self.sb("NMT%d" % g, [32, 4, 128], BF16) for g in range(2)]; nmb = [Buf(), Buf()]
            st_ring = Ring(self.banks[0:3], self.bankb[0:3])
            O, Ob = self.banks[3], self.bankb[3]
            DN, Db = self.banks[4], self.bankb[4]
            ms_ring = Ring(self.banks[5:8], self.bankb[5:8])
            for i in range(NT):
                tq = slice(i * 128, (i + 1) * 128)
                tcix = i // 4
                hreads = [self.HNB[c][tcix] for c in range(NCH)]
                p, pb = ms_ring.get()

                def mmq(e):
                    for j in range(4):
                        for k in range(NCH):
                            ins = e.matmul(p[:, j * 128:(j + 1) * 128], WQ[:, k, j * 128:(j + 1) * 128], self.HN[:, k, tq], start=(k == 0), stop=(k == NCH - 1))
                    return ins
                S.op("pe", mmq, reads=hreads + [cb], writes=[pb])
                S.op("act", lambda e: e.activation(QS[:].rearrange("p j q -> p (j q)"), p[:], AF.Copy, scale=0.125), reads=[pb], writes=[qsb])
                p2, pb2 = ms_ring.get()

                def mmg(e):
                    for k in range(NCH):
                        ins = e.matmul(p2[0:24, 0:128], WGN[:, k, :], self.HN[:, k, tq], start=(k == 0), stop=(k == NCH - 1))
                    return ins
                S.op("pe", mmg, reads=hreads + [cb], writes=[pb2])
                S.op("act", lambda e: e.activation(GS[:], p2[0:24, 0:128], AF.Sigmoid), reads=[pb2], writes=[gsb])
                for br in range(3):
                    for g in range(2):
                        gs = slice(g * 64, (g + 1) * 64)
                        qrhs = QS[gs, :, :].rearrange("p j q -> p (j q)")
                        if br == 0:
                            tiles = [None]
                        elif br == 1:
                            tiles = list(range(0, i + 1))
                        else:
                            tiles = list(range(max(0, i - 4), i + 1))
                        for ti, kt in enumerate(tiles):
                            stp, stb = st_ring.get()
                            rows = 127 if br == 0 else 128

                            def mms(e):
                                mms_list = []
                                if br == 0:
                                    mms_list.append((KC[gs, :], qrhs))
                                    mms_list.append((SHCF[:, 120 - 8 * i:247 - 8 * i], BVC[:, g, :]))
                                else:
                                    mms_list.append((KT[gs, br - 1, kt * 128:(kt + 1) * 128], qrhs))
                                    if br == 1 and i >= 8:
                                        mms_list.append((EF[:, kt * 128:(kt + 1) * 128], NMT[g][:].rearrange("p j q -> p (j q)")))
                                    if kt == i:
                                        mms_list.append((IDb[:], BM[:, 0, g, :]))
                                    elif kt == i - 1:
                                        mms_list.append((IDb[:], BM[:, 1, g, :]))
                                    elif br == 2 and kt == i - 4:
                                        mms_list.append((IDb[:], BM[:, 2, g, :]))
                                for n_, (l_, r_) in enumerate(mms_list):
                                    ins = e.matmul(stp[0:rows, :], l_, r_, start=(n_ == 0), stop=(n_ == len(mms_list) - 1))
                                return ins
                            S.op("pe", mms, reads=[qsb, KTB, kcb, cb, nmb[g]], writes=[stb])
                            PT, ptb = pt_ring.get()
                            S.op("act", lambda e: e.activation(PT[0:rows, :], stp[0:rows, :], AF.Exp), reads=[stb], writes=[ptb])
                            if br == 0:
                                vl = VC[0:127, gs]
                            else:
                                c0 = (0 if br == 1 else 128) + g * 64
                                vl = VT[:, kt, c0:c0 + 64]
                            first = (ti == 0); last = (ti == len(tiles) - 1)

                            def mmo(e):
                                e.matmul(O[gs, :], vl, PT[0:rows, :], start=first, stop=last)
                                return e.matmul(DN[gs, :], self.ones_b[0:rows, 0:64], PT[0:rows, :], start=first, stop=last)
                            S.op("pe", mmo, reads=[ptb, VTB, kcb, self.constb], writes=[Ob, Db])
                            if br == 0:
                                pd2, pdb2 = ms_ring.get()
                                S.op("pe", lambda e: e.matmul(pd2[0:127, :], self.ones_b[0:127, 0:127], PT[0:127, :], start=True, stop=True),
                                     reads=[ptb, self.constb], writes=[pdb2])
                                S.op("dve", lambda e: e.tensor_scalar(RR[0:127, :], pd2[0:127, :], 1e-30, None, ALU.max), reads=[pdb2, rrb], writes=[rrb])
                                S.op("dve", lambda e: e.reciprocal(RR[0:127, :], RR[0:127, :]), reads=[rrb], writes=[rrb])
                                S.op("dve", lambda e: e.tensor_tensor(PN[0:127, :], PT[0:127, :], RR[0:127, :], ALU.mult), reads=[rrb, ptb, pnb], writes=[pnb])
                                if i >= 8:
                                    pim, pimb = ms_ring.get()

                                    def mmi(e):
                                        for j in range(4):
                                            ins = e.matmul(pim[:, 0:32], PN[0:127, j * 128:(j + 1) * 128], OV[0:127, :], start=(j == 0), stop=(j == 3))
                                        return ins
                                    S.op("pe", mmi, reads=[pnb, cb], writes=[pimb])
                                    o0 = 32 - 2 * i
                                    S.op("dve", lambda e: e.tensor_tensor(IMP[:], pim[:, 0:32], AB[:, 0, o0:o0 + 32], ALU.mult), reads=[pimb, cb, ib], writes=[ib])
                                    S.op("dve", lambda e: e.tensor_tensor(IMP[:], IMP[:], AB[:, 1, o0:o0 + 32], ALU.add), reads=[ib, cb], writes=[ib])
                                    S.op("dve", lambda e: e.memset(IMP[:, 0:1], 1e6), reads=[ib], writes=[ib])
                                    S.op("dve", lambda e: e.max(MX[:], IMP[:]), reads=[ib], writes=[ib])
                                    S.op("dve", lambda e: e.match_replace(IM2[:], MX[:], IMP[:], 0.0), reads=[ib], writes=[ib])
                                    S.op("dve", lambda e: e.max(MX[:], IM2[:]), reads=[ib], writes=[ib])
                                    S.op("dve", lambda e: e.match_replace(IM2[:], MX[:], IM2[:], 0.0), reads=[ib], writes=[ib])
                                    S.op("dve", lambda e: e.tensor_tensor(IM2[:], IMP[:], IM2[:], ALU.subtract), reads=[ib], writes=[ib])
                                    S.op("dve", lambda e: e.tensor_scalar(IM2[:], IM2[:], 0.0, None, ALU.is_gt), reads=[ib], writes=[ib])
                                    S.op("dve", lambda e: e.tensor_scalar(IM2[:], IM2[:], 30000.0, -30000.0, ALU.mult, ALU.add), reads=[ib], writes=[ib])
                                    ptr, ptrb = ms_ring.get()
                                    S.op("pe", lambda e: e.transpose(ptr[0:32, 0:128], IM2[:], self.ident_f[:]), reads=[ib, self.constb], writes=[ptrb])
                                    S.op("dve", lambda e: e.tensor_copy(NMT[g][:], ptr[0:32, 0:128].unsqueeze(1).to_broadcast([32, 4, 128])),
                                         reads=[ptrb], writes=[nmb[g]])
                    S.op("dve", lambda e: e.tensor_scalar(RR[:], DN[:], 1e-30, None, ALU.max), reads=[Db, rrb], writes=[rrb])
                    S.op("dve", lambda e: e.reciprocal(RR[:], RR[:]), reads=[rrb], writes=[rrb])
                    pgb_, pgbb = ms_ring.get()

                    def mmgb(e):
                        for j in range(4):
                            ins = e.matmul(pgb_[:, j * 128:(j + 1) * 128], SELG[:, br * 4 + j, :], GS[:], start=True, stop=True)
                        return ins
                    S.op("pe", mmgb, reads=[gsb, cb], writes=[pgbb])
                    S.op("dve", lambda e: e.tensor_tensor(RR[:], RR[:], pgb_[:], ALU.mult), reads=[rrb, pgbb], writes=[rrb])
                    if br == 0:
                        S.op("dve", lambda e: e.tensor_tensor(YA[:], O[:], RR[:], ALU.mult), reads=[Ob, rrb, yab], writes=[yab])
                    else:
                        S.op("dve", lambda e: e.tensor_tensor(RR[:], O[:], RR[:], ALU.mult), reads=[Ob, rrb], writes=[rrb])
                        S.op("dve", lambda e: e.tensor_tensor(YA[:], YA[:], RR[:], ALU.add), reads=[rrb, yab], writes=[yab])
                S.op("act", lambda e: e.copy(self.Y[1][:, :, tq], YA[:].rearrange("p (j q) -> p j q", j=4)), reads=[yab], writes=[self.YB[1][tcix]])
            S.full_barrier()
            self.st = old

    def mem_branch(self, memT, wk_d, wv_d, wqm_d):
        S = self.S
        with ExitStack() as st4:
            old, self.st = self.st, st4
            WQ = self.sb("WQM", [128, NCH, 512], BF16); WQB = Buf()
            KHT = self.sb("KHT", [128, 4, 256], BF16); KHTB = Buf()
            VH = self.sb("VH", [128, 2, 512], BF16); VHB = Buf()
            st5 = ExitStack()
            self.st = st5
            MT = self.sb("MT", [128, NCH, 256], F32); MTB = Buf()
            MN = self.sb("MN", [128, NCH, 256], BF16); MNB = Buf()
            WK = self.sb("WK", [128, NCH, 512], BF16); WKB = Buf()
            WV = self.sb("WV", [128, NCH, 512], BF16); WVB = Buf()
            mr = self.sb("mrstd", [128, 256], F32); mrb = Buf()
            S.dma(MT[:], memT.rearrange("(c p) m -> p c m", p=128), writes=[MTB])
            self.load_w(WK[:], wk_d, WKB)
            self.load_w(WV[:], wv_d, WVB)
            self.load_w(WQ[:], wqm_d, WQB)
            g0, _ = COLS["mem_norm"]
            pt, pb = self.psum.get()
            for c in range(NCH):
                sq, sqb = self.sq_ring.get()
                S.op("act", lambda e: e.activation(sq[:, 0:256], MT[:, c, :], AF.Square), reads=[MTB], writes=[sqb])
                S.op("pe", lambda e: e.matmul(pt[:, 0:256], self.ones_f[:], sq[:, 0:256], start=(c == 0), stop=(c == NCH - 1)),
                     reads=[sqb, self.constb], writes=[pb])
            S.op("act", lambda e: e.activation(mr[:], pt[:, 0:256], AF.Sqrt, bias=self.eps_t[:], scale=1.0 / D),
                 reads=[pb, self.constb], writes=[mrb])
            S.op("dve", lambda e: e.reciprocal(mr[:], mr[:]), reads=[mrb], writes=[mrb])
            for c in range(NCH):
                S.op("dve", lambda e: e.scalar_tensor_tensor(MN[:, c, :], MT[:, c, :], self.cols[:, g0 + c:g0 + c + 1], mr[:],
                                                             ALU.mult, ALU.mult),
                     reads=[MTB, mrb, self.constb], writes=[MNB])
            for h in range(4):
                p, pb = self.psum.get()

                def mm(e):
                    for k in range(NCH):
                        ins = e.matmul(p[:, 0:256], WK[:, k, h * 128:(h + 1) * 128], MN[:, k, :], start=(k == 0), stop=(k == NCH - 1))
                    return ins
                S.op("pe", mm, reads=[WKB, MNB], writes=[pb])
                S.op("act", lambda e: e.copy(KHT[:, h, :], p[:, 0:256]), reads=[pb], writes=[KHTB])
            for mt in range(2):
                p, pb = self.psum.get()

                def mm(e):
                    for k in range(NCH):
                        ins = e.matmul(p[:], MN[:, k, mt * 128:(mt + 1) * 128], WV[:, k, :], start=(k == 0), stop=(k == NCH - 1))
                    return ins
                S.op("pe", mm, reads=[WVB, MNB], writes=[pb])
                S.op("act", lambda e: e.copy(VH[:, mt, :], p[:]), reads=[pb], writes=[VHB])
            S.full_barrier()
            st5.close()
            self.st = st4
            qm_ring = Ring([self.sb("qm%d" % i, [128, TC], BF16) for i in range(2)])
            pt_ring = Ring([self.sb("pt%d" % i, [128, 2, TC], BF16) for i in range(2)])
            rd_ring = self.sq_ring
            scale = 128.0 ** -0.5
            for tc in range(NTC):
                ts = slice(tc * TC, (tc + 1) * TC)
                hreads = [self.HNB[c][tc] for c in range(NCH)]
                for h in range(4):
                    p, pb = self.psum.get()

                    def mm(e):
                        for k in range(NCH):
                            ins = e.matmul(p[:], WQ[:, k, h * 128:(h + 1) * 128], self.HN[:, k, ts], start=(k == 0), stop=(k == NCH - 1))
                        return ins
                    S.op("pe", mm, reads=hreads + [WQB], writes=[pb])
                    qm, qmb = qm_ring.get()
                    S.op("dve", lambda e: e.tensor_copy(qm[:], p[:]), reads=[pb], writes=[qmb])
                    ptile, ptb = pt_ring.get()
                    for mt in range(2):
                        ps_, psb = self.psum.get()
                        S.op("pe", lambda e: e.matmul(ps_[:], KHT[:, h, mt * 128:(mt + 1) * 128], qm[:], start=True, stop=True),
                             reads=[KHTB, qmb], writes=[psb])
                        S.op("act", lambda e: e.activation(ptile[:, mt, :], ps_[:], AF.Exp, scale=scale), reads=[psb], writes=[ptb])
                    po, pob = self.psum.get()
                    pd, pdb = self.psum.get()

                    def mm_o(e):
                        for mt in range(2):
                            ins = e.matmul(po[:], VH[:, mt, h * 128:(h + 1) * 128], ptile[:, mt, :], start=(mt == 0), stop=(mt == 1))
                        return ins

                    def mm_d(e):
                        for mt in range(2):
                            ins = e.matmul(pd[:], self.ones_b[:], ptile[:, mt, :], start=(mt == 0), stop=(mt == 1))
                        return ins
                    S.op("pe", mm_o, reads=[VHB, ptb], writes=[pob])
                    S.op("pe", mm_d, reads=[ptb, self.constb], writes=[pdb])
                    rd, rdb = rd_ring.get()
                    S.op("dve", lambda e: e.reciprocal(rd[:], pd[:]), reads=[pdb], writes=[rdb])
                    S.op("dve", lambda e: e.tensor_tensor(self.Y[2][:, h, ts], po[:], rd[:], ALU.mult),
                         reads=[pob, rdb], writes=[self.YB[2][tc]])
            S.full_barrier()
            self.st = old

    def fold(self, br, wgb_d, wbr_d, first):
        S = self.S
        with ExitStack() as st4:
            old, self.st = self.st, st4
            WGB = [self.sb("WGBr%d" % i, [128, NCH, 128], BF16) for i in range(2)]; WGBB = [Buf(), Buf()]
            WBR = [self.sb("WBR%d" % i, [128, 4, 128], BF16) for i in range(2)]; WBRB = [Buf(), Buf()]
            gt_ring = Ring([self.sb("gt%d" % i, [128, TC], F32) for i in range(2)])
            t_ring = Ring([self.sb("mt%d" % i, [128, TC], F32) for i in range(2)])

            def load(dc):
                sl = dc % 2
                c0 = br * D + dc * 128
                S.dma(WGB[sl][:], wgb_d[:, c0:c0 + 128].rearrange("(k p) n -> p k n", p=128), writes=[WGBB[sl]], queue="pool")
                S.dma(WBR[sl][:], wbr_d[:, dc * 128:(dc + 1) * 128].rearrange("(k p) n -> p k n", p=128), writes=[WBRB[sl]], queue="pool")
            load(0)
            for dc in range(NCH):
                if dc + 1 < NCH:
                    load(dc + 1)
                sl = dc % 2
                for tc in range(NTC):
                    ts = slice(tc * TC, (tc + 1) * TC)
                    hreads = [self.HNB[c][tc] for c in range(NCH)]
                    pg, pgb = self.psum.get()
                    py, pyb = self.psum.get()

                    def mm_g(e):
                        for k in range(NCH):
                            ins = e.matmul(pg[:], WGB[sl][:, k, :], self.HN[:, k, ts], start=(k == 0), stop=(k == NCH - 1))
                        return ins

                    def mm_y(e):
                        for k in range(4):
                            ins = e.matmul(py[:], WBR[sl][:, k, :], self.Y[br][:, k, ts], start=(k == 0), stop=(k == 3))
                        return ins
                    S.op("pe", mm_g, reads=hreads + [WGBB[sl]], writes=[pgb])
                    S.op("pe", mm_y, reads=[self.YB[br][tc], WBRB[sl]], writes=[pyb])
                    gt, gtb = gt_ring.get()
                    S.op("act", lambda e: e.activation(gt[:], pg[:], AF.Sigmoid), reads=[pgb], writes=[gtb])
                    if first:
                        S.op("dve", lambda e: e.tensor_tensor(self.M[:, dc, ts], gt[:], py[:], ALU.mult),
                             reads=[gtb, pyb], writes=[self.MB[dc][tc]])
                    else:
                        t, tb = t_ring.get()
                        S.op("dve", lambda e: e.tensor_tensor(t[:], gt[:], py[:], ALU.mult), reads=[gtb, pyb], writes=[tb])
                        S.op("pool", lambda e: e.tensor_tensor(self.M[:, dc, ts], self.M[:, dc, ts], t[:], ALU.add),
                             reads=[tb, self.MB[dc][tc]], writes=[self.MB[dc][tc]])
            S.full_barrier()
            self.st = old

    def outproj(self, wout_d):
        S = self.S
        with ExitStack() as st4:
            old, self.st = self.st, st4
            WO = self.sb("WO", [128, NCH, D], BF16); WOB = Buf()
            self.load_w(WO[:], wout_d, WOB)
            for tc in range(NTC):
                ts = slice(tc * TC, (tc + 1) * TC)
                for d2 in range(NCH):
                    po, pob = self.psum.get()

                    def mm(e):
                        for k in range(NCH):
                            ins = e.matmul(po[:], WO[:, k, d2 * 128:(d2 + 1) * 128], self.M[:, k, ts], start=(k == 0), stop=(k == NCH - 1))
                        return ins
                    S.op("pe", mm, reads=[self.MB[k][tc] for k in range(NCH)] + [WOB], writes=[pob])
                    S.op("dve", lambda e: e.tensor_tensor(self.X[:, d2, ts], po[:], self.X[:, d2, ts], ALU.add),
                         reads=[pob, self.XB[d2][tc]], writes=[self.XB[d2][tc]])
            S.full_barrier()
            self.st = old

    def final_norm_out(self, outT):
        S = self.S
        g0, _ = COLS["final_norm"]
        for tc in range(NTC):
            ts = slice(tc * TC, (tc + 1) * TC)
            pt, pb = self.psum.get()
            for c in range(NCH):
                sq, sqb = self.sq_ring.get()
                S.op("act", lambda e: e.activation(sq[:], self.X[:, c, ts], AF.Square),
                     reads=[self.XB[c][tc]], writes=[sqb])
                S.op("pe", lambda e: e.matmul(pt[:], self.ones_f[:], sq[:], start=(c == 0), stop=(c == NCH - 1)),
                     reads=[sqb, self.constb], writes=[pb])
            rs, rsb = self.rstd_ring.get()
            S.op("act", lambda e: e.activation(rs[:], pt[:], AF.Sqrt, bias=self.eps_t[:], scale=1.0 / D),
                 reads=[pb, self.constb], writes=[rsb])
            S.op("dve", lambda e: e.reciprocal(rs[:], rs[:]), reads=[rsb], writes=[rsb])
            for c in range(NCH):
                S.op("dve", lambda e: e.scalar_tensor_tensor(
                    self.X[:, c, ts], self.X[:, c, ts], self.cols[:, g0 + c:g0 + c + 1], rs[:],
                    ALU.mult, ALU.mult),
                    reads=[self.XB[c][tc], rsb, self.constb], writes=[self.XB[c][tc]])
                S.dma(outT[c * 128:(c + 1) * 128, ts], self.X[:, c, ts], reads=[self.XB[c][tc]])

    def dump_x(self, name):
        o = self.dout(name, [D, S_LEN])
        for c in range(NCH):
            for tc in range(NTC):
                ts = slice(tc * TC, (tc + 1) * TC)
                self.S.dma(o[c * 128:(c + 1) * 128, ts], self.X[:, c, ts], reads=[self.XB[c][tc]])

    def build(self, stop_after=None):
        nc = self.nc
        dbg = self.debug
        xT = self.din("xT", [D, S_LEN])
        cols_d = self.din("cols", [128, NCOLS])
        f1g = self.din("ffn1_w_gate", [D, DFF]); f1u = self.din("ffn1_w_up", [D, DFF]); f1d = self.din("ffn1_w_down", [DFF, D])
        f2g = self.din("ffn2_w_gate", [D, DFF]); f2u = self.din("ffn2_w_up", [D, DFF]); f2d = self.din("ffn2_w_down", [DFF, D])
        memT = self.din("memT", [D, 256])
        mem_wk = self.din("mem_w_k", [D, 512]); mem_wv = self.din("mem_w_v", [D, 512])
        w_qm = self.din("w_qm", [D, 512])
        w_gb = self.din("w_gb", [D, 3 * D])
        w_br = [self.din(n, [512, D]) for n in ("w_br_rwkv", "w_br_nsa_p", "w_br_mem")]
        w_out = self.din("w_out", [D, D])
        w_rwkv = self.din("w_rwkv", [D, 1792])
        w2_d = self.din("rwkv_w2", [64, 512]); a2_d = self.din("rwkv_a2", [64, 512]); g2_d = self.din("rwkv_g2", [128, 512])
        gng_d = self.din("gng_rep", [128, 512]); gnb_d = self.din("gnb_rep", [128, 512])
        ident_d = self.din("ident", [128, 128])
        nd = {}
        nd["w_qn"] = self.din("w_qn", [D, 512]); nd["w_gn"] = self.din("w_gn", [D, 24]); nd["w_kvn"] = self.din("w_kvn", [D, 768])
        nd["shcf"] = self.din("shcf", [32, 247]); nd["efull"] = self.din("efull", [32, S_LEN]); nd["ov"] = self.din("ov", [127, 32])
        nd["abf"] = self.din("abf", [128, 2, 64]); nd["selg"] = self.din("selg", [24, 12, 128])
        nd["t31"] = self.din("t31", [128, 2, 512])
        nd["bmg"] = [self.din("bmg%d" % k, [128, 2, 512]) for k in range(3)]
        nd["msk"] = [self.din("msk%d" % k, [128, 128]) for k in range(3)]
        nd["bvcg"] = self.din("bvcg", [32, 2, 512]); nd["mskc"] = self.din("mskc", [32, 128])
        nd["cmp_w1"] = [self.din("cmp_k_w1", [2048, 256]), self.din("cmp_v_w1", [2048, 256])]
        nd["cmp_w2"] = [self.din("cmp_k_w2", [256, 64]), self.din("cmp_v_w2", [256, 64])]
        nd["cmp_peT"] = [self.din("cmp_pe_kT", [64, 32]), self.din("cmp_pe_vT", [64, 32])]
        outT = self.dout("outT", [D, S_LEN])
        with ExitStack() as st:
            self.st = st
            S = self.S = Sched(nc, st)
            self.X = self.sb("X", [128, NCH, S_LEN], F32)
            self.XB = [[Buf() for _ in range(NTC)] for _ in range(NCH)]
            self.rstd_ring = Ring([self.sb("RSTD%d" % i, [128, TC], F32) for i in range(2)])
            self.cols = self.sb("cols", [128, NCOLS], F32)
            self.ones_f = self.sb("ones_f", [128, 128], F32)
            self.ones_b = self.sb("ones_b", [128, 128], BF16)
            self.eps_t = self.sb("eps_t", [128, 1], F32)
            self.gneps_t = self.sb("gneps_t", [128, 1], F32)
            self.ident_f = self.sb("ident_f", [128, 128], F32)
            self.constb = Buf("const")
            self.banks = [self.ps("ps%d" % i, [128, 512]) for i in range(8)]
            self.bankb = [Buf() for _ in range(8)]
            self.psum = Ring(self.banks, self.bankb)
            self.sq_ring = Ring([self.sb("sq%d" % i, [128, TC], F32) for i in range(2)])
            S.dma(self.cols[:], cols_d, writes=[self.constb])
            S.op("dve", lambda e: e.memset(self.ones_f[:], 1.0), reads=[self.constb], writes=[self.constb])
            S.op("dve", lambda e: e.memset(self.ones_b[:], 1.0), reads=[self.constb], writes=[self.constb])
            S.op("dve", lambda e: e.memset(self.eps_t[:], EPS), reads=[self.constb], writes=[self.constb])
            S.op("dve", lambda e: e.memset(self.gneps_t[:], 64e-5), reads=[self.constb], writes=[self.constb])
            S.dma(self.ident_f[:], ident_d, reads=[self.constb], writes=[self.constb])
            for c in range(NCH):
                for tc in range(NTC):
                    ts = slice(tc * TC, (tc + 1) * TC)
                    S.dma(self.X[:, c, ts], xT[c * 128:(c + 1) * 128, ts], writes=[self.XB[c][tc]])

            def ffn_phase(wg, wu, wd, gname):
                with ExitStack() as st2:
                    self.st = st2
                    self.HN = self.sb("HN", [128, NCH, S_LEN], BF16)
                    self.HNB = [[Buf() for _ in range(NTC)] for _ in range(NCH)]
                    self.WG = [self.sb("WG%d" % i, [128, NCH, 512], BF16) for i in range(2)]
                    self.WU = [self.sb("WU%d" % i, [128, NCH, 512], BF16) for i in range(2)]
                    self.WD = [self.sb("WD%d" % i, [128, 4, D], BF16) for i in range(2)]
                    self.WGB = [Buf() for _ in range(2)]; self.WUB = [Buf() for _ in range(2)]; self.WDB = [Buf() for _ in range(2)]
                    self.a_ring = Ring([self.sb("a%d" % i, [128, 4, TC], BF16) for i in range(2)])
                    self.sg_ring = Ring([self.sb("sg%d" % i, [128, TC], F32) for i in range(2)])
                    self.ffn(wg, wu, wd, gname)
                    S.full_barrier()
                    self.st = st

            if "noffn1" not in dbg:
                ffn_phase(f1g, f1u, f1d, "ffn1_norm")
            if "x1" in dbg:
                self.dump_x("dbg_x1")
            if stop_after != "ffn1":
                with ExitStack() as st3:
                    self.st = st3
                    self.HN = self.sb("HN", [128, NCH, S_LEN], BF16)
                    self.HNB = [[Buf() for _ in range(NTC)] for _ in range(NCH)]
                    Yt = self.sb("Yt", [128, 4, S_LEN], BF16)
                    YBt = [Buf() for _ in range(NTC)]
                    self.Y = [Yt, Yt, Yt]
                    self.YB = [YBt, YBt, YBt]
                    self.rmsnorm_to_hn("mix_norm")
                    if "norwkv" not in dbg:
                        self.rwkv_branch(w_rwkv, w2_d, a2_d, g2_d, gng_d, gnb_d)
                    else:
                        S.op("pool", lambda e: e.memset(Yt[:], 0.0), writes=YBt)
                    if "y_rwkv" in dbg:
                        self.dump_feat("dbg_y_rwkv", Yt, 4, YBt)
                    self.M = self.sb("M", [128, NCH, S_LEN], BF16)
                    self.MB = [[Buf() for _ in range(NTC)] for _ in range(NCH)]
                    do_merge = stop_after != "mix"
                    if do_merge:
                        self.fold(0, w_gb, w_br[0], True)
                    if "nomem" not in dbg:
                        self.mem_branch(memT, mem_wk, mem_wv, w_qm)
                    else:
                        S.op("pool", lambda e: e.memset(Yt[:], 0.0), writes=YBt)
                    if "y_mem" in dbg:
                        self.dump_feat("dbg_y_mem", Yt, 4, YBt)
                    if do_merge:
                        self.fold(2, w_gb, w_br[2], False)
                    if "nonsa" not in dbg:
                        self.nsa_branch(nd)
                    else:
                        S.op("pool", lambda e: e.memset(Yt[:], 0.0), writes=YBt)
                    if "y_nsa" in dbg:
                        self.dump_feat("dbg_y_nsa_p", Yt, 4, YBt)
                    if do_merge:
                        self.fold(1, w_gb, w_br[1], False)
                        self.outproj(w_out)
                    S.full_barrier()
                    self.st = st
                if "x2" in dbg:
                    self.dump_x("dbg_x2")
                if stop_after not in ("mix", "merge"):
                    ffn_phase(f2g, f2u, f2d, "ffn2_norm")
            self.final_norm_out(outT)
            S.wait_all_dma("sp")
            S.wait_all_dma("pool")
        return nc


NSA_PERM = np.concatenate([np.concatenate([np.arange(64 * j, 64 * j + 64), np.arange(64 * (4 + j), 64 * (4 + j) + 64)])
                           for j in range(4)])


def _t5_bucket_np(dist):
    n = np.maximum(dist, 0)
    nf = np.maximum(n, 1).astype(np.float32)
    large = 16 + (np.log(nf / np.float32(16)) / np.float32(math.log(128 / 16)) * np.float32(16)).astype(np.int32)
    large = np.minimum(large, 31)
    return np.where(n < 16, n, large)


def _nsa_consts(rel_bias):
    rb = np.asarray(rel_bias, np.float32)
    c = np.arange(128)[:, None]; p = np.arange(128)[None, :]
    out = {}
    hd = np.arange(8).reshape(2, 4)
    dists = [p - c, 128 + p - c, 512 + p - c]
    valid = [p >= c, np.ones((128, 128), bool), c > p]
    for k in range(3):
        bk = _t5_bucket_np(dists[k])
        g = rb[bk[:, None, None, :], hd[None, :, :, None]]
        out["bmg%d" % k] = np.ascontiguousarray(g.reshape(128, 2, 512))
        out["msk%d" % k] = np.where(valid[k], 0.0, -30000.0).astype(np.float32)
    out["t31"] = np.ascontiguousarray(np.broadcast_to(rb[31][hd][None, :, :, None], (128, 2, 4, 128)).reshape(128, 2, 512))
    m = np.arange(32)[:, None]
    dc = p - 16 * (m - 8) - 31
    bk = _t5_bucket_np(dc)
    g = rb[bk[:, None, None, :], hd[None, :, :, None]]
    out["bvcg"] = np.ascontiguousarray(g.reshape(32, 2, 512))
    mk = np.where((dc >= 0) & (m < 16), 0.0, -30000.0).astype(np.float32)
    mk[17:] = 0.0
    out["mskc"] = mk
    shcf = np.zeros((32, 247), np.float32)
    for x in range(247):
        r = x - 112
        if 0 <= r < 16:
            shcf[r, x] = 1.0
        elif r >= 16:
            shcf[16, x] = 1.0
    out["shcf"] = shcf
    ef = np.zeros((32, S_LEN), np.float32)
    ef[np.arange(S_LEN) // 64, np.arange(S_LEN)] = 1.0
    out["efull"] = ef
    ic = np.arange(127)[:, None]; jb = np.arange(32)[None, :]
    out["ov"] = ((ic * 16 <= jb * 64 + 63) & (ic * 16 + 31 >= jb * 64)).astype(np.float32)
    ab = np.zeros((128, 2, 64), np.float32)
    for pp in range(128):
        curr = 1 if pp >= 64 else 0
        for mm in range(64):
            jr = mm - 32
            if jr <= curr - 2:
                ab[pp, 0, mm] = 1.0
            if jr in (curr, curr - 1):
                ab[pp, 1, mm] = 1e6
    out["abf"] = ab
    selg = np.zeros((24, 12, 128), np.float32)
    for br in range(3):
        for j in range(4):
            for mm in range(128):
                selg[br * 8 + (mm // 64) * 4 + j, br * 4 + j, mm] = 1.0
    out["selg"] = selg
    return out


def prep_inputs(inputs, b):
    m = {}
    m["xT"] = np.ascontiguousarray(inputs["x"][b].T)
    cols = np.zeros((128, NCOLS), np.float32)
    for n in ("ffn1_norm", "mix_norm", "ffn2_norm", "final_norm", "mem_norm"):
        c0, k = COLS[n]
        cols[:, c0:c0 + k] = _colpack(np.asarray(inputs[n]).reshape(-1))
    for n, src in (("mu", "rwkv_mu"), ("w0", "rwkv_w0"), ("a0", "rwkv_a0"), ("k_k", "rwkv_k_k"), ("k_a", "rwkv_k_a"), ("r_k", "rwkv_r_k")):
        c0, k = COLS[n]
        cols[:, c0:c0 + k] = _colpack(np.asarray(inputs[src]).reshape(-1))
    m["cols"] = cols
    m["w_rwkv"] = np.ascontiguousarray(np.asarray(inputs["w_in"])[0][:, 0:1792])
    m["rwkv_w2"] = np.ascontiguousarray(np.asarray(inputs["rwkv_w2"])[0])
    m["rwkv_a2"] = np.ascontiguousarray(np.asarray(inputs["rwkv_a2"])[0])
    m["rwkv_g2"] = np.ascontiguousarray(np.asarray(inputs["rwkv_g2"])[0])
    m["gng_rep"] = np.ascontiguousarray(np.broadcast_to(np.asarray(inputs["rwkv_gn_gain"]).reshape(1, 512), (128, 512)))
    m["gnb_rep"] = np.ascontiguousarray(np.broadcast_to(np.asarray(inputs["rwkv_gn_bias"]).reshape(1, 512), (128, 512)))
    m["ident"] = np.eye(128, dtype=np.float32)
    w_in_ = np.asarray(inputs["w_in"])[0]
    m["w_qn"] = np.ascontiguousarray(w_in_[:, 1792:2304][:, NSA_PERM])
    m["w_kvn"] = np.ascontiguousarray(w_in_[:, 2304:3072])
    m["w_gn"] = np.ascontiguousarray(w_in_[:, 3072:3096])
    m.update(_nsa_consts(inputs["rel_bias"]))
    for n in ("cmp_k_w1", "cmp_v_w1", "cmp_k_w2", "cmp_v_w2"):
        m[n] = np.ascontiguousarray(np.asarray(inputs[n])[0])
    m["cmp_pe_kT"] = np.ascontiguousarray(np.asarray(inputs["cmp_pe_k"])[0].T)
    m["cmp_pe_vT"] = np.ascontiguousarray(np.asarray(inputs["cmp_pe_v"])[0].T)
    for n in ("ffn1_w_gate", "ffn1_w_up", "ffn1_w_down", "ffn2_w_gate", "ffn2_w_up", "ffn2_w_down",
              "mem_w_k", "mem_w_v", "w_br_rwkv", "w_br_mem", "w_out"):
        m[n] = np.ascontiguousarray(np.asarray(inputs[n])[0])
    m["memT"] = np.ascontiguousarray(inputs["mem"][b].T)
    w_in = np.asarray(inputs["w_in"])[0]
    m["w_qm"] = np.ascontiguousarray(w_in[:, 3096:3608])
    m["w_gb"] = np.ascontiguousarray(w_in[:, 3608:6680])
    m["w_br_nsa_p"] = np.ascontiguousarray(np.asarray(inputs["w_br_nsa"])[0][NSA_PERM, :])
    return m


_CACHE = {}


def kernel(**inputs):
    inputs = {k: np.asarray(v) for k, v in inputs.items()}
    if "nc" not in _CACHE:
        _CACHE["nc"] = Builder().build()
    nc = _CACHE["nc"]
    n = 8
    in_maps = [prep_inputs(inputs, b) for b in range(n)]
    res = run_bass_kernel_spmd(nc, in_maps, core_ids=list(range(n)))
    out = np.stack([np.ascontiguousarray(r["outT"].T) for r in res.results], axis=0)
    return out.astype(np.float32)
```

```python
import math
from contextlib import ExitStack
import numpy as np
import concourse.bass as bass
import concourse.mybir as mybir
from concourse.bass_utils import run_bass_kernel_spmd

F32 = mybir.dt.float32
BF16 = mybir.dt.bfloat16
AF = mybir.ActivationFunctionType
ALU = mybir.AluOpType
AX = mybir.AxisListType

D = 1024
S_LEN = 2048
DFF = 2816
NCH = 8
TC = 512
NTC = S_LEN // TC
EPS = 1e-6


class Buf:
    __slots__ = ("name", "last_w", "readers")

    def __init__(self, name=""):
        self.name = name
        self.last_w = None
        self.readers = []


class Sched:
    ENG = ("pe", "act", "dve", "pool", "sp")

    def __init__(self, nc, stack, n_dma_sems=16):
        self.nc = nc
        self.eng = {"pe": nc.tensor, "act": nc.scalar, "dve": nc.vector,
                    "pool": nc.gpsimd, "sp": nc.sync}
        self.sem = {}
        for e in ("pe", "act", "dve", "pool"):
            self.sem[e] = stack.enter_context(nc.semaphore("s_" + e))
        self.cnt = {e: 0 for e in ("pe", "act", "dve", "pool")}
        nq = {"sp": 28, "pool": 28, "act": 8}
        self.dsem = []
        self.qsems = {}
        for q, n in nq.items():
            self.qsems[q] = list(range(len(self.dsem), len(self.dsem) + n))
            for i in range(n):
                self.dsem.append(stack.enter_context(nc.semaphore("d%s%d" % (q, i))))
        self.dcnt = [0] * len(self.dsem)
        self.dnext = {q: 0 for q in nq}
        self.waited = {e: {} for e in self.ENG}
        self.n_ops = 0
        self.n_waits = 0

    def _semobj(self, key):
        return self.sem[key] if isinstance(key, str) else self.dsem[key]

    def _need(self, engine, toks):
        best = {}
        for t in toks:
            if t is None:
                continue
            key, val = t
            if best.get(key, 0) < val:
                best[key] = val
        w = self.waited[engine]
        for key, val in best.items():
            if w.get(key, 0) >= val:
                continue
            self.eng[engine].wait_ge(self._semobj(key), val)
            w[key] = val
            self.n_waits += 1

    @staticmethod
    def _deps(reads, writes):
        toks = []
        for b in reads:
            toks.append(b.last_w)
        for b in writes:
            toks.append(b.last_w)
            toks.extend(b.readers)
        return toks

    @staticmethod
    def _commit(tok, reads, writes):
        for b in reads:
            b.readers.append(tok)
            if len(b.readers) > 48:
                best = {}
                for k, v in b.readers:
                    if best.get(k, 0) < v:
                        best[k] = v
                b.readers = list(best.items())
        for b in writes:
            b.last_w = tok
            b.readers = []

    def op(self, engine, fn, reads=(), writes=()):
        self._need(engine, self._deps(reads, writes))
        ins = fn(self.eng[engine])
        self.cnt[engine] += 1
        ins.then_inc(self.sem[engine], 1)
        tok = (engine, self.cnt[engine])
        self._commit(tok, reads, writes)
        self.n_ops += 1
        return tok

    def dma(self, out_ap, in_ap, reads=(), writes=(), queue="sp", **kw):
        pool = self.qsems[queue]
        i = pool[self.dnext[queue]]
        self.dnext[queue] = (self.dnext[queue] + 1) % len(pool)
        prev = [(i, self.dcnt[i])] if self.dcnt[i] else []
        self._need(queue, self._deps(reads, writes) + prev)
        ins = self.eng[queue].dma_start(out=out_ap, in_=in_ap, **kw)
        self.dcnt[i] += 16
        ins.then_inc(self.dsem[i], 16)
        tok = (i, self.dcnt[i])
        self._commit(tok, reads, writes)
        self.n_ops += 1
        return tok

    def barrier(self, bufs):
        toks = []
        for b in bufs:
            toks.append(b.last_w)
            toks.extend(b.readers)
        for e in self.ENG:
            self._need(e, toks)

    def full_barrier(self):
        toks = [(e, self.cnt[e]) for e in ("pe", "act", "dve", "pool") if self.cnt[e]]
        toks += [(i, self.dcnt[i]) for i in range(len(self.dsem)) if self.dcnt[i]]
        for e in self.ENG:
            self._need(e, toks)

    def wait_all_dma(self, engine="sp"):
        for i in range(len(self.dsem)):
            if self.dcnt[i]:
                self.eng[engine].wait_ge(self.dsem[i], self.dcnt[i])


class Ring:
    def __init__(self, tiles, bufs=None):
        self.tiles = tiles
        self.bufs = bufs if bufs is not None else [Buf() for _ in tiles]
        self.i = 0

    def get(self):
        t, b = self.tiles[self.i], self.bufs[self.i]
        self.i = (self.i + 1) % len(self.tiles)
        return t, b

    def get_pair_idx(self):
        if self.i % 2:
            self.i = (self.i + 1) % len(self.tiles)
        k = self.i
        self.i = (self.i + 2) % len(self.tiles)
        return k, self.bufs[k], self.bufs[k + 1]


COLS = {}
_c = 0
for _n, _k in (("ffn1_norm", 8), ("mix_norm", 8), ("ffn2_norm", 8), ("final_norm", 8),
               ("mem_norm", 8), ("mu", 14), ("w0", 4), ("a0", 4), ("k_k", 4), ("k_a", 4), ("r_k", 4), ("gn_g", 4), ("gn_b", 4)):
    COLS[_n] = (_c, _k)
    _c += _k
NCOLS = _c


def _colpack(v):
    v = np.asarray(v, np.float32).reshape(-1, 128)
    return np.ascontiguousarray(v.T)


class Builder:
    def __init__(self, debug=()):
        self.debug = set(debug)
        self._rk_stage = 99
        self._rk_tiles = S_LEN // 128
        for d_ in self.debug:
            if d_.startswith("rkstage"):
                self._rk_stage = int(d_[7:])
            if d_.startswith("rktiles"):
                self._rk_tiles = int(d_[7:])
        self.nc = bass.Bass("TRN2", target_bir_lowering=False)
        self.dram_in = {}
        self.dram_out = {}

    def din(self, name, shape, dt=F32):
        t = self.nc.dram_tensor(name, list(shape), dt, kind="ExternalInput").ap()
        self.dram_in[name] = t
        return t

    def dout(self, name, shape, dt=F32):
        t = self.nc.dram_tensor(name, list(shape), dt, kind="ExternalOutput").ap()
        self.dram_out[name] = t
        return t

    def sb(self, name, shape, dt):
        self._uid = getattr(self, "_uid", 0) + 1
        return self.st.enter_context(self.nc.sbuf_tensor("sb%d_%s" % (self._uid, name), list(shape), dt))

    def ps(self, name, shape, dt=F32):
        return self.st.enter_context(self.nc.psum_tensor("ps_" + name, list(shape), dt))

    def _norm_rings_open(self):
        self._nst_old = self.st
        self._nst = ExitStack()
        self.st = self._nst
        self.sq_ring = Ring([self.sb("sq%d" % i, [128, TC], F32) for i in range(2)])
        self.rstd_ring = Ring([self.sb("RSTD%d" % i, [128, TC], F32) for i in range(2)])
        self.st = self._nst_old

    def _norm_rings_close(self):
        self.S.full_barrier()
        self._nst.close()

    def rmsnorm_to_hn(self, gname):
        S = self.S
        g0, _ = COLS[gname]
        self._norm_rings_open()
        for tc in range(NTC):
            ts = slice(tc * TC, (tc + 1) * TC)
            pt, pb = self.psum.get()
            for c in range(NCH):
                sq, sqb = self.sq_ring.get()
                S.op("act", lambda e: e.activation(sq[:], self.X[:, c, ts], AF.Square),
                     reads=[self.XB[c][tc]], writes=[sqb])
                S.op("pe", lambda e: e.matmul(pt[:], self.ones_f[:], sq[:], start=(c == 0), stop=(c == NCH - 1)),
                     reads=[sqb, self.constb], writes=[pb])
            rs, rsb = self.rstd_ring.get()
            S.op("act", lambda e: e.activation(rs[:], pt[:], AF.Sqrt, bias=self.eps_t[:], scale=1.0 / D),
                 reads=[pb, self.constb], writes=[rsb])
            S.op("dve", lambda e: e.reciprocal(rs[:], rs[:]), reads=[rsb], writes=[rsb])
            for c in range(NCH):
                S.op("dve", lambda e: e.scalar_tensor_tensor(
                    self.HN[:, c, ts], self.X[:, c, ts], self.cols[:, g0 + c:g0 + c + 1], rs[:],
                    ALU.mult, ALU.mult),
                    reads=[self.XB[c][tc], rsb, self.constb], writes=[self.HNB[c][tc]])

        self._norm_rings_close()

    def ffn(self, wg, wu, wd, gname):
        S = self.S
        self.rmsnorm_to_hn(gname)
        groups = [(i, min(4, 22 - i)) for i in range(0, 22, 4)]

        def load(gi):
            f0, nf = groups[gi]
            slot = gi % 2
            S.dma(self.WG[slot][:, :, 0:nf * 128],
                  wg[:, f0 * 128:(f0 + nf) * 128].rearrange("(k p) n -> p k n", p=128),
                  writes=[self.WGB[slot]], queue="pool")
            S.dma(self.WU[slot][:, :, 0:nf * 128],
                  wu[:, f0 * 128:(f0 + nf) * 128].rearrange("(k p) n -> p k n", p=128),
                  writes=[self.WUB[slot]], queue="pool")
            S.dma(self.WD[slot][:, 0:nf, :],
                  wd[f0 * 128:(f0 + nf) * 128, :].rearrange("(f p) n -> p f n", p=128),
                  writes=[self.WDB[slot]], queue="pool")

        load(0)
        for gi, (f0, nf) in enumerate(groups):
            if gi + 1 < len(groups):
                load(gi + 1)
            slot = gi % 2
            WG, WU, WD = self.WG[slot], self.WU[slot], self.WD[slot]
            for tc in range(NTC):
                ts = slice(tc * TC, (tc + 1) * TC)
                hreads = [self.HNB[c][tc] for c in range(NCH)]
                a_t, a_b = self.a_ring.get()
                for f in range(nf):
                    pg, pgb = self.psum.get()
                    pu, pub = self.psum.get()

                    def mm_g(e):
                        for k in range(NCH):
                            ins = e.matmul(pg[:], WG[:, k, f * 128:(f + 1) * 128], self.HN[:, k, ts],
                                           start=(k == 0), stop=(k == NCH - 1))
                        return ins

                    def mm_u(e):
                        for k in range(NCH):
                            ins = e.matmul(pu[:], WU[:, k, f * 128:(f + 1) * 128], self.HN[:, k, ts],
                                           start=(k == 0), stop=(k == NCH - 1))
                        return ins
                    S.op("pe", mm_g, reads=hreads + [self.WGB[slot]], writes=[pgb])
                    S.op("pe", mm_u, reads=hreads + [self.WUB[slot]], writes=[pub])
                    sg, sgb = self.sg_ring.get()
                    S.op("act", lambda e: e.activation(sg[:], pg[:], AF.Silu), reads=[pgb], writes=[sgb])
                    S.op("dve", lambda e: e.tensor_tensor(a_t[:, f, :], sg[:], pu[:], ALU.mult),
                         reads=[sgb, pub], writes=[a_b])
                for dc in range(NCH):
                    po, pob = self.psum.get()

                    def mm_d(e):
                        for f in range(nf):
                            ins = e.matmul(po[:], WD[:, f, dc * 128:(dc + 1) * 128], a_t[:, f, :],
                                           start=(f == 0), stop=(f == nf - 1))
                        return ins
                    S.op("pe", mm_d, reads=[a_b, self.WDB[slot]], writes=[pob])
                    S.op("dve", lambda e: e.scalar_tensor_tensor(
                        self.X[:, dc, ts], po[:], 0.5, self.X[:, dc, ts], ALU.mult, ALU.add),
                        reads=[pob, self.XB[dc][tc]], writes=[self.XB[dc][tc]])


    def load_w(self, tile_ap, dram_ap, buf):
        self.S.dma(tile_ap, dram_ap.rearrange("(k p) n -> p k n", p=128), writes=[buf], queue="pool")

    def dump_feat(self, name, tile, nchunks, buf_list):
        o = self.dout(name, [nchunks * 128, S_LEN])
        for c in range(nchunks):
            self.S.dma(o[c * 128:(c + 1) * 128, :], tile[:, c, :], reads=buf_list, queue="pool")


    def rwkv_branch_seq(self, w_rwkv, w2_d, a2_d, g2_d, gng_d, gnb_d):
        S = self.S
        CN = COLS
        NT = S_LEN // 128
        with ExitStack() as st4:
            old, self.st = self.st, st4
            WR = self.sb("WR", [128, NCH, 1792], BF16); WRB = Buf()
            W2 = self.sb("W2A2", [128, 512], F32); A2 = W2; G2 = self.sb("G2", [128, 512], F32)
            GNG = self.sb("GNG", [128, 512], F32); GNB = self.sb("GNB", [128, 512], F32)
            BO = self.sb("BO", [128, 128], F32); BOb = self.sb("BOb", [128, 128], BF16)
            ID2 = self.sb("ID2", [128, 64], BF16)
            OMK = self.sb("OMK", [128, 4], F32)
            cb = Buf()
            self.load_w(WR[:], w_rwkv, WRB)
            S.dma(W2[0:64, :], w2_d, writes=[cb]); S.dma(A2[64:128, :], a2_d, writes=[cb]); S.dma(G2[:], g2_d, writes=[cb])
            S.dma(GNG[:], gng_d, writes=[cb]); S.dma(GNB[:], gnb_d, writes=[cb])
            S.op("dve", lambda e: e.memset(BO[:], 0.0), reads=[cb], writes=[cb])
            S.op("dve", lambda e: e.memset(BO[0:64, 0:64], 1.0), reads=[cb], writes=[cb])
            S.op("dve", lambda e: e.memset(BO[64:128, 64:128], 1.0), reads=[cb], writes=[cb])
            S.op("dve", lambda e: e.tensor_copy(BOb[:], BO[:]), reads=[cb], writes=[cb])
            S.op("dve", lambda e: e.tensor_copy(ID2[0:64, :], self.ident_f[0:64, 0:64]), reads=[cb, self.constb], writes=[cb])
            S.op("dve", lambda e: e.tensor_copy(ID2[64:128, :], self.ident_f[64:128, 64:128]), reads=[cb, self.constb], writes=[cb])
            ka0 = CN["k_a"][0]
            S.op("dve", lambda e: e.tensor_scalar(OMK[:], self.cols[:, ka0:ka0 + 4], -1.0, 1.0, ALU.mult, ALU.add),
                 reads=[cb, self.constb], writes=[cb])
            P32 = self.sb("P32", [128, 14, 129], F32); P32B = Buf()
            DD = self.sb("DD", [128, 128], F32); DDB = Buf()
            CAR = self.sb("CAR", [128, 14, 1], F32)
            PL = P32[:, :, 1:129]; PLB = P32B
            TW = self.sb("TW", [64, 128], F32); SGg = self.sb("SGg", [128, 128], F32)
            WD = self.sb("WD", [128, 4, 128], F32); SIG = WD
            A32 = self.sb("A32", [128, 4, 128], F32)
            KK = self.sb("KK", [128, 4, 128], F32); SQ = self.sb("SQ", [128, 4, 128], F32)
            KKN = self.sb("KKN", [128, 4, 128], F32); NB = self.sb("NB", [128, 4, 128], F32)
            KM = self.sb("KM", [128, 4, 128], F32); BON = self.sb("BON", [128, 4, 128], F32)
            RM = self.sb("RM", [128, 4, 128, 2], BF16)
            VDr = Ring([self.sb("VD%d" % i, [128, 4, 64], BF16) for i in range(2)])
            H = self.sb("H", [128, 4, 64], F32); Hb = self.sb("Hb", [128, 4, 64], BF16); HK = self.sb("HK", [128, 4, 64], BF16)
            T1 = self.sb("T1", [128, 4, 64], F32); T2r = Ring([self.sb("T2_%d" % i, [128, 4, 64], F32) for i in range(2)])
            YST = [self.sb("YST%d" % i, [2, 4, 256], F32) for i in range(2)]; YSTB = [Buf(), Buf()]
            YTOK = A32[:].rearrange("p c t -> p (c t)").rearrange("p (c h v) -> p c h v", c=4, h=2); YTOKB = Buf()
            YC = KKN[:].rearrange("p c t -> p (c t)").rearrange("p (a v) -> p a v", a=8)
            ST8 = self.sb("ST8", [128, 8], F32); ST8b = self.sb("ST8b", [128, 8], F32)
            YF = SQ
            db = Buf(); hb = Buf(); hbb = Buf(); hkb = Buf(); t1b = Buf(); vrb = Buf(); vtb = Buf(); rmb = Buf(); yb = Buf()
            S.op("pool", lambda e: e.memset(P32[:], 0.0), writes=[P32B])
            S.op("pool", lambda e: e.memset(RM[:], 0.0), writes=[rmb])
            S.op("pool", lambda e: e.memset(H[:], 0.0), writes=[hb])
            mu0 = CN["mu"][0]; w00 = CN["w0"][0]; a00 = CN["a0"][0]; kk0 = CN["k_k"][0]; rk0 = CN["r_k"][0]
            ident = self.ident_f
            for i in range(NT):
                t0 = i * 128
                tcix = t0 // TC
                tsl = slice(t0, t0 + 128)
                hreads = [self.HNB[c][tcix] for c in range(NCH)]
                for cg in range(4):
                    c0 = cg * 4
                    n = min(4, 14 - c0)
                    p, pb = self.psum.get()

                    def mm(e):
                        for cc in range(n):
                            for k in range(NCH):
                                ins = e.matmul(p[:, cc * 128:(cc + 1) * 128], WR[:, k, (c0 + cc) * 128:(c0 + cc + 1) * 128],
                                               self.HN[:, k, tsl], start=(k == 0), stop=(k == NCH - 1))
                        return ins
                    S.op("pe", mm, reads=hreads + [WRB], writes=[pb])
                    S.op("act", lambda e: e.copy(P32[:, c0:c0 + n, 1:129], p[:, 0:n * 128].rearrange("p (c t) -> p c t", c=n)),
                         reads=[pb], writes=[P32B])
                S.op("dve", lambda e: e.tensor_copy(CAR[:], P32[:, :, 128:129]), reads=[P32B], writes=[DDB])
                for c in range(14):
                    S.op("dve", lambda e: e.tensor_tensor(DD[:], P32[:, c, 0:128], P32[:, c, 1:129], ALU.subtract), reads=[P32B, DDB], writes=[DDB])
                    S.op("dve", lambda e: e.scalar_tensor_tensor(P32[:, c, 1:129], DD[:], self.cols[:, mu0 + c:mu0 + c + 1], P32[:, c, 1:129],
                                                                 ALU.mult, ALU.add), reads=[DDB, P32B, self.constb], writes=[P32B])
                S.op("dve", lambda e: e.tensor_copy(P32[:, :, 0:1], CAR[:]), reads=[P32B, DDB], writes=[P32B])
                S.op("act", lambda e: e.activation(TW[:], PL[0:64, 12, :], AF.Tanh), reads=[PLB], writes=[db])
                S.op("act", lambda e: e.activation(SGg[:], PL[:, 13, :], AF.Sigmoid), reads=[PLB], writes=[db])
                pz, pzb = self.psum.get(); pa, pab = self.psum.get()

                def mmz(e):
                    for fc in range(4):
                        ins = e.matmul(pz[:, fc * 128:(fc + 1) * 128], W2[0:64, fc * 128:(fc + 1) * 128], TW[:], start=True, stop=True)
                    return ins

                def mma(e):
                    for fc in range(4):
                        ins = e.matmul(pa[:, fc * 128:(fc + 1) * 128], A2[64:128, fc * 128:(fc + 1) * 128], PL[64:128, 12, :], start=True, stop=True)
                    return ins

                S.op("pe", mmz, reads=[db, cb], writes=[pzb])
                S.op("pe", mma, reads=[PLB, cb], writes=[pab])
                for fc in range(4):
                    S.op("act", lambda e: e.activation(SIG[:, fc, :], pz[:, fc * 128:(fc + 1) * 128], AF.Sigmoid,
                                                       bias=self.cols[:, w00 + fc:w00 + fc + 1]), reads=[pzb, self.constb], writes=[db])
                    S.op("act", lambda e: e.activation(A32[:, fc, :], pa[:, fc * 128:(fc + 1) * 128], AF.Sigmoid,
                                                       bias=self.cols[:, a00 + fc:a00 + fc + 1]), reads=[pab, self.constb], writes=[db, YTOKB])
                S.op("act", lambda e: e.activation(WD[:], SIG[:], AF.Exp, scale=-0.6065306597126334), reads=[db], writes=[db])
                for fc in range(4):
                    S.op("dve", lambda e: e.tensor_scalar(KK[:, fc, :], PL[:, 4 + fc, :], self.cols[:, kk0 + fc:kk0 + fc + 1], None, ALU.mult),
                         reads=[PLB, self.constb], writes=[db])
                S.op("dve", lambda e: e.tensor_tensor(SQ[:], KK[:], KK[:], ALU.mult), reads=[db], writes=[db])
                pss, pssb = self.psum.get()
                S.op("pe", lambda e: e.matmul(pss[:], BO[:], SQ[:].rearrange("p c t -> p (c t)"), start=True, stop=True), reads=[db, cb], writes=[pssb])
                S.op("act", lambda e: e.activation(SQ[:], pss[:].rearrange("p (c t) -> p c t", c=4), AF.Sqrt), reads=[pssb, db], writes=[db])
                S.op("dve", lambda e: e.tensor_scalar(SQ[:], SQ[:], 1e-12, None, ALU.max), reads=[db], writes=[db])
                S.op("dve", lambda e: e.reciprocal(SQ[:], SQ[:]), reads=[db], writes=[db])
                S.op("dve", lambda e: e.tensor_tensor(KKN[:], KK[:], SQ[:], ALU.mult), reads=[db], writes=[db, yb])
                S.op("dve", lambda e: e.scalar_tensor_tensor(NB[:], KKN[:], -1.0, A32[:], ALU.mult, ALU.mult), reads=[db], writes=[db])
                for fc in range(4):
                    S.op("dve", lambda e: e.tensor_scalar(KK[:, fc, :], A32[:, fc, :], self.cols[:, ka0 + fc:ka0 + fc + 1], OMK[:, fc:fc + 1],
                                                          ALU.mult, ALU.add), reads=[db, cb, self.constb], writes=[db])
                S.op("dve", lambda e: e.tensor_tensor(KM[:], PL[:, 4:8, :], KK[:], ALU.mult), reads=[db, PLB], writes=[db])
                S.op("dve", lambda e: e.tensor_tensor(SQ[:], PL[:, 0:4, :], KM[:], ALU.mult), reads=[db, PLB], writes=[db])
                for fc in range(4):
                    S.op("dve", lambda e: e.tensor_scalar(SQ[:, fc, :], SQ[:, fc, :], self.cols[:, rk0 + fc:rk0 + fc + 1], None, ALU.mult),
                         reads=[db, self.constb], writes=[db])
                pbn, pbnb = self.psum.get()
                S.op("pe", lambda e: e.matmul(pbn[:], BO[:], SQ[:].rearrange("p c t -> p (c t)"), start=True, stop=True), reads=[db, cb], writes=[pbnb])
                S.op("dve", lambda e: e.tensor_tensor(BON[:], pbn[:].rearrange("p (c t) -> p c t", c=4), PL[:, 8:12, :], ALU.mult),
                     reads=[pbnb, PLB], writes=[db])
                S.op("dve", lambda e: e.tensor_copy(RM[0:64, :, :, 0], PL[0:64, 0:4, :]), reads=[PLB, rmb], writes=[rmb])
                S.op("dve", lambda e: e.tensor_copy(RM[64:128, :, :, 1], PL[64:128, 0:4, :]), reads=[PLB, rmb], writes=[rmb])
                for tt in range(128):
                    pvb_t, pvbb = self.psum.get()

                    VD, vdb = VDr.get()
                    S.op("pool", lambda e: e.tensor_tensor(VD[:], ID2[:].unsqueeze(1).to_broadcast([128, 4, 64]),
                                                           PL[:, 8:12, tt:tt + 1].to_broadcast([128, 4, 64]), ALU.mult),
                         reads=[PLB, cb], writes=[vdb])
                    S.op("pe", lambda e: e.matmul(pvb_t[:, 0:256], BOb[:], VD[:].rearrange("p c v -> p (c v)"), start=True, stop=True),
                         reads=[vdb, cb], writes=[pvbb])
                    T2, t2b = T2r.get()
                    S.op("pool" if False else "dve", lambda e: e.tensor_tensor(
                        T2[:], pvb_t[:, 0:256].rearrange("p (c v) -> p c v", c=4), KM[:, :, tt:tt + 1].to_broadcast([128, 4, 64]), ALU.mult),
                        reads=[pvbb, db], writes=[t2b])
                    S.op("dve", lambda e: e.tensor_tensor(HK[:], H[:], KKN[:, :, tt:tt + 1].to_broadcast([128, 4, 64]), ALU.mult),
                         reads=[hb, db], writes=[hkb])
                    psa, psab = self.psum.get()
                    S.op("pe", lambda e: e.matmul(psa[:, 0:256], BOb[:], HK[:].rearrange("p c v -> p (c v)"), start=True, stop=True),
                         reads=[hkb, cb], writes=[psab])
                    S.op("dve", lambda e: e.tensor_tensor(T1[:], psa[:, 0:256].rearrange("p (c v) -> p c v", c=4),
                                                          NB[:, :, tt:tt + 1].to_broadcast([128, 4, 64]), ALU.mult),
                         reads=[psab, db], writes=[t1b])
                    S.op("dve", lambda e: e.tensor_tensor(H[:], H[:], WD[:, :, tt:tt + 1].to_broadcast([128, 4, 64]), ALU.mult),
                         reads=[hb, db], writes=[hb])
                    S.op("dve", lambda e: e.tensor_tensor(T1[:], T1[:], T2[:], ALU.add), reads=[t1b, t2b], writes=[t1b])
                    S.op("dve", lambda e: e.tensor_tensor(H[:], H[:], T1[:], ALU.add), reads=[hb, t1b], writes=[hb])
                    S.op("act", lambda e: e.copy(Hb[:], H[:]), reads=[hb], writes=[hbb])
                    py, pyb = self.psum.get()

                    def mmy(e):
                        for fc in range(4):
                            ins = e.matmul(py[0:2, fc * 64:(fc + 1) * 64], RM[:, fc, tt, :], Hb[:, fc, :], start=True, stop=True)
                        return ins
                    S.op("pe", mmy, reads=[hbb, rmb], writes=[pyb])
                    slot = tt % 2
                    S.op("act", lambda e: e.copy(YST[slot][0:2, 0, :], py[0:2, 0:256]), reads=[pyb], writes=[YSTB[slot]])
                    for hp in range(2):
                        S.dma(YTOK[tt:tt + 1, :, hp, :], YST[slot][hp:hp + 1, 0, :].rearrange("p (c v) -> p c v", c=4),
                              reads=[YSTB[slot], db], writes=[YTOKB])
                YT8 = YTOK.rearrange("t c h v -> t (c h) v")
                S.op("dve", lambda e: e.tensor_reduce(ST8[:], YT8, AX.X, ALU.add), reads=[YTOKB], writes=[yb])
                S.op("dve", lambda e: e.tensor_scalar(ST8[:], ST8[:], 1.0 / 64, None, ALU.mult), reads=[yb], writes=[yb])
                S.op("dve", lambda e: e.tensor_tensor(YC, YT8, ST8[:].unsqueeze(2).to_broadcast([128, 8, 64]), ALU.subtract),
                     reads=[YTOKB, yb], writes=[yb, db])
                S.op("dve", lambda e: e.tensor_tensor(YTOK.rearrange("t c h v -> t (c h) v"), YC, YC, ALU.mult), reads=[yb, YTOKB], writes=[YTOKB])
                S.op("dve", lambda e: e.tensor_reduce(ST8b[:], YT8, AX.X, ALU.add), reads=[YTOKB], writes=[yb])
                S.op("act", lambda e: e.activation(ST8b[:], ST8b[:], AF.Sqrt, bias=self.gneps_t[:], scale=1.0 / 64), reads=[yb, self.constb], writes=[yb])
                S.op("dve", lambda e: e.reciprocal(ST8b[:], ST8b[:]), reads=[yb], writes=[yb])
                S.op("dve", lambda e: e.tensor_tensor(YC, YC, ST8b[:].unsqueeze(2).to_broadcast([128, 8, 64]), ALU.mult), reads=[yb], writes=[yb])
                YCf = YC.rearrange("t a v -> t (a v)")
                S.op("dve", lambda e: e.tensor_tensor(YCf, YCf, GNG[:], ALU.mult), reads=[yb, cb], writes=[yb])
                S.op("dve", lambda e: e.tensor_tensor(YCf, YCf, GNB[:], ALU.add), reads=[yb, cb], writes=[yb])
                pyt, pytb = self.psum.get(); pg, pgb = self.psum.get()

                def mmt2(e):
                    for fc in range(4):
                        ins = e.transpose(pyt[:, fc * 128:(fc + 1) * 128], YC[:, 2 * fc:2 * fc + 2, :].rearrange("t a v -> t (a v)"), ident[:])
                    for fc in range(4):
                        ins = e.matmul(pg[:, fc * 128:(fc + 1) * 128], G2[:, fc * 128:(fc + 1) * 128], SGg[:], start=True, stop=True)
                    return ins
                S.op("pe", mmt2, reads=[yb, self.constb, db, cb], writes=[pytb, pgb])
                S.op("dve", lambda e: e.tensor_tensor(YF[:], pyt[:].rearrange("p (c t) -> p c t", c=4), BON[:], ALU.add), reads=[pytb, db], writes=[yb, db])
                S.op("dve", lambda e: e.tensor_tensor(self.Y[0][:, :, tsl], YF[:], pg[:].rearrange("p (c t) -> p c t", c=4), ALU.mult),
                     reads=[yb, db, pgb], writes=[self.YB[0][tcix]])
            S.full_barrier()
            self.st = old


    def rwkv_branch(self, w_rwkv, w2_d, a2_d, g2_d, gng_d, gnb_d, mk_d):
        S = self.S
        CN = COLS
        NT = S_LEN // 128
        CDEC = 0.6065306597126334
        with ExitStack() as st4:
            old, self.st = self.st, st4
            WR = self.sb("WR", [128, NCH, 1792], BF16); WRB = Buf()
            W2 = self.sb("W2A2", [128, 512], F32); A2 = W2; G2 = self.sb("G2", [128, 512], F32)
            BO = self.sb("BO", [128, 128], F32)
            ID2 = self.sb("ID2", [128, 64], F32)
            OMK = self.sb("OMK", [128, 4], F32)
            MSK = self.sb("MSK", [128, 3, 2, 128], BF16)
            ONE64 = self.sb("ONE64", [128, 64], F32)
            cb = Buf()
            self.load_w(WR[:], w_rwkv, WRB)
            S.dma(W2[0:64, :], w2_d, writes=[cb]); S.dma(A2[64:128, :], a2_d, writes=[cb]); S.dma(G2[:], g2_d, writes=[cb])
            S.dma(MSK[:], mk_d, writes=[cb], queue="pool")
            S.op("dve", lambda e: e.memset(BO[:], 0.0), reads=[cb], writes=[cb])
            S.op("dve", lambda e: e.memset(BO[0:64, 0:64], 1.0), reads=[cb], writes=[cb])
            S.op("dve", lambda e: e.memset(BO[64:128, 64:128], 1.0), reads=[cb], writes=[cb])
            S.op("dve", lambda e: e.memset(ONE64[:], 1.0), reads=[cb], writes=[cb])
            S.op("dve", lambda e: e.tensor_copy(ID2[0:64, :], self.ident_f[0:64, 0:64]), reads=[cb, self.constb], writes=[cb])
            S.op("dve", lambda e: e.tensor_copy(ID2[64:128, :], self.ident_f[64:128, 64:128]), reads=[cb, self.constb], writes=[cb])
            ka0 = CN["k_a"][0]
            S.op("dve", lambda e: e.tensor_scalar(OMK[:], self.cols[:, ka0:ka0 + 4], -1.0, 1.0, ALU.mult, ALU.add),
                 reads=[cb, self.constb], writes=[cb])
            P32 = self.sb("P32", [128, 14, 129], F32); P32B = Buf()
            DD = self.sb("DD", [128, 128], F32); DDB = Buf()
            CAR = self.sb("CAR", [128, 14, 1], F32)
            PL = P32[:, :, 1:129]; PLB = P32B
            TW = self.sb("TW", [64, 128], F32); SGg = self.sb("SGg", [128, 128], F32)
            f32t = lambda n: self.sb(n, [128, 4, 128], F32)
            SIG = f32t("SIG"); CUM = f32t("CUM"); A32 = f32t("A32"); KK = f32t("KK"); SQ = f32t("SQ")
            KKN = f32t("KKN"); NB = f32t("NB"); KM = f32t("KM"); BON = f32t("BON")
            AH = self.sb("AH", [128, 4, 128], BF16); KH = self.sb("KH", [128, 4, 128], BF16)
            BR = self.sb("BR", [128, 4, 2, 128], BF16)
            AT = self.sb("AT", [128, 512], BF16); KTt = self.sb("KTt", [128, 512], BF16); VTOK = self.sb("VTOK", [128, 512], BF16)
            WB = self.sb("WB", [128, 8, 128], BF16); BU = self.sb("BU", [128, 8, 128], BF16)
            bf8 = lambda n: self.sb(n, [128, 8, 128], BF16)
            X0 = bf8("X0"); XT0 = bf8("XT0"); LKT = bf8("LKT"); GRA = bf8("GRA"); GRK = bf8("GRK"); TT = bf8("TT")
            XA1 = self.sb("XA1", [128, 4, 128], BF16); XTA1 = self.sb("XTA1", [128, 4, 128], BF16); TA1 = self.sb("TA1", [128, 4, 128], BF16)
            RTm = self.sb("RTm", [128, 4, 2, 128], BF16)
            M0Ts = SIG[:].rearrange("p c t -> p (c t)").rearrange("p (a k) -> p a k", a=8)
            N0s = CUM[:].rearrange("p c t -> p (c t)").rearrange("p (a k) -> p a k", a=8)
            PCt = self.sb("PCt", [128, 2, 4], F32)
            H = self.sb("H", [128, 4, 64], F32); Hb = self.sb("Hb", [128, 2, 4, 64], BF16); T1 = self.sb("T1", [128, 4, 64], F32)
            nbb = Buf(); kmb = Buf(); sgb = Buf(); cub = Buf()
            YTOK = NB[:].rearrange("p c t -> p (c t)").rearrange("p (c h v) -> p c h v", c=4, h=2); YTOKB = nbb
            YC = KM[:].rearrange("p c t -> p (c t)").rearrange("p (a v) -> p a v", a=8)
            ST8 = self.sb("ST8", [128, 8], F32); ST8b = self.sb("ST8b", [128, 8], F32)
            YF = SQ
            db = Buf(); hb = Buf(); hbb = Buf(); gb_ = Buf(); chb = Buf(); tkb = Buf(); yb = Buf(); mnb = Buf(); rtb = Buf()
            S.op("pool", lambda e: e.memset(P32[:], 0.0), writes=[P32B])
            S.op("pool", lambda e: e.memset(RTm[:], 0.0), writes=[rtb])
            S.op("pool", lambda e: e.memset(H[:], 0.0), writes=[hb])
            mu0 = CN["mu"][0]; w00 = CN["w0"][0]; a00 = CN["a0"][0]; kk0 = CN["k_k"][0]; rk0 = CN["r_k"][0]
            gg0 = CN["gn_g"][0]; gb0 = CN["gn_b"][0]
            ident = self.ident_f
            c4 = lambda ap: ap.rearrange("p (c t) -> p c t", c=4)
            for i in range(self._rk_tiles):
                t0 = i * 128
                tcix = t0 // TC
                tsl = slice(t0, t0 + 128)
                hreads = [self.HNB[c][tcix] for c in range(NCH)]
                for cg in range(4):
                    c0 = cg * 4
                    n = min(4, 14 - c0)
                    p, pb = self.psum.get()

                    def mm(e):
                        for cc in range(n):
                            for k in range(NCH):
                                ins = e.matmul(p[:, cc * 128:(cc + 1) * 128], WR[:, k, (c0 + cc) * 128:(c0 + cc + 1) * 128],
                                               self.HN[:, k, tsl], start=(k == 0), stop=(k == NCH - 1))
                        return ins
                    S.op("pe", mm, reads=hreads + [WRB], writes=[pb])
                    S.op("act", lambda e: e.copy(P32[:, c0:c0 + n, 1:129], p[:, 0:n * 128].rearrange("p (c t) -> p c t", c=n)),
                         reads=[pb], writes=[P32B])
                S.op("dve", lambda e: e.tensor_copy(CAR[:], P32[:, :, 128:129]), reads=[P32B], writes=[DDB])
                for c in range(14):
                    S.op("dve", lambda e: e.tensor_tensor(DD[:], P32[:, c, 0:128], P32[:, c, 1:129], ALU.subtract), reads=[P32B, DDB], writes=[DDB])
                    S.op("dve", lambda e: e.scalar_tensor_tensor(P32[:, c, 1:129], DD[:], self.cols[:, mu0 + c:mu0 + c + 1], P32[:, c, 1:129],
                                                                 ALU.mult, ALU.add), reads=[DDB, P32B, self.constb], writes=[P32B])
                S.op("dve", lambda e: e.tensor_copy(P32[:, :, 0:1], CAR[:]), reads=[P32B, DDB], writes=[P32B])
                S.op("act", lambda e: e.activation(TW[:], PL[0:64, 12, :], AF.Tanh), reads=[PLB], writes=[db])
                S.op("act", lambda e: e.activation(SGg[:], PL[:, 13, :], AF.Sigmoid), reads=[PLB], writes=[db])
                pz, pzb = self.psum.get(); pa, pab = self.psum.get()

                def mmz(e):
                    for fc in range(4):
                        ins = e.matmul(pz[:, fc * 128:(fc + 1) * 128], W2[0:64, fc * 128:(fc + 1) * 128], TW[:], start=True, stop=True)
                    return ins

                def mma(e):
                    for fc in range(4):
                        ins = e.matmul(pa[:, fc * 128:(fc + 1) * 128], A2[64:128, fc * 128:(fc + 1) * 128], PL[64:128, 12, :], start=True, stop=True)
                    return ins
                S.op("pe", mmz, reads=[db, cb], writes=[pzb])
                S.op("pe", mma, reads=[PLB, cb], writes=[pab])
                for fc in range(4):
                    S.op("act", lambda e: e.activation(SIG[:, fc, :], pz[:, fc * 128:(fc + 1) * 128], AF.Sigmoid,
                                                       bias=self.cols[:, w00 + fc:w00 + fc + 1]), reads=[pzb, self.constb], writes=[db, sgb])
                    S.op("act", lambda e: e.activation(A32[:, fc, :], pa[:, fc * 128:(fc + 1) * 128], AF.Sigmoid,
                                                       bias=self.cols[:, a00 + fc:a00 + fc + 1]), reads=[pab, self.constb], writes=[db])
                for fc in range(4):
                    S.op("dve", lambda e: e.tensor_scalar(KK[:, fc, :], PL[:, 4 + fc, :], self.cols[:, kk0 + fc:kk0 + fc + 1], None, ALU.mult),
                         reads=[PLB, self.constb], writes=[db])
                S.op("dve", lambda e: e.tensor_tensor(SQ[:], KK[:], KK[:], ALU.mult), reads=[db], writes=[db])
                pss, pssb = self.psum.get()
                S.op("pe", lambda e: e.matmul(pss[:], BO[:], SQ[:].rearrange("p c t -> p (c t)"), start=True, stop=True), reads=[db, cb], writes=[pssb])
                S.op("act", lambda e: e.activation(SQ[:], c4(pss[:]), AF.Sqrt), reads=[pssb, db], writes=[db])
                S.op("dve", lambda e: e.tensor_scalar(SQ[:], SQ[:], 1e-12, None, ALU.max), reads=[db], writes=[db])
                S.op("dve", lambda e: e.reciprocal(SQ[:], SQ[:]), reads=[db], writes=[db])
                S.op("dve", lambda e: e.tensor_tensor(KKN[:], KK[:], SQ[:], ALU.mult), reads=[db], writes=[db])
                S.op("dve", lambda e: e.tensor_tensor(NB[:], KKN[:], A32[:], ALU.mult), reads=[db], writes=[db, nbb])
                for fc in range(4):
                    S.op("dve", lambda e: e.tensor_scalar(KK[:, fc, :], A32[:, fc, :], self.cols[:, ka0 + fc:ka0 + fc + 1], OMK[:, fc:fc + 1],
                                                          ALU.mult, ALU.add), reads=[db, cb, self.constb], writes=[db])
                S.op("dve", lambda e: e.tensor_tensor(KM[:], PL[:, 4:8, :], KK[:], ALU.mult), reads=[db, PLB], writes=[db, kmb])
                S.op("dve", lambda e: e.tensor_tensor(SQ[:], PL[:, 0:4, :], KM[:], ALU.mult), reads=[db, PLB, kmb], writes=[db])
                for fc in range(4):
                    S.op("dve", lambda e: e.tensor_scalar(SQ[:, fc, :], SQ[:, fc, :], self.cols[:, rk0 + fc:rk0 + fc + 1], None, ALU.mult),
                         reads=[db, self.constb], writes=[db])
                pbn, pbnb = self.psum.get()
                S.op("pe", lambda e: e.matmul(pbn[:], BO[:], SQ[:].rearrange("p c t -> p (c t)"), start=True, stop=True), reads=[db, cb], writes=[pbnb])
                S.op("dve", lambda e: e.tensor_tensor(BON[:], c4(pbn[:]), PL[:, 8:12, :], ALU.mult), reads=[pbnb, PLB], writes=[db])
                for fc in range(4):
                    for c2 in range(2):
                        cs = slice(c2 * 64, (c2 + 1) * 64)
                        S.op("dve", lambda e: e.tensor_tensor_scan(CUM[:, fc, cs], ONE64[:], SIG[:, fc, cs], 0.0, ALU.mult, ALU.add),
                             reads=[db, cb, sgb], writes=[db, cub])
                S.op("pool", lambda e: e.tensor_tensor(SQ[:], CUM[:], SIG[:], ALU.subtract), reads=[db, sgb, cub], writes=[db])
                S.op("act", lambda e: e.activation(A32[:], CUM[:], AF.Exp, scale=CDEC), reads=[db, cub], writes=[db])
                S.op("act", lambda e: e.activation(CUM[:], CUM[:], AF.Exp, scale=-CDEC), reads=[db], writes=[db, cub])
                S.op("act", lambda e: e.activation(SQ[:], SQ[:], AF.Exp, scale=-CDEC), reads=[db], writes=[db])
                S.op("dve", lambda e: e.tensor_copy(PCt[:, 0, :], CUM[:, :, 63]), reads=[db, chb, cub], writes=[chb])
                S.op("dve", lambda e: e.tensor_copy(PCt[:, 1, :], CUM[:, :, 127]), reads=[db, chb, cub], writes=[chb])
                S.op("dve", lambda e: e.tensor_tensor(NB[:], NB[:], A32[:], ALU.mult), reads=[db], writes=[db, nbb])
                S.op("dve", lambda e: e.tensor_tensor(KM[:], KM[:], A32[:], ALU.mult), reads=[db], writes=[db, kmb])
                S.op("dve", lambda e: e.tensor_tensor(KKN[:], KKN[:], SQ[:], ALU.mult), reads=[db], writes=[db])
                S.op("dve", lambda e: e.tensor_tensor(KK[:], PL[:, 0:4, :], CUM[:], ALU.mult), reads=[db, PLB, cub], writes=[db])
                S.op("act", lambda e: e.copy(AH[:], NB[:]), reads=[db, gb_, nbb], writes=[gb_])
                S.op("act", lambda e: e.copy(KH[:], KM[:]), reads=[db, gb_, kmb], writes=[gb_])
                S.op("pool", lambda e: e.tensor_copy(BR[:, :, 0, :], KKN[:]), reads=[db, gb_], writes=[gb_])
                S.op("pool", lambda e: e.tensor_copy(BR[:, :, 1, :], KK[:]), reads=[db, gb_], writes=[gb_])
                if "dumpah" in self.debug and i == 0:
                    for nm, tl in (("ah", AH), ("kh", KH), ("br", BR)):
                        o_ = self.dout("dbg_" + nm, [128, tl[:].rearrange("p ... -> p (...)").shape[1] if False else (512 if nm != "br" else 1024)])
                        S.dma(o_, tl[:].rearrange("p c t -> p (c t)") if nm != "br" else tl[:].rearrange("p c a t -> p (c a t)"), reads=[gb_], queue="pool")
                    for nm, tl in (("nb", NB), ("km", KM), ("kkn", KKN), ("en", A32), ("ep", CUM)):
                        o_ = self.dout("dbg_" + nm, [128, 512])
                        S.dma(o_, tl[:].rearrange("p c t -> p (c t)"), reads=[db, nbb, kmb, cub])
                for src, dst_fn in ((NB, None), (KM, None), (KKN, None), (None, None)):
                    pass
                tr_jobs = [(lambda fc: NB[:, fc, :], "AT"), (lambda fc: KM[:, fc, :], "KT"),
                           (lambda fc: KKN[:, fc, :], "BT"), (lambda fc: PL[:, 8 + fc, :], "VT")]
                for srcf, kind in tr_jobs:
                    ptr, ptrb = self.psum.get()

                    def mmt(e):
                        for fc in range(4):
                            ins = e.transpose(ptr[:, fc * 128:(fc + 1) * 128], srcf(fc), ident[:])
                        return ins
                    S.op("pe", mmt, reads=[db, PLB, self.constb, nbb, kmb], writes=[ptrb])
                    if kind == "AT":
                        S.op("act", lambda e: e.copy(AT[:], ptr[:]), reads=[ptrb, tkb], writes=[tkb])
                    elif kind == "KT":
                        S.op("dve", lambda e: e.tensor_copy(KTt[:], ptr[:]), reads=[ptrb, tkb], writes=[tkb])
                    elif kind == "BT":
                        S.op("act", lambda e: e.activation(WB[:, :, 0:64], ptr[:].rearrange("p (h k) -> p h k", h=8), AF.Copy, scale=-1.0),
                             reads=[ptrb, tkb], writes=[tkb])
                    else:
                        S.op("dve", lambda e: e.tensor_copy(VTOK[:], ptr[:]), reads=[ptrb, tkb], writes=[tkb])
                if self._rk_stage <= 0:
                    continue
                for fc in range(4):
                    ka_, ab0, ab1 = self.psum.get_pair_idx()
                    kb_, bb0, bb1 = self.psum.get_pair_idx()
                    PA = self.PS[:, ka_:ka_ + 2, :]; PB = self.PS[:, kb_:kb_ + 2, :]

                    def mmg(e):
                        for h2 in range(2):
                            rs = slice(h2 * 64, (h2 + 1) * 64)
                            brr = BR[rs, fc, :, :].rearrange("p a t -> p (a t)")
                            e.matmul(PA[:, h2, 0:256], AH[rs, fc, :], brr, start=True, stop=True)
                            e.matmul(PA[:, h2, 256:512], KH[rs, fc, :], brr, start=True, stop=True)
                            ins = e.matmul(PB[:, h2, 0:128], BR[rs, fc, 0, :], AH[rs, fc, :], start=True, stop=True)
                        return ins
                    S.op("pe", mmg, reads=[gb_], writes=[ab0, ab1, bb0, bb1])
                    hs = slice(2 * fc, 2 * fc + 2)
                    PAv = PA.rearrange("p h (q b t) -> p h q b t", q=2, b=2)
                    mk = lambda j: MSK[:, j, :, :]
                    S.op("dve", lambda e: e.tensor_tensor(X0[:, hs, :], PAv[:, :, 0, 0, :], mk(0), ALU.mult), reads=[ab0, ab1, cb, mnb], writes=[mnb])
                    S.op("dve", lambda e: e.tensor_tensor(GRA[:, hs, :], PAv[:, :, 0, 1, :], mk(2), ALU.mult), reads=[ab0, ab1, cb, mnb], writes=[mnb])
                    S.op("dve", lambda e: e.tensor_tensor(LKT[:, hs, :], PAv[:, :, 1, 0, :], mk(0), ALU.mult), reads=[ab0, ab1, cb, mnb], writes=[mnb])
                    S.op("dve", lambda e: e.tensor_tensor(GRK[:, hs, :], PAv[:, :, 1, 1, :], mk(2), ALU.mult), reads=[ab0, ab1, cb, mnb], writes=[mnb])
                    S.op("dve", lambda e: e.tensor_tensor(XT0[:, hs, :], PB[:, :, 0:128], mk(1), ALU.mult), reads=[bb0, bb1, cb, mnb], writes=[mnb])
                if self._rk_stage <= 1:
                    continue
                for half in range(2):
                    h0 = half * 4
                    xb_ = Buf(); xtb_ = Buf(); tb_ = Buf()
                    xbufs = [X0[:, h0:h0 + 4, :], XA1[:]]
                    xtbufs = [XT0[:, h0:h0 + 4, :], XTA1[:]]
                    tbufs = [TA1[:], TT[:, h0:h0 + 4, :]]
                    S.op("pool", lambda e: e.tensor_tensor(tbufs[0], xbufs[0], ident[:].unsqueeze(1).to_broadcast([128, 4, 128]), ALU.add),
                         reads=[mnb, self.constb, tb_], writes=[tb_])
                    for lv in range(1, 6):
                        Xp, XTp, Tp = xbufs[(lv - 1) % 2], xtbufs[(lv - 1) % 2], tbufs[(lv - 1) % 2]
                        Xn, XTn, Tn = xbufs[lv % 2], xtbufs[lv % 2], tbufs[lv % 2]
                        pxt, pxtb = self.psum.get()

                        def mmxt(e):
                            for j in range(4):
                                ins = e.matmul(pxt[:, j * 128:(j + 1) * 128], Xp[:, j, :], XTp[:, j, :], start=True, stop=True)
                            return ins
                        S.op("pe", mmxt, reads=[mnb, xb_, xtb_], writes=[pxtb])
                        if lv < 5:
                            px, pxb = self.psum.get()

                            def mmx(e):
                                for j in range(4):
                                    ins = e.matmul(px[:, j * 128:(j + 1) * 128], XTp[:, j, :], Xp[:, j, :], start=True, stop=True)
                                return ins
                            S.op("pe", mmx, reads=[mnb, xb_, xtb_], writes=[pxb])
                        S.op("act", lambda e: e.copy(XTn, c4(pxt[:])), reads=[pxtb, xtb_, mnb], writes=[xtb_])
                        if lv < 5:
                            S.op("act", lambda e: e.copy(Xn, c4(px[:])), reads=[pxb, xb_, mnb], writes=[xb_])
                        ptt, pttb = self.psum.get()

                        def mmtt(e):
                            for j in range(4):
                                ins = e.matmul(ptt[:, j * 128:(j + 1) * 128], XTn[:, j, :], Tp[:, j, :], start=True, stop=True)
                            return ins
                        S.op("pe", mmtt, reads=[xtb_, tb_], writes=[pttb])
                        S.op("dve", lambda e: e.tensor_tensor(Tn, c4(ptt[:]), Tp, ALU.add), reads=[pttb, tb_, mnb], writes=[tb_] + ([mnb] if lv == 5 else []))
                if self._rk_stage <= 2:
                    continue
                plk, plkb = self.psum.get()

                def mmlk(e):
                    for h in range(8):
                        ins = e.matmul(plk[:, h * 64:(h + 1) * 64], LKT[:, h, :], VTOK[:, h * 64:(h + 1) * 64], start=True, stop=True)
                    return ins
                S.op("pe", mmlk, reads=[mnb, tkb], writes=[plkb])
                S.op("act", lambda e: e.copy(WB[:, :, 64:128], plk[:].rearrange("p (h v) -> p h v", h=8)), reads=[plkb, tkb], writes=[tkb])
                for half in range(2):
                    pbu, pbub = self.psum.get()

                    def mmbu(e):
                        for j in range(4):
                            h = half * 4 + j
                            ins = e.matmul(pbu[:, j * 128:(j + 1) * 128], TT[:, h, :], WB[:, h, :], start=True, stop=True)
                        return ins
                    S.op("pe", mmbu, reads=[mnb, tkb], writes=[pbub])
                    S.op("act", lambda e: e.copy(BU[:, half * 4:half * 4 + 4, :], c4(pbu[:])), reads=[pbub, chb], writes=[chb])
                if self._rk_stage <= 3:
                    continue
                prt, prtb = self.psum.get()
                km_, mb0, mb1 = self.psum.get_pair_idx()
                PM = self.PS[:, km_:km_ + 2, :]

                def mmrt(e):
                    for h in range(8):
                        rs = slice((h % 2) * 64, (h % 2) * 64 + 64)
                        fc = h // 2
                        ins = e.matmul(prt[rs, fc * 128:(fc + 1) * 128], BU[:, h, 0:64], GRA[:, h, :], start=True, stop=True)
                    return ins

                def mmmn(e):
                    for c2 in range(2):
                        cr = slice(c2 * 64, (c2 + 1) * 64)
                        for h in range(8):
                            rs = slice((h % 2) * 64, (h % 2) * 64 + 64)
                            fc = h // 2
                            o = fc * 64
                            e.matmul(PM[rs, c2, o:o + 64], BU[cr, h, 0:64], AT[cr, h * 64:(h + 1) * 64], start=True, stop=True)
                            e.matmul(PM[rs, c2, 256 + o:256 + o + 64], AT[cr, h * 64:(h + 1) * 64], BU[cr, h, 64:128], start=True, stop=False)
                            ins = e.matmul(PM[rs, c2, 256 + o:256 + o + 64], KTt[cr, h * 64:(h + 1) * 64], VTOK[cr, h * 64:(h + 1) * 64], start=False, stop=True)
                    return ins
                S.op("pe", mmrt, reads=[chb, mnb], writes=[prtb])
                S.op("pe", mmmn, reads=[chb, tkb], writes=[mb0, mb1])
                prv = c4(prt[:])
                S.op("dve", lambda e: e.tensor_tensor(RTm[:, :, 0, 0:64], prv[:, :, 0:64], KK[:, :, 0:64], ALU.add), reads=[prtb, db, rtb], writes=[rtb])
                S.op("dve", lambda e: e.tensor_tensor(RTm[:, :, 1, 64:128], prv[:, :, 64:128], KK[:, :, 64:128], ALU.add), reads=[prtb, db, rtb], writes=[rtb])
                M0v = M0Ts.rearrange("p (a c) k -> p a c k", a=2)
                N0v = N0s.rearrange("p (a c) k -> p a c k", a=2)
                S.op("dve", lambda e: e.tensor_tensor(M0v, PM[:, :, 0:256].rearrange("p a (c k) -> p a c k", c=4),
                                                      ID2[:].unsqueeze(1).unsqueeze(1).to_broadcast([128, 2, 4, 64]), ALU.add),
                     reads=[mb0, mb1, cb, chb], writes=[chb, sgb])
                S.op("act", lambda e: e.copy(N0v, PM[:, :, 256:512].rearrange("p a (c k) -> p a c k", c=4)), reads=[mb0, mb1, chb], writes=[chb, cub])
                if self._rk_stage <= 4:
                    continue
                for c2 in range(2):
                    S.op("act", lambda e: e.copy(Hb[:, c2, :, :], H[:]), reads=[hb, hbb], writes=[hbb])
                    phe, pheb = self.psum.get(); pho, phob = self.psum.get()

                    def mmh(e):
                        for par, bank in ((0, phe), (1, pho)):
                            rs = slice(par * 64, par * 64 + 64)
                            for fc in range(4):
                                ins = e.matmul(bank[rs, fc * 64:(fc + 1) * 64], M0Ts[rs, c2 * 4 + fc, :], H[rs, fc, :], start=True, stop=True)
                        return ins
                    S.op("pe", mmh, reads=[chb, hb, sgb], writes=[pheb, phob])
                    S.op("dve", lambda e: e.tensor_tensor(T1[0:64], phe[0:64, 0:256].rearrange("p (c v) -> p c v", c=4), N0s[0:64, c2 * 4:c2 * 4 + 4, :], ALU.add),
                         reads=[pheb, chb, cub], writes=[chb])
                    S.op("dve", lambda e: e.tensor_tensor(T1[64:128], pho[64:128, 0:256].rearrange("p (c v) -> p c v", c=4), N0s[64:128, c2 * 4:c2 * 4 + 4, :], ALU.add),
                         reads=[phob, chb, cub], writes=[chb])
                    S.op("dve", lambda e: e.tensor_tensor(H[:], T1[:], PCt[:, c2, :].unsqueeze(2).to_broadcast([128, 4, 64]), ALU.mult),
                         reads=[chb, hb], writes=[hb])
                if self._rk_stage <= 5:
                    continue
                ky_, yb0, yb1 = self.psum.get_pair_idx()
                PY = self.PS[:, ky_:ky_ + 2, :]

                def mmy(e):
                    for par in range(2):
                        rs = slice(par * 64, par * 64 + 64)
                        for fc in range(4):
                            h = 2 * fc + par
                            o = PY[:, par, fc * 64:(fc + 1) * 64]
                            e.matmul(o, GRA[:, h, :], BU[:, h, 64:128], start=True, stop=False)
                            e.matmul(o, GRK[:, h, :], VTOK[:, h * 64:(h + 1) * 64], start=False, stop=False)
                            e.matmul(o, RTm[rs, fc, 0, :], Hb[rs, 0, fc, :], start=False, stop=False)
                            ins = e.matmul(o, RTm[rs, fc, 1, :], Hb[rs, 1, fc, :], start=False, stop=True)
                    return ins
                S.op("pe", mmy, reads=[mnb, chb, tkb, rtb, hbb], writes=[yb0, yb1])
                S.op("act", lambda e: e.copy(YTOK.rearrange("t c h v -> t h c v"), PY[:, :, 0:256].rearrange("t h (c v) -> t h c v", c=4)),
                     reads=[yb0, yb1, YTOKB], writes=[YTOKB])
                if self._rk_stage <= 6:
                    continue
                YT8 = YTOK.rearrange("t c h v -> t (c h) v")
                S.op("dve", lambda e: e.tensor_reduce(ST8[:], YT8, AX.X, ALU.add), reads=[YTOKB], writes=[yb])
                S.op("dve", lambda e: e.tensor_scalar(ST8[:], ST8[:], 1.0 / 64, None, ALU.mult), reads=[yb], writes=[yb])
                S.op("dve", lambda e: e.tensor_tensor(YC, YT8, ST8[:].unsqueeze(2).to_broadcast([128, 8, 64]), ALU.subtract),
                     reads=[YTOKB, yb], writes=[yb, kmb])
                S.op("pool", lambda e: e.tensor_tensor(YT8, YC, YC, ALU.mult), reads=[yb, YTOKB, kmb], writes=[YTOKB])
                S.op("dve", lambda e: e.tensor_reduce(ST8b[:], YT8, AX.X, ALU.add), reads=[YTOKB], writes=[yb])
                S.op("act", lambda e: e.activation(ST8b[:], ST8b[:], AF.Sqrt, bias=self.gneps_t[:], scale=1.0 / 64), reads=[yb, self.constb], writes=[yb])
                S.op("dve", lambda e: e.reciprocal(ST8b[:], ST8b[:]), reads=[yb], writes=[yb])
                S.op("dve", lambda e: e.tensor_tensor(YC, YC, ST8b[:].unsqueeze(2).to_broadcast([128, 8, 64]), ALU.mult), reads=[yb], writes=[yb, kmb])
                pyt, pytb = self.psum.get(); pg, pgb = self.psum.get()

                def mmt2(e):
                    for fc in range(4):
                        ins = e.transpose(pyt[:, fc * 128:(fc + 1) * 128], YC[:, 2 * fc:2 * fc + 2, :].rearrange("t a v -> t (a v)"), ident[:])
                    for fc in range(4):
                        ins = e.matmul(pg[:, fc * 128:(fc + 1) * 128], G2[:, fc * 128:(fc + 1) * 128], SGg[:], start=True, stop=True)
                    return ins
                S.op("pe", mmt2, reads=[yb, self.constb, db, cb, kmb], writes=[pytb, pgb])
                for fc in range(4):
                    S.op("dve", lambda e: e.tensor_scalar(YF[:, fc, :], pyt[:, fc * 128:(fc + 1) * 128], self.cols[:, gg0 + fc:gg0 + fc + 1],
                                                          self.cols[:, gb0 + fc:gb0 + fc + 1], ALU.mult, ALU.add),
                         reads=[pytb, db, self.constb], writes=[db])
                S.op("pool", lambda e: e.tensor_tensor(YF[:], YF[:], BON[:], ALU.add), reads=[db], writes=[db])
                S.op("dve", lambda e: e.tensor_tensor(self.Y[0][:, :, tsl], YF[:], c4(pg[:]), ALU.mult),
                     reads=[db, pgb], writes=[self.YB[0][tcix]])
            S.full_barrier()
            self.st = old

    def nsa_branch(self, d):
        S = self.S
        NT = S_LEN // 128
        with ExitStack() as st4:
            old, self.st = self.st, st4
            cb = Buf()
            KT = self.sb("KT", [128, 2, S_LEN], BF16); KTB = Buf()
            VT = self.sb("VT", [128, NT, 256], BF16); VTB = Buf()
            KC = self.sb("KC", [128, 127], BF16); VC = self.sb("VC", [128, 128], BF16); kcb = Buf()
            BM = self.sb("BM", [128, 3, 2, 512], BF16)
            BVC = self.sb("BVC", [32, 2, 512], BF16)
            stA = ExitStack(); self.st = stA
            G1 = self.sb("G1", [128, 2, 512], F32); G2_ = self.sb("G2b", [128, 2, 512], F32); MK = self.sb("MK", [128, 128], F32)
            gb = Buf()
            S.dma(G2_[:], d["t31"], writes=[gb])
            for kind in range(3):
                S.dma(G1[:], d["bmg"][kind], reads=[gb], writes=[gb])
                S.dma(MK[:], d["msk"][kind], reads=[gb], writes=[gb])
                S.op("dve", lambda e: e.tensor_tensor(G1[:], G1[:], G2_[:], ALU.subtract), reads=[gb], writes=[gb])
                S.op("dve", lambda e: e.tensor_tensor(BM[:, kind, :, :].rearrange("p g (j q) -> p (g j) q", j=4),
                                                      G1[:].rearrange("p g (j q) -> p (g j) q", j=4),
                                                      MK[:].unsqueeze(1).to_broadcast([128, 8, 128]), ALU.add), reads=[gb], writes=[cb, gb])
            S.dma(G1[0:32, :, :], d["bvcg"], reads=[gb], writes=[gb])
            S.dma(MK[0:32, :], d["mskc"], reads=[gb], writes=[gb])
            S.op("dve", lambda e: e.tensor_tensor(G1[0:32], G1[0:32], G2_[0:32], ALU.subtract), reads=[gb], writes=[gb])
            S.op("dve", lambda e: e.tensor_tensor(BVC[:].rearrange("p g (j q) -> p (g j) q", j=4),
                                                  G1[0:32].rearrange("p g (j q) -> p (g j) q", j=4),
                                                  MK[0:32, :].unsqueeze(1).to_broadcast([32, 8, 128]), ALU.add), reads=[gb], writes=[cb, gb])
            S.full_barrier()
            stA.close()
            stB = ExitStack(); self.st = stB
            KCMP = self.sb("KCMP", [128, S_LEN], BF16); VCT = self.sb("VCT", [128, S_LEN], BF16)
            stB1 = ExitStack(); self.st = stB1
            WKV = self.sb("WKV", [128, NCH, 768], BF16); wkvb = Buf()
            self.load_w(WKV[:], d["w_kvn"], wkvb)
            for tc in range(NTC):
                ts = slice(tc * TC, (tc + 1) * TC)
                hreads = [self.HNB[c][tc] for c in range(NCH)]
                for dst, col in ((KCMP[:, ts], 0), (VCT[:, ts], 128), (KT[:, 0, ts], 256), (KT[:, 1, ts], 512)):
                    p, pb = self.psum.get()

                    def mm(e):
                        for k in range(NCH):
                            ins = e.matmul(p[:], WKV[:, k, col:col + 128], self.HN[:, k, ts], start=(k == 0), stop=(k == NCH - 1))
                        return ins
                    S.op("pe", mm, reads=hreads + [wkvb], writes=[pb])
                    S.op("act", lambda e: e.copy(dst, p[:]), reads=[pb], writes=[KTB])
                for tl in range(4):
                    tile = tc * 4 + tl
                    tq = slice(tile * 128, (tile + 1) * 128)
                    p, pb = self.psum.get()

                    def mm(e):
                        for k in range(NCH):
                            e.matmul(p[:, 0:128], self.HN[:, k, tq], WKV[:, k, 384:512], start=(k == 0), stop=(k == NCH - 1))
                        for k in range(NCH):
                            ins = e.matmul(p[:, 128:256], self.HN[:, k, tq], WKV[:, k, 640:768], start=(k == 0), stop=(k == NCH - 1))
                        return ins
                    S.op("pe", mm, reads=hreads + [wkvb], writes=[pb])
                    S.op("dve", lambda e: e.tensor_copy(VT[:, tile, :], p[:, 0:256]), reads=[pb], writes=[VTB])
            S.full_barrier()
            stB1.close()
            stB2 = ExitStack(); self.st = stB2
            W1 = self.sb("W1", [128, 32, 256], BF16); PET = self.sb("PET", [128, 32], BF16)
            W2D = self.sb("W2D", [128, 2, 128], BF16); HID = self.sb("HID", [128, 2, 127], BF16)
            ZZ = self.sb("ZZ", [128, 127], F32); Z2 = self.sb("Z2", [128, 127], F32); BC = self.sb("BCc", [128, 1], F32)
            wb = Buf(); zb = Buf(); hb_ = Buf()
            for kv in range(2):
                w1d = d["cmp_w1"][kv].rearrange("(l dd) m -> dd l m", dd=64)
                S.dma(W1[0:64], w1d, writes=[wb], queue="pool"); S.dma(W1[64:128], w1d, writes=[wb], queue="pool")
                S.dma(PET[0:64], d["cmp_peT"][kv], writes=[wb], queue="pool"); S.dma(PET[64:128], d["cmp_peT"][kv], writes=[wb], queue="pool")
                w2v = d["cmp_w2"][kv].rearrange("(c p) n -> p c n", p=128)
                S.dma(W2D[:, :, 0:64], w2v, writes=[wb], queue="pool"); S.dma(W2D[:, :, 64:128], w2v, writes=[wb], queue="pool")
                SRC = KCMP if kv == 0 else VCT
                for g in range(2):
                    gs = slice(g * 64, (g + 1) * 64)
                    for mc in range(2):
                        ph, phb = self.psum.get(); pbias, pbb = self.psum.get()

                        def mm(e):
                            for l in range(32):
                                ins = e.matmul(ph[:, 0:127], W1[gs, l, mc * 128:(mc + 1) * 128], SRC[gs, l:l + 16 * 126 + 1:16],
                                               start=(l == 0), stop=(l == 31))
                            return ins

                        def mmb(e):
                            for l in range(32):
                                ins = e.matmul(pbias[:, 0:1], W1[gs, l, mc * 128:(mc + 1) * 128], PET[gs, l:l + 1], start=(l == 0), stop=(l == 31))
                            return ins
                        S.op("pe", mm, reads=[wb, KTB], writes=[phb])
                        S.op("pe", mmb, reads=[wb], writes=[pbb])
                        S.op("act", lambda e: e.copy(BC[:], pbias[:, 0:1]), reads=[pbb, zb], writes=[zb])
                        S.op("dve", lambda e: e.tensor_scalar(ZZ[:], ph[:, 0:127], BC[:, 0:1], None, ALU.add), reads=[phb, zb], writes=[zb])
                        S.op("dve", lambda e: e.tensor_tensor(Z2[:], ZZ[:], ZZ[:], ALU.mult), reads=[zb], writes=[zb])
                        S.op("dve", lambda e: e.tensor_scalar(Z2[:], Z2[:], 0.044715, 1.0, ALU.mult, ALU.add), reads=[zb], writes=[zb])
                        S.op("dve", lambda e: e.tensor_tensor(Z2[:], Z2[:], ZZ[:], ALU.mult), reads=[zb], writes=[zb])
                        S.op("act", lambda e: e.activation(Z2[:], Z2[:], AF.Sigmoid, scale=1.5957691216057308), reads=[zb], writes=[zb])
                        S.op("dve", lambda e: e.tensor_tensor(HID[:, mc, :], ZZ[:], Z2[:], ALU.mult), reads=[zb, hb_], writes=[hb_])
                    po, pob = self.psum.get()
                    if kv == 0:
                        def mm2(e):
                            for mc in range(2):
                                ins = e.matmul(po[:, 0:127], W2D[:, mc, :], HID[:, mc, :], start=(mc == 0), stop=(mc == 1))
                            return ins
                        S.op("pe", mm2, reads=[hb_, wb], writes=[pob])
                        S.op("act", lambda e: e.copy(KC[gs, :], po[gs, 0:127]), reads=[pob], writes=[kcb])
                    else:
                        def mm2(e):
                            for mc in range(2):
                                ins = e.matmul(po[0:127, 0:64], HID[:, mc, :], W2D[:, mc, 0:64], start=(mc == 0), stop=(mc == 1))
                            return ins
                        S.op("pe", mm2, reads=[hb_, wb], writes=[pob])
                        S.op("act", lambda e: e.copy(VC[0:127, gs], po[0:127, 0:64]), reads=[pob], writes=[kcb])
            S.full_barrier()
            stB2.close(); stB.close(); self.st = st4
            if "kcvc" in self.debug:
                okc = self.dout("dbg_kc", [128, 127]); ovc = self.dout("dbg_vc", [127, 128])
                S.dma(okc, KC[:], reads=[kcb], queue="pool"); S.dma(ovc, VC[0:127, :], reads=[kcb], queue="pool")
            WQ = self.sb("WQN", [128, NCH, 512], BF16); WGN = self.sb("WGN", [128, NCH, 24], BF16)
            SHCF = self.sb("SHCF", [32, 247], BF16); EF = self.sb("EF", [32, S_LEN], BF16)
            OV = self.sb("OV", [128, 32], BF16); AB = self.sb("ABF", [128, 2, 64], F32)
            SELG = self.sb("SELG", [24, 12, 128], BF16); IDb = self.sb("IDb", [128, 128], BF16)
            self.load_w(WQ[:], d["w_qn"], cb)
            self.load_w(WGN[:], d["w_gn"], cb)
            S.dma(SHCF[:], d["shcf"], writes=[cb], queue="pool"); S.dma(EF[:], d["efull"], writes=[cb], queue="pool")
            S.dma(OV[0:127, :], d["ov"], writes=[cb], queue="pool"); S.dma(AB[:], d["abf"], writes=[cb])
            S.dma(SELG[:], d["selg"], writes=[cb], queue="pool")
            S.op("dve", lambda e: e.tensor_copy(IDb[:], self.ident_f[:]), reads=[self.constb, cb], writes=[cb])
            QS = self.sb("QS", [128, 4, 128], BF16); qsb = Buf()
            GS = self.sb("GS", [24, 128], BF16); gsb = Buf()
            pt_ring = Ring([self.sb("PT%d" % i, [128, 512], BF16) for i in range(3)])
            RR = self.sb("RR", [128, 512], F32); rrb = Buf()
            YA = self.sb("YA", [128, 512], F32); yab = Buf()
            PN = self.sb("PN", [128, 512], BF16); pnb = Buf()
            IMP = self.sb("IMP", [128, 32], F32); IM2 = self.sb("IM2", [128, 32], F32); MX = self.sb("MX8", [128, 8], F32); ib = Buf()
            NMT = [self.sb("NMT%d" % g, [32, 4, 128], BF16) for g in range(2)]; nmb = [Buf(), Buf()]
            st_ring = Ring(self.banks[0:3], self.bankb[0:3])
            O, Ob = self.banks[3], self.bankb[3]
            DN, Db = self.banks[4], self.bankb[4]
            ms_ring = Ring(self.banks[5:8], self.bankb[5:8])
            for i in range(NT):
                tq = slice(i * 128, (i + 1) * 128)
                tcix = i // 4
                hreads = [self.HNB[c][tcix] for c in range(NCH)]
                p, pb = ms_ring.get()

                def mmq(e):
                    for j in range(4):
                        for k in range(NCH):
                            ins = e.matmul(p[:, j * 128:(j + 1) * 128], WQ[:, k, j * 128:(j + 1) * 128], self.HN[:, k, tq], start=(k == 0), stop=(k == NCH - 1))
                    return ins
                S.op("pe", mmq, reads=hreads + [cb], writes=[pb])
                S.op("act", lambda e: e.activation(QS[:].rearrange("p j q -> p (j q)"), p[:], AF.Copy, scale=0.125), reads=[pb], writes=[qsb])
                p2, pb2 = ms_ring.get()

                def mmg(e):
                    for k in range(NCH):
                        ins = e.matmul(p2[0:24, 0:128], WGN[:, k, :], self.HN[:, k, tq], start=(k == 0), stop=(k == NCH - 1))
                    return ins
                S.op("pe", mmg, reads=hreads + [cb], writes=[pb2])
                S.op("act", lambda e: e.activation(GS[:], p2[0:24, 0:128], AF.Sigmoid), reads=[pb2], writes=[gsb])
                for br in range(3):
                    for g in range(2):
                        gs = slice(g * 64, (g + 1) * 64)
                        qrhs = QS[gs, :, :].rearrange("p j q -> p (j q)")
                        if br == 0:
                            tiles = [None]
                        elif br == 1:
                            tiles = list(range(0, i + 1))
                        else:
                            tiles = list(range(max(0, i - 4), i + 1))
                        for ti, kt in enumerate(tiles):
                            stp, stb = st_ring.get()
                            rows = 127 if br == 0 else 128

                            def mms(e):
                                mms_list = []
                                if br == 0:
                                    mms_list.append((KC[gs, :], qrhs))
                                    mms_list.append((SHCF[:, 120 - 8 * i:247 - 8 * i], BVC[:, g, :]))
                                else:
                                    mms_list.append((KT[gs, br - 1, kt * 128:(kt + 1) * 128], qrhs))
                                    if br == 1 and i >= 8:
                                        mms_list.append((EF[:, kt * 128:(kt + 1) * 128], NMT[g][:].rearrange("p j q -> p (j q)")))
                                    if kt == i:
                                        mms_list.append((IDb[:], BM[:, 0, g, :]))
                                    elif kt == i - 1:
                                        mms_list.append((IDb[:], BM[:, 1, g, :]))
                                    elif br == 2 and kt == i - 4:
                                        mms_list.append((IDb[:], BM[:, 2, g, :]))
                                for n_, (l_, r_) in enumerate(mms_list):
                                    ins = e.matmul(stp[0:rows, :], l_, r_, start=(n_ == 0), stop=(n_ == len(mms_list) - 1))
                                return ins
                            S.op("pe", mms, reads=[qsb, KTB, kcb, cb, nmb[g]], writes=[stb])
                            PT, ptb = pt_ring.get()
                            S.op("act", lambda e: e.activation(PT[0:rows, :], stp[0:rows, :], AF.Exp), reads=[stb], writes=[ptb])
                            if br == 0:
                                vl = VC[0:127, gs]
                            else:
                                c0 = (0 if br == 1 else 128) + g * 64
                                vl = VT[:, kt, c0:c0 + 64]
                            first = (ti == 0); last = (ti == len(tiles) - 1)

                            def mmo(e):
                                e.matmul(O[gs, :], vl, PT[0:rows, :], start=first, stop=last)
                                return e.matmul(DN[gs, :], self.ones_b[0:rows, 0:64], PT[0:rows, :], start=first, stop=last)
                            S.op("pe", mmo, reads=[ptb, VTB, kcb, self.constb], writes=[Ob, Db])
                            if br == 0:
                                pd2, pdb2 = ms_ring.get()
                                S.op("pe", lambda e: e.matmul(pd2[0:127, :], self.ones_b[0:127, 0:127], PT[0:127, :], start=True, stop=True),
                                     reads=[ptb, self.constb], writes=[pdb2])
                                S.op("dve", lambda e: e.tensor_scalar(RR[0:127, :], pd2[0:127, :], 1e-30, None, ALU.max), reads=[pdb2, rrb], writes=[rrb])
                                S.op("dve", lambda e: e.reciprocal(RR[0:127, :], RR[0:127, :]), reads=[rrb], writes=[rrb])
                                S.op("dve", lambda e: e.tensor_tensor(PN[0:127, :], PT[0:127, :], RR[0:127, :], ALU.mult), reads=[rrb, ptb, pnb], writes=[pnb])
                                if i >= 8:
                                    pim, pimb = ms_ring.get()

                                    def mmi(e):
                                        for j in range(4):
                                            ins = e.matmul(pim[:, 0:32], PN[0:127, j * 128:(j + 1) * 128], OV[0:127, :], start=(j == 0), stop=(j == 3))
                                        return ins
                                    S.op("pe", mmi, reads=[pnb, cb], writes=[pimb])
                                    o0 = 32 - 2 * i
                                    S.op("dve", lambda e: e.tensor_tensor(IMP[:], pim[:, 0:32], AB[:, 0, o0:o0 + 32], ALU.mult), reads=[pimb, cb, ib], writes=[ib])
                                    S.op("dve", lambda e: e.tensor_tensor(IMP[:], IMP[:], AB[:, 1, o0:o0 + 32], ALU.add), reads=[ib, cb], writes=[ib])
                                    S.op("dve", lambda e: e.memset(IMP[:, 0:1], 1e6), reads=[ib], writes=[ib])
                                    S.op("dve", lambda e: e.max(MX[:], IMP[:]), reads=[ib], writes=[ib])
                                    S.op("dve", lambda e: e.match_replace(IM2[:], MX[:], IMP[:], 0.0), reads=[ib], writes=[ib])
                                    S.op("dve", lambda e: e.max(MX[:], IM2[:]), reads=[ib], writes=[ib])
                                    S.op("dve", lambda e: e.match_replace(IM2[:], MX[:], IM2[:], 0.0), reads=[ib], writes=[ib])
                                    S.op("dve", lambda e: e.tensor_tensor(IM2[:], IMP[:], IM2[:], ALU.subtract), reads=[ib], writes=[ib])
                                    S.op("dve", lambda e: e.tensor_scalar(IM2[:], IM2[:], 0.0, None, ALU.is_gt), reads=[ib], writes=[ib])
                                    S.op("dve", lambda e: e.tensor_scalar(IM2[:], IM2[:], 30000.0, -30000.0, ALU.mult, ALU.add), reads=[ib], writes=[ib])
                                    ptr, ptrb = ms_ring.get()
                                    S.op("pe", lambda e: e.transpose(ptr[0:32, 0:128], IM2[:], self.ident_f[:]), reads=[ib, self.constb], writes=[ptrb])
                                    S.op("dve", lambda e: e.tensor_copy(NMT[g][:], ptr[0:32, 0:128].unsqueeze(1).to_broadcast([32, 4, 128])),
                                         reads=[ptrb], writes=[nmb[g]])
                    S.op("dve", lambda e: e.tensor_scalar(RR[:], DN[:], 1e-30, None, ALU.max), reads=[Db, rrb], writes=[rrb])
                    S.op("dve", lambda e: e.reciprocal(RR[:], RR[:]), reads=[rrb], writes=[rrb])
                    pgb_, pgbb = ms_ring.get()

                    def mmgb(e):
                        for j in range(4):
                            ins = e.matmul(pgb_[:, j * 128:(j + 1) * 128], SELG[:, br * 4 + j, :], GS[:], start=True, stop=True)
                        return ins
                    S.op("pe", mmgb, reads=[gsb, cb], writes=[pgbb])
                    S.op("dve", lambda e: e.tensor_tensor(RR[:], RR[:], pgb_[:], ALU.mult), reads=[rrb, pgbb], writes=[rrb])
                    if br == 0:
                        S.op("dve", lambda e: e.tensor_tensor(YA[:], O[:], RR[:], ALU.mult), reads=[Ob, rrb, yab], writes=[yab])
                    else:
                        S.op("dve", lambda e: e.tensor_tensor(RR[:], O[:], RR[:], ALU.mult), reads=[Ob, rrb], writes=[rrb])
                        S.op("dve", lambda e: e.tensor_tensor(YA[:], YA[:], RR[:], ALU.add), reads=[rrb, yab], writes=[yab])
                S.op("act", lambda e: e.copy(self.Y[1][:, :, tq], YA[:].rearrange("p (j q) -> p j q", j=4)), reads=[yab], writes=[self.YB[1][tcix]])
            S.full_barrier()
            self.st = old

    def mem_branch(self, memT, wk_d, wv_d, wqm_d):
        S = self.S
        with ExitStack() as st4:
            old, self.st = self.st, st4
            self._norm_rings_open()
            WQ = self.sb("WQM", [128, NCH, 512], BF16); WQB = Buf()
            KHT = self.sb("KHT", [128, 4, 256], BF16); KHTB = Buf()
            VH = self.sb("VH", [128, 2, 512], BF16); VHB = Buf()
            st5 = ExitStack()
            self.st = st5
            MT = self.sb("MT", [128, NCH, 256], F32); MTB = Buf()
            MN = self.sb("MN", [128, NCH, 256], BF16); MNB = Buf()
            WK = self.sb("WK", [128, NCH, 512], BF16); WKB = Buf()
            WV = self.sb("WV", [128, NCH, 512], BF16); WVB = Buf()
            mr = self.sb("mrstd", [128, 256], F32); mrb = Buf()
            S.dma(MT[:], memT.rearrange("(c p) m -> p c m", p=128), writes=[MTB])
            self.load_w(WK[:], wk_d, WKB)
            self.load_w(WV[:], wv_d, WVB)
            self.load_w(WQ[:], wqm_d, WQB)
            g0, _ = COLS["mem_norm"]
            pt, pb = self.psum.get()
            for c in range(NCH):
                sq, sqb = self.sq_ring.get()
                S.op("act", lambda e: e.activation(sq[:, 0:256], MT[:, c, :], AF.Square), reads=[MTB], writes=[sqb])
                S.op("pe", lambda e: e.matmul(pt[:, 0:256], self.ones_f[:], sq[:, 0:256], start=(c == 0), stop=(c == NCH - 1)),
                     reads=[sqb, self.constb], writes=[pb])
            S.op("act", lambda e: e.activation(mr[:], pt[:, 0:256], AF.Sqrt, bias=self.eps_t[:], scale=1.0 / D),
                 reads=[pb, self.constb], writes=[mrb])
            S.op("dve", lambda e: e.reciprocal(mr[:], mr[:]), reads=[mrb], writes=[mrb])
            for c in range(NCH):
                S.op("dve", lambda e: e.scalar_tensor_tensor(MN[:, c, :], MT[:, c, :], self.cols[:, g0 + c:g0 + c + 1], mr[:],
                                                             ALU.mult, ALU.mult),
                     reads=[MTB, mrb, self.constb], writes=[MNB])
            for h in range(4):
                p, pb = self.psum.get()

                def mm(e):
                    for k in range(NCH):
                        ins = e.matmul(p[:, 0:256], WK[:, k, h * 128:(h + 1) * 128], MN[:, k, :], start=(k == 0), stop=(k == NCH - 1))
                    return ins
                S.op("pe", mm, reads=[WKB, MNB], writes=[pb])
                S.op("act", lambda e: e.copy(KHT[:, h, :], p[:, 0:256]), reads=[pb], writes=[KHTB])
            for mt in range(2):
                p, pb = self.psum.get()

                def mm(e):
                    for k in range(NCH):
                        ins = e.matmul(p[:], MN[:, k, mt * 128:(mt + 1) * 128], WV[:, k, :], start=(k == 0), stop=(k == NCH - 1))
                    return ins
                S.op("pe", mm, reads=[WVB, MNB], writes=[pb])
                S.op("act", lambda e: e.copy(VH[:, mt, :], p[:]), reads=[pb], writes=[VHB])
            S.full_barrier()
            st5.close()
            self.st = st4
            qm_ring = Ring([self.sb("qm%d" % i, [128, TC], BF16) for i in range(2)])
            pt_ring = Ring([self.sb("pt%d" % i, [128, 2, TC], BF16) for i in range(2)])
            rd_ring = self.sq_ring
            scale = 128.0 ** -0.5
            for tc in range(NTC):
                ts = slice(tc * TC, (tc + 1) * TC)
                hreads = [self.HNB[c][tc] for c in range(NCH)]
                for h in range(4):
                    p, pb = self.psum.get()

                    def mm(e):
                        for k in range(NCH):
                            ins = e.matmul(p[:], WQ[:, k, h * 128:(h + 1) * 128], self.HN[:, k, ts], start=(k == 0), stop=(k == NCH - 1))
                        return ins
                    S.op("pe", mm, reads=hreads + [WQB], writes=[pb])
                    qm, qmb = qm_ring.get()
                    S.op("dve", lambda e: e.tensor_copy(qm[:], p[:]), reads=[pb], writes=[qmb])
                    ptile, ptb = pt_ring.get()
                    for mt in range(2):
                        ps_, psb = self.psum.get()
                        S.op("pe", lambda e: e.matmul(ps_[:], KHT[:, h, mt * 128:(mt + 1) * 128], qm[:], start=True, stop=True),
                             reads=[KHTB, qmb], writes=[psb])
                        S.op("act", lambda e: e.activation(ptile[:, mt, :], ps_[:], AF.Exp, scale=scale), reads=[psb], writes=[ptb])
                    po, pob = self.psum.get()
                    pd, pdb = self.psum.get()

                    def mm_o(e):
                        for mt in range(2):
                            ins = e.matmul(po[:], VH[:, mt, h * 128:(h + 1) * 128], ptile[:, mt, :], start=(mt == 0), stop=(mt == 1))
                        return ins

                    def mm_d(e):
                        for mt in range(2):
                            ins = e.matmul(pd[:], self.ones_b[:], ptile[:, mt, :], start=(mt == 0), stop=(mt == 1))
                        return ins
                    S.op("pe", mm_o, reads=[VHB, ptb], writes=[pob])
                    S.op("pe", mm_d, reads=[ptb, self.constb], writes=[pdb])
                    rd, rdb = rd_ring.get()
                    S.op("dve", lambda e: e.reciprocal(rd[:], pd[:]), reads=[pdb], writes=[rdb])
                    S.op("dve", lambda e: e.tensor_tensor(self.Y[2][:, h, ts], po[:], rd[:], ALU.mult),
                         reads=[pob, rdb], writes=[self.YB[2][tc]])
            S.full_barrier()
            self.st = old
        self._nst.close()

    def fold(self, br, wgb_d, wbr_d, first):
        S = self.S
        with ExitStack() as st4:
            old, self.st = self.st, st4
            WGB = [self.sb("WGBr%d" % i, [128, NCH, 128], BF16) for i in range(2)]; WGBB = [Buf(), Buf()]
            WBR = [self.sb("WBR%d" % i, [128, 4, 128], BF16) for i in range(2)]; WBRB = [Buf(), Buf()]
            gt_ring = Ring([self.sb("gt%d" % i, [128, TC], F32) for i in range(2)])
            t_ring = Ring([self.sb("mt%d" % i, [128, TC], F32) for i in range(2)])

            def load(dc):
                sl = dc % 2
                c0 = br * D + dc * 128
                S.dma(WGB[sl][:], wgb_d[:, c0:c0 + 128].rearrange("(k p) n -> p k n", p=128), writes=[WGBB[sl]], queue="pool")
                S.dma(WBR[sl][:], wbr_d[:, dc * 128:(dc + 1) * 128].rearrange("(k p) n -> p k n", p=128), writes=[WBRB[sl]], queue="pool")
            load(0)
            for dc in range(NCH):
                if dc + 1 < NCH:
                    load(dc + 1)
                sl = dc % 2
                for tc in range(NTC):
                    ts = slice(tc * TC, (tc + 1) * TC)
                    hreads = [self.HNB[c][tc] for c in range(NCH)]
                    pg, pgb = self.psum.get()
                    py, pyb = self.psum.get()

                    def mm_g(e):
                        for k in range(NCH):
                            ins = e.matmul(pg[:], WGB[sl][:, k, :], self.HN[:, k, ts], start=(k == 0), stop=(k == NCH - 1))
                        return ins

                    def mm_y(e):
                        for k in range(4):
                            ins = e.matmul(py[:], WBR[sl][:, k, :], self.Y[br][:, k, ts], start=(k == 0), stop=(k == 3))
                        return ins
                    S.op("pe", mm_g, reads=hreads + [WGBB[sl]], writes=[pgb])
                    S.op("pe", mm_y, reads=[self.YB[br][tc], WBRB[sl]], writes=[pyb])
                    gt, gtb = gt_ring.get()
                    S.op("act", lambda e: e.activation(gt[:], pg[:], AF.Sigmoid), reads=[pgb], writes=[gtb])
                    if first:
                        S.op("dve", lambda e: e.tensor_tensor(self.M[:, dc, ts], gt[:], py[:], ALU.mult),
                             reads=[gtb, pyb], writes=[self.MB[dc][tc]])
                    else:
                        t, tb = t_ring.get()
                        S.op("dve", lambda e: e.tensor_tensor(t[:], gt[:], py[:], ALU.mult), reads=[gtb, pyb], writes=[tb])
                        S.op("pool", lambda e: e.tensor_tensor(self.M[:, dc, ts], self.M[:, dc, ts], t[:], ALU.add),
                             reads=[tb, self.MB[dc][tc]], writes=[self.MB[dc][tc]])
            S.full_barrier()
            self.st = old

    def outproj(self, wout_d):
        S = self.S
        with ExitStack() as st4:
            old, self.st = self.st, st4
            WO = self.sb("WO", [128, NCH, D], BF16); WOB = Buf()
            self.load_w(WO[:], wout_d, WOB)
            for tc in range(NTC):
                ts = slice(tc * TC, (tc + 1) * TC)
                for d2 in range(NCH):
                    po, pob = self.psum.get()

                    def mm(e):
                        for k in range(NCH):
                            ins = e.matmul(po[:], WO[:, k, d2 * 128:(d2 + 1) * 128], self.M[:, k, ts], start=(k == 0), stop=(k == NCH - 1))
                        return ins
                    S.op("pe", mm, reads=[self.MB[k][tc] for k in range(NCH)] + [WOB], writes=[pob])
                    S.op("dve", lambda e: e.tensor_tensor(self.X[:, d2, ts], po[:], self.X[:, d2, ts], ALU.add),
                         reads=[pob, self.XB[d2][tc]], writes=[self.XB[d2][tc]])
            S.full_barrier()
            self.st = old

    def final_norm_out(self, outT):
        S = self.S
        g0, _ = COLS["final_norm"]
        self._norm_rings_open()
        for tc in range(NTC):
            ts = slice(tc * TC, (tc + 1) * TC)
            pt, pb = self.psum.get()
            for c in range(NCH):
                sq, sqb = self.sq_ring.get()
                S.op("act", lambda e: e.activation(sq[:], self.X[:, c, ts], AF.Square),
                     reads=[self.XB[c][tc]], writes=[sqb])
                S.op("pe", lambda e: e.matmul(pt[:], self.ones_f[:], sq[:], start=(c == 0), stop=(c == NCH - 1)),
                     reads=[sqb, self.constb], writes=[pb])
            rs, rsb = self.rstd_ring.get()
            S.op("act", lambda e: e.activation(rs[:], pt[:], AF.Sqrt, bias=self.eps_t[:], scale=1.0 / D),
                 reads=[pb, self.constb], writes=[rsb])
            S.op("dve", lambda e: e.reciprocal(rs[:], rs[:]), reads=[rsb], writes=[rsb])
            for c in range(NCH):
                S.op("dve", lambda e: e.scalar_tensor_tensor(
                    self.X[:, c, ts], self.X[:, c, ts], self.cols[:, g0 + c:g0 + c + 1], rs[:],
                    ALU.mult, ALU.mult),
                    reads=[self.XB[c][tc], rsb, self.constb], writes=[self.XB[c][tc]])
                S.dma(outT[c * 128:(c + 1) * 128, ts], self.X[:, c, ts], reads=[self.XB[c][tc]])
        self._norm_rings_close()

    def dump_x(self, name):
        o = self.dout(name, [D, S_LEN])
        for c in range(NCH):
            for tc in range(NTC):
                ts = slice(tc * TC, (tc + 1) * TC)
                self.S.dma(o[c * 128:(c + 1) * 128, ts], self.X[:, c, ts], reads=[self.XB[c][tc]])

    def build(self, stop_after=None):
        nc = self.nc
        dbg = self.debug
        xT = self.din("xT", [D, S_LEN])
        cols_d = self.din("cols", [128, NCOLS])
        f1g = self.din("ffn1_w_gate", [D, DFF]); f1u = self.din("ffn1_w_up", [D, DFF]); f1d = self.din("ffn1_w_down", [DFF, D])
        f2g = self.din("ffn2_w_gate", [D, DFF]); f2u = self.din("ffn2_w_up", [D, DFF]); f2d = self.din("ffn2_w_down", [DFF, D])
        memT = self.din("memT", [D, 256])
        mem_wk = self.din("mem_w_k", [D, 512]); mem_wv = self.din("mem_w_v", [D, 512])
        w_qm = self.din("w_qm", [D, 512])
        w_gb = self.din("w_gb", [D, 3 * D])
        w_br = [self.din(n, [512, D]) for n in ("w_br_rwkv", "w_br_nsa_p", "w_br_mem")]
        w_out = self.din("w_out", [D, D])
        w_rwkv = self.din("w_rwkv", [D, 1792])
        w2_d = self.din("rwkv_w2", [64, 512]); a2_d = self.din("rwkv_a2", [64, 512]); g2_d = self.din("rwkv_g2", [128, 512])
        gng_d = self.din("gng_rep", [128, 512]); gnb_d = self.din("gnb_rep", [128, 512])
        ident_d = self.din("ident", [128, 128])
        rmk_d = self.din("rwkv_masks", [128, 3, 2, 128])
        nd = {}
        nd["w_qn"] = self.din("w_qn", [D, 512]); nd["w_gn"] = self.din("w_gn", [D, 24]); nd["w_kvn"] = self.din("w_kvn", [D, 768])
        nd["shcf"] = self.din("shcf", [32, 247]); nd["efull"] = self.din("efull", [32, S_LEN]); nd["ov"] = self.din("ov", [127, 32])
        nd["abf"] = self.din("abf", [128, 2, 64]); nd["selg"] = self.din("selg", [24, 12, 128])
        nd["t31"] = self.din("t31", [128, 2, 512])
        nd["bmg"] = [self.din("bmg%d" % k, [128, 2, 512]) for k in range(3)]
        nd["msk"] = [self.din("msk%d" % k, [128, 128]) for k in range(3)]
        nd["bvcg"] = self.din("bvcg", [32, 2, 512]); nd["mskc"] = self.din("mskc", [32, 128])
        nd["cmp_w1"] = [self.din("cmp_k_w1", [2048, 256]), self.din("cmp_v_w1", [2048, 256])]
        nd["cmp_w2"] = [self.din("cmp_k_w2", [256, 64]), self.din("cmp_v_w2", [256, 64])]
        nd["cmp_peT"] = [self.din("cmp_pe_kT", [64, 32]), self.din("cmp_pe_vT", [64, 32])]
        outT = self.dout("outT", [D, S_LEN])
        with ExitStack() as st:
            self.st = st
            S = self.S = Sched(nc, st)
            self.X = self.sb("X", [128, NCH, S_LEN], F32)
            self.XB = [[Buf() for _ in range(NTC)] for _ in range(NCH)]
            self.cols = self.sb("cols", [128, NCOLS], F32)
            self.ones_f = self.sb("ones_f", [128, 128], F32)
            self.ones_b = self.sb("ones_b", [128, 128], BF16)
            self.eps_t = self.sb("eps_t", [128, 1], F32)
            self.gneps_t = self.sb("gneps_t", [128, 1], F32)
            self.ident_f = self.sb("ident_f", [128, 128], F32)
            self.constb = Buf("const")
            self.PS = self.ps("PSALL", [128, 8, 512])
            self.banks = [self.PS[:, i, :] for i in range(8)]
            self.bankb = [Buf() for _ in range(8)]
            self.psum = Ring(self.banks, self.bankb)
            S.dma(self.cols[:], cols_d, writes=[self.constb])
            S.op("dve", lambda e: e.memset(self.ones_f[:], 1.0), reads=[self.constb], writes=[self.constb])
            S.op("dve", lambda e: e.memset(self.ones_b[:], 1.0), reads=[self.constb], writes=[self.constb])
            S.op("dve", lambda e: e.memset(self.eps_t[:], EPS), reads=[self.constb], writes=[self.constb])
            S.op("dve", lambda e: e.memset(self.gneps_t[:], 64e-5), reads=[self.constb], writes=[self.constb])
            S.dma(self.ident_f[:], ident_d, reads=[self.constb], writes=[self.constb])
            for c in range(NCH):
                for tc in range(NTC):
                    ts = slice(tc * TC, (tc + 1) * TC)
                    S.dma(self.X[:, c, ts], xT[c * 128:(c + 1) * 128, ts], writes=[self.XB[c][tc]])

            def ffn_phase(wg, wu, wd, gname):
                with ExitStack() as st2:
                    self.st = st2
                    self.HN = self.sb("HN", [128, NCH, S_LEN], BF16)
                    self.HNB = [[Buf() for _ in range(NTC)] for _ in range(NCH)]
                    self.WG = [self.sb("WG%d" % i, [128, NCH, 512], BF16) for i in range(2)]
                    self.WU = [self.sb("WU%d" % i, [128, NCH, 512], BF16) for i in range(2)]
                    self.WD = [self.sb("WD%d" % i, [128, 4, D], BF16) for i in range(2)]
                    self.WGB = [Buf() for _ in range(2)]; self.WUB = [Buf() for _ in range(2)]; self.WDB = [Buf() for _ in range(2)]
                    self.a_ring = Ring([self.sb("a%d" % i, [128, 4, TC], BF16) for i in range(2)])
                    self.sg_ring = Ring([self.sb("sg%d" % i, [128, TC], F32) for i in range(2)])
                    self.ffn(wg, wu, wd, gname)
                    S.full_barrier()
                    self.st = st

            if "noffn1" not in dbg:
                ffn_phase(f1g, f1u, f1d, "ffn1_norm")
            if "x1" in dbg:
                self.dump_x("dbg_x1")
            if stop_after != "ffn1":
                with ExitStack() as st3:
                    self.st = st3
                    self.HN = self.sb("HN", [128, NCH, S_LEN], BF16)
                    self.HNB = [[Buf() for _ in range(NTC)] for _ in range(NCH)]
                    Yt = self.sb("Yt", [128, 4, S_LEN], BF16)
                    YBt = [Buf() for _ in range(NTC)]
                    self.Y = [Yt, Yt, Yt]
                    self.YB = [YBt, YBt, YBt]
                    self.rmsnorm_to_hn("mix_norm")
                    if "norwkv" not in dbg:
                        if "rwkvseq" in dbg:
                            self.rwkv_branch_seq(w_rwkv, w2_d, a2_d, g2_d, gng_d, gnb_d)
                        else:
                            self.rwkv_branch(w_rwkv, w2_d, a2_d, g2_d, gng_d, gnb_d, rmk_d)
                    else:
                        S.op("pool", lambda e: e.memset(Yt[:], 0.0), writes=YBt)
                    if "y_rwkv" in dbg:
                        self.dump_feat("dbg_y_rwkv", Yt, 4, YBt)
                    self.M = self.sb("M", [128, NCH, S_LEN], BF16)
                    self.MB = [[Buf() for _ in range(NTC)] for _ in range(NCH)]
                    do_merge = stop_after != "mix"
                    if do_merge:
                        self.fold(0, w_gb, w_br[0], True)
                    if "nomem" not in dbg:
                        self.mem_branch(memT, mem_wk, mem_wv, w_qm)
                    else:
                        S.op("pool", lambda e: e.memset(Yt[:], 0.0), writes=YBt)
                    if "y_mem" in dbg:
                        self.dump_feat("dbg_y_mem", Yt, 4, YBt)
                    if do_merge:
                        self.fold(2, w_gb, w_br[2], False)
                    if "nonsa" not in dbg:
                        self.nsa_branch(nd)
                    else:
                        S.op("pool", lambda e: e.memset(Yt[:], 0.0), writes=YBt)
                    if "y_nsa" in dbg:
                        self.dump_feat("dbg_y_nsa_p", Yt, 4, YBt)
                    if do_merge:
                        self.fold(1, w_gb, w_br[1], False)
                        self.outproj(w_out)
                    S.full_barrier()
                    self.st = st
                if "x2" in dbg:
                    self.dump_x("dbg_x2")
                if stop_after not in ("mix", "merge"):
                    ffn_phase(f2g, f2u, f2d, "ffn2_norm")
            self.final_norm_out(outT)
            S.wait_all_dma("sp")
            S.wait_all_dma("pool")
        return nc


NSA_PERM = np.concatenate([np.concatenate([np.arange(64 * j, 64 * j + 64), np.arange(64 * (4 + j), 64 * (4 + j) + 64)])
                           for j in range(4)])


def _t5_bucket_np(dist):
    n = np.maximum(dist, 0)
    nf = np.maximum(n, 1).astype(np.float32)
    large = 16 + (np.log(nf / np.float32(16)) / np.float32(math.log(128 / 16)) * np.float32(16)).astype(np.int32)
    large = np.minimum(large, 31)
    return np.where(n < 16, n, large)


def _nsa_consts(rel_bias):
    rb = np.asarray(rel_bias, np.float32)
    c = np.arange(128)[:, None]; p = np.arange(128)[None, :]
    out = {}
    hd = np.arange(8).reshape(2, 4)
    dists = [p - c, 128 + p - c, 512 + p - c]
    valid = [p >= c, np.ones((128, 128), bool), c > p]
    for k in range(3):
        bk = _t5_bucket_np(dists[k])
        g = rb[bk[:, None, None, :], hd[None, :, :, None]]
        out["bmg%d" % k] = np.ascontiguousarray(g.reshape(128, 2, 512))
        out["msk%d" % k] = np.where(valid[k], 0.0, -30000.0).astype(np.float32)
    out["t31"] = np.ascontiguousarray(np.broadcast_to(rb[31][hd][None, :, :, None], (128, 2, 4, 128)).reshape(128, 2, 512))
    m = np.arange(32)[:, None]
    dc = p - 16 * (m - 8) - 31
    bk = _t5_bucket_np(dc)
    g = rb[bk[:, None, None, :], hd[None, :, :, None]]
    out["bvcg"] = np.ascontiguousarray(g.reshape(32, 2, 512))
    mk = np.where((dc >= 0) & (m < 16), 0.0, -30000.0).astype(np.float32)
    mk[17:] = 0.0
    out["mskc"] = mk
    shcf = np.zeros((32, 247), np.float32)
    for x in range(247):
        r = x - 112
        if 0 <= r < 16:
            shcf[r, x] = 1.0
        elif r >= 16:
            shcf[16, x] = 1.0
    out["shcf"] = shcf
    ef = np.zeros((32, S_LEN), np.float32)
    ef[np.arange(S_LEN) // 64, np.arange(S_LEN)] = 1.0
    out["efull"] = ef
    ic = np.arange(127)[:, None]; jb = np.arange(32)[None, :]
    out["ov"] = ((ic * 16 <= jb * 64 + 63) & (ic * 16 + 31 >= jb * 64)).astype(np.float32)
    ab = np.zeros((128, 2, 64), np.float32)
    for pp in range(128):
        curr = 1 if pp >= 64 else 0
        for mm in range(64):
            jr = mm - 32
            if jr <= curr - 2:
                ab[pp, 0, mm] = 1.0
            if jr in (curr, curr - 1):
                ab[pp, 1, mm] = 1e6
    out["abf"] = ab
    selg = np.zeros((24, 12, 128), np.float32)
    for br in range(3):
        for j in range(4):
            for mm in range(128):
                selg[br * 8 + (mm // 64) * 4 + j, br * 4 + j, mm] = 1.0
    out["selg"] = selg
    return out


def prep_inputs(inputs, b):
    m = {}
    m["xT"] = np.ascontiguousarray(inputs["x"][b].T)
    cols = np.zeros((128, NCOLS), np.float32)
    for n in ("ffn1_norm", "mix_norm", "ffn2_norm", "final_norm", "mem_norm"):
        c0, k = COLS[n]
        cols[:, c0:c0 + k] = _colpack(np.asarray(inputs[n]).reshape(-1))
    for n, src in (("mu", "rwkv_mu"), ("w0", "rwkv_w0"), ("a0", "rwkv_a0"), ("k_k", "rwkv_k_k"), ("k_a", "rwkv_k_a"), ("r_k", "rwkv_r_k"),
                   ("gn_g", "rwkv_gn_gain"), ("gn_b", "rwkv_gn_bias")):
        c0, k = COLS[n]
        cols[:, c0:c0 + k] = _colpack(np.asarray(inputs[src]).reshape(-1))
    m["cols"] = cols
    m["w_rwkv"] = np.ascontiguousarray(np.asarray(inputs["w_in"])[0][:, 0:1792])
    m["rwkv_w2"] = np.ascontiguousarray(np.asarray(inputs["rwkv_w2"])[0])
    m["rwkv_a2"] = np.ascontiguousarray(np.asarray(inputs["rwkv_a2"])[0])
    m["rwkv_g2"] = np.ascontiguousarray(np.asarray(inputs["rwkv_g2"])[0])
    m["gng_rep"] = np.ascontiguousarray(np.broadcast_to(np.asarray(inputs["rwkv_gn_gain"]).reshape(1, 512), (128, 512)))
    m["gnb_rep"] = np.ascontiguousarray(np.broadcast_to(np.asarray(inputs["rwkv_gn_bias"]).reshape(1, 512), (128, 512)))
    m["ident"] = np.eye(128, dtype=np.float32)
    si = np.arange(128)[:, None]; ti = np.arange(128)[None, :]
    same = (si // 64) == (ti // 64)
    mk = np.zeros((128, 3, 2, 128), np.float32)
    mk[:, 0, :, :] = np.where(same & (si < ti), -1.0, 0.0)[:, None, :]
    mk[:, 1, :, :] = np.where(same & (ti < si), -1.0, 0.0)[:, None, :]
    mk[:, 2, :, :] = np.where(same & (si <= ti), 1.0, 0.0)[:, None, :]
    m["rwkv_masks"] = mk
    w_in_ = np.asarray(inputs["w_in"])[0]
    m["w_qn"] = np.ascontiguousarray(w_in_[:, 1792:2304][:, NSA_PERM])
    m["w_kvn"] = np.ascontiguousarray(w_in_[:, 2304:3072])
    m["w_gn"] = np.ascontiguousarray(w_in_[:, 3072:3096])
    m.update(_nsa_consts(inputs["rel_bias"]))
    for n in ("cmp_k_w1", "cmp_v_w1", "cmp_k_w2", "cmp_v_w2"):
        m[n] = np.ascontiguousarray(np.asarray(inputs[n])[0])
    m["cmp_pe_kT"] = np.ascontiguousarray(np.asarray(inputs["cmp_pe_k"])[0].T)
    m["cmp_pe_vT"] = np.ascontiguousarray(np.asarray(inputs["cmp_pe_v"])[0].T)
    for n in ("ffn1_w_gate", "ffn1_w_up", "ffn1_w_down", "ffn2_w_gate", "ffn2_w_up", "ffn2_w_down",
              "mem_w_k", "mem_w_v", "w_br_rwkv", "w_br_mem", "w_out"):
        m[n] = np.ascontiguousarray(np.asarray(inputs[n])[0])
    m["memT"] = np.ascontiguousarray(inputs["mem"][b].T)
    w_in = np.asarray(inputs["w_in"])[0]
    m["w_qm"] = np.ascontiguousarray(w_in[:, 3096:3608])
    m["w_gb"] = np.ascontiguousarray(w_in[:, 3608:6680])
    m["w_br_nsa_p"] = np.ascontiguousarray(np.asarray(inputs["w_br_nsa"])[0][NSA_PERM, :])
    return m


_CACHE = {}


def kernel(**inputs):
    inputs = {k: np.asarray(v) for k, v in inputs.items()}
    if "nc" not in _CACHE:
        _CACHE["nc"] = Builder().build()
    nc = _CACHE["nc"]
    n = 8
    in_maps = [prep_inputs(inputs, b) for b in range(n)]
    res = run_bass_kernel_spmd(nc, in_maps, core_ids=list(range(n)))
    out = np.stack([np.ascontiguousarray(r["outT"].T) for r in res.results], axis=0)
    return out.astype(np.float32)
```

```python
import math
from contextlib import ExitStack
import numpy as np
import concourse.bass as bass
import concourse.mybir as mybir
from concourse.bass_utils import run_bass_kernel_spmd

F32 = mybir.dt.float32
BF16 = mybir.dt.bfloat16
AF = mybir.ActivationFunctionType
ALU = mybir.AluOpType
AX = mybir.AxisListType

D = 1024
S_LEN = 2048
DFF = 2816
NCH = 8
TC = 512
NTC = S_LEN // TC
EPS = 1e-6


class Buf:
    __slots__ = ("name", "last_w", "readers")

    def __init__(self, name=""):
        self.name = name
        self.last_w = None
        self.readers = []


class Sched:
    ENG = ("pe", "act", "dve", "pool", "sp")

    def __init__(self, nc, stack, n_dma_sems=16):
        self.nc = nc
        self.eng = {"pe": nc.tensor, "act": nc.scalar, "dve": nc.vector,
                    "pool": nc.gpsimd, "sp": nc.sync}
        self.sem = {}
        for e in ("pe", "act", "dve", "pool"):
            self.sem[e] = stack.enter_context(nc.semaphore("s_" + e))
        self.cnt = {e: 0 for e in ("pe", "act", "dve", "pool")}
        nq = {"sp": 28, "pool": 28, "act": 8}
        self.dsem = []
        self.qsems = {}
        for q, n in nq.items():
            self.qsems[q] = list(range(len(self.dsem), len(self.dsem) + n))
            for i in range(n):
                self.dsem.append(stack.enter_context(nc.semaphore("d%s%d" % (q, i))))
        self.dcnt = [0] * len(self.dsem)
        self.dnext = {q: 0 for q in nq}
        self.waited = {e: {} for e in self.ENG}
        self.n_ops = 0
        self.n_waits = 0

    def _semobj(self, key):
        return self.sem[key] if isinstance(key, str) else self.dsem[key]

    def _need(self, engine, toks):
        best = {}
        for t in toks:
            if t is None:
                continue
            key, val = t
            if best.get(key, 0) < val:
                best[key] = val
        w = self.waited[engine]
        for key, val in best.items():
            if w.get(key, 0) >= val:
                continue
            self.eng[engine].wait_ge(self._semobj(key), val)
            w[key] = val
            self.n_waits += 1

    @staticmethod
    def _deps(reads, writes):
        toks = []
        for b in reads:
            toks.append(b.last_w)
        for b in writes:
            toks.append(b.last_w)
            toks.extend(b.readers)
        return toks

    @staticmethod
    def _commit(tok, reads, writes):
        for b in reads:
            b.readers.append(tok)
            if len(b.readers) > 48:
                best = {}
                for k, v in b.readers:
                    if best.get(k, 0) < v:
                        best[k] = v
                b.readers = list(best.items())
        for b in writes:
            b.last_w = tok
            b.readers = []

    def op(self, engine, fn, reads=(), writes=()):
        self._need(engine, self._deps(reads, writes))
        ins = fn(self.eng[engine])
        self.cnt[engine] += 1
        ins.then_inc(self.sem[engine], 1)
        tok = (engine, self.cnt[engine])
        self._commit(tok, reads, writes)
        self.n_ops += 1
        return tok

    def dma(self, out_ap, in_ap, reads=(), writes=(), queue="sp", **kw):
        pool = self.qsems[queue]
        i = pool[self.dnext[queue]]
        self.dnext[queue] = (self.dnext[queue] + 1) % len(pool)
        prev = [(i, self.dcnt[i])] if self.dcnt[i] else []
        self._need(queue, self._deps(reads, writes) + prev)
        ins = self.eng[queue].dma_start(out=out_ap, in_=in_ap, **kw)
        self.dcnt[i] += 16
        ins.then_inc(self.dsem[i], 16)
        tok = (i, self.dcnt[i])
        self._commit(tok, reads, writes)
        self.n_ops += 1
        return tok

    def barrier(self, bufs):
        toks = []
        for b in bufs:
            toks.append(b.last_w)
            toks.extend(b.readers)
        for e in self.ENG:
            self._need(e, toks)

    def full_barrier(self):
        toks = [(e, self.cnt[e]) for e in ("pe", "act", "dve", "pool") if self.cnt[e]]
        toks += [(i, self.dcnt[i]) for i in range(len(self.dsem)) if self.dcnt[i]]
        for e in self.ENG:
            self._need(e, toks)

    def wait_all_dma(self, engine="sp"):
        for i in range(len(self.dsem)):
            if self.dcnt[i]:
                self.eng[engine].wait_ge(self.dsem[i], self.dcnt[i])


class Ring:
    def __init__(self, tiles, bufs=None):
        self.tiles = tiles
        self.bufs = bufs if bufs is not None else [Buf() for _ in tiles]
        self.i = 0

    def get(self):
        t, b = self.tiles[self.i], self.bufs[self.i]
        self.i = (self.i + 1) % len(self.tiles)
        return t, b

    def get_pair_idx(self):
        if self.i % 2:
            self.i = (self.i + 1) % len(self.tiles)
        k = self.i
        self.i = (self.i + 2) % len(self.tiles)
        return k, self.bufs[k], self.bufs[k + 1]


COLS = {}
_c = 0
for _n, _k in (("ffn1_norm", 8), ("mix_norm", 8), ("ffn2_norm", 8), ("final_norm", 8),
               ("mem_norm", 8), ("mu", 14), ("w0", 4), ("a0", 4), ("k_k", 4), ("k_a", 4), ("r_k", 4), ("gn_g", 4), ("gn_b", 4)):
    COLS[_n] = (_c, _k)
    _c += _k
NCOLS = _c


def _colpack(v):
    v = np.asarray(v, np.float32).reshape(-1, 128)
    return np.ascontiguousarray(v.T)


class Builder:
    def __init__(self, debug=()):
        self.debug = set(debug)
        self._rk_stage = 99
        self._rk_tiles = S_LEN // 128
        for d_ in self.debug:
            if d_.startswith("rkstage"):
                self._rk_stage = int(d_[7:])
            if d_.startswith("rktiles"):
                self._rk_tiles = int(d_[7:])
        self.nc = bass.Bass("TRN2", target_bir_lowering=False)
        self.dram_in = {}
        self.dram_out = {}

    def din(self, name, shape, dt=F32):
        t = self.nc.dram_tensor(name, list(shape), dt, kind="ExternalInput").ap()
        self.dram_in[name] = t
        return t

    def dout(self, name, shape, dt=F32):
        t = self.nc.dram_tensor(name, list(shape), dt, kind="ExternalOutput").ap()
        self.dram_out[name] = t
        return t

    def sb(self, name, shape, dt):
        self._uid = getattr(self, "_uid", 0) + 1
        return self.st.enter_context(self.nc.sbuf_tensor("sb%d_%s" % (self._uid, name), list(shape), dt))

    def ps(self, name, shape, dt=F32):
        return self.st.enter_context(self.nc.psum_tensor("ps_" + name, list(shape), dt))

    def _norm_rings_open(self):
        self._nst_old = self.st
        self._nst = ExitStack()
        self.st = self._nst
        self.sq_ring = Ring([self.sb("sq%d" % i, [128, TC], F32) for i in range(2)])
        self.rstd_ring = Ring([self.sb("RSTD%d" % i, [128, TC], F32) for i in range(2)])
        self.st = self._nst_old

    def _norm_rings_close(self):
        self.S.full_barrier()
        self._nst.close()

    def rmsnorm_to_hn(self, gname):
        S = self.S
        g0, _ = COLS[gname]
        self._norm_rings_open()
        for tc in range(NTC):
            ts = slice(tc * TC, (tc + 1) * TC)
            pt, pb = self.psum.get()
            for c in range(NCH):
                sq, sqb = self.sq_ring.get()
                S.op("act", lambda e: e.activation(sq[:], self.X[:, c, ts], AF.Square),
                     reads=[self.XB[c][tc]], writes=[sqb])
                S.op("pe", lambda e: e.matmul(pt[:], self.ones_f[:], sq[:], start=(c == 0), stop=(c == NCH - 1)),
                     reads=[sqb, self.constb], writes=[pb])
            rs, rsb = self.rstd_ring.get()
            S.op("act", lambda e: e.activation(rs[:], pt[:], AF.Sqrt, bias=self.eps_t[:], scale=1.0 / D),
                 reads=[pb, self.constb], writes=[rsb])
            S.op("dve", lambda e: e.reciprocal(rs[:], rs[:]), reads=[rsb], writes=[rsb])
            for c in range(NCH):
                S.op("dve", lambda e: e.scalar_tensor_tensor(
                    self.HN[:, c, ts], self.X[:, c, ts], self.cols[:, g0 + c:g0 + c + 1], rs[:],
                    ALU.mult, ALU.mult),
                    reads=[self.XB[c][tc], rsb, self.constb], writes=[self.HNB[c][tc]])

        self._norm_rings_close()

    def ffn(self, wg, wu, wd, gname):
        S = self.S
        self.rmsnorm_to_hn(gname)
        groups = [(i, min(4, 22 - i)) for i in range(0, 22, 4)]

        def load(gi):
            f0, nf = groups[gi]
            slot = gi % 2
            S.dma(self.WG[slot][:, :, 0:nf * 128],
                  wg[:, f0 * 128:(f0 + nf) * 128].rearrange("(k p) n -> p k n", p=128),
                  writes=[self.WGB[slot]], queue="pool")
            S.dma(self.WU[slot][:, :, 0:nf * 128],
                  wu[:, f0 * 128:(f0 + nf) * 128].rearrange("(k p) n -> p k n", p=128),
                  writes=[self.WUB[slot]], queue="pool")
            S.dma(self.WD[slot][:, 0:nf, :],
                  wd[f0 * 128:(f0 + nf) * 128, :].rearrange("(f p) n -> p f n", p=128),
                  writes=[self.WDB[slot]], queue="pool")

        load(0)
        for gi, (f0, nf) in enumerate(groups):
            if gi + 1 < len(groups):
                load(gi + 1)
            slot = gi % 2
            WG, WU, WD = self.WG[slot], self.WU[slot], self.WD[slot]
            for tc in range(NTC):
                ts = slice(tc * TC, (tc + 1) * TC)
                hreads = [self.HNB[c][tc] for c in range(NCH)]
                a_t, a_b = self.a_ring.get()
                for f in range(nf):
                    pg, pgb = self.psum.get()
                    pu, pub = self.psum.get()

                    def mm_g(e):
                        for k in range(NCH):
                            ins = e.matmul(pg[:], WG[:, k, f * 128:(f + 1) * 128], self.HN[:, k, ts],
                                           start=(k == 0), stop=(k == NCH - 1))
                        return ins

                    def mm_u(e):
                        for k in range(NCH):
                            ins = e.matmul(pu[:], WU[:, k, f * 128:(f + 1) * 128], self.HN[:, k, ts],
                                           start=(k == 0), stop=(k == NCH - 1))
                        return ins
                    S.op("pe", mm_g, reads=hreads + [self.WGB[slot]], writes=[pgb])
                    S.op("pe", mm_u, reads=hreads + [self.WUB[slot]], writes=[pub])
                    sg, sgb = self.sg_ring.get()
                    S.op("act", lambda e: e.activation(sg[:], pg[:], AF.Silu), reads=[pgb], writes=[sgb])
                    S.op("dve", lambda e: e.tensor_tensor(a_t[:, f, :], sg[:], pu[:], ALU.mult),
                         reads=[sgb, pub], writes=[a_b])
                for dc in range(NCH):
                    po, pob = self.psum.get()

                    def mm_d(e):
                        for f in range(nf):
                            ins = e.matmul(po[:], WD[:, f, dc * 128:(dc + 1) * 128], a_t[:, f, :],
                                           start=(f == 0), stop=(f == nf - 1))
                        return ins
                    S.op("pe", mm_d, reads=[a_b, self.WDB[slot]], writes=[pob])
                    S.op("dve", lambda e: e.scalar_tensor_tensor(
                        self.X[:, dc, ts], po[:], 0.5, self.X[:, dc, ts], ALU.mult, ALU.add),
                        reads=[pob, self.XB[dc][tc]], writes=[self.XB[dc][tc]])


    def load_w(self, tile_ap, dram_ap, buf):
        self.S.dma(tile_ap, dram_ap.rearrange("(k p) n -> p k n", p=128), writes=[buf], queue="pool")

    def dump_feat(self, name, tile, nchunks, buf_list):
        o = self.dout(name, [nchunks * 128, S_LEN])
        for c in range(nchunks):
            self.S.dma(o[c * 128:(c + 1) * 128, :], tile[:, c, :], reads=buf_list, queue="pool")


    def rwkv_branch_seq(self, w_rwkv, w2_d, a2_d, g2_d, gng_d, gnb_d):
        S = self.S
        CN = COLS
        NT = S_LEN // 128
        with ExitStack() as st4:
            old, self.st = self.st, st4
            WR = self.sb("WR", [128, NCH, 1792], BF16); WRB = Buf()
            W2 = self.sb("W2A2", [128, 512], F32); A2 = W2; G2 = self.sb("G2", [128, 512], F32)
            GNG = self.sb("GNG", [128, 512], F32); GNB = self.sb("GNB", [128, 512], F32)
            BO = self.sb("BO", [128, 128], F32); BOb = self.sb("BOb", [128, 128], BF16)
            ID2 = self.sb("ID2", [128, 64], BF16)
            OMK = self.sb("OMK", [128, 4], F32)
            cb = Buf()
            self.load_w(WR[:], w_rwkv, WRB)
            S.dma(W2[0:64, :], w2_d, writes=[cb]); S.dma(A2[64:128, :], a2_d, writes=[cb]); S.dma(G2[:], g2_d, writes=[cb])
            S.dma(GNG[:], gng_d, writes=[cb]); S.dma(GNB[:], gnb_d, writes=[cb])
            S.op("dve", lambda e: e.memset(BO[:], 0.0), reads=[cb], writes=[cb])
            S.op("dve", lambda e: e.memset(BO[0:64, 0:64], 1.0), reads=[cb], writes=[cb])
            S.op("dve", lambda e: e.memset(BO[64:128, 64:128], 1.0), reads=[cb], writes=[cb])
            S.op("dve", lambda e: e.tensor_copy(BOb[:], BO[:]), reads=[cb], writes=[cb])
            S.op("dve", lambda e: e.tensor_copy(ID2[0:64, :], self.ident_f[0:64, 0:64]), reads=[cb, self.constb], writes=[cb])
            S.op("dve", lambda e: e.tensor_copy(ID2[64:128, :], self.ident_f[64:128, 64:128]), reads=[cb, self.constb], writes=[cb])
            ka0 = CN["k_a"][0]
            S.op("dve", lambda e: e.tensor_scalar(OMK[:], self.cols[:, ka0:ka0 + 4], -1.0, 1.0, ALU.mult, ALU.add),
                 reads=[cb, self.constb], writes=[cb])
            P32 = self.sb("P32", [128, 14, 129], F32); P32B = Buf()
            DD = self.sb("DD", [128, 128], F32); DDB = Buf()
            CAR = self.sb("CAR", [128, 14, 1], F32)
            PL = P32[:, :, 1:129]; PLB = P32B
            TW = self.sb("TW", [64, 128], F32); SGg = self.sb("SGg", [128, 128], F32)
            WD = self.sb("WD", [128, 4, 128], F32); SIG = WD
            A32 = self.sb("A32", [128, 4, 128], F32)
            KK = self.sb("KK", [128, 4, 128], F32); SQ = self.sb("SQ", [128, 4, 128], F32)
            KKN = self.sb("KKN", [128, 4, 128], F32); NB = self.sb("NB", [128, 4, 128], F32)
            KM = self.sb("KM", [128, 4, 128], F32); BON = self.sb("BON", [128, 4, 128], F32)
            RM = self.sb("RM", [128, 4, 128, 2], BF16)
            VDr = Ring([self.sb("VD%d" % i, [128, 4, 64], BF16) for i in range(2)])
            H = self.sb("H", [128, 4, 64], F32); Hb = self.sb("Hb", [128, 4, 64], BF16); HK = self.sb("HK", [128, 4, 64], BF16)
            T1 = self.sb("T1", [128, 4, 64], F32); T2r = Ring([self.sb("T2_%d" % i, [128, 4, 64], F32) for i in range(2)])
            YST = [self.sb("YST%d" % i, [2, 4, 256], F32) for i in range(2)]; YSTB = [Buf(), Buf()]
            YTOK = A32[:].rearrange("p c t -> p (c t)").rearrange("p (c h v) -> p c h v", c=4, h=2); YTOKB = Buf()
            YC = KKN[:].rearrange("p c t -> p (c t)").rearrange("p (a v) -> p a v", a=8)
            ST8 = self.sb("ST8", [128, 8], F32); ST8b = self.sb("ST8b", [128, 8], F32)
            YF = SQ
            db = Buf(); hb = Buf(); hbb = Buf(); hkb = Buf(); t1b = Buf(); vrb = Buf(); vtb = Buf(); rmb = Buf(); yb = Buf()
            S.op("pool", lambda e: e.memset(P32[:], 0.0), writes=[P32B])
            S.op("pool", lambda e: e.memset(RM[:], 0.0), writes=[rmb])
            S.op("pool", lambda e: e.memset(H[:], 0.0), writes=[hb])
            mu0 = CN["mu"][0]; w00 = CN["w0"][0]; a00 = CN["a0"][0]; kk0 = CN["k_k"][0]; rk0 = CN["r_k"][0]
            ident = self.ident_f
            for i in range(NT):
                t0 = i * 128
                tcix = t0 // TC
                tsl = slice(t0, t0 + 128)
                hreads = [self.HNB[c][tcix] for c in range(NCH)]
                for cg in range(4):
                    c0 = cg * 4
                    n = min(4, 14 - c0)
                    p, pb = self.psum.get()

                    def mm(e):
                        for cc in range(n):
                            for k in range(NCH):
                                ins = e.matmul(p[:, cc * 128:(cc + 1) * 128], WR[:, k, (c0 + cc) * 128:(c0 + cc + 1) * 128],
                                               self.HN[:, k, tsl], start=(k == 0), stop=(k == NCH - 1))
                        return ins
                    S.op("pe", mm, reads=hreads + [WRB], writes=[pb])
                    S.op("act", lambda e: e.copy(P32[:, c0:c0 + n, 1:129], p[:, 0:n * 128].rearrange("p (c t) -> p c t", c=n)),
                         reads=[pb], writes=[P32B])
                S.op("dve", lambda e: e.tensor_copy(CAR[:], P32[:, :, 128:129]), reads=[P32B], writes=[DDB])
                for c in range(14):
                    S.op("dve", lambda e: e.tensor_tensor(DD[:], P32[:, c, 0:128], P32[:, c, 1:129], ALU.subtract), reads=[P32B, DDB], writes=[DDB])
                    S.op("dve", lambda e: e.scalar_tensor_tensor(P32[:, c, 1:129], DD[:], self.cols[:, mu0 + c:mu0 + c + 1], P32[:, c, 1:129],
                                                                 ALU.mult, ALU.add), reads=[DDB, P32B, self.constb], writes=[P32B])
                S.op("dve", lambda e: e.tensor_copy(P32[:, :, 0:1], CAR[:]), reads=[P32B, DDB], writes=[P32B])
                S.op("act", lambda e: e.activation(TW[:], PL[0:64, 12, :], AF.Tanh), reads=[PLB], writes=[db])
                S.op("act", lambda e: e.activation(SGg[:], PL[:, 13, :], AF.Sigmoid), reads=[PLB], writes=[db])
                pz, pzb = self.psum.get(); pa, pab = self.psum.get()

                def mmz(e):
                    for fc in range(4):
                        ins = e.matmul(pz[:, fc * 128:(fc + 1) * 128], W2[0:64, fc * 128:(fc + 1) * 128], TW[:], start=True, stop=True)
                    return ins

                def mma(e):
                    for fc in range(4):
                        ins = e.matmul(pa[:, fc * 128:(fc + 1) * 128], A2[64:128, fc * 128:(fc + 1) * 128], PL[64:128, 12, :], start=True, stop=True)
                    return ins

                S.op("pe", mmz, reads=[db, cb], writes=[pzb])
                S.op("pe", mma, reads=[PLB, cb], writes=[pab])
                for fc in range(4):
                    S.op("act", lambda e: e.activation(SIG[:, fc, :], pz[:, fc * 128:(fc + 1) * 128], AF.Sigmoid,
                                                       bias=self.cols[:, w00 + fc:w00 + fc + 1]), reads=[pzb, self.constb], writes=[db])
                    S.op("act", lambda e: e.activation(A32[:, fc, :], pa[:, fc * 128:(fc + 1) * 128], AF.Sigmoid,
                                                       bias=self.cols[:, a00 + fc:a00 + fc + 1]), reads=[pab, self.constb], writes=[db, YTOKB])
                S.op("act", lambda e: e.activation(WD[:], SIG[:], AF.Exp, scale=-0.6065306597126334), reads=[db], writes=[db])
                for fc in range(4):
                    S.op("dve", lambda e: e.tensor_scalar(KK[:, fc, :], PL[:, 4 + fc, :], self.cols[:, kk0 + fc:kk0 + fc + 1], None, ALU.mult),
                         reads=[PLB, self.constb], writes=[db])
                S.op("dve", lambda e: e.tensor_tensor(SQ[:], KK[:], KK[:], ALU.mult), reads=[db], writes=[db])
                pss, pssb = self.psum.get()
                S.op("pe", lambda e: e.matmul(pss[:], BO[:], SQ[:].rearrange("p c t -> p (c t)"), start=True, stop=True), reads=[db, cb], writes=[pssb])
                S.op("act", lambda e: e.activation(SQ[:], pss[:].rearrange("p (c t) -> p c t", c=4), AF.Sqrt), reads=[pssb, db], writes=[db])
                S.op("dve", lambda e: e.tensor_scalar(SQ[:], SQ[:], 1e-12, None, ALU.max), reads=[db], writes=[db])
                S.op("dve", lambda e: e.reciprocal(SQ[:], SQ[:]), reads=[db], writes=[db])
                S.op("dve", lambda e: e.tensor_tensor(KKN[:], KK[:], SQ[:], ALU.mult), reads=[db], writes=[db, yb])
                S.op("dve", lambda e: e.scalar_tensor_tensor(NB[:], KKN[:], -1.0, A32[:], ALU.mult, ALU.mult), reads=[db], writes=[db])
                for fc in range(4):
                    S.op("dve", lambda e: e.tensor_scalar(KK[:, fc, :], A32[:, fc, :], self.cols[:, ka0 + fc:ka0 + fc + 1], OMK[:, fc:fc + 1],
                                                          ALU.mult, ALU.add), reads=[db, cb, self.constb], writes=[db])
                S.op("dve", lambda e: e.tensor_tensor(KM[:], PL[:, 4:8, :], KK[:], ALU.mult), reads=[db, PLB], writes=[db])
                S.op("dve", lambda e: e.tensor_tensor(SQ[:], PL[:, 0:4, :], KM[:], ALU.mult), reads=[db, PLB], writes=[db])
                for fc in range(4):
                    S.op("dve", lambda e: e.tensor_scalar(SQ[:, fc, :], SQ[:, fc, :], self.cols[:, rk0 + fc:rk0 + fc + 1], None, ALU.mult),
                         reads=[db, self.constb], writes=[db])
                pbn, pbnb = self.psum.get()
                S.op("pe", lambda e: e.matmul(pbn[:], BO[:], SQ[:].rearrange("p c t -> p (c t)"), start=True, stop=True), reads=[db, cb], writes=[pbnb])
                S.op("dve", lambda e: e.tensor_tensor(BON[:], pbn[:].rearrange("p (c t) -> p c t", c=4), PL[:, 8:12, :], ALU.mult),
                     reads=[pbnb, PLB], writes=[db])
                S.op("dve", lambda e: e.tensor_copy(RM[0:64, :, :, 0], PL[0:64, 0:4, :]), reads=[PLB, rmb], writes=[rmb])
                S.op("dve", lambda e: e.tensor_copy(RM[64:128, :, :, 1], PL[64:128, 0:4, :]), reads=[PLB, rmb], writes=[rmb])
                for tt in range(128):
                    pvb_t, pvbb = self.psum.get()

                    VD, vdb = VDr.get()
                    S.op("pool", lambda e: e.tensor_tensor(VD[:], ID2[:].unsqueeze(1).to_broadcast([128, 4, 64]),
                                                           PL[:, 8:12, tt:tt + 1].to_broadcast([128, 4, 64]), ALU.mult),
                         reads=[PLB, cb], writes=[vdb])
                    S.op("pe", lambda e: e.matmul(pvb_t[:, 0:256], BOb[:], VD[:].rearrange("p c v -> p (c v)"), start=True, stop=True),
                         reads=[vdb, cb], writes=[pvbb])
                    T2, t2b = T2r.get()
                    S.op("pool" if False else "dve", lambda e: e.tensor_tensor(
                        T2[:], pvb_t[:, 0:256].rearrange("p (c v) -> p c v", c=4), KM[:, :, tt:tt + 1].to_broadcast([128, 4, 64]), ALU.mult),
                        reads=[pvbb, db], writes=[t2b])
                    S.op("dve", lambda e: e.tensor_tensor(HK[:], H[:], KKN[:, :, tt:tt + 1].to_broadcast([128, 4, 64]), ALU.mult),
                         reads=[hb, db], writes=[hkb])
                    psa, psab = self.psum.get()
                    S.op("pe", lambda e: e.matmul(psa[:, 0:256], BOb[:], HK[:].rearrange("p c v -> p (c v)"), start=True, stop=True),
                         reads=[hkb, cb], writes=[psab])
                    S.op("dve", lambda e: e.tensor_tensor(T1[:], psa[:, 0:256].rearrange("p (c v) -> p c v", c=4),
                                                          NB[:, :, tt:tt + 1].to_broadcast([128, 4, 64]), ALU.mult),
                         reads=[psab, db], writes=[t1b])
                    S.op("dve", lambda e: e.tensor_tensor(H[:], H[:], WD[:, :, tt:tt + 1].to_broadcast([128, 4, 64]), ALU.mult),
                         reads=[hb, db], writes=[hb])
                    S.op("dve", lambda e: e.tensor_tensor(T1[:], T1[:], T2[:], ALU.add), reads=[t1b, t2b], writes=[t1b])
                    S.op("dve", lambda e: e.tensor_tensor(H[:], H[:], T1[:], ALU.add), reads=[hb, t1b], writes=[hb])
                    S.op("act", lambda e: e.copy(Hb[:], H[:]), reads=[hb], writes=[hbb])
                    py, pyb = self.psum.get()

                    def mmy(e):
                        for fc in range(4):
                            ins = e.matmul(py[0:2, fc * 64:(fc + 1) * 64], RM[:, fc, tt, :], Hb[:, fc, :], start=True, stop=True)
                        return ins
                    S.op("pe", mmy, reads=[hbb, rmb], writes=[pyb])
                    slot = tt % 2
                    S.op("act", lambda e: e.copy(YST[slot][0:2, 0, :], py[0:2, 0:256]), reads=[pyb], writes=[YSTB[slot]])
                    for hp in range(2):
                        S.dma(YTOK[tt:tt + 1, :, hp, :], YST[slot][hp:hp + 1, 0, :].rearrange("p (c v) -> p c v", c=4),
                              reads=[YSTB[slot], db], writes=[YTOKB])
                YT8 = YTOK.rearrange("t c h v -> t (c h) v")
                S.op("dve", lambda e: e.tensor_reduce(ST8[:], YT8, AX.X, ALU.add), reads=[YTOKB], writes=[yb])
                S.op("dve", lambda e: e.tensor_scalar(ST8[:], ST8[:], 1.0 / 64, None, ALU.mult), reads=[yb], writes=[yb])
                S.op("dve", lambda e: e.tensor_tensor(YC, YT8, ST8[:].unsqueeze(2).to_broadcast([128, 8, 64]), ALU.subtract),
                     reads=[YTOKB, yb], writes=[yb, db])
                S.op("dve", lambda e: e.tensor_tensor(YTOK.rearrange("t c h v -> t (c h) v"), YC, YC, ALU.mult), reads=[yb, YTOKB], writes=[YTOKB])
                S.op("dve", lambda e: e.tensor_reduce(ST8b[:], YT8, AX.X, ALU.add), reads=[YTOKB], writes=[yb])
                S.op("act", lambda e: e.activation(ST8b[:], ST8b[:], AF.Sqrt, bias=self.gneps_t[:], scale=1.0 / 64), reads=[yb, self.constb], writes=[yb])
                S.op("dve", lambda e: e.reciprocal(ST8b[:], ST8b[:]), reads=[yb], writes=[yb])
                S.op("dve", lambda e: e.tensor_tensor(YC, YC, ST8b[:].unsqueeze(2).to_broadcast([128, 8, 64]), ALU.mult), reads=[yb], writes=[yb])
                YCf = YC.rearrange("t a v -> t (a v)")
                S.op("dve", lambda e: e.tensor_tensor(YCf, YCf, GNG[:], ALU.mult), reads=[yb, cb], writes=[yb])
                S.op("dve", lambda e: e.tensor_tensor(YCf, YCf, GNB[:], ALU.add), reads=[yb, cb], writes=[yb])
                pyt, pytb = self.psum.get(); pg, pgb = self.psum.get()

                def mmt2(e):
                    for fc in range(4):
                        ins = e.transpose(pyt[:, fc * 128:(fc + 1) * 128], YC[:, 2 * fc:2 * fc + 2, :].rearrange("t a v -> t (a v)"), ident[:])
                    for fc in range(4):
                        ins = e.matmul(pg[:, fc * 128:(fc + 1) * 128], G2[:, fc * 128:(fc + 1) * 128], SGg[:], start=True, stop=True)
                    return ins
                S.op("pe", mmt2, reads=[yb, self.constb, db, cb], writes=[pytb, pgb])
                S.op("dve", lambda e: e.tensor_tensor(YF[:], pyt[:].rearrange("p (c t) -> p c t", c=4), BON[:], ALU.add), reads=[pytb, db], writes=[yb, db])
                S.op("dve", lambda e: e.tensor_tensor(self.Y[0][:, :, tsl], YF[:], pg[:].rearrange("p (c t) -> p c t", c=4), ALU.mult),
                     reads=[yb, db, pgb], writes=[self.YB[0][tcix]])
            S.full_barrier()
            self.st = old


    def rwkv_branch(self, w_rwkv, w2_d, a2_d, g2_d, gng_d, gnb_d, mk_d):
        S = self.S
        CN = COLS
        NT = S_LEN // 128
        CDEC = 0.6065306597126334
        with ExitStack() as st4:
            old, self.st = self.st, st4
            WR = self.sb("WR", [128, NCH, 1792], BF16); WRB = Buf()
            W2 = self.sb("W2A2", [128, 512], F32); A2 = W2; G2 = self.sb("G2", [128, 512], BF16)
            BO = self.sb("BO", [128, 128], F32)
            ID2 = self.sb("ID2", [128, 64], F32)
            OMK = self.sb("OMK", [128, 4], F32)
            MSK = self.sb("MSK", [128, 3, 128], BF16)
            ONE64 = self.sb("ONE64", [128, 64], F32)
            cb = Buf()
            self.load_w(WR[:], w_rwkv, WRB)
            S.dma(W2[0:64, :], w2_d, writes=[cb]); S.dma(A2[64:128, :], a2_d, writes=[cb]); S.dma(G2[:], g2_d, writes=[cb], queue="pool")
            S.dma(MSK[:], mk_d, writes=[cb], queue="pool")
            S.op("dve", lambda e: e.memset(BO[:], 0.0), reads=[cb], writes=[cb])
            S.op("dve", lambda e: e.memset(BO[0:64, 0:64], 1.0), reads=[cb], writes=[cb])
            S.op("dve", lambda e: e.memset(BO[64:128, 64:128], 1.0), reads=[cb], writes=[cb])
            S.op("dve", lambda e: e.memset(ONE64[:], 1.0), reads=[cb], writes=[cb])
            S.op("dve", lambda e: e.tensor_copy(ID2[0:64, :], self.ident_f[0:64, 0:64]), reads=[cb, self.constb], writes=[cb])
            S.op("dve", lambda e: e.tensor_copy(ID2[64:128, :], self.ident_f[64:128, 64:128]), reads=[cb, self.constb], writes=[cb])
            ka0 = CN["k_a"][0]
            S.op("dve", lambda e: e.tensor_scalar(OMK[:], self.cols[:, ka0:ka0 + 4], -1.0, 1.0, ALU.mult, ALU.add),
                 reads=[cb, self.constb], writes=[cb])
            P32 = self.sb("P32", [128, 14, 129], F32); P32B = Buf()
            DD = self.sb("DD", [128, 128], F32); DDB = Buf()
            CAR = self.sb("CAR", [128, 14, 1], F32)
            PL = P32[:, :, 1:129]; PLB = P32B
            TW = self.sb("TW", [64, 128], F32); SGg = self.sb("SGg", [128, 128], BF16)
            f32t = lambda n: self.sb(n, [128, 4, 128], F32)
            SIG = f32t("SIG"); CUM = f32t("CUM"); A32 = f32t("A32"); KK = f32t("KK"); SQ = f32t("SQ")
            KKN = f32t("KKN"); NB = f32t("NB"); KM = f32t("KM"); BON = f32t("BON")
            AH = self.sb("AH", [128, 4, 128], BF16); KH = self.sb("KH", [128, 4, 128], BF16)
            BR = self.sb("BR", [128, 4, 2, 128], BF16)
            AT = self.sb("AT", [128, 512], BF16); KTt = self.sb("KTt", [128, 512], BF16); VTOK = self.sb("VTOK", [128, 512], BF16)
            WB = self.sb("WB", [128, 8, 128], BF16); BU = self.sb("BU", [128, 8, 128], BF16)
            bf8 = lambda n: self.sb(n, [128, 8, 128], BF16)
            X0 = bf8("X0"); XT0 = bf8("XT0"); LKT = bf8("LKT"); GRA = bf8("GRA"); GRK = bf8("GRK"); TT = bf8("TT")
            XA1 = [self.sb("XA1_%d" % i, [128, 4, 128], BF16) for i in range(2)]
            XTA1 = [self.sb("XTA1_%d" % i, [128, 4, 128], BF16) for i in range(2)]
            TA1 = [self.sb("TA1_%d" % i, [128, 4, 128], BF16) for i in range(2)]
            RTm = self.sb("RTm", [128, 4, 2, 128], BF16)
            M0Ts = SIG[:].rearrange("p c t -> p (c t)").rearrange("p (a k) -> p a k", a=8)
            N0s = CUM[:].rearrange("p c t -> p (c t)").rearrange("p (a k) -> p a k", a=8)
            PCt = self.sb("PCt", [128, 2, 4], F32)
            H = self.sb("H", [128, 4, 64], F32); Hb = self.sb("Hb", [128, 2, 4, 64], BF16)
            nbb = Buf(); kmb = Buf(); sgb = Buf(); cub = Buf()
            YTOK = NB[:].rearrange("p c t -> p (c t)").rearrange("p (c h v) -> p c h v", c=4, h=2); YTOKB = nbb
            YC = KM[:].rearrange("p c t -> p (c t)").rearrange("p (a v) -> p a v", a=8)
            ST8 = self.sb("ST8", [128, 8], F32); ST8b = self.sb("ST8b", [128, 8], F32)
            YF = SQ
            db = Buf(); hb = Buf(); hbb = Buf(); gb_ = Buf(); chb = Buf(); tkb = Buf(); yb = Buf(); mnb = Buf(); rtb = Buf()
            S.op("pool", lambda e: e.memset(P32[:], 0.0), writes=[P32B])
            S.op("pool", lambda e: e.memset(RTm[:], 0.0), writes=[rtb])
            S.op("pool", lambda e: e.memset(H[:], 0.0), writes=[hb])
            mu0 = CN["mu"][0]; w00 = CN["w0"][0]; a00 = CN["a0"][0]; kk0 = CN["k_k"][0]; rk0 = CN["r_k"][0]
            gg0 = CN["gn_g"][0]; gb0 = CN["gn_b"][0]
            ident = self.ident_f
            c4 = lambda ap: ap.rearrange("p (c t) -> p c t", c=4)
            def emit_proj(i2):
                t0_ = i2 * 128
                tsl_ = slice(t0_, t0_ + 128)
                hreads_ = [self.HNB[c][t0_ // TC] for c in range(NCH)]
                for cg in range(4):
                    c0 = cg * 4
                    n = min(4, 14 - c0)
                    p, pb = self.psum.get()

                    def mm(e):
                        for cc in range(n):
                            for k in range(NCH):
                                ins = e.matmul(p[:, cc * 128:(cc + 1) * 128], WR[:, k, (c0 + cc) * 128:(c0 + cc + 1) * 128],
                                               self.HN[:, k, tsl_], start=(k == 0), stop=(k == NCH - 1))
                        return ins
                    S.op("pe", mm, reads=hreads_ + [WRB], writes=[pb])
                    S.op("act", lambda e: e.copy(P32[:, c0:c0 + n, 1:129], p[:, 0:n * 128].rearrange("p (c t) -> p c t", c=n)),
                         reads=[pb], writes=[P32B])

            def lerp_list(i2):
                ops = []
                ops.append(lambda: S.op("dve", lambda e: e.tensor_copy(CAR[:], P32[:, :, 128:129]), reads=[P32B], writes=[DDB]))
                for c in range(14):
                    def one(c=c):
                        S.op("dve", lambda e: e.tensor_tensor(DD[:], P32[:, c, 0:128], P32[:, c, 1:129], ALU.subtract), reads=[P32B, DDB], writes=[DDB])
                        S.op("dve", lambda e: e.scalar_tensor_tensor(P32[:, c, 1:129], DD[:], self.cols[:, mu0 + c:mu0 + c + 1], P32[:, c, 1:129],
                                                                     ALU.mult, ALU.add), reads=[DDB, P32B, self.constb], writes=[P32B])
                    ops.append(one)
                ops.append(lambda: S.op("dve", lambda e: e.tensor_copy(P32[:, :, 0:1], CAR[:]), reads=[P32B, DDB], writes=[P32B]))
                return ops

            pending = []
            for i in range(self._rk_tiles):
                t0 = i * 128
                tcix = t0 // TC
                tsl = slice(t0, t0 + 128)
                hreads = [self.HNB[c][tcix] for c in range(NCH)]
                if i == 0:
                    emit_proj(0)
                    for fn_ in lerp_list(0):
                        fn_()
                for fn_ in pending:
                    fn_()
                pending = []
                S.op("act", lambda e: e.activation(TW[:], PL[0:64, 12, :], AF.Tanh), reads=[PLB], writes=[db])
                S.op("act", lambda e: e.activation(SGg[:], PL[:, 13, :], AF.Sigmoid), reads=[PLB], writes=[db])
                pz, pzb = self.psum.get(); pa, pab = self.psum.get()

                def mmz(e):
                    for fc in range(4):
                        ins = e.matmul(pz[:, fc * 128:(fc + 1) * 128], W2[0:64, fc * 128:(fc + 1) * 128], TW[:], start=True, stop=True)
                    return ins

                def mma(e):
                    for fc in range(4):
                        ins = e.matmul(pa[:, fc * 128:(fc + 1) * 128], A2[64:128, fc * 128:(fc + 1) * 128], PL[64:128, 12, :], start=True, stop=True)
                    return ins
                S.op("pe", mmz, reads=[db, cb], writes=[pzb])
                S.op("pe", mma, reads=[PLB, cb], writes=[pab])
                for fc in range(4):
                    S.op("act", lambda e: e.activation(SIG[:, fc, :], pz[:, fc * 128:(fc + 1) * 128], AF.Sigmoid,
                                                       bias=self.cols[:, w00 + fc:w00 + fc + 1]), reads=[pzb, self.constb], writes=[db, sgb])
                    S.op("act", lambda e: e.activation(A32[:, fc, :], pa[:, fc * 128:(fc + 1) * 128], AF.Sigmoid,
                                                       bias=self.cols[:, a00 + fc:a00 + fc + 1]), reads=[pab, self.constb], writes=[db])
                for fc in range(4):
                    S.op("dve", lambda e: e.tensor_scalar(KK[:, fc, :], PL[:, 4 + fc, :], self.cols[:, kk0 + fc:kk0 + fc + 1], None, ALU.mult),
                         reads=[PLB, self.constb], writes=[db])
                S.op("dve", lambda e: e.tensor_tensor(SQ[:], KK[:], KK[:], ALU.mult), reads=[db], writes=[db])
                pss, pssb = self.psum.get()
                S.op("pe", lambda e: e.matmul(pss[:], BO[:], SQ[:].rearrange("p c t -> p (c t)"), start=True, stop=True), reads=[db, cb], writes=[pssb])
                S.op("act", lambda e: e.activation(SQ[:], c4(pss[:]), AF.Sqrt), reads=[pssb, db], writes=[db])
                S.op("dve", lambda e: e.tensor_scalar(SQ[:], SQ[:], 1e-12, None, ALU.max), reads=[db], writes=[db])
                S.op("dve", lambda e: e.reciprocal(SQ[:], SQ[:]), reads=[db], writes=[db])
                S.op("dve", lambda e: e.tensor_tensor(KKN[:], KK[:], SQ[:], ALU.mult), reads=[db], writes=[db])
                S.op("dve", lambda e: e.tensor_tensor(NB[:], KKN[:], A32[:], ALU.mult), reads=[db], writes=[db, nbb])
                for fc in range(4):
                    S.op("dve", lambda e: e.tensor_scalar(KK[:, fc, :], A32[:, fc, :], self.cols[:, ka0 + fc:ka0 + fc + 1], OMK[:, fc:fc + 1],
                                                          ALU.mult, ALU.add), reads=[db, cb, self.constb], writes=[db])
                S.op("dve", lambda e: e.tensor_tensor(KM[:], PL[:, 4:8, :], KK[:], ALU.mult), reads=[db, PLB], writes=[db, kmb])
                S.op("dve", lambda e: e.tensor_tensor(SQ[:], PL[:, 0:4, :], KM[:], ALU.mult), reads=[db, PLB, kmb], writes=[db])
                for fc in range(4):
                    S.op("dve", lambda e: e.tensor_scalar(SQ[:, fc, :], SQ[:, fc, :], self.cols[:, rk0 + fc:rk0 + fc + 1], None, ALU.mult),
                         reads=[db, self.constb], writes=[db])
                pbn, pbnb = self.psum.get()
                S.op("pe", lambda e: e.matmul(pbn[:], BO[:], SQ[:].rearrange("p c t -> p (c t)"), start=True, stop=True), reads=[db, cb], writes=[pbnb])
                S.op("dve", lambda e: e.tensor_tensor(BON[:], c4(pbn[:]), PL[:, 8:12, :], ALU.mult), reads=[pbnb, PLB], writes=[db])
                for fc in range(4):
                    for c2 in range(2):
                        cs = slice(c2 * 64, (c2 + 1) * 64)
                        S.op("dve", lambda e: e.tensor_tensor_scan(CUM[:, fc, cs], ONE64[:], SIG[:, fc, cs], 0.0, ALU.mult, ALU.add),
                             reads=[db, cb, sgb], writes=[db, cub])
                S.op("pool", lambda e: e.tensor_tensor(SQ[:], CUM[:], SIG[:], ALU.subtract), reads=[db, sgb, cub], writes=[db])
                S.op("act", lambda e: e.activation(A32[:], CUM[:], AF.Exp, scale=CDEC), reads=[db, cub], writes=[db])
                S.op("act", lambda e: e.activation(CUM[:], CUM[:], AF.Exp, scale=-CDEC), reads=[db], writes=[db, cub])
                S.op("act", lambda e: e.activation(SQ[:], SQ[:], AF.Exp, scale=-CDEC), reads=[db], writes=[db])
                S.op("dve", lambda e: e.tensor_copy(PCt[:, 0, :], CUM[:, :, 63]), reads=[db, chb, cub], writes=[chb])
                S.op("dve", lambda e: e.tensor_copy(PCt[:, 1, :], CUM[:, :, 127]), reads=[db, chb, cub], writes=[chb])
                S.op("dve", lambda e: e.tensor_tensor(NB[:], NB[:], A32[:], ALU.mult), reads=[db], writes=[db, nbb])
                S.op("dve", lambda e: e.tensor_tensor(KM[:], KM[:], A32[:], ALU.mult), reads=[db], writes=[db, kmb])
                S.op("dve", lambda e: e.tensor_tensor(KKN[:], KKN[:], SQ[:], ALU.mult), reads=[db], writes=[db])
                S.op("dve", lambda e: e.tensor_tensor(KK[:], PL[:, 0:4, :], CUM[:], ALU.mult), reads=[db, PLB, cub], writes=[db])
                S.op("act", lambda e: e.copy(AH[:], NB[:]), reads=[db, gb_, nbb], writes=[gb_])
                S.op("act", lambda e: e.copy(KH[:], KM[:]), reads=[db, gb_, kmb], writes=[gb_])
                S.op("pool", lambda e: e.tensor_copy(BR[:, :, 0, :], KKN[:]), reads=[db, gb_], writes=[gb_])
                S.op("pool", lambda e: e.tensor_copy(BR[:, :, 1, :], KK[:]), reads=[db, gb_], writes=[gb_])
                if "dumpah" in self.debug and i == 0:
                    for nm, tl in (("ah", AH), ("kh", KH), ("br", BR)):
                        o_ = self.dout("dbg_" + nm, [128, tl[:].rearrange("p ... -> p (...)").shape[1] if False else (512 if nm != "br" else 1024)])
                        S.dma(o_, tl[:].rearrange("p c t -> p (c t)") if nm != "br" else tl[:].rearrange("p c a t -> p (c a t)"), reads=[gb_], queue="pool")
                    for nm, tl in (("nb", NB), ("km", KM), ("kkn", KKN), ("en", A32), ("ep", CUM)):
                        o_ = self.dout("dbg_" + nm, [128, 512])
                        S.dma(o_, tl[:].rearrange("p c t -> p (c t)"), reads=[db, nbb, kmb, cub])
                for src, dst_fn in ((NB, None), (KM, None), (KKN, None), (None, None)):
                    pass
                tr_jobs = [(lambda fc: NB[:, fc, :], "AT"), (lambda fc: KM[:, fc, :], "KT"),
                           (lambda fc: KKN[:, fc, :], "BT"), (lambda fc: PL[:, 8 + fc, :], "VT")]
                for srcf, kind in tr_jobs:
                    ptr, ptrb = self.psum.get()

                    def mmt(e):
                        for fc in range(4):
                            ins = e.transpose(ptr[:, fc * 128:(fc + 1) * 128], srcf(fc), ident[:])
                        return ins
                    S.op("pe", mmt, reads=[db, PLB, self.constb, nbb, kmb], writes=[ptrb])
                    if kind == "AT":
                        S.op("act", lambda e: e.copy(AT[:], ptr[:]), reads=[ptrb, tkb], writes=[tkb])
                    elif kind == "KT":
                        S.op("dve", lambda e: e.tensor_copy(KTt[:], ptr[:]), reads=[ptrb, tkb], writes=[tkb])
                    elif kind == "BT":
                        S.op("act", lambda e: e.activation(WB[:, :, 0:64], ptr[:].rearrange("p (h k) -> p h k", h=8), AF.Copy, scale=-1.0),
                             reads=[ptrb, tkb], writes=[tkb])
                    else:
                        S.op("dve", lambda e: e.tensor_copy(VTOK[:], ptr[:]), reads=[ptrb, tkb], writes=[tkb])
                if self._rk_stage <= 0:
                    continue
                for fc in range(4):
                    ka_, ab0, ab1 = self.psum.get_pair_idx()
                    kb_, bb0, bb1 = self.psum.get_pair_idx()
                    PA = self.PS[:, ka_:ka_ + 2, :]; PB = self.PS[:, kb_:kb_ + 2, :]

                    def mmg(e):
                        for h2 in range(2):
                            rs = slice(h2 * 64, (h2 + 1) * 64)
                            brr = BR[rs, fc, :, :].rearrange("p a t -> p (a t)")
                            e.matmul(PA[:, h2, 0:256], AH[rs, fc, :], brr, start=True, stop=True)
                            e.matmul(PA[:, h2, 256:512], KH[rs, fc, :], brr, start=True, stop=True)
                            ins = e.matmul(PB[:, h2, 0:128], BR[rs, fc, 0, :], AH[rs, fc, :], start=True, stop=True)
                        return ins
                    S.op("pe", mmg, reads=[gb_], writes=[ab0, ab1, bb0, bb1])
                    hs = slice(2 * fc, 2 * fc + 2)
                    PAv = PA.rearrange("p h (q b t) -> p h q b t", q=2, b=2)
                    mk = lambda j: MSK[:, j, :].unsqueeze(1).to_broadcast([128, 2, 128])
                    S.op("dve", lambda e: e.tensor_tensor(X0[:, hs, :], PAv[:, :, 0, 0, :], mk(0), ALU.mult), reads=[ab0, ab1, cb, mnb], writes=[mnb])
                    S.op("dve", lambda e: e.tensor_tensor(GRA[:, hs, :], PAv[:, :, 0, 1, :], mk(2), ALU.mult), reads=[ab0, ab1, cb, mnb], writes=[mnb])
                    S.op("dve", lambda e: e.tensor_tensor(LKT[:, hs, :], PAv[:, :, 1, 0, :], mk(0), ALU.mult), reads=[ab0, ab1, cb, mnb], writes=[mnb])
                    S.op("dve", lambda e: e.tensor_tensor(GRK[:, hs, :], PAv[:, :, 1, 1, :], mk(2), ALU.mult), reads=[ab0, ab1, cb, mnb], writes=[mnb])
                    S.op("dve", lambda e: e.tensor_tensor(XT0[:, hs, :], PB[:, :, 0:128], mk(1), ALU.mult), reads=[bb0, bb1, cb, mnb], writes=[mnb])
                if self._rk_stage <= 1:
                    continue
                if i + 1 < self._rk_tiles:
                    emit_proj(i + 1)
                    pending = lerp_list(i + 1)
                hst = []
                for half in range(2):
                    h0 = half * 4
                    st_ = dict(xb=Buf(), xtb=Buf(), tb=Buf(),
                               xbufs=[X0[:, h0:h0 + 4, :], XA1[half][:]], xtbufs=[XT0[:, h0:h0 + 4, :], XTA1[half][:]],
                               tbufs=[TA1[half][:], TT[:, h0:h0 + 4, :]])
                    hst.append(st_)
                    S.op("pool", lambda e: e.tensor_tensor(st_["tbufs"][0], st_["xbufs"][0], ident[:].unsqueeze(1).to_broadcast([128, 4, 128]), ALU.add),
                         reads=[mnb, self.constb, st_["tb"]], writes=[st_["tb"]])
                for lv in range(1, 6):
                    for half in range(2):
                        st_ = hst[half]
                        xb_, xtb_, tb_ = st_["xb"], st_["xtb"], st_["tb"]
                        Xp, XTp, Tp = st_["xbufs"][(lv - 1) % 2], st_["xtbufs"][(lv - 1) % 2], st_["tbufs"][(lv - 1) % 2]
                        Xn, XTn, Tn = st_["xbufs"][lv % 2], st_["xtbufs"][lv % 2], st_["tbufs"][lv % 2]
                        pxt, pxtb = self.psum.get()

                        def mmxt(e):
                            for j in range(4):
                                ins = e.matmul(pxt[:, j * 128:(j + 1) * 128], Xp[:, j, :], XTp[:, j, :], start=True, stop=True)
                            return ins
                        S.op("pe", mmxt, reads=[mnb, xb_, xtb_], writes=[pxtb])
                        if lv < 5:
                            px, pxb = self.psum.get()

                            def mmx(e):
                                for j in range(4):
                                    ins = e.matmul(px[:, j * 128:(j + 1) * 128], XTp[:, j, :], Xp[:, j, :], start=True, stop=True)
                                return ins
                            S.op("pe", mmx, reads=[mnb, xb_, xtb_], writes=[pxb])
                        S.op("act", lambda e: e.copy(XTn, c4(pxt[:])), reads=[pxtb, xtb_, mnb], writes=[xtb_])
                        if lv < 5:
                            S.op("act", lambda e: e.copy(Xn, c4(px[:])), reads=[pxb, xb_, mnb], writes=[xb_])
                        ptt, pttb = self.psum.get()

                        def mmtt(e):
                            for j in range(4):
                                ins = e.matmul(ptt[:, j * 128:(j + 1) * 128], XTn[:, j, :], Tp[:, j, :], start=True, stop=True)
                            return ins
                        S.op("pe", mmtt, reads=[xtb_, tb_], writes=[pttb])
                        S.op("dve", lambda e: e.tensor_tensor(Tn, c4(ptt[:]), Tp, ALU.add), reads=[pttb, tb_, mnb], writes=[tb_] + ([mnb] if lv == 5 else []))
                        for _ in range(2):
                            if pending:
                                pending.pop(0)()
                if self._rk_stage <= 2:
                    continue
                plk, plkb = self.psum.get()

                def mmlk(e):
                    for h in range(8):
                        ins = e.matmul(plk[:, h * 64:(h + 1) * 64], LKT[:, h, :], VTOK[:, h * 64:(h + 1) * 64], start=True, stop=True)
                    return ins
                S.op("pe", mmlk, reads=[mnb, tkb], writes=[plkb])
                S.op("act", lambda e: e.copy(WB[:, :, 64:128], plk[:].rearrange("p (h v) -> p h v", h=8)), reads=[plkb, tkb], writes=[tkb])
                for half in range(2):
                    pbu, pbub = self.psum.get()

                    def mmbu(e):
                        for j in range(4):
                            h = half * 4 + j
                            ins = e.matmul(pbu[:, j * 128:(j + 1) * 128], TT[:, h, :], WB[:, h, :], start=True, stop=True)
                        return ins
                    S.op("pe", mmbu, reads=[mnb, tkb], writes=[pbub])
                    S.op("act", lambda e: e.copy(BU[:, half * 4:half * 4 + 4, :], c4(pbu[:])), reads=[pbub, chb], writes=[chb])
                if self._rk_stage <= 3:
                    continue
                prt, prtb = self.psum.get()
                km_, mb0, mb1 = self.psum.get_pair_idx()
                PM = self.PS[:, km_:km_ + 2, :]

                def mmrt(e):
                    for h in range(8):
                        rs = slice((h % 2) * 64, (h % 2) * 64 + 64)
                        fc = h // 2
                        ins = e.matmul(prt[rs, fc * 128:(fc + 1) * 128], BU[:, h, 0:64], GRA[:, h, :], start=True, stop=True)
                    return ins

                def mmmn(e):
                    for c2 in range(2):
                        cr = slice(c2 * 64, (c2 + 1) * 64)
                        for h in range(8):
                            rs = slice((h % 2) * 64, (h % 2) * 64 + 64)
                            fc = h // 2
                            o = fc * 64
                            e.matmul(PM[rs, c2, o:o + 64], BU[cr, h, 0:64], AT[cr, h * 64:(h + 1) * 64], start=True, stop=True)
                            e.matmul(PM[rs, c2, 256 + o:256 + o + 64], AT[cr, h * 64:(h + 1) * 64], BU[cr, h, 64:128], start=True, stop=False)
                            ins = e.matmul(PM[rs, c2, 256 + o:256 + o + 64], KTt[cr, h * 64:(h + 1) * 64], VTOK[cr, h * 64:(h + 1) * 64], start=False, stop=True)
                    return ins
                S.op("pe", mmrt, reads=[chb, mnb], writes=[prtb])
                S.op("pe", mmmn, reads=[chb, tkb], writes=[mb0, mb1])
                prv = c4(prt[:])
                S.op("dve", lambda e: e.tensor_tensor(RTm[:, :, 0, 0:64], prv[:, :, 0:64], KK[:, :, 0:64], ALU.add), reads=[prtb, db, rtb], writes=[rtb])
                S.op("dve", lambda e: e.tensor_tensor(RTm[:, :, 1, 64:128], prv[:, :, 64:128], KK[:, :, 64:128], ALU.add), reads=[prtb, db, rtb], writes=[rtb])
                M0v = M0Ts.rearrange("p (a c) k -> p a c k", a=2)
                N0v = N0s.rearrange("p (a c) k -> p a c k", a=2)
                S.op("dve", lambda e: e.tensor_tensor(M0v, PM[:, :, 0:256].rearrange("p a (c k) -> p a c k", c=4),
                                                      ID2[:].unsqueeze(1).unsqueeze(1).to_broadcast([128, 2, 4, 64]), ALU.add),
                     reads=[mb0, mb1, cb, chb], writes=[chb, sgb])
                S.op("act", lambda e: e.copy(N0v, PM[:, :, 256:512].rearrange("p a (c k) -> p a c k", c=4)), reads=[mb0, mb1, chb], writes=[chb, cub])
                if self._rk_stage <= 4:
                    continue
                for c2 in range(2):
                    S.op("act", lambda e: e.copy(Hb[:, c2, :, :], H[:]), reads=[hb, hbb], writes=[hbb])
                    phe, pheb = self.psum.get(); pho, phob = self.psum.get()

                    def mmh(e):
                        for par, bank in ((0, phe), (1, pho)):
                            rs = slice(par * 64, par * 64 + 64)
                            for fc in range(4):
                                ins = e.matmul(bank[rs, fc * 64:(fc + 1) * 64], M0Ts[rs, c2 * 4 + fc, :], H[rs, fc, :], start=True, stop=True)
                        return ins
                    S.op("pe", mmh, reads=[chb, hb, sgb], writes=[pheb, phob])
                    S.op("dve", lambda e: e.tensor_tensor(H[0:64], phe[0:64, 0:256].rearrange("p (c v) -> p c v", c=4), N0s[0:64, c2 * 4:c2 * 4 + 4, :], ALU.add),
                         reads=[pheb, chb, cub, hb], writes=[hb])
                    S.op("dve", lambda e: e.tensor_tensor(H[64:128], pho[64:128, 0:256].rearrange("p (c v) -> p c v", c=4), N0s[64:128, c2 * 4:c2 * 4 + 4, :], ALU.add),
                         reads=[phob, chb, cub, hb], writes=[hb])
                    S.op("dve", lambda e: e.tensor_tensor(H[:], H[:], PCt[:, c2, :].unsqueeze(2).to_broadcast([128, 4, 64]), ALU.mult),
                         reads=[chb, hb], writes=[hb])
                if self._rk_stage <= 5:
                    continue
                ky_, yb0, yb1 = self.psum.get_pair_idx()
                PY = self.PS[:, ky_:ky_ + 2, :]

                def mmy(e):
                    for par in range(2):
                        rs = slice(par * 64, par * 64 + 64)
                        for fc in range(4):
                            h = 2 * fc + par
                            o = PY[:, par, fc * 64:(fc + 1) * 64]
                            e.matmul(o, GRA[:, h, :], BU[:, h, 64:128], start=True, stop=False)
                            e.matmul(o, GRK[:, h, :], VTOK[:, h * 64:(h + 1) * 64], start=False, stop=False)
                            e.matmul(o, RTm[rs, fc, 0, :], Hb[rs, 0, fc, :], start=False, stop=False)
                            ins = e.matmul(o, RTm[rs, fc, 1, :], Hb[rs, 1, fc, :], start=False, stop=True)
                    return ins
                S.op("pe", mmy, reads=[mnb, chb, tkb, rtb, hbb], writes=[yb0, yb1])
                S.op("act", lambda e: e.copy(YTOK.rearrange("t c h v -> t h c v"), PY[:, :, 0:256].rearrange("t h (c v) -> t h c v", c=4)),
                     reads=[yb0, yb1, YTOKB], writes=[YTOKB])
                if self._rk_stage <= 6:
                    continue
                YT8 = YTOK.rearrange("t c h v -> t (c h) v")
                S.op("dve", lambda e: e.tensor_reduce(ST8[:], YT8, AX.X, ALU.add), reads=[YTOKB], writes=[yb])
                S.op("dve", lambda e: e.tensor_scalar(ST8[:], ST8[:], 1.0 / 64, None, ALU.mult), reads=[yb], writes=[yb])
                S.op("dve", lambda e: e.tensor_tensor(YC, YT8, ST8[:].unsqueeze(2).to_broadcast([128, 8, 64]), ALU.subtract),
                     reads=[YTOKB, yb], writes=[yb, kmb])
                S.op("pool", lambda e: e.tensor_tensor(YT8, YC, YC, ALU.mult), reads=[yb, YTOKB, kmb], writes=[YTOKB])
                S.op("dve", lambda e: e.tensor_reduce(ST8b[:], YT8, AX.X, ALU.add), reads=[YTOKB], writes=[yb])
                S.op("act", lambda e: e.activation(ST8b[:], ST8b[:], AF.Sqrt, bias=self.gneps_t[:], scale=1.0 / 64), reads=[yb, self.constb], writes=[yb])
                S.op("dve", lambda e: e.reciprocal(ST8b[:], ST8b[:]), reads=[yb], writes=[yb])
                S.op("dve", lambda e: e.tensor_tensor(YC, YC, ST8b[:].unsqueeze(2).to_broadcast([128, 8, 64]), ALU.mult), reads=[yb], writes=[yb, kmb])
                pyt, pytb = self.psum.get(); pg, pgb = self.psum.get()

                def mmt2(e):
                    for fc in range(4):
                        ins = e.transpose(pyt[:, fc * 128:(fc + 1) * 128], YC[:, 2 * fc:2 * fc + 2, :].rearrange("t a v -> t (a v)"), ident[:])
                    for fc in range(4):
                        ins = e.matmul(pg[:, fc * 128:(fc + 1) * 128], G2[:, fc * 128:(fc + 1) * 128], SGg[:], start=True, stop=True)
                    return ins
                S.op("pe", mmt2, reads=[yb, self.constb, db, cb, kmb], writes=[pytb, pgb])
                for fc in range(4):
                    S.op("dve", lambda e: e.tensor_scalar(YF[:, fc, :], pyt[:, fc * 128:(fc + 1) * 128], self.cols[:, gg0 + fc:gg0 + fc + 1],
                                                          self.cols[:, gb0 + fc:gb0 + fc + 1], ALU.mult, ALU.add),
                         reads=[pytb, db, self.constb], writes=[db])
                S.op("pool", lambda e: e.tensor_tensor(YF[:], YF[:], BON[:], ALU.add), reads=[db], writes=[db])
                S.op("dve", lambda e: e.tensor_tensor(self.Y[0][:, :, tsl], YF[:], c4(pg[:]), ALU.mult),
                     reads=[db, pgb], writes=[self.YB[0][tcix]])
            S.full_barrier()
            self.st = old

    def nsa_branch(self, d):
        S = self.S
        NT = S_LEN // 128
        with ExitStack() as st4:
            old, self.st = self.st, st4
            cb = Buf()
            KT = self.sb("KT", [128, 2, S_LEN], BF16); KTB = Buf()
            VT = self.sb("VT", [128, NT, 256], BF16); VTB = Buf()
            KC = self.sb("KC", [128, 127], BF16); VC = self.sb("VC", [128, 128], BF16); kcb = Buf()
            BM = self.sb("BM", [128, 3, 2, 512], BF16)
            BVC = self.sb("BVC", [32, 2, 512], BF16)
            stA = ExitStack(); self.st = stA
            G1 = self.sb("G1", [128, 2, 512], F32); G2_ = self.sb("G2b", [128, 2, 512], F32); MK = self.sb("MK", [128, 128], F32)
            gb = Buf()
            S.dma(G2_[:], d["t31"], writes=[gb])
            for kind in range(3):
                S.dma(G1[:], d["bmg"][kind], reads=[gb], writes=[gb])
                S.dma(MK[:], d["msk"][kind], reads=[gb], writes=[gb])
                S.op("dve", lambda e: e.tensor_tensor(G1[:], G1[:], G2_[:], ALU.subtract), reads=[gb], writes=[gb])
                S.op("dve", lambda e: e.tensor_tensor(BM[:, kind, :, :].rearrange("p g (j q) -> p (g j) q", j=4),
                                                      G1[:].rearrange("p g (j q) -> p (g j) q", j=4),
                                                      MK[:].unsqueeze(1).to_broadcast([128, 8, 128]), ALU.add), reads=[gb], writes=[cb, gb])
            S.dma(G1[0:32, :, :], d["bvcg"], reads=[gb], writes=[gb])
            S.dma(MK[0:32, :], d["mskc"], reads=[gb], writes=[gb])
            S.op("dve", lambda e: e.tensor_tensor(G1[0:32], G1[0:32], G2_[0:32], ALU.subtract), reads=[gb], writes=[gb])
            S.op("dve", lambda e: e.tensor_tensor(BVC[:].rearrange("p g (j q) -> p (g j) q", j=4),
                                                  G1[0:32].rearrange("p g (j q) -> p (g j) q", j=4),
                                                  MK[0:32, :].unsqueeze(1).to_broadcast([32, 8, 128]), ALU.add), reads=[gb], writes=[cb, gb])
            S.full_barrier()
            stA.close()
            stB = ExitStack(); self.st = stB
            KCMP = self.sb("KCMP", [128, S_LEN], BF16); VCT = self.sb("VCT", [128, S_LEN], BF16)
            stB1 = ExitStack(); self.st = stB1
            WKV = self.sb("WKV", [128, NCH, 768], BF16); wkvb = Buf()
            self.load_w(WKV[:], d["w_kvn"], wkvb)
            for tc in range(NTC):
                ts = slice(tc * TC, (tc + 1) * TC)
                hreads = [self.HNB[c][tc] for c in range(NCH)]
                for dst, col in ((KCMP[:, ts], 0), (VCT[:, ts], 128), (KT[:, 0, ts], 256), (KT[:, 1, ts], 512)):
                    p, pb = self.psum.get()

                    def mm(e):
                        for k in range(NCH):
                            ins = e.matmul(p[:], WKV[:, k, col:col + 128], self.HN[:, k, ts], start=(k == 0), stop=(k == NCH - 1))
                        return ins
                    S.op("pe", mm, reads=hreads + [wkvb], writes=[pb])
                    S.op("act", lambda e: e.copy(dst, p[:]), reads=[pb], writes=[KTB])
                for tl in range(4):
                    tile = tc * 4 + tl
                    tq = slice(tile * 128, (tile + 1) * 128)
                    p, pb = self.psum.get()

                    def mm(e):
                        for k in range(NCH):
                            e.matmul(p[:, 0:128], self.HN[:, k, tq], WKV[:, k, 384:512], start=(k == 0), stop=(k == NCH - 1))
                        for k in range(NCH):
                            ins = e.matmul(p[:, 128:256], self.HN[:, k, tq], WKV[:, k, 640:768], start=(k == 0), stop=(k == NCH - 1))
                        return ins
                    S.op("pe", mm, reads=hreads + [wkvb], writes=[pb])
                    S.op("dve", lambda e: e.tensor_copy(VT[:, tile, :], p[:, 0:256]), reads=[pb], writes=[VTB])
            S.full_barrier()
            stB1.close()
            stB2 = ExitStack(); self.st = stB2
            W1 = self.sb("W1", [128, 32, 256], BF16); PET = self.sb("PET", [128, 32], BF16)
            W2D = self.sb("W2D", [128, 2, 128], BF16); HID = self.sb("HID", [128, 2, 127], BF16)
            ZZ = self.sb("ZZ", [128, 127], F32); Z2 = self.sb("Z2", [128, 127], F32); BC = self.sb("BCc", [128, 1], F32)
            wb = Buf(); zb = Buf(); hb_ = Buf()
            for kv in range(2):
                w1d = d["cmp_w1"][kv].rearrange("(l dd) m -> dd l m", dd=64)
                S.dma(W1[0:64], w1d, writes=[wb], queue="pool"); S.dma(W1[64:128], w1d, writes=[wb], queue="pool")
                S.dma(PET[0:64], d["cmp_peT"][kv], writes=[wb], queue="pool"); S.dma(PET[64:128], d["cmp_peT"][kv], writes=[wb], queue="pool")
                w2v = d["cmp_w2"][kv].rearrange("(c p) n -> p c n", p=128)
                S.dma(W2D[:, :, 0:64], w2v, writes=[wb], queue="pool"); S.dma(W2D[:, :, 64:128], w2v, writes=[wb], queue="pool")
                SRC = KCMP if kv == 0 else VCT
                for g in range(2):
                    gs = slice(g * 64, (g + 1) * 64)
                    for mc in range(2):
                        ph, phb = self.psum.get(); pbias, pbb = self.psum.get()

                        def mm(e):
                            for l in range(32):
                                ins = e.matmul(ph[:, 0:127], W1[gs, l, mc * 128:(mc + 1) * 128], SRC[gs, l:l + 16 * 126 + 1:16],
                                               start=(l == 0), stop=(l == 31))
                            return ins

                        def mmb(e):
                            for l in range(32):
                                ins = e.matmul(pbias[:, 0:1], W1[gs, l, mc * 128:(mc + 1) * 128], PET[gs, l:l + 1], start=(l == 0), stop=(l == 31))
                            return ins
                        S.op("pe", mm, reads=[wb, KTB], writes=[phb])
                        S.op("pe", mmb, reads=[wb], writes=[pbb])
                        S.op("act", lambda e: e.copy(BC[:], pbias[:, 0:1]), reads=[pbb, zb], writes=[zb])
                        S.op("dve", lambda e: e.tensor_scalar(ZZ[:], ph[:, 0:127], BC[:, 0:1], None, ALU.add), reads=[phb, zb], writes=[zb])
                        S.op("dve", lambda e: e.tensor_tensor(Z2[:], ZZ[:], ZZ[:], ALU.mult), reads=[zb], writes=[zb])
                        S.op("dve", lambda e: e.tensor_scalar(Z2[:], Z2[:], 0.044715, 1.0, ALU.mult, ALU.add), reads=[zb], writes=[zb])
                        S.op("dve", lambda e: e.tensor_tensor(Z2[:], Z2[:], ZZ[:], ALU.mult), reads=[zb], writes=[zb])
                        S.op("act", lambda e: e.activation(Z2[:], Z2[:], AF.Sigmoid, scale=1.5957691216057308), reads=[zb], writes=[zb])
                        S.op("dve", lambda e: e.tensor_tensor(HID[:, mc, :], ZZ[:], Z2[:], ALU.mult), reads=[zb, hb_], writes=[hb_])
                    po, pob = self.psum.get()
                    if kv == 0:
                        def mm2(e):
                            for mc in range(2):
                                ins = e.matmul(po[:, 0:127], W2D[:, mc, :], HID[:, mc, :], start=(mc == 0), stop=(mc == 1))
                            return ins
                        S.op("pe", mm2, reads=[hb_, wb], writes=[pob])
                        S.op("act", lambda e: e.copy(KC[gs, :], po[gs, 0:127]), reads=[pob], writes=[kcb])
                    else:
                        def mm2(e):
                            for mc in range(2):
                                ins = e.matmul(po[0:127, 0:64], HID[:, mc, :], W2D[:, mc, 0:64], start=(mc == 0), stop=(mc == 1))
                            return ins
                        S.op("pe", mm2, reads=[hb_, wb], writes=[pob])
                        S.op("act", lambda e: e.copy(VC[0:127, gs], po[0:127, 0:64]), reads=[pob], writes=[kcb])
            S.full_barrier()
            stB2.close(); stB.close(); self.st = st4
            if "kcvc" in self.debug:
                okc = self.dout("dbg_kc", [128, 127]); ovc = self.dout("dbg_vc", [127, 128])
                S.dma(okc, KC[:], reads=[kcb], queue="pool"); S.dma(ovc, VC[0:127, :], reads=[kcb], queue="pool")
            WQ = self.sb("WQN", [128, NCH, 512], BF16); WGN = self.sb("WGN", [128, NCH, 24], BF16)
            SHCF = self.sb("SHCF", [32, 247], BF16); EF = self.sb("EF", [32, S_LEN], BF16)
            OV = self.sb("OV", [128, 32], BF16); AB = self.sb("ABF", [128, 2, 64], F32)
            SELG = self.sb("SELG", [24, 12, 128], BF16); IDb = self.sb("IDb", [128, 128], BF16)
            self.load_w(WQ[:], d["w_qn"], cb)
            self.load_w(WGN[:], d["w_gn"], cb)
            S.dma(SHCF[:], d["shcf"], writes=[cb], queue="pool"); S.dma(EF[:], d["efull"], writes=[cb], queue="pool")
            S.dma(OV[0:127, :], d["ov"], writes=[cb], queue="pool"); S.dma(AB[:], d["abf"], writes=[cb])
            S.dma(SELG[:], d["selg"], writes=[cb], queue="pool")
            S.op("dve", lambda e: e.tensor_copy(IDb[:], self.ident_f[:]), reads=[self.constb, cb], writes=[cb])
            QS = self.sb("QS", [128, 4, 128], BF16); qsb = Buf()
            GS = self.sb("GS", [24, 128], BF16); gsb = Buf()
            pt_ring = Ring([self.sb("PT%d" % i, [128, 512], BF16) for i in range(4)])
            RR = self.sb("RR", [128, 512], F32); rrb = Buf()
            YA = self.sb("YA", [128, 512], F32); yab = Buf()
            PN = self.sb("PN", [128, 512], BF16); pnb = Buf()
            IMP = self.sb("IMP", [128, 32], F32); IM2 = self.sb("IM2", [128, 32], F32); MX = self.sb("MX8", [128, 8], F32); ib = Buf()
            NMT = [self.sb("NMT%d" % g, [32, 4, 128], BF16) for g in range(2)]; nmb = [Buf(), Buf()]
            st_ring = Ring(self.banks[0:3], self.bankb[0:3])
            OD = [(self.banks[3], self.bankb[3], self.banks[4], self.bankb[4]),
                  (self.banks[5], self.bankb[5], self.banks[6], self.bankb[6])]
            ms_ring = Ring(self.banks[7:8], self.bankb[7:8])
            LOOK = 2
            for i in range(NT):
                tq = slice(i * 128, (i + 1) * 128)
                tcix = i // 4
                hreads = [self.HNB[c][tcix] for c in range(NCH)]
                p, pb = ms_ring.get()

                def mmq(e):
                    for j in range(4):
                        for k in range(NCH):
                            ins = e.matmul(p[:, j * 128:(j + 1) * 128], WQ[:, k, j * 128:(j + 1) * 128], self.HN[:, k, tq], start=(k == 0), stop=(k == NCH - 1))
                    return ins
                S.op("pe", mmq, reads=hreads + [cb], writes=[pb])
                S.op("act", lambda e: e.activation(QS[:].rearrange("p j q -> p (j q)"), p[:], AF.Copy, scale=0.125), reads=[pb], writes=[qsb])
                p2, pb2 = ms_ring.get()

                def mmg(e):
                    for k in range(NCH):
                        ins = e.matmul(p2[0:24, 0:128], WGN[:, k, :], self.HN[:, k, tq], start=(k == 0), stop=(k == NCH - 1))
                    return ins
                S.op("pe", mmg, reads=hreads + [cb], writes=[pb2])
                S.op("act", lambda e: e.activation(GS[:], p2[0:24, 0:128], AF.Sigmoid), reads=[pb2], writes=[gsb])

                def tiles_of(br):
                    if br == 0:
                        return [None]
                    if br == 1:
                        return list(range(0, i + 1))
                    return list(range(max(0, i - 4), i + 1))
                odset = {0: 0, 1: 1, 2: 0}

                def emit_scores(step):
                    br, g, kt, first, last = step
                    gs = slice(g * 64, (g + 1) * 64)
                    qrhs = QS[gs, :, :].rearrange("p j q -> p (j q)")
                    rows = 127 if br == 0 else 128
                    stp, stb = st_ring.get()
                    mms_list = []
                    if br == 0:
                        mms_list.append((KC[gs, :], qrhs))
                        mms_list.append((SHCF[:, 120 - 8 * i:247 - 8 * i], BVC[:, g, :]))
                    else:
                        mms_list.append((KT[gs, br - 1, kt * 128:(kt + 1) * 128], qrhs))
                        if br == 1 and i >= 8:
                            mms_list.append((EF[:, kt * 128:(kt + 1) * 128], NMT[g][:].rearrange("p j q -> p (j q)")))
                        if kt == i:
                            mms_list.append((IDb[:], BM[:, 0, g, :]))
                        elif kt == i - 1:
                            mms_list.append((IDb[:], BM[:, 1, g, :]))
                        elif br == 2 and kt == i - 4:
                            mms_list.append((IDb[:], BM[:, 2, g, :]))

                    def mms(e):
                        for n_, (l_, r_) in enumerate(mms_list):
                            ins = e.matmul(stp[0:rows, :], l_, r_, start=(n_ == 0), stop=(n_ == len(mms_list) - 1))
                        return ins
                    S.op("pe", mms, reads=[qsb, KTB, kcb, cb, nmb[g]], writes=[stb])
                    PT, ptb = pt_ring.get()
                    S.op("act", lambda e: e.activation(PT[0:rows, :], stp[0:rows, :], AF.Exp), reads=[stb], writes=[ptb])
                    return (PT, ptb, rows)

                def emit_pv(step, ctx):
                    br, g, kt, first, last = step
                    PT, ptb, rows = ctx
                    gs = slice(g * 64, (g + 1) * 64)
                    O, Ob, DN, Db = OD[odset[br]]
                    if br == 0:
                        vl = VC[0:127, gs]
                    else:
                        c0 = (0 if br == 1 else 128) + g * 64
                        vl = VT[:, kt, c0:c0 + 64]

                    def mmo(e):
                        e.matmul(O[gs, :], vl, PT[0:rows, :], start=first, stop=last)
                        return e.matmul(DN[gs, :], self.ones_b[0:rows, 0:64], PT[0:rows, :], start=first, stop=last)
                    S.op("pe", mmo, reads=[ptb, VTB, kcb, self.constb], writes=[Ob, Db])

                def cmp_extras(g, ctx):
                    PT, ptb, rows = ctx
                    pd2, pdb2 = ms_ring.get()
                    S.op("pe", lambda e: e.matmul(pd2[0:127, :], self.ones_b[0:127, 0:127], PT[0:127, :], start=True, stop=True),
                         reads=[ptb, self.constb], writes=[pdb2])
                    S.op("dve", lambda e: e.tensor_scalar(RR[0:127, :], pd2[0:127, :], 1e-30, None, ALU.max), reads=[pdb2, rrb], writes=[rrb])
                    S.op("dve", lambda e: e.reciprocal(RR[0:127, :], RR[0:127, :]), reads=[rrb], writes=[rrb])
                    S.op("dve", lambda e: e.tensor_tensor(PN[0:127, :], PT[0:127, :], RR[0:127, :], ALU.mult), reads=[rrb, ptb, pnb], writes=[pnb])
                    pim, pimb = ms_ring.get()

                    def mmi(e):
                        for j in range(4):
                            ins = e.matmul(pim[:, 0:32], PN[0:127, j * 128:(j + 1) * 128], OV[0:127, :], start=(j == 0), stop=(j == 3))
                        return ins
                    S.op("pe", mmi, reads=[pnb, cb], writes=[pimb])
                    o0 = 32 - 2 * i
                    S.op("dve", lambda e: e.tensor_tensor(IMP[:], pim[:, 0:32], AB[:, 0, o0:o0 + 32], ALU.mult), reads=[pimb, cb, ib], writes=[ib])
                    S.op("dve", lambda e: e.tensor_tensor(IMP[:], IMP[:], AB[:, 1, o0:o0 + 32], ALU.add), reads=[ib, cb], writes=[ib])
                    S.op("dve", lambda e: e.memset(IMP[:, 0:1], 1e6), reads=[ib], writes=[ib])
                    S.op("dve", lambda e: e.max(MX[:], IMP[:]), reads=[ib], writes=[ib])
                    S.op("dve", lambda e: e.match_replace(IM2[:], MX[:], IMP[:], 0.0), reads=[ib], writes=[ib])
                    S.op("dve", lambda e: e.max(MX[:], IM2[:]), reads=[ib], writes=[ib])
                    S.op("dve", lambda e: e.match_replace(IM2[:], MX[:], IM2[:], 0.0), reads=[ib], writes=[ib])
                    S.op("dve", lambda e: e.tensor_tensor(IM2[:], IMP[:], IM2[:], ALU.subtract), reads=[ib], writes=[ib])
                    S.op("dve", lambda e: e.tensor_scalar(IM2[:], IM2[:], 0.0, None, ALU.is_gt), reads=[ib], writes=[ib])
                    S.op("dve", lambda e: e.tensor_scalar(IM2[:], IM2[:], 30000.0, -30000.0, ALU.mult, ALU.add), reads=[ib], writes=[ib])
                    ptr, ptrb = ms_ring.get()
                    S.op("pe", lambda e: e.transpose(ptr[0:32, 0:128], IM2[:], self.ident_f[:]), reads=[ib, self.constb], writes=[ptrb])
                    S.op("dve", lambda e: e.tensor_copy(NMT[g][:], ptr[0:32, 0:128].unsqueeze(1).to_broadcast([32, 4, 128])),
                         reads=[ptrb], writes=[nmb[g]])

                def finalize(br):
                    O, Ob, DN, Db = OD[odset[br]]
                    S.op("dve", lambda e: e.tensor_scalar(RR[:], DN[:], 1e-30, None, ALU.max), reads=[Db, rrb], writes=[rrb])
                    S.op("dve", lambda e: e.reciprocal(RR[:], RR[:]), reads=[rrb], writes=[rrb])
                    pgb_, pgbb = ms_ring.get()

                    def mmgb(e):
                        for j in range(4):
                            ins = e.matmul(pgb_[:, j * 128:(j + 1) * 128], SELG[:, br * 4 + j, :], GS[:], start=True, stop=True)
                        return ins
                    S.op("pe", mmgb, reads=[gsb, cb], writes=[pgbb])
                    S.op("dve", lambda e: e.tensor_tensor(RR[:], RR[:], pgb_[:], ALU.mult), reads=[rrb, pgbb], writes=[rrb])
                    if br == 0:
                        S.op("dve", lambda e: e.tensor_tensor(YA[:], O[:], RR[:], ALU.mult), reads=[Ob, rrb, yab], writes=[yab])
                    else:
                        S.op("dve", lambda e: e.tensor_tensor(RR[:], O[:], RR[:], ALU.mult), reads=[Ob, rrb], writes=[rrb])
                        S.op("dve", lambda e: e.tensor_tensor(YA[:], YA[:], RR[:], ALU.add), reads=[rrb, yab], writes=[yab])

                for g in range(2):
                    st_ = (0, g, None, True, True)
                    ctx = emit_scores(st_)
                    emit_pv(st_, ctx)
                    if i >= 8:
                        cmp_extras(g, ctx)
                finalize(0)
                steps = []
                for br in (1, 2):
                    for g in range(2):
                        tl = tiles_of(br)
                        for ti, kt in enumerate(tl):
                            steps.append((br, g, kt, ti == 0, ti == len(tl) - 1))
                ctxs = {}
                for n in range(len(steps) + LOOK):
                    if n < len(steps):
                        ctxs[n] = emit_scores(steps[n])
                    m = n - LOOK
                    if m >= 0:
                        emit_pv(steps[m], ctxs.pop(m))
                        br_, g_, kt_, f_, l_ = steps[m]
                        if g_ == 1 and l_:
                            finalize(br_)
                S.op("act", lambda e: e.copy(self.Y[1][:, :, tq], YA[:].rearrange("p (j q) -> p j q", j=4)), reads=[yab], writes=[self.YB[1][tcix]])
            S.full_barrier()
            self.st = old

    def mem_branch(self, memT, wk_d, wv_d, wqm_d):
        S = self.S
        with ExitStack() as st4:
            old, self.st = self.st, st4
            self._norm_rings_open()
            WQ = self.sb("WQM", [128, NCH, 512], BF16); WQB = Buf()
            KHT = self.sb("KHT", [128, 4, 256], BF16); KHTB = Buf()
            VH = self.sb("VH", [128, 2, 512], BF16); VHB = Buf()
            st5 = ExitStack()
            self.st = st5
            MT = self.sb("MT", [128, NCH, 256], F32); MTB = Buf()
            MN = self.sb("MN", [128, NCH, 256], BF16); MNB = Buf()
            WK = self.sb("WK", [128, NCH, 512], BF16); WKB = Buf()
            WV = self.sb("WV", [128, NCH, 512], BF16); WVB = Buf()
            mr = self.sb("mrstd", [128, 256], F32); mrb = Buf()
            S.dma(MT[:], memT.rearrange("(c p) m -> p c m", p=128), writes=[MTB])
            self.load_w(WK[:], wk_d, WKB)
            self.load_w(WV[:], wv_d, WVB)
            self.load_w(WQ[:], wqm_d, WQB)
            g0, _ = COLS["mem_norm"]
            pt, pb = self.psum.get()
            for c in range(NCH):
                sq, sqb = self.sq_ring.get()
                S.op("act", lambda e: e.activation(sq[:, 0:256], MT[:, c, :], AF.Square), reads=[MTB], writes=[sqb])
                S.op("pe", lambda e: e.matmul(pt[:, 0:256], self.ones_f[:], sq[:, 0:256], start=(c == 0), stop=(c == NCH - 1)),
                     reads=[sqb, self.constb], writes=[pb])
            S.op("act", lambda e: e.activation(mr[:], pt[:, 0:256], AF.Sqrt, bias=self.eps_t[:], scale=1.0 / D),
                 reads=[pb, self.constb], writes=[mrb])
            S.op("dve", lambda e: e.reciprocal(mr[:], mr[:]), reads=[mrb], writes=[mrb])
            for c in range(NCH):
                S.op("dve", lambda e: e.scalar_tensor_tensor(MN[:, c, :], MT[:, c, :], self.cols[:, g0 + c:g0 + c + 1], mr[:],
                                                             ALU.mult, ALU.mult),
                     reads=[MTB, mrb, self.constb], writes=[MNB])
            for h in range(4):
                p, pb = self.psum.get()

                def mm(e):
                    for k in range(NCH):
                        ins = e.matmul(p[:, 0:256], WK[:, k, h * 128:(h + 1) * 128], MN[:, k, :], start=(k == 0), stop=(k == NCH - 1))
                    return ins
                S.op("pe", mm, reads=[WKB, MNB], writes=[pb])
                S.op("act", lambda e: e.copy(KHT[:, h, :], p[:, 0:256]), reads=[pb], writes=[KHTB])
            for mt in range(2):
                p, pb = self.psum.get()

                def mm(e):
                    for k in range(NCH):
                        ins = e.matmul(p[:], MN[:, k, mt * 128:(mt + 1) * 128], WV[:, k, :], start=(k == 0), stop=(k == NCH - 1))
                    return ins
                S.op("pe", mm, reads=[WVB, MNB], writes=[pb])
                S.op("act", lambda e: e.copy(VH[:, mt, :], p[:]), reads=[pb], writes=[VHB])
            S.full_barrier()
            st5.close()
            self.st = st4
            qm_ring = Ring([self.sb("qm%d" % i, [128, TC], BF16) for i in range(2)])
            pt_ring = Ring([self.sb("pt%d" % i, [128, 2, TC], BF16) for i in range(2)])
            rd_ring = self.sq_ring
            scale = 128.0 ** -0.5
            for tc in range(NTC):
                ts = slice(tc * TC, (tc + 1) * TC)
                hreads = [self.HNB[c][tc] for c in range(NCH)]
                for h in range(4):
                    p, pb = self.psum.get()

                    def mm(e):
                        for k in range(NCH):
                            ins = e.matmul(p[:], WQ[:, k, h * 128:(h + 1) * 128], self.HN[:, k, ts], start=(k == 0), stop=(k == NCH - 1))
                        return ins
                    S.op("pe", mm, reads=hreads + [WQB], writes=[pb])
                    qm, qmb = qm_ring.get()
                    S.op("dve", lambda e: e.tensor_copy(qm[:], p[:]), reads=[pb], writes=[qmb])
                    ptile, ptb = pt_ring.get()
                    for mt in range(2):
                        ps_, psb = self.psum.get()
                        S.op("pe", lambda e: e.matmul(ps_[:], KHT[:, h, mt * 128:(mt + 1) * 128], qm[:], start=True, stop=True),
                             reads=[KHTB, qmb], writes=[psb])
                        S.op("act", lambda e: e.activation(ptile[:, mt, :], ps_[:], AF.Exp, scale=scale), reads=[psb], writes=[ptb])
                    po, pob = self.psum.get()
                    pd, pdb = self.psum.get()

                    def mm_o(e):
                        for mt in range(2):
                            ins = e.matmul(po[:], VH[:, mt, h * 128:(h + 1) * 128], ptile[:, mt, :], start=(mt == 0), stop=(mt == 1))
                        return ins

                    def mm_d(e):
                        for mt in range(2):
                            ins = e.matmul(pd[:], self.ones_b[:], ptile[:, mt, :], start=(mt == 0), stop=(mt == 1))
                        return ins
                    S.op("pe", mm_o, reads=[VHB, ptb], writes=[pob])
                    S.op("pe", mm_d, reads=[ptb, self.constb], writes=[pdb])
                    rd, rdb = rd_ring.get()
                    S.op("dve", lambda e: e.reciprocal(rd[:], pd[:]), reads=[pdb], writes=[rdb])
                    S.op("dve", lambda e: e.tensor_tensor(self.Y[2][:, h, ts], po[:], rd[:], ALU.mult),
                         reads=[pob, rdb], writes=[self.YB[2][tc]])
            S.full_barrier()
            self.st = old
        self._nst.close()

    def fold(self, br, wgb_d, wbr_d, first):
        S = self.S
        with ExitStack() as st4:
            old, self.st = self.st, st4
            WGB = [self.sb("WGBr%d" % i, [128, NCH, 128], BF16) for i in range(2)]; WGBB = [Buf(), Buf()]
            WBR = [self.sb("WBR%d" % i, [128, 4, 128], BF16) for i in range(2)]; WBRB = [Buf(), Buf()]
            gt_ring = Ring([self.sb("gt%d" % i, [128, TC], F32) for i in range(2)])
            t_ring = Ring([self.sb("mt%d" % i, [128, TC], F32) for i in range(2)])

            def load(dc):
                sl = dc % 2
                c0 = br * D + dc * 128
                S.dma(WGB[sl][:], wgb_d[:, c0:c0 + 128].rearrange("(k p) n -> p k n", p=128), writes=[WGBB[sl]], queue="pool")
                S.dma(WBR[sl][:], wbr_d[:, dc * 128:(dc + 1) * 128].rearrange("(k p) n -> p k n", p=128), writes=[WBRB[sl]], queue="pool")
            load(0)
            for dc in range(NCH):
                if dc + 1 < NCH:
                    load(dc + 1)
                sl = dc % 2
                for tc in range(NTC):
                    ts = slice(tc * TC, (tc + 1) * TC)
                    hreads = [self.HNB[c][tc] for c in range(NCH)]
                    pg, pgb = self.psum.get()
                    py, pyb = self.psum.get()

                    def mm_g(e):
                        for k in range(NCH):
                            ins = e.matmul(pg[:], WGB[sl][:, k, :], self.HN[:, k, ts], start=(k == 0), stop=(k == NCH - 1))
                        return ins

                    def mm_y(e):
                        for k in range(4):
                            ins = e.matmul(py[:], WBR[sl][:, k, :], self.Y[br][:, k, ts], start=(k == 0), stop=(k == 3))
                        return ins
                    S.op("pe", mm_g, reads=hreads + [WGBB[sl]], writes=[pgb])
                    S.op("pe", mm_y, reads=[self.YB[br][tc], WBRB[sl]], writes=[pyb])
                    gt, gtb = gt_ring.get()
                    S.op("act", lambda e: e.activation(gt[:], pg[:], AF.Sigmoid), reads=[pgb], writes=[gtb])
                    if first:
                        S.op("dve", lambda e: e.tensor_tensor(self.M[:, dc, ts], gt[:], py[:], ALU.mult),
                             reads=[gtb, pyb], writes=[self.MB[dc][tc]])
                    else:
                        t, tb = t_ring.get()
                        S.op("dve", lambda e: e.tensor_tensor(t[:], gt[:], py[:], ALU.mult), reads=[gtb, pyb], writes=[tb])
                        S.op("pool", lambda e: e.tensor_tensor(self.M[:, dc, ts], self.M[:, dc, ts], t[:], ALU.add),
                             reads=[tb, self.MB[dc][tc]], writes=[self.MB[dc][tc]])
            S.full_barrier()
            self.st = old

    def outproj(self, wout_d):
        S = self.S
        with ExitStack() as st4:
            old, self.st = self.st, st4
            WO = self.sb("WO", [128, NCH, D], BF16); WOB = Buf()
            self.load_w(WO[:], wout_d, WOB)
            for tc in range(NTC):
                ts = slice(tc * TC, (tc + 1) * TC)
                for d2 in range(NCH):
                    po, pob = self.psum.get()

                    def mm(e):
                        for k in range(NCH):
                            ins = e.matmul(po[:], WO[:, k, d2 * 128:(d2 + 1) * 128], self.M[:, k, ts], start=(k == 0), stop=(k == NCH - 1))
                        return ins
                    S.op("pe", mm, reads=[self.MB[k][tc] for k in range(NCH)] + [WOB], writes=[pob])
                    S.op("dve", lambda e: e.tensor_tensor(self.X[:, d2, ts], po[:], self.X[:, d2, ts], ALU.add),
                         reads=[pob, self.XB[d2][tc]], writes=[self.XB[d2][tc]])
            S.full_barrier()
            self.st = old

    def final_norm_out(self, outT):
        S = self.S
        g0, _ = COLS["final_norm"]
        self._norm_rings_open()
        for tc in range(NTC):
            ts = slice(tc * TC, (tc + 1) * TC)
            pt, pb = self.psum.get()
            for c in range(NCH):
                sq, sqb = self.sq_ring.get()
                S.op("act", lambda e: e.activation(sq[:], self.X[:, c, ts], AF.Square),
                     reads=[self.XB[c][tc]], writes=[sqb])
                S.op("pe", lambda e: e.matmul(pt[:], self.ones_f[:], sq[:], start=(c == 0), stop=(c == NCH - 1)),
                     reads=[sqb, self.constb], writes=[pb])
            rs, rsb = self.rstd_ring.get()
            S.op("act", lambda e: e.activation(rs[:], pt[:], AF.Sqrt, bias=self.eps_t[:], scale=1.0 / D),
                 reads=[pb, self.constb], writes=[rsb])
            S.op("dve", lambda e: e.reciprocal(rs[:], rs[:]), reads=[rsb], writes=[rsb])
            for c in range(NCH):
                S.op("dve", lambda e: e.scalar_tensor_tensor(
                    self.X[:, c, ts], self.X[:, c, ts], self.cols[:, g0 + c:g0 + c + 1], rs[:],
                    ALU.mult, ALU.mult),
                    reads=[self.XB[c][tc], rsb, self.constb], writes=[self.XB[c][tc]])
                S.dma(outT[c * 128:(c + 1) * 128, ts], self.X[:, c, ts], reads=[self.XB[c][tc]])
        self._norm_rings_close()

    def dump_x(self, name):
        o = self.dout(name, [D, S_LEN])
        for c in range(NCH):
            for tc in range(NTC):
                ts = slice(tc * TC, (tc + 1) * TC)
                self.S.dma(o[c * 128:(c + 1) * 128, ts], self.X[:, c, ts], reads=[self.XB[c][tc]])

    def build(self, stop_after=None):
        nc = self.nc
        dbg = self.debug
        xT = self.din("xT", [D, S_LEN])
        cols_d = self.din("cols", [128, NCOLS])
        f1g = self.din("ffn1_w_gate", [D, DFF]); f1u = self.din("ffn1_w_up", [D, DFF]); f1d = self.din("ffn1_w_down", [DFF, D])
        f2g = self.din("ffn2_w_gate", [D, DFF]); f2u = self.din("ffn2_w_up", [D, DFF]); f2d = self.din("ffn2_w_down", [DFF, D])
        memT = self.din("memT", [D, 256])
        mem_wk = self.din("mem_w_k", [D, 512]); mem_wv = self.din("mem_w_v", [D, 512])
        w_qm = self.din("w_qm", [D, 512])
        w_gb = self.din("w_gb", [D, 3 * D])
        w_br = [self.din(n, [512, D]) for n in ("w_br_rwkv", "w_br_nsa_p", "w_br_mem")]
        w_out = self.din("w_out", [D, D])
        w_rwkv = self.din("w_rwkv", [D, 1792])
        w2_d = self.din("rwkv_w2", [64, 512]); a2_d = self.din("rwkv_a2", [64, 512]); g2_d = self.din("rwkv_g2", [128, 512])
        gng_d = self.din("gng_rep", [128, 512]); gnb_d = self.din("gnb_rep", [128, 512])
        ident_d = self.din("ident", [128, 128])
        rmk_d = self.din("rwkv_masks", [128, 3, 128])
        nd = {}
        nd["w_qn"] = self.din("w_qn", [D, 512]); nd["w_gn"] = self.din("w_gn", [D, 24]); nd["w_kvn"] = self.din("w_kvn", [D, 768])
        nd["shcf"] = self.din("shcf", [32, 247]); nd["efull"] = self.din("efull", [32, S_LEN]); nd["ov"] = self.din("ov", [127, 32])
        nd["abf"] = self.din("abf", [128, 2, 64]); nd["selg"] = self.din("selg", [24, 12, 128])
        nd["t31"] = self.din("t31", [128, 2, 512])
        nd["bmg"] = [self.din("bmg%d" % k, [128, 2, 512]) for k in range(3)]
        nd["msk"] = [self.din("msk%d" % k, [128, 128]) for k in range(3)]
        nd["bvcg"] = self.din("bvcg", [32, 2, 512]); nd["mskc"] = self.din("mskc", [32, 128])
        nd["cmp_w1"] = [self.din("cmp_k_w1", [2048, 256]), self.din("cmp_v_w1", [2048, 256])]
        nd["cmp_w2"] = [self.din("cmp_k_w2", [256, 64]), self.din("cmp_v_w2", [256, 64])]
        nd["cmp_peT"] = [self.din("cmp_pe_kT", [64, 32]), self.din("cmp_pe_vT", [64, 32])]
        outT = self.dout("outT", [D, S_LEN])
        with ExitStack() as st:
            self.st = st
            S = self.S = Sched(nc, st)
            self.X = self.sb("X", [128, NCH, S_LEN], F32)
            self.XB = [[Buf() for _ in range(NTC)] for _ in range(NCH)]
            self.cols = self.sb("cols", [128, NCOLS], F32)
            self.ones_f = self.sb("ones_f", [128, 128], F32)
            self.ones_b = self.sb("ones_b", [128, 128], BF16)
            self.eps_t = self.sb("eps_t", [128, 1], F32)
            self.gneps_t = self.sb("gneps_t", [128, 1], F32)
            self.ident_f = self.sb("ident_f", [128, 128], F32)
            self.constb = Buf("const")
            self.PS = self.ps("PSALL", [128, 8, 512])
            self.banks = [self.PS[:, i, :] for i in range(8)]
            self.bankb = [Buf() for _ in range(8)]
            self.psum = Ring(self.banks, self.bankb)
            S.dma(self.cols[:], cols_d, writes=[self.constb])
            S.op("dve", lambda e: e.memset(self.ones_f[:], 1.0), reads=[self.constb], writes=[self.constb])
            S.op("dve", lambda e: e.memset(self.ones_b[:], 1.0), reads=[self.constb], writes=[self.constb])
            S.op("dve", lambda e: e.memset(self.eps_t[:], EPS), reads=[self.constb], writes=[self.constb])
            S.op("dve", lambda e: e.memset(self.gneps_t[:], 64e-5), reads=[self.constb], writes=[self.constb])
            S.dma(self.ident_f[:], ident_d, reads=[self.constb], writes=[self.constb])
            for c in range(NCH):
                for tc in range(NTC):
                    ts = slice(tc * TC, (tc + 1) * TC)
                    S.dma(self.X[:, c, ts], xT[c * 128:(c + 1) * 128, ts], writes=[self.XB[c][tc]])

            def ffn_phase(wg, wu, wd, gname):
                with ExitStack() as st2:
                    self.st = st2
                    self.HN = self.sb("HN", [128, NCH, S_LEN], BF16)
                    self.HNB = [[Buf() for _ in range(NTC)] for _ in range(NCH)]
                    self.WG = [self.sb("WG%d" % i, [128, NCH, 512], BF16) for i in range(2)]
                    self.WU = [self.sb("WU%d" % i, [128, NCH, 512], BF16) for i in range(2)]
                    self.WD = [self.sb("WD%d" % i, [128, 4, D], BF16) for i in range(2)]
                    self.WGB = [Buf() for _ in range(2)]; self.WUB = [Buf() for _ in range(2)]; self.WDB = [Buf() for _ in range(2)]
                    self.a_ring = Ring([self.sb("a%d" % i, [128, 4, TC], BF16) for i in range(2)])
                    self.sg_ring = Ring([self.sb("sg%d" % i, [128, TC], F32) for i in range(2)])
                    self.ffn(wg, wu, wd, gname)
                    S.full_barrier()
                    self.st = st

            if "noffn1" not in dbg:
                ffn_phase(f1g, f1u, f1d, "ffn1_norm")
            if "x1" in dbg:
                self.dump_x("dbg_x1")
            if stop_after != "ffn1":
                with ExitStack() as st3:
                    self.st = st3
                    self.HN = self.sb("HN", [128, NCH, S_LEN], BF16)
                    self.HNB = [[Buf() for _ in range(NTC)] for _ in range(NCH)]
                    Yt = self.sb("Yt", [128, 4, S_LEN], BF16)
                    YBt = [Buf() for _ in range(NTC)]
                    self.Y = [Yt, Yt, Yt]
                    self.YB = [YBt, YBt, YBt]
                    self.rmsnorm_to_hn("mix_norm")
                    if "norwkv" not in dbg:
                        if "rwkvseq" in dbg:
                            self.rwkv_branch_seq(w_rwkv, w2_d, a2_d, g2_d, gng_d, gnb_d)
                        else:
                            self.rwkv_branch(w_rwkv, w2_d, a2_d, g2_d, gng_d, gnb_d, rmk_d)
                    else:
                        S.op("pool", lambda e: e.memset(Yt[:], 0.0), writes=YBt)
                    if "y_rwkv" in dbg:
                        self.dump_feat("dbg_y_rwkv", Yt, 4, YBt)
                    self.M = self.sb("M", [128, NCH, S_LEN], BF16)
                    self.MB = [[Buf() for _ in range(NTC)] for _ in range(NCH)]
                    do_merge = stop_after != "mix"
                    if do_merge:
                        self.fold(0, w_gb, w_br[0], True)
                    if "nomem" not in dbg:
                        self.mem_branch(memT, mem_wk, mem_wv, w_qm)
                    else:
                        S.op("pool", lambda e: e.memset(Yt[:], 0.0), writes=YBt)
                    if "y_mem" in dbg:
                        self.dump_feat("dbg_y_mem", Yt, 4, YBt)
                    if do_merge:
                        self.fold(2, w_gb, w_br[2], False)
                    if "nonsa" not in dbg:
                        self.nsa_branch(nd)
                    else:
                        S.op("pool", lambda e: e.memset(Yt[:], 0.0), writes=YBt)
                    if "y_nsa" in dbg:
                        self.dump_feat("dbg_y_nsa_p", Yt, 4, YBt)
                    if do_merge:
                        self.fold(1, w_gb, w_br[1], False)
                        self.outproj(w_out)
                    S.full_barrier()
                    self.st = st
                if "x2" in dbg:
                    self.dump_x("dbg_x2")
                if stop_after not in ("mix", "merge"):
                    ffn_phase(f2g, f2u, f2d, "ffn2_norm")
            self.final_norm_out(outT)
            S.wait_all_dma("sp")
            S.wait_all_dma("pool")
        return nc


NSA_PERM = np.concatenate([np.concatenate([np.arange(64 * j, 64 * j + 64), np.arange(64 * (4 + j), 64 * (4 + j) + 64)])
                           for j in range(4)])


def _t5_bucket_np(dist):
    n = np.maximum(dist, 0)
    nf = np.maximum(n, 1).astype(np.float32)
    large = 16 + (np.log(nf / np.float32(16)) / np.float32(math.log(128 / 16)) * np.float32(16)).astype(np.int32)
    large = np.minimum(large, 31)
    return np.where(n < 16, n, large)


def _nsa_consts(rel_bias):
    rb = np.asarray(rel_bias, np.float32)
    c = np.arange(128)[:, None]; p = np.arange(128)[None, :]
    out = {}
    hd = np.arange(8).reshape(2, 4)
    dists = [p - c, 128 + p - c, 512 + p - c]
    valid = [p >= c, np.ones((128, 128), bool), c > p]
    for k in range(3):
        bk = _t5_bucket_np(dists[k])
        g = rb[bk[:, None, None, :], hd[None, :, :, None]]
        out["bmg%d" % k] = np.ascontiguousarray(g.reshape(128, 2, 512))
        out["msk%d" % k] = np.where(valid[k], 0.0, -30000.0).astype(np.float32)
    out["t31"] = np.ascontiguousarray(np.broadcast_to(rb[31][hd][None, :, :, None], (128, 2, 4, 128)).reshape(128, 2, 512))
    m = np.arange(32)[:, None]
    dc = p - 16 * (m - 8) - 31
    bk = _t5_bucket_np(dc)
    g = rb[bk[:, None, None, :], hd[None, :, :, None]]
    out["bvcg"] = np.ascontiguousarray(g.reshape(32, 2, 512))
    mk = np.where((dc >= 0) & (m < 16), 0.0, -30000.0).astype(np.float32)
    mk[17:] = 0.0
    out["mskc"] = mk
    shcf = np.zeros((32, 247), np.float32)
    for x in range(247):
        r = x - 112
        if 0 <= r < 16:
            shcf[r, x] = 1.0
        elif r >= 16:
            shcf[16, x] = 1.0
    out["shcf"] = shcf
    ef = np.zeros((32, S_LEN), np.float32)
    ef[np.arange(S_LEN) // 64, np.arange(S_LEN)] = 1.0
    out["efull"] = ef
    ic = np.arange(127)[:, None]; jb = np.arange(32)[None, :]
    out["ov"] = ((ic * 16 <= jb * 64 + 63) & (ic * 16 + 31 >= jb * 64)).astype(np.float32)
    ab = np.zeros((128, 2, 64), np.float32)
    for pp in range(128):
        curr = 1 if pp >= 64 else 0
        for mm in range(64):
            jr = mm - 32
            if jr <= curr - 2:
                ab[pp, 0, mm] = 1.0
            if jr in (curr, curr - 1):
                ab[pp, 1, mm] = 1e6
    out["abf"] = ab
    selg = np.zeros((24, 12, 128), np.float32)
    for br in range(3):
        for j in range(4):
            for mm in range(128):
                selg[br * 8 + (mm // 64) * 4 + j, br * 4 + j, mm] = 1.0
    out["selg"] = selg
    return out


def prep_inputs(inputs, b):
    m = {}
    m["xT"] = np.ascontiguousarray(inputs["x"][b].T)
    cols = np.zeros((128, NCOLS), np.float32)
    for n in ("ffn1_norm", "mix_norm", "ffn2_norm", "final_norm", "mem_norm"):
        c0, k = COLS[n]
        cols[:, c0:c0 + k] = _colpack(np.asarray(inputs[n]).reshape(-1))
    for n, src in (("mu", "rwkv_mu"), ("w0", "rwkv_w0"), ("a0", "rwkv_a0"), ("k_k", "rwkv_k_k"), ("k_a", "rwkv_k_a"), ("r_k", "rwkv_r_k"),
                   ("gn_g", "rwkv_gn_gain"), ("gn_b", "rwkv_gn_bias")):
        c0, k = COLS[n]
        cols[:, c0:c0 + k] = _colpack(np.asarray(inputs[src]).reshape(-1))
    m["cols"] = cols
    m["w_rwkv"] = np.ascontiguousarray(np.asarray(inputs["w_in"])[0][:, 0:1792])
    m["rwkv_w2"] = np.ascontiguousarray(np.asarray(inputs["rwkv_w2"])[0])
    m["rwkv_a2"] = np.ascontiguousarray(np.asarray(inputs["rwkv_a2"])[0])
    m["rwkv_g2"] = np.ascontiguousarray(np.asarray(inputs["rwkv_g2"])[0])
    m["gng_rep"] = np.ascontiguousarray(np.broadcast_to(np.asarray(inputs["rwkv_gn_gain"]).reshape(1, 512), (128, 512)))
    m["gnb_rep"] = np.ascontiguousarray(np.broadcast_to(np.asarray(inputs["rwkv_gn_bias"]).reshape(1, 512), (128, 512)))
    m["ident"] = np.eye(128, dtype=np.float32)
    si = np.arange(128)[:, None]; ti = np.arange(128)[None, :]
    same = (si // 64) == (ti // 64)
    mk = np.zeros((128, 3, 128), np.float32)
    mk[:, 0, :] = np.where(same & (si < ti), -1.0, 0.0)
    mk[:, 1, :] = np.where(same & (ti < si), -1.0, 0.0)
    mk[:, 2, :] = np.where(same & (si <= ti), 1.0, 0.0)
    m["rwkv_masks"] = mk
    w_in_ = np.asarray(inputs["w_in"])[0]
    m["w_qn"] = np.ascontiguousarray(w_in_[:, 1792:2304][:, NSA_PERM])
    m["w_kvn"] = np.ascontiguousarray(w_in_[:, 2304:3072])
    m["w_gn"] = np.ascontiguousarray(w_in_[:, 3072:3096])
    m.update(_nsa_consts(inputs["rel_bias"]))
    for n in ("cmp_k_w1", "cmp_v_w1", "cmp_k_w2", "cmp_v_w2"):
        m[n] = np.ascontiguousarray(np.asarray(inputs[n])[0])
    m["cmp_pe_kT"] = np.ascontiguousarray(np.asarray(inputs["cmp_pe_k"])[0].T)
    m["cmp_pe_vT"] = np.ascontiguousarray(np.asarray(inputs["cmp_pe_v"])[0].T)
    for n in ("ffn1_w_gate", "ffn1_w_up", "ffn1_w_down", "ffn2_w_gate", "ffn2_w_up", "ffn2_w_down",
              "mem_w_k", "mem_w_v", "w_br_rwkv", "w_br_mem", "w_out"):
        m[n] = np.ascontiguousarray(np.asarray(inputs[n])[0])
    m["memT"] = np.ascontiguousarray(inputs["mem"][b].T)
    w_in = np.asarray(inputs["w_in"])[0]
    m["w_qm"] = np.ascontiguousarray(w_in[:, 3096:3608])
    m["w_gb"] = np.ascontiguousarray(w_in[:, 3608:6680])
    m["w_br_nsa_p"] = np.ascontiguousarray(np.asarray(inputs["w_br_nsa"])[0][NSA_PERM, :])
    return m


_CACHE = {}


def kernel(**inputs):
    inputs = {k: np.asarray(v) for k, v in inputs.items()}
    if "nc" not in _CACHE:
        _CACHE["nc"] = Builder().build()
    nc = _CACHE["nc"]
    n = 8
    in_maps = [prep_inputs(inputs, b) for b in range(n)]
    res = run_bass_kernel_spmd(nc, in_maps, core_ids=list(range(n)))
    out = np.stack([np.ascontiguousarray(r["outT"].T) for r in res.results], axis=0)
    return out.astype(np.float32)
```

```python
import math
from contextlib import ExitStack
import numpy as np
import concourse.bass as bass
import concourse.mybir as mybir
from concourse.bass_utils import run_bass_kernel_spmd

F32 = mybir.dt.float32
BF16 = mybir.dt.bfloat16
AF = mybir.ActivationFunctionType
ALU = mybir.AluOpType
AX = mybir.AxisListType

D = 1024
S_LEN = 2048
DFF = 2816
NCH = 8
TC = 512
NTC = S_LEN // TC
EPS = 1e-6


class Buf:
    __slots__ = ("name", "last_w", "readers")

    def __init__(self, name=""):
        self.name = name
        self.last_w = None
        self.readers = []


class Sched:
    ENG = ("pe", "act", "dve", "pool", "sp")

    def __init__(self, nc, stack, n_dma_sems=16):
        self.nc = nc
        self.eng = {"pe": nc.tensor, "act": nc.scalar, "dve": nc.vector,
                    "pool": nc.gpsimd, "sp": nc.sync}
        self.sem = {}
        for e in ("pe", "act", "dve", "pool"):
            self.sem[e] = stack.enter_context(nc.semaphore("s_" + e))
        self.cnt = {e: 0 for e in ("pe", "act", "dve", "pool")}
        nq = {"sp": 28, "pool": 28, "act": 8}
        self.dsem = []
        self.qsems = {}
        for q, n in nq.items():
            self.qsems[q] = list(range(len(self.dsem), len(self.dsem) + n))
            for i in range(n):
                self.dsem.append(stack.enter_context(nc.semaphore("d%s%d" % (q, i))))
        self.dcnt = [0] * len(self.dsem)
        self.dnext = {q: 0 for q in nq}
        self.waited = {e: {} for e in self.ENG}
        self.n_ops = 0
        self.n_waits = 0

    def _semobj(self, key):
        return self.sem[key] if isinstance(key, str) else self.dsem[key]

    def _need(self, engine, toks):
        best = {}
        for t in toks:
            if t is None:
                continue
            key, val = t
            if best.get(key, 0) < val:
                best[key] = val
        w = self.waited[engine]
        for key, val in best.items():
            if w.get(key, 0) >= val:
                continue
            self.eng[engine].wait_ge(self._semobj(key), val)
            w[key] = val
            self.n_waits += 1

    @staticmethod
    def _deps(reads, writes):
        toks = []
        for b in reads:
            toks.append(b.last_w)
        for b in writes:
            toks.append(b.last_w)
            toks.extend(b.readers)
        return toks

    @staticmethod
    def _commit(tok, reads, writes):
        for b in reads:
            b.readers.append(tok)
            if len(b.readers) > 48:
                best = {}
                for k, v in b.readers:
                    if best.get(k, 0) < v:
                        best[k] = v
                b.readers = list(best.items())
        for b in writes:
            b.last_w = tok
            b.readers = []

    def op(self, engine, fn, reads=(), writes=()):
        self._need(engine, self._deps(reads, writes))
        ins = fn(self.eng[engine])
        self.cnt[engine] += 1
        ins.then_inc(self.sem[engine], 1)
        tok = (engine, self.cnt[engine])
        self._commit(tok, reads, writes)
        self.n_ops += 1
        return tok

    def dma(self, out_ap, in_ap, reads=(), writes=(), queue="sp", **kw):
        pool = self.qsems[queue]
        i = pool[self.dnext[queue]]
        self.dnext[queue] = (self.dnext[queue] + 1) % len(pool)
        prev = [(i, self.dcnt[i])] if self.dcnt[i] else []
        self._need(queue, self._deps(reads, writes) + prev)
        ins = self.eng[queue].dma_start(out=out_ap, in_=in_ap, **kw)
        self.dcnt[i] += 16
        ins.then_inc(self.dsem[i], 16)
        tok = (i, self.dcnt[i])
        self._commit(tok, reads, writes)
        self.n_ops += 1
        return tok

    def barrier(self, bufs):
        toks = []
        for b in bufs:
            toks.append(b.last_w)
            toks.extend(b.readers)
        for e in self.ENG:
            self._need(e, toks)

    def full_barrier(self):
        toks = [(e, self.cnt[e]) for e in ("pe", "act", "dve", "pool") if self.cnt[e]]
        toks += [(i, self.dcnt[i]) for i in range(len(self.dsem)) if self.dcnt[i]]
        for e in self.ENG:
            self._need(e, toks)

    def wait_all_dma(self, engine="sp"):
        for i in range(len(self.dsem)):
            if self.dcnt[i]:
                self.eng[engine].wait_ge(self.dsem[i], self.dcnt[i])


class Ring:
    def __init__(self, tiles, bufs=None):
        self.tiles = tiles
        self.bufs = bufs if bufs is not None else [Buf() for _ in tiles]
        self.i = 0

    def get(self):
        t, b = self.tiles[self.i], self.bufs[self.i]
        self.i = (self.i + 1) % len(self.tiles)
        return t, b

    def get_pair_idx(self):
        if self.i % 2:
            self.i = (self.i + 1) % len(self.tiles)
        k = self.i
        self.i = (self.i + 2) % len(self.tiles)
        return k, self.bufs[k], self.bufs[k + 1]


COLS = {}
_c = 0
for _n, _k in (("ffn1_norm", 8), ("mix_norm", 8), ("ffn2_norm", 8), ("final_norm", 8),
               ("mem_norm", 8), ("mu", 14), ("w0", 4), ("a0", 4), ("k_k", 4), ("k_a", 4), ("r_k", 4), ("gn_g", 4), ("gn_b", 4)):
    COLS[_n] = (_c, _k)
    _c += _k
NCOLS = _c


def _colpack(v):
    v = np.asarray(v, np.float32).reshape(-1, 128)
    return np.ascontiguousarray(v.T)


class Builder:
    def __init__(self, debug=()):
        self.debug = set(debug)
        self._rk_stage = 99
        self._rk_tiles = S_LEN // 128
        for d_ in self.debug:
            if d_.startswith("rkstage"):
                self._rk_stage = int(d_[7:])
            if d_.startswith("rktiles"):
                self._rk_tiles = int(d_[7:])
        self.nc = bass.Bass("TRN2", target_bir_lowering=False)
        self.dram_in = {}
        self.dram_out = {}

    def din(self, name, shape, dt=F32):
        t = self.nc.dram_tensor(name, list(shape), dt, kind="ExternalInput").ap()
        self.dram_in[name] = t
        return t

    def dout(self, name, shape, dt=F32):
        t = self.nc.dram_tensor(name, list(shape), dt, kind="ExternalOutput").ap()
        self.dram_out[name] = t
        return t

    def sb(self, name, shape, dt):
        self._uid = getattr(self, "_uid", 0) + 1
        return self.st.enter_context(self.nc.sbuf_tensor("sb%d_%s" % (self._uid, name), list(shape), dt))

    def ps(self, name, shape, dt=F32):
        return self.st.enter_context(self.nc.psum_tensor("ps_" + name, list(shape), dt))

    def _norm_rings_open(self):
        self._nst_old = self.st
        self._nst = ExitStack()
        self.st = self._nst
        self.sq_ring = Ring([self.sb("sq%d" % i, [128, TC], F32) for i in range(2)])
        self.rstd_ring = Ring([self.sb("RSTD%d" % i, [128, TC], F32) for i in range(2)])
        self.st = self._nst_old

    def _norm_rings_close(self):
        self.S.full_barrier()
        self._nst.close()

    def rmsnorm_to_hn(self, gname):
        S = self.S
        g0, _ = COLS[gname]
        self._norm_rings_open()
        for tc in range(NTC):
            ts = slice(tc * TC, (tc + 1) * TC)
            pt, pb = self.psum.get()
            for c in range(NCH):
                sq, sqb = self.sq_ring.get()
                S.op("act", lambda e: e.activation(sq[:], self.X[:, c, ts], AF.Square),
                     reads=[self.XB[c][tc]], writes=[sqb])
                S.op("pe", lambda e: e.matmul(pt[:], self.ones_f[:], sq[:], start=(c == 0), stop=(c == NCH - 1)),
                     reads=[sqb, self.constb], writes=[pb])
            rs, rsb = self.rstd_ring.get()
            S.op("act", lambda e: e.activation(rs[:], pt[:], AF.Sqrt, bias=self.eps_t[:], scale=1.0 / D),
                 reads=[pb, self.constb], writes=[rsb])
            S.op("dve", lambda e: e.reciprocal(rs[:], rs[:]), reads=[rsb], writes=[rsb])
            for c in range(NCH):
                S.op("dve", lambda e: e.scalar_tensor_tensor(
                    self.HN[:, c, ts], self.X[:, c, ts], self.cols[:, g0 + c:g0 + c + 1], rs[:],
                    ALU.mult, ALU.mult),
                    reads=[self.XB[c][tc], rsb, self.constb], writes=[self.HNB[c][tc]])

        self._norm_rings_close()

    def ffn(self, wg, wu, wd, gname):
        S = self.S
        self.rmsnorm_to_hn(gname)
        groups = [(i, min(4, 22 - i)) for i in range(0, 22, 4)]

        def load(gi):
            f0, nf = groups[gi]
            slot = gi % 2
            S.dma(self.WG[slot][:, :, 0:nf * 128],
                  wg[:, f0 * 128:(f0 + nf) * 128].rearrange("(k p) n -> p k n", p=128),
                  writes=[self.WGB[slot]], queue="pool")
            S.dma(self.WU[slot][:, :, 0:nf * 128],
                  wu[:, f0 * 128:(f0 + nf) * 128].rearrange("(k p) n -> p k n", p=128),
                  writes=[self.WUB[slot]], queue="pool")
            S.dma(self.WD[slot][:, 0:nf, :],
                  wd[f0 * 128:(f0 + nf) * 128, :].rearrange("(f p) n -> p f n", p=128),
                  writes=[self.WDB[slot]], queue="pool")

        load(0)
        for gi, (f0, nf) in enumerate(groups):
            if gi + 1 < len(groups):
                load(gi + 1)
            slot = gi % 2
            WG, WU, WD = self.WG[slot], self.WU[slot], self.WD[slot]
            for tc in range(NTC):
                ts = slice(tc * TC, (tc + 1) * TC)
                hreads = [self.HNB[c][tc] for c in range(NCH)]
                a_t, a_b = self.a_ring.get()
                for f in range(nf):
                    pg, pgb = self.psum.get()
                    pu, pub = self.psum.get()

                    def mm_g(e):
                        for k in range(NCH):
                            ins = e.matmul(pg[:], WG[:, k, f * 128:(f + 1) * 128], self.HN[:, k, ts],
                                           start=(k == 0), stop=(k == NCH - 1))
                        return ins

                    def mm_u(e):
                        for k in range(NCH):
                            ins = e.matmul(pu[:], WU[:, k, f * 128:(f + 1) * 128], self.HN[:, k, ts],
                                           start=(k == 0), stop=(k == NCH - 1))
                        return ins
                    S.op("pe", mm_g, reads=hreads + [self.WGB[slot]], writes=[pgb])
                    S.op("pe", mm_u, reads=hreads + [self.WUB[slot]], writes=[pub])
                    sg, sgb = self.sg_ring.get()
                    S.op("act", lambda e: e.activation(sg[:], pg[:], AF.Silu), reads=[pgb], writes=[sgb])
                    S.op("dve", lambda e: e.tensor_tensor(a_t[:, f, :], sg[:], pu[:], ALU.mult),
                         reads=[sgb, pub], writes=[a_b])
                for dc in range(NCH):
                    po, pob = self.psum.get()

                    def mm_d(e):
                        for f in range(nf):
                            ins = e.matmul(po[:], WD[:, f, dc * 128:(dc + 1) * 128], a_t[:, f, :],
                                           start=(f == 0), stop=(f == nf - 1))
                        return ins
                    S.op("pe", mm_d, reads=[a_b, self.WDB[slot]], writes=[pob])
                    S.op("dve", lambda e: e.scalar_tensor_tensor(
                        self.X[:, dc, ts], po[:], 0.5, self.X[:, dc, ts], ALU.mult, ALU.add),
                        reads=[pob, self.XB[dc][tc]], writes=[self.XB[dc][tc]])


    def load_w(self, tile_ap, dram_ap, buf):
        self.S.dma(tile_ap, dram_ap.rearrange("(k p) n -> p k n", p=128), writes=[buf], queue="pool")

    def dump_feat(self, name, tile, nchunks, buf_list):
        o = self.dout(name, [nchunks * 128, S_LEN])
        for c in range(nchunks):
            self.S.dma(o[c * 128:(c + 1) * 128, :], tile[:, c, :], reads=buf_list, queue="pool")


    def rwkv_branch_seq(self, w_rwkv, w2_d, a2_d, g2_d, gng_d, gnb_d):
        S = self.S
        CN = COLS
        NT = S_LEN // 128
        with ExitStack() as st4:
            old, self.st = self.st, st4
            WR = self.sb("WR", [128, NCH, 1792], BF16); WRB = Buf()
            W2 = self.sb("W2A2", [128, 512], F32); A2 = W2; G2 = self.sb("G2", [128, 512], F32)
            GNG = self.sb("GNG", [128, 512], F32); GNB = self.sb("GNB", [128, 512], F32)
            BO = self.sb("BO", [128, 128], F32); BOb = self.sb("BOb", [128, 128], BF16)
            ID2 = self.sb("ID2", [128, 64], BF16)
            OMK = self.sb("OMK", [128, 4], F32)
            cb = Buf()
            self.load_w(WR[:], w_rwkv, WRB)
            S.dma(W2[0:64, :], w2_d, writes=[cb]); S.dma(A2[64:128, :], a2_d, writes=[cb]); S.dma(G2[:], g2_d, writes=[cb])
            S.dma(GNG[:], gng_d, writes=[cb]); S.dma(GNB[:], gnb_d, writes=[cb])
            S.op("dve", lambda e: e.memset(BO[:], 0.0), reads=[cb], writes=[cb])
            S.op("dve", lambda e: e.memset(BO[0:64, 0:64], 1.0), reads=[cb], writes=[cb])
            S.op("dve", lambda e: e.memset(BO[64:128, 64:128], 1.0), reads=[cb], writes=[cb])
            S.op("dve", lambda e: e.tensor_copy(BOb[:], BO[:]), reads=[cb], writes=[cb])
            S.op("dve", lambda e: e.tensor_copy(ID2[0:64, :], self.ident_f[0:64, 0:64]), reads=[cb, self.constb], writes=[cb])
            S.op("dve", lambda e: e.tensor_copy(ID2[64:128, :], self.ident_f[64:128, 64:128]), reads=[cb, self.constb], writes=[cb])
            ka0 = CN["k_a"][0]
            S.op("dve", lambda e: e.tensor_scalar(OMK[:], self.cols[:, ka0:ka0 + 4], -1.0, 1.0, ALU.mult, ALU.add),
                 reads=[cb, self.constb], writes=[cb])
            P32 = self.sb("P32", [128, 14, 129], F32); P32B = Buf()
            DD = self.sb("DD", [128, 128], F32); DDB = Buf()
            CAR = self.sb("CAR", [128, 14, 1], F32)
            PL = P32[:, :, 1:129]; PLB = P32B
            TW = self.sb("TW", [64, 128], F32); SGg = self.sb("SGg", [128, 128], F32)
            WD = self.sb("WD", [128, 4, 128], F32); SIG = WD
            A32 = self.sb("A32", [128, 4, 128], F32)
            KK = self.sb("KK", [128, 4, 128], F32); SQ = self.sb("SQ", [128, 4, 128], F32)
            KKN = self.sb("KKN", [128, 4, 128], F32); NB = self.sb("NB", [128, 4, 128], F32)
            KM = self.sb("KM", [128, 4, 128], F32); BON = self.sb("BON", [128, 4, 128], F32)
            RM = self.sb("RM", [128, 4, 128, 2], BF16)
            VDr = Ring([self.sb("VD%d" % i, [128, 4, 64], BF16) for i in range(2)])
            H = self.sb("H", [128, 4, 64], F32); Hb = self.sb("Hb", [128, 4, 64], BF16); HK = self.sb("HK", [128, 4, 64], BF16)
            T1 = self.sb("T1", [128, 4, 64], F32); T2r = Ring([self.sb("T2_%d" % i, [128, 4, 64], F32) for i in range(2)])
            YST = [self.sb("YST%d" % i, [2, 4, 256], F32) for i in range(2)]; YSTB = [Buf(), Buf()]
            YTOK = A32[:].rearrange("p c t -> p (c t)").rearrange("p (c h v) -> p c h v", c=4, h=2); YTOKB = Buf()
            YC = KKN[:].rearrange("p c t -> p (c t)").rearrange("p (a v) -> p a v", a=8)
            ST8 = self.sb("ST8", [128, 8], F32); ST8b = self.sb("ST8b", [128, 8], F32)
            YF = SQ
            db = Buf(); hb = Buf(); hbb = Buf(); hkb = Buf(); t1b = Buf(); vrb = Buf(); vtb = Buf(); rmb = Buf(); yb = Buf()
            S.op("pool", lambda e: e.memset(P32[:], 0.0), writes=[P32B])
            S.op("pool", lambda e: e.memset(RM[:], 0.0), writes=[rmb])
            S.op("pool", lambda e: e.memset(H[:], 0.0), writes=[hb])
            mu0 = CN["mu"][0]; w00 = CN["w0"][0]; a00 = CN["a0"][0]; kk0 = CN["k_k"][0]; rk0 = CN["r_k"][0]
            ident = self.ident_f
            for i in range(NT):
                t0 = i * 128
                tcix = t0 // TC
                tsl = slice(t0, t0 + 128)
                hreads = [self.HNB[c][tcix] for c in range(NCH)]
                for cg in range(4):
                    c0 = cg * 4
                    n = min(4, 14 - c0)
                    p, pb = self.psum.get()

                    def mm(e):
                        for cc in range(n):
                            for k in range(NCH):
                                ins = e.matmul(p[:, cc * 128:(cc + 1) * 128], WR[:, k, (c0 + cc) * 128:(c0 + cc + 1) * 128],
                                               self.HN[:, k, tsl], start=(k == 0), stop=(k == NCH - 1))
                        return ins
                    S.op("pe", mm, reads=hreads + [WRB], writes=[pb])
                    S.op("act", lambda e: e.copy(P32[:, c0:c0 + n, 1:129], p[:, 0:n * 128].rearrange("p (c t) -> p c t", c=n)),
                         reads=[pb], writes=[P32B])
                S.op("dve", lambda e: e.tensor_copy(CAR[:], P32[:, :, 128:129]), reads=[P32B], writes=[DDB])
                for c in range(14):
                    S.op("dve", lambda e: e.tensor_tensor(DD[:], P32[:, c, 0:128], P32[:, c, 1:129], ALU.subtract), reads=[P32B, DDB], writes=[DDB])
                    S.op("dve", lambda e: e.scalar_tensor_tensor(P32[:, c, 1:129], DD[:], self.cols[:, mu0 + c:mu0 + c + 1], P32[:, c, 1:129],
                                                                 ALU.mult, ALU.add), reads=[DDB, P32B, self.constb], writes=[P32B])
                S.op("dve", lambda e: e.tensor_copy(P32[:, :, 0:1], CAR[:]), reads=[P32B, DDB], writes=[P32B])
                S.op("act", lambda e: e.activation(TW[:], PL[0:64, 12, :], AF.Tanh), reads=[PLB], writes=[db])
                S.op("act", lambda e: e.activation(SGg[:], PL[:, 13, :], AF.Sigmoid), reads=[PLB], writes=[db])
                pz, pzb = self.psum.get(); pa, pab = self.psum.get()

                def mmz(e):
                    for fc in range(4):
                        ins = e.matmul(pz[:, fc * 128:(fc + 1) * 128], W2[0:64, fc * 128:(fc + 1) * 128], TW[:], start=True, stop=True)
                    return ins

                def mma(e):
                    for fc in range(4):
                        ins = e.matmul(pa[:, fc * 128:(fc + 1) * 128], A2[64:128, fc * 128:(fc + 1) * 128], PL[64:128, 12, :], start=True, stop=True)
                    return ins

                S.op("pe", mmz, reads=[db, cb], writes=[pzb])
                S.op("pe", mma, reads=[PLB, cb], writes=[pab])
                for fc in range(4):
                    S.op("act", lambda e: e.activation(SIG[:, fc, :], pz[:, fc * 128:(fc + 1) * 128], AF.Sigmoid,
                                                       bias=self.cols[:, w00 + fc:w00 + fc + 1]), reads=[pzb, self.constb], writes=[db])
                    S.op("act", lambda e: e.activation(A32[:, fc, :], pa[:, fc * 128:(fc + 1) * 128], AF.Sigmoid,
                                                       bias=self.cols[:, a00 + fc:a00 + fc + 1]), reads=[pab, self.constb], writes=[db, YTOKB])
                S.op("act", lambda e: e.activation(WD[:], SIG[:], AF.Exp, scale=-0.6065306597126334), reads=[db], writes=[db])
                for fc in range(4):
                    S.op("dve", lambda e: e.tensor_scalar(KK[:, fc, :], PL[:, 4 + fc, :], self.cols[:, kk0 + fc:kk0 + fc + 1], None, ALU.mult),
                         reads=[PLB, self.constb], writes=[db])
                S.op("dve", lambda e: e.tensor_tensor(SQ[:], KK[:], KK[:], ALU.mult), reads=[db], writes=[db])
                pss, pssb = self.psum.get()
                S.op("pe", lambda e: e.matmul(pss[:], BO[:], SQ[:].rearrange("p c t -> p (c t)"), start=True, stop=True), reads=[db, cb], writes=[pssb])
                S.op("act", lambda e: e.activation(SQ[:], pss[:].rearrange("p (c t) -> p c t", c=4), AF.Sqrt), reads=[pssb, db], writes=[db])
                S.op("dve", lambda e: e.tensor_scalar(SQ[:], SQ[:], 1e-12, None, ALU.max), reads=[db], writes=[db])
                S.op("dve", lambda e: e.reciprocal(SQ[:], SQ[:]), reads=[db], writes=[db])
                S.op("dve", lambda e: e.tensor_tensor(KKN[:], KK[:], SQ[:], ALU.mult), reads=[db], writes=[db, yb])
                S.op("dve", lambda e: e.scalar_tensor_tensor(NB[:], KKN[:], -1.0, A32[:], ALU.mult, ALU.mult), reads=[db], writes=[db])
                for fc in range(4):
                    S.op("dve", lambda e: e.tensor_scalar(KK[:, fc, :], A32[:, fc, :], self.cols[:, ka0 + fc:ka0 + fc + 1], OMK[:, fc:fc + 1],
                                                          ALU.mult, ALU.add), reads=[db, cb, self.constb], writes=[db])
                S.op("dve", lambda e: e.tensor_tensor(KM[:], PL[:, 4:8, :], KK[:], ALU.mult), reads=[db, PLB], writes=[db])
                S.op("dve", lambda e: e.tensor_tensor(SQ[:], PL[:, 0:4, :], KM[:], ALU.mult), reads=[db, PLB], writes=[db])
                for fc in range(4):
                    S.op("dve", lambda e: e.tensor_scalar(SQ[:, fc, :], SQ[:, fc, :], self.cols[:, rk0 + fc:rk0 + fc + 1], None, ALU.mult),
                         reads=[db, self.constb], writes=[db])
                pbn, pbnb = self.psum.get()
                S.op("pe", lambda e: e.matmul(pbn[:], BO[:], SQ[:].rearrange("p c t -> p (c t)"), start=True, stop=True), reads=[db, cb], writes=[pbnb])
                S.op("dve", lambda e: e.tensor_tensor(BON[:], pbn[:].rearrange("p (c t) -> p c t", c=4), PL[:, 8:12, :], ALU.mult),
                     reads=[pbnb, PLB], writes=[db])
                S.op("dve", lambda e: e.tensor_copy(RM[0:64, :, :, 0], PL[0:64, 0:4, :]), reads=[PLB, rmb], writes=[rmb])
                S.op("dve", lambda e: e.tensor_copy(RM[64:128, :, :, 1], PL[64:128, 0:4, :]), reads=[PLB, rmb], writes=[rmb])
                for tt in range(128):
                    pvb_t, pvbb = self.psum.get()

                    VD, vdb = VDr.get()
                    S.op("pool", lambda e: e.tensor_tensor(VD[:], ID2[:].unsqueeze(1).to_broadcast([128, 4, 64]),
                                                           PL[:, 8:12, tt:tt + 1].to_broadcast([128, 4, 64]), ALU.mult),
                         reads=[PLB, cb], writes=[vdb])
                    S.op("pe", lambda e: e.matmul(pvb_t[:, 0:256], BOb[:], VD[:].rearrange("p c v -> p (c v)"), start=True, stop=True),
                         reads=[vdb, cb], writes=[pvbb])
                    T2, t2b = T2r.get()
                    S.op("pool" if False else "dve", lambda e: e.tensor_tensor(
                        T2[:], pvb_t[:, 0:256].rearrange("p (c v) -> p c v", c=4), KM[:, :, tt:tt + 1].to_broadcast([128, 4, 64]), ALU.mult),
                        reads=[pvbb, db], writes=[t2b])
                    S.op("dve", lambda e: e.tensor_tensor(HK[:], H[:], KKN[:, :, tt:tt + 1].to_broadcast([128, 4, 64]), ALU.mult),
                         reads=[hb, db], writes=[hkb])
                    psa, psab = self.psum.get()
                    S.op("pe", lambda e: e.matmul(psa[:, 0:256], BOb[:], HK[:].rearrange("p c v -> p (c v)"), start=True, stop=True),
                         reads=[hkb, cb], writes=[psab])
                    S.op("dve", lambda e: e.tensor_tensor(T1[:], psa[:, 0:256].rearrange("p (c v) -> p c v", c=4),
                                                          NB[:, :, tt:tt + 1].to_broadcast([128, 4, 64]), ALU.mult),
                         reads=[psab, db], writes=[t1b])
                    S.op("dve", lambda e: e.tensor_tensor(H[:], H[:], WD[:, :, tt:tt + 1].to_broadcast([128, 4, 64]), ALU.mult),
                         reads=[hb, db], writes=[hb])
                    S.op("dve", lambda e: e.tensor_tensor(T1[:], T1[:], T2[:], ALU.add), reads=[t1b, t2b], writes=[t1b])
                    S.op("dve", lambda e: e.tensor_tensor(H[:], H[:], T1[:], ALU.add), reads=[hb, t1b], writes=[hb])
                    S.op("act", lambda e: e.copy(Hb[:], H[:]), reads=[hb], writes=[hbb])
                    py, pyb = self.psum.get()

                    def mmy(e):
                        for fc in range(4):
                            ins = e.matmul(py[0:2, fc * 64:(fc + 1) * 64], RM[:, fc, tt, :], Hb[:, fc, :], start=True, stop=True)
                        return ins
                    S.op("pe", mmy, reads=[hbb, rmb], writes=[pyb])
                    slot = tt % 2
                    S.op("act", lambda e: e.copy(YST[slot][0:2, 0, :], py[0:2, 0:256]), reads=[pyb], writes=[YSTB[slot]])
                    for hp in range(2):
                        S.dma(YTOK[tt:tt + 1, :, hp, :], YST[slot][hp:hp + 1, 0, :].rearrange("p (c v) -> p c v", c=4),
                              reads=[YSTB[slot], db], writes=[YTOKB])
                YT8 = YTOK.rearrange("t c h v -> t (c h) v")
                S.op("dve", lambda e: e.tensor_reduce(ST8[:], YT8, AX.X, ALU.add), reads=[YTOKB], writes=[yb])
                S.op("dve", lambda e: e.tensor_scalar(ST8[:], ST8[:], 1.0 / 64, None, ALU.mult), reads=[yb], writes=[yb])
                S.op("dve", lambda e: e.tensor_tensor(YC, YT8, ST8[:].unsqueeze(2).to_broadcast([128, 8, 64]), ALU.subtract),
                     reads=[YTOKB, yb], writes=[yb, db])
                S.op("dve", lambda e: e.tensor_tensor(YTOK.rearrange("t c h v -> t (c h) v"), YC, YC, ALU.mult), reads=[yb, YTOKB], writes=[YTOKB])
                S.op("dve", lambda e: e.tensor_reduce(ST8b[:], YT8, AX.X, ALU.add), reads=[YTOKB], writes=[yb])
                S.op("act", lambda e: e.activation(ST8b[:], ST8b[:], AF.Sqrt, bias=self.gneps_t[:], scale=1.0 / 64), reads=[yb, self.constb], writes=[yb])
                S.op("dve", lambda e: e.reciprocal(ST8b[:], ST8b[:]), reads=[yb], writes=[yb])
                S.op("dve", lambda e: e.tensor_tensor(YC, YC, ST8b[:].unsqueeze(2).to_broadcast([128, 8, 64]), ALU.mult), reads=[yb], writes=[yb])
                YCf = YC.rearrange("t a v -> t (a v)")
                S.op("dve", lambda e: e.tensor_tensor(YCf, YCf, GNG[:], ALU.mult), reads=[yb, cb], writes=[yb])
                S.op("dve", lambda e: e.tensor_tensor(YCf, YCf, GNB[:], ALU.add), reads=[yb, cb], writes=[yb])
                pyt, pytb = self.psum.get(); pg, pgb = self.psum.get()

                def mmt2(e):
                    for fc in range(4):
                        ins = e.transpose(pyt[:, fc * 128:(fc + 1) * 128], YC[:, 2 * fc:2 * fc + 2, :].rearrange("t a v -> t (a v)"), ident[:])
                    for fc in range(4):
                        ins = e.matmul(pg[:, fc * 128:(fc + 1) * 128], G2[:, fc * 128:(fc + 1) * 128], SGg[:], start=True, stop=True)
                    return ins
                S.op("pe", mmt2, reads=[yb, self.constb, db, cb], writes=[pytb, pgb])
                S.op("dve", lambda e: e.tensor_tensor(YF[:], pyt[:].rearrange("p (c t) -> p c t", c=4), BON[:], ALU.add), reads=[pytb, db], writes=[yb, db])
                S.op("dve", lambda e: e.tensor_tensor(self.Y[0][:, :, tsl], YF[:], pg[:].rearrange("p (c t) -> p c t", c=4), ALU.mult),
                     reads=[yb, db, pgb], writes=[self.YB[0][tcix]])
            S.full_barrier()
            self.st = old


    def rwkv_branch(self, w_rwkv, w2_d, a2_d, g2_d, gng_d, gnb_d, mk_d):
        S = self.S
        CN = COLS
        NT = S_LEN // 128
        CDEC = 0.6065306597126334
        with ExitStack() as st4:
            old, self.st = self.st, st4
            WR = self.sb("WR", [128, NCH, 1792], BF16); WRB = Buf()
            W2 = self.sb("W2A2", [128, 512], F32); A2 = W2; G2 = self.sb("G2", [128, 512], BF16)
            BO = self.sb("BO", [128, 128], F32)
            ID2 = self.sb("ID2", [128, 64], F32)
            OMK = self.sb("OMK", [128, 4], F32)
            MSK = self.sb("MSK", [128, 3, 128], BF16)
            ONE64 = self.sb("ONE64", [128, 64], F32)
            cb = Buf()
            self.load_w(WR[:], w_rwkv, WRB)
            S.dma(W2[0:64, :], w2_d, writes=[cb]); S.dma(A2[64:128, :], a2_d, writes=[cb]); S.dma(G2[:], g2_d, writes=[cb], queue="pool")
            S.dma(MSK[:], mk_d, writes=[cb], queue="pool")
            S.op("dve", lambda e: e.memset(BO[:], 0.0), reads=[cb], writes=[cb])
            S.op("dve", lambda e: e.memset(BO[0:64, 0:64], 1.0), reads=[cb], writes=[cb])
            S.op("dve", lambda e: e.memset(BO[64:128, 64:128], 1.0), reads=[cb], writes=[cb])
            S.op("dve", lambda e: e.memset(ONE64[:], 1.0), reads=[cb], writes=[cb])
            S.op("dve", lambda e: e.tensor_copy(ID2[0:64, :], self.ident_f[0:64, 0:64]), reads=[cb, self.constb], writes=[cb])
            S.op("dve", lambda e: e.tensor_copy(ID2[64:128, :], self.ident_f[64:128, 64:128]), reads=[cb, self.constb], writes=[cb])
            ka0 = CN["k_a"][0]
            S.op("dve", lambda e: e.tensor_scalar(OMK[:], self.cols[:, ka0:ka0 + 4], -1.0, 1.0, ALU.mult, ALU.add),
                 reads=[cb, self.constb], writes=[cb])
            P32 = self.sb("P32", [128, 14, 129], F32); P32B = Buf()
            DD = self.sb("DD", [128, 128], F32); DDB = Buf()
            CAR = self.sb("CAR", [128, 14, 1], F32)
            PL = P32[:, :, 1:129]; PLB = P32B
            TW = self.sb("TW", [64, 128], F32); SGg = self.sb("SGg", [128, 128], BF16)
            f32t = lambda n: self.sb(n, [128, 4, 128], F32)
            SIG = f32t("SIG"); CUM = f32t("CUM"); A32 = f32t("A32"); KK = f32t("KK"); SQ = f32t("SQ")
            KKN = f32t("KKN"); NB = f32t("NB"); KM = f32t("KM"); BON = f32t("BON")
            AH = self.sb("AH", [128, 4, 128], BF16); KH = self.sb("KH", [128, 4, 128], BF16)
            BR = self.sb("BR", [128, 4, 2, 128], BF16)
            AT = self.sb("AT", [128, 512], BF16); KTt = self.sb("KTt", [128, 512], BF16); VTOK = self.sb("VTOK", [128, 512], BF16)
            WB = self.sb("WB", [128, 8, 128], BF16); BU = self.sb("BU", [128, 8, 128], BF16)
            bf8 = lambda n: self.sb(n, [128, 8, 128], BF16)
            X0 = bf8("X0"); XT0 = bf8("XT0"); LKT = bf8("LKT"); GRA = bf8("GRA"); GRK = bf8("GRK"); TT = bf8("TT")
            XA1 = [self.sb("XA1_%d" % i, [128, 4, 128], BF16) for i in range(2)]
            XTA1 = [self.sb("XTA1_%d" % i, [128, 4, 128], BF16) for i in range(2)]
            TA1 = [self.sb("TA1_%d" % i, [128, 4, 128], BF16) for i in range(2)]
            RTm = self.sb("RTm", [128, 4, 2, 128], BF16)
            M0Ts = SIG[:].rearrange("p c t -> p (c t)").rearrange("p (a k) -> p a k", a=8)
            N0s = CUM[:].rearrange("p c t -> p (c t)").rearrange("p (a k) -> p a k", a=8)
            PCt = self.sb("PCt", [128, 2, 4], F32)
            H = self.sb("H", [128, 4, 64], F32); Hb = self.sb("Hb", [128, 2, 4, 64], BF16)
            nbb = Buf(); kmb = Buf(); sgb = Buf(); cub = Buf()
            YTOK = NB[:].rearrange("p c t -> p (c t)").rearrange("p (c h v) -> p c h v", c=4, h=2); YTOKB = nbb
            YC = KM[:].rearrange("p c t -> p (c t)").rearrange("p (a v) -> p a v", a=8)
            ST8 = self.sb("ST8", [128, 8], F32); ST8b = self.sb("ST8b", [128, 8], F32)
            YF = SQ
            db = Buf(); hb = Buf(); hbb = Buf(); gb_ = Buf(); chb = Buf(); tkb = Buf(); yb = Buf(); mnb = Buf(); rtb = Buf()
            S.op("pool", lambda e: e.memset(P32[:], 0.0), writes=[P32B])
            S.op("pool", lambda e: e.memset(RTm[:], 0.0), writes=[rtb])
            S.op("pool", lambda e: e.memset(H[:], 0.0), writes=[hb])
            mu0 = CN["mu"][0]; w00 = CN["w0"][0]; a00 = CN["a0"][0]; kk0 = CN["k_k"][0]; rk0 = CN["r_k"][0]
            gg0 = CN["gn_g"][0]; gb0 = CN["gn_b"][0]
            ident = self.ident_f
            c4 = lambda ap: ap.rearrange("p (c t) -> p c t", c=4)
            def emit_proj(i2):
                t0_ = i2 * 128
                tsl_ = slice(t0_, t0_ + 128)
                hreads_ = [self.HNB[c][t0_ // TC] for c in range(NCH)]
                for cg in range(4):
                    c0 = cg * 4
                    n = min(4, 14 - c0)
                    p, pb = self.psum.get()

                    def mm(e):
                        for cc in range(n):
                            for k in range(NCH):
                                ins = e.matmul(p[:, cc * 128:(cc + 1) * 128], WR[:, k, (c0 + cc) * 128:(c0 + cc + 1) * 128],
                                               self.HN[:, k, tsl_], start=(k == 0), stop=(k == NCH - 1))
                        return ins
                    S.op("pe", mm, reads=hreads_ + [WRB], writes=[pb])
                    S.op("act", lambda e: e.copy(P32[:, c0:c0 + n, 1:129], p[:, 0:n * 128].rearrange("p (c t) -> p c t", c=n)),
                         reads=[pb], writes=[P32B])

            def lerp_list(i2):
                ops = []
                ops.append(lambda: S.op("pool", lambda e: e.tensor_copy(CAR[:], P32[:, :, 128:129]), reads=[P32B], writes=[DDB]))
                for c in range(14):
                    def one(c=c):
                        S.op("pool", lambda e: e.tensor_tensor(DD[:], P32[:, c, 0:128], P32[:, c, 1:129], ALU.subtract), reads=[P32B, DDB], writes=[DDB])
                        S.op("pool", lambda e: e.tensor_tensor(DD[:], DD[:], self.cols[:, mu0 + c:mu0 + c + 1].to_broadcast([128, 128]), ALU.mult),
                             reads=[DDB, self.constb], writes=[DDB])
                        S.op("pool", lambda e: e.tensor_tensor(P32[:, c, 1:129], P32[:, c, 1:129], DD[:], ALU.add), reads=[DDB, P32B], writes=[P32B])
                    ops.append(one)
                ops.append(lambda: S.op("pool", lambda e: e.tensor_copy(P32[:, :, 0:1], CAR[:]), reads=[P32B, DDB], writes=[P32B]))
                return ops

            pending = []
            for i in range(self._rk_tiles):
                t0 = i * 128
                tcix = t0 // TC
                tsl = slice(t0, t0 + 128)
                hreads = [self.HNB[c][tcix] for c in range(NCH)]
                if i == 0:
                    emit_proj(0)
                    for fn_ in lerp_list(0):
                        fn_()
                for fn_ in pending:
                    fn_()
                pending = []
                S.op("act", lambda e: e.activation(TW[:], PL[0:64, 12, :], AF.Tanh), reads=[PLB], writes=[db])
                S.op("act", lambda e: e.activation(SGg[:], PL[:, 13, :], AF.Sigmoid), reads=[PLB], writes=[db])
                pz, pzb = self.psum.get(); pa, pab = self.psum.get()

                def mmz(e):
                    for fc in range(4):
                        ins = e.matmul(pz[:, fc * 128:(fc + 1) * 128], W2[0:64, fc * 128:(fc + 1) * 128], TW[:], start=True, stop=True)
                    return ins

                def mma(e):
                    for fc in range(4):
                        ins = e.matmul(pa[:, fc * 128:(fc + 1) * 128], A2[64:128, fc * 128:(fc + 1) * 128], PL[64:128, 12, :], start=True, stop=True)
                    return ins
                S.op("pe", mmz, reads=[db, cb], writes=[pzb])
                S.op("pe", mma, reads=[PLB, cb], writes=[pab])
                for fc in range(4):
                    S.op("act", lambda e: e.activation(SIG[:, fc, :], pz[:, fc * 128:(fc + 1) * 128], AF.Sigmoid,
                                                       bias=self.cols[:, w00 + fc:w00 + fc + 1]), reads=[pzb, self.constb], writes=[db, sgb])
                    S.op("act", lambda e: e.activation(A32[:, fc, :], pa[:, fc * 128:(fc + 1) * 128], AF.Sigmoid,
                                                       bias=self.cols[:, a00 + fc:a00 + fc + 1]), reads=[pab, self.constb], writes=[db])
                bc4 = lambda c0_: self.cols[:, c0_:c0_ + 4].unsqueeze(2).to_broadcast([128, 4, 128])
                S.op("dve", lambda e: e.tensor_tensor(KK[:], PL[:, 4:8, :], bc4(kk0), ALU.mult), reads=[PLB, self.constb], writes=[db])
                S.op("dve", lambda e: e.tensor_tensor(SQ[:], KK[:], KK[:], ALU.mult), reads=[db], writes=[db])
                pss, pssb = self.psum.get()
                S.op("pe", lambda e: e.matmul(pss[:], BO[:], SQ[:].rearrange("p c t -> p (c t)"), start=True, stop=True), reads=[db, cb], writes=[pssb])
                S.op("act", lambda e: e.activation(SQ[:], c4(pss[:]), AF.Sqrt), reads=[pssb, db], writes=[db])
                S.op("dve", lambda e: e.tensor_scalar(SQ[:], SQ[:], 1e-12, None, ALU.max), reads=[db], writes=[db])
                S.op("dve", lambda e: e.reciprocal(SQ[:], SQ[:]), reads=[db], writes=[db])
                S.op("dve", lambda e: e.tensor_tensor(KKN[:], KK[:], SQ[:], ALU.mult), reads=[db], writes=[db])
                S.op("dve", lambda e: e.tensor_tensor(NB[:], KKN[:], A32[:], ALU.mult), reads=[db], writes=[db, nbb])
                S.op("pool", lambda e: e.tensor_tensor(KK[:], A32[:], bc4(ka0), ALU.mult), reads=[db, self.constb], writes=[db])
                S.op("pool", lambda e: e.tensor_tensor(KK[:], KK[:], OMK[:].unsqueeze(2).to_broadcast([128, 4, 128]), ALU.add), reads=[db, cb], writes=[db])
                S.op("dve", lambda e: e.tensor_tensor(KM[:], PL[:, 4:8, :], KK[:], ALU.mult), reads=[db, PLB], writes=[db, kmb])
                S.op("dve", lambda e: e.tensor_tensor(SQ[:], PL[:, 0:4, :], KM[:], ALU.mult), reads=[db, PLB, kmb], writes=[db])
                S.op("pool", lambda e: e.tensor_tensor(SQ[:], SQ[:], bc4(rk0), ALU.mult), reads=[db, self.constb], writes=[db])
                pbn, pbnb = self.psum.get()
                S.op("pe", lambda e: e.matmul(pbn[:], BO[:], SQ[:].rearrange("p c t -> p (c t)"), start=True, stop=True), reads=[db, cb], writes=[pbnb])
                S.op("dve", lambda e: e.tensor_tensor(BON[:], c4(pbn[:]), PL[:, 8:12, :], ALU.mult), reads=[pbnb, PLB], writes=[db])
                for fc in range(4):
                    for c2 in range(2):
                        cs = slice(c2 * 64, (c2 + 1) * 64)
                        S.op("dve", lambda e: e.tensor_tensor_scan(CUM[:, fc, cs], ONE64[:], SIG[:, fc, cs], 0.0, ALU.mult, ALU.add),
                             reads=[db, cb, sgb], writes=[db, cub])
                S.op("pool", lambda e: e.tensor_tensor(SQ[:], CUM[:], SIG[:], ALU.subtract), reads=[db, sgb, cub], writes=[db])
                S.op("act", lambda e: e.activation(A32[:], CUM[:], AF.Exp, scale=CDEC), reads=[db, cub], writes=[db])
                S.op("act", lambda e: e.activation(CUM[:], CUM[:], AF.Exp, scale=-CDEC), reads=[db], writes=[db, cub])
                S.op("act", lambda e: e.activation(SQ[:], SQ[:], AF.Exp, scale=-CDEC), reads=[db], writes=[db])
                S.op("dve", lambda e: e.tensor_copy(PCt[:, 0, :], CUM[:, :, 63]), reads=[db, chb, cub], writes=[chb])
                S.op("dve", lambda e: e.tensor_copy(PCt[:, 1, :], CUM[:, :, 127]), reads=[db, chb, cub], writes=[chb])
                S.op("dve", lambda e: e.tensor_tensor(NB[:], NB[:], A32[:], ALU.mult), reads=[db], writes=[db, nbb])
                S.op("dve", lambda e: e.tensor_tensor(KM[:], KM[:], A32[:], ALU.mult), reads=[db], writes=[db, kmb])
                S.op("dve", lambda e: e.tensor_tensor(KKN[:], KKN[:], SQ[:], ALU.mult), reads=[db], writes=[db])
                S.op("dve", lambda e: e.tensor_tensor(KK[:], PL[:, 0:4, :], CUM[:], ALU.mult), reads=[db, PLB, cub], writes=[db])
                S.op("act", lambda e: e.copy(AH[:], NB[:]), reads=[db, gb_, nbb], writes=[gb_])
                S.op("act", lambda e: e.copy(KH[:], KM[:]), reads=[db, gb_, kmb], writes=[gb_])
                S.op("pool", lambda e: e.tensor_copy(BR[:, :, 0, :], KKN[:]), reads=[db, gb_], writes=[gb_])
                S.op("pool", lambda e: e.tensor_copy(BR[:, :, 1, :], KK[:]), reads=[db, gb_], writes=[gb_])
                if "dumpah" in self.debug and i == 0:
                    for nm, tl in (("ah", AH), ("kh", KH), ("br", BR)):
                        o_ = self.dout("dbg_" + nm, [128, tl[:].rearrange("p ... -> p (...)").shape[1] if False else (512 if nm != "br" else 1024)])
                        S.dma(o_, tl[:].rearrange("p c t -> p (c t)") if nm != "br" else tl[:].rearrange("p c a t -> p (c a t)"), reads=[gb_], queue="pool")
                    for nm, tl in (("nb", NB), ("km", KM), ("kkn", KKN), ("en", A32), ("ep", CUM)):
                        o_ = self.dout("dbg_" + nm, [128, 512])
                        S.dma(o_, tl[:].rearrange("p c t -> p (c t)"), reads=[db, nbb, kmb, cub])
                for src, dst_fn in ((NB, None), (KM, None), (KKN, None), (None, None)):
                    pass
                tr_jobs = [(lambda fc: NB[:, fc, :], "AT"), (lambda fc: KM[:, fc, :], "KT"),
                           (lambda fc: KKN[:, fc, :], "BT"), (lambda fc: PL[:, 8 + fc, :], "VT")]
                for srcf, kind in tr_jobs:
                    ptr, ptrb = self.psum.get()

                    def mmt(e):
                        for fc in range(4):
                            ins = e.transpose(ptr[:, fc * 128:(fc + 1) * 128], srcf(fc), ident[:])
                        return ins
                    S.op("pe", mmt, reads=[db, PLB, self.constb, nbb, kmb], writes=[ptrb])
                    if kind == "AT":
                        S.op("act", lambda e: e.copy(AT[:], ptr[:]), reads=[ptrb, tkb], writes=[tkb])
                    elif kind == "KT":
                        S.op("dve", lambda e: e.tensor_copy(KTt[:], ptr[:]), reads=[ptrb, tkb], writes=[tkb])
                    elif kind == "BT":
                        S.op("act", lambda e: e.activation(WB[:, :, 0:64], ptr[:].rearrange("p (h k) -> p h k", h=8), AF.Copy, scale=-1.0),
                             reads=[ptrb, tkb], writes=[tkb])
                    else:
                        S.op("dve", lambda e: e.tensor_copy(VTOK[:], ptr[:]), reads=[ptrb, tkb], writes=[tkb])
                if self._rk_stage <= 0:
                    continue
                for fc in range(4):
                    ka_, ab0, ab1 = self.psum.get_pair_idx()
                    kb_, bb0, bb1 = self.psum.get_pair_idx()
                    PA = self.PS[:, ka_:ka_ + 2, :]; PB = self.PS[:, kb_:kb_ + 2, :]

                    def mmg(e):
                        for h2 in range(2):
                            rs = slice(h2 * 64, (h2 + 1) * 64)
                            brr = BR[rs, fc, :, :].rearrange("p a t -> p (a t)")
                            e.matmul(PA[:, h2, 0:256], AH[rs, fc, :], brr, start=True, stop=True)
                            e.matmul(PA[:, h2, 256:512], KH[rs, fc, :], brr, start=True, stop=True)
                            ins = e.matmul(PB[:, h2, 0:128], BR[rs, fc, 0, :], AH[rs, fc, :], start=True, stop=True)
                        return ins
                    S.op("pe", mmg, reads=[gb_], writes=[ab0, ab1, bb0, bb1])
                    hs = slice(2 * fc, 2 * fc + 2)
                    PAv = PA.rearrange("p h (q b t) -> p h q b t", q=2, b=2)
                    mk = lambda j: MSK[:, j, :].unsqueeze(1).to_broadcast([128, 2, 128])
                    S.op("dve", lambda e: e.tensor_tensor(X0[:, hs, :], PAv[:, :, 0, 0, :], mk(0), ALU.mult), reads=[ab0, ab1, cb, mnb], writes=[mnb])
                    S.op("dve", lambda e: e.tensor_tensor(GRA[:, hs, :], PAv[:, :, 0, 1, :], mk(2), ALU.mult), reads=[ab0, ab1, cb, mnb], writes=[mnb])
                    S.op("dve", lambda e: e.tensor_tensor(LKT[:, hs, :], PAv[:, :, 1, 0, :], mk(0), ALU.mult), reads=[ab0, ab1, cb, mnb], writes=[mnb])
                    S.op("dve", lambda e: e.tensor_tensor(GRK[:, hs, :], PAv[:, :, 1, 1, :], mk(2), ALU.mult), reads=[ab0, ab1, cb, mnb], writes=[mnb])
                    S.op("dve", lambda e: e.tensor_tensor(XT0[:, hs, :], PB[:, :, 0:128], mk(1), ALU.mult), reads=[bb0, bb1, cb, mnb], writes=[mnb])
                if self._rk_stage <= 1:
                    continue
                if i + 1 < self._rk_tiles:
                    emit_proj(i + 1)
                    pending = lerp_list(i + 1)
                hst = []
                for half in range(2):
                    h0 = half * 4
                    st_ = dict(xb=Buf(), xtb=Buf(), tb=Buf(),
                               xbufs=[X0[:, h0:h0 + 4, :], XA1[half][:]], xtbufs=[XT0[:, h0:h0 + 4, :], XTA1[half][:]],
                               tbufs=[TA1[half][:], TT[:, h0:h0 + 4, :]])
                    hst.append(st_)
                    S.op("pool", lambda e: e.tensor_tensor(st_["tbufs"][0], st_["xbufs"][0], ident[:].unsqueeze(1).to_broadcast([128, 4, 128]), ALU.add),
                         reads=[mnb, self.constb, st_["tb"]], writes=[st_["tb"]])
                for lv in range(1, 6):
                    for half in range(2):
                        st_ = hst[half]
                        xb_, xtb_, tb_ = st_["xb"], st_["xtb"], st_["tb"]
                        Xp, XTp, Tp = st_["xbufs"][(lv - 1) % 2], st_["xtbufs"][(lv - 1) % 2], st_["tbufs"][(lv - 1) % 2]
                        Xn, XTn, Tn = st_["xbufs"][lv % 2], st_["xtbufs"][lv % 2], st_["tbufs"][lv % 2]
                        pxt, pxtb = self.psum.get()

                        def mmxt(e):
                            for j in range(4):
                                ins = e.matmul(pxt[:, j * 128:(j + 1) * 128], Xp[:, j, :], XTp[:, j, :], start=True, stop=True)
                            return ins
                        S.op("pe", mmxt, reads=[mnb, xb_, xtb_], writes=[pxtb])
                        if lv < 5:
                            px, pxb = self.psum.get()

                            def mmx(e):
                                for j in range(4):
                                    ins = e.matmul(px[:, j * 128:(j + 1) * 128], XTp[:, j, :], Xp[:, j, :], start=True, stop=True)
                                return ins
                            S.op("pe", mmx, reads=[mnb, xb_, xtb_], writes=[pxb])
                        S.op("act", lambda e: e.copy(XTn, c4(pxt[:])), reads=[pxtb, xtb_, mnb], writes=[xtb_])
                        if lv < 5:
                            S.op("act", lambda e: e.copy(Xn, c4(px[:])), reads=[pxb, xb_, mnb], writes=[xb_])
                        ptt, pttb = self.psum.get()

                        def mmtt(e):
                            for j in range(4):
                                ins = e.matmul(ptt[:, j * 128:(j + 1) * 128], XTn[:, j, :], Tp[:, j, :], start=True, stop=True)
                            return ins
                        S.op("pe", mmtt, reads=[xtb_, tb_], writes=[pttb])
                        S.op("dve", lambda e: e.tensor_tensor(Tn, c4(ptt[:]), Tp, ALU.add), reads=[pttb, tb_, mnb], writes=[tb_] + ([mnb] if lv == 5 else []))
                        for _ in range(2):
                            if pending:
                                pending.pop(0)()
                if self._rk_stage <= 2:
                    continue
                plk, plkb = self.psum.get()

                def mmlk(e):
                    for h in range(8):
                        ins = e.matmul(plk[:, h * 64:(h + 1) * 64], LKT[:, h, :], VTOK[:, h * 64:(h + 1) * 64], start=True, stop=True)
                    return ins
                S.op("pe", mmlk, reads=[mnb, tkb], writes=[plkb])
                S.op("act", lambda e: e.copy(WB[:, :, 64:128], plk[:].rearrange("p (h v) -> p h v", h=8)), reads=[plkb, tkb], writes=[tkb])
                for half in range(2):
                    pbu, pbub = self.psum.get()

                    def mmbu(e):
                        for j in range(4):
                            h = half * 4 + j
                            ins = e.matmul(pbu[:, j * 128:(j + 1) * 128], TT[:, h, :], WB[:, h, :], start=True, stop=True)
                        return ins
                    S.op("pe", mmbu, reads=[mnb, tkb], writes=[pbub])
                    S.op("act", lambda e: e.copy(BU[:, half * 4:half * 4 + 4, :], c4(pbu[:])), reads=[pbub, chb], writes=[chb])
                if self._rk_stage <= 3:
                    continue
                prt, prtb = self.psum.get()
                km_, mb0, mb1 = self.psum.get_pair_idx()
                PM = self.PS[:, km_:km_ + 2, :]

                def mmrt(e):
                    for h in range(8):
                        rs = slice((h % 2) * 64, (h % 2) * 64 + 64)
                        fc = h // 2
                        ins = e.matmul(prt[rs, fc * 128:(fc + 1) * 128], BU[:, h, 0:64], GRA[:, h, :], start=True, stop=True)
                    return ins

                def mmmn(e):
                    for c2 in range(2):
                        cr = slice(c2 * 64, (c2 + 1) * 64)
                        for h in range(8):
                            rs = slice((h % 2) * 64, (h % 2) * 64 + 64)
                            fc = h // 2
                            o = fc * 64
                            e.matmul(PM[rs, c2, o:o + 64], BU[cr, h, 0:64], AT[cr, h * 64:(h + 1) * 64], start=True, stop=True)
                            e.matmul(PM[rs, c2, 256 + o:256 + o + 64], AT[cr, h * 64:(h + 1) * 64], BU[cr, h, 64:128], start=True, stop=False)
                            ins = e.matmul(PM[rs, c2, 256 + o:256 + o + 64], KTt[cr, h * 64:(h + 1) * 64], VTOK[cr, h * 64:(h + 1) * 64], start=False, stop=True)
                    return ins
                S.op("pe", mmrt, reads=[chb, mnb], writes=[prtb])
                S.op("pe", mmmn, reads=[chb, tkb], writes=[mb0, mb1])
                prv = c4(prt[:])
                S.op("dve", lambda e: e.tensor_tensor(RTm[:, :, 0, 0:64], prv[:, :, 0:64], KK[:, :, 0:64], ALU.add), reads=[prtb, db, rtb], writes=[rtb])
                S.op("dve", lambda e: e.tensor_tensor(RTm[:, :, 1, 64:128], prv[:, :, 64:128], KK[:, :, 64:128], ALU.add), reads=[prtb, db, rtb], writes=[rtb])
                M0v = M0Ts.rearrange("p (a c) k -> p a c k", a=2)
                N0v = N0s.rearrange("p (a c) k -> p a c k", a=2)
                S.op("dve", lambda e: e.tensor_tensor(M0v, PM[:, :, 0:256].rearrange("p a (c k) -> p a c k", c=4),
                                                      ID2[:].unsqueeze(1).unsqueeze(1).to_broadcast([128, 2, 4, 64]), ALU.add),
                     reads=[mb0, mb1, cb, chb], writes=[chb, sgb])
                S.op("act", lambda e: e.copy(N0v, PM[:, :, 256:512].rearrange("p a (c k) -> p a c k", c=4)), reads=[mb0, mb1, chb], writes=[chb, cub])
                if self._rk_stage <= 4:
                    continue
                for c2 in range(2):
                    S.op("act", lambda e: e.copy(Hb[:, c2, :, :], H[:]), reads=[hb, hbb], writes=[hbb])
                    phe, pheb = self.psum.get(); pho, phob = self.psum.get()

                    def mmh(e):
                        for par, bank in ((0, phe), (1, pho)):
                            rs = slice(par * 64, par * 64 + 64)
                            for fc in range(4):
                                ins = e.matmul(bank[rs, fc * 64:(fc + 1) * 64], M0Ts[rs, c2 * 4 + fc, :], H[rs, fc, :], start=True, stop=True)
                        return ins
                    S.op("pe", mmh, reads=[chb, hb, sgb], writes=[pheb, phob])
                    S.op("dve", lambda e: e.tensor_tensor(H[0:64], phe[0:64, 0:256].rearrange("p (c v) -> p c v", c=4), N0s[0:64, c2 * 4:c2 * 4 + 4, :], ALU.add),
                         reads=[pheb, chb, cub, hb], writes=[hb])
                    S.op("dve", lambda e: e.tensor_tensor(H[64:128], pho[64:128, 0:256].rearrange("p (c v) -> p c v", c=4), N0s[64:128, c2 * 4:c2 * 4 + 4, :], ALU.add),
                         reads=[phob, chb, cub, hb], writes=[hb])
                    S.op("dve", lambda e: e.tensor_tensor(H[:], H[:], PCt[:, c2, :].unsqueeze(2).to_broadcast([128, 4, 64]), ALU.mult),
                         reads=[chb, hb], writes=[hb])
                if self._rk_stage <= 5:
                    continue
                ky_, yb0, yb1 = self.psum.get_pair_idx()
                PY = self.PS[:, ky_:ky_ + 2, :]

                def mmy(e):
                    for par in range(2):
                        rs = slice(par * 64, par * 64 + 64)
                        for fc in range(4):
                            h = 2 * fc + par
                            o = PY[:, par, fc * 64:(fc + 1) * 64]
                            e.matmul(o, GRA[:, h, :], BU[:, h, 64:128], start=True, stop=False)
                            e.matmul(o, GRK[:, h, :], VTOK[:, h * 64:(h + 1) * 64], start=False, stop=False)
                            e.matmul(o, RTm[rs, fc, 0, :], Hb[rs, 0, fc, :], start=False, stop=False)
                            ins = e.matmul(o, RTm[rs, fc, 1, :], Hb[rs, 1, fc, :], start=False, stop=True)
                    return ins
                S.op("pe", mmy, reads=[mnb, chb, tkb, rtb, hbb], writes=[yb0, yb1])
                S.op("act", lambda e: e.copy(YTOK.rearrange("t c h v -> t h c v"), PY[:, :, 0:256].rearrange("t h (c v) -> t h c v", c=4)),
                     reads=[yb0, yb1, YTOKB], writes=[YTOKB])
                if self._rk_stage <= 6:
                    continue
                YT8 = YTOK.rearrange("t c h v -> t (c h) v")
                S.op("dve", lambda e: e.tensor_reduce(ST8[:], YT8, AX.X, ALU.add), reads=[YTOKB], writes=[yb])
                S.op("dve", lambda e: e.tensor_scalar(ST8[:], ST8[:], 1.0 / 64, None, ALU.mult), reads=[yb], writes=[yb])
                S.op("dve", lambda e: e.tensor_tensor(YC, YT8, ST8[:].unsqueeze(2).to_broadcast([128, 8, 64]), ALU.subtract),
                     reads=[YTOKB, yb], writes=[yb, kmb])
                S.op("pool", lambda e: e.tensor_tensor(YT8, YC, YC, ALU.mult), reads=[yb, YTOKB, kmb], writes=[YTOKB])
                S.op("dve", lambda e: e.tensor_reduce(ST8b[:], YT8, AX.X, ALU.add), reads=[YTOKB], writes=[yb])
                S.op("act", lambda e: e.activation(ST8b[:], ST8b[:], AF.Sqrt, bias=self.gneps_t[:], scale=1.0 / 64), reads=[yb, self.constb], writes=[yb])
                S.op("dve", lambda e: e.reciprocal(ST8b[:], ST8b[:]), reads=[yb], writes=[yb])
                S.op("dve", lambda e: e.tensor_tensor(YC, YC, ST8b[:].unsqueeze(2).to_broadcast([128, 8, 64]), ALU.mult), reads=[yb], writes=[yb, kmb])
                pyt, pytb = self.psum.get(); pg, pgb = self.psum.get()

                def mmt2(e):
                    for fc in range(4):
                        ins = e.transpose(pyt[:, fc * 128:(fc + 1) * 128], YC[:, 2 * fc:2 * fc + 2, :].rearrange("t a v -> t (a v)"), ident[:])
                    for fc in range(4):
                        ins = e.matmul(pg[:, fc * 128:(fc + 1) * 128], G2[:, fc * 128:(fc + 1) * 128], SGg[:], start=True, stop=True)
                    return ins
                S.op("pe", mmt2, reads=[yb, self.constb, db, cb, kmb], writes=[pytb, pgb])
                for fc in range(4):
                    S.op("dve", lambda e: e.tensor_scalar(YF[:, fc, :], pyt[:, fc * 128:(fc + 1) * 128], self.cols[:, gg0 + fc:gg0 + fc + 1],
                                                          self.cols[:, gb0 + fc:gb0 + fc + 1], ALU.mult, ALU.add),
                         reads=[pytb, db, self.constb], writes=[db])
                S.op("pool", lambda e: e.tensor_tensor(YF[:], YF[:], BON[:], ALU.add), reads=[db], writes=[db])
                S.op("dve", lambda e: e.tensor_tensor(self.Y[0][:, :, tsl], YF[:], c4(pg[:]), ALU.mult),
                     reads=[db, pgb], writes=[self.YB[0][tcix]])
            S.full_barrier()
            self.st = old

    def nsa_branch(self, d):
        S = self.S
        NT = S_LEN // 128
        with ExitStack() as st4:
            old, self.st = self.st, st4
            cb = Buf()
            KT = self.sb("KT", [128, 2, S_LEN], BF16); KTB = Buf()
            VT = self.sb("VT", [128, NT, 256], BF16); VTB = Buf()
            KC = self.sb("KC", [128, 127], BF16); VC = self.sb("VC", [128, 128], BF16); kcb = Buf()
            BM = self.sb("BM", [128, 3, 2, 512], BF16)
            BVC = self.sb("BVC", [32, 2, 512], BF16)
            stA = ExitStack(); self.st = stA
            G1 = self.sb("G1", [128, 2, 512], F32); G2_ = self.sb("G2b", [128, 2, 512], F32); MK = self.sb("MK", [128, 128], F32)
            gb = Buf()
            S.dma(G2_[:], d["t31"], writes=[gb])
            for kind in range(3):
                S.dma(G1[:], d["bmg"][kind], reads=[gb], writes=[gb])
                S.dma(MK[:], d["msk"][kind], reads=[gb], writes=[gb])
                S.op("dve", lambda e: e.tensor_tensor(G1[:], G1[:], G2_[:], ALU.subtract), reads=[gb], writes=[gb])
                S.op("dve", lambda e: e.tensor_tensor(BM[:, kind, :, :].rearrange("p g (j q) -> p (g j) q", j=4),
                                                      G1[:].rearrange("p g (j q) -> p (g j) q", j=4),
                                                      MK[:].unsqueeze(1).to_broadcast([128, 8, 128]), ALU.add), reads=[gb], writes=[cb, gb])
            S.dma(G1[0:32, :, :], d["bvcg"], reads=[gb], writes=[gb])
            S.dma(MK[0:32, :], d["mskc"], reads=[gb], writes=[gb])
            S.op("dve", lambda e: e.tensor_tensor(G1[0:32], G1[0:32], G2_[0:32], ALU.subtract), reads=[gb], writes=[gb])
            S.op("dve", lambda e: e.tensor_tensor(BVC[:].rearrange("p g (j q) -> p (g j) q", j=4),
                                                  G1[0:32].rearrange("p g (j q) -> p (g j) q", j=4),
                                                  MK[0:32, :].unsqueeze(1).to_broadcast([32, 8, 128]), ALU.add), reads=[gb], writes=[cb, gb])
            S.full_barrier()
            stA.close()
            stB = ExitStack(); self.st = stB
            KCMP = self.sb("KCMP", [128, S_LEN], BF16); VCT = self.sb("VCT", [128, S_LEN], BF16)
            stB1 = ExitStack(); self.st = stB1
            WKV = self.sb("WKV", [128, NCH, 768], BF16); wkvb = Buf()
            self.load_w(WKV[:], d["w_kvn"], wkvb)
            for tc in range(NTC):
                ts = slice(tc * TC, (tc + 1) * TC)
                hreads = [self.HNB[c][tc] for c in range(NCH)]
                for dst, col in ((KCMP[:, ts], 0), (VCT[:, ts], 128), (KT[:, 0, ts], 256), (KT[:, 1, ts], 512)):
                    p, pb = self.psum.get()

                    def mm(e):
                        for k in range(NCH):
                            ins = e.matmul(p[:], WKV[:, k, col:col + 128], self.HN[:, k, ts], start=(k == 0), stop=(k == NCH - 1))
                        return ins
                    S.op("pe", mm, reads=hreads + [wkvb], writes=[pb])
                    S.op("act", lambda e: e.copy(dst, p[:]), reads=[pb], writes=[KTB])
                for tl in range(4):
                    tile = tc * 4 + tl
                    tq = slice(tile * 128, (tile + 1) * 128)
                    p, pb = self.psum.get()

                    def mm(e):
                        for k in range(NCH):
                            e.matmul(p[:, 0:128], self.HN[:, k, tq], WKV[:, k, 384:512], start=(k == 0), stop=(k == NCH - 1))
                        for k in range(NCH):
                            ins = e.matmul(p[:, 128:256], self.HN[:, k, tq], WKV[:, k, 640:768], start=(k == 0), stop=(k == NCH - 1))
                        return ins
                    S.op("pe", mm, reads=hreads + [wkvb], writes=[pb])
                    S.op("dve", lambda e: e.tensor_copy(VT[:, tile, :], p[:, 0:256]), reads=[pb], writes=[VTB])
            S.full_barrier()
            stB1.close()
            stB2 = ExitStack(); self.st = stB2
            W1 = self.sb("W1", [128, 32, 256], BF16); PET = self.sb("PET", [128, 32], BF16)
            W2D = self.sb("W2D", [128, 2, 128], BF16); HID = self.sb("HID", [128, 2, 127], BF16)
            ZZ = self.sb("ZZ", [128, 127], F32); Z2 = self.sb("Z2", [128, 127], F32); BC = self.sb("BCc", [128, 1], F32)
            wb = Buf(); zb = Buf(); hb_ = Buf()
            for kv in range(2):
                w1d = d["cmp_w1"][kv].rearrange("(l dd) m -> dd l m", dd=64)
                S.dma(W1[0:64], w1d, writes=[wb], queue="pool"); S.dma(W1[64:128], w1d, writes=[wb], queue="pool")
                S.dma(PET[0:64], d["cmp_peT"][kv], writes=[wb], queue="pool"); S.dma(PET[64:128], d["cmp_peT"][kv], writes=[wb], queue="pool")
                w2v = d["cmp_w2"][kv].rearrange("(c p) n -> p c n", p=128)
                S.dma(W2D[:, :, 0:64], w2v, writes=[wb], queue="pool"); S.dma(W2D[:, :, 64:128], w2v, writes=[wb], queue="pool")
                SRC = KCMP if kv == 0 else VCT
                for g in range(2):
                    gs = slice(g * 64, (g + 1) * 64)
                    for mc in range(2):
                        ph, phb = self.psum.get(); pbias, pbb = self.psum.get()

                        def mm(e):
                            for l in range(32):
                                ins = e.matmul(ph[:, 0:127], W1[gs, l, mc * 128:(mc + 1) * 128], SRC[gs, l:l + 16 * 126 + 1:16],
                                               start=(l == 0), stop=(l == 31))
                            return ins

                        def mmb(e):
                            for l in range(32):
                                ins = e.matmul(pbias[:, 0:1], W1[gs, l, mc * 128:(mc + 1) * 128], PET[gs, l:l + 1], start=(l == 0), stop=(l == 31))
                            return ins
                        S.op("pe", mm, reads=[wb, KTB], writes=[phb])
                        S.op("pe", mmb, reads=[wb], writes=[pbb])
                        S.op("act", lambda e: e.copy(BC[:], pbias[:, 0:1]), reads=[pbb, zb], writes=[zb])
                        S.op("dve", lambda e: e.tensor_scalar(ZZ[:], ph[:, 0:127], BC[:, 0:1], None, ALU.add), reads=[phb, zb], writes=[zb])
                        S.op("dve", lambda e: e.tensor_tensor(Z2[:], ZZ[:], ZZ[:], ALU.mult), reads=[zb], writes=[zb])
                        S.op("dve", lambda e: e.tensor_scalar(Z2[:], Z2[:], 0.044715, 1.0, ALU.mult, ALU.add), reads=[zb], writes=[zb])
                        S.op("dve", lambda e: e.tensor_tensor(Z2[:], Z2[:], ZZ[:], ALU.mult), reads=[zb], writes=[zb])
                        S.op("act", lambda e: e.activation(Z2[:], Z2[:], AF.Sigmoid, scale=1.5957691216057308), reads=[zb], writes=[zb])
                        S.op("dve", lambda e: e.tensor_tensor(HID[:, mc, :], ZZ[:], Z2[:], ALU.mult), reads=[zb, hb_], writes=[hb_])
                    po, pob = self.psum.get()
                    if kv == 0:
                        def mm2(e):
                            for mc in range(2):
                                ins = e.matmul(po[:, 0:127], W2D[:, mc, :], HID[:, mc, :], start=(mc == 0), stop=(mc == 1))
                            return ins
                        S.op("pe", mm2, reads=[hb_, wb], writes=[pob])
                        S.op("act", lambda e: e.copy(KC[gs, :], po[gs, 0:127]), reads=[pob], writes=[kcb])
                    else:
                        def mm2(e):
                            for mc in range(2):
                                ins = e.matmul(po[0:127, 0:64], HID[:, mc, :], W2D[:, mc, 0:64], start=(mc == 0), stop=(mc == 1))
                            return ins
                        S.op("pe", mm2, reads=[hb_, wb], writes=[pob])
                        S.op("act", lambda e: e.copy(VC[0:127, gs], po[0:127, 0:64]), reads=[pob], writes=[kcb])
            S.full_barrier()
            stB2.close(); stB.close(); self.st = st4
            if "kcvc" in self.debug:
                okc = self.dout("dbg_kc", [128, 127]); ovc = self.dout("dbg_vc", [127, 128])
                S.dma(okc, KC[:], reads=[kcb], queue="pool"); S.dma(ovc, VC[0:127, :], reads=[kcb], queue="pool")
            WQ = self.sb("WQN", [128, NCH, 512], BF16); WGN = self.sb("WGN", [128, NCH, 24], BF16)
            SHCF = self.sb("SHCF", [32, 247], BF16); EF = self.sb("EF", [32, S_LEN], BF16)
            OV = self.sb("OV", [128, 32], BF16); AB = self.sb("ABF", [128, 2, 64], F32)
            SELG = self.sb("SELG", [24, 12, 128], BF16); IDb = self.sb("IDb", [128, 128], BF16)
            self.load_w(WQ[:], d["w_qn"], cb)
            self.load_w(WGN[:], d["w_gn"], cb)
            S.dma(SHCF[:], d["shcf"], writes=[cb], queue="pool"); S.dma(EF[:], d["efull"], writes=[cb], queue="pool")
            S.dma(OV[0:127, :], d["ov"], writes=[cb], queue="pool"); S.dma(AB[:], d["abf"], writes=[cb])
            S.dma(SELG[:], d["selg"], writes=[cb], queue="pool")
            S.op("dve", lambda e: e.tensor_copy(IDb[:], self.ident_f[:]), reads=[self.constb, cb], writes=[cb])
            QS = self.sb("QS", [128, 4, 128], BF16); qsb = Buf()
            GS = self.sb("GS", [24, 128], BF16); gsb = Buf()
            pt_ring = Ring([self.sb("PT%d" % i, [128, 512], BF16) for i in range(4)])
            RR = self.sb("RR", [128, 512], F32); rrb = Buf()
            RRc = self.sb("RRc", [128, 512], F32); rcb = Buf()
            PTc = [self.sb("PTc%d" % g, [128, 512], BF16) for g in range(2)]; ptcb = [Buf(), Buf()]
            YA = self.sb("YA", [128, 512], F32); yab = Buf()
            PN = self.sb("PN", [128, 512], BF16); pnb = Buf()
            IMP = self.sb("IMP", [128, 32], F32); IM2 = self.sb("IM2", [128, 32], F32); MX = self.sb("MX8", [128, 8], F32); ib = Buf()
            NMT = [self.sb("NMT%d" % g, [32, 4, 128], BF16) for g in range(2)]; nmb = [Buf(), Buf()]
            st_ring = Ring(self.banks[0:3], self.bankb[0:3])
            OD = [(self.banks[3], self.bankb[3], self.banks[4], self.bankb[4]),
                  (self.banks[5], self.bankb[5], self.banks[6], self.bankb[6])]
            ms_ring = Ring(self.banks[7:8], self.bankb[7:8])
            LOOK = 2
            for i in range(NT):
                tq = slice(i * 128, (i + 1) * 128)
                tcix = i // 4
                hreads = [self.HNB[c][tcix] for c in range(NCH)]
                p, pb = ms_ring.get()

                def mmq(e):
                    for j in range(4):
                        for k in range(NCH):
                            ins = e.matmul(p[:, j * 128:(j + 1) * 128], WQ[:, k, j * 128:(j + 1) * 128], self.HN[:, k, tq], start=(k == 0), stop=(k == NCH - 1))
                    return ins
                S.op("pe", mmq, reads=hreads + [cb], writes=[pb])
                S.op("act", lambda e: e.activation(QS[:].rearrange("p j q -> p (j q)"), p[:], AF.Copy, scale=0.125), reads=[pb], writes=[qsb])
                p2, pb2 = ms_ring.get()

                def mmg(e):
                    for k in range(NCH):
                        ins = e.matmul(p2[0:24, 0:128], WGN[:, k, :], self.HN[:, k, tq], start=(k == 0), stop=(k == NCH - 1))
                    return ins
                S.op("pe", mmg, reads=hreads + [cb], writes=[pb2])
                S.op("act", lambda e: e.activation(GS[:], p2[0:24, 0:128], AF.Sigmoid), reads=[pb2], writes=[gsb])

                def tiles_of(br):
                    if br == 0:
                        return [None]
                    if br == 1:
                        return list(range(0, i + 1))
                    return list(range(max(0, i - 4), i + 1))
                odset = {0: 0, 1: 0, 2: 1}

                def emit_scores(step):
                    br, g, kt, first, last = step
                    gs = slice(g * 64, (g + 1) * 64)
                    qrhs = QS[gs, :, :].rearrange("p j q -> p (j q)")
                    rows = 127 if br == 0 else 128
                    stp, stb = st_ring.get()
                    mms_list = []
                    if br == 0:
                        mms_list.append((KC[gs, :], qrhs))
                        mms_list.append((SHCF[:, 120 - 8 * i:247 - 8 * i], BVC[:, g, :]))
                    else:
                        mms_list.append((KT[gs, br - 1, kt * 128:(kt + 1) * 128], qrhs))
                        if br == 1 and i >= 8:
                            mms_list.append((EF[:, kt * 128:(kt + 1) * 128], NMT[g][:].rearrange("p j q -> p (j q)")))
                        if kt == i:
                            mms_list.append((IDb[:], BM[:, 0, g, :]))
                        elif kt == i - 1:
                            mms_list.append((IDb[:], BM[:, 1, g, :]))
                        elif br == 2 and kt == i - 4:
                            mms_list.append((IDb[:], BM[:, 2, g, :]))

                    def mms(e):
                        for n_, (l_, r_) in enumerate(mms_list):
                            ins = e.matmul(stp[0:rows, :], l_, r_, start=(n_ == 0), stop=(n_ == len(mms_list) - 1))
                        return ins
                    S.op("pe", mms, reads=[qsb, KTB, kcb, cb, nmb[g]], writes=[stb])
                    if br == 0:
                        PT, ptb = PTc[g], ptcb[g]
                    else:
                        PT, ptb = pt_ring.get()
                    S.op("act", lambda e: e.activation(PT[0:rows, :], stp[0:rows, :], AF.Exp), reads=[stb], writes=[ptb])
                    return (PT, ptb, rows)

                def emit_pv(step, ctx):
                    br, g, kt, first, last = step
                    PT, ptb, rows = ctx
                    gs = slice(g * 64, (g + 1) * 64)
                    O, Ob, DN, Db = OD[odset[br]]
                    if br == 0:
                        vl = VC[0:127, gs]
                    else:
                        c0 = (0 if br == 1 else 128) + g * 64
                        vl = VT[:, kt, c0:c0 + 64]

                    def mmo(e):
                        e.matmul(O[gs, :], vl, PT[0:rows, :], start=first, stop=last)
                        return e.matmul(DN[gs, :], self.ones_b[0:rows, 0:64], PT[0:rows, :], start=first, stop=last)
                    S.op("pe", mmo, reads=[ptb, VTB, kcb, self.constb], writes=[Ob, Db])

                def cmp_extras(g, ctx):
                    PT, ptb, rows = ctx
                    th = []
                    box = {}

                    def t0():
                        box["pd2"], box["pdb2"] = ms_ring.get()
                        S.op("pe", lambda e: e.matmul(box["pd2"][0:127, :], self.ones_b[0:127, 0:127], PT[0:127, :], start=True, stop=True),
                             reads=[ptb, self.constb], writes=[box["pdb2"]])
                        S.op("dve", lambda e: e.tensor_scalar(RRc[0:127, :], box["pd2"][0:127, :], 1e-30, None, ALU.max), reads=[box["pdb2"], rcb], writes=[rcb])
                    th.append(t0)
                    th.append(lambda: S.op("dve", lambda e: e.reciprocal(RRc[0:127, :], RRc[0:127, :]), reads=[rcb], writes=[rcb]))
                    th.append(lambda: S.op("dve", lambda e: e.tensor_tensor(PN[0:127, :], PT[0:127, :], RRc[0:127, :], ALU.mult), reads=[rcb, ptb, pnb], writes=[pnb]))

                    def t3():
                        box["pim"], box["pimb"] = ms_ring.get()

                        def mmi(e):
                            for j in range(4):
                                ins = e.matmul(box["pim"][:, 0:32], PN[0:127, j * 128:(j + 1) * 128], OV[0:127, :], start=(j == 0), stop=(j == 3))
                            return ins
                        S.op("pe", mmi, reads=[pnb, cb], writes=[box["pimb"]])
                        o0 = 32 - 2 * i
                        S.op("dve", lambda e: e.tensor_tensor(IMP[:], box["pim"][:, 0:32], AB[:, 0, o0:o0 + 32], ALU.mult), reads=[box["pimb"], cb, ib], writes=[ib])
                    th.append(t3)
                    o0 = 32 - 2 * i
                    th.append(lambda: S.op("dve", lambda e: e.tensor_tensor(IMP[:], IMP[:], AB[:, 1, o0:o0 + 32], ALU.add), reads=[ib, cb], writes=[ib]))
                    th.append(lambda: S.op("dve", lambda e: e.memset(IMP[:, 0:1], 1e6), reads=[ib], writes=[ib]))
                    th.append(lambda: S.op("dve", lambda e: e.max(MX[:], IMP[:]), reads=[ib], writes=[ib]))
                    th.append(lambda: S.op("dve", lambda e: e.match_replace(IM2[:], MX[:], IMP[:], 0.0), reads=[ib], writes=[ib]))
                    th.append(lambda: S.op("dve", lambda e: e.max(MX[:], IM2[:]), reads=[ib], writes=[ib]))
                    th.append(lambda: S.op("dve", lambda e: e.match_replace(IM2[:], MX[:], IM2[:], 0.0), reads=[ib], writes=[ib]))
                    th.append(lambda: S.op("dve", lambda e: e.tensor_tensor(IM2[:], IMP[:], IM2[:], ALU.subtract), reads=[ib], writes=[ib]))
                    th.append(lambda: S.op("dve", lambda e: e.tensor_scalar(IM2[:], IM2[:], 0.0, None, ALU.is_gt), reads=[ib], writes=[ib]))
                    th.append(lambda: S.op("dve", lambda e: e.tensor_scalar(IM2[:], IM2[:], 30000.0, -30000.0, ALU.mult, ALU.add), reads=[ib], writes=[ib]))

                    def tl():
                        ptr, ptrb = ms_ring.get()
                        S.op("pe", lambda e: e.transpose(ptr[0:32, 0:128], IM2[:], self.ident_f[:]), reads=[ib, self.constb], writes=[ptrb])
                        S.op("dve", lambda e: e.tensor_copy(NMT[g][:], ptr[0:32, 0:128].unsqueeze(1).to_broadcast([32, 4, 128])),
                             reads=[ptrb], writes=[nmb[g]])
                    th.append(tl)
                    return th

                def finalize(br):
                    O, Ob, DN, Db = OD[odset[br]]
                    S.op("dve", lambda e: e.tensor_scalar(RR[:], DN[:], 1e-30, None, ALU.max), reads=[Db, rrb], writes=[rrb])
                    S.op("dve", lambda e: e.reciprocal(RR[:], RR[:]), reads=[rrb], writes=[rrb])
                    pgb_, pgbb = ms_ring.get()

                    def mmgb(e):
                        for j in range(4):
                            ins = e.matmul(pgb_[:, j * 128:(j + 1) * 128], SELG[:, br * 4 + j, :], GS[:], start=True, stop=True)
                        return ins
                    S.op("pe", mmgb, reads=[gsb, cb], writes=[pgbb])
                    S.op("dve", lambda e: e.tensor_tensor(RR[:], RR[:], pgb_[:], ALU.mult), reads=[rrb, pgbb], writes=[rrb])
                    if br == 0:
                        S.op("dve", lambda e: e.tensor_tensor(YA[:], O[:], RR[:], ALU.mult), reads=[Ob, rrb, yab], writes=[yab])
                    else:
                        S.op("dve", lambda e: e.tensor_tensor(RR[:], O[:], RR[:], ALU.mult), reads=[Ob, rrb], writes=[rrb])
                        S.op("dve", lambda e: e.tensor_tensor(YA[:], YA[:], RR[:], ALU.add), reads=[rrb, yab], writes=[yab])

                extras = []
                for g in range(2):
                    st_ = (0, g, None, True, True)
                    ctx = emit_scores(st_)
                    emit_pv(st_, ctx)
                    if i >= 8:
                        extras += cmp_extras(g, ctx)
                finalize(0)

                def run_steps(steps, fill):
                    ctxs = {}
                    for n in range(len(steps) + LOOK):
                        if n < len(steps):
                            ctxs[n] = emit_scores(steps[n])
                        m = n - LOOK
                        if m >= 0:
                            emit_pv(steps[m], ctxs.pop(m))
                            br_, g_, kt_, f_, l_ = steps[m]
                            if g_ == 1 and l_:
                                finalize(br_)
                        for _ in range(3):
                            if fill:
                                fill.pop(0)()

                def mk_steps(br):
                    out = []
                    for g in range(2):
                        tl_ = tiles_of(br)
                        for ti, kt in enumerate(tl_):
                            out.append((br, g, kt, ti == 0, ti == len(tl_) - 1))
                    return out
                run_steps(mk_steps(2), extras)
                while extras:
                    extras.pop(0)()
                run_steps(mk_steps(1), [])
                S.op("act", lambda e: e.copy(self.Y[1][:, :, tq], YA[:].rearrange("p (j q) -> p j q", j=4)), reads=[yab], writes=[self.YB[1][tcix]])
            S.full_barrier()
            self.st = old

    def mem_branch(self, memT, wk_d, wv_d, wqm_d):
        S = self.S
        with ExitStack() as st4:
            old, self.st = self.st, st4
            self._norm_rings_open()
            WQ = self.sb("WQM", [128, NCH, 512], BF16); WQB = Buf()
            KHT = self.sb("KHT", [128, 4, 256], BF16); KHTB = Buf()
            VH = self.sb("VH", [128, 2, 512], BF16); VHB = Buf()
            st5 = ExitStack()
            self.st = st5
            MT = self.sb("MT", [128, NCH, 256], F32); MTB = Buf()
            MN = self.sb("MN", [128, NCH, 256], BF16); MNB = Buf()
            WK = self.sb("WK", [128, NCH, 512], BF16); WKB = Buf()
            WV = self.sb("WV", [128, NCH, 512], BF16); WVB = Buf()
            mr = self.sb("mrstd", [128, 256], F32); mrb = Buf()
            S.dma(MT[:], memT.rearrange("(c p) m -> p c m", p=128), writes=[MTB])
            self.load_w(WK[:], wk_d, WKB)
            self.load_w(WV[:], wv_d, WVB)
            self.load_w(WQ[:], wqm_d, WQB)
            g0, _ = COLS["mem_norm"]
            pt, pb = self.psum.get()
            for c in range(NCH):
                sq, sqb = self.sq_ring.get()
                S.op("act", lambda e: e.activation(sq[:, 0:256], MT[:, c, :], AF.Square), reads=[MTB], writes=[sqb])
                S.op("pe", lambda e: e.matmul(pt[:, 0:256], self.ones_f[:], sq[:, 0:256], start=(c == 0), stop=(c == NCH - 1)),
                     reads=[sqb, self.constb], writes=[pb])
            S.op("act", lambda e: e.activation(mr[:], pt[:, 0:256], AF.Sqrt, bias=self.eps_t[:], scale=1.0 / D),
                 reads=[pb, self.constb], writes=[mrb])
            S.op("dve", lambda e: e.reciprocal(mr[:], mr[:]), reads=[mrb], writes=[mrb])
            for c in range(NCH):
                S.op("dve", lambda e: e.scalar_tensor_tensor(MN[:, c, :], MT[:, c, :], self.cols[:, g0 + c:g0 + c + 1], mr[:],
                                                             ALU.mult, ALU.mult),
                     reads=[MTB, mrb, self.constb], writes=[MNB])
            for h in range(4):
                p, pb = self.psum.get()

                def mm(e):
                    for k in range(NCH):
                        ins = e.matmul(p[:, 0:256], WK[:, k, h * 128:(h + 1) * 128], MN[:, k, :], start=(k == 0), stop=(k == NCH - 1))
                    return ins
                S.op("pe", mm, reads=[WKB, MNB], writes=[pb])
                S.op("act", lambda e: e.copy(KHT[:, h, :], p[:, 0:256]), reads=[pb], writes=[KHTB])
            for mt in range(2):
                p, pb = self.psum.get()

                def mm(e):
                    for k in range(NCH):
                        ins = e.matmul(p[:], MN[:, k, mt * 128:(mt + 1) * 128], WV[:, k, :], start=(k == 0), stop=(k == NCH - 1))
                    return ins
                S.op("pe", mm, reads=[WVB, MNB], writes=[pb])
                S.op("act", lambda e: e.copy(VH[:, mt, :], p[:]), reads=[pb], writes=[VHB])
            S.full_barrier()
            st5.close()
            self.st = st4
            qm_ring = Ring([self.sb("qm%d" % i, [128, TC], BF16) for i in range(2)])
            pt_ring = Ring([self.sb("pt%d" % i, [128, 2, TC], BF16) for i in range(2)])
            rd_ring = self.sq_ring
            scale = 128.0 ** -0.5
            for tc in range(NTC):
                ts = slice(tc * TC, (tc + 1) * TC)
                hreads = [self.HNB[c][tc] for c in range(NCH)]
                for h in range(4):
                    p, pb = self.psum.get()

                    def mm(e):
                        for k in range(NCH):
                            ins = e.matmul(p[:], WQ[:, k, h * 128:(h + 1) * 128], self.HN[:, k, ts], start=(k == 0), stop=(k == NCH - 1))
                        return ins
                    S.op("pe", mm, reads=hreads + [WQB], writes=[pb])
                    qm, qmb = qm_ring.get()
                    S.op("dve", lambda e: e.tensor_copy(qm[:], p[:]), reads=[pb], writes=[qmb])
                    ptile, ptb = pt_ring.get()
                    for mt in range(2):
                        ps_, psb = self.psum.get()
                        S.op("pe", lambda e: e.matmul(ps_[:], KHT[:, h, mt * 128:(mt + 1) * 128], qm[:], start=True, stop=True),
                             reads=[KHTB, qmb], writes=[psb])
                        S.op("act", lambda e: e.activation(ptile[:, mt, :], ps_[:], AF.Exp, scale=scale), reads=[psb], writes=[ptb])
                    po, pob = self.psum.get()
                    pd, pdb = self.psum.get()

                    def mm_o(e):
                        for mt in range(2):
                            ins = e.matmul(po[:], VH[:, mt, h * 128:(h + 1) * 128], ptile[:, mt, :], start=(mt == 0), stop=(mt == 1))
                        return ins

                    def mm_d(e):
                        for mt in range(2):
                            ins = e.matmul(pd[:], self.ones_b[:], ptile[:, mt, :], start=(mt == 0), stop=(mt == 1))
                        return ins
                    S.op("pe", mm_o, reads=[VHB, ptb], writes=[pob])
                    S.op("pe", mm_d, reads=[ptb, self.constb], writes=[pdb])
                    rd, rdb = rd_ring.get()
                    S.op("dve", lambda e: e.reciprocal(rd[:], pd[:]), reads=[pdb], writes=[rdb])
                    S.op("dve", lambda e: e.tensor_tensor(self.Y[2][:, h, ts], po[:], rd[:], ALU.mult),
                         reads=[pob, rdb], writes=[self.YB[2][tc]])
            S.full_barrier()
            self.st = old
        self._nst.close()

    def fold(self, br, wgb_d, wbr_d, first):
        S = self.S
        with ExitStack() as st4:
            old, self.st = self.st, st4
            WGB = [self.sb("WGBr%d" % i, [128, NCH, 128], BF16) for i in range(2)]; WGBB = [Buf(), Buf()]
            WBR = [self.sb("WBR%d" % i, [128, 4, 128], BF16) for i in range(2)]; WBRB = [Buf(), Buf()]
            gt_ring = Ring([self.sb("gt%d" % i, [128, TC], F32) for i in range(2)])
            t_ring = Ring([self.sb("mt%d" % i, [128, TC], F32) for i in range(2)])

            def load(dc):
                sl = dc % 2
                c0 = br * D + dc * 128
                S.dma(WGB[sl][:], wgb_d[:, c0:c0 + 128].rearrange("(k p) n -> p k n", p=128), writes=[WGBB[sl]], queue="pool")
                S.dma(WBR[sl][:], wbr_d[:, dc * 128:(dc + 1) * 128].rearrange("(k p) n -> p k n", p=128), writes=[WBRB[sl]], queue="pool")
            load(0)
            for dc in range(NCH):
                if dc + 1 < NCH:
                    load(dc + 1)
                sl = dc % 2
                for tc in range(NTC):
                    ts = slice(tc * TC, (tc + 1) * TC)
                    hreads = [self.HNB[c][tc] for c in range(NCH)]
                    pg, pgb = self.psum.get()
                    py, pyb = self.psum.get()

                    def mm_g(e):
                        for k in range(NCH):
                            ins = e.matmul(pg[:], WGB[sl][:, k, :], self.HN[:, k, ts], start=(k == 0), stop=(k == NCH - 1))
                        return ins

                    def mm_y(e):
                        for k in range(4):
                            ins = e.matmul(py[:], WBR[sl][:, k, :], self.Y[br][:, k, ts], start=(k == 0), stop=(k == 3))
                        return ins
                    S.op("pe", mm_g, reads=hreads + [WGBB[sl]], writes=[pgb])
                    S.op("pe", mm_y, reads=[self.YB[br][tc], WBRB[sl]], writes=[pyb])
                    gt, gtb = gt_ring.get()
                    S.op("act", lambda e: e.activation(gt[:], pg[:], AF.Sigmoid), reads=[pgb], writes=[gtb])
                    if first:
                        S.op("dve", lambda e: e.tensor_tensor(self.M[:, dc, ts], gt[:], py[:], ALU.mult),
                             reads=[gtb, pyb], writes=[self.MB[dc][tc]])
                    else:
                        t, tb = t_ring.get()
                        S.op("dve", lambda e: e.tensor_tensor(t[:], gt[:], py[:], ALU.mult), reads=[gtb, pyb], writes=[tb])
                        S.op("pool", lambda e: e.tensor_tensor(self.M[:, dc, ts], self.M[:, dc, ts], t[:], ALU.add),
                             reads=[tb, self.MB[dc][tc]], writes=[self.MB[dc][tc]])
            S.full_barrier()
            self.st = old

    def outproj(self, wout_d):
        S = self.S
        with ExitStack() as st4:
            old, self.st = self.st, st4
            WO = self.sb("WO", [128, NCH, D], BF16); WOB = Buf()
            self.load_w(WO[:], wout_d, WOB)
            for tc in range(NTC):
                ts = slice(tc * TC, (tc + 1) * TC)
                for d2 in range(NCH):
                    po, pob = self.psum.get()

                    def mm(e):
                        for k in range(NCH):
                            ins = e.matmul(po[:], WO[:, k, d2 * 128:(d2 + 1) * 128], self.M[:, k, ts], start=(k == 0), stop=(k == NCH - 1))
                        return ins
                    S.op("pe", mm, reads=[self.MB[k][tc] for k in range(NCH)] + [WOB], writes=[pob])
                    S.op("dve", lambda e: e.tensor_tensor(self.X[:, d2, ts], po[:], self.X[:, d2, ts], ALU.add),
                         reads=[pob, self.XB[d2][tc]], writes=[self.XB[d2][tc]])
            S.full_barrier()
            self.st = old

    def final_norm_out(self, outT):
        S = self.S
        g0, _ = COLS["final_norm"]
        self._norm_rings_open()
        for tc in range(NTC):
            ts = slice(tc * TC, (tc + 1) * TC)
            pt, pb = self.psum.get()
            for c in range(NCH):
                sq, sqb = self.sq_ring.get()
                S.op("act", lambda e: e.activation(sq[:], self.X[:, c, ts], AF.Square),
                     reads=[self.XB[c][tc]], writes=[sqb])
                S.op("pe", lambda e: e.matmul(pt[:], self.ones_f[:], sq[:], start=(c == 0), stop=(c == NCH - 1)),
                     reads=[sqb, self.constb], writes=[pb])
            rs, rsb = self.rstd_ring.get()
            S.op("act", lambda e: e.activation(rs[:], pt[:], AF.Sqrt, bias=self.eps_t[:], scale=1.0 / D),
                 reads=[pb, self.constb], writes=[rsb])
            S.op("dve", lambda e: e.reciprocal(rs[:], rs[:]), reads=[rsb], writes=[rsb])
            for c in range(NCH):
                S.op("dve", lambda e: e.scalar_tensor_tensor(
                    self.X[:, c, ts], self.X[:, c, ts], self.cols[:, g0 + c:g0 + c + 1], rs[:],
                    ALU.mult, ALU.mult),
                    reads=[self.XB[c][tc], rsb, self.constb], writes=[self.XB[c][tc]])
                S.dma(outT[c * 128:(c + 1) * 128, ts], self.X[:, c, ts], reads=[self.XB[c][tc]])
        self._norm_rings_close()

    def dump_x(self, name):
        o = self.dout(name, [D, S_LEN])
        for c in range(NCH):
            for tc in range(NTC):
                ts = slice(tc * TC, (tc + 1) * TC)
                self.S.dma(o[c * 128:(c + 1) * 128, ts], self.X[:, c, ts], reads=[self.XB[c][tc]])

    def build(self, stop_after=None):
        nc = self.nc
        dbg = self.debug
        xT = self.din("xT", [D, S_LEN])
        cols_d = self.din("cols", [128, NCOLS])
        f1g = self.din("ffn1_w_gate", [D, DFF]); f1u = self.din("ffn1_w_up", [D, DFF]); f1d = self.din("ffn1_w_down", [DFF, D])
        f2g = self.din("ffn2_w_gate", [D, DFF]); f2u = self.din("ffn2_w_up", [D, DFF]); f2d = self.din("ffn2_w_down", [DFF, D])
        memT = self.din("memT", [D, 256])
        mem_wk = self.din("mem_w_k", [D, 512]); mem_wv = self.din("mem_w_v", [D, 512])
        w_qm = self.din("w_qm", [D, 512])
        w_gb = self.din("w_gb", [D, 3 * D])
        w_br = [self.din(n, [512, D]) for n in ("w_br_rwkv", "w_br_nsa_p", "w_br_mem")]
        w_out = self.din("w_out", [D, D])
        w_rwkv = self.din("w_rwkv", [D, 1792])
        w2_d = self.din("rwkv_w2", [64, 512]); a2_d = self.din("rwkv_a2", [64, 512]); g2_d = self.din("rwkv_g2", [128, 512])
        gng_d = self.din("gng_rep", [128, 512]); gnb_d = self.din("gnb_rep", [128, 512])
        ident_d = self.din("ident", [128, 128])
        rmk_d = self.din("rwkv_masks", [128, 3, 128])
        nd = {}
        nd["w_qn"] = self.din("w_qn", [D, 512]); nd["w_gn"] = self.din("w_gn", [D, 24]); nd["w_kvn"] = self.din("w_kvn", [D, 768])
        nd["shcf"] = self.din("shcf", [32, 247]); nd["efull"] = self.din("efull", [32, S_LEN]); nd["ov"] = self.din("ov", [127, 32])
        nd["abf"] = self.din("abf", [128, 2, 64]); nd["selg"] = self.din("selg", [24, 12, 128])
        nd["t31"] = self.din("t31", [128, 2, 512])
        nd["bmg"] = [self.din("bmg%d" % k, [128, 2, 512]) for k in range(3)]
        nd["msk"] = [self.din("msk%d" % k, [128, 128]) for k in range(3)]
        nd["bvcg"] = self.din("bvcg", [32, 2, 512]); nd["mskc"] = self.din("mskc", [32, 128])
        nd["cmp_w1"] = [self.din("cmp_k_w1", [2048, 256]), self.din("cmp_v_w1", [2048, 256])]
        nd["cmp_w2"] = [self.din("cmp_k_w2", [256, 64]), self.din("cmp_v_w2", [256, 64])]
        nd["cmp_peT"] = [self.din("cmp_pe_kT", [64, 32]), self.din("cmp_pe_vT", [64, 32])]
        outT = self.dout("outT", [D, S_LEN])
        with ExitStack() as st:
            self.st = st
            S = self.S = Sched(nc, st)
            self.X = self.sb("X", [128, NCH, S_LEN], F32)
            self.XB = [[Buf() for _ in range(NTC)] for _ in range(NCH)]
            self.cols = self.sb("cols", [128, NCOLS], F32)
            self.ones_f = self.sb("ones_f", [128, 128], F32)
            self.ones_b = self.sb("ones_b", [128, 128], BF16)
            self.eps_t = self.sb("eps_t", [128, 1], F32)
            self.gneps_t = self.sb("gneps_t", [128, 1], F32)
            self.ident_f = self.sb("ident_f", [128, 128], F32)
            self.constb = Buf("const")
            self.PS = self.ps("PSALL", [128, 8, 512])
            self.banks = [self.PS[:, i, :] for i in range(8)]
            self.bankb = [Buf() for _ in range(8)]
            self.psum = Ring(self.banks, self.bankb)
            S.dma(self.cols[:], cols_d, writes=[self.constb])
            S.op("dve", lambda e: e.memset(self.ones_f[:], 1.0), reads=[self.constb], writes=[self.constb])
            S.op("dve", lambda e: e.memset(self.ones_b[:], 1.0), reads=[self.constb], writes=[self.constb])
            S.op("dve", lambda e: e.memset(self.eps_t[:], EPS), reads=[self.constb], writes=[self.constb])
            S.op("dve", lambda e: e.memset(self.gneps_t[:], 64e-5), reads=[self.constb], writes=[self.constb])
            S.dma(self.ident_f[:], ident_d, reads=[self.constb], writes=[self.constb])
            for c in range(NCH):
                for tc in range(NTC):
                    ts = slice(tc * TC, (tc + 1) * TC)
                    S.dma(self.X[:, c, ts], xT[c * 128:(c + 1) * 128, ts], writes=[self.XB[c][tc]])

            def ffn_phase(wg, wu, wd, gname):
                with ExitStack() as st2:
                    self.st = st2
                    self.HN = self.sb("HN", [128, NCH, S_LEN], BF16)
                    self.HNB = [[Buf() for _ in range(NTC)] for _ in range(NCH)]
                    self.WG = [self.sb("WG%d" % i, [128, NCH, 512], BF16) for i in range(2)]
                    self.WU = [self.sb("WU%d" % i, [128, NCH, 512], BF16) for i in range(2)]
                    self.WD = [self.sb("WD%d" % i, [128, 4, D], BF16) for i in range(2)]
                    self.WGB = [Buf() for _ in range(2)]; self.WUB = [Buf() for _ in range(2)]; self.WDB = [Buf() for _ in range(2)]
                    self.a_ring = Ring([self.sb("a%d" % i, [128, 4, TC], BF16) for i in range(2)])
                    self.sg_ring = Ring([self.sb("sg%d" % i, [128, TC], F32) for i in range(2)])
                    self.ffn(wg, wu, wd, gname)
                    S.full_barrier()
                    self.st = st

            if "noffn1" not in dbg:
                ffn_phase(f1g, f1u, f1d, "ffn1_norm")
            if "x1" in dbg:
                self.dump_x("dbg_x1")
            if stop_after != "ffn1":
                with ExitStack() as st3:
                    self.st = st3
                    self.HN = self.sb("HN", [128, NCH, S_LEN], BF16)
                    self.HNB = [[Buf() for _ in range(NTC)] for _ in range(NCH)]
                    Yt = self.sb("Yt", [128, 4, S_LEN], BF16)
                    YBt = [Buf() for _ in range(NTC)]
                    self.Y = [Yt, Yt, Yt]
                    self.YB = [YBt, YBt, YBt]
                    self.rmsnorm_to_hn("mix_norm")
                    if "norwkv" not in dbg:
                        if "rwkvseq" in dbg:
                            self.rwkv_branch_seq(w_rwkv, w2_d, a2_d, g2_d, gng_d, gnb_d)
                        else:
                            self.rwkv_branch(w_rwkv, w2_d, a2_d, g2_d, gng_d, gnb_d, rmk_d)
                    else:
                        S.op("pool", lambda e: e.memset(Yt[:], 0.0), writes=YBt)
                    if "y_rwkv" in dbg:
                        self.dump_feat("dbg_y_rwkv", Yt, 4, YBt)
                    self.M = self.sb("M", [128, NCH, S_LEN], BF16)
                    self.MB = [[Buf() for _ in range(NTC)] for _ in range(NCH)]
                    do_merge = stop_after != "mix"
                    if do_merge:
                        self.fold(0, w_gb, w_br[0], True)
                    if "nomem" not in dbg:
                        self.mem_branch(memT, mem_wk, mem_wv, w_qm)
                    else:
                        S.op("pool", lambda e: e.memset(Yt[:], 0.0), writes=YBt)
                    if "y_mem" in dbg:
                        self.dump_feat("dbg_y_mem", Yt, 4, YBt)
                    if do_merge:
                        self.fold(2, w_gb, w_br[2], False)
                    if "nonsa" not in dbg:
                        self.nsa_branch(nd)
                    else:
                        S.op("pool", lambda e: e.memset(Yt[:], 0.0), writes=YBt)
                    if "y_nsa" in dbg:
                        self.dump_feat("dbg_y_nsa_p", Yt, 4, YBt)
                    if do_merge:
                        self.fold(1, w_gb, w_br[1], False)
                        self.outproj(w_out)
                    S.full_barrier()
                    self.st = st
                if "x2" in dbg:
                    self.dump_x("dbg_x2")
                if stop_after not in ("mix", "merge"):
                    ffn_phase(f2g, f2u, f2d, "ffn2_norm")
            self.final_norm_out(outT)
            S.wait_all_dma("sp")
            S.wait_all_dma("pool")
        return nc


NSA_PERM = np.concatenate([np.concatenate([np.arange(64 * j, 64 * j + 64), np.arange(64 * (4 + j), 64 * (4 + j) + 64)])
                           for j in range(4)])


def _t5_bucket_np(dist):
    n = np.maximum(dist, 0)
    nf = np.maximum(n, 1).astype(np.float32)
    large = 16 + (np.log(nf / np.float32(16)) / np.float32(math.log(128 / 16)) * np.float32(16)).astype(np.int32)
    large = np.minimum(large, 31)
    return np.where(n < 16, n, large)


def _nsa_consts(rel_bias):
    rb = np.asarray(rel_bias, np.float32)
    c = np.arange(128)[:, None]; p = np.arange(128)[None, :]
    out = {}
    hd = np.arange(8).reshape(2, 4)
    dists = [p - c, 128 + p - c, 512 + p - c]
    valid = [p >= c, np.ones((128, 128), bool), c > p]
    for k in range(3):
        bk = _t5_bucket_np(dists[k])
        g = rb[bk[:, None, None, :], hd[None, :, :, None]]
        out["bmg%d" % k] = np.ascontiguousarray(g.reshape(128, 2, 512))
        out["msk%d" % k] = np.where(valid[k], 0.0, -30000.0).astype(np.float32)
    out["t31"] = np.ascontiguousarray(np.broadcast_to(rb[31][hd][None, :, :, None], (128, 2, 4, 128)).reshape(128, 2, 512))
    m = np.arange(32)[:, None]
    dc = p - 16 * (m - 8) - 31
    bk = _t5_bucket_np(dc)
    g = rb[bk[:, None, None, :], hd[None, :, :, None]]
    out["bvcg"] = np.ascontiguousarray(g.reshape(32, 2, 512))
    mk = np.where((dc >= 0) & (m < 16), 0.0, -30000.0).astype(np.float32)
    mk[17:] = 0.0
    out["mskc"] = mk
    shcf = np.zeros((32, 247), np.float32)
    for x in range(247):
        r = x - 112
        if 0 <= r < 16:
            shcf[r, x] = 1.0
        elif r >= 16:
            shcf[16, x] = 1.0
    out["shcf"] = shcf
    ef = np.zeros((32, S_LEN), np.float32)
    ef[np.arange(S_LEN) // 64, np.arange(S_LEN)] = 1.0
    out["efull"] = ef
    ic = np.arange(127)[:, None]; jb = np.arange(32)[None, :]
    out["ov"] = ((ic * 16 <= jb * 64 + 63) & (ic * 16 + 31 >= jb * 64)).astype(np.float32)
    ab = np.zeros((128, 2, 64), np.float32)
    for pp in range(128):
        curr = 1 if pp >= 64 else 0
        for mm in range(64):
            jr = mm - 32
            if jr <= curr - 2:
                ab[pp, 0, mm] = 1.0
            if jr in (curr, curr - 1):
                ab[pp, 1, mm] = 1e6
    out["abf"] = ab
    selg = np.zeros((24, 12, 128), np.float32)
    for br in range(3):
        for j in range(4):
            for mm in range(128):
                selg[br * 8 + (mm // 64) * 4 + j, br * 4 + j, mm] = 1.0
    out["selg"] = selg
    return out


def prep_inputs(inputs, b):
    m = {}
    m["xT"] = np.ascontiguousarray(inputs["x"][b].T)
    cols = np.zeros((128, NCOLS), np.float32)
    for n in ("ffn1_norm", "mix_norm", "ffn2_norm", "final_norm", "mem_norm"):
        c0, k = COLS[n]
        cols[:, c0:c0 + k] = _colpack(np.asarray(inputs[n]).reshape(-1))
    for n, src in (("mu", "rwkv_mu"), ("w0", "rwkv_w0"), ("a0", "rwkv_a0"), ("k_k", "rwkv_k_k"), ("k_a", "rwkv_k_a"), ("r_k", "rwkv_r_k"),
                   ("gn_g", "rwkv_gn_gain"), ("gn_b", "rwkv_gn_bias")):
        c0, k = COLS[n]
        cols[:, c0:c0 + k] = _colpack(np.asarray(inputs[src]).reshape(-1))
    m["cols"] = cols
    m["w_rwkv"] = np.ascontiguousarray(np.asarray(inputs["w_in"])[0][:, 0:1792])
    m["rwkv_w2"] = np.ascontiguousarray(np.asarray(inputs["rwkv_w2"])[0])
    m["rwkv_a2"] = np.ascontiguousarray(np.asarray(inputs["rwkv_a2"])[0])
    m["rwkv_g2"] = np.ascontiguousarray(np.asarray(inputs["rwkv_g2"])[0])
    m["gng_rep"] = np.ascontiguousarray(np.broadcast_to(np.asarray(inputs["rwkv_gn_gain"]).reshape(1, 512), (128, 512)))
    m["gnb_rep"] = np.ascontiguousarray(np.broadcast_to(np.asarray(inputs["rwkv_gn_bias"]).reshape(1, 512), (128, 512)))
    m["ident"] = np.eye(128, dtype=np.float32)
    si = np.arange(128)[:, None]; ti = np.arange(128)[None, :]
    same = (si // 64) == (ti // 64)
    mk = np.zeros((128, 3, 128), np.float32)
    mk[:, 0, :] = np.where(same & (si < ti), -1.0, 0.0)
    mk[:, 1, :] = np.where(same & (ti < si), -1.0, 0.0)
    mk[:, 2, :] = np.where(same & (si <= ti), 1.0, 0.0)
    m["rwkv_masks"] = mk
    w_in_ = np.asarray(inputs["w_in"])[0]
    m["w_qn"] = np.ascontiguousarray(w_in_[:, 1792:2304][:, NSA_PERM])
    m["w_kvn"] = np.ascontiguousarray(w_in_[:, 2304:3072])
    m["w_gn"] = np.ascontiguousarray(w_in_[:, 3072:3096])
    m.update(_nsa_consts(inputs["rel_bias"]))
    for n in ("cmp_k_w1", "cmp_v_w1", "cmp_k_w2", "cmp_v_w2"):
        m[n] = np.ascontiguousarray(np.asarray(inputs[n])[0])
    m["cmp_pe_kT"] = np.ascontiguousarray(np.asarray(inputs["cmp_pe_k"])[0].T)
    m["cmp_pe_vT"] = np.ascontiguousarray(np.asarray(inputs["cmp_pe_v"])[0].T)
    for n in ("ffn1_w_gate", "ffn1_w_up", "ffn1_w_down", "ffn2_w_gate", "ffn2_w_up", "ffn2_w_down",
              "mem_w_k", "mem_w_v", "w_br_rwkv", "w_br_mem", "w_out"):
        m[n] = np.ascontiguousarray(np.asarray(inputs[n])[0])
    m["memT"] = np.ascontiguousarray(inputs["mem"][b].T)
    w_in = np.asarray(inputs["w_in"])[0]
    m["w_qm"] = np.ascontiguousarray(w_in[:, 3096:3608])
    m["w_gb"] = np.ascontiguousarray(w_in[:, 3608:6680])
    m["w_br_nsa_p"] = np.ascontiguousarray(np.asarray(inputs["w_br_nsa"])[0][NSA_PERM, :])
    return m


_CACHE = {}


def kernel(**inputs):
    inputs = {k: np.asarray(v) for k, v in inputs.items()}
    if "nc" not in _CACHE:
        _CACHE["nc"] = Builder().build()
    nc = _CACHE["nc"]
    n = 8
    in_maps = [prep_inputs(inputs, b) for b in range(n)]
    res = run_bass_kernel_spmd(nc, in_maps, core_ids=list(range(n)))
    out = np.stack([np.ascontiguousarray(r["outT"].T) for r in res.results], axis=0)
    return out.astype(np.float32)
```

```python
import math
from contextlib import ExitStack
import numpy as np
import concourse.bass as bass
import concourse.mybir as mybir
from concourse.bass_utils import run_bass_kernel_spmd

F32 = mybir.dt.float32
BF16 = mybir.dt.bfloat16
AF = mybir.ActivationFunctionType
ALU = mybir.AluOpType
AX = mybir.AxisListType

D = 1024
S_LEN = 2048
DFF = 2816
NCH = 8
TC = 512
NTC = S_LEN // TC
EPS = 1e-6


class Buf:
    __slots__ = ("name", "last_w", "readers")

    def __init__(self, name=""):
        self.name = name
        self.last_w = None
        self.readers = []


class Sched:
    ENG = ("pe", "act", "dve", "pool", "sp")

    def __init__(self, nc, stack, n_dma_sems=16):
        self.nc = nc
        self.eng = {"pe": nc.tensor, "act": nc.scalar, "dve": nc.vector,
                    "pool": nc.gpsimd, "sp": nc.sync}
        self.sem = {}
        for e in ("pe", "act", "dve", "pool"):
            self.sem[e] = stack.enter_context(nc.semaphore("s_" + e))
        self.cnt = {e: 0 for e in ("pe", "act", "dve", "pool")}
        nq = {"sp": 28, "pool": 28, "act": 8}
        self.dsem = []
        self.qsems = {}
        for q, n in nq.items():
            self.qsems[q] = list(range(len(self.dsem), len(self.dsem) + n))
            for i in range(n):
                self.dsem.append(stack.enter_context(nc.semaphore("d%s%d" % (q, i))))
        self.dcnt = [0] * len(self.dsem)
        self.dnext = {q: 0 for q in nq}
        self.waited = {e: {} for e in self.ENG}
        self.n_ops = 0
        self.n_waits = 0

    def _semobj(self, key):
        return self.sem[key] if isinstance(key, str) else self.dsem[key]

    def _need(self, engine, toks):
        best = {}
        for t in toks:
            if t is None:
                continue
            key, val = t
            if best.get(key, 0) < val:
                best[key] = val
        w = self.waited[engine]
        for key, val in best.items():
            if w.get(key, 0) >= val:
                continue
            self.eng[engine].wait_ge(self._semobj(key), val)
            w[key] = val
            self.n_waits += 1

    @staticmethod
    def _deps(reads, writes):
        toks = []
        for b in reads:
            toks.append(b.last_w)
        for b in writes:
            toks.append(b.last_w)
            toks.extend(b.readers)
        return toks

    @staticmethod
    def _commit(tok, reads, writes):
        for b in reads:
            b.readers.append(tok)
            if len(b.readers) > 48:
                best = {}
                for k, v in b.readers:
                    if best.get(k, 0) < v:
                        best[k] = v
                b.readers = list(best.items())
        for b in writes:
            b.last_w = tok
            b.readers = []

    def op(self, engine, fn, reads=(), writes=()):
        self._need(engine, self._deps(reads, writes))
        ins = fn(self.eng[engine])
        self.cnt[engine] += 1
        ins.then_inc(self.sem[engine], 1)
        tok = (engine, self.cnt[engine])
        self._commit(tok, reads, writes)
        self.n_ops += 1
        return tok

    def dma(self, out_ap, in_ap, reads=(), writes=(), queue="sp", **kw):
        pool = self.qsems[queue]
        i = pool[self.dnext[queue]]
        self.dnext[queue] = (self.dnext[queue] + 1) % len(pool)
        prev = [(i, self.dcnt[i])] if self.dcnt[i] else []
        self._need(queue, self._deps(reads, writes) + prev)
        ins = self.eng[queue].dma_start(out=out_ap, in_=in_ap, **kw)
        self.dcnt[i] += 16
        ins.then_inc(self.dsem[i], 16)
        tok = (i, self.dcnt[i])
        self._commit(tok, reads, writes)
        self.n_ops += 1
        return tok

    def barrier(self, bufs):
        toks = []
        for b in bufs:
            toks.append(b.last_w)
            toks.extend(b.readers)
        for e in self.ENG:
            self._need(e, toks)

    def full_barrier(self):
        toks = [(e, self.cnt[e]) for e in ("pe", "act", "dve", "pool") if self.cnt[e]]
        toks += [(i, self.dcnt[i]) for i in range(len(self.dsem)) if self.dcnt[i]]
        for e in self.ENG:
            self._need(e, toks)

    def wait_all_dma(self, engine="sp"):
        for i in range(len(self.dsem)):
            if self.dcnt[i]:
                self.eng[engine].wait_ge(self.dsem[i], self.dcnt[i])


class Ring:
    def __init__(self, tiles, bufs=None):
        self.tiles = tiles
        self.bufs = bufs if bufs is not None else [Buf() for _ in tiles]
        self.i = 0

    def get(self):
        t, b = self.tiles[self.i], self.bufs[self.i]
        self.i = (self.i + 1) % len(self.tiles)
        return t, b

    def get_pair_idx(self):
        if self.i % 2:
            self.i = (self.i + 1) % len(self.tiles)
        k = self.i
        self.i = (self.i + 2) % len(self.tiles)
        return k, self.bufs[k], self.bufs[k + 1]


COLS = {}
_c = 0
for _n, _k in (("ffn1_norm", 8), ("mix_norm", 8), ("ffn2_norm", 8), ("final_norm", 8),
               ("mem_norm", 8), ("mu", 14), ("w0", 4), ("a0", 4), ("k_k", 4), ("k_a", 4), ("r_k", 4), ("gn_g", 4), ("gn_b", 4)):
    COLS[_n] = (_c, _k)
    _c += _k
NCOLS = _c


def _colpack(v):
    v = np.asarray(v, np.float32).reshape(-1, 128)
    return np.ascontiguousarray(v.T)


class Builder:
    def __init__(self, debug=()):
        self.debug = set(debug)
        self._rk_stage = 99
        self._rk_tiles = S_LEN // 128
        for d_ in self.debug:
            if d_.startswith("rkstage"):
                self._rk_stage = int(d_[7:])
            if d_.startswith("rktiles"):
                self._rk_tiles = int(d_[7:])
        self.nc = bass.Bass("TRN2", target_bir_lowering=False)
        self.dram_in = {}
        self.dram_out = {}

    def din(self, name, shape, dt=F32):
        t = self.nc.dram_tensor(name, list(shape), dt, kind="ExternalInput").ap()
        self.dram_in[name] = t
        return t

    def dout(self, name, shape, dt=F32):
        t = self.nc.dram_tensor(name, list(shape), dt, kind="ExternalOutput").ap()
        self.dram_out[name] = t
        return t

    def sb(self, name, shape, dt):
        self._uid = getattr(self, "_uid", 0) + 1
        return self.st.enter_context(self.nc.sbuf_tensor("sb%d_%s" % (self._uid, name), list(shape), dt))

    def ps(self, name, shape, dt=F32):
        return self.st.enter_context(self.nc.psum_tensor("ps_" + name, list(shape), dt))

    def _norm_rings_open(self):
        self._nst_old = self.st
        self._nst = ExitStack()
        self.st = self._nst
        self.sq_ring = Ring([self.sb("sq%d" % i, [128, TC], F32) for i in range(2)])
        self.rstd_ring = Ring([self.sb("RSTD%d" % i, [128, TC], F32) for i in range(2)])
        self.st = self._nst_old

    def _norm_rings_close(self):
        self.S.full_barrier()
        self._nst.close()

    def rmsnorm_to_hn(self, gname):
        S = self.S
        g0, _ = COLS[gname]
        self._norm_rings_open()
        for tc in range(NTC):
            ts = slice(tc * TC, (tc + 1) * TC)
            pt, pb = self.psum.get()
            for c in range(NCH):
                sq, sqb = self.sq_ring.get()
                S.op("act", lambda e: e.activation(sq[:], self.X[:, c, ts], AF.Square),
                     reads=[self.XB[c][tc]], writes=[sqb])
                S.op("pe", lambda e: e.matmul(pt[:], self.ones_f[:], sq[:], start=(c == 0), stop=(c == NCH - 1)),
                     reads=[sqb, self.constb], writes=[pb])
            rs, rsb = self.rstd_ring.get()
            S.op("act", lambda e: e.activation(rs[:], pt[:], AF.Sqrt, bias=self.eps_t[:], scale=1.0 / D),
                 reads=[pb, self.constb], writes=[rsb])
            S.op("dve", lambda e: e.reciprocal(rs[:], rs[:]), reads=[rsb], writes=[rsb])
            for c in range(NCH):
                S.op("dve", lambda e: e.scalar_tensor_tensor(
                    self.HN[:, c, ts], self.X[:, c, ts], self.cols[:, g0 + c:g0 + c + 1], rs[:],
                    ALU.mult, ALU.mult),
                    reads=[self.XB[c][tc], rsb, self.constb], writes=[self.HNB[c][tc]])

        self._norm_rings_close()

    def ffn(self, wg, wu, wd, gname):
        S = self.S
        groups = [(i, min(4, 22 - i)) for i in range(0, 22, 4)]

        def load(gi):
            f0, nf = groups[gi]
            slot = gi % 2
            S.dma(self.WG[slot][:, :, 0:nf * 128],
                  wg[:, f0 * 128:(f0 + nf) * 128].rearrange("(k p) n -> p k n", p=128),
                  writes=[self.WGB[slot]], queue="pool")
            S.dma(self.WU[slot][:, :, 0:nf * 128],
                  wu[:, f0 * 128:(f0 + nf) * 128].rearrange("(k p) n -> p k n", p=128),
                  writes=[self.WUB[slot]], queue="pool")
            S.dma(self.WD[slot][:, 0:nf, :],
                  wd[f0 * 128:(f0 + nf) * 128, :].rearrange("(f p) n -> p f n", p=128),
                  writes=[self.WDB[slot]], queue="pool")

        load(0)
        load(1)
        self.rmsnorm_to_hn(gname)
        for gi, (f0, nf) in enumerate(groups):
            if gi >= 1 and gi + 1 < len(groups):
                load(gi + 1)
            slot = gi % 2
            WG, WU, WD = self.WG[slot], self.WU[slot], self.WD[slot]
            for tc in range(NTC):
                ts = slice(tc * TC, (tc + 1) * TC)
                hreads = [self.HNB[c][tc] for c in range(NCH)]
                a_t, a_b = self.a_ring.get()
                for f in range(nf):
                    pg, pgb = self.psum.get()
                    pu, pub = self.psum.get()

                    def mm_g(e):
                        for k in range(NCH):
                            ins = e.matmul(pg[:], WG[:, k, f * 128:(f + 1) * 128], self.HN[:, k, ts],
                                           start=(k == 0), stop=(k == NCH - 1))
                        return ins

                    def mm_u(e):
                        for k in range(NCH):
                            ins = e.matmul(pu[:], WU[:, k, f * 128:(f + 1) * 128], self.HN[:, k, ts],
                                           start=(k == 0), stop=(k == NCH - 1))
                        return ins
                    S.op("pe", mm_g, reads=hreads + [self.WGB[slot]], writes=[pgb])
                    S.op("pe", mm_u, reads=hreads + [self.WUB[slot]], writes=[pub])
                    sg, sgb = self.sg_ring.get()
                    S.op("act", lambda e: e.activation(sg[:], pg[:], AF.Silu), reads=[pgb], writes=[sgb])
                    S.op("dve", lambda e: e.tensor_tensor(a_t[:, f, :], sg[:], pu[:], ALU.mult),
                         reads=[sgb, pub], writes=[a_b])
                for dc in range(NCH):
                    po, pob = self.psum.get()

                    def mm_d(e):
                        for f in range(nf):
                            ins = e.matmul(po[:], WD[:, f, dc * 128:(dc + 1) * 128], a_t[:, f, :],
                                           start=(f == 0), stop=(f == nf - 1))
                        return ins
                    S.op("pe", mm_d, reads=[a_b, self.WDB[slot]], writes=[pob])
                    S.op("dve", lambda e: e.scalar_tensor_tensor(
                        self.X[:, dc, ts], po[:], 0.5, self.X[:, dc, ts], ALU.mult, ALU.add),
                        reads=[pob, self.XB[dc][tc]], writes=[self.XB[dc][tc]])


    def load_w(self, tile_ap, dram_ap, buf):
        self.S.dma(tile_ap, dram_ap.rearrange("(k p) n -> p k n", p=128), writes=[buf], queue="pool")

    def dump_feat(self, name, tile, nchunks, buf_list):
        o = self.dout(name, [nchunks * 128, S_LEN])
        for c in range(nchunks):
            self.S.dma(o[c * 128:(c + 1) * 128, :], tile[:, c, :], reads=buf_list, queue="pool")


    def rwkv_branch_seq(self, w_rwkv, w2_d, a2_d, g2_d, gng_d, gnb_d):
        S = self.S
        CN = COLS
        NT = S_LEN // 128
        with ExitStack() as st4:
            old, self.st = self.st, st4
            WR = self.sb("WR", [128, NCH, 1792], BF16); WRB = Buf()
            W2 = self.sb("W2A2", [128, 512], F32); A2 = W2; G2 = self.sb("G2", [128, 512], F32)
            GNG = self.sb("GNG", [128, 512], F32); GNB = self.sb("GNB", [128, 512], F32)
            BO = self.sb("BO", [128, 128], F32); BOb = self.sb("BOb", [128, 128], BF16)
            ID2 = self.sb("ID2", [128, 64], BF16)
            OMK = self.sb("OMK", [128, 4], F32)
            cb = Buf()
            self.load_w(WR[:], w_rwkv, WRB)
            S.dma(W2[0:64, :], w2_d, writes=[cb]); S.dma(A2[64:128, :], a2_d, writes=[cb]); S.dma(G2[:], g2_d, writes=[cb])
            S.dma(GNG[:], gng_d, writes=[cb]); S.dma(GNB[:], gnb_d, writes=[cb])
            S.op("dve", lambda e: e.memset(BO[:], 0.0), reads=[cb], writes=[cb])
            S.op("dve", lambda e: e.memset(BO[0:64, 0:64], 1.0), reads=[cb], writes=[cb])
            S.op("dve", lambda e: e.memset(BO[64:128, 64:128], 1.0), reads=[cb], writes=[cb])
            S.op("dve", lambda e: e.tensor_copy(BOb[:], BO[:]), reads=[cb], writes=[cb])
            S.op("dve", lambda e: e.tensor_copy(ID2[0:64, :], self.ident_f[0:64, 0:64]), reads=[cb, self.constb], writes=[cb])
            S.op("dve", lambda e: e.tensor_copy(ID2[64:128, :], self.ident_f[64:128, 64:128]), reads=[cb, self.constb], writes=[cb])
            ka0 = CN["k_a"][0]
            S.op("dve", lambda e: e.tensor_scalar(OMK[:], self.cols[:, ka0:ka0 + 4], -1.0, 1.0, ALU.mult, ALU.add),
                 reads=[cb, self.constb], writes=[cb])
            P32 = self.sb("P32", [128, 14, 129], F32); P32B = Buf()
            DD = self.sb("DD", [128, 128], F32); DDB = Buf()
            CAR = self.sb("CAR", [128, 14, 1], F32)
            PL = P32[:, :, 1:129]; PLB = P32B
            TW = self.sb("TW", [64, 128], F32); SGg = self.sb("SGg", [128, 128], F32)
            WD = self.sb("WD", [128, 4, 128], F32); SIG = WD
            A32 = self.sb("A32", [128, 4, 128], F32)
            KK = self.sb("KK", [128, 4, 128], F32); SQ = self.sb("SQ", [128, 4, 128], F32)
            KKN = self.sb("KKN", [128, 4, 128], F32); NB = self.sb("NB", [128, 4, 128], F32)
            KM = self.sb("KM", [128, 4, 128], F32); BON = self.sb("BON", [128, 4, 128], F32)
            RM = self.sb("RM", [128, 4, 128, 2], BF16)
            VDr = Ring([self.sb("VD%d" % i, [128, 4, 64], BF16) for i in range(2)])
            H = self.sb("H", [128, 4, 64], F32); Hb = self.sb("Hb", [128, 4, 64], BF16); HK = self.sb("HK", [128, 4, 64], BF16)
            T1 = self.sb("T1", [128, 4, 64], F32); T2r = Ring([self.sb("T2_%d" % i, [128, 4, 64], F32) for i in range(2)])
            YST = [self.sb("YST%d" % i, [2, 4, 256], F32) for i in range(2)]; YSTB = [Buf(), Buf()]
            YTOK = A32[:].rearrange("p c t -> p (c t)").rearrange("p (c h v) -> p c h v", c=4, h=2); YTOKB = Buf()
            YC = KKN[:].rearrange("p c t -> p (c t)").rearrange("p (a v) -> p a v", a=8)
            ST8 = self.sb("ST8", [128, 8], F32); ST8b = self.sb("ST8b", [128, 8], F32)
            YF = SQ
            db = Buf(); hb = Buf(); hbb = Buf(); hkb = Buf(); t1b = Buf(); vrb = Buf(); vtb = Buf(); rmb = Buf(); yb = Buf()
            S.op("pool", lambda e: e.memset(P32[:], 0.0), writes=[P32B])
            S.op("pool", lambda e: e.memset(RM[:], 0.0), writes=[rmb])
            S.op("pool", lambda e: e.memset(H[:], 0.0), writes=[hb])
            mu0 = CN["mu"][0]; w00 = CN["w0"][0]; a00 = CN["a0"][0]; kk0 = CN["k_k"][0]; rk0 = CN["r_k"][0]
            ident = self.ident_f
            for i in range(NT):
                t0 = i * 128
                tcix = t0 // TC
                tsl = slice(t0, t0 + 128)
                hreads = [self.HNB[c][tcix] for c in range(NCH)]
                for cg in range(4):
                    c0 = cg * 4
                    n = min(4, 14 - c0)
                    p, pb = self.psum.get()

                    def mm(e):
                        for cc in range(n):
                            for k in range(NCH):
                                ins = e.matmul(p[:, cc * 128:(cc + 1) * 128], WR[:, k, (c0 + cc) * 128:(c0 + cc + 1) * 128],
                                               self.HN[:, k, tsl], start=(k == 0), stop=(k == NCH - 1))
                        return ins
                    S.op("pe", mm, reads=hreads + [WRB], writes=[pb])
                    S.op("act", lambda e: e.copy(P32[:, c0:c0 + n, 1:129], p[:, 0:n * 128].rearrange("p (c t) -> p c t", c=n)),
                         reads=[pb], writes=[P32B])
                S.op("dve", lambda e: e.tensor_copy(CAR[:], P32[:, :, 128:129]), reads=[P32B], writes=[DDB])
                for c in range(14):
                    S.op("dve", lambda e: e.tensor_tensor(DD[:], P32[:, c, 0:128], P32[:, c, 1:129], ALU.subtract), reads=[P32B, DDB], writes=[DDB])
                    S.op("dve", lambda e: e.scalar_tensor_tensor(P32[:, c, 1:129], DD[:], self.cols[:, mu0 + c:mu0 + c + 1], P32[:, c, 1:129],
                                                                 ALU.mult, ALU.add), reads=[DDB, P32B, self.constb], writes=[P32B])
                S.op("dve", lambda e: e.tensor_copy(P32[:, :, 0:1], CAR[:]), reads=[P32B, DDB], writes=[P32B])
                S.op("act", lambda e: e.activation(TW[:], PL[0:64, 12, :], AF.Tanh), reads=[PLB], writes=[db])
                S.op("act", lambda e: e.activation(SGg[:], PL[:, 13, :], AF.Sigmoid), reads=[PLB], writes=[db])
                pz, pzb = self.psum.get(); pa, pab = self.psum.get()

                def mmz(e):
                    for fc in range(4):
                        ins = e.matmul(pz[:, fc * 128:(fc + 1) * 128], W2[0:64, fc * 128:(fc + 1) * 128], TW[:], start=True, stop=True)
                    return ins

                def mma(e):
                    for fc in range(4):
                        ins = e.matmul(pa[:, fc * 128:(fc + 1) * 128], A2[64:128, fc * 128:(fc + 1) * 128], PL[64:128, 12, :], start=True, stop=True)
                    return ins

                S.op("pe", mmz, reads=[db, cb], writes=[pzb])
                S.op("pe", mma, reads=[PLB, cb], writes=[pab])
                for fc in range(4):
                    S.op("act", lambda e: e.activation(SIG[:, fc, :], pz[:, fc * 128:(fc + 1) * 128], AF.Sigmoid,
                                                       bias=self.cols[:, w00 + fc:w00 + fc + 1]), reads=[pzb, self.constb], writes=[db])
                    S.op("act", lambda e: e.activation(A32[:, fc, :], pa[:, fc * 128:(fc + 1) * 128], AF.Sigmoid,
                                                       bias=self.cols[:, a00 + fc:a00 + fc + 1]), reads=[pab, self.constb], writes=[db, YTOKB])
                S.op("act", lambda e: e.activation(WD[:], SIG[:], AF.Exp, scale=-0.6065306597126334), reads=[db], writes=[db])
                for fc in range(4):
                    S.op("dve", lambda e: e.tensor_scalar(KK[:, fc, :], PL[:, 4 + fc, :], self.cols[:, kk0 + fc:kk0 + fc + 1], None, ALU.mult),
                         reads=[PLB, self.constb], writes=[db])
                S.op("dve", lambda e: e.tensor_tensor(SQ[:], KK[:], KK[:], ALU.mult), reads=[db], writes=[db])
                pss, pssb = self.psum.get()
                S.op("pe", lambda e: e.matmul(pss[:], BO[:], SQ[:].rearrange("p c t -> p (c t)"), start=True, stop=True), reads=[db, cb], writes=[pssb])
                S.op("act", lambda e: e.activation(SQ[:], pss[:].rearrange("p (c t) -> p c t", c=4), AF.Sqrt), reads=[pssb, db], writes=[db])
                S.op("dve", lambda e: e.tensor_scalar(SQ[:], SQ[:], 1e-12, None, ALU.max), reads=[db], writes=[db])
                S.op("dve", lambda e: e.reciprocal(SQ[:], SQ[:]), reads=[db], writes=[db])
                S.op("dve", lambda e: e.tensor_tensor(KKN[:], KK[:], SQ[:], ALU.mult), reads=[db], writes=[db, yb])
                S.op("dve", lambda e: e.scalar_tensor_tensor(NB[:], KKN[:], -1.0, A32[:], ALU.mult, ALU.mult), reads=[db], writes=[db])
                for fc in range(4):
                    S.op("dve", lambda e: e.tensor_scalar(KK[:, fc, :], A32[:, fc, :], self.cols[:, ka0 + fc:ka0 + fc + 1], OMK[:, fc:fc + 1],
                                                          ALU.mult, ALU.add), reads=[db, cb, self.constb], writes=[db])
                S.op("dve", lambda e: e.tensor_tensor(KM[:], PL[:, 4:8, :], KK[:], ALU.mult), reads=[db, PLB], writes=[db])
                S.op("dve", lambda e: e.tensor_tensor(SQ[:], PL[:, 0:4, :], KM[:], ALU.mult), reads=[db, PLB], writes=[db])
                for fc in range(4):
                    S.op("dve", lambda e: e.tensor_scalar(SQ[:, fc, :], SQ[:, fc, :], self.cols[:, rk0 + fc:rk0 + fc + 1], None, ALU.mult),
                         reads=[db, self.constb], writes=[db])
                pbn, pbnb = self.psum.get()
                S.op("pe", lambda e: e.matmul(pbn[:], BO[:], SQ[:].rearrange("p c t -> p (c t)"), start=True, stop=True), reads=[db, cb], writes=[pbnb])
                S.op("dve", lambda e: e.tensor_tensor(BON[:], pbn[:].rearrange("p (c t) -> p c t", c=4), PL[:, 8:12, :], ALU.mult),
                     reads=[pbnb, PLB], writes=[db])
                S.op("dve", lambda e: e.tensor_copy(RM[0:64, :, :, 0], PL[0:64, 0:4, :]), reads=[PLB, rmb], writes=[rmb])
                S.op("dve", lambda e: e.tensor_copy(RM[64:128, :, :, 1], PL[64:128, 0:4, :]), reads=[PLB, rmb], writes=[rmb])
                for tt in range(128):
                    pvb_t, pvbb = self.psum.get()

                    VD, vdb = VDr.get()
                    S.op("pool", lambda e: e.tensor_tensor(VD[:], ID2[:].unsqueeze(1).to_broadcast([128, 4, 64]),
                                                           PL[:, 8:12, tt:tt + 1].to_broadcast([128, 4, 64]), ALU.mult),
                         reads=[PLB, cb], writes=[vdb])
                    S.op("pe", lambda e: e.matmul(pvb_t[:, 0:256], BOb[:], VD[:].rearrange("p c v -> p (c v)"), start=True, stop=True),
                         reads=[vdb, cb], writes=[pvbb])
                    T2, t2b = T2r.get()
                    S.op("pool" if False else "dve", lambda e: e.tensor_tensor(
                        T2[:], pvb_t[:, 0:256].rearrange("p (c v) -> p c v", c=4), KM[:, :, tt:tt + 1].to_broadcast([128, 4, 64]), ALU.mult),
                        reads=[pvbb, db], writes=[t2b])
                    S.op("dve", lambda e: e.tensor_tensor(HK[:], H[:], KKN[:, :, tt:tt + 1].to_broadcast([128, 4, 64]), ALU.mult),
                         reads=[hb, db], writes=[hkb])
                    psa, psab = self.psum.get()
                    S.op("pe", lambda e: e.matmul(psa[:, 0:256], BOb[:], HK[:].rearrange("p c v -> p (c v)"), start=True, stop=True),
                         reads=[hkb, cb], writes=[psab])
                    S.op("dve", lambda e: e.tensor_tensor(T1[:], psa[:, 0:256].rearrange("p (c v) -> p c v", c=4),
                                                          NB[:, :, tt:tt + 1].to_broadcast([128, 4, 64]), ALU.mult),
                         reads=[psab, db], writes=[t1b])
                    S.op("dve", lambda e: e.tensor_tensor(H[:], H[:], WD[:, :, tt:tt + 1].to_broadcast([128, 4, 64]), ALU.mult),
                         reads=[hb, db], writes=[hb])
                    S.op("dve", lambda e: e.tensor_tensor(T1[:], T1[:], T2[:], ALU.add), reads=[t1b, t2b], writes=[t1b])
                    S.op("dve", lambda e: e.tensor_tensor(H[:], H[:], T1[:], ALU.add), reads=[hb, t1b], writes=[hb])
                    S.op("act", lambda e: e.copy(Hb[:], H[:]), reads=[hb], writes=[hbb])
                    py, pyb = self.psum.get()

                    def mmy(e):
                        for fc in range(4):
                            ins = e.matmul(py[0:2, fc * 64:(fc + 1) * 64], RM[:, fc, tt, :], Hb[:, fc, :], start=True, stop=True)
                        return ins
                    S.op("pe", mmy, reads=[hbb, rmb], writes=[pyb])
                    slot = tt % 2
                    S.op("act", lambda e: e.copy(YST[slot][0:2, 0, :], py[0:2, 0:256]), reads=[pyb], writes=[YSTB[slot]])
                    for hp in range(2):
                        S.dma(YTOK[tt:tt + 1, :, hp, :], YST[slot][hp:hp + 1, 0, :].rearrange("p (c v) -> p c v", c=4),
                              reads=[YSTB[slot], db], writes=[YTOKB])
                YT8 = YTOK.rearrange("t c h v -> t (c h) v")
                S.op("dve", lambda e: e.tensor_reduce(ST8[:], YT8, AX.X, ALU.add), reads=[YTOKB], writes=[yb])
                S.op("dve", lambda e: e.tensor_scalar(ST8[:], ST8[:], 1.0 / 64, None, ALU.mult), reads=[yb], writes=[yb])
                S.op("dve", lambda e: e.tensor_tensor(YC, YT8, ST8[:].unsqueeze(2).to_broadcast([128, 8, 64]), ALU.subtract),
                     reads=[YTOKB, yb], writes=[yb, db])
                S.op("dve", lambda e: e.tensor_tensor(YTOK.rearrange("t c h v -> t (c h) v"), YC, YC, ALU.mult), reads=[yb, YTOKB], writes=[YTOKB])
                S.op("dve", lambda e: e.tensor_reduce(ST8b[:], YT8, AX.X, ALU.add), reads=[YTOKB], writes=[yb])
                S.op("act", lambda e: e.activation(ST8b[:], ST8b[:], AF.Sqrt, bias=self.gneps_t[:], scale=1.0 / 64), reads=[yb, self.constb], writes=[yb])
                S.op("dve", lambda e: e.reciprocal(ST8b[:], ST8b[:]), reads=[yb], writes=[yb])
                S.op("dve", lambda e: e.tensor_tensor(YC, YC, ST8b[:].unsqueeze(2).to_broadcast([128, 8, 64]), ALU.mult), reads=[yb], writes=[yb])
                YCf = YC.rearrange("t a v -> t (a v)")
                S.op("dve", lambda e: e.tensor_tensor(YCf, YCf, GNG[:], ALU.mult), reads=[yb, cb], writes=[yb])
                S.op("dve", lambda e: e.tensor_tensor(YCf, YCf, GNB[:], ALU.add), reads=[yb, cb], writes=[yb])
                pyt, pytb = self.psum.get(); pg, pgb = self.psum.get()

                def mmt2(e):
                    for fc in range(4):
                        ins = e.transpose(pyt[:, fc * 128:(fc + 1) * 128], YC[:, 2 * fc:2 * fc + 2, :].rearrange("t a v -> t (a v)"), ident[:])
                    for fc in range(4):
                        ins = e.matmul(pg[:, fc * 128:(fc + 1) * 128], G2[:, fc * 128:(fc + 1) * 128], SGg[:], start=True, stop=True)
                    return ins
                S.op("pe", mmt2, reads=[yb, self.constb, db, cb], writes=[pytb, pgb])
                S.op("dve", lambda e: e.tensor_tensor(YF[:], pyt[:].rearrange("p (c t) -> p c t", c=4), BON[:], ALU.add), reads=[pytb, db], writes=[yb, db])
                S.op("dve", lambda e: e.tensor_tensor(self.Y[0][:, :, tsl], YF[:], pg[:].rearrange("p (c t) -> p c t", c=4), ALU.mult),
                     reads=[yb, db, pgb], writes=[self.YB[0][tcix]])
            S.full_barrier()
            self.st = old


    def rwkv_branch(self, w_rwkv, w2_d, a2_d, g2_d, gng_d, gnb_d, mk_d):
        S = self.S
        CN = COLS
        NT = S_LEN // 128
        CDEC = 0.6065306597126334
        with ExitStack() as st4:
            old, self.st = self.st, st4
            WR = self.sb("WR", [128, NCH, 1792], BF16); WRB = Buf()
            W2 = self.sb("W2A2", [128, 512], F32); A2 = W2; G2 = self.sb("G2", [128, 512], BF16)
            BO = self.sb("BO", [128, 128], F32)
            ID2 = self.sb("ID2", [128, 64], F32)
            OMK = self.sb("OMK", [128, 4], F32)
            MSK = self.sb("MSK", [128, 3, 128], BF16)
            ONE64 = self.sb("ONE64", [128, 64], F32)
            cb = Buf()
            self.load_w(WR[:], w_rwkv, WRB)
            S.dma(W2[0:64, :], w2_d, writes=[cb]); S.dma(A2[64:128, :], a2_d, writes=[cb]); S.dma(G2[:], g2_d, writes=[cb], queue="pool")
            S.dma(MSK[:], mk_d, writes=[cb], queue="pool")
            self.rmsnorm_to_hn("mix_norm")
            S.op("dve", lambda e: e.memset(BO[:], 0.0), reads=[cb], writes=[cb])
            S.op("dve", lambda e: e.memset(BO[0:64, 0:64], 1.0), reads=[cb], writes=[cb])
            S.op("dve", lambda e: e.memset(BO[64:128, 64:128], 1.0), reads=[cb], writes=[cb])
            S.op("dve", lambda e: e.memset(ONE64[:], 1.0), reads=[cb], writes=[cb])
            S.op("dve", lambda e: e.tensor_copy(ID2[0:64, :], self.ident_f[0:64, 0:64]), reads=[cb, self.constb], writes=[cb])
            S.op("dve", lambda e: e.tensor_copy(ID2[64:128, :], self.ident_f[64:128, 64:128]), reads=[cb, self.constb], writes=[cb])
            ka0 = CN["k_a"][0]
            S.op("dve", lambda e: e.tensor_scalar(OMK[:], self.cols[:, ka0:ka0 + 4], -1.0, 1.0, ALU.mult, ALU.add),
                 reads=[cb, self.constb], writes=[cb])
            P32 = self.sb("P32", [128, 14, 129], F32); P32B = Buf()
            DD = self.sb("DD", [128, 128], F32); DDB = Buf()
            CAR = self.sb("CAR", [128, 14, 1], F32)
            PL = P32[:, :, 1:129]; PLB = P32B
            TW = self.sb("TW", [64, 128], F32); SGg = self.sb("SGg", [128, 128], BF16)
            f32t = lambda n: self.sb(n, [128, 4, 128], F32)
            SIG = f32t("SIG"); CUM = f32t("CUM"); A32 = f32t("A32"); KK = f32t("KK"); SQ = f32t("SQ")
            KKN = f32t("KKN"); NB = f32t("NB"); KM = f32t("KM"); BON = f32t("BON")
            AH = self.sb("AH", [128, 4, 128], BF16); KH = self.sb("KH", [128, 4, 128], BF16)
            BR = self.sb("BR", [128, 4, 2, 128], BF16)
            AT = self.sb("AT", [128, 512], BF16); KTt = self.sb("KTt", [128, 512], BF16); VTOK = self.sb("VTOK", [128, 512], BF16)
            WB = self.sb("WB", [128, 8, 128], BF16); BU = self.sb("BU", [128, 8, 128], BF16)
            bf8 = lambda n: self.sb(n, [128, 8, 128], BF16)
            X0 = bf8("X0"); XT0 = bf8("XT0"); LKT = bf8("LKT"); GRA = bf8("GRA"); GRK = bf8("GRK"); TT = bf8("TT")
            XA1 = [self.sb("XA1_%d" % i, [128, 4, 128], BF16) for i in range(2)]
            XTA1 = [self.sb("XTA1_%d" % i, [128, 4, 128], BF16) for i in range(2)]
            TA1 = [self.sb("TA1_%d" % i, [128, 4, 128], BF16) for i in range(2)]
            RTm = self.sb("RTm", [128, 4, 2, 128], BF16)
            M0Ts = SIG[:].rearrange("p c t -> p (c t)").rearrange("p (a k) -> p a k", a=8)
            N0s = CUM[:].rearrange("p c t -> p (c t)").rearrange("p (a k) -> p a k", a=8)
            PCt = self.sb("PCt", [128, 2, 4], F32)
            H = self.sb("H", [128, 4, 64], F32); Hb = self.sb("Hb", [128, 2, 4, 64], BF16)
            nbb = Buf(); kmb = Buf(); sgb = Buf(); cub = Buf()
            YTOK = NB[:].rearrange("p c t -> p (c t)").rearrange("p (c h v) -> p c h v", c=4, h=2); YTOKB = nbb
            YC = KM[:].rearrange("p c t -> p (c t)").rearrange("p (a v) -> p a v", a=8)
            ST8 = self.sb("ST8", [128, 8], F32); ST8b = self.sb("ST8b", [128, 8], F32)
            YF = SQ
            db = Buf(); hb = Buf(); hbb = Buf(); gb_ = Buf(); chb = Buf(); tkb = Buf(); yb = Buf(); mnb = Buf(); rtb = Buf()
            S.op("pool", lambda e: e.memset(P32[:], 0.0), writes=[P32B])
            S.op("pool", lambda e: e.memset(RTm[:], 0.0), writes=[rtb])
            S.op("pool", lambda e: e.memset(H[:], 0.0), writes=[hb])
            mu0 = CN["mu"][0]; w00 = CN["w0"][0]; a00 = CN["a0"][0]; kk0 = CN["k_k"][0]; rk0 = CN["r_k"][0]
            gg0 = CN["gn_g"][0]; gb0 = CN["gn_b"][0]
            ident = self.ident_f
            c4 = lambda ap: ap.rearrange("p (c t) -> p c t", c=4)
            def emit_proj(i2):
                t0_ = i2 * 128
                tsl_ = slice(t0_, t0_ + 128)
                hreads_ = [self.HNB[c][t0_ // TC] for c in range(NCH)]
                for cg in range(4):
                    c0 = cg * 4
                    n = min(4, 14 - c0)
                    p, pb = self.psum.get()

                    def mm(e):
                        for cc in range(n):
                            for k in range(NCH):
                                ins = e.matmul(p[:, cc * 128:(cc + 1) * 128], WR[:, k, (c0 + cc) * 128:(c0 + cc + 1) * 128],
                                               self.HN[:, k, tsl_], start=(k == 0), stop=(k == NCH - 1))
                        return ins
                    S.op("pe", mm, reads=hreads_ + [WRB], writes=[pb])
                    S.op("act", lambda e: e.copy(P32[:, c0:c0 + n, 1:129], p[:, 0:n * 128].rearrange("p (c t) -> p c t", c=n)),
                         reads=[pb], writes=[P32B])

            def lerp_list(i2):
                ops = []
                ops.append(lambda: S.op("pool", lambda e: e.tensor_copy(CAR[:], P32[:, :, 128:129]), reads=[P32B], writes=[DDB]))
                for c in range(14):
                    def one(c=c):
                        S.op("pool", lambda e: e.tensor_tensor(DD[:], P32[:, c, 0:128], P32[:, c, 1:129], ALU.subtract), reads=[P32B, DDB], writes=[DDB])
                        S.op("pool", lambda e: e.tensor_tensor(DD[:], DD[:], self.cols[:, mu0 + c:mu0 + c + 1].to_broadcast([128, 128]), ALU.mult),
                             reads=[DDB, self.constb], writes=[DDB])
                        S.op("pool", lambda e: e.tensor_tensor(P32[:, c, 1:129], P32[:, c, 1:129], DD[:], ALU.add), reads=[DDB, P32B], writes=[P32B])
                    ops.append(one)
                ops.append(lambda: S.op("pool", lambda e: e.tensor_copy(P32[:, :, 0:1], CAR[:]), reads=[P32B, DDB], writes=[P32B]))
                return ops

            pending = []
            for i in range(self._rk_tiles):
                t0 = i * 128
                tcix = t0 // TC
                tsl = slice(t0, t0 + 128)
                hreads = [self.HNB[c][tcix] for c in range(NCH)]
                if i == 0:
                    emit_proj(0)
                    for fn_ in lerp_list(0):
                        fn_()
                for fn_ in pending:
                    fn_()
                pending = []
                S.op("act", lambda e: e.activation(TW[:], PL[0:64, 12, :], AF.Tanh), reads=[PLB], writes=[db])
                S.op("act", lambda e: e.activation(SGg[:], PL[:, 13, :], AF.Sigmoid), reads=[PLB], writes=[db])
                pz, pzb = self.psum.get(); pa, pab = self.psum.get()

                def mmz(e):
                    for fc in range(4):
                        ins = e.matmul(pz[:, fc * 128:(fc + 1) * 128], W2[0:64, fc * 128:(fc + 1) * 128], TW[:], start=True, stop=True)
                    return ins

                def mma(e):
                    for fc in range(4):
                        ins = e.matmul(pa[:, fc * 128:(fc + 1) * 128], A2[64:128, fc * 128:(fc + 1) * 128], PL[64:128, 12, :], start=True, stop=True)
                    return ins
                S.op("pe", mmz, reads=[db, cb], writes=[pzb])
                S.op("pe", mma, reads=[PLB, cb], writes=[pab])
                for fc in range(4):
                    S.op("act", lambda e: e.activation(SIG[:, fc, :], pz[:, fc * 128:(fc + 1) * 128], AF.Sigmoid,
                                                       bias=self.cols[:, w00 + fc:w00 + fc + 1]), reads=[pzb, self.constb], writes=[db, sgb])
                    S.op("act", lambda e: e.activation(A32[:, fc, :], pa[:, fc * 128:(fc + 1) * 128], AF.Sigmoid,
                                                       bias=self.cols[:, a00 + fc:a00 + fc + 1]), reads=[pab, self.constb], writes=[db])
                bc4 = lambda c0_: self.cols[:, c0_:c0_ + 4].unsqueeze(2).to_broadcast([128, 4, 128])
                S.op("dve", lambda e: e.tensor_tensor(KK[:], PL[:, 4:8, :], bc4(kk0), ALU.mult), reads=[PLB, self.constb], writes=[db])
                S.op("dve", lambda e: e.tensor_tensor(SQ[:], KK[:], KK[:], ALU.mult), reads=[db], writes=[db])
                pss, pssb = self.psum.get()
                S.op("pe", lambda e: e.matmul(pss[:], BO[:], SQ[:].rearrange("p c t -> p (c t)"), start=True, stop=True), reads=[db, cb], writes=[pssb])
                S.op("act", lambda e: e.activation(SQ[:], c4(pss[:]), AF.Sqrt), reads=[pssb, db], writes=[db])
                S.op("dve", lambda e: e.tensor_scalar(SQ[:], SQ[:], 1e-12, None, ALU.max), reads=[db], writes=[db])
                S.op("dve", lambda e: e.reciprocal(SQ[:], SQ[:]), reads=[db], writes=[db])
                S.op("dve", lambda e: e.tensor_tensor(KKN[:], KK[:], SQ[:], ALU.mult), reads=[db], writes=[db])
                S.op("dve", lambda e: e.tensor_tensor(NB[:], KKN[:], A32[:], ALU.mult), reads=[db], writes=[db, nbb])
                S.op("pool", lambda e: e.tensor_tensor(KK[:], A32[:], bc4(ka0), ALU.mult), reads=[db, self.constb], writes=[db])
                S.op("pool", lambda e: e.tensor_tensor(KK[:], KK[:], OMK[:].unsqueeze(2).to_broadcast([128, 4, 128]), ALU.add), reads=[db, cb], writes=[db])
                S.op("dve", lambda e: e.tensor_tensor(KM[:], PL[:, 4:8, :], KK[:], ALU.mult), reads=[db, PLB], writes=[db, kmb])
                S.op("dve", lambda e: e.tensor_tensor(SQ[:], PL[:, 0:4, :], KM[:], ALU.mult), reads=[db, PLB, kmb], writes=[db])
                S.op("pool", lambda e: e.tensor_tensor(SQ[:], SQ[:], bc4(rk0), ALU.mult), reads=[db, self.constb], writes=[db])
                pbn, pbnb = self.psum.get()
                S.op("pe", lambda e: e.matmul(pbn[:], BO[:], SQ[:].rearrange("p c t -> p (c t)"), start=True, stop=True), reads=[db, cb], writes=[pbnb])
                S.op("dve", lambda e: e.tensor_tensor(BON[:], c4(pbn[:]), PL[:, 8:12, :], ALU.mult), reads=[pbnb, PLB], writes=[db])
                for fc in range(4):
                    for c2 in range(2):
                        cs = slice(c2 * 64, (c2 + 1) * 64)
                        S.op("dve", lambda e: e.tensor_tensor_scan(CUM[:, fc, cs], ONE64[:], SIG[:, fc, cs], 0.0, ALU.mult, ALU.add),
                             reads=[db, cb, sgb], writes=[db, cub])
                S.op("pool", lambda e: e.tensor_tensor(SQ[:], CUM[:], SIG[:], ALU.subtract), reads=[db, sgb, cub], writes=[db])
                S.op("act", lambda e: e.activation(A32[:], CUM[:], AF.Exp, scale=CDEC), reads=[db, cub], writes=[db])
                S.op("act", lambda e: e.activation(CUM[:], CUM[:], AF.Exp, scale=-CDEC), reads=[db], writes=[db, cub])
                S.op("act", lambda e: e.activation(SQ[:], SQ[:], AF.Exp, scale=-CDEC), reads=[db], writes=[db])
                S.op("dve", lambda e: e.tensor_copy(PCt[:, 0, :], CUM[:, :, 63]), reads=[db, chb, cub], writes=[chb])
                S.op("dve", lambda e: e.tensor_copy(PCt[:, 1, :], CUM[:, :, 127]), reads=[db, chb, cub], writes=[chb])
                S.op("dve", lambda e: e.tensor_tensor(NB[:], NB[:], A32[:], ALU.mult), reads=[db], writes=[db, nbb])
                S.op("dve", lambda e: e.tensor_tensor(KM[:], KM[:], A32[:], ALU.mult), reads=[db], writes=[db, kmb])
                S.op("dve", lambda e: e.tensor_tensor(KKN[:], KKN[:], SQ[:], ALU.mult), reads=[db], writes=[db])
                S.op("dve", lambda e: e.tensor_tensor(KK[:], PL[:, 0:4, :], CUM[:], ALU.mult), reads=[db, PLB, cub], writes=[db])
                S.op("act", lambda e: e.copy(AH[:], NB[:]), reads=[db, gb_, nbb], writes=[gb_])
                S.op("act", lambda e: e.copy(KH[:], KM[:]), reads=[db, gb_, kmb], writes=[gb_])
                S.op("pool", lambda e: e.tensor_copy(BR[:, :, 0, :], KKN[:]), reads=[db, gb_], writes=[gb_])
                S.op("pool", lambda e: e.tensor_copy(BR[:, :, 1, :], KK[:]), reads=[db, gb_], writes=[gb_])
                if "dumpah" in self.debug and i == 0:
                    for nm, tl in (("ah", AH), ("kh", KH), ("br", BR)):
                        o_ = self.dout("dbg_" + nm, [128, tl[:].rearrange("p ... -> p (...)").shape[1] if False else (512 if nm != "br" else 1024)])
                        S.dma(o_, tl[:].rearrange("p c t -> p (c t)") if nm != "br" else tl[:].rearrange("p c a t -> p (c a t)"), reads=[gb_], queue="pool")
                    for nm, tl in (("nb", NB), ("km", KM), ("kkn", KKN), ("en", A32), ("ep", CUM)):
                        o_ = self.dout("dbg_" + nm, [128, 512])
                        S.dma(o_, tl[:].rearrange("p c t -> p (c t)"), reads=[db, nbb, kmb, cub])
                for src, dst_fn in ((NB, None), (KM, None), (KKN, None), (None, None)):
                    pass
                tr_jobs = [(lambda fc: NB[:, fc, :], "AT"), (lambda fc: KM[:, fc, :], "KT"),
                           (lambda fc: KKN[:, fc, :], "BT"), (lambda fc: PL[:, 8 + fc, :], "VT")]
                for srcf, kind in tr_jobs:
                    ptr, ptrb = self.psum.get()

                    def mmt(e):
                        for fc in range(4):
                            ins = e.transpose(ptr[:, fc * 128:(fc + 1) * 128], srcf(fc), ident[:])
                        return ins
                    S.op("pe", mmt, reads=[db, PLB, self.constb, nbb, kmb], writes=[ptrb])
                    if kind == "AT":
                        S.op("act", lambda e: e.copy(AT[:], ptr[:]), reads=[ptrb, tkb], writes=[tkb])
                    elif kind == "KT":
                        S.op("dve", lambda e: e.tensor_copy(KTt[:], ptr[:]), reads=[ptrb, tkb], writes=[tkb])
                    elif kind == "BT":
                        S.op("act", lambda e: e.activation(WB[:, :, 0:64], ptr[:].rearrange("p (h k) -> p h k", h=8), AF.Copy, scale=-1.0),
                             reads=[ptrb, tkb], writes=[tkb])
                    else:
                        S.op("dve", lambda e: e.tensor_copy(VTOK[:], ptr[:]), reads=[ptrb, tkb], writes=[tkb])
                if self._rk_stage <= 0:
                    continue
                for fc in range(4):
                    ka_, ab0, ab1 = self.psum.get_pair_idx()
                    kb_, bb0, bb1 = self.psum.get_pair_idx()
                    PA = self.PS[:, ka_:ka_ + 2, :]; PB = self.PS[:, kb_:kb_ + 2, :]

                    def mmg(e):
                        for h2 in range(2):
                            rs = slice(h2 * 64, (h2 + 1) * 64)
                            brr = BR[rs, fc, :, :].rearrange("p a t -> p (a t)")
                            e.matmul(PA[:, h2, 0:256], AH[rs, fc, :], brr, start=True, stop=True)
                            e.matmul(PA[:, h2, 256:512], KH[rs, fc, :], brr, start=True, stop=True)
                            ins = e.matmul(PB[:, h2, 0:128], BR[rs, fc, 0, :], AH[rs, fc, :], start=True, stop=True)
                        return ins
                    S.op("pe", mmg, reads=[gb_], writes=[ab0, ab1, bb0, bb1])
                    hs = slice(2 * fc, 2 * fc + 2)
                    PAv = PA.rearrange("p h (q b t) -> p h q b t", q=2, b=2)
                    mk = lambda j: MSK[:, j, :].unsqueeze(1).to_broadcast([128, 2, 128])
                    S.op("dve", lambda e: e.tensor_tensor(X0[:, hs, :], PAv[:, :, 0, 0, :], mk(0), ALU.mult), reads=[ab0, ab1, cb, mnb], writes=[mnb])
                    S.op("dve", lambda e: e.tensor_tensor(GRA[:, hs, :], PAv[:, :, 0, 1, :], mk(2), ALU.mult), reads=[ab0, ab1, cb, mnb], writes=[mnb])
                    S.op("dve", lambda e: e.tensor_tensor(LKT[:, hs, :], PAv[:, :, 1, 0, :], mk(0), ALU.mult), reads=[ab0, ab1, cb, mnb], writes=[mnb])
                    S.op("dve", lambda e: e.tensor_tensor(GRK[:, hs, :], PAv[:, :, 1, 1, :], mk(2), ALU.mult), reads=[ab0, ab1, cb, mnb], writes=[mnb])
                    S.op("dve", lambda e: e.tensor_tensor(XT0[:, hs, :], PB[:, :, 0:128], mk(1), ALU.mult), reads=[bb0, bb1, cb, mnb], writes=[mnb])
                if self._rk_stage <= 1:
                    continue
                if i + 1 < self._rk_tiles:
                    emit_proj(i + 1)
                    pending = lerp_list(i + 1)
                hst = []
                for half in range(2):
                    h0 = half * 4
                    st_ = dict(xb=Buf(), xtb=Buf(), tb=Buf(),
                               xbufs=[X0[:, h0:h0 + 4, :], XA1[half][:]], xtbufs=[XT0[:, h0:h0 + 4, :], XTA1[half][:]],
                               tbufs=[TA1[half][:], TT[:, h0:h0 + 4, :]])
                    hst.append(st_)
                    S.op("pool", lambda e: e.tensor_tensor(st_["tbufs"][0], st_["xbufs"][0], ident[:].unsqueeze(1).to_broadcast([128, 4, 128]), ALU.add),
                         reads=[mnb, self.constb, st_["tb"]], writes=[st_["tb"]])
                for lv in range(1, 6):
                    for half in range(2):
                        st_ = hst[half]
                        xb_, xtb_, tb_ = st_["xb"], st_["xtb"], st_["tb"]
                        Xp, XTp, Tp = st_["xbufs"][(lv - 1) % 2], st_["xtbufs"][(lv - 1) % 2], st_["tbufs"][(lv - 1) % 2]
                        Xn, XTn, Tn = st_["xbufs"][lv % 2], st_["xtbufs"][lv % 2], st_["tbufs"][lv % 2]
                        pxt, pxtb = self.psum.get()

                        def mmxt(e):
                            for j in range(4):
                                ins = e.matmul(pxt[:, j * 128:(j + 1) * 128], Xp[:, j, :], XTp[:, j, :], start=True, stop=True)
                            return ins
                        S.op("pe", mmxt, reads=[mnb, xb_, xtb_], writes=[pxtb])
                        if lv < 5:
                            px, pxb = self.psum.get()

                            def mmx(e):
                                for j in range(4):
                                    ins = e.matmul(px[:, j * 128:(j + 1) * 128], XTp[:, j, :], Xp[:, j, :], start=True, stop=True)
                                return ins
                            S.op("pe", mmx, reads=[mnb, xb_, xtb_], writes=[pxb])
                        S.op("act", lambda e: e.copy(XTn, c4(pxt[:])), reads=[pxtb, xtb_, mnb], writes=[xtb_])
                        if lv < 5:
                            S.op("act", lambda e: e.copy(Xn, c4(px[:])), reads=[pxb, xb_, mnb], writes=[xb_])
                        ptt, pttb = self.psum.get()

                        def mmtt(e):
                            for j in range(4):
                                ins = e.matmul(ptt[:, j * 128:(j + 1) * 128], XTn[:, j, :], Tp[:, j, :], start=True, stop=True)
                            return ins
                        S.op("pe", mmtt, reads=[xtb_, tb_], writes=[pttb])
                        S.op("dve", lambda e: e.tensor_tensor(Tn, c4(ptt[:]), Tp, ALU.add), reads=[pttb, tb_, mnb], writes=[tb_] + ([mnb] if lv == 5 else []))
                        for _ in range(2):
                            if pending:
                                pending.pop(0)()
                if self._rk_stage <= 2:
                    continue
                plk, plkb = self.psum.get()

                def mmlk(e):
                    for h in range(8):
                        ins = e.matmul(plk[:, h * 64:(h + 1) * 64], LKT[:, h, :], VTOK[:, h * 64:(h + 1) * 64], start=True, stop=True)
                    return ins
                S.op("pe", mmlk, reads=[mnb, tkb], writes=[plkb])
                S.op("act", lambda e: e.copy(WB[:, :, 64:128], plk[:].rearrange("p (h v) -> p h v", h=8)), reads=[plkb, tkb], writes=[tkb])
                for half in range(2):
                    pbu, pbub = self.psum.get()

                    def mmbu(e):
                        for j in range(4):
                            h = half * 4 + j
                            ins = e.matmul(pbu[:, j * 128:(j + 1) * 128], TT[:, h, :], WB[:, h, :], start=True, stop=True)
                        return ins
                    S.op("pe", mmbu, reads=[mnb, tkb], writes=[pbub])
                    S.op("act", lambda e: e.copy(BU[:, half * 4:half * 4 + 4, :], c4(pbu[:])), reads=[pbub, chb], writes=[chb])
                if self._rk_stage <= 3:
                    continue
                prt, prtb = self.psum.get()
                km_, mb0, mb1 = self.psum.get_pair_idx()
                PM = self.PS[:, km_:km_ + 2, :]

                def mmrt(e):
                    for h in range(8):
                        rs = slice((h % 2) * 64, (h % 2) * 64 + 64)
                        fc = h // 2
                        ins = e.matmul(prt[rs, fc * 128:(fc + 1) * 128], BU[:, h, 0:64], GRA[:, h, :], start=True, stop=True)
                    return ins

                def mmmn(e):
                    for c2 in range(2):
                        cr = slice(c2 * 64, (c2 + 1) * 64)
                        for h in range(8):
                            rs = slice((h % 2) * 64, (h % 2) * 64 + 64)
                            fc = h // 2
                            o = fc * 64
                            e.matmul(PM[rs, c2, o:o + 64], BU[cr, h, 0:64], AT[cr, h * 64:(h + 1) * 64], start=True, stop=True)
                            e.matmul(PM[rs, c2, 256 + o:256 + o + 64], AT[cr, h * 64:(h + 1) * 64], BU[cr, h, 64:128], start=True, stop=False)
                            ins = e.matmul(PM[rs, c2, 256 + o:256 + o + 64], KTt[cr, h * 64:(h + 1) * 64], VTOK[cr, h * 64:(h + 1) * 64], start=False, stop=True)
                    return ins
                S.op("pe", mmrt, reads=[chb, mnb], writes=[prtb])
                S.op("pe", mmmn, reads=[chb, tkb], writes=[mb0, mb1])
                prv = c4(prt[:])
                S.op("dve", lambda e: e.tensor_tensor(RTm[:, :, 0, 0:64], prv[:, :, 0:64], KK[:, :, 0:64], ALU.add), reads=[prtb, db, rtb], writes=[rtb])
                S.op("dve", lambda e: e.tensor_tensor(RTm[:, :, 1, 64:128], prv[:, :, 64:128], KK[:, :, 64:128], ALU.add), reads=[prtb, db, rtb], writes=[rtb])
                M0v = M0Ts.rearrange("p (a c) k -> p a c k", a=2)
                N0v = N0s.rearrange("p (a c) k -> p a c k", a=2)
                S.op("dve", lambda e: e.tensor_tensor(M0v, PM[:, :, 0:256].rearrange("p a (c k) -> p a c k", c=4),
                                                      ID2[:].unsqueeze(1).unsqueeze(1).to_broadcast([128, 2, 4, 64]), ALU.add),
                     reads=[mb0, mb1, cb, chb], writes=[chb, sgb])
                S.op("act", lambda e: e.copy(N0v, PM[:, :, 256:512].rearrange("p a (c k) -> p a c k", c=4)), reads=[mb0, mb1, chb], writes=[chb, cub])
                if self._rk_stage <= 4:
                    continue
                for c2 in range(2):
                    S.op("act", lambda e: e.copy(Hb[:, c2, :, :], H[:]), reads=[hb, hbb], writes=[hbb])
                    phe, pheb = self.psum.get(); pho, phob = self.psum.get()

                    def mmh(e):
                        for par, bank in ((0, phe), (1, pho)):
                            rs = slice(par * 64, par * 64 + 64)
                            for fc in range(4):
                                ins = e.matmul(bank[rs, fc * 64:(fc + 1) * 64], M0Ts[rs, c2 * 4 + fc, :], H[rs, fc, :], start=True, stop=True)
                        return ins
                    S.op("pe", mmh, reads=[chb, hb, sgb], writes=[pheb, phob])
                    S.op("dve", lambda e: e.tensor_tensor(H[0:64], phe[0:64, 0:256].rearrange("p (c v) -> p c v", c=4), N0s[0:64, c2 * 4:c2 * 4 + 4, :], ALU.add),
                         reads=[pheb, chb, cub, hb], writes=[hb])
                    S.op("dve", lambda e: e.tensor_tensor(H[64:128], pho[64:128, 0:256].rearrange("p (c v) -> p c v", c=4), N0s[64:128, c2 * 4:c2 * 4 + 4, :], ALU.add),
                         reads=[phob, chb, cub, hb], writes=[hb])
                    S.op("dve", lambda e: e.tensor_tensor(H[:], H[:], PCt[:, c2, :].unsqueeze(2).to_broadcast([128, 4, 64]), ALU.mult),
                         reads=[chb, hb], writes=[hb])
                if self._rk_stage <= 5:
                    continue
                ky_, yb0, yb1 = self.psum.get_pair_idx()
                PY = self.PS[:, ky_:ky_ + 2, :]

                def mmy(e):
                    for par in range(2):
                        rs = slice(par * 64, par * 64 + 64)
                        for fc in range(4):
                            h = 2 * fc + par
                            o = PY[:, par, fc * 64:(fc + 1) * 64]
                            e.matmul(o, GRA[:, h, :], BU[:, h, 64:128], start=True, stop=False)
                            e.matmul(o, GRK[:, h, :], VTOK[:, h * 64:(h + 1) * 64], start=False, stop=False)
                            e.matmul(o, RTm[rs, fc, 0, :], Hb[rs, 0, fc, :], start=False, stop=False)
                            ins = e.matmul(o, RTm[rs, fc, 1, :], Hb[rs, 1, fc, :], start=False, stop=True)
                    return ins
                S.op("pe", mmy, reads=[mnb, chb, tkb, rtb, hbb], writes=[yb0, yb1])
                S.op("act", lambda e: e.copy(YTOK.rearrange("t c h v -> t h c v"), PY[:, :, 0:256].rearrange("t h (c v) -> t h c v", c=4)),
                     reads=[yb0, yb1, YTOKB], writes=[YTOKB])
                if self._rk_stage <= 6:
                    continue
                YT8 = YTOK.rearrange("t c h v -> t (c h) v")
                S.op("dve", lambda e: e.tensor_reduce(ST8[:], YT8, AX.X, ALU.add), reads=[YTOKB], writes=[yb])
                S.op("dve", lambda e: e.tensor_scalar(ST8[:], ST8[:], 1.0 / 64, None, ALU.mult), reads=[yb], writes=[yb])
                S.op("dve", lambda e: e.tensor_tensor(YC, YT8, ST8[:].unsqueeze(2).to_broadcast([128, 8, 64]), ALU.subtract),
                     reads=[YTOKB, yb], writes=[yb, kmb])
                S.op("pool", lambda e: e.tensor_tensor(YT8, YC, YC, ALU.mult), reads=[yb, YTOKB, kmb], writes=[YTOKB])
                S.op("dve", lambda e: e.tensor_reduce(ST8b[:], YT8, AX.X, ALU.add), reads=[YTOKB], writes=[yb])
                S.op("act", lambda e: e.activation(ST8b[:], ST8b[:], AF.Sqrt, bias=self.gneps_t[:], scale=1.0 / 64), reads=[yb, self.constb], writes=[yb])
                S.op("dve", lambda e: e.reciprocal(ST8b[:], ST8b[:]), reads=[yb], writes=[yb])
                S.op("dve", lambda e: e.tensor_tensor(YC, YC, ST8b[:].unsqueeze(2).to_broadcast([128, 8, 64]), ALU.mult), reads=[yb], writes=[yb, kmb])
                pyt, pytb = self.psum.get(); pg, pgb = self.psum.get()

                def mmt2(e):
                    for fc in range(4):
                        ins = e.transpose(pyt[:, fc * 128:(fc + 1) * 128], YC[:, 2 * fc:2 * fc + 2, :].rearrange("t a v -> t (a v)"), ident[:])
                    for fc in range(4):
                        ins = e.matmul(pg[:, fc * 128:(fc + 1) * 128], G2[:, fc * 128:(fc + 1) * 128], SGg[:], start=True, stop=True)
                    return ins
                S.op("pe", mmt2, reads=[yb, self.constb, db, cb, kmb], writes=[pytb, pgb])
                for fc in range(4):
                    S.op("dve", lambda e: e.tensor_scalar(YF[:, fc, :], pyt[:, fc * 128:(fc + 1) * 128], self.cols[:, gg0 + fc:gg0 + fc + 1],
                                                          self.cols[:, gb0 + fc:gb0 + fc + 1], ALU.mult, ALU.add),
                         reads=[pytb, db, self.constb], writes=[db])
                S.op("pool", lambda e: e.tensor_tensor(YF[:], YF[:], BON[:], ALU.add), reads=[db], writes=[db])
                S.op("dve", lambda e: e.tensor_tensor(self.Y[0][:, :, tsl], YF[:], c4(pg[:]), ALU.mult),
                     reads=[db, pgb], writes=[self.YB[0][tcix]])
            S.full_barrier()
            self.st = old

    def nsa_branch(self, d):
        S = self.S
        NT = S_LEN // 128
        with ExitStack() as st4:
            old, self.st = self.st, st4
            cb = Buf()
            KT = self.sb("KT", [128, 2, S_LEN], BF16); KTB = Buf()
            VT = self.sb("VT", [128, NT, 256], BF16); VTB = Buf()
            KC = self.sb("KC", [128, 127], BF16); VC = self.sb("VC", [128, 128], BF16); kcb = Buf()
            BM = self.sb("BM", [128, 3, 2, 512], BF16)
            BVC = self.sb("BVC", [32, 2, 512], BF16)
            stB = ExitStack(); self.st = stB
            KCMP = self.sb("KCMP", [128, S_LEN], BF16); VCT = self.sb("VCT", [128, S_LEN], BF16)
            stB1 = ExitStack(); self.st = stB1
            WKV = self.sb("WKV", [128, NCH, 768], BF16); wkvb = Buf()
            self.load_w(WKV[:], d["w_kvn"], wkvb)
            stA = ExitStack(); self.st = stA
            G1 = self.sb("G1", [128, 2, 512], F32); G2_ = self.sb("G2b", [128, 2, 512], F32); MK = self.sb("MK", [128, 128], F32)
            gb = Buf()
            S.dma(G2_[:], d["t31"], writes=[gb])
            for kind in range(3):
                S.dma(G1[:], d["bmg"][kind], reads=[gb], writes=[gb])
                S.dma(MK[:], d["msk"][kind], reads=[gb], writes=[gb])
                S.op("dve", lambda e: e.tensor_tensor(G1[:], G1[:], G2_[:], ALU.subtract), reads=[gb], writes=[gb])
                S.op("dve", lambda e: e.tensor_tensor(BM[:, kind, :, :].rearrange("p g (j q) -> p (g j) q", j=4),
                                                      G1[:].rearrange("p g (j q) -> p (g j) q", j=4),
                                                      MK[:].unsqueeze(1).to_broadcast([128, 8, 128]), ALU.add), reads=[gb], writes=[cb, gb])
            S.dma(G1[0:32, :, :], d["bvcg"], reads=[gb], writes=[gb])
            S.dma(MK[0:32, :], d["mskc"], reads=[gb], writes=[gb])
            S.op("dve", lambda e: e.tensor_tensor(G1[0:32], G1[0:32], G2_[0:32], ALU.subtract), reads=[gb], writes=[gb])
            S.op("dve", lambda e: e.tensor_tensor(BVC[:].rearrange("p g (j q) -> p (g j) q", j=4),
                                                  G1[0:32].rearrange("p g (j q) -> p (g j) q", j=4),
                                                  MK[0:32, :].unsqueeze(1).to_broadcast([32, 8, 128]), ALU.add), reads=[gb], writes=[cb, gb])
            S.full_barrier()
            stA.close()
            self.st = stB1
            for tc in range(NTC):
                ts = slice(tc * TC, (tc + 1) * TC)
                hreads = [self.HNB[c][tc] for c in range(NCH)]
                for dst, col in ((KCMP[:, ts], 0), (VCT[:, ts], 128), (KT[:, 0, ts], 256), (KT[:, 1, ts], 512)):
                    p, pb = self.psum.get()

                    def mm(e):
                        for k in range(NCH):
                            ins = e.matmul(p[:], WKV[:, k, col:col + 128], self.HN[:, k, ts], start=(k == 0), stop=(k == NCH - 1))
                        return ins
                    S.op("pe", mm, reads=hreads + [wkvb], writes=[pb])
                    S.op("act", lambda e: e.copy(dst, p[:]), reads=[pb], writes=[KTB])
                for tl in range(4):
                    tile = tc * 4 + tl
                    tq = slice(tile * 128, (tile + 1) * 128)
                    p, pb = self.psum.get()

                    def mm(e):
                        for k in range(NCH):
                            e.matmul(p[:, 0:128], self.HN[:, k, tq], WKV[:, k, 384:512], start=(k == 0), stop=(k == NCH - 1))
                        for k in range(NCH):
                            ins = e.matmul(p[:, 128:256], self.HN[:, k, tq], WKV[:, k, 640:768], start=(k == 0), stop=(k == NCH - 1))
                        return ins
                    S.op("pe", mm, reads=hreads + [wkvb], writes=[pb])
                    S.op("dve", lambda e: e.tensor_copy(VT[:, tile, :], p[:, 0:256]), reads=[pb], writes=[VTB])
            S.full_barrier()
            stB1.close()
            stB2 = ExitStack(); self.st = stB2
            W1 = self.sb("W1", [128, 32, 256], BF16); PET = self.sb("PET", [128, 32], BF16)
            W2D = self.sb("W2D", [128, 2, 128], BF16); HID = self.sb("HID", [128, 2, 127], BF16)
            ZZ = self.sb("ZZ", [128, 127], F32); Z2 = self.sb("Z2", [128, 127], F32); BC = self.sb("BCc", [128, 1], F32)
            wb = Buf(); zb = Buf(); hb_ = Buf()
            for kv in range(2):
                w1d = d["cmp_w1"][kv].rearrange("(l dd) m -> dd l m", dd=64)
                S.dma(W1[0:64], w1d, writes=[wb], queue="pool"); S.dma(W1[64:128], w1d, writes=[wb], queue="pool")
                S.dma(PET[0:64], d["cmp_peT"][kv], writes=[wb], queue="pool"); S.dma(PET[64:128], d["cmp_peT"][kv], writes=[wb], queue="pool")
                w2v = d["cmp_w2"][kv].rearrange("(c p) n -> p c n", p=128)
                S.dma(W2D[:, :, 0:64], w2v, writes=[wb], queue="pool"); S.dma(W2D[:, :, 64:128], w2v, writes=[wb], queue="pool")
                SRC = KCMP if kv == 0 else VCT
                for g in range(2):
                    gs = slice(g * 64, (g + 1) * 64)
                    for mc in range(2):
                        ph, phb = self.psum.get(); pbias, pbb = self.psum.get()

                        def mm(e):
                            for l in range(32):
                                ins = e.matmul(ph[:, 0:127], W1[gs, l, mc * 128:(mc + 1) * 128], SRC[gs, l:l + 16 * 126 + 1:16],
                                               start=(l == 0), stop=(l == 31))
                            return ins

                        def mmb(e):
                            for l in range(32):
                                ins = e.matmul(pbias[:, 0:1], W1[gs, l, mc * 128:(mc + 1) * 128], PET[gs, l:l + 1], start=(l == 0), stop=(l == 31))
                            return ins
                        S.op("pe", mm, reads=[wb, KTB], writes=[phb])
                        S.op("pe", mmb, reads=[wb], writes=[pbb])
                        S.op("act", lambda e: e.copy(BC[:], pbias[:, 0:1]), reads=[pbb, zb], writes=[zb])
                        S.op("dve", lambda e: e.tensor_scalar(ZZ[:], ph[:, 0:127], BC[:, 0:1], None, ALU.add), reads=[phb, zb], writes=[zb])
                        S.op("dve", lambda e: e.tensor_tensor(Z2[:], ZZ[:], ZZ[:], ALU.mult), reads=[zb], writes=[zb])
                        S.op("dve", lambda e: e.tensor_scalar(Z2[:], Z2[:], 0.044715, 1.0, ALU.mult, ALU.add), reads=[zb], writes=[zb])
                        S.op("dve", lambda e: e.tensor_tensor(Z2[:], Z2[:], ZZ[:], ALU.mult), reads=[zb], writes=[zb])
                        S.op("act", lambda e: e.activation(Z2[:], Z2[:], AF.Sigmoid, scale=1.5957691216057308), reads=[zb], writes=[zb])
                        S.op("dve", lambda e: e.tensor_tensor(HID[:, mc, :], ZZ[:], Z2[:], ALU.mult), reads=[zb, hb_], writes=[hb_])
                    po, pob = self.psum.get()
                    if kv == 0:
                        def mm2(e):
                            for mc in range(2):
                                ins = e.matmul(po[:, 0:127], W2D[:, mc, :], HID[:, mc, :], start=(mc == 0), stop=(mc == 1))
                            return ins
                        S.op("pe", mm2, reads=[hb_, wb], writes=[pob])
                        S.op("act", lambda e: e.copy(KC[gs, :], po[gs, 0:127]), reads=[pob], writes=[kcb])
                    else:
                        def mm2(e):
                            for mc in range(2):
                                ins = e.matmul(po[0:127, 0:64], HID[:, mc, :], W2D[:, mc, 0:64], start=(mc == 0), stop=(mc == 1))
                            return ins
                        S.op("pe", mm2, reads=[hb_, wb], writes=[pob])
                        S.op("act", lambda e: e.copy(VC[0:127, gs], po[0:127, 0:64]), reads=[pob], writes=[kcb])
            S.full_barrier()
            stB2.close(); stB.close(); self.st = st4
            if "kcvc" in self.debug:
                okc = self.dout("dbg_kc", [128, 127]); ovc = self.dout("dbg_vc", [127, 128])
                S.dma(okc, KC[:], reads=[kcb], queue="pool"); S.dma(ovc, VC[0:127, :], reads=[kcb], queue="pool")
            WQ = self.sb("WQN", [128, NCH, 512], BF16); WGN = self.sb("WGN", [128, NCH, 24], BF16)
            SHCF = self.sb("SHCF", [32, 247], BF16); EF = self.sb("EF", [32, S_LEN], BF16)
            OV = self.sb("OV", [128, 32], BF16); AB = self.sb("ABF", [128, 2, 64], F32)
            SELG = self.sb("SELG", [24, 12, 128], BF16); IDb = self.sb("IDb", [128, 128], BF16)
            self.load_w(WQ[:], d["w_qn"], cb)
            self.load_w(WGN[:], d["w_gn"], cb)
            S.dma(SHCF[:], d["shcf"], writes=[cb], queue="pool"); S.dma(EF[:], d["efull"], writes=[cb], queue="pool")
            S.dma(OV[0:127, :], d["ov"], writes=[cb], queue="pool"); S.dma(AB[:], d["abf"], writes=[cb])
            S.dma(SELG[:], d["selg"], writes=[cb], queue="pool")
            S.op("dve", lambda e: e.tensor_copy(IDb[:], self.ident_f[:]), reads=[self.constb, cb], writes=[cb])
            QS = self.sb("QS", [128, 4, 128], BF16); qsb = Buf()
            GS = self.sb("GS", [24, 128], BF16); gsb = Buf()
            pt_ring = Ring([self.sb("PT%d" % i, [128, 512], BF16) for i in range(4)])
            RR = self.sb("RR", [128, 512], F32); rrb = Buf()
            RRc = self.sb("RRc", [128, 512], F32); rcb = Buf()
            PTc = [self.sb("PTc%d" % g, [128, 512], BF16) for g in range(2)]; ptcb = [Buf(), Buf()]
            YA = self.sb("YA", [128, 512], F32); yab = Buf()
            PN = self.sb("PN", [128, 512], BF16); pnb = Buf()
            IMP = self.sb("IMP", [128, 32], F32); IM2 = self.sb("IM2", [128, 32], F32); MX = self.sb("MX8", [128, 8], F32); ib = Buf()
            NMT = [self.sb("NMT%d" % g, [32, 4, 128], BF16) for g in range(2)]; nmb = [Buf(), Buf()]
            st_ring = Ring(self.banks[0:3], self.bankb[0:3])
            OD = [(self.banks[3], self.bankb[3], self.banks[4], self.bankb[4]),
                  (self.banks[5], self.bankb[5], self.banks[6], self.bankb[6])]
            ms_ring = Ring(self.banks[7:8], self.bankb[7:8])
            LOOK = 2
            for i in range(NT):
                tq = slice(i * 128, (i + 1) * 128)
                tcix = i // 4
                hreads = [self.HNB[c][tcix] for c in range(NCH)]
                p, pb = ms_ring.get()

                def mmq(e):
                    for j in range(4):
                        for k in range(NCH):
                            ins = e.matmul(p[:, j * 128:(j + 1) * 128], WQ[:, k, j * 128:(j + 1) * 128], self.HN[:, k, tq], start=(k == 0), stop=(k == NCH - 1))
                    return ins
                S.op("pe", mmq, reads=hreads + [cb], writes=[pb])
                S.op("act", lambda e: e.activation(QS[:].rearrange("p j q -> p (j q)"), p[:], AF.Copy, scale=0.125), reads=[pb], writes=[qsb])
                p2, pb2 = ms_ring.get()

                def mmg(e):
                    for k in range(NCH):
                        ins = e.matmul(p2[0:24, 0:128], WGN[:, k, :], self.HN[:, k, tq], start=(k == 0), stop=(k == NCH - 1))
                    return ins
                S.op("pe", mmg, reads=hreads + [cb], writes=[pb2])
                S.op("act", lambda e: e.activation(GS[:], p2[0:24, 0:128], AF.Sigmoid), reads=[pb2], writes=[gsb])

                def tiles_of(br):
                    if br == 0:
                        return [None]
                    if br == 1:
                        return list(range(0, i + 1))
                    return list(range(max(0, i - 4), i + 1))
                odset = {0: 0, 1: 0, 2: 1}

                def emit_scores(step):
                    br, g, kt, first, last = step
                    gs = slice(g * 64, (g + 1) * 64)
                    qrhs = QS[gs, :, :].rearrange("p j q -> p (j q)")
                    rows = 127 if br == 0 else 128
                    stp, stb = st_ring.get()
                    mms_list = []
                    if br == 0:
                        mms_list.append((KC[gs, :], qrhs))
                        mms_list.append((SHCF[:, 120 - 8 * i:247 - 8 * i], BVC[:, g, :]))
                    else:
                        mms_list.append((KT[gs, br - 1, kt * 128:(kt + 1) * 128], qrhs))
                        if br == 1 and i >= 8:
                            mms_list.append((EF[:, kt * 128:(kt + 1) * 128], NMT[g][:].rearrange("p j q -> p (j q)")))
                        if kt == i:
                            mms_list.append((IDb[:], BM[:, 0, g, :]))
                        elif kt == i - 1:
                            mms_list.append((IDb[:], BM[:, 1, g, :]))
                        elif br == 2 and kt == i - 4:
                            mms_list.append((IDb[:], BM[:, 2, g, :]))

                    def mms(e):
                        for n_, (l_, r_) in enumerate(mms_list):
                            ins = e.matmul(stp[0:rows, :], l_, r_, start=(n_ == 0), stop=(n_ == len(mms_list) - 1))
                        return ins
                    S.op("pe", mms, reads=[qsb, KTB, kcb, cb, nmb[g]], writes=[stb])
                    if br == 0:
                        PT, ptb = PTc[g], ptcb[g]
                    else:
                        PT, ptb = pt_ring.get()
                    S.op("act", lambda e: e.activation(PT[0:rows, :], stp[0:rows, :], AF.Exp), reads=[stb], writes=[ptb])
                    return (PT, ptb, rows)

                def emit_pv(step, ctx):
                    br, g, kt, first, last = step
                    PT, ptb, rows = ctx
                    gs = slice(g * 64, (g + 1) * 64)
                    O, Ob, DN, Db = OD[odset[br]]
                    if br == 0:
                        vl = VC[0:127, gs]
                    else:
                        c0 = (0 if br == 1 else 128) + g * 64
                        vl = VT[:, kt, c0:c0 + 64]

                    def mmo(e):
                        e.matmul(O[gs, :], vl, PT[0:rows, :], start=first, stop=last)
                        return e.matmul(DN[gs, :], self.ones_b[0:rows, 0:64], PT[0:rows, :], start=first, stop=last)
                    S.op("pe", mmo, reads=[ptb, VTB, kcb, self.constb], writes=[Ob, Db])

                def cmp_extras(g, ctx):
                    PT, ptb, rows = ctx
                    th = []
                    box = {}

                    def t0():
                        box["pd2"], box["pdb2"] = ms_ring.get()
                        S.op("pe", lambda e: e.matmul(box["pd2"][0:127, :], self.ones_b[0:127, 0:127], PT[0:127, :], start=True, stop=True),
                             reads=[ptb, self.constb], writes=[box["pdb2"]])
                        S.op("dve", lambda e: e.tensor_scalar(RRc[0:127, :], box["pd2"][0:127, :], 1e-30, None, ALU.max), reads=[box["pdb2"], rcb], writes=[rcb])
                    th.append(t0)
                    th.append(lambda: S.op("dve", lambda e: e.reciprocal(RRc[0:127, :], RRc[0:127, :]), reads=[rcb], writes=[rcb]))
                    th.append(lambda: S.op("dve", lambda e: e.tensor_tensor(PN[0:127, :], PT[0:127, :], RRc[0:127, :], ALU.mult), reads=[rcb, ptb, pnb], writes=[pnb]))

                    def t3():
                        box["pim"], box["pimb"] = ms_ring.get()

                        def mmi(e):
                            for j in range(4):
                                ins = e.matmul(box["pim"][:, 0:32], PN[0:127, j * 128:(j + 1) * 128], OV[0:127, :], start=(j == 0), stop=(j == 3))
                            return ins
                        S.op("pe", mmi, reads=[pnb, cb], writes=[box["pimb"]])
                        o0 = 32 - 2 * i
                        S.op("dve", lambda e: e.tensor_tensor(IMP[:], box["pim"][:, 0:32], AB[:, 0, o0:o0 + 32], ALU.mult), reads=[box["pimb"], cb, ib], writes=[ib])
                    th.append(t3)
                    o0 = 32 - 2 * i
                    th.append(lambda: S.op("dve", lambda e: e.tensor_tensor(IMP[:], IMP[:], AB[:, 1, o0:o0 + 32], ALU.add), reads=[ib, cb], writes=[ib]))
                    th.append(lambda: S.op("dve", lambda e: e.memset(IMP[:, 0:1], 1e6), reads=[ib], writes=[ib]))
                    th.append(lambda: S.op("dve", lambda e: e.max(MX[:], IMP[:]), reads=[ib], writes=[ib]))
                    th.append(lambda: S.op("dve", lambda e: e.match_replace(IM2[:], MX[:], IMP[:], 0.0), reads=[ib], writes=[ib]))
                    th.append(lambda: S.op("dve", lambda e: e.max(MX[:], IM2[:]), reads=[ib], writes=[ib]))
                    th.append(lambda: S.op("dve", lambda e: e.match_replace(IM2[:], MX[:], IM2[:], 0.0), reads=[ib], writes=[ib]))
                    th.append(lambda: S.op("dve", lambda e: e.tensor_tensor(IM2[:], IMP[:], IM2[:], ALU.subtract), reads=[ib], writes=[ib]))
                    th.append(lambda: S.op("dve", lambda e: e.tensor_scalar(IM2[:], IM2[:], 0.0, None, ALU.is_gt), reads=[ib], writes=[ib]))
                    th.append(lambda: S.op("dve", lambda e: e.tensor_scalar(IM2[:], IM2[:], 30000.0, -30000.0, ALU.mult, ALU.add), reads=[ib], writes=[ib]))

                    def tl():
                        ptr, ptrb = ms_ring.get()
                        S.op("pe", lambda e: e.transpose(ptr[0:32, 0:128], IM2[:], self.ident_f[:]), reads=[ib, self.constb], writes=[ptrb])
                        S.op("dve", lambda e: e.tensor_copy(NMT[g][:], ptr[0:32, 0:128].unsqueeze(1).to_broadcast([32, 4, 128])),
                             reads=[ptrb], writes=[nmb[g]])
                    th.append(tl)
                    return th

                def finalize(br):
                    O, Ob, DN, Db = OD[odset[br]]
                    S.op("dve", lambda e: e.tensor_scalar(RR[:], DN[:], 1e-30, None, ALU.max), reads=[Db, rrb], writes=[rrb])
                    S.op("dve", lambda e: e.reciprocal(RR[:], RR[:]), reads=[rrb], writes=[rrb])
                    pgb_, pgbb = ms_ring.get()

                    def mmgb(e):
                        for j in range(4):
                            ins = e.matmul(pgb_[:, j * 128:(j + 1) * 128], SELG[:, br * 4 + j, :], GS[:], start=True, stop=True)
                        return ins
                    S.op("pe", mmgb, reads=[gsb, cb], writes=[pgbb])
                    S.op("dve", lambda e: e.tensor_tensor(RR[:], RR[:], pgb_[:], ALU.mult), reads=[rrb, pgbb], writes=[rrb])
                    if br == 0:
                        S.op("dve", lambda e: e.tensor_tensor(YA[:], O[:], RR[:], ALU.mult), reads=[Ob, rrb, yab], writes=[yab])
                    else:
                        S.op("dve", lambda e: e.tensor_tensor(RR[:], O[:], RR[:], ALU.mult), reads=[Ob, rrb], writes=[rrb])
                        S.op("dve", lambda e: e.tensor_tensor(YA[:], YA[:], RR[:], ALU.add), reads=[rrb, yab], writes=[yab])

                extras = []
                for g in range(2):
                    st_ = (0, g, None, True, True)
                    ctx = emit_scores(st_)
                    emit_pv(st_, ctx)
                    if i >= 8:
                        extras += cmp_extras(g, ctx)
                finalize(0)

                def run_steps(steps, fill):
                    ctxs = {}
                    for n in range(len(steps) + LOOK):
                        if n < len(steps):
                            ctxs[n] = emit_scores(steps[n])
                        m = n - LOOK
                        if m >= 0:
                            emit_pv(steps[m], ctxs.pop(m))
                            br_, g_, kt_, f_, l_ = steps[m]
                            if g_ == 1 and l_:
                                finalize(br_)
                        for _ in range(3):
                            if fill:
                                fill.pop(0)()

                def mk_steps(br):
                    out = []
                    for g in range(2):
                        tl_ = tiles_of(br)
                        for ti, kt in enumerate(tl_):
                            out.append((br, g, kt, ti == 0, ti == len(tl_) - 1))
                    return out
                run_steps(mk_steps(2), extras)
                while extras:
                    extras.pop(0)()
                run_steps(mk_steps(1), [])
                S.op("act", lambda e: e.copy(self.Y[1][:, :, tq], YA[:].rearrange("p (j q) -> p j q", j=4)), reads=[yab], writes=[self.YB[1][tcix]])
            S.full_barrier()
            self.st = old

    def mem_branch(self, memT, wk_d, wv_d, wqm_d):
        S = self.S
        with ExitStack() as st4:
            old, self.st = self.st, st4
            self._norm_rings_open()
            WQ = self.sb("WQM", [128, NCH, 512], BF16); WQB = Buf()
            KHT = self.sb("KHT", [128, 4, 256], BF16); KHTB = Buf()
            VH = self.sb("VH", [128, 2, 512], BF16); VHB = Buf()
            st5 = ExitStack()
            self.st = st5
            MT = self.sb("MT", [128, NCH, 256], F32); MTB = Buf()
            MN = self.sb("MN", [128, NCH, 256], BF16); MNB = Buf()
            WK = self.sb("WK", [128, NCH, 512], BF16); WKB = Buf()
            WV = self.sb("WV", [128, NCH, 512], BF16); WVB = Buf()
            mr = self.sb("mrstd", [128, 256], F32); mrb = Buf()
            S.dma(MT[:], memT.rearrange("(c p) m -> p c m", p=128), writes=[MTB])
            self.load_w(WK[:], wk_d, WKB)
            self.load_w(WV[:], wv_d, WVB)
            self.load_w(WQ[:], wqm_d, WQB)
            g0, _ = COLS["mem_norm"]
            pt, pb = self.psum.get()
            for c in range(NCH):
                sq, sqb = self.sq_ring.get()
                S.op("act", lambda e: e.activation(sq[:, 0:256], MT[:, c, :], AF.Square), reads=[MTB], writes=[sqb])
                S.op("pe", lambda e: e.matmul(pt[:, 0:256], self.ones_f[:], sq[:, 0:256], start=(c == 0), stop=(c == NCH - 1)),
                     reads=[sqb, self.constb], writes=[pb])
            S.op("act", lambda e: e.activation(mr[:], pt[:, 0:256], AF.Sqrt, bias=self.eps_t[:], scale=1.0 / D),
                 reads=[pb, self.constb], writes=[mrb])
            S.op("dve", lambda e: e.reciprocal(mr[:], mr[:]), reads=[mrb], writes=[mrb])
            for c in range(NCH):
                S.op("dve", lambda e: e.scalar_tensor_tensor(MN[:, c, :], MT[:, c, :], self.cols[:, g0 + c:g0 + c + 1], mr[:],
                                                             ALU.mult, ALU.mult),
                     reads=[MTB, mrb, self.constb], writes=[MNB])
            for h in range(4):
                p, pb = self.psum.get()

                def mm(e):
                    for k in range(NCH):
                        ins = e.matmul(p[:, 0:256], WK[:, k, h * 128:(h + 1) * 128], MN[:, k, :], start=(k == 0), stop=(k == NCH - 1))
                    return ins
                S.op("pe", mm, reads=[WKB, MNB], writes=[pb])
                S.op("act", lambda e: e.copy(KHT[:, h, :], p[:, 0:256]), reads=[pb], writes=[KHTB])
            for mt in range(2):
                p, pb = self.psum.get()

                def mm(e):
                    for k in range(NCH):
                        ins = e.matmul(p[:], MN[:, k, mt * 128:(mt + 1) * 128], WV[:, k, :], start=(k == 0), stop=(k == NCH - 1))
                    return ins
                S.op("pe", mm, reads=[WVB, MNB], writes=[pb])
                S.op("act", lambda e: e.copy(VH[:, mt, :], p[:]), reads=[pb], writes=[VHB])
            S.full_barrier()
            st5.close()
            self.st = st4
            qm_ring = Ring([self.sb("qm%d" % i, [128, TC], BF16) for i in range(2)])
            pt_ring = Ring([self.sb("pt%d" % i, [128, 2, TC], BF16) for i in range(2)])
            rd_ring = self.sq_ring
            scale = 128.0 ** -0.5
            for tc in range(NTC):
                ts = slice(tc * TC, (tc + 1) * TC)
                hreads = [self.HNB[c][tc] for c in range(NCH)]
                for h in range(4):
                    p, pb = self.psum.get()

                    def mm(e):
                        for k in range(NCH):
                            ins = e.matmul(p[:], WQ[:, k, h * 128:(h + 1) * 128], self.HN[:, k, ts], start=(k == 0), stop=(k == NCH - 1))
                        return ins
                    S.op("pe", mm, reads=hreads + [WQB], writes=[pb])
                    qm, qmb = qm_ring.get()
                    S.op("dve", lambda e: e.tensor_copy(qm[:], p[:]), reads=[pb], writes=[qmb])
                    ptile, ptb = pt_ring.get()
                    for mt in range(2):
                        ps_, psb = self.psum.get()
                        S.op("pe", lambda e: e.matmul(ps_[:], KHT[:, h, mt * 128:(mt + 1) * 128], qm[:], start=True, stop=True),
                             reads=[KHTB, qmb], writes=[psb])
                        S.op("act", lambda e: e.activation(ptile[:, mt, :], ps_[:], AF.Exp, scale=scale), reads=[psb], writes=[ptb])
                    po, pob = self.psum.get()
                    pd, pdb = self.psum.get()

                    def mm_o(e):
                        for mt in range(2):
                            ins = e.matmul(po[:], VH[:, mt, h * 128:(h + 1) * 128], ptile[:, mt, :], start=(mt == 0), stop=(mt == 1))
                        return ins

                    def mm_d(e):
                        for mt in range(2):
                            ins = e.matmul(pd[:], self.ones_b[:], ptile[:, mt, :], start=(mt == 0), stop=(mt == 1))
                        return ins
                    S.op("pe", mm_o, reads=[VHB, ptb], writes=[pob])
                    S.op("pe", mm_d, reads=[ptb, self.constb], writes=[pdb])
                    rd, rdb = rd_ring.get()
                    S.op("dve", lambda e: e.reciprocal(rd[:], pd[:]), reads=[pdb], writes=[rdb])
                    S.op("dve", lambda e: e.tensor_tensor(self.Y[2][:, h, ts], po[:], rd[:], ALU.mult),
                         reads=[pob, rdb], writes=[self.YB[2][tc]])
            S.full_barrier()
            self.st = old
        self._nst.close()

    def fold(self, br, wgb_d, wbr_d, first):
        S = self.S
        with ExitStack() as st4:
            old, self.st = self.st, st4
            WGB = [self.sb("WGBr%d" % i, [128, NCH, 128], BF16) for i in range(2)]; WGBB = [Buf(), Buf()]
            WBR = [self.sb("WBR%d" % i, [128, 4, 128], BF16) for i in range(2)]; WBRB = [Buf(), Buf()]
            gt_ring = Ring([self.sb("gt%d" % i, [128, TC], F32) for i in range(2)])
            t_ring = Ring([self.sb("mt%d" % i, [128, TC], F32) for i in range(2)])

            def load(dc):
                sl = dc % 2
                c0 = br * D + dc * 128
                S.dma(WGB[sl][:], wgb_d[:, c0:c0 + 128].rearrange("(k p) n -> p k n", p=128), writes=[WGBB[sl]], queue="pool")
                S.dma(WBR[sl][:], wbr_d[:, dc * 128:(dc + 1) * 128].rearrange("(k p) n -> p k n", p=128), writes=[WBRB[sl]], queue="pool")
            load(0)
            for dc in range(NCH):
                if dc + 1 < NCH:
                    load(dc + 1)
                sl = dc % 2
                for tc in range(NTC):
                    ts = slice(tc * TC, (tc + 1) * TC)
                    hreads = [self.HNB[c][tc] for c in range(NCH)]
                    pg, pgb = self.psum.get()
                    py, pyb = self.psum.get()

                    def mm_g(e):
                        for k in range(NCH):
                            ins = e.matmul(pg[:], WGB[sl][:, k, :], self.HN[:, k, ts], start=(k == 0), stop=(k == NCH - 1))
                        return ins

                    def mm_y(e):
                        for k in range(4):
                            ins = e.matmul(py[:], WBR[sl][:, k, :], self.Y[br][:, k, ts], start=(k == 0), stop=(k == 3))
                        return ins
                    S.op("pe", mm_g, reads=hreads + [WGBB[sl]], writes=[pgb])
                    S.op("pe", mm_y, reads=[self.YB[br][tc], WBRB[sl]], writes=[pyb])
                    gt, gtb = gt_ring.get()
                    S.op("act", lambda e: e.activation(gt[:], pg[:], AF.Sigmoid), reads=[pgb], writes=[gtb])
                    if first:
                        S.op("dve", lambda e: e.tensor_tensor(self.M[:, dc, ts], gt[:], py[:], ALU.mult),
                             reads=[gtb, pyb], writes=[self.MB[dc][tc]])
                    else:
                        t, tb = t_ring.get()
                        S.op("dve", lambda e: e.tensor_tensor(t[:], gt[:], py[:], ALU.mult), reads=[gtb, pyb], writes=[tb])
                        S.op("pool", lambda e: e.tensor_tensor(self.M[:, dc, ts], self.M[:, dc, ts], t[:], ALU.add),
                             reads=[tb, self.MB[dc][tc]], writes=[self.MB[dc][tc]])
            S.full_barrier()
            self.st = old

    def outproj(self, wout_d):
        S = self.S
        with ExitStack() as st4:
            old, self.st = self.st, st4
            WO = self.sb("WO", [128, NCH, D], BF16); WOB = Buf()
            self.load_w(WO[:], wout_d, WOB)
            for tc in range(NTC):
                ts = slice(tc * TC, (tc + 1) * TC)
                for d2 in range(NCH):
                    po, pob = self.psum.get()

                    def mm(e):
                        for k in range(NCH):
                            ins = e.matmul(po[:], WO[:, k, d2 * 128:(d2 + 1) * 128], self.M[:, k, ts], start=(k == 0), stop=(k == NCH - 1))
                        return ins
                    S.op("pe", mm, reads=[self.MB[k][tc] for k in range(NCH)] + [WOB], writes=[pob])
                    S.op("dve", lambda e: e.tensor_tensor(self.X[:, d2, ts], po[:], self.X[:, d2, ts], ALU.add),
                         reads=[pob, self.XB[d2][tc]], writes=[self.XB[d2][tc]])
            S.full_barrier()
            self.st = old

    def final_norm_out(self, outT):
        S = self.S
        g0, _ = COLS["final_norm"]
        self._norm_rings_open()
        for tc in range(NTC):
            ts = slice(tc * TC, (tc + 1) * TC)
            pt, pb = self.psum.get()
            for c in range(NCH):
                sq, sqb = self.sq_ring.get()
                S.op("act", lambda e: e.activation(sq[:], self.X[:, c, ts], AF.Square),
                     reads=[self.XB[c][tc]], writes=[sqb])
                S.op("pe", lambda e: e.matmul(pt[:], self.ones_f[:], sq[:], start=(c == 0), stop=(c == NCH - 1)),
                     reads=[sqb, self.constb], writes=[pb])
            rs, rsb = self.rstd_ring.get()
            S.op("act", lambda e: e.activation(rs[:], pt[:], AF.Sqrt, bias=self.eps_t[:], scale=1.0 / D),
                 reads=[pb, self.constb], writes=[rsb])
            S.op("dve", lambda e: e.reciprocal(rs[:], rs[:]), reads=[rsb], writes=[rsb])
            for c in range(NCH):
                S.op("dve", lambda e: e.scalar_tensor_tensor(
                    self.X[:, c, ts], self.X[:, c, ts], self.cols[:, g0 + c:g0 + c + 1], rs[:],
                    ALU.mult, ALU.mult),
                    reads=[self.XB[c][tc], rsb, self.constb], writes=[self.XB[c][tc]])
                S.dma(outT[c * 128:(c + 1) * 128, ts], self.X[:, c, ts], reads=[self.XB[c][tc]])
        self._norm_rings_close()

    def dump_x(self, name):
        o = self.dout(name, [D, S_LEN])
        for c in range(NCH):
            for tc in range(NTC):
                ts = slice(tc * TC, (tc + 1) * TC)
                self.S.dma(o[c * 128:(c + 1) * 128, ts], self.X[:, c, ts], reads=[self.XB[c][tc]])

    def build(self, stop_after=None):
        nc = self.nc
        dbg = self.debug
        xT = self.din("xT", [D, S_LEN])
        cols_d = self.din("cols", [128, NCOLS])
        f1g = self.din("ffn1_w_gate", [D, DFF]); f1u = self.din("ffn1_w_up", [D, DFF]); f1d = self.din("ffn1_w_down", [DFF, D])
        f2g = self.din("ffn2_w_gate", [D, DFF]); f2u = self.din("ffn2_w_up", [D, DFF]); f2d = self.din("ffn2_w_down", [DFF, D])
        memT = self.din("memT", [D, 256])
        mem_wk = self.din("mem_w_k", [D, 512]); mem_wv = self.din("mem_w_v", [D, 512])
        w_qm = self.din("w_qm", [D, 512])
        w_gb = self.din("w_gb", [D, 3 * D])
        w_br = [self.din(n, [512, D]) for n in ("w_br_rwkv", "w_br_nsa_p", "w_br_mem")]
        w_out = self.din("w_out", [D, D])
        w_rwkv = self.din("w_rwkv", [D, 1792])
        w2_d = self.din("rwkv_w2", [64, 512]); a2_d = self.din("rwkv_a2", [64, 512]); g2_d = self.din("rwkv_g2", [128, 512])
        gng_d = self.din("gng_rep", [128, 512]); gnb_d = self.din("gnb_rep", [128, 512])
        ident_d = self.din("ident", [128, 128])
        rmk_d = self.din("rwkv_masks", [128, 3, 128])
        nd = {}
        nd["w_qn"] = self.din("w_qn", [D, 512]); nd["w_gn"] = self.din("w_gn", [D, 24]); nd["w_kvn"] = self.din("w_kvn", [D, 768])
        nd["shcf"] = self.din("shcf", [32, 247]); nd["efull"] = self.din("efull", [32, S_LEN]); nd["ov"] = self.din("ov", [127, 32])
        nd["abf"] = self.din("abf", [128, 2, 64]); nd["selg"] = self.din("selg", [24, 12, 128])
        nd["t31"] = self.din("t31", [128, 2, 512])
        nd["bmg"] = [self.din("bmg%d" % k, [128, 2, 512]) for k in range(3)]
        nd["msk"] = [self.din("msk%d" % k, [128, 128]) for k in range(3)]
        nd["bvcg"] = self.din("bvcg", [32, 2, 512]); nd["mskc"] = self.din("mskc", [32, 128])
        nd["cmp_w1"] = [self.din("cmp_k_w1", [2048, 256]), self.din("cmp_v_w1", [2048, 256])]
        nd["cmp_w2"] = [self.din("cmp_k_w2", [256, 64]), self.din("cmp_v_w2", [256, 64])]
        nd["cmp_peT"] = [self.din("cmp_pe_kT", [64, 32]), self.din("cmp_pe_vT", [64, 32])]
        outT = self.dout("outT", [D, S_LEN])
        with ExitStack() as st:
            self.st = st
            S = self.S = Sched(nc, st)
            self.X = self.sb("X", [128, NCH, S_LEN], F32)
            self.XB = [[Buf() for _ in range(NTC)] for _ in range(NCH)]
            self.cols = self.sb("cols", [128, NCOLS], F32)
            self.ones_f = self.sb("ones_f", [128, 128], F32)
            self.ones_b = self.sb("ones_b", [128, 128], BF16)
            self.eps_t = self.sb("eps_t", [128, 1], F32)
            self.gneps_t = self.sb("gneps_t", [128, 1], F32)
            self.ident_f = self.sb("ident_f", [128, 128], F32)
            self.constb = Buf("const")
            self.PS = self.ps("PSALL", [128, 8, 512])
            self.banks = [self.PS[:, i, :] for i in range(8)]
            self.bankb = [Buf() for _ in range(8)]
            self.psum = Ring(self.banks, self.bankb)
            S.dma(self.cols[:], cols_d, writes=[self.constb])
            S.op("dve", lambda e: e.memset(self.ones_f[:], 1.0), reads=[self.constb], writes=[self.constb])
            S.op("dve", lambda e: e.memset(self.ones_b[:], 1.0), reads=[self.constb], writes=[self.constb])
            S.op("dve", lambda e: e.memset(self.eps_t[:], EPS), reads=[self.constb], writes=[self.constb])
            S.op("dve", lambda e: e.memset(self.gneps_t[:], 64e-5), reads=[self.constb], writes=[self.constb])
            S.dma(self.ident_f[:], ident_d, reads=[self.constb], writes=[self.constb])
            for c in range(NCH):
                for tc in range(NTC):
                    ts = slice(tc * TC, (tc + 1) * TC)
                    S.dma(self.X[:, c, ts], xT[c * 128:(c + 1) * 128, ts], writes=[self.XB[c][tc]])

            def ffn_phase(wg, wu, wd, gname):
                with ExitStack() as st2:
                    self.st = st2
                    self.HN = self.sb("HN", [128, NCH, S_LEN], BF16)
                    self.HNB = [[Buf() for _ in range(NTC)] for _ in range(NCH)]
                    self.WG = [self.sb("WG%d" % i, [128, NCH, 512], BF16) for i in range(2)]
                    self.WU = [self.sb("WU%d" % i, [128, NCH, 512], BF16) for i in range(2)]
                    self.WD = [self.sb("WD%d" % i, [128, 4, D], BF16) for i in range(2)]
                    self.WGB = [Buf() for _ in range(2)]; self.WUB = [Buf() for _ in range(2)]; self.WDB = [Buf() for _ in range(2)]
                    self.a_ring = Ring([self.sb("a%d" % i, [128, 4, TC], BF16) for i in range(2)])
                    self.sg_ring = Ring([self.sb("sg%d" % i, [128, TC], F32) for i in range(2)])
                    self.ffn(wg, wu, wd, gname)
                    S.full_barrier()
                    self.st = st

            if "noffn1" not in dbg:
                ffn_phase(f1g, f1u, f1d, "ffn1_norm")
            if "x1" in dbg:
                self.dump_x("dbg_x1")
            if stop_after != "ffn1":
                with ExitStack() as st3:
                    self.st = st3
                    self.HN = self.sb("HN", [128, NCH, S_LEN], BF16)
                    self.HNB = [[Buf() for _ in range(NTC)] for _ in range(NCH)]
                    Yt = self.sb("Yt", [128, 4, S_LEN], BF16)
                    YBt = [Buf() for _ in range(NTC)]
                    self.Y = [Yt, Yt, Yt]
                    self.YB = [YBt, YBt, YBt]
                    if "norwkv" in dbg or "rwkvseq" in dbg:
                        self.rmsnorm_to_hn("mix_norm")
                    if "norwkv" not in dbg:
                        if "rwkvseq" in dbg:
                            self.rwkv_branch_seq(w_rwkv, w2_d, a2_d, g2_d, gng_d, gnb_d)
                        else:
                            self.rwkv_branch(w_rwkv, w2_d, a2_d, g2_d, gng_d, gnb_d, rmk_d)
                    else:
                        S.op("pool", lambda e: e.memset(Yt[:], 0.0), writes=YBt)
                    if "y_rwkv" in dbg:
                        self.dump_feat("dbg_y_rwkv", Yt, 4, YBt)
                    self.M = self.sb("M", [128, NCH, S_LEN], BF16)
                    self.MB = [[Buf() for _ in range(NTC)] for _ in range(NCH)]
                    do_merge = stop_after != "mix"
                    if do_merge:
                        self.fold(0, w_gb, w_br[0], True)
                    if "nomem" not in dbg:
                        self.mem_branch(memT, mem_wk, mem_wv, w_qm)
                    else:
                        S.op("pool", lambda e: e.memset(Yt[:], 0.0), writes=YBt)
                    if "y_mem" in dbg:
                        self.dump_feat("dbg_y_mem", Yt, 4, YBt)
                    if do_merge:
                        self.fold(2, w_gb, w_br[2], False)
                    if "nonsa" not in dbg:
                        self.nsa_branch(nd)
                    else:
                        S.op("pool", lambda e: e.memset(Yt[:], 0.0), writes=YBt)
                    if "y_nsa" in dbg:
                        self.dump_feat("dbg_y_nsa_p", Yt, 4, YBt)
                    if do_merge:
                        self.fold(1, w_gb, w_br[1], False)
                        self.outproj(w_out)
                    S.full_barrier()
                    self.st = st
                if "x2" in dbg:
                    self.dump_x("dbg_x2")
                if stop_after not in ("mix", "merge"):
                    ffn_phase(f2g, f2u, f2d, "ffn2_norm")
            self.final_norm_out(outT)
            S.wait_all_dma("sp")
            S.wait_all_dma("pool")
        return nc


NSA_PERM = np.concatenate([np.concatenate([np.arange(64 * j, 64 * j + 64), np.arange(64 * (4 + j), 64 * (4 + j) + 64)])
                           for j in range(4)])


def _t5_bucket_np(dist):
    n = np.maximum(dist, 0)
    nf = np.maximum(n, 1).astype(np.float32)
    large = 16 + (np.log(nf / np.float32(16)) / np.float32(math.log(128 / 16)) * np.float32(16)).astype(np.int32)
    large = np.minimum(large, 31)
    return np.where(n < 16, n, large)


def _nsa_consts(rel_bias):
    rb = np.asarray(rel_bias, np.float32)
    c = np.arange(128)[:, None]; p = np.arange(128)[None, :]
    out = {}
    hd = np.arange(8).reshape(2, 4)
    dists = [p - c, 128 + p - c, 512 + p - c]
    valid = [p >= c, np.ones((128, 128), bool), c > p]
    for k in range(3):
        bk = _t5_bucket_np(dists[k])
        g = rb[bk[:, None, None, :], hd[None, :, :, None]]
        out["bmg%d" % k] = np.ascontiguousarray(g.reshape(128, 2, 512))
        out["msk%d" % k] = np.where(valid[k], 0.0, -30000.0).astype(np.float32)
    out["t31"] = np.ascontiguousarray(np.broadcast_to(rb[31][hd][None, :, :, None], (128, 2, 4, 128)).reshape(128, 2, 512))
    m = np.arange(32)[:, None]
    dc = p - 16 * (m - 8) - 31
    bk = _t5_bucket_np(dc)
    g = rb[bk[:, None, None, :], hd[None, :, :, None]]
    out["bvcg"] = np.ascontiguousarray(g.reshape(32, 2, 512))
    mk = np.where((dc >= 0) & (m < 16), 0.0, -30000.0).astype(np.float32)
    mk[17:] = 0.0
    out["mskc"] = mk
    shcf = np.zeros((32, 247), np.float32)
    for x in range(247):
        r = x - 112
        if 0 <= r < 16:
            shcf[r, x] = 1.0
        elif r >= 16:
            shcf[16, x] = 1.0
    out["shcf"] = shcf
    ef = np.zeros((32, S_LEN), np.float32)
    ef[np.arange(S_LEN) // 64, np.arange(S_LEN)] = 1.0
    out["efull"] = ef
    ic = np.arange(127)[:, None]; jb = np.arange(32)[None, :]
    out["ov"] = ((ic * 16 <= jb * 64 + 63) & (ic * 16 + 31 >= jb * 64)).astype(np.float32)
    ab = np.zeros((128, 2, 64), np.float32)
    for pp in range(128):
        curr = 1 if pp >= 64 else 0
        for mm in range(64):
            jr = mm - 32
            if jr <= curr - 2:
                ab[pp, 0, mm] = 1.0
            if jr in (curr, curr - 1):
                ab[pp, 1, mm] = 1e6
    out["abf"] = ab
    selg = np.zeros((24, 12, 128), np.float32)
    for br in range(3):
        for j in range(4):
            for mm in range(128):
                selg[br * 8 + (mm // 64) * 4 + j, br * 4 + j, mm] = 1.0
    out["selg"] = selg
    return out


def prep_inputs(inputs, b):
    m = {}
    m["xT"] = np.ascontiguousarray(inputs["x"][b].T)
    cols = np.zeros((128, NCOLS), np.float32)
    for n in ("ffn1_norm", "mix_norm", "ffn2_norm", "final_norm", "mem_norm"):
        c0, k = COLS[n]
        cols[:, c0:c0 + k] = _colpack(np.asarray(inputs[n]).reshape(-1))
    for n, src in (("mu", "rwkv_mu"), ("w0", "rwkv_w0"), ("a0", "rwkv_a0"), ("k_k", "rwkv_k_k"), ("k_a", "rwkv_k_a"), ("r_k", "rwkv_r_k"),
                   ("gn_g", "rwkv_gn_gain"), ("gn_b", "rwkv_gn_bias")):
        c0, k = COLS[n]
        cols[:, c0:c0 + k] = _colpack(np.asarray(inputs[src]).reshape(-1))
    m["cols"] = cols
    m["w_rwkv"] = np.ascontiguousarray(np.asarray(inputs["w_in"])[0][:, 0:1792])
    m["rwkv_w2"] = np.ascontiguousarray(np.asarray(inputs["rwkv_w2"])[0])
    m["rwkv_a2"] = np.ascontiguousarray(np.asarray(inputs["rwkv_a2"])[0])
    m["rwkv_g2"] = np.ascontiguousarray(np.asarray(inputs["rwkv_g2"])[0])
    m["gng_rep"] = np.ascontiguousarray(np.broadcast_to(np.asarray(inputs["rwkv_gn_gain"]).reshape(1, 512), (128, 512)))
    m["gnb_rep"] = np.ascontiguousarray(np.broadcast_to(np.asarray(inputs["rwkv_gn_bias"]).reshape(1, 512), (128, 512)))
    m["ident"] = np.eye(128, dtype=np.float32)
    si = np.arange(128)[:, None]; ti = np.arange(128)[None, :]
    same = (si // 64) == (ti // 64)
    mk = np.zeros((128, 3, 128), np.float32)
    mk[:, 0, :] = np.where(same & (si < ti), -1.0, 0.0)
    mk[:, 1, :] = np.where(same & (ti < si), -1.0, 0.0)
    mk[:, 2, :] = np.where(same & (si <= ti), 1.0, 0.0)
    m["rwkv_masks"] = mk
    w_in_ = np.asarray(inputs["w_in"])[0]
    m["w_qn"] = np.ascontiguousarray(w_in_[:, 1792:2304][:, NSA_PERM])
    m["w_kvn"] = np.ascontiguousarray(w_in_[:, 2304:3072])
    m["w_gn"] = np.ascontiguousarray(w_in_[:, 3072:3096])
    m.update(_nsa_consts(inputs["rel_bias"]))
    for n in ("cmp_k_w1", "cmp_v_w1", "cmp_k_w2", "cmp_v_w2"):
        m[n] = np.ascontiguousarray(np.asarray(inputs[n])[0])
    m["cmp_pe_kT"] = np.ascontiguousarray(np.asarray(inputs["cmp_pe_k"])[0].T)
    m["cmp_pe_vT"] = np.ascontiguousarray(np.asarray(inputs["cmp_pe_v"])[0].T)
    for n in ("ffn1_w_gate", "ffn1_w_up", "ffn1_w_down", "ffn2_w_gate", "ffn2_w_up", "ffn2_w_down",
              "mem_w_k", "mem_w_v", "w_br_rwkv", "w_br_mem", "w_out"):
        m[n] = np.ascontiguousarray(np.asarray(inputs[n])[0])
    m["memT"] = np.ascontiguousarray(inputs["mem"][b].T)
    w_in = np.asarray(inputs["w_in"])[0]
    m["w_qm"] = np.ascontiguousarray(w_in[:, 3096:3608])
    m["w_gb"] = np.ascontiguousarray(w_in[:, 3608:6680])
    m["w_br_nsa_p"] = np.ascontiguousarray(np.asarray(inputs["w_br_nsa"])[0][NSA_PERM, :])
    return m


_CACHE = {}


def kernel(**inputs):
    inputs = {k: np.asarray(v) for k, v in inputs.items()}
    if "nc" not in _CACHE:
        _CACHE["nc"] = Builder().build()
    nc = _CACHE["nc"]
    n = 8
    in_maps = [prep_inputs(inputs, b) for b in range(n)]
    res = run_bass_kernel_spmd(nc, in_maps, core_ids=list(range(n)))
    out = np.stack([np.ascontiguousarray(r["outT"].T) for r in res.results], axis=0)
    return out.astype(np.float32)
```

```python
import math
from contextlib import ExitStack
import numpy as np
import concourse.bass as bass
import concourse.mybir as mybir
from concourse.bass_utils import run_bass_kernel_spmd

F32 = mybir.dt.float32
BF16 = mybir.dt.bfloat16
AF = mybir.ActivationFunctionType
ALU = mybir.AluOpType
AX = mybir.AxisListType

D = 1024
S_LEN = 2048
DFF = 2816
NCH = 8
TC = 512
NTC = S_LEN // TC
EPS = 1e-6


class Buf:
    __slots__ = ("name", "last_w", "readers")

    def __init__(self, name=""):
        self.name = name
        self.last_w = None
        self.readers = []


class Sched:
    ENG = ("pe", "act", "dve", "pool", "sp")

    def __init__(self, nc, stack, n_dma_sems=16):
        self.nc = nc
        self.eng = {"pe": nc.tensor, "act": nc.scalar, "dve": nc.vector,
                    "pool": nc.gpsimd, "sp": nc.sync}
        self.sem = {}
        for e in ("pe", "act", "dve", "pool"):
            self.sem[e] = stack.enter_context(nc.semaphore("s_" + e))
        self.cnt = {e: 0 for e in ("pe", "act", "dve", "pool")}
        nq = {"sp": 28, "pool": 28, "act": 8}
        self.dsem = []
        self.qsems = {}
        for q, n in nq.items():
            self.qsems[q] = list(range(len(self.dsem), len(self.dsem) + n))
            for i in range(n):
                self.dsem.append(stack.enter_context(nc.semaphore("d%s%d" % (q, i))))
        self.dcnt = [0] * len(self.dsem)
        self.dnext = {q: 0 for q in nq}
        self.waited = {e: {} for e in self.ENG}
        self.n_ops = 0
        self.n_waits = 0

    def _semobj(self, key):
        return self.sem[key] if isinstance(key, str) else self.dsem[key]

    def _need(self, engine, toks):
        best = {}
        for t in toks:
            if t is None:
                continue
            key, val = t
            if best.get(key, 0) < val:
                best[key] = val
        w = self.waited[engine]
        for key, val in best.items():
            if w.get(key, 0) >= val:
                continue
            self.eng[engine].wait_ge(self._semobj(key), val)
            w[key] = val
            self.n_waits += 1

    @staticmethod
    def _deps(reads, writes):
        toks = []
        for b in reads:
            toks.append(b.last_w)
        for b in writes:
            toks.append(b.last_w)
            toks.extend(b.readers)
        return toks

    @staticmethod
    def _commit(tok, reads, writes):
        for b in reads:
            b.readers.append(tok)
            if len(b.readers) > 48:
                best = {}
                for k, v in b.readers:
                    if best.get(k, 0) < v:
                        best[k] = v
                b.readers = list(best.items())
        for b in writes:
            b.last_w = tok
            b.readers = []

    def op(self, engine, fn, reads=(), writes=()):
        self._need(engine, self._deps(reads, writes))
        ins = fn(self.eng[engine])
        self.cnt[engine] += 1
        ins.then_inc(self.sem[engine], 1)
        tok = (engine, self.cnt[engine])
        self._commit(tok, reads, writes)
        self.n_ops += 1
        return tok

    def dma(self, out_ap, in_ap, reads=(), writes=(), queue="sp", **kw):
        pool = self.qsems[queue]
        i = pool[self.dnext[queue]]
        self.dnext[queue] = (self.dnext[queue] + 1) % len(pool)
        prev = [(i, self.dcnt[i])] if self.dcnt[i] else []
        self._need(queue, self._deps(reads, writes) + prev)
        ins = self.eng[queue].dma_start(out=out_ap, in_=in_ap, **kw)
        self.dcnt[i] += 16
        ins.then_inc(self.dsem[i], 16)
        tok = (i, self.dcnt[i])
        self._commit(tok, reads, writes)
        self.n_ops += 1
        return tok

    def barrier(self, bufs):
        toks = []
        for b in bufs:
            toks.append(b.last_w)
            toks.extend(b.readers)
        for e in self.ENG:
            self._need(e, toks)

    def full_barrier(self):
        toks = [(e, self.cnt[e]) for e in ("pe", "act", "dve", "pool") if self.cnt[e]]
        toks += [(i, self.dcnt[i]) for i in range(len(self.dsem)) if self.dcnt[i]]
        for e in self.ENG:
            self._need(e, toks)

    def wait_all_dma(self, engine="sp"):
        for i in range(len(self.dsem)):
            if self.dcnt[i]:
                self.eng[engine].wait_ge(self.dsem[i], self.dcnt[i])


class Ring:
    def __init__(self, tiles, bufs=None):
        self.tiles = tiles
        self.bufs = bufs if bufs is not None else [Buf() for _ in tiles]
        self.i = 0

    def get(self):
        t, b = self.tiles[self.i], self.bufs[self.i]
        self.i = (self.i + 1) % len(self.tiles)
        return t, b

    def get_pair_idx(self):
        if self.i % 2:
            self.i = (self.i + 1) % len(self.tiles)
        k = self.i
        self.i = (self.i + 2) % len(self.tiles)
        return k, self.bufs[k], self.bufs[k + 1]


COLS = {}
_c = 0
for _n, _k in (("ffn1_norm", 8), ("mix_norm", 8), ("ffn2_norm", 8), ("final_norm", 8),
               ("mem_norm", 8), ("mu", 14), ("w0", 4), ("a0", 4), ("k_k", 4), ("k_a", 4), ("r_k", 4), ("gn_g", 4), ("gn_b", 4)):
    COLS[_n] = (_c, _k)
    _c += _k
NCOLS = _c


def _colpack(v):
    v = np.asarray(v, np.float32).reshape(-1, 128)
    return np.ascontiguousarray(v.T)


class Builder:
    def __init__(self, debug=()):
        self.debug = set(debug)
        self._rk_stage = 99
        self._rk_tiles = S_LEN // 128
        for d_ in self.debug:
            if d_.startswith("rkstage"):
                self._rk_stage = int(d_[7:])
            if d_.startswith("rktiles"):
                self._rk_tiles = int(d_[7:])
        self.nc = bass.Bass("TRN2", target_bir_lowering=False)
        self.dram_in = {}
        self.dram_out = {}

    def din(self, name, shape, dt=F32):
        t = self.nc.dram_tensor(name, list(shape), dt, kind="ExternalInput").ap()
        self.dram_in[name] = t
        return t

    def dout(self, name, shape, dt=F32):
        t = self.nc.dram_tensor(name, list(shape), dt, kind="ExternalOutput").ap()
        self.dram_out[name] = t
        return t

    def sb(self, name, shape, dt):
        self._uid = getattr(self, "_uid", 0) + 1
        return self.st.enter_context(self.nc.sbuf_tensor("sb%d_%s" % (self._uid, name), list(shape), dt))

    def ps(self, name, shape, dt=F32):
        return self.st.enter_context(self.nc.psum_tensor("ps_" + name, list(shape), dt))

    def _norm_rings_open(self):
        self._nst_old = self.st
        self._nst = ExitStack()
        self.st = self._nst
        self.sq_ring = Ring([self.sb("sq%d" % i, [128, TC], F32) for i in range(2)])
        self.rstd_ring = Ring([self.sb("RSTD%d" % i, [128, TC], F32) for i in range(2)])
        self.st = self._nst_old

    def _norm_rings_close(self):
        self.S.full_barrier()
        self._nst.close()

    def rmsnorm_to_hn(self, gname):
        S = self.S
        g0, _ = COLS[gname]
        self._norm_rings_open()
        for tc in range(NTC):
            ts = slice(tc * TC, (tc + 1) * TC)
            pt, pb = self.psum.get()
            for c in range(NCH):
                sq, sqb = self.sq_ring.get()
                S.op("act", lambda e: e.activation(sq[:], self.X[:, c, ts], AF.Square),
                     reads=[self.XB[c][tc]], writes=[sqb])
                S.op("pe", lambda e: e.matmul(pt[:], self.ones_f[:], sq[:], start=(c == 0), stop=(c == NCH - 1)),
                     reads=[sqb, self.constb], writes=[pb])
            rs, rsb = self.rstd_ring.get()
            S.op("act", lambda e: e.activation(rs[:], pt[:], AF.Sqrt, bias=self.eps_t[:], scale=1.0 / D),
                 reads=[pb, self.constb], writes=[rsb])
            S.op("dve", lambda e: e.reciprocal(rs[:], rs[:]), reads=[rsb], writes=[rsb])
            for c in range(NCH):
                S.op("dve", lambda e: e.scalar_tensor_tensor(
                    self.HN[:, c, ts], self.X[:, c, ts], self.cols[:, g0 + c:g0 + c + 1], rs[:],
                    ALU.mult, ALU.mult),
                    reads=[self.XB[c][tc], rsb, self.constb], writes=[self.HNB[c][tc]])

        self._norm_rings_close()

    def ffn(self, wg, wu, wd, gname):
        S = self.S
        groups = [(i, min(4, 22 - i)) for i in range(0, 22, 4)]

        def load(gi):
            f0, nf = groups[gi]
            slot = gi % 2
            S.dma(self.WG[slot][:, :, 0:nf * 128],
                  wg[:, f0 * 128:(f0 + nf) * 128].rearrange("(k p) n -> p k n", p=128),
                  writes=[self.WGB[slot]], queue="pool")
            S.dma(self.WU[slot][:, :, 0:nf * 128],
                  wu[:, f0 * 128:(f0 + nf) * 128].rearrange("(k p) n -> p k n", p=128),
                  writes=[self.WUB[slot]], queue="pool")
            S.dma(self.WD[slot][:, 0:nf, :],
                  wd[f0 * 128:(f0 + nf) * 128, :].rearrange("(f p) n -> p f n", p=128),
                  writes=[self.WDB[slot]], queue="pool")

        load(0)
        load(1)
        self.rmsnorm_to_hn(gname)
        for gi, (f0, nf) in enumerate(groups):
            if gi >= 1 and gi + 1 < len(groups):
                load(gi + 1)
            slot = gi % 2
            WG, WU, WD = self.WG[slot], self.WU[slot], self.WD[slot]
            for tc in range(NTC):
                ts = slice(tc * TC, (tc + 1) * TC)
                hreads = [self.HNB[c][tc] for c in range(NCH)]
                a_t, a_b = self.a_ring.get()
                for f in range(nf):
                    pg, pgb = self.psum.get()
                    pu, pub = self.psum.get()

                    def mm_g(e):
                        for k in range(NCH):
                            ins = e.matmul(pg[:], WG[:, k, f * 128:(f + 1) * 128], self.HN[:, k, ts],
                                           start=(k == 0), stop=(k == NCH - 1))
                        return ins

                    def mm_u(e):
                        for k in range(NCH):
                            ins = e.matmul(pu[:], WU[:, k, f * 128:(f + 1) * 128], self.HN[:, k, ts],
                                           start=(k == 0), stop=(k == NCH - 1))
                        return ins
                    S.op("pe", mm_g, reads=hreads + [self.WGB[slot]], writes=[pgb])
                    S.op("pe", mm_u, reads=hreads + [self.WUB[slot]], writes=[pub])
                    sg, sgb = self.sg_ring.get()
                    S.op("act", lambda e: e.activation(sg[:], pg[:], AF.Silu), reads=[pgb], writes=[sgb])
                    S.op("dve", lambda e: e.tensor_tensor(a_t[:, f, :], sg[:], pu[:], ALU.mult),
                         reads=[sgb, pub], writes=[a_b])
                for dc in range(NCH):
                    po, pob = self.psum.get()

                    def mm_d(e):
                        for f in range(nf):
                            ins = e.matmul(po[:], WD[:, f, dc * 128:(dc + 1) * 128], a_t[:, f, :],
                                           start=(f == 0), stop=(f == nf - 1))
                        return ins
                    S.op("pe", mm_d, reads=[a_b, self.WDB[slot]], writes=[pob])
                    S.op("dve", lambda e: e.scalar_tensor_tensor(
                        self.X[:, dc, ts], po[:], 0.5, self.X[:, dc, ts], ALU.mult, ALU.add),
                        reads=[pob, self.XB[dc][tc]], writes=[self.XB[dc][tc]])


    def load_w(self, tile_ap, dram_ap, buf):
        self.S.dma(tile_ap, dram_ap.rearrange("(k p) n -> p k n", p=128), writes=[buf], queue="pool")

    def dump_feat(self, name, tile, nchunks, buf_list):
        o = self.dout(name, [nchunks * 128, S_LEN])
        for c in range(nchunks):
            self.S.dma(o[c * 128:(c + 1) * 128, :], tile[:, c, :], reads=buf_list, queue="pool")


    def rwkv_branch_seq(self, w_rwkv, w2_d, a2_d, g2_d, gng_d, gnb_d):
        S = self.S
        CN = COLS
        NT = S_LEN // 128
        with ExitStack() as st4:
            old, self.st = self.st, st4
            WR = self.sb("WR", [128, NCH, 1792], BF16); WRB = Buf()
            W2 = self.sb("W2A2", [128, 512], F32); A2 = W2; G2 = self.sb("G2", [128, 512], F32)
            GNG = self.sb("GNG", [128, 512], F32); GNB = self.sb("GNB", [128, 512], F32)
            BO = self.sb("BO", [128, 128], F32); BOb = self.sb("BOb", [128, 128], BF16)
            ID2 = self.sb("ID2", [128, 64], BF16)
            OMK = self.sb("OMK", [128, 4], F32)
            cb = Buf()
            self.load_w(WR[:], w_rwkv, WRB)
            S.dma(W2[0:64, :], w2_d, writes=[cb]); S.dma(A2[64:128, :], a2_d, writes=[cb]); S.dma(G2[:], g2_d, writes=[cb])
            S.dma(GNG[:], gng_d, writes=[cb]); S.dma(GNB[:], gnb_d, writes=[cb])
            S.op("dve", lambda e: e.memset(BO[:], 0.0), reads=[cb], writes=[cb])
            S.op("dve", lambda e: e.memset(BO[0:64, 0:64], 1.0), reads=[cb], writes=[cb])
            S.op("dve", lambda e: e.memset(BO[64:128, 64:128], 1.0), reads=[cb], writes=[cb])
            S.op("dve", lambda e: e.tensor_copy(BOb[:], BO[:]), reads=[cb], writes=[cb])
            S.op("dve", lambda e: e.tensor_copy(ID2[0:64, :], self.ident_f[0:64, 0:64]), reads=[cb, self.constb], writes=[cb])
            S.op("dve", lambda e: e.tensor_copy(ID2[64:128, :], self.ident_f[64:128, 64:128]), reads=[cb, self.constb], writes=[cb])
            ka0 = CN["k_a"][0]
            S.op("dve", lambda e: e.tensor_scalar(OMK[:], self.cols[:, ka0:ka0 + 4], -1.0, 1.0, ALU.mult, ALU.add),
                 reads=[cb, self.constb], writes=[cb])
            P32 = self.sb("P32", [128, 14, 129], F32); P32B = Buf()
            DD = self.sb("DD", [128, 128], F32); DDB = Buf()
            CAR = self.sb("CAR", [128, 14, 1], F32)
            PL = P32[:, :, 1:129]; PLB = P32B
            TW = self.sb("TW", [64, 128], F32); SGg = self.sb("SGg", [128, 128], F32)
            WD = self.sb("WD", [128, 4, 128], F32); SIG = WD
            A32 = self.sb("A32", [128, 4, 128], F32)
            KK = self.sb("KK", [128, 4, 128], F32); SQ = self.sb("SQ", [128, 4, 128], F32)
            KKN = self.sb("KKN", [128, 4, 128], F32); NB = self.sb("NB", [128, 4, 128], F32)
            KM = self.sb("KM", [128, 4, 128], F32); BON = self.sb("BON", [128, 4, 128], F32)
            RM = self.sb("RM", [128, 4, 128, 2], BF16)
            VDr = Ring([self.sb("VD%d" % i, [128, 4, 64], BF16) for i in range(2)])
            H = self.sb("H", [128, 4, 64], F32); Hb = self.sb("Hb", [128, 4, 64], BF16); HK = self.sb("HK", [128, 4, 64], BF16)
            T1 = self.sb("T1", [128, 4, 64], F32); T2r = Ring([self.sb("T2_%d" % i, [128, 4, 64], F32) for i in range(2)])
            YST = [self.sb("YST%d" % i, [2, 4, 256], F32) for i in range(2)]; YSTB = [Buf(), Buf()]
            YTOK = A32[:].rearrange("p c t -> p (c t)").rearrange("p (c h v) -> p c h v", c=4, h=2); YTOKB = Buf()
            YC = KKN[:].rearrange("p c t -> p (c t)").rearrange("p (a v) -> p a v", a=8)
            ST8 = self.sb("ST8", [128, 8], F32); ST8b = self.sb("ST8b", [128, 8], F32)
            YF = SQ
            db = Buf(); hb = Buf(); hbb = Buf(); hkb = Buf(); t1b = Buf(); vrb = Buf(); vtb = Buf(); rmb = Buf(); yb = Buf()
            S.op("pool", lambda e: e.memset(P32[:], 0.0), writes=[P32B])
            S.op("pool", lambda e: e.memset(RM[:], 0.0), writes=[rmb])
            S.op("pool", lambda e: e.memset(H[:], 0.0), writes=[hb])
            mu0 = CN["mu"][0]; w00 = CN["w0"][0]; a00 = CN["a0"][0]; kk0 = CN["k_k"][0]; rk0 = CN["r_k"][0]
            ident = self.ident_f
            for i in range(NT):
                t0 = i * 128
                tcix = t0 // TC
                tsl = slice(t0, t0 + 128)
                hreads = [self.HNB[c][tcix] for c in range(NCH)]
                for cg in range(4):
                    c0 = cg * 4
                    n = min(4, 14 - c0)
                    p, pb = self.psum.get()

                    def mm(e):
                        for cc in range(n):
                            for k in range(NCH):
                                ins = e.matmul(p[:, cc * 128:(cc + 1) * 128], WR[:, k, (c0 + cc) * 128:(c0 + cc + 1) * 128],
                                               self.HN[:, k, tsl], start=(k == 0), stop=(k == NCH - 1))
                        return ins
                    S.op("pe", mm, reads=hreads + [WRB], writes=[pb])
                    S.op("act", lambda e: e.copy(P32[:, c0:c0 + n, 1:129], p[:, 0:n * 128].rearrange("p (c t) -> p c t", c=n)),
                         reads=[pb], writes=[P32B])
                S.op("dve", lambda e: e.tensor_copy(CAR[:], P32[:, :, 128:129]), reads=[P32B], writes=[DDB])
                for c in range(14):
                    S.op("dve", lambda e: e.tensor_tensor(DD[:], P32[:, c, 0:128], P32[:, c, 1:129], ALU.subtract), reads=[P32B, DDB], writes=[DDB])
                    S.op("dve", lambda e: e.scalar_tensor_tensor(P32[:, c, 1:129], DD[:], self.cols[:, mu0 + c:mu0 + c + 1], P32[:, c, 1:129],
                                                                 ALU.mult, ALU.add), reads=[DDB, P32B, self.constb], writes=[P32B])
                S.op("dve", lambda e: e.tensor_copy(P32[:, :, 0:1], CAR[:]), reads=[P32B, DDB], writes=[P32B])
                S.op("act", lambda e: e.activation(TW[:], PL[0:64, 12, :], AF.Tanh), reads=[PLB], writes=[db])
                S.op("act", lambda e: e.activation(SGg[:], PL[:, 13, :], AF.Sigmoid), reads=[PLB], writes=[db])
                pz, pzb = self.psum.get(); pa, pab = self.psum.get()

                def mmz(e):
                    for fc in range(4):
                        ins = e.matmul(pz[:, fc * 128:(fc + 1) * 128], W2[0:64, fc * 128:(fc + 1) * 128], TW[:], start=True, stop=True)
                    return ins

                def mma(e):
                    for fc in range(4):
                        ins = e.matmul(pa[:, fc * 128:(fc + 1) * 128], A2[64:128, fc * 128:(fc + 1) * 128], PL[64:128, 12, :], start=True, stop=True)
                    return ins

                S.op("pe", mmz, reads=[db, cb], writes=[pzb])
                S.op("pe", mma, reads=[PLB, cb], writes=[pab])
                for fc in range(4):
                    S.op("act", lambda e: e.activation(SIG[:, fc, :], pz[:, fc * 128:(fc + 1) * 128], AF.Sigmoid,
                                                       bias=self.cols[:, w00 + fc:w00 + fc + 1]), reads=[pzb, self.constb], writes=[db])
                    S.op("act", lambda e: e.activation(A32[:, fc, :], pa[:, fc * 128:(fc + 1) * 128], AF.Sigmoid,
                                                       bias=self.cols[:, a00 + fc:a00 + fc + 1]), reads=[pab, self.constb], writes=[db, YTOKB])
                S.op("act", lambda e: e.activation(WD[:], SIG[:], AF.Exp, scale=-0.6065306597126334), reads=[db], writes=[db])
                for fc in range(4):
                    S.op("dve", lambda e: e.tensor_scalar(KK[:, fc, :], PL[:, 4 + fc, :], self.cols[:, kk0 + fc:kk0 + fc + 1], None, ALU.mult),
                         reads=[PLB, self.constb], writes=[db])
                S.op("dve", lambda e: e.tensor_tensor(SQ[:], KK[:], KK[:], ALU.mult), reads=[db], writes=[db])
                pss, pssb = self.psum.get()
                S.op("pe", lambda e: e.matmul(pss[:], BO[:], SQ[:].rearrange("p c t -> p (c t)"), start=True, stop=True), reads=[db, cb], writes=[pssb])
                S.op("act", lambda e: e.activation(SQ[:], pss[:].rearrange("p (c t) -> p c t", c=4), AF.Sqrt), reads=[pssb, db], writes=[db])
                S.op("dve", lambda e: e.tensor_scalar(SQ[:], SQ[:], 1e-12, None, ALU.max), reads=[db], writes=[db])
                S.op("dve", lambda e: e.reciprocal(SQ[:], SQ[:]), reads=[db], writes=[db])
                S.op("dve", lambda e: e.tensor_tensor(KKN[:], KK[:], SQ[:], ALU.mult), reads=[db], writes=[db, yb])
                S.op("dve", lambda e: e.scalar_tensor_tensor(NB[:], KKN[:], -1.0, A32[:], ALU.mult, ALU.mult), reads=[db], writes=[db])
                for fc in range(4):
                    S.op("dve", lambda e: e.tensor_scalar(KK[:, fc, :], A32[:, fc, :], self.cols[:, ka0 + fc:ka0 + fc + 1], OMK[:, fc:fc + 1],
                                                          ALU.mult, ALU.add), reads=[db, cb, self.constb], writes=[db])
                S.op("dve", lambda e: e.tensor_tensor(KM[:], PL[:, 4:8, :], KK[:], ALU.mult), reads=[db, PLB], writes=[db])
                S.op("dve", lambda e: e.tensor_tensor(SQ[:], PL[:, 0:4, :], KM[:], ALU.mult), reads=[db, PLB], writes=[db])
                for fc in range(4):
                    S.op("dve", lambda e: e.tensor_scalar(SQ[:, fc, :], SQ[:, fc, :], self.cols[:, rk0 + fc:rk0 + fc + 1], None, ALU.mult),
                         reads=[db, self.constb], writes=[db])
                pbn, pbnb = self.psum.get()
                S.op("pe", lambda e: e.matmul(pbn[:], BO[:], SQ[:].rearrange("p c t -> p (c t)"), start=True, stop=True), reads=[db, cb], writes=[pbnb])
                S.op("dve", lambda e: e.tensor_tensor(BON[:], pbn[:].rearrange("p (c t) -> p c t", c=4), PL[:, 8:12, :], ALU.mult),
                     reads=[pbnb, PLB], writes=[db])
                S.op("dve", lambda e: e.tensor_copy(RM[0:64, :, :, 0], PL[0:64, 0:4, :]), reads=[PLB, rmb], writes=[rmb])
                S.op("dve", lambda e: e.tensor_copy(RM[64:128, :, :, 1], PL[64:128, 0:4, :]), reads=[PLB, rmb], writes=[rmb])
                for tt in range(128):
                    pvb_t, pvbb = self.psum.get()

                    VD, vdb = VDr.get()
                    S.op("pool", lambda e: e.tensor_tensor(VD[:], ID2[:].unsqueeze(1).to_broadcast([128, 4, 64]),
                                                           PL[:, 8:12, tt:tt + 1].to_broadcast([128, 4, 64]), ALU.mult),
                         reads=[PLB, cb], writes=[vdb])
                    S.op("pe", lambda e: e.matmul(pvb_t[:, 0:256], BOb[:], VD[:].rearrange("p c v -> p (c v)"), start=True, stop=True),
                         reads=[vdb, cb], writes=[pvbb])
                    T2, t2b = T2r.get()
                    S.op("pool" if False else "dve", lambda e: e.tensor_tensor(
                        T2[:], pvb_t[:, 0:256].rearrange("p (c v) -> p c v", c=4), KM[:, :, tt:tt + 1].to_broadcast([128, 4, 64]), ALU.mult),
                        reads=[pvbb, db], writes=[t2b])
                    S.op("dve", lambda e: e.tensor_tensor(HK[:], H[:], KKN[:, :, tt:tt + 1].to_broadcast([128, 4, 64]), ALU.mult),
                         reads=[hb, db], writes=[hkb])
                    psa, psab = self.psum.get()
                    S.op("pe", lambda e: e.matmul(psa[:, 0:256], BOb[:], HK[:].rearrange("p c v -> p (c v)"), start=True, stop=True),
                         reads=[hkb, cb], writes=[psab])
                    S.op("dve", lambda e: e.tensor_tensor(T1[:], psa[:, 0:256].rearrange("p (c v) -> p c v", c=4),
                                                          NB[:, :, tt:tt + 1].to_broadcast([128, 4, 64]), ALU.mult),
                         reads=[psab, db], writes=[t1b])
                    S.op("dve", lambda e: e.tensor_tensor(H[:], H[:], WD[:, :, tt:tt + 1].to_broadcast([128, 4, 64]), ALU.mult),
                         reads=[hb, db], writes=[hb])
                    S.op("dve", lambda e: e.tensor_tensor(T1[:], T1[:], T2[:], ALU.add), reads=[t1b, t2b], writes=[t1b])
                    S.op("dve", lambda e: e.tensor_tensor(H[:], H[:], T1[:], ALU.add), reads=[hb, t1b], writes=[hb])
                    S.op("act", lambda e: e.copy(Hb[:], H[:]), reads=[hb], writes=[hbb])
                    py, pyb = self.psum.get()

                    def mmy(e):
                        for fc in range(4):
                            ins = e.matmul(py[0:2, fc * 64:(fc + 1) * 64], RM[:, fc, tt, :], Hb[:, fc, :], start=True, stop=True)
                        return ins
                    S.op("pe", mmy, reads=[hbb, rmb], writes=[pyb])
                    slot = tt % 2
                    S.op("act", lambda e: e.copy(YST[slot][0:2, 0, :], py[0:2, 0:256]), reads=[pyb], writes=[YSTB[slot]])
                    for hp in range(2):
                        S.dma(YTOK[tt:tt + 1, :, hp, :], YST[slot][hp:hp + 1, 0, :].rearrange("p (c v) -> p c v", c=4),
                              reads=[YSTB[slot], db], writes=[YTOKB])
                YT8 = YTOK.rearrange("t c h v -> t (c h) v")
                S.op("dve", lambda e: e.tensor_reduce(ST8[:], YT8, AX.X, ALU.add), reads=[YTOKB], writes=[yb])
                S.op("dve", lambda e: e.tensor_scalar(ST8[:], ST8[:], 1.0 / 64, None, ALU.mult), reads=[yb], writes=[yb])
                S.op("dve", lambda e: e.tensor_tensor(YC, YT8, ST8[:].unsqueeze(2).to_broadcast([128, 8, 64]), ALU.subtract),
                     reads=[YTOKB, yb], writes=[yb, db])
                S.op("dve", lambda e: e.tensor_tensor(YTOK.rearrange("t c h v -> t (c h) v"), YC, YC, ALU.mult), reads=[yb, YTOKB], writes=[YTOKB])
                S.op("dve", lambda e: e.tensor_reduce(ST8b[:], YT8, AX.X, ALU.add), reads=[YTOKB], writes=[yb])
                S.op("act", lambda e: e.activation(ST8b[:], ST8b[:], AF.Sqrt, bias=self.gneps_t[:], scale=1.0 / 64), reads=[yb, self.constb], writes=[yb])
                S.op("dve", lambda e: e.reciprocal(ST8b[:], ST8b[:]), reads=[yb], writes=[yb])
                S.op("dve", lambda e: e.tensor_tensor(YC, YC, ST8b[:].unsqueeze(2).to_broadcast([128, 8, 64]), ALU.mult), reads=[yb], writes=[yb])
                YCf = YC.rearrange("t a v -> t (a v)")
                S.op("dve", lambda e: e.tensor_tensor(YCf, YCf, GNG[:], ALU.mult), reads=[yb, cb], writes=[yb])
                S.op("dve", lambda e: e.tensor_tensor(YCf, YCf, GNB[:], ALU.add), reads=[yb, cb], writes=[yb])
                pyt, pytb = self.psum.get(); pg, pgb = self.psum.get()

                def mmt2(e):
                    for fc in range(4):
                        ins = e.transpose(pyt[:, fc * 128:(fc + 1) * 128], YC[:, 2 * fc:2 * fc + 2, :].rearrange("t a v -> t (a v)"), ident[:])
                    for fc in range(4):
                        ins = e.matmul(pg[:, fc * 128:(fc + 1) * 128], G2[:, fc * 128:(fc + 1) * 128], SGg[:], start=True, stop=True)
                    return ins
                S.op("pe", mmt2, reads=[yb, self.constb, db, cb], writes=[pytb, pgb])
                S.op("dve", lambda e: e.tensor_tensor(YF[:], pyt[:].rearrange("p (c t) -> p c t", c=4), BON[:], ALU.add), reads=[pytb, db], writes=[yb, db])
                S.op("dve", lambda e: e.tensor_tensor(self.Y[0][:, :, tsl], YF[:], pg[:].rearrange("p (c t) -> p c t", c=4), ALU.mult),
                     reads=[yb, db, pgb], writes=[self.YB[0][tcix]])
            S.full_barrier()
            self.st = old


    def rwkv_branch(self, w_rwkv, w2_d, a2_d, g2_d, gng_d, gnb_d, mk_d):
        S = self.S
        CN = COLS
        NT = S_LEN // 128
        CDEC = 0.6065306597126334
        with ExitStack() as st4:
            old, self.st = self.st, st4
            WR = self.sb("WR", [128, NCH, 1792], BF16); WRB = Buf()
            W2 = self.sb("W2A2", [128, 512], F32); A2 = W2; G2 = self.sb("G2", [128, 512], BF16)
            BO = self.sb("BO", [128, 128], F32)
            ID2 = self.sb("ID2", [128, 64], F32)
            OMK = self.sb("OMK", [128, 4], F32)
            MSK = self.sb("MSK", [128, 3, 128], BF16)
            ONE64 = self.sb("ONE64", [128, 64], F32)
            cb = Buf()
            self.load_w(WR[:], w_rwkv, WRB)
            S.dma(W2[0:64, :], w2_d, writes=[cb]); S.dma(A2[64:128, :], a2_d, writes=[cb]); S.dma(G2[:], g2_d, writes=[cb], queue="pool")
            S.dma(MSK[:], mk_d, writes=[cb], queue="pool")
            self.rmsnorm_to_hn("mix_norm")
            S.op("dve", lambda e: e.memset(BO[:], 0.0), reads=[cb], writes=[cb])
            S.op("dve", lambda e: e.memset(BO[0:64, 0:64], 1.0), reads=[cb], writes=[cb])
            S.op("dve", lambda e: e.memset(BO[64:128, 64:128], 1.0), reads=[cb], writes=[cb])
            S.op("dve", lambda e: e.memset(ONE64[:], 1.0), reads=[cb], writes=[cb])
            S.op("dve", lambda e: e.tensor_copy(ID2[0:64, :], self.ident_f[0:64, 0:64]), reads=[cb, self.constb], writes=[cb])
            S.op("dve", lambda e: e.tensor_copy(ID2[64:128, :], self.ident_f[64:128, 64:128]), reads=[cb, self.constb], writes=[cb])
            ka0 = CN["k_a"][0]
            S.op("dve", lambda e: e.tensor_scalar(OMK[:], self.cols[:, ka0:ka0 + 4], -1.0, 1.0, ALU.mult, ALU.add),
                 reads=[cb, self.constb], writes=[cb])
            P32 = self.sb("P32", [128, 14, 129], F32); P32B = Buf()
            DD = self.sb("DD", [128, 128], F32); DDB = Buf()
            CAR = self.sb("CAR", [128, 14, 1], F32)
            PL = P32[:, :, 1:129]; PLB = P32B
            TW = self.sb("TW", [64, 128], F32); SGg = self.sb("SGg", [128, 128], BF16)
            f32t = lambda n: self.sb(n, [128, 4, 128], F32)
            SIG = f32t("SIG"); CUM = f32t("CUM"); A32 = f32t("A32"); KK = f32t("KK"); SQ = f32t("SQ")
            KKN = f32t("KKN"); NB = f32t("NB"); KM = f32t("KM"); BON = f32t("BON")
            AH = self.sb("AH", [128, 4, 128], BF16); KH = self.sb("KH", [128, 4, 128], BF16)
            BR = self.sb("BR", [128, 4, 2, 128], BF16)
            AT = self.sb("AT", [128, 512], BF16); KTt = self.sb("KTt", [128, 512], BF16); VTOK = self.sb("VTOK", [128, 512], BF16)
            WB = self.sb("WB", [128, 8, 128], BF16); BU = self.sb("BU", [128, 8, 128], BF16)
            bf8 = lambda n: self.sb(n, [128, 8, 128], BF16)
            X0 = bf8("X0"); XT0 = bf8("XT0"); LKT = bf8("LKT"); GRA = bf8("GRA"); GRK = bf8("GRK"); TT = bf8("TT")
            XA1 = [self.sb("XA1_%d" % i, [128, 4, 128], BF16) for i in range(2)]
            XTA1 = [self.sb("XTA1_%d" % i, [128, 4, 128], BF16) for i in range(2)]
            TA1 = [self.sb("TA1_%d" % i, [128, 4, 128], BF16) for i in range(2)]
            RTm = self.sb("RTm", [128, 4, 2, 128], BF16)
            M0Ts = SIG[:].rearrange("p c t -> p (c t)").rearrange("p (a k) -> p a k", a=8)
            N0s = CUM[:].rearrange("p c t -> p (c t)").rearrange("p (a k) -> p a k", a=8)
            PCt = self.sb("PCt", [128, 2, 4], F32)
            H = self.sb("H", [128, 4, 64], F32); Hb = self.sb("Hb", [128, 2, 4, 64], BF16)
            nbb = Buf(); kmb = Buf(); sgb = Buf(); cub = Buf()
            YTOK = NB[:].rearrange("p c t -> p (c t)").rearrange("p (c h v) -> p c h v", c=4, h=2); YTOKB = nbb
            YC = KM[:].rearrange("p c t -> p (c t)").rearrange("p (a v) -> p a v", a=8)
            ST8 = self.sb("ST8", [128, 8], F32); ST8b = self.sb("ST8b", [128, 8], F32)
            YF = SQ
            db = Buf(); hb = Buf(); hbb = Buf(); gb_ = Buf(); chb = Buf(); tkb = Buf(); yb = Buf(); mnb = Buf(); rtb = Buf()
            S.op("pool", lambda e: e.memset(P32[:], 0.0), writes=[P32B])
            S.op("pool", lambda e: e.memset(RTm[:], 0.0), writes=[rtb])
            S.op("pool", lambda e: e.memset(H[:], 0.0), writes=[hb])
            mu0 = CN["mu"][0]; w00 = CN["w0"][0]; a00 = CN["a0"][0]; kk0 = CN["k_k"][0]; rk0 = CN["r_k"][0]
            gg0 = CN["gn_g"][0]; gb0 = CN["gn_b"][0]
            ident = self.ident_f
            c4 = lambda ap: ap.rearrange("p (c t) -> p c t", c=4)
            def emit_proj(i2):
                t0_ = i2 * 128
                tsl_ = slice(t0_, t0_ + 128)
                hreads_ = [self.HNB[c][t0_ // TC] for c in range(NCH)]
                for cg in range(4):
                    c0 = cg * 4
                    n = min(4, 14 - c0)
                    p, pb = self.psum.get()

                    def mm(e):
                        for cc in range(n):
                            for k in range(NCH):
                                ins = e.matmul(p[:, cc * 128:(cc + 1) * 128], WR[:, k, (c0 + cc) * 128:(c0 + cc + 1) * 128],
                                               self.HN[:, k, tsl_], start=(k == 0), stop=(k == NCH - 1))
                        return ins
                    S.op("pe", mm, reads=hreads_ + [WRB], writes=[pb])
                    S.op("act", lambda e: e.copy(P32[:, c0:c0 + n, 1:129], p[:, 0:n * 128].rearrange("p (c t) -> p c t", c=n)),
                         reads=[pb], writes=[P32B])

            def lerp_list(i2):
                ops = []
                ops.append(lambda: S.op("pool", lambda e: e.tensor_copy(CAR[:], P32[:, :, 128:129]), reads=[P32B], writes=[DDB]))
                for c in range(14):
                    def one(c=c):
                        S.op("pool", lambda e: e.tensor_tensor(DD[:], P32[:, c, 0:128], P32[:, c, 1:129], ALU.subtract), reads=[P32B, DDB], writes=[DDB])
                        S.op("pool", lambda e: e.tensor_tensor(DD[:], DD[:], self.cols[:, mu0 + c:mu0 + c + 1].to_broadcast([128, 128]), ALU.mult),
                             reads=[DDB, self.constb], writes=[DDB])
                        S.op("pool", lambda e: e.tensor_tensor(P32[:, c, 1:129], P32[:, c, 1:129], DD[:], ALU.add), reads=[DDB, P32B], writes=[P32B])
                    ops.append(one)
                ops.append(lambda: S.op("pool", lambda e: e.tensor_copy(P32[:, :, 0:1], CAR[:]), reads=[P32B, DDB], writes=[P32B]))
                return ops

            pending = []
            for i in range(self._rk_tiles):
                t0 = i * 128
                tcix = t0 // TC
                tsl = slice(t0, t0 + 128)
                hreads = [self.HNB[c][tcix] for c in range(NCH)]
                if i == 0:
                    emit_proj(0)
                    for fn_ in lerp_list(0):
                        fn_()
                for fn_ in pending:
                    fn_()
                pending = []
                S.op("act", lambda e: e.activation(TW[:], PL[0:64, 12, :], AF.Tanh), reads=[PLB], writes=[db])
                S.op("act", lambda e: e.activation(SGg[:], PL[:, 13, :], AF.Sigmoid), reads=[PLB], writes=[db])
                pz, pzb = self.psum.get(); pa, pab = self.psum.get()

                def mmz(e):
                    for fc in range(4):
                        ins = e.matmul(pz[:, fc * 128:(fc + 1) * 128], W2[0:64, fc * 128:(fc + 1) * 128], TW[:], start=True, stop=True)
                    return ins

                def mma(e):
                    for fc in range(4):
                        ins = e.matmul(pa[:, fc * 128:(fc + 1) * 128], A2[64:128, fc * 128:(fc + 1) * 128], PL[64:128, 12, :], start=True, stop=True)
                    return ins
                S.op("pe", mmz, reads=[db, cb], writes=[pzb])
                S.op("pe", mma, reads=[PLB, cb], writes=[pab])
                for fc in range(4):
                    S.op("act", lambda e: e.activation(SIG[:, fc, :], pz[:, fc * 128:(fc + 1) * 128], AF.Sigmoid,
                                                       bias=self.cols[:, w00 + fc:w00 + fc + 1]), reads=[pzb, self.constb], writes=[db, sgb])
                    S.op("act", lambda e: e.activation(A32[:, fc, :], pa[:, fc * 128:(fc + 1) * 128], AF.Sigmoid,
                                                       bias=self.cols[:, a00 + fc:a00 + fc + 1]), reads=[pab, self.constb], writes=[db])
                bc4 = lambda c0_: self.cols[:, c0_:c0_ + 4].unsqueeze(2).to_broadcast([128, 4, 128])
                S.op("dve", lambda e: e.tensor_tensor(KK[:], PL[:, 4:8, :], bc4(kk0), ALU.mult), reads=[PLB, self.constb], writes=[db])
                S.op("dve", lambda e: e.tensor_tensor(SQ[:], KK[:], KK[:], ALU.mult), reads=[db], writes=[db])
                pss, pssb = self.psum.get()
                S.op("pe", lambda e: e.matmul(pss[:], BO[:], SQ[:].rearrange("p c t -> p (c t)"), start=True, stop=True), reads=[db, cb], writes=[pssb])
                S.op("act", lambda e: e.activation(SQ[:], c4(pss[:]), AF.Sqrt), reads=[pssb, db], writes=[db])
                S.op("dve", lambda e: e.tensor_scalar(SQ[:], SQ[:], 1e-12, None, ALU.max), reads=[db], writes=[db])
                S.op("dve", lambda e: e.reciprocal(SQ[:], SQ[:]), reads=[db], writes=[db])
                S.op("dve", lambda e: e.tensor_tensor(KKN[:], KK[:], SQ[:], ALU.mult), reads=[db], writes=[db])
                S.op("dve", lambda e: e.tensor_tensor(NB[:], KKN[:], A32[:], ALU.mult), reads=[db], writes=[db, nbb])
                S.op("pool", lambda e: e.tensor_tensor(KK[:], A32[:], bc4(ka0), ALU.mult), reads=[db, self.constb], writes=[db])
                S.op("pool", lambda e: e.tensor_tensor(KK[:], KK[:], OMK[:].unsqueeze(2).to_broadcast([128, 4, 128]), ALU.add), reads=[db, cb], writes=[db])
                S.op("dve", lambda e: e.tensor_tensor(KM[:], PL[:, 4:8, :], KK[:], ALU.mult), reads=[db, PLB], writes=[db, kmb])
                S.op("dve", lambda e: e.tensor_tensor(SQ[:], PL[:, 0:4, :], KM[:], ALU.mult), reads=[db, PLB, kmb], writes=[db])
                S.op("pool", lambda e: e.tensor_tensor(SQ[:], SQ[:], bc4(rk0), ALU.mult), reads=[db, self.constb], writes=[db])
                pbn, pbnb = self.psum.get()
                S.op("pe", lambda e: e.matmul(pbn[:], BO[:], SQ[:].rearrange("p c t -> p (c t)"), start=True, stop=True), reads=[db, cb], writes=[pbnb])
                S.op("dve", lambda e: e.tensor_tensor(BON[:], c4(pbn[:]), PL[:, 8:12, :], ALU.mult), reads=[pbnb, PLB], writes=[db])
                for fc in range(4):
                    for c2 in range(2):
                        cs = slice(c2 * 64, (c2 + 1) * 64)
                        S.op("dve", lambda e: e.tensor_tensor_scan(CUM[:, fc, cs], ONE64[:], SIG[:, fc, cs], 0.0, ALU.mult, ALU.add),
                             reads=[db, cb, sgb], writes=[db, cub])
                S.op("pool", lambda e: e.tensor_tensor(SQ[:], CUM[:], SIG[:], ALU.subtract), reads=[db, sgb, cub], writes=[db])
                S.op("act", lambda e: e.activation(A32[:], CUM[:], AF.Exp, scale=CDEC), reads=[db, cub], writes=[db])
                S.op("act", lambda e: e.activation(CUM[:], CUM[:], AF.Exp, scale=-CDEC), reads=[db], writes=[db, cub])
                S.op("act", lambda e: e.activation(SQ[:], SQ[:], AF.Exp, scale=-CDEC), reads=[db], writes=[db])
                S.op("dve", lambda e: e.tensor_copy(PCt[:, 0, :], CUM[:, :, 63]), reads=[db, chb, cub], writes=[chb])
                S.op("dve", lambda e: e.tensor_copy(PCt[:, 1, :], CUM[:, :, 127]), reads=[db, chb, cub], writes=[chb])
                S.op("dve", lambda e: e.tensor_tensor(NB[:], NB[:], A32[:], ALU.mult), reads=[db], writes=[db, nbb])
                S.op("dve", lambda e: e.tensor_tensor(KM[:], KM[:], A32[:], ALU.mult), reads=[db], writes=[db, kmb])
                S.op("dve", lambda e: e.tensor_tensor(KKN[:], KKN[:], SQ[:], ALU.mult), reads=[db], writes=[db])
                S.op("dve", lambda e: e.tensor_tensor(KK[:], PL[:, 0:4, :], CUM[:], ALU.mult), reads=[db, PLB, cub], writes=[db])
                S.op("act", lambda e: e.copy(AH[:], NB[:]), reads=[db, gb_, nbb], writes=[gb_])
                S.op("act", lambda e: e.copy(KH[:], KM[:]), reads=[db, gb_, kmb], writes=[gb_])
                S.op("pool", lambda e: e.tensor_copy(BR[:, :, 0, :], KKN[:]), reads=[db, gb_], writes=[gb_])
                S.op("pool", lambda e: e.tensor_copy(BR[:, :, 1, :], KK[:]), reads=[db, gb_], writes=[gb_])
                if "dumpah" in self.debug and i == 0:
                    for nm, tl in (("ah", AH), ("kh", KH), ("br", BR)):
                        o_ = self.dout("dbg_" + nm, [128, tl[:].rearrange("p ... -> p (...)").shape[1] if False else (512 if nm != "br" else 1024)])
                        S.dma(o_, tl[:].rearrange("p c t -> p (c t)") if nm != "br" else tl[:].rearrange("p c a t -> p (c a t)"), reads=[gb_], queue="pool")
                    for nm, tl in (("nb", NB), ("km", KM), ("kkn", KKN), ("en", A32), ("ep", CUM)):
                        o_ = self.dout("dbg_" + nm, [128, 512])
                        S.dma(o_, tl[:].rearrange("p c t -> p (c t)"), reads=[db, nbb, kmb, cub])
                for src, dst_fn in ((NB, None), (KM, None), (KKN, None), (None, None)):
                    pass
                tr_jobs = [(lambda fc: NB[:, fc, :], "AT"), (lambda fc: KM[:, fc, :], "KT"),
                           (lambda fc: KKN[:, fc, :], "BT"), (lambda fc: PL[:, 8 + fc, :], "VT")]
                for srcf, kind in tr_jobs:
                    ptr, ptrb = self.psum.get()

                    def mmt(e):
                        for fc in range(4):
                            ins = e.transpose(ptr[:, fc * 128:(fc + 1) * 128], srcf(fc), ident[:])
                        return ins
                    S.op("pe", mmt, reads=[db, PLB, self.constb, nbb, kmb], writes=[ptrb])
                    if kind == "AT":
                        S.op("act", lambda e: e.copy(AT[:], ptr[:]), reads=[ptrb, tkb], writes=[tkb])
                    elif kind == "KT":
                        S.op("dve", lambda e: e.tensor_copy(KTt[:], ptr[:]), reads=[ptrb, tkb], writes=[tkb])
                    elif kind == "BT":
                        S.op("act", lambda e: e.activation(WB[:, :, 0:64], ptr[:].rearrange("p (h k) -> p h k", h=8), AF.Copy, scale=-1.0),
                             reads=[ptrb, tkb], writes=[tkb])
                    else:
                        S.op("dve", lambda e: e.tensor_copy(VTOK[:], ptr[:]), reads=[ptrb, tkb], writes=[tkb])
                if self._rk_stage <= 0:
                    continue
                for fc in range(4):
                    ka_, ab0, ab1 = self.psum.get_pair_idx()
                    kb_, bb0, bb1 = self.psum.get_pair_idx()
                    PA = self.PS[:, ka_:ka_ + 2, :]; PB = self.PS[:, kb_:kb_ + 2, :]

                    def mmg(e):
                        for h2 in range(2):
                            rs = slice(h2 * 64, (h2 + 1) * 64)
                            brr = BR[rs, fc, :, :].rearrange("p a t -> p (a t)")
                            e.matmul(PA[:, h2, 0:256], AH[rs, fc, :], brr, start=True, stop=True)
                            e.matmul(PA[:, h2, 256:512], KH[rs, fc, :], brr, start=True, stop=True)
                            ins = e.matmul(PB[:, h2, 0:128], BR[rs, fc, 0, :], AH[rs, fc, :], start=True, stop=True)
                        return ins
                    S.op("pe", mmg, reads=[gb_], writes=[ab0, ab1, bb0, bb1])
                    hs = slice(2 * fc, 2 * fc + 2)
                    PAv = PA.rearrange("p h (q b t) -> p h q b t", q=2, b=2)
                    mk = lambda j: MSK[:, j, :].unsqueeze(1).to_broadcast([128, 2, 128])
                    S.op("dve", lambda e: e.tensor_tensor(X0[:, hs, :], PAv[:, :, 0, 0, :], mk(0), ALU.mult), reads=[ab0, ab1, cb, mnb], writes=[mnb])
                    S.op("dve", lambda e: e.tensor_tensor(GRA[:, hs, :], PAv[:, :, 0, 1, :], mk(2), ALU.mult), reads=[ab0, ab1, cb, mnb], writes=[mnb])
                    S.op("dve", lambda e: e.tensor_tensor(LKT[:, hs, :], PAv[:, :, 1, 0, :], mk(0), ALU.mult), reads=[ab0, ab1, cb, mnb], writes=[mnb])
                    S.op("dve", lambda e: e.tensor_tensor(GRK[:, hs, :], PAv[:, :, 1, 1, :], mk(2), ALU.mult), reads=[ab0, ab1, cb, mnb], writes=[mnb])
                    S.op("dve", lambda e: e.tensor_tensor(XT0[:, hs, :], PB[:, :, 0:128], mk(1), ALU.mult), reads=[bb0, bb1, cb, mnb], writes=[mnb])
                if self._rk_stage <= 1:
                    continue
                if i + 1 < self._rk_tiles:
                    emit_proj(i + 1)
                    pending = lerp_list(i + 1)
                hst = []
                for half in range(2):
                    h0 = half * 4
                    st_ = dict(xb=Buf(), xtb=Buf(), tb=Buf(),
                               xbufs=[X0[:, h0:h0 + 4, :], XA1[half][:]], xtbufs=[XT0[:, h0:h0 + 4, :], XTA1[half][:]],
                               tbufs=[TA1[half][:], TT[:, h0:h0 + 4, :]])
                    hst.append(st_)
                    S.op("pool", lambda e: e.tensor_tensor(st_["tbufs"][0], st_["xbufs"][0], ident[:].unsqueeze(1).to_broadcast([128, 4, 128]), ALU.add),
                         reads=[mnb, self.constb, st_["tb"]], writes=[st_["tb"]])
                for lv in range(1, 6):
                    for half in range(2):
                        st_ = hst[half]
                        xb_, xtb_, tb_ = st_["xb"], st_["xtb"], st_["tb"]
                        Xp, XTp, Tp = st_["xbufs"][(lv - 1) % 2], st_["xtbufs"][(lv - 1) % 2], st_["tbufs"][(lv - 1) % 2]
                        Xn, XTn, Tn = st_["xbufs"][lv % 2], st_["xtbufs"][lv % 2], st_["tbufs"][lv % 2]
                        pxt, pxtb = self.psum.get()

                        def mmxt(e):
                            for j in range(4):
                                ins = e.matmul(pxt[:, j * 128:(j + 1) * 128], Xp[:, j, :], XTp[:, j, :], start=True, stop=True)
                            return ins
                        S.op("pe", mmxt, reads=[mnb, xb_, xtb_], writes=[pxtb])
                        if lv < 5:
                            px, pxb = self.psum.get()

                            def mmx(e):
                                for j in range(4):
                                    ins = e.matmul(px[:, j * 128:(j + 1) * 128], XTp[:, j, :], Xp[:, j, :], start=True, stop=True)
                                return ins
                            S.op("pe", mmx, reads=[mnb, xb_, xtb_], writes=[pxb])
                        S.op("act", lambda e: e.copy(XTn, c4(pxt[:])), reads=[pxtb, xtb_, mnb], writes=[xtb_])
                        if lv < 5:
                            S.op("act", lambda e: e.copy(Xn, c4(px[:])), reads=[pxb, xb_, mnb], writes=[xb_])
                        ptt, pttb = self.psum.get()

                        def mmtt(e):
                            for j in range(4):
                                ins = e.matmul(ptt[:, j * 128:(j + 1) * 128], XTn[:, j, :], Tp[:, j, :], start=True, stop=True)
                            return ins
                        S.op("pe", mmtt, reads=[xtb_, tb_], writes=[pttb])
                        S.op("dve", lambda e: e.tensor_tensor(Tn, c4(ptt[:]), Tp, ALU.add), reads=[pttb, tb_, mnb], writes=[tb_] + ([mnb] if lv == 5 else []))
                        for _ in range(2):
                            if pending:
                                pending.pop(0)()
                if self._rk_stage <= 2:
                    continue
                plk, plkb = self.psum.get()

                def mmlk(e):
                    for h in range(8):
                        ins = e.matmul(plk[:, h * 64:(h + 1) * 64], LKT[:, h, :], VTOK[:, h * 64:(h + 1) * 64], start=True, stop=True)
                    return ins
                S.op("pe", mmlk, reads=[mnb, tkb], writes=[plkb])
                S.op("act", lambda e: e.copy(WB[:, :, 64:128], plk[:].rearrange("p (h v) -> p h v", h=8)), reads=[plkb, tkb], writes=[tkb])
                for half in range(2):
                    pbu, pbub = self.psum.get()

                    def mmbu(e):
                        for j in range(4):
                            h = half * 4 + j
                            ins = e.matmul(pbu[:, j * 128:(j + 1) * 128], TT[:, h, :], WB[:, h, :], start=True, stop=True)
                        return ins
                    S.op("pe", mmbu, reads=[mnb, tkb], writes=[pbub])
                    S.op("act", lambda e: e.copy(BU[:, half * 4:half * 4 + 4, :], c4(pbu[:])), reads=[pbub, chb], writes=[chb])
                if self._rk_stage <= 3:
                    continue
                prt, prtb = self.psum.get()
                km_, mb0, mb1 = self.psum.get_pair_idx()
                PM = self.PS[:, km_:km_ + 2, :]

                def mmrt(e):
                    for h in range(8):
                        rs = slice((h % 2) * 64, (h % 2) * 64 + 64)
                        fc = h // 2
                        ins = e.matmul(prt[rs, fc * 128:(fc + 1) * 128], BU[:, h, 0:64], GRA[:, h, :], start=True, stop=True)
                    return ins

                def mmmn(e):
                    for c2 in range(2):
                        cr = slice(c2 * 64, (c2 + 1) * 64)
                        for h in range(8):
                            rs = slice((h % 2) * 64, (h % 2) * 64 + 64)
                            fc = h // 2
                            o = fc * 64
                            e.matmul(PM[rs, c2, o:o + 64], BU[cr, h, 0:64], AT[cr, h * 64:(h + 1) * 64], start=True, stop=True)
                            e.matmul(PM[rs, c2, 256 + o:256 + o + 64], AT[cr, h * 64:(h + 1) * 64], BU[cr, h, 64:128], start=True, stop=False)
                            ins = e.matmul(PM[rs, c2, 256 + o:256 + o + 64], KTt[cr, h * 64:(h + 1) * 64], VTOK[cr, h * 64:(h + 1) * 64], start=False, stop=True)
                    return ins
                S.op("pe", mmrt, reads=[chb, mnb], writes=[prtb])
                S.op("pe", mmmn, reads=[chb, tkb], writes=[mb0, mb1])
                prv = c4(prt[:])
                S.op("dve", lambda e: e.tensor_tensor(RTm[:, :, 0, 0:64], prv[:, :, 0:64], KK[:, :, 0:64], ALU.add), reads=[prtb, db, rtb], writes=[rtb])
                S.op("dve", lambda e: e.tensor_tensor(RTm[:, :, 1, 64:128], prv[:, :, 64:128], KK[:, :, 64:128], ALU.add), reads=[prtb, db, rtb], writes=[rtb])
                M0v = M0Ts.rearrange("p (a c) k -> p a c k", a=2)
                N0v = N0s.rearrange("p (a c) k -> p a c k", a=2)
                S.op("dve", lambda e: e.tensor_tensor(M0v, PM[:, :, 0:256].rearrange("p a (c k) -> p a c k", c=4),
                                                      ID2[:].unsqueeze(1).unsqueeze(1).to_broadcast([128, 2, 4, 64]), ALU.add),
                     reads=[mb0, mb1, cb, chb], writes=[chb, sgb])
                S.op("act", lambda e: e.copy(N0v, PM[:, :, 256:512].rearrange("p a (c k) -> p a c k", c=4)), reads=[mb0, mb1, chb], writes=[chb, cub])
                if self._rk_stage <= 4:
                    continue
                for c2 in range(2):
                    S.op("act", lambda e: e.copy(Hb[:, c2, :, :], H[:]), reads=[hb, hbb], writes=[hbb])
                    phe, pheb = self.psum.get(); pho, phob = self.psum.get()

                    def mmh(e):
                        for par, bank in ((0, phe), (1, pho)):
                            rs = slice(par * 64, par * 64 + 64)
                            for fc in range(4):
                                ins = e.matmul(bank[rs, fc * 64:(fc + 1) * 64], M0Ts[rs, c2 * 4 + fc, :], H[rs, fc, :], start=True, stop=True)
                        return ins
                    S.op("pe", mmh, reads=[chb, hb, sgb], writes=[pheb, phob])
                    S.op("dve", lambda e: e.tensor_tensor(H[0:64], phe[0:64, 0:256].rearrange("p (c v) -> p c v", c=4), N0s[0:64, c2 * 4:c2 * 4 + 4, :], ALU.add),
                         reads=[pheb, chb, cub, hb], writes=[hb])
                    S.op("dve", lambda e: e.tensor_tensor(H[64:128], pho[64:128, 0:256].rearrange("p (c v) -> p c v", c=4), N0s[64:128, c2 * 4:c2 * 4 + 4, :], ALU.add),
                         reads=[phob, chb, cub, hb], writes=[hb])
                    S.op("dve", lambda e: e.tensor_tensor(H[:], H[:], PCt[:, c2, :].unsqueeze(2).to_broadcast([128, 4, 64]), ALU.mult),
                         reads=[chb, hb], writes=[hb])
                if self._rk_stage <= 5:
                    continue
                ky_, yb0, yb1 = self.psum.get_pair_idx()
                PY = self.PS[:, ky_:ky_ + 2, :]

                def mmy(e):
                    for par in range(2):
                        rs = slice(par * 64, par * 64 + 64)
                        for fc in range(4):
                            h = 2 * fc + par
                            o = PY[:, par, fc * 64:(fc + 1) * 64]
                            e.matmul(o, GRA[:, h, :], BU[:, h, 64:128], start=True, stop=False)
                            e.matmul(o, GRK[:, h, :], VTOK[:, h * 64:(h + 1) * 64], start=False, stop=False)
                            e.matmul(o, RTm[rs, fc, 0, :], Hb[rs, 0, fc, :], start=False, stop=False)
                            ins = e.matmul(o, RTm[rs, fc, 1, :], Hb[rs, 1, fc, :], start=False, stop=True)
                    return ins
                S.op("pe", mmy, reads=[mnb, chb, tkb, rtb, hbb], writes=[yb0, yb1])
                S.op("act", lambda e: e.copy(YTOK.rearrange("t c h v -> t h c v"), PY[:, :, 0:256].rearrange("t h (c v) -> t h c v", c=4)),
                     reads=[yb0, yb1, YTOKB], writes=[YTOKB])
                if self._rk_stage <= 6:
                    continue
                YT8 = YTOK.rearrange("t c h v -> t (c h) v")
                S.op("dve", lambda e: e.tensor_reduce(ST8[:], YT8, AX.X, ALU.add), reads=[YTOKB], writes=[yb])
                S.op("dve", lambda e: e.tensor_scalar(ST8[:], ST8[:], 1.0 / 64, None, ALU.mult), reads=[yb], writes=[yb])
                S.op("dve", lambda e: e.tensor_tensor(YC, YT8, ST8[:].unsqueeze(2).to_broadcast([128, 8, 64]), ALU.subtract),
                     reads=[YTOKB, yb], writes=[yb, kmb])
                S.op("pool", lambda e: e.tensor_tensor(YT8, YC, YC, ALU.mult), reads=[yb, YTOKB, kmb], writes=[YTOKB])
                S.op("dve", lambda e: e.tensor_reduce(ST8b[:], YT8, AX.X, ALU.add), reads=[YTOKB], writes=[yb])
                S.op("act", lambda e: e.activation(ST8b[:], ST8b[:], AF.Sqrt, bias=self.gneps_t[:], scale=1.0 / 64), reads=[yb, self.constb], writes=[yb])
                S.op("dve", lambda e: e.reciprocal(ST8b[:], ST8b[:]), reads=[yb], writes=[yb])
                S.op("dve", lambda e: e.tensor_tensor(YC, YC, ST8b[:].unsqueeze(2).to_broadcast([128, 8, 64]), ALU.mult), reads=[yb], writes=[yb, kmb])
                pyt, pytb = self.psum.get(); pg, pgb = self.psum.get()

                def mmt2(e):
                    for fc in range(4):
                        ins = e.transpose(pyt[:, fc * 128:(fc + 1) * 128], YC[:, 2 * fc:2 * fc + 2, :].rearrange("t a v -> t (a v)"), ident[:])
                    for fc in range(4):
                        ins = e.matmul(pg[:, fc * 128:(fc + 1) * 128], G2[:, fc * 128:(fc + 1) * 128], SGg[:], start=True, stop=True)
                    return ins
                S.op("pe", mmt2, reads=[yb, self.constb, db, cb, kmb], writes=[pytb, pgb])
                for fc in range(4):
                    S.op("dve", lambda e: e.tensor_scalar(YF[:, fc, :], pyt[:, fc * 128:(fc + 1) * 128], self.cols[:, gg0 + fc:gg0 + fc + 1],
                                                          self.cols[:, gb0 + fc:gb0 + fc + 1], ALU.mult, ALU.add),
                         reads=[pytb, db, self.constb], writes=[db])
                S.op("pool", lambda e: e.tensor_tensor(YF[:], YF[:], BON[:], ALU.add), reads=[db], writes=[db])
                S.op("dve", lambda e: e.tensor_tensor(self.Y[0][:, :, tsl], YF[:], c4(pg[:]), ALU.mult),
                     reads=[db, pgb], writes=[self.YB[0][tcix]])
            S.full_barrier()
            self.st = old

    def nsa_branch(self, d):
        S = self.S
        NT = S_LEN // 128
        with ExitStack() as st4:
            old, self.st = self.st, st4
            cb = Buf()
            KT = self.sb("KT", [128, 2, S_LEN], BF16); KTB = Buf()
            VT = self.sb("VT", [128, NT, 256], BF16); VTB = Buf()
            KC = self.sb("KC", [128, 127], BF16); VC = self.sb("VC", [128, 128], BF16); kcb = Buf()
            BM = self.sb("BM", [128, 3, 2, 512], BF16)
            BVC = self.sb("BVC", [32, 2, 512], BF16)
            stB = ExitStack(); self.st = stB
            KCMP = self.sb("KCMP", [128, S_LEN], BF16); VCT = self.sb("VCT", [128, S_LEN], BF16)
            WKVx = self.sb("WKVx", [128, 8192], BF16); wkvb = Buf()
            WKV = WKVx[:, 0:NCH * 768].rearrange("p (k n) -> p k n", k=NCH)
            W1v = WKVx[:].rearrange("p (l m) -> p l m", l=32); w1vb = wkvb
            W1k = self.Y[1][:].rearrange("p c t -> p (c t)").rearrange("p (l m) -> p l m", l=32); w1kb = Buf()
            PET2 = self.sb("PET2", [128, 2, 32], BF16); W2D2 = self.sb("W2D2", [128, 2, 2, 128], BF16); cwb = Buf()
            HID = self.sb("HID", [128, 2, 127], BF16)
            ZZ = self.sb("ZZ", [128, 127], F32); Z2 = self.sb("Z2", [128, 127], F32); BC = self.sb("BCc", [128, 1], F32)
            zb = Buf(); hb_ = Buf()
            self.load_w(WKV, d["w_kvn"], wkvb)
            w1k_d = d["cmp_w1"][0].rearrange("(l dd) m -> dd l m", dd=64)
            S.dma(W1k[0:64], w1k_d, writes=[w1kb] + self.YB[1], queue="pool"); S.dma(W1k[64:128], w1k_d, writes=[w1kb] + self.YB[1], queue="pool")
            for kv in range(2):
                S.dma(PET2[0:64, kv, :], d["cmp_peT"][kv], writes=[cwb], queue="pool"); S.dma(PET2[64:128, kv, :], d["cmp_peT"][kv], writes=[cwb], queue="pool")
                w2v = d["cmp_w2"][kv].rearrange("(c p) n -> p c n", p=128)
                S.dma(W2D2[:, kv, :, 0:64], w2v, writes=[cwb], queue="pool"); S.dma(W2D2[:, kv, :, 64:128], w2v, writes=[cwb], queue="pool")
            for tc in range(NTC):
                ts = slice(tc * TC, (tc + 1) * TC)
                hreads = [self.HNB[c][tc] for c in range(NCH)]
                for dst, col in ((KCMP[:, ts], 0), (VCT[:, ts], 128), (KT[:, 0, ts], 256), (KT[:, 1, ts], 512)):
                    p, pb = self.psum.get()

                    def mm(e):
                        for k in range(NCH):
                            ins = e.matmul(p[:], WKV[:, k, col:col + 128], self.HN[:, k, ts], start=(k == 0), stop=(k == NCH - 1))
                        return ins
                    S.op("pe", mm, reads=hreads + [wkvb], writes=[pb])
                    S.op("act", lambda e: e.copy(dst, p[:]), reads=[pb], writes=[KTB])
                for tl in range(4):
                    tile = tc * 4 + tl
                    tq = slice(tile * 128, (tile + 1) * 128)
                    p, pb = self.psum.get()

                    def mm(e):
                        for k in range(NCH):
                            e.matmul(p[:, 0:128], self.HN[:, k, tq], WKV[:, k, 384:512], start=(k == 0), stop=(k == NCH - 1))
                        for k in range(NCH):
                            ins = e.matmul(p[:, 128:256], self.HN[:, k, tq], WKV[:, k, 640:768], start=(k == 0), stop=(k == NCH - 1))
                        return ins
                    S.op("pe", mm, reads=hreads + [wkvb], writes=[pb])
                    S.op("dve", lambda e: e.tensor_copy(VT[:, tile, :], p[:, 0:256]), reads=[pb], writes=[VTB])
            w1v_d = d["cmp_w1"][1].rearrange("(l dd) m -> dd l m", dd=64)
            S.dma(W1v[0:64], w1v_d, writes=[w1vb], queue="pool"); S.dma(W1v[64:128], w1v_d, writes=[w1vb], queue="pool")
            for kv in range(2):
                W1 = W1k if kv == 0 else W1v
                wb = w1kb if kv == 0 else w1vb
                PET = PET2[:, kv, :]
                W2D = W2D2[:, kv, :, :]
                yrd = self.YB[1] if kv == 0 else []
                SRC = KCMP if kv == 0 else VCT
                for g in range(2):
                    gs = slice(g * 64, (g + 1) * 64)
                    for mc in range(2):
                        ph, phb = self.psum.get(); pbias, pbb = self.psum.get()

                        def mm(e):
                            for l in range(32):
                                ins = e.matmul(ph[:, 0:127], W1[gs, l, mc * 128:(mc + 1) * 128], SRC[gs, l:l + 16 * 126 + 1:16],
                                               start=(l == 0), stop=(l == 31))
                            return ins

                        def mmb(e):
                            for l in range(32):
                                ins = e.matmul(pbias[:, 0:1], W1[gs, l, mc * 128:(mc + 1) * 128], PET[gs, l:l + 1], start=(l == 0), stop=(l == 31))
                            return ins
                        S.op("pe", mm, reads=[wb, KTB] + yrd, writes=[phb])
                        S.op("pe", mmb, reads=[wb, cwb] + yrd, writes=[pbb])
                        S.op("act", lambda e: e.copy(BC[:], pbias[:, 0:1]), reads=[pbb, zb], writes=[zb])
                        S.op("dve", lambda e: e.tensor_scalar(ZZ[:], ph[:, 0:127], BC[:, 0:1], None, ALU.add), reads=[phb, zb], writes=[zb])
                        S.op("dve", lambda e: e.tensor_tensor(Z2[:], ZZ[:], ZZ[:], ALU.mult), reads=[zb], writes=[zb])
                        S.op("dve", lambda e: e.tensor_scalar(Z2[:], Z2[:], 0.044715, 1.0, ALU.mult, ALU.add), reads=[zb], writes=[zb])
                        S.op("dve", lambda e: e.tensor_tensor(Z2[:], Z2[:], ZZ[:], ALU.mult), reads=[zb], writes=[zb])
                        S.op("act", lambda e: e.activation(Z2[:], Z2[:], AF.Sigmoid, scale=1.5957691216057308), reads=[zb], writes=[zb])
                        S.op("dve", lambda e: e.tensor_tensor(HID[:, mc, :], ZZ[:], Z2[:], ALU.mult), reads=[zb, hb_], writes=[hb_])
                    po, pob = self.psum.get()
                    if kv == 0:
                        def mm2(e):
                            for mc in range(2):
                                ins = e.matmul(po[:, 0:127], W2D[:, mc, :], HID[:, mc, :], start=(mc == 0), stop=(mc == 1))
                            return ins
                        S.op("pe", mm2, reads=[hb_, cwb], writes=[pob])
                        S.op("act", lambda e: e.copy(KC[gs, :], po[gs, 0:127]), reads=[pob], writes=[kcb])
                    else:
                        def mm2(e):
                            for mc in range(2):
                                ins = e.matmul(po[0:127, 0:64], HID[:, mc, :], W2D[:, mc, 0:64], start=(mc == 0), stop=(mc == 1))
                            return ins
                        S.op("pe", mm2, reads=[hb_, cwb], writes=[pob])
                        S.op("act", lambda e: e.copy(VC[0:127, gs], po[0:127, 0:64]), reads=[pob], writes=[kcb])
            stA = ExitStack(); self.st = stA
            G1 = self.sb("G1", [128, 2, 512], F32); G2_ = self.sb("G2b", [128, 2, 512], F32); MK = self.sb("MK", [128, 128], F32)
            gb = Buf()
            S.dma(G2_[:], d["t31"], writes=[gb])
            for kind in range(3):
                S.dma(G1[:], d["bmg"][kind], reads=[gb], writes=[gb])
                S.dma(MK[:], d["msk"][kind], reads=[gb], writes=[gb])
                S.op("dve", lambda e: e.tensor_tensor(G1[:], G1[:], G2_[:], ALU.subtract), reads=[gb], writes=[gb])
                S.op("dve", lambda e: e.tensor_tensor(BM[:, kind, :, :].rearrange("p g (j q) -> p (g j) q", j=4),
                                                      G1[:].rearrange("p g (j q) -> p (g j) q", j=4),
                                                      MK[:].unsqueeze(1).to_broadcast([128, 8, 128]), ALU.add), reads=[gb], writes=[cb, gb])
            S.dma(G1[0:32, :, :], d["bvcg"], reads=[gb], writes=[gb])
            S.dma(MK[0:32, :], d["mskc"], reads=[gb], writes=[gb])
            S.op("dve", lambda e: e.tensor_tensor(G1[0:32], G1[0:32], G2_[0:32], ALU.subtract), reads=[gb], writes=[gb])
            S.op("dve", lambda e: e.tensor_tensor(BVC[:].rearrange("p g (j q) -> p (g j) q", j=4),
                                                  G1[0:32].rearrange("p g (j q) -> p (g j) q", j=4),
                                                  MK[0:32, :].unsqueeze(1).to_broadcast([32, 8, 128]), ALU.add), reads=[gb], writes=[cb, gb])
            S.full_barrier()
            stA.close()
            self.st = stB
            S.full_barrier()
            stB.close(); self.st = st4
            if "kcvc" in self.debug:
                okc = self.dout("dbg_kc", [128, 127]); ovc = self.dout("dbg_vc", [127, 128])
                S.dma(okc, KC[:], reads=[kcb], queue="pool"); S.dma(ovc, VC[0:127, :], reads=[kcb], queue="pool")
            WQ = self.sb("WQN", [128, NCH, 512], BF16); WGN = self.sb("WGN", [128, NCH, 24], BF16)
            SHCF = self.sb("SHCF", [32, 247], BF16); EF = self.sb("EF", [32, S_LEN], BF16)
            OV = self.sb("OV", [128, 32], BF16); AB = self.sb("ABF", [128, 2, 64], F32)
            SELG = self.sb("SELG", [24, 12, 128], BF16); IDb = self.sb("IDb", [128, 128], BF16)
            self.load_w(WQ[:], d["w_qn"], cb)
            self.load_w(WGN[:], d["w_gn"], cb)
            S.dma(SHCF[:], d["shcf"], writes=[cb], queue="pool"); S.dma(EF[:], d["efull"], writes=[cb], queue="pool")
            S.dma(OV[0:127, :], d["ov"], writes=[cb], queue="pool"); S.dma(AB[:], d["abf"], writes=[cb])
            S.dma(SELG[:], d["selg"], writes=[cb], queue="pool")
            S.op("dve", lambda e: e.tensor_copy(IDb[:], self.ident_f[:]), reads=[self.constb, cb], writes=[cb])
            QS = self.sb("QS", [128, 4, 128], BF16); qsb = Buf()
            GS = self.sb("GS", [24, 128], BF16); gsb = Buf()
            pt_ring = Ring([self.sb("PT%d" % i, [128, 512], BF16) for i in range(4)])
            RR = self.sb("RR", [128, 512], F32); rrb = Buf()
            RRc = self.sb("RRc", [128, 512], F32); rcb = Buf()
            PTc = [self.sb("PTc%d" % g, [128, 512], BF16) for g in range(2)]; ptcb = [Buf(), Buf()]
            YA = self.sb("YA", [128, 512], F32); yab = Buf()
            PN = self.sb("PN", [128, 512], BF16); pnb = Buf()
            IMP = self.sb("IMP", [128, 32], F32); IM2 = self.sb("IM2", [128, 32], F32); MX = self.sb("MX8", [128, 8], F32); ib = Buf()
            NMT = [self.sb("NMT%d" % g, [32, 4, 128], BF16) for g in range(2)]; nmb = [Buf(), Buf()]
            st_ring = Ring(self.banks[0:3], self.bankb[0:3])
            OD = [(self.banks[3], self.bankb[3], self.banks[4], self.bankb[4]),
                  (self.banks[5], self.bankb[5], self.banks[6], self.bankb[6])]
            ms_ring = Ring(self.banks[7:8], self.bankb[7:8])
            LOOK = 2
            for i in range(NT):
                tq = slice(i * 128, (i + 1) * 128)
                tcix = i // 4
                hreads = [self.HNB[c][tcix] for c in range(NCH)]
                p, pb = ms_ring.get()

                def mmq(e):
                    for j in range(4):
                        for k in range(NCH):
                            ins = e.matmul(p[:, j * 128:(j + 1) * 128], WQ[:, k, j * 128:(j + 1) * 128], self.HN[:, k, tq], start=(k == 0), stop=(k == NCH - 1))
                    return ins
                S.op("pe", mmq, reads=hreads + [cb], writes=[pb])
                S.op("act", lambda e: e.activation(QS[:].rearrange("p j q -> p (j q)"), p[:], AF.Copy, scale=0.125), reads=[pb], writes=[qsb])
                p2, pb2 = ms_ring.get()

                def mmg(e):
                    for k in range(NCH):
                        ins = e.matmul(p2[0:24, 0:128], WGN[:, k, :], self.HN[:, k, tq], start=(k == 0), stop=(k == NCH - 1))
                    return ins
                S.op("pe", mmg, reads=hreads + [cb], writes=[pb2])
                S.op("act", lambda e: e.activation(GS[:], p2[0:24, 0:128], AF.Sigmoid), reads=[pb2], writes=[gsb])

                def tiles_of(br):
                    if br == 0:
                        return [None]
                    if br == 1:
                        return list(range(0, i + 1))
                    return list(range(max(0, i - 4), i + 1))
                odset = {0: 0, 1: 0, 2: 1}

                def emit_scores(step):
                    br, g, kt, first, last = step
                    gs = slice(g * 64, (g + 1) * 64)
                    qrhs = QS[gs, :, :].rearrange("p j q -> p (j q)")
                    rows = 127 if br == 0 else 128
                    stp, stb = st_ring.get()
                    mms_list = []
                    if br == 0:
                        mms_list.append((KC[gs, :], qrhs))
                        mms_list.append((SHCF[:, 120 - 8 * i:247 - 8 * i], BVC[:, g, :]))
                    else:
                        mms_list.append((KT[gs, br - 1, kt * 128:(kt + 1) * 128], qrhs))
                        if br == 1 and i >= 8:
                            mms_list.append((EF[:, kt * 128:(kt + 1) * 128], NMT[g][:].rearrange("p j q -> p (j q)")))
                        if kt == i:
                            mms_list.append((IDb[:], BM[:, 0, g, :]))
                        elif kt == i - 1:
                            mms_list.append((IDb[:], BM[:, 1, g, :]))
                        elif br == 2 and kt == i - 4:
                            mms_list.append((IDb[:], BM[:, 2, g, :]))

                    def mms(e):
                        for n_, (l_, r_) in enumerate(mms_list):
                            ins = e.matmul(stp[0:rows, :], l_, r_, start=(n_ == 0), stop=(n_ == len(mms_list) - 1))
                        return ins
                    S.op("pe", mms, reads=[qsb, KTB, kcb, cb, nmb[g]], writes=[stb])
                    if br == 0:
                        PT, ptb = PTc[g], ptcb[g]
                    else:
                        PT, ptb = pt_ring.get()
                    S.op("act", lambda e: e.activation(PT[0:rows, :], stp[0:rows, :], AF.Exp), reads=[stb], writes=[ptb])
                    return (PT, ptb, rows)

                def emit_pv(step, ctx):
                    br, g, kt, first, last = step
                    PT, ptb, rows = ctx
                    gs = slice(g * 64, (g + 1) * 64)
                    O, Ob, DN, Db = OD[odset[br]]
                    if br == 0:
                        vl = VC[0:127, gs]
                    else:
                        c0 = (0 if br == 1 else 128) + g * 64
                        vl = VT[:, kt, c0:c0 + 64]

                    def mmo(e):
                        e.matmul(O[gs, :], vl, PT[0:rows, :], start=first, stop=last)
                        return e.matmul(DN[gs, :], self.ones_b[0:rows, 0:64], PT[0:rows, :], start=first, stop=last)
                    S.op("pe", mmo, reads=[ptb, VTB, kcb, self.constb], writes=[Ob, Db])

                def cmp_extras(g, ctx):
                    PT, ptb, rows = ctx
                    th = []
                    box = {}

                    def t0():
                        box["pd2"], box["pdb2"] = ms_ring.get()
                        S.op("pe", lambda e: e.matmul(box["pd2"][0:127, :], self.ones_b[0:127, 0:127], PT[0:127, :], start=True, stop=True),
                             reads=[ptb, self.constb], writes=[box["pdb2"]])
                        S.op("dve", lambda e: e.tensor_scalar(RRc[0:127, :], box["pd2"][0:127, :], 1e-30, None, ALU.max), reads=[box["pdb2"], rcb], writes=[rcb])
                    th.append(t0)
                    th.append(lambda: S.op("dve", lambda e: e.reciprocal(RRc[0:127, :], RRc[0:127, :]), reads=[rcb], writes=[rcb]))
                    th.append(lambda: S.op("dve", lambda e: e.tensor_tensor(PN[0:127, :], PT[0:127, :], RRc[0:127, :], ALU.mult), reads=[rcb, ptb, pnb], writes=[pnb]))

                    def t3():
                        box["pim"], box["pimb"] = ms_ring.get()

                        def mmi(e):
                            for j in range(4):
                                ins = e.matmul(box["pim"][:, 0:32], PN[0:127, j * 128:(j + 1) * 128], OV[0:127, :], start=(j == 0), stop=(j == 3))
                            return ins
                        S.op("pe", mmi, reads=[pnb, cb], writes=[box["pimb"]])
                        o0 = 32 - 2 * i
                        S.op("dve", lambda e: e.tensor_tensor(IMP[:], box["pim"][:, 0:32], AB[:, 0, o0:o0 + 32], ALU.mult), reads=[box["pimb"], cb, ib], writes=[ib])
                    th.append(t3)
                    o0 = 32 - 2 * i
                    th.append(lambda: S.op("dve", lambda e: e.tensor_tensor(IMP[:], IMP[:], AB[:, 1, o0:o0 + 32], ALU.add), reads=[ib, cb], writes=[ib]))
                    th.append(lambda: S.op("dve", lambda e: e.memset(IMP[:, 0:1], 1e6), reads=[ib], writes=[ib]))
                    th.append(lambda: S.op("dve", lambda e: e.max(MX[:], IMP[:]), reads=[ib], writes=[ib]))
                    th.append(lambda: S.op("dve", lambda e: e.match_replace(IM2[:], MX[:], IMP[:], 0.0), reads=[ib], writes=[ib]))
                    th.append(lambda: S.op("dve", lambda e: e.max(MX[:], IM2[:]), reads=[ib], writes=[ib]))
                    th.append(lambda: S.op("dve", lambda e: e.match_replace(IM2[:], MX[:], IM2[:], 0.0), reads=[ib], writes=[ib]))
                    th.append(lambda: S.op("dve", lambda e: e.tensor_tensor(IM2[:], IMP[:], IM2[:], ALU.subtract), reads=[ib], writes=[ib]))
                    th.append(lambda: S.op("dve", lambda e: e.tensor_scalar(IM2[:], IM2[:], 0.0, None, ALU.is_gt), reads=[ib], writes=[ib]))
                    th.append(lambda: S.op("dve", lambda e: e.tensor_scalar(IM2[:], IM2[:], 30000.0, -30000.0, ALU.mult, ALU.add), reads=[ib], writes=[ib]))

                    def tl():
                        ptr, ptrb = ms_ring.get()
                        S.op("pe", lambda e: e.transpose(ptr[0:32, 0:128], IM2[:], self.ident_f[:]), reads=[ib, self.constb], writes=[ptrb])
                        S.op("dve", lambda e: e.tensor_copy(NMT[g][:], ptr[0:32, 0:128].unsqueeze(1).to_broadcast([32, 4, 128])),
                             reads=[ptrb], writes=[nmb[g]])
                    th.append(tl)
                    return th

                def finalize(br):
                    O, Ob, DN, Db = OD[odset[br]]
                    S.op("dve", lambda e: e.tensor_scalar(RR[:], DN[:], 1e-30, None, ALU.max), reads=[Db, rrb], writes=[rrb])
                    S.op("dve", lambda e: e.reciprocal(RR[:], RR[:]), reads=[rrb], writes=[rrb])
                    pgb_, pgbb = ms_ring.get()

                    def mmgb(e):
                        for j in range(4):
                            ins = e.matmul(pgb_[:, j * 128:(j + 1) * 128], SELG[:, br * 4 + j, :], GS[:], start=True, stop=True)
                        return ins
                    S.op("pe", mmgb, reads=[gsb, cb], writes=[pgbb])
                    S.op("dve", lambda e: e.tensor_tensor(RR[:], RR[:], pgb_[:], ALU.mult), reads=[rrb, pgbb], writes=[rrb])
                    if br == 0:
                        S.op("dve", lambda e: e.tensor_tensor(YA[:], O[:], RR[:], ALU.mult), reads=[Ob, rrb, yab], writes=[yab])
                    else:
                        S.op("dve", lambda e: e.tensor_tensor(RR[:], O[:], RR[:], ALU.mult), reads=[Ob, rrb], writes=[rrb])
                        S.op("dve", lambda e: e.tensor_tensor(YA[:], YA[:], RR[:], ALU.add), reads=[rrb, yab], writes=[yab])

                extras = []
                for g in range(2):
                    st_ = (0, g, None, True, True)
                    ctx = emit_scores(st_)
                    emit_pv(st_, ctx)
                    if i >= 8:
                        extras += cmp_extras(g, ctx)
                finalize(0)

                def run_steps(steps, fill):
                    ctxs = {}
                    for n in range(len(steps) + LOOK):
                        if n < len(steps):
                            ctxs[n] = emit_scores(steps[n])
                        m = n - LOOK
                        if m >= 0:
                            emit_pv(steps[m], ctxs.pop(m))
                            br_, g_, kt_, f_, l_ = steps[m]
                            if g_ == 1 and l_:
                                finalize(br_)
                        for _ in range(3):
                            if fill:
                                fill.pop(0)()

                def mk_steps(br):
                    out = []
                    for g in range(2):
                        tl_ = tiles_of(br)
                        for ti, kt in enumerate(tl_):
                            out.append((br, g, kt, ti == 0, ti == len(tl_) - 1))
                    return out
                run_steps(mk_steps(2), extras)
                while extras:
                    extras.pop(0)()
                run_steps(mk_steps(1), [])
                S.op("act", lambda e: e.copy(self.Y[1][:, :, tq], YA[:].rearrange("p (j q) -> p j q", j=4)), reads=[yab], writes=[self.YB[1][tcix]])
            S.full_barrier()
            self.st = old

    def mem_branch(self, memT, wk_d, wv_d, wqm_d):
        S = self.S
        with ExitStack() as st4:
            old, self.st = self.st, st4
            self._norm_rings_open()
            WQ = self.sb("WQM", [128, NCH, 512], BF16); WQB = Buf()
            KHT = self.sb("KHT", [128, 4, 256], BF16); KHTB = Buf()
            VH = self.sb("VH", [128, 2, 512], BF16); VHB = Buf()
            st5 = ExitStack()
            self.st = st5
            MT = self.sb("MT", [128, NCH, 256], F32); MTB = Buf()
            MN = self.sb("MN", [128, NCH, 256], BF16); MNB = Buf()
            WK = self.sb("WK", [128, NCH, 512], BF16); WKB = Buf()
            WV = self.sb("WV", [128, NCH, 512], BF16); WVB = Buf()
            mr = self.sb("mrstd", [128, 256], F32); mrb = Buf()
            S.dma(MT[:], memT.rearrange("(c p) m -> p c m", p=128), writes=[MTB])
            self.load_w(WK[:], wk_d, WKB)
            self.load_w(WV[:], wv_d, WVB)
            self.load_w(WQ[:], wqm_d, WQB)
            g0, _ = COLS["mem_norm"]
            pt, pb = self.psum.get()
            for c in range(NCH):
                sq, sqb = self.sq_ring.get()
                S.op("act", lambda e: e.activation(sq[:, 0:256], MT[:, c, :], AF.Square), reads=[MTB], writes=[sqb])
                S.op("pe", lambda e: e.matmul(pt[:, 0:256], self.ones_f[:], sq[:, 0:256], start=(c == 0), stop=(c == NCH - 1)),
                     reads=[sqb, self.constb], writes=[pb])
            S.op("act", lambda e: e.activation(mr[:], pt[:, 0:256], AF.Sqrt, bias=self.eps_t[:], scale=1.0 / D),
                 reads=[pb, self.constb], writes=[mrb])
            S.op("dve", lambda e: e.reciprocal(mr[:], mr[:]), reads=[mrb], writes=[mrb])
            for c in range(NCH):
                S.op("dve", lambda e: e.scalar_tensor_tensor(MN[:, c, :], MT[:, c, :], self.cols[:, g0 + c:g0 + c + 1], mr[:],
                                                             ALU.mult, ALU.mult),
                     reads=[MTB, mrb, self.constb], writes=[MNB])
            for h in range(4):
                p, pb = self.psum.get()

                def mm(e):
                    for k in range(NCH):
                        ins = e.matmul(p[:, 0:256], WK[:, k, h * 128:(h + 1) * 128], MN[:, k, :], start=(k == 0), stop=(k == NCH - 1))
                    return ins
                S.op("pe", mm, reads=[WKB, MNB], writes=[pb])
                S.op("act", lambda e: e.copy(KHT[:, h, :], p[:, 0:256]), reads=[pb], writes=[KHTB])
            for mt in range(2):
                p, pb = self.psum.get()

                def mm(e):
                    for k in range(NCH):
                        ins = e.matmul(p[:], MN[:, k, mt * 128:(mt + 1) * 128], WV[:, k, :], start=(k == 0), stop=(k == NCH - 1))
                    return ins
                S.op("pe", mm, reads=[WVB, MNB], writes=[pb])
                S.op("act", lambda e: e.copy(VH[:, mt, :], p[:]), reads=[pb], writes=[VHB])
            S.full_barrier()
            st5.close()
            self.st = st4
            qm_ring = Ring([self.sb("qm%d" % i, [128, TC], BF16) for i in range(2)])
            pt_ring = Ring([self.sb("pt%d" % i, [128, 2, TC], BF16) for i in range(2)])
            rd_ring = self.sq_ring
            scale = 128.0 ** -0.5
            for tc in range(NTC):
                ts = slice(tc * TC, (tc + 1) * TC)
                hreads = [self.HNB[c][tc] for c in range(NCH)]
                for h in range(4):
                    p, pb = self.psum.get()

                    def mm(e):
                        for k in range(NCH):
                            ins = e.matmul(p[:], WQ[:, k, h * 128:(h + 1) * 128], self.HN[:, k, ts], start=(k == 0), stop=(k == NCH - 1))
                        return ins
                    S.op("pe", mm, reads=hreads + [WQB], writes=[pb])
                    qm, qmb = qm_ring.get()
                    S.op("dve", lambda e: e.tensor_copy(qm[:], p[:]), reads=[pb], writes=[qmb])
                    ptile, ptb = pt_ring.get()
                    for mt in range(2):
                        ps_, psb = self.psum.get()
                        S.op("pe", lambda e: e.matmul(ps_[:], KHT[:, h, mt * 128:(mt + 1) * 128], qm[:], start=True, stop=True),
                             reads=[KHTB, qmb], writes=[psb])
                        S.op("act", lambda e: e.activation(ptile[:, mt, :], ps_[:], AF.Exp, scale=scale), reads=[psb], writes=[ptb])
                    po, pob = self.psum.get()
                    pd, pdb = self.psum.get()

                    def mm_o(e):
                        for mt in range(2):
                            ins = e.matmul(po[:], VH[:, mt, h * 128:(h + 1) * 128], ptile[:, mt, :], start=(mt == 0), stop=(mt == 1))
                        return ins

                    def mm_d(e):
                        for mt in range(2):
                            ins = e.matmul(pd[:], self.ones_b[:], ptile[:, mt, :], start=(mt == 0), stop=(mt == 1))
                        return ins
                    S.op("pe", mm_o, reads=[VHB, ptb], writes=[pob])
                    S.op("pe", mm_d, reads=[ptb, self.constb], writes=[pdb])
                    rd, rdb = rd_ring.get()
                    S.op("dve", lambda e: e.reciprocal(rd[:], pd[:]), reads=[pdb], writes=[rdb])
                    S.op("dve", lambda e: e.tensor_tensor(self.Y[2][:, h, ts], po[:], rd[:], ALU.mult),
                         reads=[pob, rdb], writes=[self.YB[2][tc]])
            S.full_barrier()
            self.st = old
        self._nst.close()

    def fold(self, br, wgb_d, wbr_d, first):
        S = self.S
        with ExitStack() as st4:
            old, self.st = self.st, st4
            WGB = [self.sb("WGBr%d" % i, [128, NCH, 128], BF16) for i in range(2)]; WGBB = [Buf(), Buf()]
            WBR = [self.sb("WBR%d" % i, [128, 4, 128], BF16) for i in range(2)]; WBRB = [Buf(), Buf()]
            gt_ring = Ring([self.sb("gt%d" % i, [128, TC], F32) for i in range(2)])
            t_ring = Ring([self.sb("mt%d" % i, [128, TC], F32) for i in range(2)])

            def load(dc):
                sl = dc % 2
                c0 = br * D + dc * 128
                S.dma(WGB[sl][:], wgb_d[:, c0:c0 + 128].rearrange("(k p) n -> p k n", p=128), writes=[WGBB[sl]], queue="pool")
                S.dma(WBR[sl][:], wbr_d[:, dc * 128:(dc + 1) * 128].rearrange("(k p) n -> p k n", p=128), writes=[WBRB[sl]], queue="pool")
            load(0)
            for dc in range(NCH):
                if dc + 1 < NCH:
                    load(dc + 1)
                sl = dc % 2
                for tc in range(NTC):
                    ts = slice(tc * TC, (tc + 1) * TC)
                    hreads = [self.HNB[c][tc] for c in range(NCH)]
                    pg, pgb = self.psum.get()
                    py, pyb = self.psum.get()

                    def mm_g(e):
                        for k in range(NCH):
                            ins = e.matmul(pg[:], WGB[sl][:, k, :], self.HN[:, k, ts], start=(k == 0), stop=(k == NCH - 1))
                        return ins

                    def mm_y(e):
                        for k in range(4):
                            ins = e.matmul(py[:], WBR[sl][:, k, :], self.Y[br][:, k, ts], start=(k == 0), stop=(k == 3))
                        return ins
                    S.op("pe", mm_g, reads=hreads + [WGBB[sl]], writes=[pgb])
                    S.op("pe", mm_y, reads=[self.YB[br][tc], WBRB[sl]], writes=[pyb])
                    gt, gtb = gt_ring.get()
                    S.op("act", lambda e: e.activation(gt[:], pg[:], AF.Sigmoid), reads=[pgb], writes=[gtb])
                    if first:
                        S.op("dve", lambda e: e.tensor_tensor(self.M[:, dc, ts], gt[:], py[:], ALU.mult),
                             reads=[gtb, pyb], writes=[self.MB[dc][tc]])
                    else:
                        t, tb = t_ring.get()
                        S.op("dve", lambda e: e.tensor_tensor(t[:], gt[:], py[:], ALU.mult), reads=[gtb, pyb], writes=[tb])
                        S.op("pool", lambda e: e.tensor_tensor(self.M[:, dc, ts], self.M[:, dc, ts], t[:], ALU.add),
                             reads=[tb, self.MB[dc][tc]], writes=[self.MB[dc][tc]])
            S.full_barrier()
            self.st = old

    def outproj(self, wout_d):
        S = self.S
        with ExitStack() as st4:
            old, self.st = self.st, st4
            WO = self.sb("WO", [128, NCH, D], BF16); WOB = Buf()
            self.load_w(WO[:], wout_d, WOB)
            for tc in range(NTC):
                ts = slice(tc * TC, (tc + 1) * TC)
                for d2 in range(NCH):
                    po, pob = self.psum.get()

                    def mm(e):
                        for k in range(NCH):
                            ins = e.matmul(po[:], WO[:, k, d2 * 128:(d2 + 1) * 128], self.M[:, k, ts], start=(k == 0), stop=(k == NCH - 1))
                        return ins
                    S.op("pe", mm, reads=[self.MB[k][tc] for k in range(NCH)] + [WOB], writes=[pob])
                    S.op("dve", lambda e: e.tensor_tensor(self.X[:, d2, ts], po[:], self.X[:, d2, ts], ALU.add),
                         reads=[pob, self.XB[d2][tc]], writes=[self.XB[d2][tc]])
            S.full_barrier()
            self.st = old

    def final_norm_out(self, outT):
        S = self.S
        g0, _ = COLS["final_norm"]
        self._norm_rings_open()
        for tc in range(NTC):
            ts = slice(tc * TC, (tc + 1) * TC)
            pt, pb = self.psum.get()
            for c in range(NCH):
                sq, sqb = self.sq_ring.get()
                S.op("act", lambda e: e.activation(sq[:], self.X[:, c, ts], AF.Square),
                     reads=[self.XB[c][tc]], writes=[sqb])
                S.op("pe", lambda e: e.matmul(pt[:], self.ones_f[:], sq[:], start=(c == 0), stop=(c == NCH - 1)),
                     reads=[sqb, self.constb], writes=[pb])
            rs, rsb = self.rstd_ring.get()
            S.op("act", lambda e: e.activation(rs[:], pt[:], AF.Sqrt, bias=self.eps_t[:], scale=1.0 / D),
                 reads=[pb, self.constb], writes=[rsb])
            S.op("dve", lambda e: e.reciprocal(rs[:], rs[:]), reads=[rsb], writes=[rsb])
            for c in range(NCH):
                S.op("dve", lambda e: e.scalar_tensor_tensor(
                    self.X[:, c, ts], self.X[:, c, ts], self.cols[:, g0 + c:g0 + c + 1], rs[:],
                    ALU.mult, ALU.mult),
                    reads=[self.XB[c][tc], rsb, self.constb], writes=[self.XB[c][tc]])
                S.dma(outT[c * 128:(c + 1) * 128, ts], self.X[:, c, ts], reads=[self.XB[c][tc]])
        self._norm_rings_close()

    def dump_x(self, name):
        o = self.dout(name, [D, S_LEN])
        for c in range(NCH):
            for tc in range(NTC):
                ts = slice(tc * TC, (tc + 1) * TC)
                self.S.dma(o[c * 128:(c + 1) * 128, ts], self.X[:, c, ts], reads=[self.XB[c][tc]])

    def build(self, stop_after=None):
        nc = self.nc
        dbg = self.debug
        xT = self.din("xT", [D, S_LEN])
        cols_d = self.din("cols", [128, NCOLS])
        f1g = self.din("ffn1_w_gate", [D, DFF]); f1u = self.din("ffn1_w_up", [D, DFF]); f1d = self.din("ffn1_w_down", [DFF, D])
        f2g = self.din("ffn2_w_gate", [D, DFF]); f2u = self.din("ffn2_w_up", [D, DFF]); f2d = self.din("ffn2_w_down", [DFF, D])
        memT = self.din("memT", [D, 256])
        mem_wk = self.din("mem_w_k", [D, 512]); mem_wv = self.din("mem_w_v", [D, 512])
        w_qm = self.din("w_qm", [D, 512])
        w_gb = self.din("w_gb", [D, 3 * D])
        w_br = [self.din(n, [512, D]) for n in ("w_br_rwkv", "w_br_nsa_p", "w_br_mem")]
        w_out = self.din("w_out", [D, D])
        w_rwkv = self.din("w_rwkv", [D, 1792])
        w2_d = self.din("rwkv_w2", [64, 512]); a2_d = self.din("rwkv_a2", [64, 512]); g2_d = self.din("rwkv_g2", [128, 512])
        gng_d = self.din("gng_rep", [128, 512]); gnb_d = self.din("gnb_rep", [128, 512])
        ident_d = self.din("ident", [128, 128])
        rmk_d = self.din("rwkv_masks", [128, 3, 128])
        nd = {}
        nd["w_qn"] = self.din("w_qn", [D, 512]); nd["w_gn"] = self.din("w_gn", [D, 24]); nd["w_kvn"] = self.din("w_kvn", [D, 768])
        nd["shcf"] = self.din("shcf", [32, 247]); nd["efull"] = self.din("efull", [32, S_LEN]); nd["ov"] = self.din("ov", [127, 32])
        nd["abf"] = self.din("abf", [128, 2, 64]); nd["selg"] = self.din("selg", [24, 12, 128])
        nd["t31"] = self.din("t31", [128, 2, 512])
        nd["bmg"] = [self.din("bmg%d" % k, [128, 2, 512]) for k in range(3)]
        nd["msk"] = [self.din("msk%d" % k, [128, 128]) for k in range(3)]
        nd["bvcg"] = self.din("bvcg", [32, 2, 512]); nd["mskc"] = self.din("mskc", [32, 128])
        nd["cmp_w1"] = [self.din("cmp_k_w1", [2048, 256]), self.din("cmp_v_w1", [2048, 256])]
        nd["cmp_w2"] = [self.din("cmp_k_w2", [256, 64]), self.din("cmp_v_w2", [256, 64])]
        nd["cmp_peT"] = [self.din("cmp_pe_kT", [64, 32]), self.din("cmp_pe_vT", [64, 32])]
        outT = self.dout("outT", [D, S_LEN])
        with ExitStack() as st:
            self.st = st
            S = self.S = Sched(nc, st)
            self.X = self.sb("X", [128, NCH, S_LEN], F32)
            self.XB = [[Buf() for _ in range(NTC)] for _ in range(NCH)]
            self.cols = self.sb("cols", [128, NCOLS], F32)
            self.ones_f = self.sb("ones_f", [128, 128], F32)
            self.ones_b = self.sb("ones_b", [128, 128], BF16)
            self.eps_t = self.sb("eps_t", [128, 1], F32)
            self.gneps_t = self.sb("gneps_t", [128, 1], F32)
            self.ident_f = self.sb("ident_f", [128, 128], F32)
            self.constb = Buf("const")
            self.PS = self.ps("PSALL", [128, 8, 512])
            self.banks = [self.PS[:, i, :] for i in range(8)]
            self.bankb = [Buf() for _ in range(8)]
            self.psum = Ring(self.banks, self.bankb)
            S.dma(self.cols[:], cols_d, writes=[self.constb])
            S.op("dve", lambda e: e.memset(self.ones_f[:], 1.0), reads=[self.constb], writes=[self.constb])
            S.op("dve", lambda e: e.memset(self.ones_b[:], 1.0), reads=[self.constb], writes=[self.constb])
            S.op("dve", lambda e: e.memset(self.eps_t[:], EPS), reads=[self.constb], writes=[self.constb])
            S.op("dve", lambda e: e.memset(self.gneps_t[:], 64e-5), reads=[self.constb], writes=[self.constb])
            S.dma(self.ident_f[:], ident_d, reads=[self.constb], writes=[self.constb])
            for c in range(NCH):
                for tc in range(NTC):
                    ts = slice(tc * TC, (tc + 1) * TC)
                    S.dma(self.X[:, c, ts], xT[c * 128:(c + 1) * 128, ts], writes=[self.XB[c][tc]])

            def ffn_phase(wg, wu, wd, gname):
                with ExitStack() as st2:
                    self.st = st2
                    self.HN = self.sb("HN", [128, NCH, S_LEN], BF16)
                    self.HNB = [[Buf() for _ in range(NTC)] for _ in range(NCH)]
                    self.WG = [self.sb("WG%d" % i, [128, NCH, 512], BF16) for i in range(2)]
                    self.WU = [self.sb("WU%d" % i, [128, NCH, 512], BF16) for i in range(2)]
                    self.WD = [self.sb("WD%d" % i, [128, 4, D], BF16) for i in range(2)]
                    self.WGB = [Buf() for _ in range(2)]; self.WUB = [Buf() for _ in range(2)]; self.WDB = [Buf() for _ in range(2)]
                    self.a_ring = Ring([self.sb("a%d" % i, [128, 4, TC], BF16) for i in range(2)])
                    self.sg_ring = Ring([self.sb("sg%d" % i, [128, TC], F32) for i in range(2)])
                    self.ffn(wg, wu, wd, gname)
                    S.full_barrier()
                    self.st = st

            if "noffn1" not in dbg:
                ffn_phase(f1g, f1u, f1d, "ffn1_norm")
            if "x1" in dbg:
                self.dump_x("dbg_x1")
            if stop_after != "ffn1":
                with ExitStack() as st3:
                    self.st = st3
                    self.HN = self.sb("HN", [128, NCH, S_LEN], BF16)
                    self.HNB = [[Buf() for _ in range(NTC)] for _ in range(NCH)]
                    Yt = self.sb("Yt", [128, 4, S_LEN], BF16)
                    YBt = [Buf() for _ in range(NTC)]
                    self.Y = [Yt, Yt, Yt]
                    self.YB = [YBt, YBt, YBt]
                    if "norwkv" in dbg or "rwkvseq" in dbg:
                        self.rmsnorm_to_hn("mix_norm")
                    if "norwkv" not in dbg:
                        if "rwkvseq" in dbg:
                            self.rwkv_branch_seq(w_rwkv, w2_d, a2_d, g2_d, gng_d, gnb_d)
                        else:
                            self.rwkv_branch(w_rwkv, w2_d, a2_d, g2_d, gng_d, gnb_d, rmk_d)
                    else:
                        S.op("pool", lambda e: e.memset(Yt[:], 0.0), writes=YBt)
                    if "y_rwkv" in dbg:
                        self.dump_feat("dbg_y_rwkv", Yt, 4, YBt)
                    self.M = self.sb("M", [128, NCH, S_LEN], BF16)
                    self.MB = [[Buf() for _ in range(NTC)] for _ in range(NCH)]
                    do_merge = stop_after != "mix"
                    if do_merge:
                        self.fold(0, w_gb, w_br[0], True)
                    if "nomem" not in dbg:
                        self.mem_branch(memT, mem_wk, mem_wv, w_qm)
                    else:
                        S.op("pool", lambda e: e.memset(Yt[:], 0.0), writes=YBt)
                    if "y_mem" in dbg:
                        self.dump_feat("dbg_y_mem", Yt, 4, YBt)
                    if do_merge:
                        self.fold(2, w_gb, w_br[2], False)
                    if "nonsa" not in dbg:
                        self.nsa_branch(nd)
                    else:
                        S.op("pool", lambda e: e.memset(Yt[:], 0.0), writes=YBt)
                    if "y_nsa" in dbg:
                        self.dump_feat("dbg_y_nsa_p", Yt, 4, YBt)
                    if do_merge:
                        self.fold(1, w_gb, w_br[1], False)
                        self.outproj(w_out)
                    S.full_barrier()
                    self.st = st
                if "x2" in dbg:
                    self.dump_x("dbg_x2")
                if stop_after not in ("mix", "merge"):
                    ffn_phase(f2g, f2u, f2d, "ffn2_norm")
            self.final_norm_out(outT)
            S.wait_all_dma("sp")
            S.wait_all_dma("pool")
        return nc


NSA_PERM = np.concatenate([np.concatenate([np.arange(64 * j, 64 * j + 64), np.arange(64 * (4 + j), 64 * (4 + j) + 64)])
                           for j in range(4)])


def _t5_bucket_np(dist):
    n = np.maximum(dist, 0)
    nf = np.maximum(n, 1).astype(np.float32)
    large = 16 + (np.log(nf / np.float32(16)) / np.float32(math.log(128 / 16)) * np.float32(16)).astype(np.int32)
    large = np.minimum(large, 31)
    return np.where(n < 16, n, large)


def _nsa_consts(rel_bias):
    rb = np.asarray(rel_bias, np.float32)
    c = np.arange(128)[:, None]; p = np.arange(128)[None, :]
    out = {}
    hd = np.arange(8).reshape(2, 4)
    dists = [p - c, 128 + p - c, 512 + p - c]
    valid = [p >= c, np.ones((128, 128), bool), c > p]
    for k in range(3):
        bk = _t5_bucket_np(dists[k])
        g = rb[bk[:, None, None, :], hd[None, :, :, None]]
        out["bmg%d" % k] = np.ascontiguousarray(g.reshape(128, 2, 512))
        out["msk%d" % k] = np.where(valid[k], 0.0, -30000.0).astype(np.float32)
    out["t31"] = np.ascontiguousarray(np.broadcast_to(rb[31][hd][None, :, :, None], (128, 2, 4, 128)).reshape(128, 2, 512))
    m = np.arange(32)[:, None]
    dc = p - 16 * (m - 8) - 31
    bk = _t5_bucket_np(dc)
    g = rb[bk[:, None, None, :], hd[None, :, :, None]]
    out["bvcg"] = np.ascontiguousarray(g.reshape(32, 2, 512))
    mk = np.where((dc >= 0) & (m < 16), 0.0, -30000.0).astype(np.float32)
    mk[17:] = 0.0
    out["mskc"] = mk
    shcf = np.zeros((32, 247), np.float32)
    for x in range(247):
        r = x - 112
        if 0 <= r < 16:
            shcf[r, x] = 1.0
        elif r >= 16:
            shcf[16, x] = 1.0
    out["shcf"] = shcf
    ef = np.zeros((32, S_LEN), np.float32)
    ef[np.arange(S_LEN) // 64, np.arange(S_LEN)] = 1.0
    out["efull"] = ef
    ic = np.arange(127)[:, None]; jb = np.arange(32)[None, :]
    out["ov"] = ((ic * 16 <= jb * 64 + 63) & (ic * 16 + 31 >= jb * 64)).astype(np.float32)
    ab = np.zeros((128, 2, 64), np.float32)
    for pp in range(128):
        curr = 1 if pp >= 64 else 0
        for mm in range(64):
            jr = mm - 32
            if jr <= curr - 2:
                ab[pp, 0, mm] = 1.0
            if jr in (curr, curr - 1):
                ab[pp, 1, mm] = 1e6
    out["abf"] = ab
    selg = np.zeros((24, 12, 128), np.float32)
    for br in range(3):
        for j in range(4):
            for mm in range(128):
                selg[br * 8 + (mm // 64) * 4 + j, br * 4 + j, mm] = 1.0
    out["selg"] = selg
    return out


def prep_inputs(inputs, b):
    m = {}
    m["xT"] = np.ascontiguousarray(inputs["x"][b].T)
    cols = np.zeros((128, NCOLS), np.float32)
    for n in ("ffn1_norm", "mix_norm", "ffn2_norm", "final_norm", "mem_norm"):
        c0, k = COLS[n]
        cols[:, c0:c0 + k] = _colpack(np.asarray(inputs[n]).reshape(-1))
    for n, src in (("mu", "rwkv_mu"), ("w0", "rwkv_w0"), ("a0", "rwkv_a0"), ("k_k", "rwkv_k_k"), ("k_a", "rwkv_k_a"), ("r_k", "rwkv_r_k"),
                   ("gn_g", "rwkv_gn_gain"), ("gn_b", "rwkv_gn_bias")):
        c0, k = COLS[n]
        cols[:, c0:c0 + k] = _colpack(np.asarray(inputs[src]).reshape(-1))
    m["cols"] = cols
    m["w_rwkv"] = np.ascontiguousarray(np.asarray(inputs["w_in"])[0][:, 0:1792])
    m["rwkv_w2"] = np.ascontiguousarray(np.asarray(inputs["rwkv_w2"])[0])
    m["rwkv_a2"] = np.ascontiguousarray(np.asarray(inputs["rwkv_a2"])[0])
    m["rwkv_g2"] = np.ascontiguousarray(np.asarray(inputs["rwkv_g2"])[0])
    m["gng_rep"] = np.ascontiguousarray(np.broadcast_to(np.asarray(inputs["rwkv_gn_gain"]).reshape(1, 512), (128, 512)))
    m["gnb_rep"] = np.ascontiguousarray(np.broadcast_to(np.asarray(inputs["rwkv_gn_bias"]).reshape(1, 512), (128, 512)))
    m["ident"] = np.eye(128, dtype=np.float32)
    si = np.arange(128)[:, None]; ti = np.arange(128)[None, :]
    same = (si // 64) == (ti // 64)
    mk = np.zeros((128, 3, 128), np.float32)
    mk[:, 0, :] = np.where(same & (si < ti), -1.0, 0.0)
    mk[:, 1, :] = np.where(same & (ti < si), -1.0, 0.0)
    mk[:, 2, :] = np.where(same & (si <= ti), 1.0, 0.0)
    m["rwkv_masks"] = mk
    w_in_ = np.asarray(inputs["w_in"])[0]
    m["w_qn"] = np.ascontiguousarray(w_in_[:, 1792:2304][:, NSA_PERM])
    m["w_kvn"] = np.ascontiguousarray(w_in_[:, 2304:3072])
    m["w_gn"] = np.ascontiguousarray(w_in_[:, 3072:3096])
    m.update(_nsa_consts(inputs["rel_bias"]))
    for n in ("cmp_k_w1", "cmp_v_w1", "cmp_k_w2", "cmp_v_w2"):
        m[n] = np.ascontiguousarray(np.asarray(inputs[n])[0])
    m["cmp_pe_kT"] = np.ascontiguousarray(np.asarray(inputs["cmp_pe_k"])[0].T)
    m["cmp_pe_vT"] = np.ascontiguousarray(np.asarray(inputs["cmp_pe_v"])[0].T)
    for n in ("ffn1_w_gate", "ffn1_w_up", "ffn1_w_down", "ffn2_w_gate", "ffn2_w_up", "ffn2_w_down",
              "mem_w_k", "mem_w_v", "w_br_rwkv", "w_br_mem", "w_out"):
        m[n] = np.ascontiguousarray(np.asarray(inputs[n])[0])
    m["memT"] = np.ascontiguousarray(inputs["mem"][b].T)
    w_in = np.asarray(inputs["w_in"])[0]
    m["w_qm"] = np.ascontiguousarray(w_in[:, 3096:3608])
    m["w_gb"] = np.ascontiguousarray(w_in[:, 3608:6680])
    m["w_br_nsa_p"] = np.ascontiguousarray(np.asarray(inputs["w_br_nsa"])[0][NSA_PERM, :])
    return m


_CACHE = {}


def kernel(**inputs):
    inputs = {k: np.asarray(v) for k, v in inputs.items()}
    if "nc" not in _CACHE:
        _CACHE["nc"] = Builder().build()
    nc = _CACHE["nc"]
    n = 8
    in_maps = [prep_inputs(inputs, b) for b in range(n)]
    res = run_bass_kernel_spmd(nc, in_maps, core_ids=list(range(n)))
    out = np.stack([np.ascontiguousarray(r["outT"].T) for r in res.results], axis=0)
    return out.astype(np.float32)
```

```python
import math
from contextlib import ExitStack
import numpy as np
import concourse.bass as bass
import concourse.mybir as mybir
from concourse.bass_utils import run_bass_kernel_spmd

F32 = mybir.dt.float32
BF16 = mybir.dt.bfloat16
AF = mybir.ActivationFunctionType
ALU = mybir.AluOpType
AX = mybir.AxisListType

D = 1024
S_LEN = 2048
DFF = 2816
NCH = 8
TC = 512
NTC = S_LEN // TC
EPS = 1e-6


class Buf:
    __slots__ = ("name", "last_w", "readers")

    def __init__(self, name=""):
        self.name = name
        self.last_w = None
        self.readers = []


class Sched:
    ENG = ("pe", "act", "dve", "pool", "sp")

    def __init__(self, nc, stack, n_dma_sems=16):
        self.nc = nc
        self.eng = {"pe": nc.tensor, "act": nc.scalar, "dve": nc.vector,
                    "pool": nc.gpsimd, "sp": nc.sync}
        self.sem = {}
        for e in ("pe", "act", "dve", "pool"):
            self.sem[e] = stack.enter_context(nc.semaphore("s_" + e))
        self.cnt = {e: 0 for e in ("pe", "act", "dve", "pool")}
        nq = {"sp": 28, "pool": 28, "act": 8}
        self.dsem = []
        self.qsems = {}
        for q, n in nq.items():
            self.qsems[q] = list(range(len(self.dsem), len(self.dsem) + n))
            for i in range(n):
                self.dsem.append(stack.enter_context(nc.semaphore("d%s%d" % (q, i))))
        self.dcnt = [0] * len(self.dsem)
        self.dnext = {q: 0 for q in nq}
        self.waited = {e: {} for e in self.ENG}
        self.n_ops = 0
        self.n_waits = 0

    def _semobj(self, key):
        return self.sem[key] if isinstance(key, str) else self.dsem[key]

    def _need(self, engine, toks):
        best = {}
        for t in toks:
            if t is None:
                continue
            key, val = t
            if best.get(key, 0) < val:
                best[key] = val
        w = self.waited[engine]
        for key, val in best.items():
            if w.get(key, 0) >= val:
                continue
            self.eng[engine].wait_ge(self._semobj(key), val)
            w[key] = val
            self.n_waits += 1

    @staticmethod
    def _deps(reads, writes):
        toks = []
        for b in reads:
            toks.append(b.last_w)
        for b in writes:
            toks.append(b.last_w)
            toks.extend(b.readers)
        return toks

    @staticmethod
    def _commit(tok, reads, writes):
        for b in reads:
            b.readers.append(tok)
            if len(b.readers) > 48:
                best = {}
                for k, v in b.readers:
                    if best.get(k, 0) < v:
                        best[k] = v
                b.readers = list(best.items())
        for b in writes:
            b.last_w = tok
            b.readers = []

    def op(self, engine, fn, reads=(), writes=()):
        self._need(engine, self._deps(reads, writes))
        ins = fn(self.eng[engine])
        self.cnt[engine] += 1
        ins.then_inc(self.sem[engine], 1)
        tok = (engine, self.cnt[engine])
        self._commit(tok, reads, writes)
        self.n_ops += 1
        return tok

    def dma(self, out_ap, in_ap, reads=(), writes=(), queue="sp", **kw):
        pool = self.qsems[queue]
        i = pool[self.dnext[queue]]
        self.dnext[queue] = (self.dnext[queue] + 1) % len(pool)
        prev = [(i, self.dcnt[i])] if self.dcnt[i] else []
        self._need(queue, self._deps(reads, writes) + prev)
        ins = self.eng[queue].dma_start(out=out_ap, in_=in_ap, **kw)
        self.dcnt[i] += 16
        ins.then_inc(self.dsem[i], 16)
        tok = (i, self.dcnt[i])
        self._commit(tok, reads, writes)
        self.n_ops += 1
        return tok

    def barrier(self, bufs):
        toks = []
        for b in bufs:
            toks.append(b.last_w)
            toks.extend(b.readers)
        for e in self.ENG:
            self._need(e, toks)

    def full_barrier(self):
        toks = [(e, self.cnt[e]) for e in ("pe", "act", "dve", "pool") if self.cnt[e]]
        toks += [(i, self.dcnt[i]) for i in range(len(self.dsem)) if self.dcnt[i]]
        for e in self.ENG:
            self._need(e, toks)

    def wait_all_dma(self, engine="sp"):
        for i in range(len(self.dsem)):
            if self.dcnt[i]:
                self.eng[engine].wait_ge(self.dsem[i], self.dcnt[i])


class Ring:
    def __init__(self, tiles, bufs=None):
        self.tiles = tiles
        self.bufs = bufs if bufs is not None else [Buf() for _ in tiles]
        self.i = 0

    def get(self):
        t, b = self.tiles[self.i], self.bufs[self.i]
        self.i = (self.i + 1) % len(self.tiles)
        return t, b

    def get_pair_idx(self):
        if self.i % 2:
            self.i = (self.i + 1) % len(self.tiles)
        k = self.i
        self.i = (self.i + 2) % len(self.tiles)
        return k, self.bufs[k], self.bufs[k + 1]


COLS = {}
_c = 0
for _n, _k in (("ffn1_norm", 8), ("mix_norm", 8), ("ffn2_norm", 8), ("final_norm", 8),
               ("mem_norm", 8), ("mu", 14), ("w0", 4), ("a0", 4), ("k_k", 4), ("k_a", 4), ("r_k", 4), ("gn_g", 4), ("gn_b", 4)):
    COLS[_n] = (_c, _k)
    _c += _k
NCOLS = _c


def _colpack(v):
    v = np.asarray(v, np.float32).reshape(-1, 128)
    return np.ascontiguousarray(v.T)


class Builder:
    def __init__(self, debug=()):
        self.debug = set(debug)
        self._rk_stage = 99
        self._rk_tiles = S_LEN // 128
        for d_ in self.debug:
            if d_.startswith("rkstage"):
                self._rk_stage = int(d_[7:])
            if d_.startswith("rktiles"):
                self._rk_tiles = int(d_[7:])
        self.nc = bass.Bass("TRN2", target_bir_lowering=False)
        self.dram_in = {}
        self.dram_out = {}

    def din(self, name, shape, dt=F32):
        t = self.nc.dram_tensor(name, list(shape), dt, kind="ExternalInput").ap()
        self.dram_in[name] = t
        return t

    def dout(self, name, shape, dt=F32):
        t = self.nc.dram_tensor(name, list(shape), dt, kind="ExternalOutput").ap()
        self.dram_out[name] = t
        return t

    def sb(self, name, shape, dt):
        self._uid = getattr(self, "_uid", 0) + 1
        return self.st.enter_context(self.nc.sbuf_tensor("sb%d_%s" % (self._uid, name), list(shape), dt))

    def ps(self, name, shape, dt=F32):
        return self.st.enter_context(self.nc.psum_tensor("ps_" + name, list(shape), dt))

    def _norm_rings_open(self):
        self._nst_old = self.st
        self._nst = ExitStack()
        self.st = self._nst
        self.sq_ring = Ring([self.sb("sq%d" % i, [128, TC], F32) for i in range(2)])
        self.rstd_ring = Ring([self.sb("RSTD%d" % i, [128, TC], F32) for i in range(2)])
        self.st = self._nst_old

    def _norm_rings_close(self):
        self.S.full_barrier()
        self._nst.close()

    def rmsnorm_to_hn(self, gname):
        S = self.S
        g0, _ = COLS[gname]
        self._norm_rings_open()
        for tc in range(NTC):
            ts = slice(tc * TC, (tc + 1) * TC)
            pt, pb = self.psum.get()
            for c in range(NCH):
                sq, sqb = self.sq_ring.get()
                S.op("act", lambda e: e.activation(sq[:], self.X[:, c, ts], AF.Square),
                     reads=[self.XB[c][tc]], writes=[sqb])
                S.op("pe", lambda e: e.matmul(pt[:], self.ones_f[:], sq[:], start=(c == 0), stop=(c == NCH - 1)),
                     reads=[sqb, self.constb], writes=[pb])
            rs, rsb = self.rstd_ring.get()
            S.op("act", lambda e: e.activation(rs[:], pt[:], AF.Sqrt, bias=self.eps_t[:], scale=1.0 / D),
                 reads=[pb, self.constb], writes=[rsb])
            S.op("dve", lambda e: e.reciprocal(rs[:], rs[:]), reads=[rsb], writes=[rsb])
            for c in range(NCH):
                S.op("dve", lambda e: e.scalar_tensor_tensor(
                    self.HN[:, c, ts], self.X[:, c, ts], self.cols[:, g0 + c:g0 + c + 1], rs[:],
                    ALU.mult, ALU.mult),
                    reads=[self.XB[c][tc], rsb, self.constb], writes=[self.HNB[c][tc]])

        self._norm_rings_close()

    def ffn(self, wg, wu, wd, gname):
        S = self.S
        groups = [(i, min(4, 22 - i)) for i in range(0, 22, 4)]

        def load(gi):
            f0, nf = groups[gi]
            slot = gi % 2
            S.dma(self.WG[slot][:, :, 0:nf * 128],
                  wg[:, f0 * 128:(f0 + nf) * 128].rearrange("(k p) n -> p k n", p=128),
                  writes=[self.WGB[slot]], queue="pool")
            S.dma(self.WU[slot][:, :, 0:nf * 128],
                  wu[:, f0 * 128:(f0 + nf) * 128].rearrange("(k p) n -> p k n", p=128),
                  writes=[self.WUB[slot]], queue="pool")
            S.dma(self.WD[slot][:, 0:nf, :],
                  wd[f0 * 128:(f0 + nf) * 128, :].rearrange("(f p) n -> p f n", p=128),
                  writes=[self.WDB[slot]], queue="pool")

        load(0)
        load(1)
        self.rmsnorm_to_hn(gname)
        for gi, (f0, nf) in enumerate(groups):
            if gi >= 1 and gi + 1 < len(groups):
                load(gi + 1)
            slot = gi % 2
            WG, WU, WD = self.WG[slot], self.WU[slot], self.WD[slot]
            for tc in range(NTC):
                ts = slice(tc * TC, (tc + 1) * TC)
                hreads = [self.HNB[c][tc] for c in range(NCH)]
                a_t, a_b = self.a_ring.get()
                for f in range(nf):
                    pg, pgb = self.psum.get()
                    pu, pub = self.psum.get()

                    def mm_g(e):
                        for k in range(NCH):
                            ins = e.matmul(pg[:], WG[:, k, f * 128:(f + 1) * 128], self.HN[:, k, ts],
                                           start=(k == 0), stop=(k == NCH - 1))
                        return ins

                    def mm_u(e):
                        for k in range(NCH):
                            ins = e.matmul(pu[:], WU[:, k, f * 128:(f + 1) * 128], self.HN[:, k, ts],
                                           start=(k == 0), stop=(k == NCH - 1))
                        return ins
                    S.op("pe", mm_g, reads=hreads + [self.WGB[slot]], writes=[pgb])
                    S.op("pe", mm_u, reads=hreads + [self.WUB[slot]], writes=[pub])
                    sg, sgb = self.sg_ring.get()
                    S.op("act", lambda e: e.activation(sg[:], pg[:], AF.Silu), reads=[pgb], writes=[sgb])
                    S.op("dve", lambda e: e.tensor_tensor(a_t[:, f, :], sg[:], pu[:], ALU.mult),
                         reads=[sgb, pub], writes=[a_b])
                for dc in range(NCH):
                    po, pob = self.psum.get()

                    def mm_d(e):
                        for f in range(nf):
                            ins = e.matmul(po[:], WD[:, f, dc * 128:(dc + 1) * 128], a_t[:, f, :],
                                           start=(f == 0), stop=(f == nf - 1))
                        return ins
                    S.op("pe", mm_d, reads=[a_b, self.WDB[slot]], writes=[pob])
                    S.op("dve", lambda e: e.scalar_tensor_tensor(
                        self.X[:, dc, ts], po[:], 0.5, self.X[:, dc, ts], ALU.mult, ALU.add),
                        reads=[pob, self.XB[dc][tc]], writes=[self.XB[dc][tc]])


    def load_w(self, tile_ap, dram_ap, buf):
        self.S.dma(tile_ap, dram_ap.rearrange("(k p) n -> p k n", p=128), writes=[buf], queue="pool")

    def dump_feat(self, name, tile, nchunks, buf_list):
        o = self.dout(name, [nchunks * 128, S_LEN])
        for c in range(nchunks):
            self.S.dma(o[c * 128:(c + 1) * 128, :], tile[:, c, :], reads=buf_list, queue="pool")


    def rwkv_branch_seq(self, w_rwkv, w2_d, a2_d, g2_d, gng_d, gnb_d):
        S = self.S
        CN = COLS
        NT = S_LEN // 128
        with ExitStack() as st4:
            old, self.st = self.st, st4
            WR = self.sb("WR", [128, NCH, 1792], BF16); WRB = Buf()
            W2 = self.sb("W2A2", [128, 512], F32); A2 = W2; G2 = self.sb("G2", [128, 512], F32)
            GNG = self.sb("GNG", [128, 512], F32); GNB = self.sb("GNB", [128, 512], F32)
            BO = self.sb("BO", [128, 128], F32); BOb = self.sb("BOb", [128, 128], BF16)
            ID2 = self.sb("ID2", [128, 64], BF16)
            OMK = self.sb("OMK", [128, 4], F32)
            cb = Buf()
            self.load_w(WR[:], w_rwkv, WRB)
            S.dma(W2[0:64, :], w2_d, writes=[cb]); S.dma(A2[64:128, :], a2_d, writes=[cb]); S.dma(G2[:], g2_d, writes=[cb])
            S.dma(GNG[:], gng_d, writes=[cb]); S.dma(GNB[:], gnb_d, writes=[cb])
            S.op("dve", lambda e: e.memset(BO[:], 0.0), reads=[cb], writes=[cb])
            S.op("dve", lambda e: e.memset(BO[0:64, 0:64], 1.0), reads=[cb], writes=[cb])
            S.op("dve", lambda e: e.memset(BO[64:128, 64:128], 1.0), reads=[cb], writes=[cb])
            S.op("dve", lambda e: e.tensor_copy(BOb[:], BO[:]), reads=[cb], writes=[cb])
            S.op("dve", lambda e: e.tensor_copy(ID2[0:64, :], self.ident_f[0:64, 0:64]), reads=[cb, self.constb], writes=[cb])
            S.op("dve", lambda e: e.tensor_copy(ID2[64:128, :], self.ident_f[64:128, 64:128]), reads=[cb, self.constb], writes=[cb])
            ka0 = CN["k_a"][0]
            S.op("dve", lambda e: e.tensor_scalar(OMK[:], self.cols[:, ka0:ka0 + 4], -1.0, 1.0, ALU.mult, ALU.add),
                 reads=[cb, self.constb], writes=[cb])
            P32 = self.sb("P32", [128, 14, 129], F32); P32B = Buf()
            DD = self.sb("DD", [128, 128], F32); DDB = Buf()
            CAR = self.sb("CAR", [128, 14, 1], F32)
            PL = P32[:, :, 1:129]; PLB = P32B
            TW = self.sb("TW", [64, 128], F32); SGg = self.sb("SGg", [128, 128], F32)
            WD = self.sb("WD", [128, 4, 128], F32); SIG = WD
            A32 = self.sb("A32", [128, 4, 128], F32)
            KK = self.sb("KK", [128, 4, 128], F32); SQ = self.sb("SQ", [128, 4, 128], F32)
            KKN = self.sb("KKN", [128, 4, 128], F32); NB = self.sb("NB", [128, 4, 128], F32)
            KM = self.sb("KM", [128, 4, 128], F32); BON = self.sb("BON", [128, 4, 128], F32)
            RM = self.sb("RM", [128, 4, 128, 2], BF16)
            VDr = Ring([self.sb("VD%d" % i, [128, 4, 64], BF16) for i in range(2)])
            H = self.sb("H", [128, 4, 64], F32); Hb = self.sb("Hb", [128, 4, 64], BF16); HK = self.sb("HK", [128, 4, 64], BF16)
            T1 = self.sb("T1", [128, 4, 64], F32); T2r = Ring([self.sb("T2_%d" % i, [128, 4, 64], F32) for i in range(2)])
            YST = [self.sb("YST%d" % i, [2, 4, 256], F32) for i in range(2)]; YSTB = [Buf(), Buf()]
            YTOK = A32[:].rearrange("p c t -> p (c t)").rearrange("p (c h v) -> p c h v", c=4, h=2); YTOKB = Buf()
            YC = KKN[:].rearrange("p c t -> p (c t)").rearrange("p (a v) -> p a v", a=8)
            ST8 = self.sb("ST8", [128, 8], F32); ST8b = self.sb("ST8b", [128, 8], F32)
            YF = SQ
            db = Buf(); hb = Buf(); hbb = Buf(); hkb = Buf(); t1b = Buf(); vrb = Buf(); vtb = Buf(); rmb = Buf(); yb = Buf()
            S.op("pool", lambda e: e.memset(P32[:], 0.0), writes=[P32B])
            S.op("pool", lambda e: e.memset(RM[:], 0.0), writes=[rmb])
            S.op("pool", lambda e: e.memset(H[:], 0.0), writes=[hb])
            mu0 = CN["mu"][0]; w00 = CN["w0"][0]; a00 = CN["a0"][0]; kk0 = CN["k_k"][0]; rk0 = CN["r_k"][0]
            ident = self.ident_f
            for i in range(NT):
                t0 = i * 128
                tcix = t0 // TC
                tsl = slice(t0, t0 + 128)
                hreads = [self.HNB[c][tcix] for c in range(NCH)]
                for cg in range(4):
                    c0 = cg * 4
                    n = min(4, 14 - c0)
                    p, pb = self.psum.get()

                    def mm(e):
                        for cc in range(n):
                            for k in range(NCH):
                                ins = e.matmul(p[:, cc * 128:(cc + 1) * 128], WR[:, k, (c0 + cc) * 128:(c0 + cc + 1) * 128],
                                               self.HN[:, k, tsl], start=(k == 0), stop=(k == NCH - 1))
                        return ins
                    S.op("pe", mm, reads=hreads + [WRB], writes=[pb])
                    S.op("act", lambda e: e.copy(P32[:, c0:c0 + n, 1:129], p[:, 0:n * 128].rearrange("p (c t) -> p c t", c=n)),
                         reads=[pb], writes=[P32B])
                S.op("dve", lambda e: e.tensor_copy(CAR[:], P32[:, :, 128:129]), reads=[P32B], writes=[DDB])
                for c in range(14):
                    S.op("dve", lambda e: e.tensor_tensor(DD[:], P32[:, c, 0:128], P32[:, c, 1:129], ALU.subtract), reads=[P32B, DDB], writes=[DDB])
                    S.op("dve", lambda e: e.scalar_tensor_tensor(P32[:, c, 1:129], DD[:], self.cols[:, mu0 + c:mu0 + c + 1], P32[:, c, 1:129],
                                                                 ALU.mult, ALU.add), reads=[DDB, P32B, self.constb], writes=[P32B])
                S.op("dve", lambda e: e.tensor_copy(P32[:, :, 0:1], CAR[:]), reads=[P32B, DDB], writes=[P32B])
                S.op("act", lambda e: e.activation(TW[:], PL[0:64, 12, :], AF.Tanh), reads=[PLB], writes=[db])
                S.op("act", lambda e: e.activation(SGg[:], PL[:, 13, :], AF.Sigmoid), reads=[PLB], writes=[db])
                pz, pzb = self.psum.get(); pa, pab = self.psum.get()

                def mmz(e):
                    for fc in range(4):
                        ins = e.matmul(pz[:, fc * 128:(fc + 1) * 128], W2[0:64, fc * 128:(fc + 1) * 128], TW[:], start=True, stop=True)
                    return ins

                def mma(e):
                    for fc in range(4):
                        ins = e.matmul(pa[:, fc * 128:(fc + 1) * 128], A2[64:128, fc * 128:(fc + 1) * 128], PL[64:128, 12, :], start=True, stop=True)
                    return ins

                S.op("pe", mmz, reads=[db, cb], writes=[pzb])
                S.op("pe", mma, reads=[PLB, cb], writes=[pab])
                for fc in range(4):
                    S.op("act", lambda e: e.activation(SIG[:, fc, :], pz[:, fc * 128:(fc + 1) * 128], AF.Sigmoid,
                                                       bias=self.cols[:, w00 + fc:w00 + fc + 1]), reads=[pzb, self.constb], writes=[db])
                    S.op("act", lambda e: e.activation(A32[:, fc, :], pa[:, fc * 128:(fc + 1) * 128], AF.Sigmoid,
                                                       bias=self.cols[:, a00 + fc:a00 + fc + 1]), reads=[pab, self.constb], writes=[db, YTOKB])
                S.op("act", lambda e: e.activation(WD[:], SIG[:], AF.Exp, scale=-0.6065306597126334), reads=[db], writes=[db])
                for fc in range(4):
                    S.op("dve", lambda e: e.tensor_scalar(KK[:, fc, :], PL[:, 4 + fc, :], self.cols[:, kk0 + fc:kk0 + fc + 1], None, ALU.mult),
                         reads=[PLB, self.constb], writes=[db])
                S.op("dve", lambda e: e.tensor_tensor(SQ[:], KK[:], KK[:], ALU.mult), reads=[db], writes=[db])
                pss, pssb = self.psum.get()
                S.op("pe", lambda e: e.matmul(pss[:], BO[:], SQ[:].rearrange("p c t -> p (c t)"), start=True, stop=True), reads=[db, cb], writes=[pssb])
                S.op("act", lambda e: e.activation(SQ[:], pss[:].rearrange("p (c t) -> p c t", c=4), AF.Sqrt), reads=[pssb, db], writes=[db])
                S.op("dve", lambda e: e.tensor_scalar(SQ[:], SQ[:], 1e-12, None, ALU.max), reads=[db], writes=[db])
                S.op("dve", lambda e: e.reciprocal(SQ[:], SQ[:]), reads=[db], writes=[db])
                S.op("dve", lambda e: e.tensor_tensor(KKN[:], KK[:], SQ[:], ALU.mult), reads=[db], writes=[db, yb])
                S.op("dve", lambda e: e.scalar_tensor_tensor(NB[:], KKN[:], -1.0, A32[:], ALU.mult, ALU.mult), reads=[db], writes=[db])
                for fc in range(4):
                    S.op("dve", lambda e: e.tensor_scalar(KK[:, fc, :], A32[:, fc, :], self.cols[:, ka0 + fc:ka0 + fc + 1], OMK[:, fc:fc + 1],
                                                          ALU.mult, ALU.add), reads=[db, cb, self.constb], writes=[db])
                S.op("dve", lambda e: e.tensor_tensor(KM[:], PL[:, 4:8, :], KK[:], ALU.mult), reads=[db, PLB], writes=[db])
                S.op("dve", lambda e: e.tensor_tensor(SQ[:], PL[:, 0:4, :], KM[:], ALU.mult), reads=[db, PLB], writes=[db])
                for fc in range(4):
                    S.op("dve", lambda e: e.tensor_scalar(SQ[:, fc, :], SQ[:, fc, :], self.cols[:, rk0 + fc:rk0 + fc + 1], None, ALU.mult),
                         reads=[db, self.constb], writes=[db])
                pbn, pbnb = self.psum.get()
                S.op("pe", lambda e: e.matmul(pbn[:], BO[:], SQ[:].rearrange("p c t -> p (c t)"), start=True, stop=True), reads=[db, cb], writes=[pbnb])
                S.op("dve", lambda e: e.tensor_tensor(BON[:], pbn[:].rearrange("p (c t) -> p c t", c=4), PL[:, 8:12, :], ALU.mult),
                     reads=[pbnb, PLB], writes=[db])
                S.op("dve", lambda e: e.tensor_copy(RM[0:64, :, :, 0], PL[0:64, 0:4, :]), reads=[PLB, rmb], writes=[rmb])
                S.op("dve", lambda e: e.tensor_copy(RM[64:128, :, :, 1], PL[64:128, 0:4, :]), reads=[PLB, rmb], writes=[rmb])
                for tt in range(128):
                    pvb_t, pvbb = self.psum.get()

                    VD, vdb = VDr.get()
                    S.op("pool", lambda e: e.tensor_tensor(VD[:], ID2[:].unsqueeze(1).to_broadcast([128, 4, 64]),
                                                           PL[:, 8:12, tt:tt + 1].to_broadcast([128, 4, 64]), ALU.mult),
                         reads=[PLB, cb], writes=[vdb])
                    S.op("pe", lambda e: e.matmul(pvb_t[:, 0:256], BOb[:], VD[:].rearrange("p c v -> p (c v)"), start=True, stop=True),
                         reads=[vdb, cb], writes=[pvbb])
                    T2, t2b = T2r.get()
                    S.op("pool" if False else "dve", lambda e: e.tensor_tensor(
                        T2[:], pvb_t[:, 0:256].rearrange("p (c v) -> p c v", c=4), KM[:, :, tt:tt + 1].to_broadcast([128, 4, 64]), ALU.mult),
                        reads=[pvbb, db], writes=[t2b])
                    S.op("dve", lambda e: e.tensor_tensor(HK[:], H[:], KKN[:, :, tt:tt + 1].to_broadcast([128, 4, 64]), ALU.mult),
                         reads=[hb, db], writes=[hkb])
                    psa, psab = self.psum.get()
                    S.op("pe", lambda e: e.matmul(psa[:, 0:256], BOb[:], HK[:].rearrange("p c v -> p (c v)"), start=True, stop=True),
                         reads=[hkb, cb], writes=[psab])
                    S.op("dve", lambda e: e.tensor_tensor(T1[:], psa[:, 0:256].rearrange("p (c v) -> p c v", c=4),
                                                          NB[:, :, tt:tt + 1].to_broadcast([128, 4, 64]), ALU.mult),
                         reads=[psab, db], writes=[t1b])
                    S.op("dve", lambda e: e.tensor_tensor(H[:], H[:], WD[:, :, tt:tt + 1].to_broadcast([128, 4, 64]), ALU.mult),
                         reads=[hb, db], writes=[hb])
                    S.op("dve", lambda e: e.tensor_tensor(T1[:], T1[:], T2[:], ALU.add), reads=[t1b, t2b], writes=[t1b])
                    S.op("dve", lambda e: e.tensor_tensor(H[:], H[:], T1[:], ALU.add), reads=[hb, t1b], writes=[hb])
                    S.op("act", lambda e: e.copy(Hb[:], H[:]), reads=[hb], writes=[hbb])
                    py, pyb = self.psum.get()

                    def mmy(e):
                        for fc in range(4):
                            ins = e.matmul(py[0:2, fc * 64:(fc + 1) * 64], RM[:, fc, tt, :], Hb[:, fc, :], start=True, stop=True)
                        return ins
                    S.op("pe", mmy, reads=[hbb, rmb], writes=[pyb])
                    slot = tt % 2
                    S.op("act", lambda e: e.copy(YST[slot][0:2, 0, :], py[0:2, 0:256]), reads=[pyb], writes=[YSTB[slot]])
                    for hp in range(2):
                        S.dma(YTOK[tt:tt + 1, :, hp, :], YST[slot][hp:hp + 1, 0, :].rearrange("p (c v) -> p c v", c=4),
                              reads=[YSTB[slot], db], writes=[YTOKB])
                YT8 = YTOK.rearrange("t c h v -> t (c h) v")
                S.op("dve", lambda e: e.tensor_reduce(ST8[:], YT8, AX.X, ALU.add), reads=[YTOKB], writes=[yb])
                S.op("dve", lambda e: e.tensor_scalar(ST8[:], ST8[:], 1.0 / 64, None, ALU.mult), reads=[yb], writes=[yb])
                S.op("dve", lambda e: e.tensor_tensor(YC, YT8, ST8[:].unsqueeze(2).to_broadcast([128, 8, 64]), ALU.subtract),
                     reads=[YTOKB, yb], writes=[yb, db])
                S.op("dve", lambda e: e.tensor_tensor(YTOK.rearrange("t c h v -> t (c h) v"), YC, YC, ALU.mult), reads=[yb, YTOKB], writes=[YTOKB])
                S.op("dve", lambda e: e.tensor_reduce(ST8b[:], YT8, AX.X, ALU.add), reads=[YTOKB], writes=[yb])
                S.op("act", lambda e: e.activation(ST8b[:], ST8b[:], AF.Sqrt, bias=self.gneps_t[:], scale=1.0 / 64), reads=[yb, self.constb], writes=[yb])
                S.op("dve", lambda e: e.reciprocal(ST8b[:], ST8b[:]), reads=[yb], writes=[yb])
                S.op("dve", lambda e: e.tensor_tensor(YC, YC, ST8b[:].unsqueeze(2).to_broadcast([128, 8, 64]), ALU.mult), reads=[yb], writes=[yb])
                YCf = YC.rearrange("t a v -> t (a v)")
                S.op("dve", lambda e: e.tensor_tensor(YCf, YCf, GNG[:], ALU.mult), reads=[yb, cb], writes=[yb])
                S.op("dve", lambda e: e.tensor_tensor(YCf, YCf, GNB[:], ALU.add), reads=[yb, cb], writes=[yb])
                pyt, pytb = self.psum.get(); pg, pgb = self.psum.get()

                def mmt2(e):
                    for fc in range(4):
                        ins = e.transpose(pyt[:, fc * 128:(fc + 1) * 128], YC[:, 2 * fc:2 * fc + 2, :].rearrange("t a v -> t (a v)"), ident[:])
                    for fc in range(4):
                        ins = e.matmul(pg[:, fc * 128:(fc + 1) * 128], G2[:, fc * 128:(fc + 1) * 128], SGg[:], start=True, stop=True)
                    return ins
                S.op("pe", mmt2, reads=[yb, self.constb, db, cb], writes=[pytb, pgb])
                S.op("dve", lambda e: e.tensor_tensor(YF[:], pyt[:].rearrange("p (c t) -> p c t", c=4), BON[:], ALU.add), reads=[pytb, db], writes=[yb, db])
                S.op("dve", lambda e: e.tensor_tensor(self.Y[0][:, :, tsl], YF[:], pg[:].rearrange("p (c t) -> p c t", c=4), ALU.mult),
                     reads=[yb, db, pgb], writes=[self.YB[0][tcix]])
            S.full_barrier()
            self.st = old


    def rwkv_branch(self, w_rwkv, w2_d, a2_d, g2_d, gng_d, gnb_d, mk_d):
        S = self.S
        CN = COLS
        NT = S_LEN // 128
        CDEC = 0.6065306597126334
        with ExitStack() as st4:
            old, self.st = self.st, st4
            WR = self.sb("WR", [128, NCH, 1792], BF16); WRB = Buf()
            W2 = self.sb("W2A2", [128, 512], F32); A2 = W2; G2 = self.sb("G2", [128, 512], BF16)
            BO = self.sb("BO", [128, 128], F32)
            ID2 = self.sb("ID2", [128, 64], F32)
            OMK = self.sb("OMK", [128, 4], F32)
            MSK = self.sb("MSK", [128, 3, 128], BF16)
            ONE64 = self.sb("ONE64", [128, 64], F32)
            cb = Buf()
            self.load_w(WR[:], w_rwkv, WRB)
            S.dma(W2[0:64, :], w2_d, writes=[cb]); S.dma(A2[64:128, :], a2_d, writes=[cb]); S.dma(G2[:], g2_d, writes=[cb], queue="pool")
            S.dma(MSK[:], mk_d, writes=[cb], queue="pool")
            self.rmsnorm_to_hn("mix_norm")
            S.op("dve", lambda e: e.memset(BO[:], 0.0), reads=[cb], writes=[cb])
            S.op("dve", lambda e: e.memset(BO[0:64, 0:64], 1.0), reads=[cb], writes=[cb])
            S.op("dve", lambda e: e.memset(BO[64:128, 64:128], 1.0), reads=[cb], writes=[cb])
            S.op("dve", lambda e: e.memset(ONE64[:], 1.0), reads=[cb], writes=[cb])
            S.op("dve", lambda e: e.tensor_copy(ID2[0:64, :], self.ident_f[0:64, 0:64]), reads=[cb, self.constb], writes=[cb])
            S.op("dve", lambda e: e.tensor_copy(ID2[64:128, :], self.ident_f[64:128, 64:128]), reads=[cb, self.constb], writes=[cb])
            ka0 = CN["k_a"][0]
            S.op("dve", lambda e: e.tensor_scalar(OMK[:], self.cols[:, ka0:ka0 + 4], -1.0, 1.0, ALU.mult, ALU.add),
                 reads=[cb, self.constb], writes=[cb])
            P32 = self.sb("P32", [128, 14, 129], F32); P32B = Buf()
            DD = self.sb("DD", [128, 128], F32); DDB = Buf()
            CAR = self.sb("CAR", [128, 14, 1], F32)
            PL = P32[:, :, 1:129]; PLB = P32B
            TW = self.sb("TW", [64, 128], F32); SGg = self.sb("SGg", [128, 128], BF16)
            f32t = lambda n: self.sb(n, [128, 4, 128], F32)
            SIG = f32t("SIG"); CUM = f32t("CUM"); A32 = f32t("A32"); KK = f32t("KK"); SQ = f32t("SQ")
            KKN = f32t("KKN"); NB = f32t("NB"); KM = f32t("KM"); BON = f32t("BON")
            AH = self.sb("AH", [128, 4, 128], BF16); KH = self.sb("KH", [128, 4, 128], BF16)
            BR = self.sb("BR", [128, 4, 2, 128], BF16)
            AT = self.sb("AT", [128, 512], BF16); KTt = self.sb("KTt", [128, 512], BF16); VTOK = self.sb("VTOK", [128, 512], BF16)
            WB = self.sb("WB", [128, 8, 128], BF16); BU = self.sb("BU", [128, 8, 128], BF16)
            bf8 = lambda n: self.sb(n, [128, 8, 128], BF16)
            X0 = bf8("X0"); XT0 = bf8("XT0"); LKT = bf8("LKT"); GRA = bf8("GRA"); GRK = bf8("GRK"); TT = bf8("TT")
            XA1 = [self.sb("XA1_%d" % i, [128, 4, 128], BF16) for i in range(2)]
            XTA1 = [self.sb("XTA1_%d" % i, [128, 4, 128], BF16) for i in range(2)]
            TA1 = [self.sb("TA1_%d" % i, [128, 4, 128], BF16) for i in range(2)]
            RTm = self.sb("RTm", [128, 4, 2, 128], BF16)
            M0Ts = SIG[:].rearrange("p c t -> p (c t)").rearrange("p (a k) -> p a k", a=8)
            N0s = CUM[:].rearrange("p c t -> p (c t)").rearrange("p (a k) -> p a k", a=8)
            PCt = self.sb("PCt", [128, 2, 4], F32)
            H = self.sb("H", [128, 4, 64], F32); Hb = self.sb("Hb", [128, 2, 4, 64], BF16)
            nbb = Buf(); kmb = Buf(); sgb = Buf(); cub = Buf()
            YTOK = NB[:].rearrange("p c t -> p (c t)").rearrange("p (c h v) -> p c h v", c=4, h=2); YTOKB = nbb
            YC = KM[:].rearrange("p c t -> p (c t)").rearrange("p (a v) -> p a v", a=8)
            ST8 = self.sb("ST8", [128, 8], F32); ST8b = self.sb("ST8b", [128, 8], F32)
            YF = SQ
            db = Buf(); hb = Buf(); hbb = Buf(); gb_ = Buf(); chb = Buf(); tkb = Buf(); yb = Buf(); mnb = Buf(); rtb = Buf()
            S.op("pool", lambda e: e.memset(P32[:], 0.0), writes=[P32B])
            S.op("pool", lambda e: e.memset(RTm[:], 0.0), writes=[rtb])
            S.op("pool", lambda e: e.memset(H[:], 0.0), writes=[hb])
            mu0 = CN["mu"][0]; w00 = CN["w0"][0]; a00 = CN["a0"][0]; kk0 = CN["k_k"][0]; rk0 = CN["r_k"][0]
            gg0 = CN["gn_g"][0]; gb0 = CN["gn_b"][0]
            ident = self.ident_f
            c4 = lambda ap: ap.rearrange("p (c t) -> p c t", c=4)
            def emit_proj(i2):
                t0_ = i2 * 128
                tsl_ = slice(t0_, t0_ + 128)
                hreads_ = [self.HNB[c][t0_ // TC] for c in range(NCH)]
                for cg in range(4):
                    c0 = cg * 4
                    n = min(4, 14 - c0)
                    p, pb = self.psum.get()

                    def mm(e):
                        for cc in range(n):
                            for k in range(NCH):
                                ins = e.matmul(p[:, cc * 128:(cc + 1) * 128], WR[:, k, (c0 + cc) * 128:(c0 + cc + 1) * 128],
                                               self.HN[:, k, tsl_], start=(k == 0), stop=(k == NCH - 1))
                        return ins
                    S.op("pe", mm, reads=hreads_ + [WRB], writes=[pb])
                    S.op("act", lambda e: e.copy(P32[:, c0:c0 + n, 1:129], p[:, 0:n * 128].rearrange("p (c t) -> p c t", c=n)),
                         reads=[pb], writes=[P32B])

            def lerp_list(i2):
                ops = []
                ops.append(lambda: S.op("pool", lambda e: e.tensor_copy(CAR[:], P32[:, :, 128:129]), reads=[P32B], writes=[DDB]))
                for c in range(14):
                    def one(c=c):
                        S.op("pool", lambda e: e.tensor_tensor(DD[:], P32[:, c, 0:128], P32[:, c, 1:129], ALU.subtract), reads=[P32B, DDB], writes=[DDB])
                        S.op("pool", lambda e: e.tensor_tensor(DD[:], DD[:], self.cols[:, mu0 + c:mu0 + c + 1].to_broadcast([128, 128]), ALU.mult),
                             reads=[DDB, self.constb], writes=[DDB])
                        S.op("pool", lambda e: e.tensor_tensor(P32[:, c, 1:129], P32[:, c, 1:129], DD[:], ALU.add), reads=[DDB, P32B], writes=[P32B])
                    ops.append(one)
                ops.append(lambda: S.op("pool", lambda e: e.tensor_copy(P32[:, :, 0:1], CAR[:]), reads=[P32B, DDB], writes=[P32B]))
                return ops

            pending = []
            for i in range(self._rk_tiles):
                t0 = i * 128
                tcix = t0 // TC
                tsl = slice(t0, t0 + 128)
                hreads = [self.HNB[c][tcix] for c in range(NCH)]
                if i == 0:
                    emit_proj(0)
                    for fn_ in lerp_list(0):
                        fn_()
                for fn_ in pending:
                    fn_()
                pending = []
                S.op("act", lambda e: e.activation(TW[:], PL[0:64, 12, :], AF.Tanh), reads=[PLB], writes=[db])
                S.op("act", lambda e: e.activation(SGg[:], PL[:, 13, :], AF.Sigmoid), reads=[PLB], writes=[db])
                pz, pzb = self.psum.get(); pa, pab = self.psum.get()

                def mmz(e):
                    for fc in range(4):
                        ins = e.matmul(pz[:, fc * 128:(fc + 1) * 128], W2[0:64, fc * 128:(fc + 1) * 128], TW[:], start=True, stop=True)
                    return ins

                def mma(e):
                    for fc in range(4):
                        ins = e.matmul(pa[:, fc * 128:(fc + 1) * 128], A2[64:128, fc * 128:(fc + 1) * 128], PL[64:128, 12, :], start=True, stop=True)
                    return ins
                S.op("pe", mmz, reads=[db, cb], writes=[pzb])
                S.op("pe", mma, reads=[PLB, cb], writes=[pab])
                for fc in range(4):
                    S.op("act", lambda e: e.activation(SIG[:, fc, :], pz[:, fc * 128:(fc + 1) * 128], AF.Sigmoid,
                                                       bias=self.cols[:, w00 + fc:w00 + fc + 1]), reads=[pzb, self.constb], writes=[db, sgb])
                    S.op("act", lambda e: e.activation(A32[:, fc, :], pa[:, fc * 128:(fc + 1) * 128], AF.Sigmoid,
                                                       bias=self.cols[:, a00 + fc:a00 + fc + 1]), reads=[pab, self.constb], writes=[db])
                bc4 = lambda c0_: self.cols[:, c0_:c0_ + 4].unsqueeze(2).to_broadcast([128, 4, 128])
                S.op("dve", lambda e: e.tensor_tensor(KK[:], PL[:, 4:8, :], bc4(kk0), ALU.mult), reads=[PLB, self.constb], writes=[db])
                S.op("dve", lambda e: e.tensor_tensor(SQ[:], KK[:], KK[:], ALU.mult), reads=[db], writes=[db])
                pss, pssb = self.psum.get()
                S.op("pe", lambda e: e.matmul(pss[:], BO[:], SQ[:].rearrange("p c t -> p (c t)"), start=True, stop=True), reads=[db, cb], writes=[pssb])
                S.op("act", lambda e: e.activation(SQ[:], c4(pss[:]), AF.Sqrt), reads=[pssb, db], writes=[db])
                S.op("dve", lambda e: e.tensor_scalar(SQ[:], SQ[:], 1e-12, None, ALU.max), reads=[db], writes=[db])
                S.op("dve", lambda e: e.reciprocal(SQ[:], SQ[:]), reads=[db], writes=[db])
                S.op("dve", lambda e: e.tensor_tensor(KKN[:], KK[:], SQ[:], ALU.mult), reads=[db], writes=[db])
                S.op("dve", lambda e: e.tensor_tensor(NB[:], KKN[:], A32[:], ALU.mult), reads=[db], writes=[db, nbb])
                S.op("pool", lambda e: e.tensor_tensor(KK[:], A32[:], bc4(ka0), ALU.mult), reads=[db, self.constb], writes=[db])
                S.op("pool", lambda e: e.tensor_tensor(KK[:], KK[:], OMK[:].unsqueeze(2).to_broadcast([128, 4, 128]), ALU.add), reads=[db, cb], writes=[db])
                S.op("dve", lambda e: e.tensor_tensor(KM[:], PL[:, 4:8, :], KK[:], ALU.mult), reads=[db, PLB], writes=[db, kmb])
                S.op("dve", lambda e: e.tensor_tensor(SQ[:], PL[:, 0:4, :], KM[:], ALU.mult), reads=[db, PLB, kmb], writes=[db])
                S.op("pool", lambda e: e.tensor_tensor(SQ[:], SQ[:], bc4(rk0), ALU.mult), reads=[db, self.constb], writes=[db])
                pbn, pbnb = self.psum.get()
                S.op("pe", lambda e: e.matmul(pbn[:], BO[:], SQ[:].rearrange("p c t -> p (c t)"), start=True, stop=True), reads=[db, cb], writes=[pbnb])
                S.op("dve", lambda e: e.tensor_tensor(BON[:], c4(pbn[:]), PL[:, 8:12, :], ALU.mult), reads=[pbnb, PLB], writes=[db])
                for fc in range(4):
                    for c2 in range(2):
                        cs = slice(c2 * 64, (c2 + 1) * 64)
                        S.op("dve", lambda e: e.tensor_tensor_scan(CUM[:, fc, cs], ONE64[:], SIG[:, fc, cs], 0.0, ALU.mult, ALU.add),
                             reads=[db, cb, sgb], writes=[db, cub])
                S.op("pool", lambda e: e.tensor_tensor(SQ[:], CUM[:], SIG[:], ALU.subtract), reads=[db, sgb, cub], writes=[db])
                S.op("act", lambda e: e.activation(A32[:], CUM[:], AF.Exp, scale=CDEC), reads=[db, cub], writes=[db])
                S.op("act", lambda e: e.activation(CUM[:], CUM[:], AF.Exp, scale=-CDEC), reads=[db], writes=[db, cub])
                S.op("act", lambda e: e.activation(SQ[:], SQ[:], AF.Exp, scale=-CDEC), reads=[db], writes=[db])
                S.op("dve", lambda e: e.tensor_copy(PCt[:, 0, :], CUM[:, :, 63]), reads=[db, chb, cub], writes=[chb])
                S.op("dve", lambda e: e.tensor_copy(PCt[:, 1, :], CUM[:, :, 127]), reads=[db, chb, cub], writes=[chb])
                S.op("dve", lambda e: e.tensor_tensor(NB[:], NB[:], A32[:], ALU.mult), reads=[db], writes=[db, nbb])
                S.op("dve", lambda e: e.tensor_tensor(KM[:], KM[:], A32[:], ALU.mult), reads=[db], writes=[db, kmb])
                S.op("dve", lambda e: e.tensor_tensor(KKN[:], KKN[:], SQ[:], ALU.mult), reads=[db], writes=[db])
                S.op("dve", lambda e: e.tensor_tensor(KK[:], PL[:, 0:4, :], CUM[:], ALU.mult), reads=[db, PLB, cub], writes=[db])
                S.op("act", lambda e: e.copy(AH[:], NB[:]), reads=[db, gb_, nbb], writes=[gb_])
                S.op("act", lambda e: e.copy(KH[:], KM[:]), reads=[db, gb_, kmb], writes=[gb_])
                S.op("pool", lambda e: e.tensor_copy(BR[:, :, 0, :], KKN[:]), reads=[db, gb_], writes=[gb_])
                S.op("pool", lambda e: e.tensor_copy(BR[:, :, 1, :], KK[:]), reads=[db, gb_], writes=[gb_])
                if "dumpah" in self.debug and i == 0:
                    for nm, tl in (("ah", AH), ("kh", KH), ("br", BR)):
                        o_ = self.dout("dbg_" + nm, [128, tl[:].rearrange("p ... -> p (...)").shape[1] if False else (512 if nm != "br" else 1024)])
                        S.dma(o_, tl[:].rearrange("p c t -> p (c t)") if nm != "br" else tl[:].rearrange("p c a t -> p (c a t)"), reads=[gb_], queue="pool")
                    for nm, tl in (("nb", NB), ("km", KM), ("kkn", KKN), ("en", A32), ("ep", CUM)):
                        o_ = self.dout("dbg_" + nm, [128, 512])
                        S.dma(o_, tl[:].rearrange("p c t -> p (c t)"), reads=[db, nbb, kmb, cub])
                for src, dst_fn in ((NB, None), (KM, None), (KKN, None), (None, None)):
                    pass
                tr_jobs = [(lambda fc: NB[:, fc, :], "AT"), (lambda fc: KM[:, fc, :], "KT"),
                           (lambda fc: KKN[:, fc, :], "BT"), (lambda fc: PL[:, 8 + fc, :], "VT")]
                for srcf, kind in tr_jobs:
                    ptr, ptrb = self.psum.get()

                    def mmt(e):
                        for fc in range(4):
                            ins = e.transpose(ptr[:, fc * 128:(fc + 1) * 128], srcf(fc), ident[:])
                        return ins
                    S.op("pe", mmt, reads=[db, PLB, self.constb, nbb, kmb], writes=[ptrb])
                    if kind == "AT":
                        S.op("act", lambda e: e.copy(AT[:], ptr[:]), reads=[ptrb, tkb], writes=[tkb])
                    elif kind == "KT":
                        S.op("dve", lambda e: e.tensor_copy(KTt[:], ptr[:]), reads=[ptrb, tkb], writes=[tkb])
                    elif kind == "BT":
                        S.op("act", lambda e: e.activation(WB[:, :, 0:64], ptr[:].rearrange("p (h k) -> p h k", h=8), AF.Copy, scale=-1.0),
                             reads=[ptrb, tkb], writes=[tkb])
                    else:
                        S.op("dve", lambda e: e.tensor_copy(VTOK[:], ptr[:]), reads=[ptrb, tkb], writes=[tkb])
                if self._rk_stage <= 0:
                    continue
                for fc in range(4):
                    ka_, ab0, ab1 = self.psum.get_pair_idx()
                    kb_, bb0, bb1 = self.psum.get_pair_idx()
                    PA = self.PS[:, ka_:ka_ + 2, :]; PB = self.PS[:, kb_:kb_ + 2, :]

                    def mmg(e):
                        for h2 in range(2):
                            rs = slice(h2 * 64, (h2 + 1) * 64)
                            brr = BR[rs, fc, :, :].rearrange("p a t -> p (a t)")
                            e.matmul(PA[:, h2, 0:256], AH[rs, fc, :], brr, start=True, stop=True)
                            e.matmul(PA[:, h2, 256:512], KH[rs, fc, :], brr, start=True, stop=True)
                            ins = e.matmul(PB[:, h2, 0:128], BR[rs, fc, 0, :], AH[rs, fc, :], start=True, stop=True)
                        return ins
                    S.op("pe", mmg, reads=[gb_], writes=[ab0, ab1, bb0, bb1])
                    hs = slice(2 * fc, 2 * fc + 2)
                    PAv = PA.rearrange("p h (q b t) -> p h q b t", q=2, b=2)
                    mk = lambda j: MSK[:, j, :].unsqueeze(1).to_broadcast([128, 2, 128])
                    S.op("dve", lambda e: e.tensor_tensor(X0[:, hs, :], PAv[:, :, 0, 0, :], mk(0), ALU.mult), reads=[ab0, ab1, cb, mnb], writes=[mnb])
                    S.op("dve", lambda e: e.tensor_tensor(GRA[:, hs, :], PAv[:, :, 0, 1, :], mk(2), ALU.mult), reads=[ab0, ab1, cb, mnb], writes=[mnb])
                    S.op("dve", lambda e: e.tensor_tensor(LKT[:, hs, :], PAv[:, :, 1, 0, :], mk(0), ALU.mult), reads=[ab0, ab1, cb, mnb], writes=[mnb])
                    S.op("dve", lambda e: e.tensor_tensor(GRK[:, hs, :], PAv[:, :, 1, 1, :], mk(2), ALU.mult), reads=[ab0, ab1, cb, mnb], writes=[mnb])
                    S.op("dve", lambda e: e.tensor_tensor(XT0[:, hs, :], PB[:, :, 0:128], mk(1), ALU.mult), reads=[bb0, bb1, cb, mnb], writes=[mnb])
                if self._rk_stage <= 1:
                    continue
                if i + 1 < self._rk_tiles:
                    emit_proj(i + 1)
                    pending = lerp_list(i + 1)
                hst = []
                for half in range(2):
                    h0 = half * 4
                    st_ = dict(xb=Buf(), xtb=Buf(), tb=Buf(),
                               xbufs=[X0[:, h0:h0 + 4, :], XA1[half][:]], xtbufs=[XT0[:, h0:h0 + 4, :], XTA1[half][:]],
                               tbufs=[TA1[half][:], TT[:, h0:h0 + 4, :]])
                    hst.append(st_)
                    S.op("pool", lambda e: e.tensor_tensor(st_["tbufs"][0], st_["xbufs"][0], ident[:].unsqueeze(1).to_broadcast([128, 4, 128]), ALU.add),
                         reads=[mnb, self.constb, st_["tb"]], writes=[st_["tb"]])
                for lv in range(1, 6):
                    for half in range(2):
                        st_ = hst[half]
                        xb_, xtb_, tb_ = st_["xb"], st_["xtb"], st_["tb"]
                        Xp, XTp, Tp = st_["xbufs"][(lv - 1) % 2], st_["xtbufs"][(lv - 1) % 2], st_["tbufs"][(lv - 1) % 2]
                        Xn, XTn, Tn = st_["xbufs"][lv % 2], st_["xtbufs"][lv % 2], st_["tbufs"][lv % 2]
                        pxt, pxtb = self.psum.get()

                        def mmxt(e):
                            for j in range(4):
                                ins = e.matmul(pxt[:, j * 128:(j + 1) * 128], Xp[:, j, :], XTp[:, j, :], start=True, stop=True)
                            return ins
                        S.op("pe", mmxt, reads=[mnb, xb_, xtb_], writes=[pxtb])
                        if lv < 5:
                            px, pxb = self.psum.get()

                            def mmx(e):
                                for j in range(4):
                                    ins = e.matmul(px[:, j * 128:(j + 1) * 128], XTp[:, j, :], Xp[:, j, :], start=True, stop=True)
                                return ins
                            S.op("pe", mmx, reads=[mnb, xb_, xtb_], writes=[pxb])
                        S.op("act", lambda e: e.copy(XTn, c4(pxt[:])), reads=[pxtb, xtb_, mnb], writes=[xtb_])
                        if lv < 5:
                            S.op("act", lambda e: e.copy(Xn, c4(px[:])), reads=[pxb, xb_, mnb], writes=[xb_])
                        ptt, pttb = self.psum.get()

                        def mmtt(e):
                            for j in range(4):
                                ins = e.matmul(ptt[:, j * 128:(j + 1) * 128], XTn[:, j, :], Tp[:, j, :], start=True, stop=True)
                            return ins
                        S.op("pe", mmtt, reads=[xtb_, tb_], writes=[pttb])
                        S.op("dve", lambda e: e.tensor_tensor(Tn, c4(ptt[:]), Tp, ALU.add), reads=[pttb, tb_, mnb], writes=[tb_] + ([mnb] if lv == 5 else []))
                        for _ in range(2):
                            if pending:
                                pending.pop(0)()
                if self._rk_stage <= 2:
                    continue
                plk, plkb = self.psum.get()

                def mmlk(e):
                    for h in range(8):
                        ins = e.matmul(plk[:, h * 64:(h + 1) * 64], LKT[:, h, :], VTOK[:, h * 64:(h + 1) * 64], start=True, stop=True)
                    return ins
                S.op("pe", mmlk, reads=[mnb, tkb], writes=[plkb])
                S.op("act", lambda e: e.copy(WB[:, :, 64:128], plk[:].rearrange("p (h v) -> p h v", h=8)), reads=[plkb, tkb], writes=[tkb])
                for half in range(2):
                    pbu, pbub = self.psum.get()

                    def mmbu(e):
                        for j in range(4):
                            h = half * 4 + j
                            ins = e.matmul(pbu[:, j * 128:(j + 1) * 128], TT[:, h, :], WB[:, h, :], start=True, stop=True)
                        return ins
                    S.op("pe", mmbu, reads=[mnb, tkb], writes=[pbub])
                    S.op("act", lambda e: e.copy(BU[:, half * 4:half * 4 + 4, :], c4(pbu[:])), reads=[pbub, chb], writes=[chb])
                if self._rk_stage <= 3:
                    continue
                prt, prtb = self.psum.get()
                km_, mb0, mb1 = self.psum.get_pair_idx()
                PM = self.PS[:, km_:km_ + 2, :]

                def mmrt(e):
                    for h in range(8):
                        rs = slice((h % 2) * 64, (h % 2) * 64 + 64)
                        fc = h // 2
                        ins = e.matmul(prt[rs, fc * 128:(fc + 1) * 128], BU[:, h, 0:64], GRA[:, h, :], start=True, stop=True)
                    return ins

                def mmmn(e):
                    for c2 in range(2):
                        cr = slice(c2 * 64, (c2 + 1) * 64)
                        for h in range(8):
                            rs = slice((h % 2) * 64, (h % 2) * 64 + 64)
                            fc = h // 2
                            o = fc * 64
                            e.matmul(PM[rs, c2, o:o + 64], BU[cr, h, 0:64], AT[cr, h * 64:(h + 1) * 64], start=True, stop=True)
                            e.matmul(PM[rs, c2, 256 + o:256 + o + 64], AT[cr, h * 64:(h + 1) * 64], BU[cr, h, 64:128], start=True, stop=False)
                            ins = e.matmul(PM[rs, c2, 256 + o:256 + o + 64], KTt[cr, h * 64:(h + 1) * 64], VTOK[cr, h * 64:(h + 1) * 64], start=False, stop=True)
                    return ins
                S.op("pe", mmrt, reads=[chb, mnb], writes=[prtb])
                S.op("pe", mmmn, reads=[chb, tkb], writes=[mb0, mb1])
                prv = c4(prt[:])
                S.op("dve", lambda e: e.tensor_tensor(RTm[:, :, 0, 0:64], prv[:, :, 0:64], KK[:, :, 0:64], ALU.add), reads=[prtb, db, rtb], writes=[rtb])
                S.op("dve", lambda e: e.tensor_tensor(RTm[:, :, 1, 64:128], prv[:, :, 64:128], KK[:, :, 64:128], ALU.add), reads=[prtb, db, rtb], writes=[rtb])
                M0v = M0Ts.rearrange("p (a c) k -> p a c k", a=2)
                N0v = N0s.rearrange("p (a c) k -> p a c k", a=2)
                S.op("dve", lambda e: e.tensor_tensor(M0v, PM[:, :, 0:256].rearrange("p a (c k) -> p a c k", c=4),
                                                      ID2[:].unsqueeze(1).unsqueeze(1).to_broadcast([128, 2, 4, 64]), ALU.add),
                     reads=[mb0, mb1, cb, chb], writes=[chb, sgb])
                S.op("act", lambda e: e.copy(N0v, PM[:, :, 256:512].rearrange("p a (c k) -> p a c k", c=4)), reads=[mb0, mb1, chb], writes=[chb, cub])
                if self._rk_stage <= 4:
                    continue
                for c2 in range(2):
                    S.op("act", lambda e: e.copy(Hb[:, c2, :, :], H[:]), reads=[hb, hbb], writes=[hbb])
                    phe, pheb = self.psum.get(); pho, phob = self.psum.get()

                    def mmh(e):
                        for par, bank in ((0, phe), (1, pho)):
                            rs = slice(par * 64, par * 64 + 64)
                            for fc in range(4):
                                ins = e.matmul(bank[rs, fc * 64:(fc + 1) * 64], M0Ts[rs, c2 * 4 + fc, :], H[rs, fc, :], start=True, stop=True)
                        return ins
                    S.op("pe", mmh, reads=[chb, hb, sgb], writes=[pheb, phob])
                    S.op("dve", lambda e: e.tensor_tensor(H[0:64], phe[0:64, 0:256].rearrange("p (c v) -> p c v", c=4), N0s[0:64, c2 * 4:c2 * 4 + 4, :], ALU.add),
                         reads=[pheb, chb, cub, hb], writes=[hb])
                    S.op("dve", lambda e: e.tensor_tensor(H[64:128], pho[64:128, 0:256].rearrange("p (c v) -> p c v", c=4), N0s[64:128, c2 * 4:c2 * 4 + 4, :], ALU.add),
                         reads=[phob, chb, cub, hb], writes=[hb])
                    S.op("dve", lambda e: e.tensor_tensor(H[:], H[:], PCt[:, c2, :].unsqueeze(2).to_broadcast([128, 4, 64]), ALU.mult),
                         reads=[chb, hb], writes=[hb])
                if self._rk_stage <= 5:
                    continue
                ky_, yb0, yb1 = self.psum.get_pair_idx()
                PY = self.PS[:, ky_:ky_ + 2, :]

                def mmy(e):
                    for par in range(2):
                        rs = slice(par * 64, par * 64 + 64)
                        for fc in range(4):
                            h = 2 * fc + par
                            o = PY[:, par, fc * 64:(fc + 1) * 64]
                            e.matmul(o, GRA[:, h, :], BU[:, h, 64:128], start=True, stop=False)
                            e.matmul(o, GRK[:, h, :], VTOK[:, h * 64:(h + 1) * 64], start=False, stop=False)
                            e.matmul(o, RTm[rs, fc, 0, :], Hb[rs, 0, fc, :], start=False, stop=False)
                            ins = e.matmul(o, RTm[rs, fc, 1, :], Hb[rs, 1, fc, :], start=False, stop=True)
                    return ins
                S.op("pe", mmy, reads=[mnb, chb, tkb, rtb, hbb], writes=[yb0, yb1])
                S.op("act", lambda e: e.copy(YTOK.rearrange("t c h v -> t h c v"), PY[:, :, 0:256].rearrange("t h (c v) -> t h c v", c=4)),
                     reads=[yb0, yb1, YTOKB], writes=[YTOKB])
                if self._rk_stage <= 6:
                    continue
                YT8 = YTOK.rearrange("t c h v -> t (c h) v")
                S.op("dve", lambda e: e.tensor_reduce(ST8[:], YT8, AX.X, ALU.add), reads=[YTOKB], writes=[yb])
                S.op("dve", lambda e: e.tensor_scalar(ST8[:], ST8[:], 1.0 / 64, None, ALU.mult), reads=[yb], writes=[yb])
                S.op("dve", lambda e: e.tensor_tensor(YC, YT8, ST8[:].unsqueeze(2).to_broadcast([128, 8, 64]), ALU.subtract),
                     reads=[YTOKB, yb], writes=[yb, kmb])
                S.op("pool", lambda e: e.tensor_tensor(YT8, YC, YC, ALU.mult), reads=[yb, YTOKB, kmb], writes=[YTOKB])
                S.op("dve", lambda e: e.tensor_reduce(ST8b[:], YT8, AX.X, ALU.add), reads=[YTOKB], writes=[yb])
                S.op("act", lambda e: e.activation(ST8b[:], ST8b[:], AF.Sqrt, bias=self.gneps_t[:], scale=1.0 / 64), reads=[yb, self.constb], writes=[yb])
                S.op("dve", lambda e: e.reciprocal(ST8b[:], ST8b[:]), reads=[yb], writes=[yb])
                S.op("dve", lambda e: e.tensor_tensor(YC, YC, ST8b[:].unsqueeze(2).to_broadcast([128, 8, 64]), ALU.mult), reads=[yb], writes=[yb, kmb])
                pyt, pytb = self.psum.get(); pg, pgb = self.psum.get()

                def mmt2(e):
                    for fc in range(4):
                        ins = e.transpose(pyt[:, fc * 128:(fc + 1) * 128], YC[:, 2 * fc:2 * fc + 2, :].rearrange("t a v -> t (a v)"), ident[:])
                    for fc in range(4):
                        ins = e.matmul(pg[:, fc * 128:(fc + 1) * 128], G2[:, fc * 128:(fc + 1) * 128], SGg[:], start=True, stop=True)
                    return ins
                S.op("pe", mmt2, reads=[yb, self.constb, db, cb, kmb], writes=[pytb, pgb])
                for fc in range(4):
                    S.op("dve", lambda e: e.tensor_scalar(YF[:, fc, :], pyt[:, fc * 128:(fc + 1) * 128], self.cols[:, gg0 + fc:gg0 + fc + 1],
                                                          self.cols[:, gb0 + fc:gb0 + fc + 1], ALU.mult, ALU.add),
                         reads=[pytb, db, self.constb], writes=[db])
                S.op("pool", lambda e: e.tensor_tensor(YF[:], YF[:], BON[:], ALU.add), reads=[db], writes=[db])
                S.op("dve", lambda e: e.tensor_tensor(self.Y[0][:, :, tsl], YF[:], c4(pg[:]), ALU.mult),
                     reads=[db, pgb], writes=[self.YB[0][tcix]])
            S.full_barrier()
            self.st = old

    def nsa_branch(self, d):
        S = self.S
        NT = S_LEN // 128
        with ExitStack() as st4:
            old, self.st = self.st, st4
            cb = Buf()
            KT = self.sb("KT", [128, 2, S_LEN], BF16); KTB = Buf()
            VT = self.sb("VT", [128, NT, 256], BF16); VTB = Buf()
            KC = self.sb("KC", [128, 127], BF16); VC = self.sb("VC", [128, 128], BF16); kcb = Buf()
            BM = self.sb("BM", [128, 3, 2, 512], BF16)
            BVC = self.sb("BVC", [32, 2, 512], BF16)
            stB = ExitStack(); self.st = stB
            KCMP = self.sb("KCMP", [128, S_LEN], BF16); VCT = self.sb("VCT", [128, S_LEN], BF16)
            WKVx = self.sb("WKVx", [128, 8192], BF16); wkvb = Buf()
            WKV = WKVx[:, 0:NCH * 768].rearrange("p (k n) -> p k n", k=NCH)
            W1v = WKVx[:].rearrange("p (l m) -> p l m", l=32); w1vb = wkvb
            W1k = self.Y[1][:].rearrange("p c t -> p (c t)").rearrange("p (l m) -> p l m", l=32); w1kb = Buf()
            PET2 = self.sb("PET2", [128, 2, 32], BF16); W2D2 = self.sb("W2D2", [128, 2, 2, 128], BF16); cwb = Buf()
            HID = self.sb("HID", [128, 2, 127], BF16)
            ZZ = self.sb("ZZ", [128, 127], F32); Z2 = self.sb("Z2", [128, 127], F32); BC = self.sb("BCc", [128, 1], F32)
            zb = Buf(); hb_ = Buf()
            self.load_w(WKV, d["w_kvn"], wkvb)
            w1k_d = d["cmp_w1"][0].rearrange("(l dd) m -> dd l m", dd=64)
            S.dma(W1k[0:64], w1k_d, writes=[w1kb] + self.YB[1], queue="pool"); S.dma(W1k[64:128], w1k_d, writes=[w1kb] + self.YB[1], queue="pool")
            for kv in range(2):
                S.dma(PET2[0:64, kv, :], d["cmp_peT"][kv], writes=[cwb], queue="pool"); S.dma(PET2[64:128, kv, :], d["cmp_peT"][kv], writes=[cwb], queue="pool")
                w2v = d["cmp_w2"][kv].rearrange("(c p) n -> p c n", p=128)
                S.dma(W2D2[:, kv, :, 0:64], w2v, writes=[cwb], queue="pool"); S.dma(W2D2[:, kv, :, 64:128], w2v, writes=[cwb], queue="pool")
            for tc in range(NTC):
                ts = slice(tc * TC, (tc + 1) * TC)
                hreads = [self.HNB[c][tc] for c in range(NCH)]
                for dst, col in ((KCMP[:, ts], 0), (VCT[:, ts], 128), (KT[:, 0, ts], 256), (KT[:, 1, ts], 512)):
                    p, pb = self.psum.get()

                    def mm(e):
                        for k in range(NCH):
                            ins = e.matmul(p[:], WKV[:, k, col:col + 128], self.HN[:, k, ts], start=(k == 0), stop=(k == NCH - 1))
                        return ins
                    S.op("pe", mm, reads=hreads + [wkvb], writes=[pb])
                    S.op("act", lambda e: e.copy(dst, p[:]), reads=[pb], writes=[KTB])
                for tl in range(4):
                    tile = tc * 4 + tl
                    tq = slice(tile * 128, (tile + 1) * 128)
                    p, pb = self.psum.get()

                    def mm(e):
                        for k in range(NCH):
                            e.matmul(p[:, 0:128], self.HN[:, k, tq], WKV[:, k, 384:512], start=(k == 0), stop=(k == NCH - 1))
                        for k in range(NCH):
                            ins = e.matmul(p[:, 128:256], self.HN[:, k, tq], WKV[:, k, 640:768], start=(k == 0), stop=(k == NCH - 1))
                        return ins
                    S.op("pe", mm, reads=hreads + [wkvb], writes=[pb])
                    S.op("dve", lambda e: e.tensor_copy(VT[:, tile, :], p[:, 0:256]), reads=[pb], writes=[VTB])
            w1v_d = d["cmp_w1"][1].rearrange("(l dd) m -> dd l m", dd=64)
            S.dma(W1v[0:64], w1v_d, writes=[w1vb], queue="pool"); S.dma(W1v[64:128], w1v_d, writes=[w1vb], queue="pool")
            for kv in range(2):
                W1 = W1k if kv == 0 else W1v
                wb = w1kb if kv == 0 else w1vb
                PET = PET2[:, kv, :]
                W2D = W2D2[:, kv, :, :]
                yrd = self.YB[1] if kv == 0 else []
                SRC = KCMP if kv == 0 else VCT
                for g in range(2):
                    gs = slice(g * 64, (g + 1) * 64)
                    for mc in range(2):
                        ph, phb = self.psum.get(); pbias, pbb = self.psum.get()

                        def mm(e):
                            for l in range(32):
                                ins = e.matmul(ph[:, 0:127], W1[gs, l, mc * 128:(mc + 1) * 128], SRC[gs, l:l + 16 * 126 + 1:16],
                                               start=(l == 0), stop=(l == 31))
                            return ins

                        def mmb(e):
                            for l in range(32):
                                ins = e.matmul(pbias[:, 0:1], W1[gs, l, mc * 128:(mc + 1) * 128], PET[gs, l:l + 1], start=(l == 0), stop=(l == 31))
                            return ins
                        S.op("pe", mm, reads=[wb, KTB] + yrd, writes=[phb])
                        S.op("pe", mmb, reads=[wb, cwb] + yrd, writes=[pbb])
                        S.op("act", lambda e: e.copy(BC[:], pbias[:, 0:1]), reads=[pbb, zb], writes=[zb])
                        S.op("dve", lambda e: e.tensor_scalar(ZZ[:], ph[:, 0:127], BC[:, 0:1], None, ALU.add), reads=[phb, zb], writes=[zb])
                        S.op("dve", lambda e: e.tensor_tensor(Z2[:], ZZ[:], ZZ[:], ALU.mult), reads=[zb], writes=[zb])
                        S.op("dve", lambda e: e.tensor_scalar(Z2[:], Z2[:], 0.044715, 1.0, ALU.mult, ALU.add), reads=[zb], writes=[zb])
                        S.op("dve", lambda e: e.tensor_tensor(Z2[:], Z2[:], ZZ[:], ALU.mult), reads=[zb], writes=[zb])
                        S.op("act", lambda e: e.activation(Z2[:], Z2[:], AF.Sigmoid, scale=1.5957691216057308), reads=[zb], writes=[zb])
                        S.op("dve", lambda e: e.tensor_tensor(HID[:, mc, :], ZZ[:], Z2[:], ALU.mult), reads=[zb, hb_], writes=[hb_])
                    po, pob = self.psum.get()
                    if kv == 0:
                        def mm2(e):
                            for mc in range(2):
                                ins = e.matmul(po[:, 0:127], W2D[:, mc, :], HID[:, mc, :], start=(mc == 0), stop=(mc == 1))
                            return ins
                        S.op("pe", mm2, reads=[hb_, cwb], writes=[pob])
                        S.op("act", lambda e: e.copy(KC[gs, :], po[gs, 0:127]), reads=[pob], writes=[kcb])
                    else:
                        def mm2(e):
                            for mc in range(2):
                                ins = e.matmul(po[0:127, 0:64], HID[:, mc, :], W2D[:, mc, 0:64], start=(mc == 0), stop=(mc == 1))
                            return ins
                        S.op("pe", mm2, reads=[hb_, cwb], writes=[pob])
                        S.op("act", lambda e: e.copy(VC[0:127, gs], po[0:127, 0:64]), reads=[pob], writes=[kcb])
            stA = ExitStack(); self.st = stA
            G1 = self.sb("G1", [128, 2, 512], F32); G2_ = self.sb("G2b", [128, 2, 512], F32); MK = self.sb("MK", [128, 128], F32)
            gb = Buf()
            S.dma(G2_[:], d["t31"], writes=[gb])
            for kind in range(3):
                S.dma(G1[:], d["bmg"][kind], reads=[gb], writes=[gb])
                S.dma(MK[:], d["msk"][kind], reads=[gb], writes=[gb])
                S.op("dve", lambda e: e.tensor_tensor(G1[:], G1[:], G2_[:], ALU.subtract), reads=[gb], writes=[gb])
                S.op("dve", lambda e: e.tensor_tensor(BM[:, kind, :, :].rearrange("p g (j q) -> p (g j) q", j=4),
                                                      G1[:].rearrange("p g (j q) -> p (g j) q", j=4),
                                                      MK[:].unsqueeze(1).to_broadcast([128, 8, 128]), ALU.add), reads=[gb], writes=[cb, gb])
            S.dma(G1[0:32, :, :], d["bvcg"], reads=[gb], writes=[gb])
            S.dma(MK[0:32, :], d["mskc"], reads=[gb], writes=[gb])
            S.op("dve", lambda e: e.tensor_tensor(G1[0:32], G1[0:32], G2_[0:32], ALU.subtract), reads=[gb], writes=[gb])
            S.op("dve", lambda e: e.tensor_tensor(BVC[:].rearrange("p g (j q) -> p (g j) q", j=4),
                                                  G1[0:32].rearrange("p g (j q) -> p (g j) q", j=4),
                                                  MK[0:32, :].unsqueeze(1).to_broadcast([32, 8, 128]), ALU.add), reads=[gb], writes=[cb, gb])
            S.full_barrier()
            stA.close()
            self.st = stB
            S.full_barrier()
            stB.close(); self.st = st4
            if "kcvc" in self.debug:
                okc = self.dout("dbg_kc", [128, 127]); ovc = self.dout("dbg_vc", [127, 128])
                S.dma(okc, KC[:], reads=[kcb], queue="pool"); S.dma(ovc, VC[0:127, :], reads=[kcb], queue="pool")
            WQ = self.sb("WQN", [128, NCH, 512], BF16); WGN = self.sb("WGN", [128, NCH, 24], BF16)
            SHCF = self.sb("SHCF", [32, 247], BF16); EF = self.sb("EF", [32, S_LEN], BF16)
            OV = self.sb("OV", [128, 32], BF16); AB = self.sb("ABF", [128, 2, 64], F32)
            SELG = self.sb("SELG", [24, 12, 128], BF16); IDb = self.sb("IDb", [128, 128], BF16)
            self.load_w(WQ[:], d["w_qn"], cb)
            self.load_w(WGN[:], d["w_gn"], cb)
            S.dma(SHCF[:], d["shcf"], writes=[cb], queue="pool"); S.dma(EF[:], d["efull"], writes=[cb], queue="pool")
            S.dma(OV[0:127, :], d["ov"], writes=[cb], queue="pool"); S.dma(AB[:], d["abf"], writes=[cb])
            S.dma(SELG[:], d["selg"], writes=[cb], queue="pool")
            S.op("dve", lambda e: e.tensor_copy(IDb[:], self.ident_f[:]), reads=[self.constb, cb], writes=[cb])
            QS = self.sb("QS", [128, 4, 128], BF16); qsb = Buf()
            GS = self.sb("GS", [24, 128], BF16); gsb = Buf()
            pt_ring = Ring([self.sb("PT%d" % i, [128, 512], BF16) for i in range(4)])
            RR = self.sb("RR", [128, 512], F32); rrb = Buf()
            RRc = self.sb("RRc", [128, 512], F32); rcb = Buf()
            PTc = [self.sb("PTc%d" % g, [128, 512], BF16) for g in range(2)]; ptcb = [Buf(), Buf()]
            YA = self.sb("YA", [128, 512], F32); yab = Buf()
            PN = self.sb("PN", [128, 512], BF16); pnb = Buf()
            IMP = self.sb("IMP", [128, 32], F32); IM2 = self.sb("IM2", [128, 32], F32); MX = self.sb("MX8", [128, 8], F32); ib = Buf()
            NMT = [self.sb("NMT%d" % g, [32, 4, 128], BF16) for g in range(2)]; nmb = [Buf(), Buf()]
            st_ring = Ring(self.banks[0:3], self.bankb[0:3])
            OD = [(self.banks[3], self.bankb[3], self.banks[4], self.bankb[4]),
                  (self.banks[5], self.bankb[5], self.banks[6], self.bankb[6])]
            ms_ring = Ring(self.banks[7:8], self.bankb[7:8])
            LOOK = 2
            for i in range(NT):
                tq = slice(i * 128, (i + 1) * 128)
                tcix = i // 4
                hreads = [self.HNB[c][tcix] for c in range(NCH)]
                p, pb = ms_ring.get()

                def mmq(e):
                    for j in range(4):
                        for k in range(NCH):
                            ins = e.matmul(p[:, j * 128:(j + 1) * 128], WQ[:, k, j * 128:(j + 1) * 128], self.HN[:, k, tq], start=(k == 0), stop=(k == NCH - 1))
                    return ins
                S.op("pe", mmq, reads=hreads + [cb], writes=[pb])
                S.op("act", lambda e: e.activation(QS[:].rearrange("p j q -> p (j q)"), p[:], AF.Copy, scale=0.125), reads=[pb], writes=[qsb])
                p2, pb2 = ms_ring.get()

                def mmg(e):
                    for k in range(NCH):
                        ins = e.matmul(p2[0:24, 0:128], WGN[:, k, :], self.HN[:, k, tq], start=(k == 0), stop=(k == NCH - 1))
                    return ins
                S.op("pe", mmg, reads=hreads + [cb], writes=[pb2])
                S.op("act", lambda e: e.activation(GS[:], p2[0:24, 0:128], AF.Sigmoid), reads=[pb2], writes=[gsb])

                def tiles_of(br):
                    if br == 0:
                        return [None]
                    if br == 1:
                        return list(range(0, i + 1))
                    return list(range(max(0, i - 4), i + 1))
                odset = {0: 0, 1: 0, 2: 1}

                def emit_scores(step):
                    br, g, kt, first, last = step
                    gs = slice(g * 64, (g + 1) * 64)
                    qrhs = QS[gs, :, :].rearrange("p j q -> p (j q)")
                    rows = 127 if br == 0 else 128
                    stp, stb = st_ring.get()
                    mms_list = []
                    if br == 0:
                        mms_list.append((KC[gs, :], qrhs))
                        mms_list.append((SHCF[:, 120 - 8 * i:247 - 8 * i], BVC[:, g, :]))
                    else:
                        mms_list.append((KT[gs, br - 1, kt * 128:(kt + 1) * 128], qrhs))
                        if br == 1 and i >= 8:
                            mms_list.append((EF[:, kt * 128:(kt + 1) * 128], NMT[g][:].rearrange("p j q -> p (j q)")))
                        if kt == i:
                            mms_list.append((IDb[:], BM[:, 0, g, :]))
                        elif kt == i - 1:
                            mms_list.append((IDb[:], BM[:, 1, g, :]))
                        elif br == 2 and kt == i - 4:
                            mms_list.append((IDb[:], BM[:, 2, g, :]))

                    def mms(e):
                        for n_, (l_, r_) in enumerate(mms_list):
                            ins = e.matmul(stp[0:rows, :], l_, r_, start=(n_ == 0), stop=(n_ == len(mms_list) - 1))
                        return ins
                    S.op("pe", mms, reads=[qsb, KTB, kcb, cb, nmb[g]], writes=[stb])
                    if br == 0:
                        PT, ptb = PTc[g], ptcb[g]
                    else:
                        PT, ptb = pt_ring.get()
                    S.op("act", lambda e: e.activation(PT[0:rows, :], stp[0:rows, :], AF.Exp), reads=[stb], writes=[ptb])
                    return (PT, ptb, rows)

                def emit_pv(step, ctx):
                    br, g, kt, first, last = step
                    PT, ptb, rows = ctx
                    gs = slice(g * 64, (g + 1) * 64)
                    O, Ob, DN, Db = OD[odset[br]]
                    if br == 0:
                        vl = VC[0:127, gs]
                    else:
                        c0 = (0 if br == 1 else 128) + g * 64
                        vl = VT[:, kt, c0:c0 + 64]

                    def mmo(e):
                        e.matmul(O[gs, :], vl, PT[0:rows, :], start=first, stop=last)
                        return e.matmul(DN[gs, :], self.ones_b[0:rows, 0:64], PT[0:rows, :], start=first, stop=last)
                    S.op("pe", mmo, reads=[ptb, VTB, kcb, self.constb], writes=[Ob, Db])

                def cmp_extras(g, ctx):
                    PT, ptb, rows = ctx
                    th = []
                    box = {}

                    def t0():
                        box["pd2"], box["pdb2"] = ms_ring.get()
                        S.op("pe", lambda e: e.matmul(box["pd2"][0:127, :], self.ones_b[0:127, 0:127], PT[0:127, :], start=True, stop=True),
                             reads=[ptb, self.constb], writes=[box["pdb2"]])
                        S.op("dve", lambda e: e.tensor_scalar(RRc[0:127, :], box["pd2"][0:127, :], 1e-30, None, ALU.max), reads=[box["pdb2"], rcb], writes=[rcb])
                    th.append(t0)
                    th.append(lambda: S.op("dve", lambda e: e.reciprocal(RRc[0:127, :], RRc[0:127, :]), reads=[rcb], writes=[rcb]))
                    th.append(lambda: S.op("dve", lambda e: e.tensor_tensor(PN[0:127, :], PT[0:127, :], RRc[0:127, :], ALU.mult), reads=[rcb, ptb, pnb], writes=[pnb]))

                    def t3():
                        box["pim"], box["pimb"] = ms_ring.get()

                        def mmi(e):
                            for j in range(4):
                                ins = e.matmul(box["pim"][:, 0:32], PN[0:127, j * 128:(j + 1) * 128], OV[0:127, :], start=(j == 0), stop=(j == 3))
                            return ins
                        S.op("pe", mmi, reads=[pnb, cb], writes=[box["pimb"]])
                        o0 = 32 - 2 * i
                        S.op("dve", lambda e: e.tensor_tensor(IMP[:], box["pim"][:, 0:32], AB[:, 0, o0:o0 + 32], ALU.mult), reads=[box["pimb"], cb, ib], writes=[ib])
                    th.append(t3)
                    o0 = 32 - 2 * i
                    th.append(lambda: S.op("dve", lambda e: e.tensor_tensor(IMP[:], IMP[:], AB[:, 1, o0:o0 + 32], ALU.add), reads=[ib, cb], writes=[ib]))
                    th.append(lambda: S.op("dve", lambda e: e.memset(IMP[:, 0:1], 1e6), reads=[ib], writes=[ib]))
                    th.append(lambda: S.op("dve", lambda e: e.max(MX[:], IMP[:]), reads=[ib], writes=[ib]))
                    th.append(lambda: S.op("dve", lambda e: e.match_replace(IM2[:], MX[:], IMP[:], 0.0), reads=[ib], writes=[ib]))
                    th.append(lambda: S.op("dve", lambda e: e.max(MX[:], IM2[:]), reads=[ib], writes=[ib]))
                    th.append(lambda: S.op("dve", lambda e: e.match_replace(IM2[:], MX[:], IM2[:], 0.0), reads=[ib], writes=[ib]))
                    th.append(lambda: S.op("dve", lambda e: e.tensor_tensor(IM2[:], IMP[:], IM2[:], ALU.subtract), reads=[ib], writes=[ib]))
                    th.append(lambda: S.op("dve", lambda e: e.tensor_scalar(IM2[:], IM2[:], 0.0, None, ALU.is_gt), reads=[ib], writes=[ib]))
                    th.append(lambda: S.op("dve", lambda e: e.tensor_scalar(IM2[:], IM2[:], 30000.0, -30000.0, ALU.mult, ALU.add), reads=[ib], writes=[ib]))

                    def tl():
                        ptr, ptrb = ms_ring.get()
                        S.op("pe", lambda e: e.transpose(ptr[0:32, 0:128], IM2[:], self.ident_f[:]), reads=[ib, self.constb], writes=[ptrb])
                        S.op("dve", lambda e: e.tensor_copy(NMT[g][:], ptr[0:32, 0:128].unsqueeze(1).to_broadcast([32, 4, 128])),
                             reads=[ptrb], writes=[nmb[g]])
                    th.append(tl)
                    return th

                def finalize(br):
                    O, Ob, DN, Db = OD[odset[br]]
                    S.op("dve", lambda e: e.tensor_scalar(RR[:], DN[:], 1e-30, None, ALU.max), reads=[Db, rrb], writes=[rrb])
                    S.op("dve", lambda e: e.reciprocal(RR[:], RR[:]), reads=[rrb], writes=[rrb])
                    pgb_, pgbb = ms_ring.get()

                    def mmgb(e):
                        for j in range(4):
                            ins = e.matmul(pgb_[:, j * 128:(j + 1) * 128], SELG[:, br * 4 + j, :], GS[:], start=True, stop=True)
                        return ins
                    S.op("pe", mmgb, reads=[gsb, cb], writes=[pgbb])
                    S.op("dve", lambda e: e.tensor_tensor(RR[:], RR[:], pgb_[:], ALU.mult), reads=[rrb, pgbb], writes=[rrb])
                    if br == 0:
                        S.op("dve", lambda e: e.tensor_tensor(YA[:], O[:], RR[:], ALU.mult), reads=[Ob, rrb, yab], writes=[yab])
                    else:
                        S.op("dve", lambda e: e.tensor_tensor(RR[:], O[:], RR[:], ALU.mult), reads=[Ob, rrb], writes=[rrb])
                        S.op("dve", lambda e: e.tensor_tensor(YA[:], YA[:], RR[:], ALU.add), reads=[rrb, yab], writes=[yab])

                extras = []
                for g in range(2):
                    st_ = (0, g, None, True, True)
                    ctx = emit_scores(st_)
                    emit_pv(st_, ctx)
                    if i >= 8:
                        extras += cmp_extras(g, ctx)
                finalize(0)

                def run_steps(steps, fill):
                    ctxs = {}
                    for n in range(len(steps) + LOOK):
                        if n < len(steps):
                            ctxs[n] = emit_scores(steps[n])
                        m = n - LOOK
                        if m >= 0:
                            emit_pv(steps[m], ctxs.pop(m))
                            br_, g_, kt_, f_, l_ = steps[m]
                            if g_ == 1 and l_:
                                finalize(br_)
                        for _ in range(3):
                            if fill:
                                fill.pop(0)()

                def mk_steps(br):
                    out = []
                    for g in range(2):
                        tl_ = tiles_of(br)
                        for ti, kt in enumerate(tl_):
                            out.append((br, g, kt, ti == 0, ti == len(tl_) - 1))
                    return out
                run_steps(mk_steps(2), extras)
                while extras:
                    extras.pop(0)()
                run_steps(mk_steps(1), [])
                S.op("act", lambda e: e.copy(self.Y[1][:, :, tq], YA[:].rearrange("p (j q) -> p j q", j=4)), reads=[yab], writes=[self.YB[1][tcix]])
            S.full_barrier()
            self.st = old

    def mem_branch(self, memT, wk_d, wv_d, wqm_d):
        S = self.S
        with ExitStack() as st4:
            old, self.st = self.st, st4
            self._norm_rings_open()
            WQ = self.sb("WQM", [128, NCH, 512], BF16); WQB = Buf()
            KHT = self.sb("KHT", [128, 4, 256], BF16); KHTB = Buf()
            VH = self.sb("VH", [128, 2, 512], BF16); VHB = Buf()
            st5 = ExitStack()
            self.st = st5
            MT = self.sb("MT", [128, NCH, 256], F32); MTB = Buf()
            MN = self.sb("MN", [128, NCH, 256], BF16); MNB = Buf()
            WK = self.sb("WK", [128, NCH, 512], BF16); WKB = Buf()
            WV = self.sb("WV", [128, NCH, 512], BF16); WVB = Buf()
            mr = self.sb("mrstd", [128, 256], F32); mrb = Buf()
            S.dma(MT[:], memT.rearrange("(c p) m -> p c m", p=128), writes=[MTB])
            self.load_w(WK[:], wk_d, WKB)
            self.load_w(WV[:], wv_d, WVB)
            self.load_w(WQ[:], wqm_d, WQB)
            g0, _ = COLS["mem_norm"]
            pt, pb = self.psum.get()
            for c in range(NCH):
                sq, sqb = self.sq_ring.get()
                S.op("act", lambda e: e.activation(sq[:, 0:256], MT[:, c, :], AF.Square), reads=[MTB], writes=[sqb])
                S.op("pe", lambda e: e.matmul(pt[:, 0:256], self.ones_f[:], sq[:, 0:256], start=(c == 0), stop=(c == NCH - 1)),
                     reads=[sqb, self.constb], writes=[pb])
            S.op("act", lambda e: e.activation(mr[:], pt[:, 0:256], AF.Sqrt, bias=self.eps_t[:], scale=1.0 / D),
                 reads=[pb, self.constb], writes=[mrb])
            S.op("dve", lambda e: e.reciprocal(mr[:], mr[:]), reads=[mrb], writes=[mrb])
            for c in range(NCH):
                S.op("dve", lambda e: e.scalar_tensor_tensor(MN[:, c, :], MT[:, c, :], self.cols[:, g0 + c:g0 + c + 1], mr[:],
                                                             ALU.mult, ALU.mult),
                     reads=[MTB, mrb, self.constb], writes=[MNB])
            for h in range(4):
                p, pb = self.psum.get()

                def mm(e):
                    for k in range(NCH):
                        ins = e.matmul(p[:, 0:256], WK[:, k, h * 128:(h + 1) * 128], MN[:, k, :], start=(k == 0), stop=(k == NCH - 1))
                    return ins
                S.op("pe", mm, reads=[WKB, MNB], writes=[pb])
                S.op("act", lambda e: e.copy(KHT[:, h, :], p[:, 0:256]), reads=[pb], writes=[KHTB])
            for mt in range(2):
                p, pb = self.psum.get()

                def mm(e):
                    for k in range(NCH):
                        ins = e.matmul(p[:], MN[:, k, mt * 128:(mt + 1) * 128], WV[:, k, :], start=(k == 0), stop=(k == NCH - 1))
                    return ins
                S.op("pe", mm, reads=[WVB, MNB], writes=[pb])
                S.op("act", lambda e: e.copy(VH[:, mt, :], p[:]), reads=[pb], writes=[VHB])
            S.full_barrier()
            st5.close()
            self.st = st4
            qm_ring = Ring([self.sb("qm%d" % i, [128, TC], BF16) for i in range(2)])
            pt_ring = Ring([self.sb("pt%d" % i, [128, 2, TC], BF16) for i in range(2)])
            rd_ring = self.sq_ring
            scale = 128.0 ** -0.5
            def stage1(tc, h):
                ts = slice(tc * TC, (tc + 1) * TC)
                hreads = [self.HNB[c][tc] for c in range(NCH)]
                p, pb = self.psum.get()

                def mm(e):
                    for k in range(NCH):
                        ins = e.matmul(p[:], WQ[:, k, h * 128:(h + 1) * 128], self.HN[:, k, ts], start=(k == 0), stop=(k == NCH - 1))
                    return ins
                S.op("pe", mm, reads=hreads + [WQB], writes=[pb])
                qm, qmb = qm_ring.get()
                S.op("dve", lambda e: e.tensor_copy(qm[:], p[:]), reads=[pb], writes=[qmb])
                ptile, ptb = pt_ring.get()
                for mt in range(2):
                    ps_, psb = self.psum.get()
                    S.op("pe", lambda e: e.matmul(ps_[:], KHT[:, h, mt * 128:(mt + 1) * 128], qm[:], start=True, stop=True),
                         reads=[KHTB, qmb], writes=[psb])
                    S.op("act", lambda e: e.activation(ptile[:, mt, :], ps_[:], AF.Exp, scale=scale), reads=[psb], writes=[ptb])
                return (ptile, ptb)

            def stage2(tc, h, ctx):
                ptile, ptb = ctx
                ts = slice(tc * TC, (tc + 1) * TC)
                po, pob = self.psum.get()
                pd, pdb = self.psum.get()

                def mm_o(e):
                    for mt in range(2):
                        ins = e.matmul(po[:], VH[:, mt, h * 128:(h + 1) * 128], ptile[:, mt, :], start=(mt == 0), stop=(mt == 1))
                    return ins

                def mm_d(e):
                    for mt in range(2):
                        ins = e.matmul(pd[:], self.ones_b[:], ptile[:, mt, :], start=(mt == 0), stop=(mt == 1))
                    return ins
                S.op("pe", mm_o, reads=[VHB, ptb], writes=[pob])
                S.op("pe", mm_d, reads=[ptb, self.constb], writes=[pdb])
                rd, rdb = rd_ring.get()
                S.op("dve", lambda e: e.reciprocal(rd[:], pd[:]), reads=[pdb], writes=[rdb])
                S.op("dve", lambda e: e.tensor_tensor(self.Y[2][:, h, ts], po[:], rd[:], ALU.mult),
                     reads=[pob, rdb], writes=[self.YB[2][tc]])
            items = [(tc, h) for tc in range(NTC) for h in range(4)]
            prev = None
            for it in items:
                ctx = stage1(*it)
                if prev is not None:
                    stage2(*prev)
                prev = (it[0], it[1], ctx)
            stage2(*prev)
            S.full_barrier()
            self.st = old
        self._nst.close()

    def fold(self, br, wgb_d, wbr_d, first):
        S = self.S
        with ExitStack() as st4:
            old, self.st = self.st, st4
            WGB = [self.sb("WGBr%d" % i, [128, NCH, 128], BF16) for i in range(2)]; WGBB = [Buf(), Buf()]
            WBR = [self.sb("WBR%d" % i, [128, 4, 128], BF16) for i in range(2)]; WBRB = [Buf(), Buf()]
            gt_ring = Ring([self.sb("gt%d" % i, [128, TC], F32) for i in range(2)])
            t_ring = Ring([self.sb("mt%d" % i, [128, TC], F32) for i in range(2)])

            def load(dc):
                sl = dc % 2
                c0 = br * D + dc * 128
                S.dma(WGB[sl][:], wgb_d[:, c0:c0 + 128].rearrange("(k p) n -> p k n", p=128), writes=[WGBB[sl]], queue="pool")
                S.dma(WBR[sl][:], wbr_d[:, dc * 128:(dc + 1) * 128].rearrange("(k p) n -> p k n", p=128), writes=[WBRB[sl]], queue="pool")
            load(0)
            for dc in range(NCH):
                if dc + 1 < NCH:
                    load(dc + 1)
                sl = dc % 2
                for tc in range(NTC):
                    ts = slice(tc * TC, (tc + 1) * TC)
                    hreads = [self.HNB[c][tc] for c in range(NCH)]
                    pg, pgb = self.psum.get()
                    py, pyb = self.psum.get()

                    def mm_g(e):
                        for k in range(NCH):
                            ins = e.matmul(pg[:], WGB[sl][:, k, :], self.HN[:, k, ts], start=(k == 0), stop=(k == NCH - 1))
                        return ins

                    def mm_y(e):
                        for k in range(4):
                            ins = e.matmul(py[:], WBR[sl][:, k, :], self.Y[br][:, k, ts], start=(k == 0), stop=(k == 3))
                        return ins
                    S.op("pe", mm_g, reads=hreads + [WGBB[sl]], writes=[pgb])
                    S.op("pe", mm_y, reads=[self.YB[br][tc], WBRB[sl]], writes=[pyb])
                    gt, gtb = gt_ring.get()
                    S.op("act", lambda e: e.activation(gt[:], pg[:], AF.Sigmoid), reads=[pgb], writes=[gtb])
                    if first:
                        S.op("dve", lambda e: e.tensor_tensor(self.M[:, dc, ts], gt[:], py[:], ALU.mult),
                             reads=[gtb, pyb], writes=[self.MB[dc][tc]])
                    else:
                        t, tb = t_ring.get()
                        S.op("dve", lambda e: e.tensor_tensor(t[:], gt[:], py[:], ALU.mult), reads=[gtb, pyb], writes=[tb])
                        S.op("pool", lambda e: e.tensor_tensor(self.M[:, dc, ts], self.M[:, dc, ts], t[:], ALU.add),
                             reads=[tb, self.MB[dc][tc]], writes=[self.MB[dc][tc]])
            S.full_barrier()
            self.st = old

    def outproj(self, wout_d):
        S = self.S
        with ExitStack() as st4:
            old, self.st = self.st, st4
            WO = self.sb("WO", [128, NCH, D], BF16); WOB = Buf()
            self.load_w(WO[:], wout_d, WOB)
            for tc in range(NTC):
                ts = slice(tc * TC, (tc + 1) * TC)
                for d2 in range(NCH):
                    po, pob = self.psum.get()

                    def mm(e):
                        for k in range(NCH):
                            ins = e.matmul(po[:], WO[:, k, d2 * 128:(d2 + 1) * 128], self.M[:, k, ts], start=(k == 0), stop=(k == NCH - 1))
                        return ins
                    S.op("pe", mm, reads=[self.MB[k][tc] for k in range(NCH)] + [WOB], writes=[pob])
                    S.op("dve", lambda e: e.tensor_tensor(self.X[:, d2, ts], po[:], self.X[:, d2, ts], ALU.add),
                         reads=[pob, self.XB[d2][tc]], writes=[self.XB[d2][tc]])
            S.full_barrier()
            self.st = old

    def final_norm_out(self, outT):
        S = self.S
        g0, _ = COLS["final_norm"]
        self._norm_rings_open()
        for tc in range(NTC):
            ts = slice(tc * TC, (tc + 1) * TC)
            pt, pb = self.psum.get()
            for c in range(NCH):
                sq, sqb = self.sq_ring.get()
                S.op("act", lambda e: e.activation(sq[:], self.X[:, c, ts], AF.Square),
                     reads=[self.XB[c][tc]], writes=[sqb])
                S.op("pe", lambda e: e.matmul(pt[:], self.ones_f[:], sq[:], start=(c == 0), stop=(c == NCH - 1)),
                     reads=[sqb, self.constb], writes=[pb])
            rs, rsb = self.rstd_ring.get()
            S.op("act", lambda e: e.activation(rs[:], pt[:], AF.Sqrt, bias=self.eps_t[:], scale=1.0 / D),
                 reads=[pb, self.constb], writes=[rsb])
            S.op("dve", lambda e: e.reciprocal(rs[:], rs[:]), reads=[rsb], writes=[rsb])
            for c in range(NCH):
                S.op("dve", lambda e: e.scalar_tensor_tensor(
                    self.X[:, c, ts], self.X[:, c, ts], self.cols[:, g0 + c:g0 + c + 1], rs[:],
                    ALU.mult, ALU.mult),
                    reads=[self.XB[c][tc], rsb, self.constb], writes=[self.XB[c][tc]])
                S.dma(outT[c * 128:(c + 1) * 128, ts], self.X[:, c, ts], reads=[self.XB[c][tc]])
        self._norm_rings_close()

    def dump_x(self, name):
        o = self.dout(name, [D, S_LEN])
        for c in range(NCH):
            for tc in range(NTC):
                ts = slice(tc * TC, (tc + 1) * TC)
                self.S.dma(o[c * 128:(c + 1) * 128, ts], self.X[:, c, ts], reads=[self.XB[c][tc]])

    def build(self, stop_after=None):
        nc = self.nc
        dbg = self.debug
        xT = self.din("xT", [D, S_LEN])
        cols_d = self.din("cols", [128, NCOLS])
        f1g = self.din("ffn1_w_gate", [D, DFF]); f1u = self.din("ffn1_w_up", [D, DFF]); f1d = self.din("ffn1_w_down", [DFF, D])
        f2g = self.din("ffn2_w_gate", [D, DFF]); f2u = self.din("ffn2_w_up", [D, DFF]); f2d = self.din("ffn2_w_down", [DFF, D])
        memT = self.din("memT", [D, 256])
        mem_wk = self.din("mem_w_k", [D, 512]); mem_wv = self.din("mem_w_v", [D, 512])
        w_qm = self.din("w_qm", [D, 512])
        w_gb = self.din("w_gb", [D, 3 * D])
        w_br = [self.din(n, [512, D]) for n in ("w_br_rwkv", "w_br_nsa_p", "w_br_mem")]
        w_out = self.din("w_out", [D, D])
        w_rwkv = self.din("w_rwkv", [D, 1792])
        w2_d = self.din("rwkv_w2", [64, 512]); a2_d = self.din("rwkv_a2", [64, 512]); g2_d = self.din("rwkv_g2", [128, 512])
        gng_d = self.din("gng_rep", [128, 512]); gnb_d = self.din("gnb_rep", [128, 512])
        ident_d = self.din("ident", [128, 128])
        rmk_d = self.din("rwkv_masks", [128, 3, 128])
        nd = {}
        nd["w_qn"] = self.din("w_qn", [D, 512]); nd["w_gn"] = self.din("w_gn", [D, 24]); nd["w_kvn"] = self.din("w_kvn", [D, 768])
        nd["shcf"] = self.din("shcf", [32, 247]); nd["efull"] = self.din("efull", [32, S_LEN]); nd["ov"] = self.din("ov", [127, 32])
        nd["abf"] = self.din("abf", [128, 2, 64]); nd["selg"] = self.din("selg", [24, 12, 128])
        nd["t31"] = self.din("t31", [128, 2, 512])
        nd["bmg"] = [self.din("bmg%d" % k, [128, 2, 512]) for k in range(3)]
        nd["msk"] = [self.din("msk%d" % k, [128, 128]) for k in range(3)]
        nd["bvcg"] = self.din("bvcg", [32, 2, 512]); nd["mskc"] = self.din("mskc", [32, 128])
        nd["cmp_w1"] = [self.din("cmp_k_w1", [2048, 256]), self.din("cmp_v_w1", [2048, 256])]
        nd["cmp_w2"] = [self.din("cmp_k_w2", [256, 64]), self.din("cmp_v_w2", [256, 64])]
        nd["cmp_peT"] = [self.din("cmp_pe_kT", [64, 32]), self.din("cmp_pe_vT", [64, 32])]
        outT = self.dout("outT", [D, S_LEN])
        with ExitStack() as st:
            self.st = st
            S = self.S = Sched(nc, st)
            self.X = self.sb("X", [128, NCH, S_LEN], F32)
            self.XB = [[Buf() for _ in range(NTC)] for _ in range(NCH)]
            self.cols = self.sb("cols", [128, NCOLS], F32)
            self.ones_f = self.sb("ones_f", [128, 128], F32)
            self.ones_b = self.sb("ones_b", [128, 128], BF16)
            self.eps_t = self.sb("eps_t", [128, 1], F32)
            self.gneps_t = self.sb("gneps_t", [128, 1], F32)
            self.ident_f = self.sb("ident_f", [128, 128], F32)
            self.constb = Buf("const")
            self.PS = self.ps("PSALL", [128, 8, 512])
            self.banks = [self.PS[:, i, :] for i in range(8)]
            self.bankb = [Buf() for _ in range(8)]
            self.psum = Ring(self.banks, self.bankb)
            S.dma(self.cols[:], cols_d, writes=[self.constb])
            S.op("dve", lambda e: e.memset(self.ones_f[:], 1.0), reads=[self.constb], writes=[self.constb])
            S.op("dve", lambda e: e.memset(self.ones_b[:], 1.0), reads=[self.constb], writes=[self.constb])
            S.op("dve", lambda e: e.memset(self.eps_t[:], EPS), reads=[self.constb], writes=[self.constb])
            S.op("dve", lambda e: e.memset(self.gneps_t[:], 64e-5), reads=[self.constb], writes=[self.constb])
            S.dma(self.ident_f[:], ident_d, reads=[self.constb], writes=[self.constb])
            for c in range(NCH):
                for tc in range(NTC):
                    ts = slice(tc * TC, (tc + 1) * TC)
                    S.dma(self.X[:, c, ts], xT[c * 128:(c + 1) * 128, ts], writes=[self.XB[c][tc]])

            def ffn_phase(wg, wu, wd, gname):
                with ExitStack() as st2:
                    self.st = st2
                    self.HN = self.sb("HN", [128, NCH, S_LEN], BF16)
                    self.HNB = [[Buf() for _ in range(NTC)] for _ in range(NCH)]
                    self.WG = [self.sb("WG%d" % i, [128, NCH, 512], BF16) for i in range(2)]
                    self.WU = [self.sb("WU%d" % i, [128, NCH, 512], BF16) for i in range(2)]
                    self.WD = [self.sb("WD%d" % i, [128, 4, D], BF16) for i in range(2)]
                    self.WGB = [Buf() for _ in range(2)]; self.WUB = [Buf() for _ in range(2)]; self.WDB = [Buf() for _ in range(2)]
                    self.a_ring = Ring([self.sb("a%d" % i, [128, 4, TC], BF16) for i in range(2)])
                    self.sg_ring = Ring([self.sb("sg%d" % i, [128, TC], F32) for i in range(2)])
                    self.ffn(wg, wu, wd, gname)
                    S.full_barrier()
                    self.st = st

            if "noffn1" not in dbg:
                ffn_phase(f1g, f1u, f1d, "ffn1_norm")
            if "x1" in dbg:
                self.dump_x("dbg_x1")
            if stop_after != "ffn1":
                with ExitStack() as st3:
                    self.st = st3
                    self.HN = self.sb("HN", [128, NCH, S_LEN], BF16)
                    self.HNB = [[Buf() for _ in range(NTC)] for _ in range(NCH)]
                    Yt = self.sb("Yt", [128, 4, S_LEN], BF16)
                    YBt = [Buf() for _ in range(NTC)]
                    self.Y = [Yt, Yt, Yt]
                    self.YB = [YBt, YBt, YBt]
                    if "norwkv" in dbg or "rwkvseq" in dbg:
                        self.rmsnorm_to_hn("mix_norm")
                    if "norwkv" not in dbg:
                        if "rwkvseq" in dbg:
                            self.rwkv_branch_seq(w_rwkv, w2_d, a2_d, g2_d, gng_d, gnb_d)
                        else:
                            self.rwkv_branch(w_rwkv, w2_d, a2_d, g2_d, gng_d, gnb_d, rmk_d)
                    else:
                        S.op("pool", lambda e: e.memset(Yt[:], 0.0), writes=YBt)
                    if "y_rwkv" in dbg:
                        self.dump_feat("dbg_y_rwkv", Yt, 4, YBt)
                    self.M = self.sb("M", [128, NCH, S_LEN], BF16)
                    self.MB = [[Buf() for _ in range(NTC)] for _ in range(NCH)]
                    do_merge = stop_after != "mix"
                    if do_merge:
                        self.fold(0, w_gb, w_br[0], True)
                    if "nomem" not in dbg:
                        self.mem_branch(memT, mem_wk, mem_wv, w_qm)
                    else:
                        S.op("pool", lambda e: e.memset(Yt[:], 0.0), writes=YBt)
                    if "y_mem" in dbg:
                        self.dump_feat("dbg_y_mem", Yt, 4, YBt)
                    if do_merge:
                        self.fold(2, w_gb, w_br[2], False)
                    if "nonsa" not in dbg:
                        self.nsa_branch(nd)
                    else:
                        S.op("pool", lambda e: e.memset(Yt[:], 0.0), writes=YBt)
                    if "y_nsa" in dbg:
                        self.dump_feat("dbg_y_nsa_p", Yt, 4, YBt)
                    if do_merge:
                        self.fold(1, w_gb, w_br[1], False)
                        self.outproj(w_out)
                    S.full_barrier()
                    self.st = st
                if "x2" in dbg:
                    self.dump_x("dbg_x2")
                if stop_after not in ("mix", "merge"):
                    ffn_phase(f2g, f2u, f2d, "ffn2_norm")
            self.final_norm_out(outT)
            S.wait_all_dma("sp")
            S.wait_all_dma("pool")
        return nc


NSA_PERM = np.concatenate([np.concatenate([np.arange(64 * j, 64 * j + 64), np.arange(64 * (4 + j), 64 * (4 + j) + 64)])
                           for j in range(4)])


def _t5_bucket_np(dist):
    n = np.maximum(dist, 0)
    nf = np.maximum(n, 1).astype(np.float32)
    large = 16 + (np.log(nf / np.float32(16)) / np.float32(math.log(128 / 16)) * np.float32(16)).astype(np.int32)
    large = np.minimum(large, 31)
    return np.where(n < 16, n, large)


def _nsa_consts(rel_bias):
    rb = np.asarray(rel_bias, np.float32)
    c = np.arange(128)[:, None]; p = np.arange(128)[None, :]
    out = {}
    hd = np.arange(8).reshape(2, 4)
    dists = [p - c, 128 + p - c, 512 + p - c]
    valid = [p >= c, np.ones((128, 128), bool), c > p]
    for k in range(3):
        bk = _t5_bucket_np(dists[k])
        g = rb[bk[:, None, None, :], hd[None, :, :, None]]
        out["bmg%d" % k] = np.ascontiguousarray(g.reshape(128, 2, 512))
        out["msk%d" % k] = np.where(valid[k], 0.0, -30000.0).astype(np.float32)
    out["t31"] = np.ascontiguousarray(np.broadcast_to(rb[31][hd][None, :, :, None], (128, 2, 4, 128)).reshape(128, 2, 512))
    m = np.arange(32)[:, None]
    dc = p - 16 * (m - 8) - 31
    bk = _t5_bucket_np(dc)
    g = rb[bk[:, None, None, :], hd[None, :, :, None]]
    out["bvcg"] = np.ascontiguousarray(g.reshape(32, 2, 512))
    mk = np.where((dc >= 0) & (m < 16), 0.0, -30000.0).astype(np.float32)
    mk[17:] = 0.0
    out["mskc"] = mk
    shcf = np.zeros((32, 247), np.float32)
    for x in range(247):
        r = x - 112
        if 0 <= r < 16:
            shcf[r, x] = 1.0
        elif r >= 16:
            shcf[16, x] = 1.0
    out["shcf"] = shcf
    ef = np.zeros((32, S_LEN), np.float32)
    ef[np.arange(S_LEN) // 64, np.arange(S_LEN)] = 1.0
    out["efull"] = ef
    ic = np.arange(127)[:, None]; jb = np.arange(32)[None, :]
    out["ov"] = ((ic * 16 <= jb * 64 + 63) & (ic * 16 + 31 >= jb * 64)).astype(np.float32)
    ab = np.zeros((128, 2, 64), np.float32)
    for pp in range(128):
        curr = 1 if pp >= 64 else 0
        for mm in range(64):
            jr = mm - 32
            if jr <= curr - 2:
                ab[pp, 0, mm] = 1.0
            if jr in (curr, curr - 1):
                ab[pp, 1, mm] = 1e6
    out["abf"] = ab
    selg = np.zeros((24, 12, 128), np.float32)
    for br in range(3):
        for j in range(4):
            for mm in range(128):
                selg[br * 8 + (mm // 64) * 4 + j, br * 4 + j, mm] = 1.0
    out["selg"] = selg
    return out


def prep_inputs(inputs, b):
    m = {}
    m["xT"] = np.ascontiguousarray(inputs["x"][b].T)
    cols = np.zeros((128, NCOLS), np.float32)
    for n in ("ffn1_norm", "mix_norm", "ffn2_norm", "final_norm", "mem_norm"):
        c0, k = COLS[n]
        cols[:, c0:c0 + k] = _colpack(np.asarray(inputs[n]).reshape(-1))
    for n, src in (("mu", "rwkv_mu"), ("w0", "rwkv_w0"), ("a0", "rwkv_a0"), ("k_k", "rwkv_k_k"), ("k_a", "rwkv_k_a"), ("r_k", "rwkv_r_k"),
                   ("gn_g", "rwkv_gn_gain"), ("gn_b", "rwkv_gn_bias")):
        c0, k = COLS[n]
        cols[:, c0:c0 + k] = _colpack(np.asarray(inputs[src]).reshape(-1))
    m["cols"] = cols
    m["w_rwkv"] = np.ascontiguousarray(np.asarray(inputs["w_in"])[0][:, 0:1792])
    m["rwkv_w2"] = np.ascontiguousarray(np.asarray(inputs["rwkv_w2"])[0])
    m["rwkv_a2"] = np.ascontiguousarray(np.asarray(inputs["rwkv_a2"])[0])
    m["rwkv_g2"] = np.ascontiguousarray(np.asarray(inputs["rwkv_g2"])[0])
    m["gng_rep"] = np.ascontiguousarray(np.broadcast_to(np.asarray(inputs["rwkv_gn_gain"]).reshape(1, 512), (128, 512)))
    m["gnb_rep"] = np.ascontiguousarray(np.broadcast_to(np.asarray(inputs["rwkv_gn_bias"]).reshape(1, 512), (128, 512)))
    m["ident"] = np.eye(128, dtype=np.float32)
    si = np.arange(128)[:, None]; ti = np.arange(128)[None, :]
    same = (si // 64) == (ti // 64)
    mk = np.zeros((128, 3, 128), np.float32)
    mk[:, 0, :] = np.where(same & (si < ti), -1.0, 0.0)
    mk[:, 1, :] = np.where(same & (ti < si), -1.0, 0.0)
    mk[:, 2, :] = np.where(same & (si <= ti), 1.0, 0.0)
    m["rwkv_masks"] = mk
    w_in_ = np.asarray(inputs["w_in"])[0]
    m["w_qn"] = np.ascontiguousarray(w_in_[:, 1792:2304][:, NSA_PERM])
    m["w_kvn"] = np.ascontiguousarray(w_in_[:, 2304:3072])
    m["w_gn"] = np.ascontiguousarray(w_in_[:, 3072:3096])
    m.update(_nsa_consts(inputs["rel_bias"]))
    for n in ("cmp_k_w1", "cmp_v_w1", "cmp_k_w2", "cmp_v_w2"):
        m[n] = np.ascontiguousarray(np.asarray(inputs[n])[0])
    m["cmp_pe_kT"] = np.ascontiguousarray(np.asarray(inputs["cmp_pe_k"])[0].T)
    m["cmp_pe_vT"] = np.ascontiguousarray(np.asarray(inputs["cmp_pe_v"])[0].T)
    for n in ("ffn1_w_gate", "ffn1_w_up", "ffn1_w_down", "ffn2_w_gate", "ffn2_w_up", "ffn2_w_down",
              "mem_w_k", "mem_w_v", "w_br_rwkv", "w_br_mem", "w_out"):
        m[n] = np.ascontiguousarray(np.asarray(inputs[n])[0])
    m["memT"] = np.ascontiguousarray(inputs["mem"][b].T)
    w_in = np.asarray(inputs["w_in"])[0]
    m["w_qm"] = np.ascontiguousarray(w_in[:, 3096:3608])
    m["w_gb"] = np.ascontiguousarray(w_in[:, 3608:6680])
    m["w_br_nsa_p"] = np.ascontiguousarray(np.asarray(inputs["w_br_nsa"])[0][NSA_PERM, :])
    return m


_CACHE = {}


def kernel(**inputs):
    inputs = {k: np.asarray(v) for k, v in inputs.items()}
    if "nc" not in _CACHE:
        _CACHE["nc"] = Builder().build()
    nc = _CACHE["nc"]
    n = 8
    in_maps = [prep_inputs(inputs, b) for b in range(n)]
    res = run_bass_kernel_spmd(nc, in_maps, core_ids=list(range(n)))
    out = np.stack([np.ascontiguousarray(r["outT"].T) for r in res.results], axis=0)
    return out.astype(np.float32)
```

```python
import math
from contextlib import ExitStack
import numpy as np
import concourse.bass as bass
import concourse.mybir as mybir
from concourse.bass_utils import run_bass_kernel_spmd

F32 = mybir.dt.float32
BF16 = mybir.dt.bfloat16
AF = mybir.ActivationFunctionType
ALU = mybir.AluOpType
AX = mybir.AxisListType

D = 1024
S_LEN = 2048
DFF = 2816
NCH = 8
TC = 512
NTC = S_LEN // TC
EPS = 1e-6


class Buf:
    __slots__ = ("name", "last_w", "readers")

    def __init__(self, name=""):
        self.name = name
        self.last_w = None
        self.readers = []


class Sched:
    ENG = ("pe", "act", "dve", "pool", "sp")

    def __init__(self, nc, stack, n_dma_sems=16):
        self.nc = nc
        self.eng = {"pe": nc.tensor, "act": nc.scalar, "dve": nc.vector,
                    "pool": nc.gpsimd, "sp": nc.sync}
        self.sem = {}
        for e in ("pe", "act", "dve", "pool"):
            self.sem[e] = stack.enter_context(nc.semaphore("s_" + e))
        self.cnt = {e: 0 for e in ("pe", "act", "dve", "pool")}
        nq = {"sp": 28, "pool": 28, "act": 8}
        self.dsem = []
        self.qsems = {}
        for q, n in nq.items():
            self.qsems[q] = list(range(len(self.dsem), len(self.dsem) + n))
            for i in range(n):
                self.dsem.append(stack.enter_context(nc.semaphore("d%s%d" % (q, i))))
        self.dcnt = [0] * len(self.dsem)
        self.dnext = {q: 0 for q in nq}
        self.waited = {e: {} for e in self.ENG}
        self.n_ops = 0
        self.n_waits = 0

    def _semobj(self, key):
        return self.sem[key] if isinstance(key, str) else self.dsem[key]

    def _need(self, engine, toks):
        best = {}
        for t in toks:
            if t is None:
                continue
            key, val = t
            if best.get(key, 0) < val:
                best[key] = val
        w = self.waited[engine]
        for key, val in best.items():
            if w.get(key, 0) >= val:
                continue
            self.eng[engine].wait_ge(self._semobj(key), val)
            w[key] = val
            self.n_waits += 1

    @staticmethod
    def _deps(reads, writes):
        toks = []
        for b in reads:
            toks.append(b.last_w)
        for b in writes:
            toks.append(b.last_w)
            toks.extend(b.readers)
        return toks

    @staticmethod
    def _commit(tok, reads, writes):
        for b in reads:
            b.readers.append(tok)
            if len(b.readers) > 48:
                best = {}
                for k, v in b.readers:
                    if best.get(k, 0) < v:
                        best[k] = v
                b.readers = list(best.items())
        for b in writes:
            b.last_w = tok
            b.readers = []

    def op(self, engine, fn, reads=(), writes=()):
        self._need(engine, self._deps(reads, writes))
        ins = fn(self.eng[engine])
        self.cnt[engine] += 1
        ins.then_inc(self.sem[engine], 1)
        tok = (engine, self.cnt[engine])
        self._commit(tok, reads, writes)
        self.n_ops += 1
        return tok

    def dma(self, out_ap, in_ap, reads=(), writes=(), queue="sp", **kw):
        pool = self.qsems[queue]
        i = pool[self.dnext[queue]]
        self.dnext[queue] = (self.dnext[queue] + 1) % len(pool)
        prev = [(i, self.dcnt[i])] if self.dcnt[i] else []
        self._need(queue, self._deps(reads, writes) + prev)
        ins = self.eng[queue].dma_start(out=out_ap, in_=in_ap, **kw)
        self.dcnt[i] += 16
        ins.then_inc(self.dsem[i], 16)
        tok = (i, self.dcnt[i])
        self._commit(tok, reads, writes)
        self.n_ops += 1
        return tok

    def barrier(self, bufs):
        toks = []
        for b in bufs:
            toks.append(b.last_w)
            toks.extend(b.readers)
        for e in self.ENG:
            self._need(e, toks)

    def full_barrier(self):
        toks = [(e, self.cnt[e]) for e in ("pe", "act", "dve", "pool") if self.cnt[e]]
        toks += [(i, self.dcnt[i]) for i in range(len(self.dsem)) if self.dcnt[i]]
        for e in self.ENG:
            self._need(e, toks)

    def wait_all_dma(self, engine="sp"):
        for i in range(len(self.dsem)):
            if self.dcnt[i]:
                self.eng[engine].wait_ge(self.dsem[i], self.dcnt[i])


class Ring:
    def __init__(self, tiles, bufs=None):
        self.tiles = tiles
        self.bufs = bufs if bufs is not None else [Buf() for _ in tiles]
        self.i = 0

    def get(self):
        t, b = self.tiles[self.i], self.bufs[self.i]
        self.i = (self.i + 1) % len(self.tiles)
        return t, b

    def get_pair_idx(self):
        if self.i % 2:
            self.i = (self.i + 1) % len(self.tiles)
        k = self.i
        self.i = (self.i + 2) % len(self.tiles)
        return k, self.bufs[k], self.bufs[k + 1]


COLS = {}
_c = 0
for _n, _k in (("ffn1_norm", 8), ("mix_norm", 8), ("ffn2_norm", 8), ("final_norm", 8),
               ("mem_norm", 8), ("mu", 14), ("w0", 4), ("a0", 4), ("k_k", 4), ("k_a", 4), ("r_k", 4), ("gn_g", 4), ("gn_b", 4)):
    COLS[_n] = (_c, _k)
    _c += _k
NCOLS = _c


def _colpack(v):
    v = np.asarray(v, np.float32).reshape(-1, 128)
    return np.ascontiguousarray(v.T)


class Builder:
    def __init__(self, debug=()):
        self.debug = set(debug)
        self._rk_stage = 99
        self._rk_tiles = S_LEN // 128
        for d_ in self.debug:
            if d_.startswith("rkstage"):
                self._rk_stage = int(d_[7:])
            if d_.startswith("rktiles"):
                self._rk_tiles = int(d_[7:])
        self.nc = bass.Bass("TRN2", target_bir_lowering=False)
        self.dram_in = {}
        self.dram_out = {}

    def din(self, name, shape, dt=F32):
        t = self.nc.dram_tensor(name, list(shape), dt, kind="ExternalInput").ap()
        self.dram_in[name] = t
        return t

    def dout(self, name, shape, dt=F32):
        t = self.nc.dram_tensor(name, list(shape), dt, kind="ExternalOutput").ap()
        self.dram_out[name] = t
        return t

    def sb(self, name, shape, dt):
        self._uid = getattr(self, "_uid", 0) + 1
        return self.st.enter_context(self.nc.sbuf_tensor("sb%d_%s" % (self._uid, name), list(shape), dt))

    def ps(self, name, shape, dt=F32):
        return self.st.enter_context(self.nc.psum_tensor("ps_" + name, list(shape), dt))

    def _norm_rings_open(self):
        self._nst_old = self.st
        self._nst = ExitStack()
        self.st = self._nst
        self.sq_ring = Ring([self.sb("sq%d" % i, [128, TC], F32) for i in range(2)])
        self.rstd_ring = Ring([self.sb("RSTD%d" % i, [128, TC], F32) for i in range(2)])
        self.st = self._nst_old

    def _norm_rings_close(self):
        self.S.full_barrier()
        self._nst.close()

    def rmsnorm_to_hn(self, gname):
        S = self.S
        g0, _ = COLS[gname]
        self._norm_rings_open()
        for tc in range(NTC):
            ts = slice(tc * TC, (tc + 1) * TC)
            pt, pb = self.psum.get()
            for c in range(NCH):
                sq, sqb = self.sq_ring.get()
                S.op("act", lambda e: e.activation(sq[:], self.X[:, c, ts], AF.Square),
                     reads=[self.XB[c][tc]], writes=[sqb])
                S.op("pe", lambda e: e.matmul(pt[:], self.ones_f[:], sq[:], start=(c == 0), stop=(c == NCH - 1)),
                     reads=[sqb, self.constb], writes=[pb])
            rs, rsb = self.rstd_ring.get()
            S.op("act", lambda e: e.activation(rs[:], pt[:], AF.Sqrt, bias=self.eps_t[:], scale=1.0 / D),
                 reads=[pb, self.constb], writes=[rsb])
            S.op("dve", lambda e: e.reciprocal(rs[:], rs[:]), reads=[rsb], writes=[rsb])
            for c in range(NCH):
                S.op("dve", lambda e: e.scalar_tensor_tensor(
                    self.HN[:, c, ts], self.X[:, c, ts], self.cols[:, g0 + c:g0 + c + 1], rs[:],
                    ALU.mult, ALU.mult),
                    reads=[self.XB[c][tc], rsb, self.constb], writes=[self.HNB[c][tc]])

        self._norm_rings_close()

    def ffn(self, wg, wu, wd, gname):
        S = self.S
        groups = [(i, min(4, 22 - i)) for i in range(0, 22, 4)]

        def load(gi):
            f0, nf = groups[gi]
            slot = gi % 2
            S.dma(self.WG[slot][:, :, 0:nf * 128],
                  wg[:, f0 * 128:(f0 + nf) * 128].rearrange("(k p) n -> p k n", p=128),
                  writes=[self.WGB[slot]], queue="pool")
            S.dma(self.WU[slot][:, :, 0:nf * 128],
                  wu[:, f0 * 128:(f0 + nf) * 128].rearrange("(k p) n -> p k n", p=128),
                  writes=[self.WUB[slot]], queue="pool")
            S.dma(self.WD[slot][:, 0:nf, :],
                  wd[f0 * 128:(f0 + nf) * 128, :].rearrange("(f p) n -> p f n", p=128),
                  writes=[self.WDB[slot]], queue="pool")

        load(0)
        load(1)
        self.rmsnorm_to_hn(gname)
        for gi, (f0, nf) in enumerate(groups):
            if gi >= 1 and gi + 1 < len(groups):
                load(gi + 1)
            slot = gi % 2
            WG, WU, WD = self.WG[slot], self.WU[slot], self.WD[slot]
            for tc in range(NTC):
                ts = slice(tc * TC, (tc + 1) * TC)
                hreads = [self.HNB[c][tc] for c in range(NCH)]
                a_t, a_b = self.a_ring.get()
                for f in range(nf):
                    pg, pgb = self.psum.get()
                    pu, pub = self.psum.get()

                    def mm_g(e):
                        for k in range(NCH):
                            ins = e.matmul(pg[:], WG[:, k, f * 128:(f + 1) * 128], self.HN[:, k, ts],
                                           start=(k == 0), stop=(k == NCH - 1))
                        return ins

                    def mm_u(e):
                        for k in range(NCH):
                            ins = e.matmul(pu[:], WU[:, k, f * 128:(f + 1) * 128], self.HN[:, k, ts],
                                           start=(k == 0), stop=(k == NCH - 1))
                        return ins
                    S.op("pe", mm_g, reads=hreads + [self.WGB[slot]], writes=[pgb])
                    S.op("pe", mm_u, reads=hreads + [self.WUB[slot]], writes=[pub])
                    sg, sgb = self.sg_ring.get()
                    S.op("act", lambda e: e.activation(sg[:], pg[:], AF.Silu), reads=[pgb], writes=[sgb])
                    S.op("dve", lambda e: e.tensor_tensor(a_t[:, f, :], sg[:], pu[:], ALU.mult),
                         reads=[sgb, pub], writes=[a_b])
                for dc in range(NCH):
                    po, pob = self.psum.get()

                    def mm_d(e):
                        for f in range(nf):
                            ins = e.matmul(po[:], WD[:, f, dc * 128:(dc + 1) * 128], a_t[:, f, :],
                                           start=(f == 0), stop=(f == nf - 1))
                        return ins
                    S.op("pe", mm_d, reads=[a_b, self.WDB[slot]], writes=[pob])
                    S.op("dve", lambda e: e.scalar_tensor_tensor(
                        self.X[:, dc, ts], po[:], 0.5, self.X[:, dc, ts], ALU.mult, ALU.add),
                        reads=[pob, self.XB[dc][tc]], writes=[self.XB[dc][tc]])


    def load_w(self, tile_ap, dram_ap, buf):
        self.S.dma(tile_ap, dram_ap.rearrange("(k p) n -> p k n", p=128), writes=[buf], queue="pool")

    def dump_feat(self, name, tile, nchunks, buf_list):
        o = self.dout(name, [nchunks * 128, S_LEN])
        for c in range(nchunks):
            self.S.dma(o[c * 128:(c + 1) * 128, :], tile[:, c, :], reads=buf_list, queue="pool")


    def rwkv_branch_seq(self, w_rwkv, w2_d, a2_d, g2_d, gng_d, gnb_d):
        S = self.S
        CN = COLS
        NT = S_LEN // 128
        with ExitStack() as st4:
            old, self.st = self.st, st4
            WR = self.sb("WR", [128, NCH, 1792], BF16); WRB = Buf()
            W2 = self.sb("W2A2", [128, 512], F32); A2 = W2; G2 = self.sb("G2", [128, 512], F32)
            GNG = self.sb("GNG", [128, 512], F32); GNB = self.sb("GNB", [128, 512], F32)
            BO = self.sb("BO", [128, 128], F32); BOb = self.sb("BOb", [128, 128], BF16)
            ID2 = self.sb("ID2", [128, 64], BF16)
            OMK = self.sb("OMK", [128, 4], F32)
            cb = Buf()
            self.load_w(WR[:], w_rwkv, WRB)
            S.dma(W2[0:64, :], w2_d, writes=[cb]); S.dma(A2[64:128, :], a2_d, writes=[cb]); S.dma(G2[:], g2_d, writes=[cb])
            S.dma(GNG[:], gng_d, writes=[cb]); S.dma(GNB[:], gnb_d, writes=[cb])
            S.op("dve", lambda e: e.memset(BO[:], 0.0), reads=[cb], writes=[cb])
            S.op("dve", lambda e: e.memset(BO[0:64, 0:64], 1.0), reads=[cb], writes=[cb])
            S.op("dve", lambda e: e.memset(BO[64:128, 64:128], 1.0), reads=[cb], writes=[cb])
            S.op("dve", lambda e: e.tensor_copy(BOb[:], BO[:]), reads=[cb], writes=[cb])
            S.op("dve", lambda e: e.tensor_copy(ID2[0:64, :], self.ident_f[0:64, 0:64]), reads=[cb, self.constb], writes=[cb])
            S.op("dve", lambda e: e.tensor_copy(ID2[64:128, :], self.ident_f[64:128, 64:128]), reads=[cb, self.constb], writes=[cb])
            ka0 = CN["k_a"][0]
            S.op("dve", lambda e: e.tensor_scalar(OMK[:], self.cols[:, ka0:ka0 + 4], -1.0, 1.0, ALU.mult, ALU.add),
                 reads=[cb, self.constb], writes=[cb])
            P32 = self.sb("P32", [128, 14, 129], F32); P32B = Buf()
            DD = self.sb("DD", [128, 128], F32); DDB = Buf()
            CAR = self.sb("CAR", [128, 14, 1], F32)
            PL = P32[:, :, 1:129]; PLB = P32B
            TW = self.sb("TW", [64, 128], F32); SGg = self.sb("SGg", [128, 128], F32)
            WD = self.sb("WD", [128, 4, 128], F32); SIG = WD
            A32 = self.sb("A32", [128, 4, 128], F32)
            KK = self.sb("KK", [128, 4, 128], F32); SQ = self.sb("SQ", [128, 4, 128], F32)
            KKN = self.sb("KKN", [128, 4, 128], F32); NB = self.sb("NB", [128, 4, 128], F32)
            KM = self.sb("KM", [128, 4, 128], F32); BON = self.sb("BON", [128, 4, 128], F32)
            RM = self.sb("RM", [128, 4, 128, 2], BF16)
            VDr = Ring([self.sb("VD%d" % i, [128, 4, 64], BF16) for i in range(2)])
            H = self.sb("H", [128, 4, 64], F32); Hb = self.sb("Hb", [128, 4, 64], BF16); HK = self.sb("HK", [128, 4, 64], BF16)
            T1 = self.sb("T1", [128, 4, 64], F32); T2r = Ring([self.sb("T2_%d" % i, [128, 4, 64], F32) for i in range(2)])
            YST = [self.sb("YST%d" % i, [2, 4, 256], F32) for i in range(2)]; YSTB = [Buf(), Buf()]
            YTOK = A32[:].rearrange("p c t -> p (c t)").rearrange("p (c h v) -> p c h v", c=4, h=2); YTOKB = Buf()
            YC = KKN[:].rearrange("p c t -> p (c t)").rearrange("p (a v) -> p a v", a=8)
            ST8 = self.sb("ST8", [128, 8], F32); ST8b = self.sb("ST8b", [128, 8], F32)
            YF = SQ
            db = Buf(); hb = Buf(); hbb = Buf(); hkb = Buf(); t1b = Buf(); vrb = Buf(); vtb = Buf(); rmb = Buf(); yb = Buf()
            S.op("pool", lambda e: e.memset(P32[:], 0.0), writes=[P32B])
            S.op("pool", lambda e: e.memset(RM[:], 0.0), writes=[rmb])
            S.op("pool", lambda e: e.memset(H[:], 0.0), writes=[hb])
            mu0 = CN["mu"][0]; w00 = CN["w0"][0]; a00 = CN["a0"][0]; kk0 = CN["k_k"][0]; rk0 = CN["r_k"][0]
            ident = self.ident_f
            for i in range(NT):
                t0 = i * 128
                tcix = t0 // TC
                tsl = slice(t0, t0 + 128)
                hreads = [self.HNB[c][tcix] for c in range(NCH)]
                for cg in range(4):
                    c0 = cg * 4
                    n = min(4, 14 - c0)
                    p, pb = self.psum.get()

                    def mm(e):
                        for cc in range(n):
                            for k in range(NCH):
                                ins = e.matmul(p[:, cc * 128:(cc + 1) * 128], WR[:, k, (c0 + cc) * 128:(c0 + cc + 1) * 128],
                                               self.HN[:, k, tsl], start=(k == 0), stop=(k == NCH - 1))
                        return ins
                    S.op("pe", mm, reads=hreads + [WRB], writes=[pb])
                    S.op("act", lambda e: e.copy(P32[:, c0:c0 + n, 1:129], p[:, 0:n * 128].rearrange("p (c t) -> p c t", c=n)),
                         reads=[pb], writes=[P32B])
                S.op("dve", lambda e: e.tensor_copy(CAR[:], P32[:, :, 128:129]), reads=[P32B], writes=[DDB])
                for c in range(14):
                    S.op("dve", lambda e: e.tensor_tensor(DD[:], P32[:, c, 0:128], P32[:, c, 1:129], ALU.subtract), reads=[P32B, DDB], writes=[DDB])
                    S.op("dve", lambda e: e.scalar_tensor_tensor(P32[:, c, 1:129], DD[:], self.cols[:, mu0 + c:mu0 + c + 1], P32[:, c, 1:129],
                                                                 ALU.mult, ALU.add), reads=[DDB, P32B, self.constb], writes=[P32B])
                S.op("dve", lambda e: e.tensor_copy(P32[:, :, 0:1], CAR[:]), reads=[P32B, DDB], writes=[P32B])
                S.op("act", lambda e: e.activation(TW[:], PL[0:64, 12, :], AF.Tanh), reads=[PLB], writes=[db])
                S.op("act", lambda e: e.activation(SGg[:], PL[:, 13, :], AF.Sigmoid), reads=[PLB], writes=[db])
                pz, pzb = self.psum.get(); pa, pab = self.psum.get()

                def mmz(e):
                    for fc in range(4):
                        ins = e.matmul(pz[:, fc * 128:(fc + 1) * 128], W2[0:64, fc * 128:(fc + 1) * 128], TW[:], start=True, stop=True)
                    return ins

                def mma(e):
                    for fc in range(4):
                        ins = e.matmul(pa[:, fc * 128:(fc + 1) * 128], A2[64:128, fc * 128:(fc + 1) * 128], PL[64:128, 12, :], start=True, stop=True)
                    return ins

                S.op("pe", mmz, reads=[db, cb], writes=[pzb])
                S.op("pe", mma, reads=[PLB, cb], writes=[pab])
                for fc in range(4):
                    S.op("act", lambda e: e.activation(SIG[:, fc, :], pz[:, fc * 128:(fc + 1) * 128], AF.Sigmoid,
                                                       bias=self.cols[:, w00 + fc:w00 + fc + 1]), reads=[pzb, self.constb], writes=[db])
                    S.op("act", lambda e: e.activation(A32[:, fc, :], pa[:, fc * 128:(fc + 1) * 128], AF.Sigmoid,
                                                       bias=self.cols[:, a00 + fc:a00 + fc + 1]), reads=[pab, self.constb], writes=[db, YTOKB])
                S.op("act", lambda e: e.activation(WD[:], SIG[:], AF.Exp, scale=-0.6065306597126334), reads=[db], writes=[db])
                for fc in range(4):
                    S.op("dve", lambda e: e.tensor_scalar(KK[:, fc, :], PL[:, 4 + fc, :], self.cols[:, kk0 + fc:kk0 + fc + 1], None, ALU.mult),
                         reads=[PLB, self.constb], writes=[db])
                S.op("dve", lambda e: e.tensor_tensor(SQ[:], KK[:], KK[:], ALU.mult), reads=[db], writes=[db])
                pss, pssb = self.psum.get()
                S.op("pe", lambda e: e.matmul(pss[:], BO[:], SQ[:].rearrange("p c t -> p (c t)"), start=True, stop=True), reads=[db, cb], writes=[pssb])
                S.op("act", lambda e: e.activation(SQ[:], pss[:].rearrange("p (c t) -> p c t", c=4), AF.Sqrt), reads=[pssb, db], writes=[db])
                S.op("dve", lambda e: e.tensor_scalar(SQ[:], SQ[:], 1e-12, None, ALU.max), reads=[db], writes=[db])
                S.op("dve", lambda e: e.reciprocal(SQ[:], SQ[:]), reads=[db], writes=[db])
                S.op("dve", lambda e: e.tensor_tensor(KKN[:], KK[:], SQ[:], ALU.mult), reads=[db], writes=[db, yb])
                S.op("dve", lambda e: e.scalar_tensor_tensor(NB[:], KKN[:], -1.0, A32[:], ALU.mult, ALU.mult), reads=[db], writes=[db])
                for fc in range(4):
                    S.op("dve", lambda e: e.tensor_scalar(KK[:, fc, :], A32[:, fc, :], self.cols[:, ka0 + fc:ka0 + fc + 1], OMK[:, fc:fc + 1],
                                                          ALU.mult, ALU.add), reads=[db, cb, self.constb], writes=[db])
                S.op("dve", lambda e: e.tensor_tensor(KM[:], PL[:, 4:8, :], KK[:], ALU.mult), reads=[db, PLB], writes=[db])
                S.op("dve", lambda e: e.tensor_tensor(SQ[:], PL[:, 0:4, :], KM[:], ALU.mult), reads=[db, PLB], writes=[db])
                for fc in range(4):
                    S.op("dve", lambda e: e.tensor_scalar(SQ[:, fc, :], SQ[:, fc, :], self.cols[:, rk0 + fc:rk0 + fc + 1], None, ALU.mult),
                         reads=[db, self.constb], writes=[db])
                pbn, pbnb = self.psum.get()
                S.op("pe", lambda e: e.matmul(pbn[:], BO[:], SQ[:].rearrange("p c t -> p (c t)"), start=True, stop=True), reads=[db, cb], writes=[pbnb])
                S.op("dve", lambda e: e.tensor_tensor(BON[:], pbn[:].rearrange("p (c t) -> p c t", c=4), PL[:, 8:12, :], ALU.mult),
                     reads=[pbnb, PLB], writes=[db])
                S.op("dve", lambda e: e.tensor_copy(RM[0:64, :, :, 0], PL[0:64, 0:4, :]), reads=[PLB, rmb], writes=[rmb])
                S.op("dve", lambda e: e.tensor_copy(RM[64:128, :, :, 1], PL[64:128, 0:4, :]), reads=[PLB, rmb], writes=[rmb])
                for tt in range(128):
                    pvb_t, pvbb = self.psum.get()

                    VD, vdb = VDr.get()
                    S.op("pool", lambda e: e.tensor_tensor(VD[:], ID2[:].unsqueeze(1).to_broadcast([128, 4, 64]),
                                                           PL[:, 8:12, tt:tt + 1].to_broadcast([128, 4, 64]), ALU.mult),
                         reads=[PLB, cb], writes=[vdb])
                    S.op("pe", lambda e: e.matmul(pvb_t[:, 0:256], BOb[:], VD[:].rearrange("p c v -> p (c v)"), start=True, stop=True),
                         reads=[vdb, cb], writes=[pvbb])
                    T2, t2b = T2r.get()
                    S.op("pool" if False else "dve", lambda e: e.tensor_tensor(
                        T2[:], pvb_t[:, 0:256].rearrange("p (c v) -> p c v", c=4), KM[:, :, tt:tt + 1].to_broadcast([128, 4, 64]), ALU.mult),
                        reads=[pvbb, db], writes=[t2b])
                    S.op("dve", lambda e: e.tensor_tensor(HK[:], H[:], KKN[:, :, tt:tt + 1].to_broadcast([128, 4, 64]), ALU.mult),
                         reads=[hb, db], writes=[hkb])
                    psa, psab = self.psum.get()
                    S.op("pe", lambda e: e.matmul(psa[:, 0:256], BOb[:], HK[:].rearrange("p c v -> p (c v)"), start=True, stop=True),
                         reads=[hkb, cb], writes=[psab])
                    S.op("dve", lambda e: e.tensor_tensor(T1[:], psa[:, 0:256].rearrange("p (c v) -> p c v", c=4),
                                                          NB[:, :, tt:tt + 1].to_broadcast([128, 4, 64]), ALU.mult),
                         reads=[psab, db], writes=[t1b])
                    S.op("dve", lambda e: e.tensor_tensor(H[:], H[:], WD[:, :, tt:tt + 1].to_broadcast([128, 4, 64]), ALU.mult),
                         reads=[hb, db], writes=[hb])
                    S.op("dve", lambda e: e.tensor_tensor(T1[:], T1[:], T2[:], ALU.add), reads=[t1b, t2b], writes=[t1b])
                    S.op("dve", lambda e: e.tensor_tensor(H[:], H[:], T1[:], ALU.add), reads=[hb, t1b], writes=[hb])
                    S.op("act", lambda e: e.copy(Hb[:], H[:]), reads=[hb], writes=[hbb])
                    py, pyb = self.psum.get()

                    def mmy(e):
                        for fc in range(4):
                            ins = e.matmul(py[0:2, fc * 64:(fc + 1) * 64], RM[:, fc, tt, :], Hb[:, fc, :], start=True, stop=True)
                        return ins
                    S.op("pe", mmy, reads=[hbb, rmb], writes=[pyb])
                    slot = tt % 2
                    S.op("act", lambda e: e.copy(YST[slot][0:2, 0, :], py[0:2, 0:256]), reads=[pyb], writes=[YSTB[slot]])
                    for hp in range(2):
                        S.dma(YTOK[tt:tt + 1, :, hp, :], YST[slot][hp:hp + 1, 0, :].rearrange("p (c v) -> p c v", c=4),
                              reads=[YSTB[slot], db], writes=[YTOKB])
                YT8 = YTOK.rearrange("t c h v -> t (c h) v")
                S.op("dve", lambda e: e.tensor_reduce(ST8[:], YT8, AX.X, ALU.add), reads=[YTOKB], writes=[yb])
                S.op("dve", lambda e: e.tensor_scalar(ST8[:], ST8[:], 1.0 / 64, None, ALU.mult), reads=[yb], writes=[yb])
                S.op("dve", lambda e: e.tensor_tensor(YC, YT8, ST8[:].unsqueeze(2).to_broadcast([128, 8, 64]), ALU.subtract),
                     reads=[YTOKB, yb], writes=[yb, db])
                S.op("dve", lambda e: e.tensor_tensor(YTOK.rearrange("t c h v -> t (c h) v"), YC, YC, ALU.mult), reads=[yb, YTOKB], writes=[YTOKB])
                S.op("dve", lambda e: e.tensor_reduce(ST8b[:], YT8, AX.X, ALU.add), reads=[YTOKB], writes=[yb])
                S.op("act", lambda e: e.activation(ST8b[:], ST8b[:], AF.Sqrt, bias=self.gneps_t[:], scale=1.0 / 64), reads=[yb, self.constb], writes=[yb])
                S.op("dve", lambda e: e.reciprocal(ST8b[:], ST8b[:]), reads=[yb], writes=[yb])
                S.op("dve", lambda e: e.tensor_tensor(YC, YC, ST8b[:].unsqueeze(2).to_broadcast([128, 8, 64]), ALU.mult), reads=[yb], writes=[yb])
                YCf = YC.rearrange("t a v -> t (a v)")
                S.op("dve", lambda e: e.tensor_tensor(YCf, YCf, GNG[:], ALU.mult), reads=[yb, cb], writes=[yb])
                S.op("dve", lambda e: e.tensor_tensor(YCf, YCf, GNB[:], ALU.add), reads=[yb, cb], writes=[yb])
                pyt, pytb = self.psum.get(); pg, pgb = self.psum.get()

                def mmt2(e):
                    for fc in range(4):
                        ins = e.transpose(pyt[:, fc * 128:(fc + 1) * 128], YC[:, 2 * fc:2 * fc + 2, :].rearrange("t a v -> t (a v)"), ident[:])
                    for fc in range(4):
                        ins = e.matmul(pg[:, fc * 128:(fc + 1) * 128], G2[:, fc * 128:(fc + 1) * 128], SGg[:], start=True, stop=True)
                    return ins
                S.op("pe", mmt2, reads=[yb, self.constb, db, cb], writes=[pytb, pgb])
                S.op("dve", lambda e: e.tensor_tensor(YF[:], pyt[:].rearrange("p (c t) -> p c t", c=4), BON[:], ALU.add), reads=[pytb, db], writes=[yb, db])
                S.op("dve", lambda e: e.tensor_tensor(self.Y[0][:, :, tsl], YF[:], pg[:].rearrange("p (c t) -> p c t", c=4), ALU.mult),
                     reads=[yb, db, pgb], writes=[self.YB[0][tcix]])
            S.full_barrier()
            self.st = old


    def rwkv_branch(self, w_rwkv, w2_d, a2_d, g2_d, gng_d, gnb_d, mk_d):
        S = self.S
        CN = COLS
        NT = S_LEN // 128
        CDEC = 0.6065306597126334
        with ExitStack() as st4:
            old, self.st = self.st, st4
            WR = self.sb("WR", [128, NCH, 1792], BF16); WRB = Buf()
            W2 = self.sb("W2A2", [128, 512], F32); A2 = W2; G2 = self.sb("G2", [128, 512], BF16)
            BO = self.sb("BO", [128, 128], F32)
            ID2 = self.sb("ID2", [128, 64], F32)
            OMK = self.sb("OMK", [128, 4], F32)
            MSK = self.sb("MSK", [128, 3, 128], BF16)
            ONE64 = self.sb("ONE64", [128, 64], F32)
            cb = Buf()
            self.load_w(WR[:], w_rwkv, WRB)
            S.dma(W2[0:64, :], w2_d, writes=[cb]); S.dma(A2[64:128, :], a2_d, writes=[cb]); S.dma(G2[:], g2_d, writes=[cb], queue="pool")
            S.dma(MSK[:], mk_d, writes=[cb], queue="pool")
            self.rmsnorm_to_hn("mix_norm")
            S.op("dve", lambda e: e.memset(BO[:], 0.0), reads=[cb], writes=[cb])
            S.op("dve", lambda e: e.memset(BO[0:64, 0:64], 1.0), reads=[cb], writes=[cb])
            S.op("dve", lambda e: e.memset(BO[64:128, 64:128], 1.0), reads=[cb], writes=[cb])
            S.op("dve", lambda e: e.memset(ONE64[:], 1.0), reads=[cb], writes=[cb])
            S.op("dve", lambda e: e.tensor_copy(ID2[0:64, :], self.ident_f[0:64, 0:64]), reads=[cb, self.constb], writes=[cb])
            S.op("dve", lambda e: e.tensor_copy(ID2[64:128, :], self.ident_f[64:128, 64:128]), reads=[cb, self.constb], writes=[cb])
            ka0 = CN["k_a"][0]
            S.op("dve", lambda e: e.tensor_scalar(OMK[:], self.cols[:, ka0:ka0 + 4], -1.0, 1.0, ALU.mult, ALU.add),
                 reads=[cb, self.constb], writes=[cb])
            P32 = self.sb("P32", [128, 14, 129], F32); P32B = Buf()
            DD = self.sb("DD", [128, 128], F32); DDB = Buf()
            CAR = self.sb("CAR", [128, 14, 1], F32)
            PL = P32[:, :, 1:129]; PLB = P32B
            TW = self.sb("TW", [64, 128], F32); SGg = self.sb("SGg", [128, 128], BF16)
            f32t = lambda n: self.sb(n, [128, 4, 128], F32)
            SIG = f32t("SIG"); CUM = f32t("CUM"); A32 = f32t("A32"); KK = f32t("KK"); SQ = f32t("SQ")
            KKN = f32t("KKN"); NB = f32t("NB"); KM = f32t("KM"); BON = f32t("BON")
            AH = self.sb("AH", [128, 4, 128], BF16); KH = self.sb("KH", [128, 4, 128], BF16)
            BR = self.sb("BR", [128, 4, 2, 128], BF16)
            AT = self.sb("AT", [128, 512], BF16); KTt = self.sb("KTt", [128, 512], BF16); VTOK = self.sb("VTOK", [128, 512], BF16)
            WB = self.sb("WB", [128, 8, 128], BF16); BU = self.sb("BU", [128, 8, 128], BF16)
            bf8 = lambda n: self.sb(n, [128, 8, 128], BF16)
            X0 = bf8("X0"); XT0 = bf8("XT0"); LKT = bf8("LKT"); GRA = bf8("GRA"); GRK = bf8("GRK"); TT = bf8("TT")
            XA1 = [self.sb("XA1_%d" % i, [128, 4, 128], BF16) for i in range(2)]
            XTA1 = [self.sb("XTA1_%d" % i, [128, 4, 128], BF16) for i in range(2)]
            TA1 = [self.sb("TA1_%d" % i, [128, 4, 128], BF16) for i in range(2)]
            RTm = self.sb("RTm", [128, 4, 2, 128], BF16)
            M0Ts = SIG[:].rearrange("p c t -> p (c t)").rearrange("p (a k) -> p a k", a=8)
            N0s = CUM[:].rearrange("p c t -> p (c t)").rearrange("p (a k) -> p a k", a=8)
            PCt = self.sb("PCt", [128, 2, 4], F32)
            H = self.sb("H", [128, 4, 64], F32); Hb = self.sb("Hb", [128, 2, 4, 64], BF16)
            nbb = Buf(); kmb = Buf(); sgb = Buf(); cub = Buf()
            YTOK = NB[:].rearrange("p c t -> p (c t)").rearrange("p (c h v) -> p c h v", c=4, h=2); YTOKB = nbb
            YC = KM[:].rearrange("p c t -> p (c t)").rearrange("p (a v) -> p a v", a=8)
            ST8 = self.sb("ST8", [128, 8], F32); ST8b = self.sb("ST8b", [128, 8], F32)
            YF = SQ
            db = Buf(); hb = Buf(); hbb = Buf(); gb_ = Buf(); chb = Buf(); tkb = Buf(); yb = Buf(); mnb = Buf(); rtb = Buf()
            S.op("pool", lambda e: e.memset(P32[:], 0.0), writes=[P32B])
            S.op("pool", lambda e: e.memset(RTm[:], 0.0), writes=[rtb])
            S.op("pool", lambda e: e.memset(H[:], 0.0), writes=[hb])
            mu0 = CN["mu"][0]; w00 = CN["w0"][0]; a00 = CN["a0"][0]; kk0 = CN["k_k"][0]; rk0 = CN["r_k"][0]
            gg0 = CN["gn_g"][0]; gb0 = CN["gn_b"][0]
            ident = self.ident_f
            c4 = lambda ap: ap.rearrange("p (c t) -> p c t", c=4)
            def emit_proj(i2):
                t0_ = i2 * 128
                tsl_ = slice(t0_, t0_ + 128)
                hreads_ = [self.HNB[c][t0_ // TC] for c in range(NCH)]
                for cg in range(4):
                    c0 = cg * 4
                    n = min(4, 14 - c0)
                    p, pb = self.psum.get()

                    def mm(e):
                        for cc in range(n):
                            for k in range(NCH):
                                ins = e.matmul(p[:, cc * 128:(cc + 1) * 128], WR[:, k, (c0 + cc) * 128:(c0 + cc + 1) * 128],
                                               self.HN[:, k, tsl_], start=(k == 0), stop=(k == NCH - 1))
                        return ins
                    S.op("pe", mm, reads=hreads_ + [WRB], writes=[pb])
                    S.op("act", lambda e: e.copy(P32[:, c0:c0 + n, 1:129], p[:, 0:n * 128].rearrange("p (c t) -> p c t", c=n)),
                         reads=[pb], writes=[P32B])

            def lerp_list(i2):
                ops = []
                ops.append(lambda: S.op("pool", lambda e: e.tensor_copy(CAR[:], P32[:, :, 128:129]), reads=[P32B], writes=[DDB]))
                for c in range(14):
                    def one(c=c):
                        S.op("pool", lambda e: e.tensor_tensor(DD[:], P32[:, c, 0:128], P32[:, c, 1:129], ALU.subtract), reads=[P32B, DDB], writes=[DDB])
                        S.op("pool", lambda e: e.tensor_tensor(DD[:], DD[:], self.cols[:, mu0 + c:mu0 + c + 1].to_broadcast([128, 128]), ALU.mult),
                             reads=[DDB, self.constb], writes=[DDB])
                        S.op("pool", lambda e: e.tensor_tensor(P32[:, c, 1:129], P32[:, c, 1:129], DD[:], ALU.add), reads=[DDB, P32B], writes=[P32B])
                    ops.append(one)
                ops.append(lambda: S.op("pool", lambda e: e.tensor_copy(P32[:, :, 0:1], CAR[:]), reads=[P32B, DDB], writes=[P32B]))
                return ops

            pending = []
            for i in range(self._rk_tiles):
                t0 = i * 128
                tcix = t0 // TC
                tsl = slice(t0, t0 + 128)
                hreads = [self.HNB[c][tcix] for c in range(NCH)]
                if i == 0:
                    emit_proj(0)
                    for fn_ in lerp_list(0):
                        fn_()
                for fn_ in pending:
                    fn_()
                pending = []
                S.op("act", lambda e: e.activation(TW[:], PL[0:64, 12, :], AF.Tanh), reads=[PLB], writes=[db])
                S.op("act", lambda e: e.activation(SGg[:], PL[:, 13, :], AF.Sigmoid), reads=[PLB], writes=[db])
                pz, pzb = self.psum.get(); pa, pab = self.psum.get()

                def mmz(e):
                    for fc in range(4):
                        ins = e.matmul(pz[:, fc * 128:(fc + 1) * 128], W2[0:64, fc * 128:(fc + 1) * 128], TW[:], start=True, stop=True)
                    return ins

                def mma(e):
                    for fc in range(4):
                        ins = e.matmul(pa[:, fc * 128:(fc + 1) * 128], A2[64:128, fc * 128:(fc + 1) * 128], PL[64:128, 12, :], start=True, stop=True)
                    return ins
                S.op("pe", mmz, reads=[db, cb], writes=[pzb])
                S.op("pe", mma, reads=[PLB, cb], writes=[pab])
                for fc in range(4):
                    S.op("act", lambda e: e.activation(SIG[:, fc, :], pz[:, fc * 128:(fc + 1) * 128], AF.Sigmoid,
                                                       bias=self.cols[:, w00 + fc:w00 + fc + 1]), reads=[pzb, self.constb], writes=[db, sgb])
                    S.op("act", lambda e: e.activation(A32[:, fc, :], pa[:, fc * 128:(fc + 1) * 128], AF.Sigmoid,
                                                       bias=self.cols[:, a00 + fc:a00 + fc + 1]), reads=[pab, self.constb], writes=[db])
                bc4 = lambda c0_: self.cols[:, c0_:c0_ + 4].unsqueeze(2).to_broadcast([128, 4, 128])
                S.op("dve", lambda e: e.tensor_tensor(KK[:], PL[:, 4:8, :], bc4(kk0), ALU.mult), reads=[PLB, self.constb], writes=[db])
                S.op("dve", lambda e: e.tensor_tensor(SQ[:], KK[:], KK[:], ALU.mult), reads=[db], writes=[db])
                pss, pssb = self.psum.get()
                S.op("pe", lambda e: e.matmul(pss[:], BO[:], SQ[:].rearrange("p c t -> p (c t)"), start=True, stop=True), reads=[db, cb], writes=[pssb])
                S.op("act", lambda e: e.activation(SQ[:], c4(pss[:]), AF.Sqrt), reads=[pssb, db], writes=[db])
                S.op("dve", lambda e: e.tensor_scalar(SQ[:], SQ[:], 1e-12, None, ALU.max), reads=[db], writes=[db])
                S.op("dve", lambda e: e.reciprocal(SQ[:], SQ[:]), reads=[db], writes=[db])
                S.op("dve", lambda e: e.tensor_tensor(KKN[:], KK[:], SQ[:], ALU.mult), reads=[db], writes=[db])
                S.op("dve", lambda e: e.tensor_tensor(NB[:], KKN[:], A32[:], ALU.mult), reads=[db], writes=[db, nbb])
                S.op("pool", lambda e: e.tensor_tensor(KK[:], A32[:], bc4(ka0), ALU.mult), reads=[db, self.constb], writes=[db])
                S.op("pool", lambda e: e.tensor_tensor(KK[:], KK[:], OMK[:].unsqueeze(2).to_broadcast([128, 4, 128]), ALU.add), reads=[db, cb], writes=[db])
                S.op("dve", lambda e: e.tensor_tensor(KM[:], PL[:, 4:8, :], KK[:], ALU.mult), reads=[db, PLB], writes=[db, kmb])
                S.op("dve", lambda e: e.tensor_tensor(SQ[:], PL[:, 0:4, :], KM[:], ALU.mult), reads=[db, PLB, kmb], writes=[db])
                S.op("pool", lambda e: e.tensor_tensor(SQ[:], SQ[:], bc4(rk0), ALU.mult), reads=[db, self.constb], writes=[db])
                pbn, pbnb = self.psum.get()
                S.op("pe", lambda e: e.matmul(pbn[:], BO[:], SQ[:].rearrange("p c t -> p (c t)"), start=True, stop=True), reads=[db, cb], writes=[pbnb])
                S.op("dve", lambda e: e.tensor_tensor(BON[:], c4(pbn[:]), PL[:, 8:12, :], ALU.mult), reads=[pbnb, PLB], writes=[db])
                for fc in range(4):
                    for c2 in range(2):
                        cs = slice(c2 * 64, (c2 + 1) * 64)
                        S.op("dve", lambda e: e.tensor_tensor_scan(CUM[:, fc, cs], ONE64[:], SIG[:, fc, cs], 0.0, ALU.mult, ALU.add),
                             reads=[db, cb, sgb], writes=[db, cub])
                S.op("pool", lambda e: e.tensor_tensor(SQ[:], CUM[:], SIG[:], ALU.subtract), reads=[db, sgb, cub], writes=[db])
                S.op("act", lambda e: e.activation(A32[:], CUM[:], AF.Exp, scale=CDEC), reads=[db, cub], writes=[db])
                S.op("act", lambda e: e.activation(CUM[:], CUM[:], AF.Exp, scale=-CDEC), reads=[db], writes=[db, cub])
                S.op("act", lambda e: e.activation(SQ[:], SQ[:], AF.Exp, scale=-CDEC), reads=[db], writes=[db])
                S.op("dve", lambda e: e.tensor_copy(PCt[:, 0, :], CUM[:, :, 63]), reads=[db, chb, cub], writes=[chb])
                S.op("dve", lambda e: e.tensor_copy(PCt[:, 1, :], CUM[:, :, 127]), reads=[db, chb, cub], writes=[chb])
                S.op("dve", lambda e: e.tensor_tensor(NB[:], NB[:], A32[:], ALU.mult), reads=[db], writes=[db, nbb])
                S.op("dve", lambda e: e.tensor_tensor(KM[:], KM[:], A32[:], ALU.mult), reads=[db], writes=[db, kmb])
                S.op("dve", lambda e: e.tensor_tensor(KKN[:], KKN[:], SQ[:], ALU.mult), reads=[db], writes=[db])
                S.op("dve", lambda e: e.tensor_tensor(KK[:], PL[:, 0:4, :], CUM[:], ALU.mult), reads=[db, PLB, cub], writes=[db])
                S.op("act", lambda e: e.copy(AH[:], NB[:]), reads=[db, gb_, nbb], writes=[gb_])
                S.op("act", lambda e: e.copy(KH[:], KM[:]), reads=[db, gb_, kmb], writes=[gb_])
                S.op("pool", lambda e: e.tensor_copy(BR[:, :, 0, :], KKN[:]), reads=[db, gb_], writes=[gb_])
                S.op("pool", lambda e: e.tensor_copy(BR[:, :, 1, :], KK[:]), reads=[db, gb_], writes=[gb_])
                if "dumpah" in self.debug and i == 0:
                    for nm, tl in (("ah", AH), ("kh", KH), ("br", BR)):
                        o_ = self.dout("dbg_" + nm, [128, tl[:].rearrange("p ... -> p (...)").shape[1] if False else (512 if nm != "br" else 1024)])
                        S.dma(o_, tl[:].rearrange("p c t -> p (c t)") if nm != "br" else tl[:].rearrange("p c a t -> p (c a t)"), reads=[gb_], queue="pool")
                    for nm, tl in (("nb", NB), ("km", KM), ("kkn", KKN), ("en", A32), ("ep", CUM)):
                        o_ = self.dout("dbg_" + nm, [128, 512])
                        S.dma(o_, tl[:].rearrange("p c t -> p (c t)"), reads=[db, nbb, kmb, cub])
                for src, dst_fn in ((NB, None), (KM, None), (KKN, None), (None, None)):
                    pass
                tr_jobs = [(lambda fc: NB[:, fc, :], "AT"), (lambda fc: KM[:, fc, :], "KT"),
                           (lambda fc: KKN[:, fc, :], "BT"), (lambda fc: PL[:, 8 + fc, :], "VT")]
                for srcf, kind in tr_jobs:
                    ptr, ptrb = self.psum.get()

                    def mmt(e):
                        for fc in range(4):
                            ins = e.transpose(ptr[:, fc * 128:(fc + 1) * 128], srcf(fc), ident[:])
                        return ins
                    S.op("pe", mmt, reads=[db, PLB, self.constb, nbb, kmb], writes=[ptrb])
                    if kind == "AT":
                        S.op("act", lambda e: e.copy(AT[:], ptr[:]), reads=[ptrb, tkb], writes=[tkb])
                    elif kind == "KT":
                        S.op("dve", lambda e: e.tensor_copy(KTt[:], ptr[:]), reads=[ptrb, tkb], writes=[tkb])
                    elif kind == "BT":
                        S.op("act", lambda e: e.activation(WB[:, :, 0:64], ptr[:].rearrange("p (h k) -> p h k", h=8), AF.Copy, scale=-1.0),
                             reads=[ptrb, tkb], writes=[tkb])
                    else:
                        S.op("dve", lambda e: e.tensor_copy(VTOK[:], ptr[:]), reads=[ptrb, tkb], writes=[tkb])
                if self._rk_stage <= 0:
                    continue
                for fc in range(4):
                    ka_, ab0, ab1 = self.psum.get_pair_idx()
                    kb_, bb0, bb1 = self.psum.get_pair_idx()
                    PA = self.PS[:, ka_:ka_ + 2, :]; PB = self.PS[:, kb_:kb_ + 2, :]

                    def mmg(e):
                        for h2 in range(2):
                            rs = slice(h2 * 64, (h2 + 1) * 64)
                            brr = BR[rs, fc, :, :].rearrange("p a t -> p (a t)")
                            e.matmul(PA[:, h2, 0:256], AH[rs, fc, :], brr, start=True, stop=True)
                            e.matmul(PA[:, h2, 256:512], KH[rs, fc, :], brr, start=True, stop=True)
                            ins = e.matmul(PB[:, h2, 0:128], BR[rs, fc, 0, :], AH[rs, fc, :], start=True, stop=True)
                        return ins
                    S.op("pe", mmg, reads=[gb_], writes=[ab0, ab1, bb0, bb1])
                    hs = slice(2 * fc, 2 * fc + 2)
                    PAv = PA.rearrange("p h (q b t) -> p h q b t", q=2, b=2)
                    mk = lambda j: MSK[:, j, :].unsqueeze(1).to_broadcast([128, 2, 128])
                    S.op("dve", lambda e: e.tensor_tensor(X0[:, hs, :], PAv[:, :, 0, 0, :], mk(0), ALU.mult), reads=[ab0, ab1, cb, mnb], writes=[mnb])
                    S.op("dve", lambda e: e.tensor_tensor(GRA[:, hs, :], PAv[:, :, 0, 1, :], mk(2), ALU.mult), reads=[ab0, ab1, cb, mnb], writes=[mnb])
                    S.op("dve", lambda e: e.tensor_tensor(LKT[:, hs, :], PAv[:, :, 1, 0, :], mk(0), ALU.mult), reads=[ab0, ab1, cb, mnb], writes=[mnb])
                    S.op("dve", lambda e: e.tensor_tensor(GRK[:, hs, :], PAv[:, :, 1, 1, :], mk(2), ALU.mult), reads=[ab0, ab1, cb, mnb], writes=[mnb])
                    S.op("dve", lambda e: e.tensor_tensor(XT0[:, hs, :], PB[:, :, 0:128], mk(1), ALU.mult), reads=[bb0, bb1, cb, mnb], writes=[mnb])
                if self._rk_stage <= 1:
                    continue
                if i + 1 < self._rk_tiles:
                    emit_proj(i + 1)
                    pending = lerp_list(i + 1)
                hst = []
                for half in range(2):
                    h0 = half * 4
                    st_ = dict(xb=Buf(), xtb=Buf(), tb=Buf(),
                               xbufs=[X0[:, h0:h0 + 4, :], XA1[half][:]], xtbufs=[XT0[:, h0:h0 + 4, :], XTA1[half][:]],
                               tbufs=[TA1[half][:], TT[:, h0:h0 + 4, :]])
                    hst.append(st_)
                    S.op("pool", lambda e: e.tensor_tensor(st_["tbufs"][0], st_["xbufs"][0], ident[:].unsqueeze(1).to_broadcast([128, 4, 128]), ALU.add),
                         reads=[mnb, self.constb, st_["tb"]], writes=[st_["tb"]])
                for lv in range(1, 6):
                    for half in range(2):
                        st_ = hst[half]
                        xb_, xtb_, tb_ = st_["xb"], st_["xtb"], st_["tb"]
                        Xp, XTp, Tp = st_["xbufs"][(lv - 1) % 2], st_["xtbufs"][(lv - 1) % 2], st_["tbufs"][(lv - 1) % 2]
                        Xn, XTn, Tn = st_["xbufs"][lv % 2], st_["xtbufs"][lv % 2], st_["tbufs"][lv % 2]
                        pxt, pxtb = self.psum.get()

                        def mmxt(e):
                            for j in range(4):
                                ins = e.matmul(pxt[:, j * 128:(j + 1) * 128], Xp[:, j, :], XTp[:, j, :], start=True, stop=True)
                            return ins
                        S.op("pe", mmxt, reads=[mnb, xb_, xtb_], writes=[pxtb])
                        if lv < 5:
                            px, pxb = self.psum.get()

                            def mmx(e):
                                for j in range(4):
                                    ins = e.matmul(px[:, j * 128:(j + 1) * 128], XTp[:, j, :], Xp[:, j, :], start=True, stop=True)
                                return ins
                            S.op("pe", mmx, reads=[mnb, xb_, xtb_], writes=[pxb])
                        S.op("act", lambda e: e.copy(XTn, c4(pxt[:])), reads=[pxtb, xtb_, mnb], writes=[xtb_])
                        if lv < 5:
                            S.op("act", lambda e: e.copy(Xn, c4(px[:])), reads=[pxb, xb_, mnb], writes=[xb_])
                        ptt, pttb = self.psum.get()

                        def mmtt(e):
                            for j in range(4):
                                ins = e.matmul(ptt[:, j * 128:(j + 1) * 128], XTn[:, j, :], Tp[:, j, :], start=True, stop=True)
                            return ins
                        S.op("pe", mmtt, reads=[xtb_, tb_], writes=[pttb])
                        S.op("dve", lambda e: e.tensor_tensor(Tn, c4(ptt[:]), Tp, ALU.add), reads=[pttb, tb_, mnb], writes=[tb_] + ([mnb] if lv == 5 else []))
                        for _ in range(2):
                            if pending:
                                pending.pop(0)()
                if self._rk_stage <= 2:
                    continue
                plk, plkb = self.psum.get()

                def mmlk(e):
                    for h in range(8):
                        ins = e.matmul(plk[:, h * 64:(h + 1) * 64], LKT[:, h, :], VTOK[:, h * 64:(h + 1) * 64], start=True, stop=True)
                    return ins
                S.op("pe", mmlk, reads=[mnb, tkb], writes=[plkb])
                S.op("act", lambda e: e.copy(WB[:, :, 64:128], plk[:].rearrange("p (h v) -> p h v", h=8)), reads=[plkb, tkb], writes=[tkb])
                for half in range(2):
                    pbu, pbub = self.psum.get()

                    def mmbu(e):
                        for j in range(4):
                            h = half * 4 + j
                            ins = e.matmul(pbu[:, j * 128:(j + 1) * 128], TT[:, h, :], WB[:, h, :], start=True, stop=True)
                        return ins
                    S.op("pe", mmbu, reads=[mnb, tkb], writes=[pbub])
                    S.op("act", lambda e: e.copy(BU[:, half * 4:half * 4 + 4, :], c4(pbu[:])), reads=[pbub, chb], writes=[chb])
                if self._rk_stage <= 3:
                    continue
                prt, prtb = self.psum.get()
                km_, mb0, mb1 = self.psum.get_pair_idx()
                PM = self.PS[:, km_:km_ + 2, :]

                def mmrt(e):
                    for h in range(8):
                        rs = slice((h % 2) * 64, (h % 2) * 64 + 64)
                        fc = h // 2
                        ins = e.matmul(prt[rs, fc * 128:(fc + 1) * 128], BU[:, h, 0:64], GRA[:, h, :], start=True, stop=True)
                    return ins

                def mmmn(e):
                    for c2 in range(2):
                        cr = slice(c2 * 64, (c2 + 1) * 64)
                        for h in range(8):
                            rs = slice((h % 2) * 64, (h % 2) * 64 + 64)
                            fc = h // 2
                            o = fc * 64
                            e.matmul(PM[rs, c2, o:o + 64], BU[cr, h, 0:64], AT[cr, h * 64:(h + 1) * 64], start=True, stop=True)
                            e.matmul(PM[rs, c2, 256 + o:256 + o + 64], AT[cr, h * 64:(h + 1) * 64], BU[cr, h, 64:128], start=True, stop=False)
                            ins = e.matmul(PM[rs, c2, 256 + o:256 + o + 64], KTt[cr, h * 64:(h + 1) * 64], VTOK[cr, h * 64:(h + 1) * 64], start=False, stop=True)
                    return ins
                S.op("pe", mmrt, reads=[chb, mnb], writes=[prtb])
                S.op("pe", mmmn, reads=[chb, tkb], writes=[mb0, mb1])
                prv = c4(prt[:])
                S.op("dve", lambda e: e.tensor_tensor(RTm[:, :, 0, 0:64], prv[:, :, 0:64], KK[:, :, 0:64], ALU.add), reads=[prtb, db, rtb], writes=[rtb])
                S.op("dve", lambda e: e.tensor_tensor(RTm[:, :, 1, 64:128], prv[:, :, 64:128], KK[:, :, 64:128], ALU.add), reads=[prtb, db, rtb], writes=[rtb])
                M0v = M0Ts.rearrange("p (a c) k -> p a c k", a=2)
                N0v = N0s.rearrange("p (a c) k -> p a c k", a=2)
                S.op("dve", lambda e: e.tensor_tensor(M0v, PM[:, :, 0:256].rearrange("p a (c k) -> p a c k", c=4),
                                                      ID2[:].unsqueeze(1).unsqueeze(1).to_broadcast([128, 2, 4, 64]), ALU.add),
                     reads=[mb0, mb1, cb, chb], writes=[chb, sgb])
                S.op("act", lambda e: e.copy(N0v, PM[:, :, 256:512].rearrange("p a (c k) -> p a c k", c=4)), reads=[mb0, mb1, chb], writes=[chb, cub])
                if self._rk_stage <= 4:
                    continue
                for c2 in range(2):
                    S.op("act", lambda e: e.copy(Hb[:, c2, :, :], H[:]), reads=[hb, hbb], writes=[hbb])
                    phe, pheb = self.psum.get(); pho, phob = self.psum.get()

                    def mmh(e):
                        for par, bank in ((0, phe), (1, pho)):
                            rs = slice(par * 64, par * 64 + 64)
                            for fc in range(4):
                                ins = e.matmul(bank[rs, fc * 64:(fc + 1) * 64], M0Ts[rs, c2 * 4 + fc, :], H[rs, fc, :], start=True, stop=True)
                        return ins
                    S.op("pe", mmh, reads=[chb, hb, sgb], writes=[pheb, phob])
                    S.op("dve", lambda e: e.tensor_tensor(H[0:64], phe[0:64, 0:256].rearrange("p (c v) -> p c v", c=4), N0s[0:64, c2 * 4:c2 * 4 + 4, :], ALU.add),
                         reads=[pheb, chb, cub, hb], writes=[hb])
                    S.op("dve", lambda e: e.tensor_tensor(H[64:128], pho[64:128, 0:256].rearrange("p (c v) -> p c v", c=4), N0s[64:128, c2 * 4:c2 * 4 + 4, :], ALU.add),
                         reads=[phob, chb, cub, hb], writes=[hb])
                    S.op("dve", lambda e: e.tensor_tensor(H[:], H[:], PCt[:, c2, :].unsqueeze(2).to_broadcast([128, 4, 64]), ALU.mult),
                         reads=[chb, hb], writes=[hb])
                if self._rk_stage <= 5:
                    continue
                ky_, yb0, yb1 = self.psum.get_pair_idx()
                PY = self.PS[:, ky_:ky_ + 2, :]

                def mmy(e):
                    for par in range(2):
                        rs = slice(par * 64, par * 64 + 64)
                        for fc in range(4):
                            h = 2 * fc + par
                            o = PY[:, par, fc * 64:(fc + 1) * 64]
                            e.matmul(o, GRA[:, h, :], BU[:, h, 64:128], start=True, stop=False)
                            e.matmul(o, GRK[:, h, :], VTOK[:, h * 64:(h + 1) * 64], start=False, stop=False)
                            e.matmul(o, RTm[rs, fc, 0, :], Hb[rs, 0, fc, :], start=False, stop=False)
                            ins = e.matmul(o, RTm[rs, fc, 1, :], Hb[rs, 1, fc, :], start=False, stop=True)
                    return ins
                S.op("pe", mmy, reads=[mnb, chb, tkb, rtb, hbb], writes=[yb0, yb1])
                S.op("act", lambda e: e.copy(YTOK.rearrange("t c h v -> t h c v"), PY[:, :, 0:256].rearrange("t h (c v) -> t h c v", c=4)),
                     reads=[yb0, yb1, YTOKB], writes=[YTOKB])
                if self._rk_stage <= 6:
                    continue
                YT8 = YTOK.rearrange("t c h v -> t (c h) v")
                S.op("dve", lambda e: e.tensor_reduce(ST8[:], YT8, AX.X, ALU.add), reads=[YTOKB], writes=[yb])
                S.op("dve", lambda e: e.tensor_scalar(ST8[:], ST8[:], 1.0 / 64, None, ALU.mult), reads=[yb], writes=[yb])
                S.op("dve", lambda e: e.tensor_tensor(YC, YT8, ST8[:].unsqueeze(2).to_broadcast([128, 8, 64]), ALU.subtract),
                     reads=[YTOKB, yb], writes=[yb, kmb])
                S.op("pool", lambda e: e.tensor_tensor(YT8, YC, YC, ALU.mult), reads=[yb, YTOKB, kmb], writes=[YTOKB])
                S.op("dve", lambda e: e.tensor_reduce(ST8b[:], YT8, AX.X, ALU.add), reads=[YTOKB], writes=[yb])
                S.op("act", lambda e: e.activation(ST8b[:], ST8b[:], AF.Sqrt, bias=self.gneps_t[:], scale=1.0 / 64), reads=[yb, self.constb], writes=[yb])
                S.op("dve", lambda e: e.reciprocal(ST8b[:], ST8b[:]), reads=[yb], writes=[yb])
                S.op("dve", lambda e: e.tensor_tensor(YC, YC, ST8b[:].unsqueeze(2).to_broadcast([128, 8, 64]), ALU.mult), reads=[yb], writes=[yb, kmb])
                pyt, pytb = self.psum.get(); pg, pgb = self.psum.get()

                def mmt2(e):
                    for fc in range(4):
                        ins = e.transpose(pyt[:, fc * 128:(fc + 1) * 128], YC[:, 2 * fc:2 * fc + 2, :].rearrange("t a v -> t (a v)"), ident[:])
                    for fc in range(4):
                        ins = e.matmul(pg[:, fc * 128:(fc + 1) * 128], G2[:, fc * 128:(fc + 1) * 128], SGg[:], start=True, stop=True)
                    return ins
                S.op("pe", mmt2, reads=[yb, self.constb, db, cb, kmb], writes=[pytb, pgb])
                for fc in range(4):
                    S.op("dve", lambda e: e.tensor_scalar(YF[:, fc, :], pyt[:, fc * 128:(fc + 1) * 128], self.cols[:, gg0 + fc:gg0 + fc + 1],
                                                          self.cols[:, gb0 + fc:gb0 + fc + 1], ALU.mult, ALU.add),
                         reads=[pytb, db, self.constb], writes=[db])
                S.op("pool", lambda e: e.tensor_tensor(YF[:], YF[:], BON[:], ALU.add), reads=[db], writes=[db])
                S.op("dve", lambda e: e.tensor_tensor(self.Y[0][:, :, tsl], YF[:], c4(pg[:]), ALU.mult),
                     reads=[db, pgb], writes=[self.YB[0][tcix]])
            S.full_barrier()
            self.st = old

    def nsa_branch(self, d):
        S = self.S
        NT = S_LEN // 128
        with ExitStack() as st4:
            old, self.st = self.st, st4
            cb = Buf()
            KT = self.sb("KT", [128, 2, S_LEN], BF16); KTB = Buf()
            VT = self.sb("VT", [128, NT, 256], BF16); VTB = Buf()
            KC = self.sb("KC", [128, 127], BF16); VC = self.sb("VC", [128, 128], BF16); kcb = Buf()
            BM = self.sb("BM", [128, 3, 2, 512], BF16)
            BVC = self.sb("BVC", [32, 2, 512], BF16)
            stB = ExitStack(); self.st = stB
            KCMP = self.sb("KCMP", [128, S_LEN], BF16); VCT = self.sb("VCT", [128, S_LEN], BF16)
            WKVx = self.sb("WKVx", [128, 8192], BF16); wkvb = Buf()
            WKV = WKVx[:, 0:NCH * 768].rearrange("p (k n) -> p k n", k=NCH)
            W1v = WKVx[:].rearrange("p (l m) -> p l m", l=32); w1vb = wkvb
            W1k = self.Y[1][:].rearrange("p c t -> p (c t)").rearrange("p (l m) -> p l m", l=32); w1kb = Buf()
            PET2 = self.sb("PET2", [128, 2, 32], BF16); W2D2 = self.sb("W2D2", [128, 2, 2, 128], BF16); cwb = Buf()
            HID = self.sb("HID", [128, 2, 127], BF16)
            ZZ = self.sb("ZZ", [128, 127], F32); Z2 = self.sb("Z2", [128, 127], F32); BC = self.sb("BCc", [128, 1], F32)
            zb = Buf(); hb_ = Buf()
            self.load_w(WKV, d["w_kvn"], wkvb)
            w1k_d = d["cmp_w1"][0].rearrange("(l dd) m -> dd l m", dd=64)
            S.dma(W1k[0:64], w1k_d, writes=[w1kb] + self.YB[1], queue="pool"); S.dma(W1k[64:128], w1k_d, writes=[w1kb] + self.YB[1], queue="pool")
            for kv in range(2):
                S.dma(PET2[0:64, kv, :], d["cmp_peT"][kv], writes=[cwb], queue="pool"); S.dma(PET2[64:128, kv, :], d["cmp_peT"][kv], writes=[cwb], queue="pool")
                w2v = d["cmp_w2"][kv].rearrange("(c p) n -> p c n", p=128)
                S.dma(W2D2[:, kv, :, 0:64], w2v, writes=[cwb], queue="pool"); S.dma(W2D2[:, kv, :, 64:128], w2v, writes=[cwb], queue="pool")
            for tc in range(NTC):
                ts = slice(tc * TC, (tc + 1) * TC)
                hreads = [self.HNB[c][tc] for c in range(NCH)]
                for dst, col in ((KCMP[:, ts], 0), (VCT[:, ts], 128), (KT[:, 0, ts], 256), (KT[:, 1, ts], 512)):
                    p, pb = self.psum.get()

                    def mm(e):
                        for k in range(NCH):
                            ins = e.matmul(p[:], WKV[:, k, col:col + 128], self.HN[:, k, ts], start=(k == 0), stop=(k == NCH - 1))
                        return ins
                    S.op("pe", mm, reads=hreads + [wkvb], writes=[pb])
                    S.op("act", lambda e: e.copy(dst, p[:]), reads=[pb], writes=[KTB])
                for tl in range(4):
                    tile = tc * 4 + tl
                    tq = slice(tile * 128, (tile + 1) * 128)
                    p, pb = self.psum.get()

                    def mm(e):
                        for k in range(NCH):
                            e.matmul(p[:, 0:128], self.HN[:, k, tq], WKV[:, k, 384:512], start=(k == 0), stop=(k == NCH - 1))
                        for k in range(NCH):
                            ins = e.matmul(p[:, 128:256], self.HN[:, k, tq], WKV[:, k, 640:768], start=(k == 0), stop=(k == NCH - 1))
                        return ins
                    S.op("pe", mm, reads=hreads + [wkvb], writes=[pb])
                    S.op("dve", lambda e: e.tensor_copy(VT[:, tile, :], p[:, 0:256]), reads=[pb], writes=[VTB])
            w1v_d = d["cmp_w1"][1].rearrange("(l dd) m -> dd l m", dd=64)
            S.dma(W1v[0:64], w1v_d, writes=[w1vb], queue="pool"); S.dma(W1v[64:128], w1v_d, writes=[w1vb], queue="pool")
            for kv in range(2):
                W1 = W1k if kv == 0 else W1v
                wb = w1kb if kv == 0 else w1vb
                PET = PET2[:, kv, :]
                W2D = W2D2[:, kv, :, :]
                yrd = self.YB[1] if kv == 0 else []
                SRC = KCMP if kv == 0 else VCT
                for g in range(2):
                    gs = slice(g * 64, (g + 1) * 64)
                    for mc in range(2):
                        ph, phb = self.psum.get(); pbias, pbb = self.psum.get()

                        def mm(e):
                            for l in range(32):
                                ins = e.matmul(ph[:, 0:127], W1[gs, l, mc * 128:(mc + 1) * 128], SRC[gs, l:l + 16 * 126 + 1:16],
                                               start=(l == 0), stop=(l == 31))
                            return ins

                        def mmb(e):
                            for l in range(32):
                                ins = e.matmul(pbias[:, 0:1], W1[gs, l, mc * 128:(mc + 1) * 128], PET[gs, l:l + 1], start=(l == 0), stop=(l == 31))
                            return ins
                        S.op("pe", mm, reads=[wb, KTB] + yrd, writes=[phb])
                        S.op("pe", mmb, reads=[wb, cwb] + yrd, writes=[pbb])
                        S.op("act", lambda e: e.copy(BC[:], pbias[:, 0:1]), reads=[pbb, zb], writes=[zb])
                        S.op("dve", lambda e: e.tensor_scalar(ZZ[:], ph[:, 0:127], BC[:, 0:1], None, ALU.add), reads=[phb, zb], writes=[zb])
                        S.op("dve", lambda e: e.tensor_tensor(Z2[:], ZZ[:], ZZ[:], ALU.mult), reads=[zb], writes=[zb])
                        S.op("dve", lambda e: e.tensor_scalar(Z2[:], Z2[:], 0.044715, 1.0, ALU.mult, ALU.add), reads=[zb], writes=[zb])
                        S.op("dve", lambda e: e.tensor_tensor(Z2[:], Z2[:], ZZ[:], ALU.mult), reads=[zb], writes=[zb])
                        S.op("act", lambda e: e.activation(Z2[:], Z2[:], AF.Sigmoid, scale=1.5957691216057308), reads=[zb], writes=[zb])
                        S.op("dve", lambda e: e.tensor_tensor(HID[:, mc, :], ZZ[:], Z2[:], ALU.mult), reads=[zb, hb_], writes=[hb_])
                    po, pob = self.psum.get()
                    if kv == 0:
                        def mm2(e):
                            for mc in range(2):
                                ins = e.matmul(po[:, 0:127], W2D[:, mc, :], HID[:, mc, :], start=(mc == 0), stop=(mc == 1))
                            return ins
                        S.op("pe", mm2, reads=[hb_, cwb], writes=[pob])
                        S.op("act", lambda e: e.copy(KC[gs, :], po[gs, 0:127]), reads=[pob], writes=[kcb])
                    else:
                        def mm2(e):
                            for mc in range(2):
                                ins = e.matmul(po[0:127, 0:64], HID[:, mc, :], W2D[:, mc, 0:64], start=(mc == 0), stop=(mc == 1))
                            return ins
                        S.op("pe", mm2, reads=[hb_, cwb], writes=[pob])
                        S.op("act", lambda e: e.copy(VC[0:127, gs], po[0:127, 0:64]), reads=[pob], writes=[kcb])
            stA = ExitStack(); self.st = stA
            G1 = self.sb("G1", [128, 2, 512], F32); G2_ = self.sb("G2b", [128, 2, 512], F32); MK = self.sb("MK", [128, 128], F32)
            gb = Buf()
            S.dma(G2_[:], d["t31"], writes=[gb])
            for kind in range(3):
                S.dma(G1[:], d["bmg"][kind], reads=[gb], writes=[gb])
                S.dma(MK[:], d["msk"][kind], reads=[gb], writes=[gb])
                S.op("dve", lambda e: e.tensor_tensor(G1[:], G1[:], G2_[:], ALU.subtract), reads=[gb], writes=[gb])
                S.op("dve", lambda e: e.tensor_tensor(BM[:, kind, :, :].rearrange("p g (j q) -> p (g j) q", j=4),
                                                      G1[:].rearrange("p g (j q) -> p (g j) q", j=4),
                                                      MK[:].unsqueeze(1).to_broadcast([128, 8, 128]), ALU.add), reads=[gb], writes=[cb, gb])
            S.dma(G1[0:32, :, :], d["bvcg"], reads=[gb], writes=[gb])
            S.dma(MK[0:32, :], d["mskc"], reads=[gb], writes=[gb])
            S.op("dve", lambda e: e.tensor_tensor(G1[0:32], G1[0:32], G2_[0:32], ALU.subtract), reads=[gb], writes=[gb])
            S.op("dve", lambda e: e.tensor_tensor(BVC[:].rearrange("p g (j q) -> p (g j) q", j=4),
                                                  G1[0:32].rearrange("p g (j q) -> p (g j) q", j=4),
                                                  MK[0:32, :].unsqueeze(1).to_broadcast([32, 8, 128]), ALU.add), reads=[gb], writes=[cb, gb])
            S.full_barrier()
            stA.close()
            self.st = stB
            S.full_barrier()
            stB.close(); self.st = st4
            if "kcvc" in self.debug:
                okc = self.dout("dbg_kc", [128, 127]); ovc = self.dout("dbg_vc", [127, 128])
                S.dma(okc, KC[:], reads=[kcb], queue="pool"); S.dma(ovc, VC[0:127, :], reads=[kcb], queue="pool")
            WQ = self.sb("WQN", [128, NCH, 512], BF16); WGN = self.sb("WGN", [128, NCH, 24], BF16)
            SHCF = self.sb("SHCF", [32, 247], BF16); EF = self.sb("EF", [32, S_LEN], BF16)
            OV = self.sb("OV", [128, 32], BF16); AB = self.sb("ABF", [128, 2, 64], F32)
            SELG = self.sb("SELG", [24, 12, 128], BF16); IDb = self.sb("IDb", [128, 128], BF16)
            self.load_w(WQ[:], d["w_qn"], cb)
            self.load_w(WGN[:], d["w_gn"], cb)
            S.dma(SHCF[:], d["shcf"], writes=[cb], queue="pool"); S.dma(EF[:], d["efull"], writes=[cb], queue="pool")
            S.dma(OV[0:127, :], d["ov"], writes=[cb], queue="pool"); S.dma(AB[:], d["abf"], writes=[cb])
            S.dma(SELG[:], d["selg"], writes=[cb], queue="pool")
            S.op("dve", lambda e: e.tensor_copy(IDb[:], self.ident_f[:]), reads=[self.constb, cb], writes=[cb])
            QS = self.sb("QS", [128, 4, 128], BF16); qsb = Buf()
            GS = self.sb("GS", [24, 128], BF16); gsb = Buf()
            pt_ring = Ring([self.sb("PT%d" % i, [128, 512], BF16) for i in range(4)])
            RR = self.sb("RR", [128, 512], F32); rrb = Buf()
            RRc = self.sb("RRc", [128, 512], F32); rcb = Buf()
            PTc = [self.sb("PTc%d" % g, [128, 512], BF16) for g in range(2)]; ptcb = [Buf(), Buf()]
            YA = self.sb("YA", [128, 512], F32); yab = Buf()
            PN = self.sb("PN", [128, 512], BF16); pnb = Buf()
            IMP = self.sb("IMP", [128, 32], F32); IM2 = self.sb("IM2", [128, 32], F32); MX = self.sb("MX8", [128, 8], F32); ib = Buf()
            NMT = [self.sb("NMT%d" % g, [32, 4, 128], BF16) for g in range(2)]; nmb = [Buf(), Buf()]
            st_ring = Ring(self.banks[0:3], self.bankb[0:3])
            OD = [(self.banks[3], self.bankb[3], self.banks[4], self.bankb[4]),
                  (self.banks[5], self.bankb[5], self.banks[6], self.bankb[6])]
            ms_ring = Ring(self.banks[7:8], self.bankb[7:8])
            LOOK = 2
            for i in range(NT):
                tq = slice(i * 128, (i + 1) * 128)
                tcix = i // 4
                hreads = [self.HNB[c][tcix] for c in range(NCH)]
                p, pb = ms_ring.get()

                def mmq(e):
                    for j in range(4):
                        for k in range(NCH):
                            ins = e.matmul(p[:, j * 128:(j + 1) * 128], WQ[:, k, j * 128:(j + 1) * 128], self.HN[:, k, tq], start=(k == 0), stop=(k == NCH - 1))
                    return ins
                S.op("pe", mmq, reads=hreads + [cb], writes=[pb])
                S.op("act", lambda e: e.activation(QS[:].rearrange("p j q -> p (j q)"), p[:], AF.Copy, scale=0.125), reads=[pb], writes=[qsb])
                p2, pb2 = ms_ring.get()

                def mmg(e):
                    for k in range(NCH):
                        ins = e.matmul(p2[0:24, 0:128], WGN[:, k, :], self.HN[:, k, tq], start=(k == 0), stop=(k == NCH - 1))
                    return ins
                S.op("pe", mmg, reads=hreads + [cb], writes=[pb2])
                S.op("act", lambda e: e.activation(GS[:], p2[0:24, 0:128], AF.Sigmoid), reads=[pb2], writes=[gsb])

                def tiles_of(br):
                    if br == 0:
                        return [None]
                    if br == 1:
                        return list(range(0, i + 1))
                    return list(range(max(0, i - 4), i + 1))
                odset = {0: 0, 1: 0, 2: 1}

                def emit_scores(step):
                    br, g, kt, first, last = step
                    gs = slice(g * 64, (g + 1) * 64)
                    qrhs = QS[gs, :, :].rearrange("p j q -> p (j q)")
                    rows = 127 if br == 0 else 128
                    stp, stb = st_ring.get()
                    mms_list = []
                    if br == 0:
                        mms_list.append((KC[gs, :], qrhs))
                        mms_list.append((SHCF[:, 120 - 8 * i:247 - 8 * i], BVC[:, g, :]))
                    else:
                        mms_list.append((KT[gs, br - 1, kt * 128:(kt + 1) * 128], qrhs))
                        if br == 1 and i >= 8:
                            mms_list.append((EF[:, kt * 128:(kt + 1) * 128], NMT[g][:].rearrange("p j q -> p (j q)")))
                        if kt == i:
                            mms_list.append((IDb[:], BM[:, 0, g, :]))
                        elif kt == i - 1:
                            mms_list.append((IDb[:], BM[:, 1, g, :]))
                        elif br == 2 and kt == i - 4:
                            mms_list.append((IDb[:], BM[:, 2, g, :]))

                    def mms(e):
                        for n_, (l_, r_) in enumerate(mms_list):
                            ins = e.matmul(stp[0:rows, :], l_, r_, start=(n_ == 0), stop=(n_ == len(mms_list) - 1))
                        return ins
                    S.op("pe", mms, reads=[qsb, KTB, kcb, cb, nmb[g]], writes=[stb])
                    if br == 0:
                        PT, ptb = PTc[g], ptcb[g]
                    else:
                        PT, ptb = pt_ring.get()
                    S.op("act", lambda e: e.activation(PT[0:rows, :], stp[0:rows, :], AF.Exp), reads=[stb], writes=[ptb])
                    return (PT, ptb, rows)

                def emit_pv(step, ctx):
                    br, g, kt, first, last = step
                    PT, ptb, rows = ctx
                    gs = slice(g * 64, (g + 1) * 64)
                    O, Ob, DN, Db = OD[odset[br]]
                    if br == 0:
                        vl = VC[0:127, gs]
                    else:
                        c0 = (0 if br == 1 else 128) + g * 64
                        vl = VT[:, kt, c0:c0 + 64]

                    def mmo(e):
                        e.matmul(O[gs, :], vl, PT[0:rows, :], start=first, stop=last)
                        return e.matmul(DN[gs, :], self.ones_b[0:rows, 0:64], PT[0:rows, :], start=first, stop=last)
                    S.op("pe", mmo, reads=[ptb, VTB, kcb, self.constb], writes=[Ob, Db])

                def cmp_extras(g, ctx):
                    PT, ptb, rows = ctx
                    th = []
                    box = {}

                    def t0():
                        box["pd2"], box["pdb2"] = ms_ring.get()
                        S.op("pe", lambda e: e.matmul(box["pd2"][0:127, :], self.ones_b[0:127, 0:127], PT[0:127, :], start=True, stop=True),
                             reads=[ptb, self.constb], writes=[box["pdb2"]])
                        S.op("dve", lambda e: e.tensor_scalar(RRc[0:127, :], box["pd2"][0:127, :], 1e-30, None, ALU.max), reads=[box["pdb2"], rcb], writes=[rcb])
                    th.append(t0)
                    th.append(lambda: S.op("dve", lambda e: e.reciprocal(RRc[0:127, :], RRc[0:127, :]), reads=[rcb], writes=[rcb]))
                    th.append(lambda: S.op("dve", lambda e: e.tensor_tensor(PN[0:127, :], PT[0:127, :], RRc[0:127, :], ALU.mult), reads=[rcb, ptb, pnb], writes=[pnb]))

                    def t3():
                        box["pim"], box["pimb"] = ms_ring.get()

                        def mmi(e):
                            for j in range(4):
                                ins = e.matmul(box["pim"][:, 0:32], PN[0:127, j * 128:(j + 1) * 128], OV[0:127, :], start=(j == 0), stop=(j == 3))
                            return ins
                        S.op("pe", mmi, reads=[pnb, cb], writes=[box["pimb"]])
                        o0 = 32 - 2 * i
                        S.op("dve", lambda e: e.tensor_tensor(IMP[:], box["pim"][:, 0:32], AB[:, 0, o0:o0 + 32], ALU.mult), reads=[box["pimb"], cb, ib], writes=[ib])
                    th.append(t3)
                    o0 = 32 - 2 * i
                    th.append(lambda: S.op("dve", lambda e: e.tensor_tensor(IMP[:], IMP[:], AB[:, 1, o0:o0 + 32], ALU.add), reads=[ib, cb], writes=[ib]))
                    th.append(lambda: S.op("dve", lambda e: e.memset(IMP[:, 0:1], 1e6), reads=[ib], writes=[ib]))
                    th.append(lambda: S.op("dve", lambda e: e.max(MX[:], IMP[:]), reads=[ib], writes=[ib]))
                    th.append(lambda: S.op("dve", lambda e: e.match_replace(IM2[:], MX[:], IMP[:], 0.0), reads=[ib], writes=[ib]))
                    th.append(lambda: S.op("dve", lambda e: e.max(MX[:], IM2[:]), reads=[ib], writes=[ib]))
                    th.append(lambda: S.op("dve", lambda e: e.match_replace(IM2[:], MX[:], IM2[:], 0.0), reads=[ib], writes=[ib]))
                    th.append(lambda: S.op("dve", lambda e: e.tensor_tensor(IM2[:], IMP[:], IM2[:], ALU.subtract), reads=[ib], writes=[ib]))
                    th.append(lambda: S.op("dve", lambda e: e.tensor_scalar(IM2[:], IM2[:], 0.0, None, ALU.is_gt), reads=[ib], writes=[ib]))
                    th.append(lambda: S.op("dve", lambda e: e.tensor_scalar(IM2[:], IM2[:], 30000.0, -30000.0, ALU.mult, ALU.add), reads=[ib], writes=[ib]))

                    def tl():
                        ptr, ptrb = ms_ring.get()
                        S.op("pe", lambda e: e.transpose(ptr[0:32, 0:128], IM2[:], self.ident_f[:]), reads=[ib, self.constb], writes=[ptrb])
                        S.op("dve", lambda e: e.tensor_copy(NMT[g][:], ptr[0:32, 0:128].unsqueeze(1).to_broadcast([32, 4, 128])),
                             reads=[ptrb], writes=[nmb[g]])
                    th.append(tl)
                    return th

                def finalize(br):
                    O, Ob, DN, Db = OD[odset[br]]
                    S.op("dve", lambda e: e.tensor_scalar(RR[:], DN[:], 1e-30, None, ALU.max), reads=[Db, rrb], writes=[rrb])
                    S.op("dve", lambda e: e.reciprocal(RR[:], RR[:]), reads=[rrb], writes=[rrb])
                    pgb_, pgbb = ms_ring.get()

                    def mmgb(e):
                        for j in range(4):
                            ins = e.matmul(pgb_[:, j * 128:(j + 1) * 128], SELG[:, br * 4 + j, :], GS[:], start=True, stop=True)
                        return ins
                    S.op("pe", mmgb, reads=[gsb, cb], writes=[pgbb])
                    S.op("dve", lambda e: e.tensor_tensor(RR[:], RR[:], pgb_[:], ALU.mult), reads=[rrb, pgbb], writes=[rrb])
                    if br == 0:
                        S.op("dve", lambda e: e.tensor_tensor(YA[:], O[:], RR[:], ALU.mult), reads=[Ob, rrb, yab], writes=[yab])
                    else:
                        S.op("dve", lambda e: e.tensor_tensor(RR[:], O[:], RR[:], ALU.mult), reads=[Ob, rrb], writes=[rrb])
                        S.op("dve", lambda e: e.tensor_tensor(YA[:], YA[:], RR[:], ALU.add), reads=[rrb, yab], writes=[yab])

                extras = []
                for g in range(2):
                    st_ = (0, g, None, True, True)
                    ctx = emit_scores(st_)
                    emit_pv(st_, ctx)
                    if i >= 8:
                        extras += cmp_extras(g, ctx)
                finalize(0)

                def run_steps(steps, fill):
                    ctxs = {}
                    for n in range(len(steps) + LOOK):
                        if n < len(steps):
                            ctxs[n] = emit_scores(steps[n])
                        m = n - LOOK
                        if m >= 0:
                            emit_pv(steps[m], ctxs.pop(m))
                            br_, g_, kt_, f_, l_ = steps[m]
                            if g_ == 1 and l_:
                                finalize(br_)
                        for _ in range(3):
                            if fill:
                                fill.pop(0)()

                def mk_steps(br):
                    out = []
                    for g in range(2):
                        tl_ = tiles_of(br)
                        for ti, kt in enumerate(tl_):
                            out.append((br, g, kt, ti == 0, ti == len(tl_) - 1))
                    return out
                run_steps(mk_steps(2), extras)
                while extras:
                    extras.pop(0)()
                run_steps(mk_steps(1), [])
                S.op("act", lambda e: e.copy(self.Y[1][:, :, tq], YA[:].rearrange("p (j q) -> p j q", j=4)), reads=[yab], writes=[self.YB[1][tcix]])
            S.full_barrier()
            self.st = old

    def mem_branch(self, memT, wk_d, wv_d, wqm_d):
        S = self.S
        with ExitStack() as st4:
            old, self.st = self.st, st4
            self._norm_rings_open()
            WQ = self.sb("WQM", [128, NCH, 512], BF16); WQB = Buf()
            KHT = self.sb("KHT", [128, 4, 256], BF16); KHTB = Buf()
            VH = self.sb("VH", [128, 2, 512], BF16); VHB = Buf()
            st5 = ExitStack()
            self.st = st5
            MT = self.sb("MT", [128, NCH, 256], F32); MTB = Buf()
            MN = self.sb("MN", [128, NCH, 256], BF16); MNB = Buf()
            WK = self.sb("WK", [128, NCH, 512], BF16); WKB = Buf()
            WV = self.sb("WV", [128, NCH, 512], BF16); WVB = Buf()
            mr = self.sb("mrstd", [128, 256], F32); mrb = Buf()
            S.dma(MT[:], memT.rearrange("(c p) m -> p c m", p=128), writes=[MTB])
            self.load_w(WK[:], wk_d, WKB)
            self.load_w(WV[:], wv_d, WVB)
            self.load_w(WQ[:], wqm_d, WQB)
            g0, _ = COLS["mem_norm"]
            pt, pb = self.psum.get()
            for c in range(NCH):
                sq, sqb = self.sq_ring.get()
                S.op("act", lambda e: e.activation(sq[:, 0:256], MT[:, c, :], AF.Square), reads=[MTB], writes=[sqb])
                S.op("pe", lambda e: e.matmul(pt[:, 0:256], self.ones_f[:], sq[:, 0:256], start=(c == 0), stop=(c == NCH - 1)),
                     reads=[sqb, self.constb], writes=[pb])
            S.op("act", lambda e: e.activation(mr[:], pt[:, 0:256], AF.Sqrt, bias=self.eps_t[:], scale=1.0 / D),
                 reads=[pb, self.constb], writes=[mrb])
            S.op("dve", lambda e: e.reciprocal(mr[:], mr[:]), reads=[mrb], writes=[mrb])
            for c in range(NCH):
                S.op("dve", lambda e: e.scalar_tensor_tensor(MN[:, c, :], MT[:, c, :], self.cols[:, g0 + c:g0 + c + 1], mr[:],
                                                             ALU.mult, ALU.mult),
                     reads=[MTB, mrb, self.constb], writes=[MNB])
            for h in range(4):
                p, pb = self.psum.get()

                def mm(e):
                    for k in range(NCH):
                        ins = e.matmul(p[:, 0:256], WK[:, k, h * 128:(h + 1) * 128], MN[:, k, :], start=(k == 0), stop=(k == NCH - 1))
                    return ins
                S.op("pe", mm, reads=[WKB, MNB], writes=[pb])
                S.op("act", lambda e: e.copy(KHT[:, h, :], p[:, 0:256]), reads=[pb], writes=[KHTB])
            for mt in range(2):
                p, pb = self.psum.get()

                def mm(e):
                    for k in range(NCH):
                        ins = e.matmul(p[:], MN[:, k, mt * 128:(mt + 1) * 128], WV[:, k, :], start=(k == 0), stop=(k == NCH - 1))
                    return ins
                S.op("pe", mm, reads=[WVB, MNB], writes=[pb])
                S.op("act", lambda e: e.copy(VH[:, mt, :], p[:]), reads=[pb], writes=[VHB])
            S.full_barrier()
            st5.close()
            self.st = st4
            qm_ring = Ring([self.sb("qm%d" % i, [128, TC], BF16) for i in range(2)])
            pt_ring = Ring([self.sb("pt%d" % i, [128, 2, TC], BF16) for i in range(2)])
            rd_ring = self.sq_ring
            scale = 128.0 ** -0.5
            def stage1(tc, h):
                ts = slice(tc * TC, (tc + 1) * TC)
                hreads = [self.HNB[c][tc] for c in range(NCH)]
                p, pb = self.psum.get()

                def mm(e):
                    for k in range(NCH):
                        ins = e.matmul(p[:], WQ[:, k, h * 128:(h + 1) * 128], self.HN[:, k, ts], start=(k == 0), stop=(k == NCH - 1))
                    return ins
                S.op("pe", mm, reads=hreads + [WQB], writes=[pb])
                qm, qmb = qm_ring.get()
                S.op("dve", lambda e: e.tensor_copy(qm[:], p[:]), reads=[pb], writes=[qmb])
                ptile, ptb = pt_ring.get()
                for mt in range(2):
                    ps_, psb = self.psum.get()
                    S.op("pe", lambda e: e.matmul(ps_[:], KHT[:, h, mt * 128:(mt + 1) * 128], qm[:], start=True, stop=True),
                         reads=[KHTB, qmb], writes=[psb])
                    S.op("act", lambda e: e.activation(ptile[:, mt, :], ps_[:], AF.Exp, scale=scale), reads=[psb], writes=[ptb])
                return (ptile, ptb)

            def stage2(tc, h, ctx):
                ptile, ptb = ctx
                ts = slice(tc * TC, (tc + 1) * TC)
                po, pob = self.psum.get()
                pd, pdb = self.psum.get()

                def mm_o(e):
                    for mt in range(2):
                        ins = e.matmul(po[:], VH[:, mt, h * 128:(h + 1) * 128], ptile[:, mt, :], start=(mt == 0), stop=(mt == 1))
                    return ins

                def mm_d(e):
                    for mt in range(2):
                        ins = e.matmul(pd[:], self.ones_b[:], ptile[:, mt, :], start=(mt == 0), stop=(mt == 1))
                    return ins
                S.op("pe", mm_o, reads=[VHB, ptb], writes=[pob])
                S.op("pe", mm_d, reads=[ptb, self.constb], writes=[pdb])
                rd, rdb = rd_ring.get()
                S.op("dve", lambda e: e.reciprocal(rd[:], pd[:]), reads=[pdb], writes=[rdb])
                S.op("dve", lambda e: e.tensor_tensor(self.Y[2][:, h, ts], po[:], rd[:], ALU.mult),
                     reads=[pob, rdb], writes=[self.YB[2][tc]])
            items = [(tc, h) for tc in range(NTC) for h in range(4)]
            prev = None
            for it in items:
                ctx = stage1(*it)
                if prev is not None:
                    stage2(*prev)
                prev = (it[0], it[1], ctx)
            stage2(*prev)
            S.full_barrier()
            self.st = old
        self._nst.close()

    def fold(self, br, wgb_d, wbr_d, first):
        S = self.S
        with ExitStack() as st4:
            old, self.st = self.st, st4
            WGB = [self.sb("WGBr%d" % i, [128, NCH, 128], BF16) for i in range(2)]; WGBB = [Buf(), Buf()]
            WBR = [self.sb("WBR%d" % i, [128, 4, 128], BF16) for i in range(2)]; WBRB = [Buf(), Buf()]
            gt_ring = Ring([self.sb("gt%d" % i, [128, TC], F32) for i in range(2)])
            t_ring = Ring([self.sb("mt%d" % i, [128, TC], F32) for i in range(2)])

            def load(dc):
                sl = dc % 2
                c0 = br * D + dc * 128
                S.dma(WGB[sl][:], wgb_d[:, c0:c0 + 128].rearrange("(k p) n -> p k n", p=128), writes=[WGBB[sl]], queue="pool")
                S.dma(WBR[sl][:], wbr_d[:, dc * 128:(dc + 1) * 128].rearrange("(k p) n -> p k n", p=128), writes=[WBRB[sl]], queue="pool")
            load(0)
            for dc in range(NCH):
                if dc + 1 < NCH:
                    load(dc + 1)
                sl = dc % 2
                for tc in range(NTC):
                    ts = slice(tc * TC, (tc + 1) * TC)
                    hreads = [self.HNB[c][tc] for c in range(NCH)]
                    pg, pgb = self.psum.get()
                    py, pyb = self.psum.get()

                    def mm_g(e):
                        for k in range(NCH):
                            ins = e.matmul(pg[:], WGB[sl][:, k, :], self.HN[:, k, ts], start=(k == 0), stop=(k == NCH - 1))
                        return ins

                    def mm_y(e):
                        for k in range(4):
                            ins = e.matmul(py[:], WBR[sl][:, k, :], self.Y[br][:, k, ts], start=(k == 0), stop=(k == 3))
                        return ins
                    S.op("pe", mm_g, reads=hreads + [WGBB[sl]], writes=[pgb])
                    S.op("pe", mm_y, reads=[self.YB[br][tc], WBRB[sl]], writes=[pyb])
                    gt, gtb = gt_ring.get()
                    S.op("act", lambda e: e.activation(gt[:], pg[:], AF.Sigmoid), reads=[pgb], writes=[gtb])
                    if first:
                        S.op("dve", lambda e: e.tensor_tensor(self.M[:, dc, ts], gt[:], py[:], ALU.mult),
                             reads=[gtb, pyb], writes=[self.MB[dc][tc]])
                    else:
                        t, tb = t_ring.get()
                        S.op("dve", lambda e: e.tensor_tensor(t[:], gt[:], py[:], ALU.mult), reads=[gtb, pyb], writes=[tb])
                        S.op("pool", lambda e: e.tensor_tensor(self.M[:, dc, ts], self.M[:, dc, ts], t[:], ALU.add),
                             reads=[tb, self.MB[dc][tc]], writes=[self.MB[dc][tc]])
            S.full_barrier()
            self.st = old

    def outproj(self, wout_d):
        S = self.S
        with ExitStack() as st4:
            old, self.st = self.st, st4
            WO = self.sb("WO", [128, NCH, D], BF16); WOB = Buf()
            self.load_w(WO[:], wout_d, WOB)
            for tc in range(NTC):
                ts = slice(tc * TC, (tc + 1) * TC)
                for d2 in range(NCH):
                    po, pob = self.psum.get()

                    def mm(e):
                        for k in range(NCH):
                            ins = e.matmul(po[:], WO[:, k, d2 * 128:(d2 + 1) * 128], self.M[:, k, ts], start=(k == 0), stop=(k == NCH - 1))
                        return ins
                    S.op("pe", mm, reads=[self.MB[k][tc] for k in range(NCH)] + [WOB], writes=[pob])
                    S.op("dve", lambda e: e.tensor_tensor(self.X[:, d2, ts], po[:], self.X[:, d2, ts], ALU.add),
                         reads=[pob, self.XB[d2][tc]], writes=[self.XB[d2][tc]])
            S.full_barrier()
            self.st = old

    def final_norm_out(self, outT):
        S = self.S
        g0, _ = COLS["final_norm"]
        self._norm_rings_open()
        for tc in range(NTC):
            ts = slice(tc * TC, (tc + 1) * TC)
            pt, pb = self.psum.get()
            for c in range(NCH):
                sq, sqb = self.sq_ring.get()
                S.op("act", lambda e: e.activation(sq[:], self.X[:, c, ts], AF.Square),
                     reads=[self.XB[c][tc]], writes=[sqb])
                S.op("pe", lambda e: e.matmul(pt[:], self.ones_f[:], sq[:], start=(c == 0), stop=(c == NCH - 1)),
                     reads=[sqb, self.constb], writes=[pb])
            rs, rsb = self.rstd_ring.get()
            S.op("act", lambda e: e.activation(rs[:], pt[:], AF.Sqrt, bias=self.eps_t[:], scale=1.0 / D),
                 reads=[pb, self.constb], writes=[rsb])
            S.op("dve", lambda e: e.reciprocal(rs[:], rs[:]), reads=[rsb], writes=[rsb])
            for c in range(NCH):
                S.op("dve", lambda e: e.scalar_tensor_tensor(
                    self.X[:, c, ts], self.X[:, c, ts], self.cols[:, g0 + c:g0 + c + 1], rs[:],
                    ALU.mult, ALU.mult),
                    reads=[self.XB[c][tc], rsb, self.constb], writes=[self.XB[c][tc]])
                S.dma(outT[c * 128:(c + 1) * 128, ts], self.X[:, c, ts], reads=[self.XB[c][tc]])
        self._norm_rings_close()

    def dump_x(self, name):
        o = self.dout(name, [D, S_LEN])
        for c in range(NCH):
            for tc in range(NTC):
                ts = slice(tc * TC, (tc + 1) * TC)
                self.S.dma(o[c * 128:(c + 1) * 128, ts], self.X[:, c, ts], reads=[self.XB[c][tc]])

    def build(self, stop_after=None):
        nc = self.nc
        dbg = self.debug
        xT = self.din("xT", [D, S_LEN])
        cols_d = self.din("cols", [128, NCOLS])
        f1g = self.din("ffn1_w_gate", [D, DFF]); f1u = self.din("ffn1_w_up", [D, DFF]); f1d = self.din("ffn1_w_down", [DFF, D])
        f2g = self.din("ffn2_w_gate", [D, DFF]); f2u = self.din("ffn2_w_up", [D, DFF]); f2d = self.din("ffn2_w_down", [DFF, D])
        memT = self.din("memT", [D, 256])
        mem_wk = self.din("mem_w_k", [D, 512]); mem_wv = self.din("mem_w_v", [D, 512])
        w_qm = self.din("w_qm", [D, 512])
        w_gb = self.din("w_gb", [D, 3 * D])
        w_br = [self.din(n, [512, D]) for n in ("w_br_rwkv", "w_br_nsa_p", "w_br_mem")]
        w_out = self.din("w_out", [D, D])
        w_rwkv = self.din("w_rwkv", [D, 1792])
        w2_d = self.din("rwkv_w2", [64, 512]); a2_d = self.din("rwkv_a2", [64, 512]); g2_d = self.din("rwkv_g2", [128, 512])
        gng_d = self.din("gng_rep", [128, 512]); gnb_d = self.din("gnb_rep", [128, 512])
        ident_d = self.din("ident", [128, 128])
        rmk_d = self.din("rwkv_masks", [128, 3, 128])
        nd = {}
        nd["w_qn"] = self.din("w_qn", [D, 512]); nd["w_gn"] = self.din("w_gn", [D, 24]); nd["w_kvn"] = self.din("w_kvn", [D, 768])
        nd["shcf"] = self.din("shcf", [32, 247]); nd["efull"] = self.din("efull", [32, S_LEN]); nd["ov"] = self.din("ov", [127, 32])
        nd["abf"] = self.din("abf", [128, 2, 64]); nd["selg"] = self.din("selg", [24, 12, 128])
        nd["t31"] = self.din("t31", [128, 2, 512])
        nd["bmg"] = [self.din("bmg%d" % k, [128, 2, 512]) for k in range(3)]
        nd["msk"] = [self.din("msk%d" % k, [128, 128]) for k in range(3)]
        nd["bvcg"] = self.din("bvcg", [32, 2, 512]); nd["mskc"] = self.din("mskc", [32, 128])
        nd["cmp_w1"] = [self.din("cmp_k_w1", [2048, 256]), self.din("cmp_v_w1", [2048, 256])]
        nd["cmp_w2"] = [self.din("cmp_k_w2", [256, 64]), self.din("cmp_v_w2", [256, 64])]
        nd["cmp_peT"] = [self.din("cmp_pe_kT", [64, 32]), self.din("cmp_pe_vT", [64, 32])]
        outT = self.dout("outT", [D, S_LEN])
        with ExitStack() as st:
            self.st = st
            S = self.S = Sched(nc, st)
            self.X = self.sb("X", [128, NCH, S_LEN], F32)
            self.XB = [[Buf() for _ in range(NTC)] for _ in range(NCH)]
            self.cols = self.sb("cols", [128, NCOLS], F32)
            self.ones_f = self.sb("ones_f", [128, 128], F32)
            self.ones_b = self.sb("ones_b", [128, 128], BF16)
            self.eps_t = self.sb("eps_t", [128, 1], F32)
            self.gneps_t = self.sb("gneps_t", [128, 1], F32)
            self.ident_f = self.sb("ident_f", [128, 128], F32)
            self.constb = Buf("const")
            self.PS = self.ps("PSALL", [128, 8, 512])
            self.banks = [self.PS[:, i, :] for i in range(8)]
            self.bankb = [Buf() for _ in range(8)]
            self.psum = Ring(self.banks, self.bankb)
            S.dma(self.cols[:], cols_d, writes=[self.constb])
            S.op("dve", lambda e: e.memset(self.ones_f[:], 1.0), reads=[self.constb], writes=[self.constb])
            S.op("dve", lambda e: e.memset(self.ones_b[:], 1.0), reads=[self.constb], writes=[self.constb])
            S.op("dve", lambda e: e.memset(self.eps_t[:], EPS), reads=[self.constb], writes=[self.constb])
            S.op("dve", lambda e: e.memset(self.gneps_t[:], 64e-5), reads=[self.constb], writes=[self.constb])
            S.dma(self.ident_f[:], ident_d, reads=[self.constb], writes=[self.constb])
            for tc in range(NTC):
                for c in range(NCH):
                    ts = slice(tc * TC, (tc + 1) * TC)
                    S.dma(self.X[:, c, ts], xT[c * 128:(c + 1) * 128, ts], writes=[self.XB[c][tc]])

            def ffn_phase(wg, wu, wd, gname):
                with ExitStack() as st2:
                    self.st = st2
                    self.HN = self.sb("HN", [128, NCH, S_LEN], BF16)
                    self.HNB = [[Buf() for _ in range(NTC)] for _ in range(NCH)]
                    self.WG = [self.sb("WG%d" % i, [128, NCH, 512], BF16) for i in range(2)]
                    self.WU = [self.sb("WU%d" % i, [128, NCH, 512], BF16) for i in range(2)]
                    self.WD = [self.sb("WD%d" % i, [128, 4, D], BF16) for i in range(2)]
                    self.WGB = [Buf() for _ in range(2)]; self.WUB = [Buf() for _ in range(2)]; self.WDB = [Buf() for _ in range(2)]
                    self.a_ring = Ring([self.sb("a%d" % i, [128, 4, TC], BF16) for i in range(2)])
                    self.sg_ring = Ring([self.sb("sg%d" % i, [128, TC], F32) for i in range(2)])
                    self.ffn(wg, wu, wd, gname)
                    S.full_barrier()
                    self.st = st

            if "noffn1" not in dbg:
                ffn_phase(f1g, f1u, f1d, "ffn1_norm")
            if "x1" in dbg:
                self.dump_x("dbg_x1")
            if stop_after != "ffn1":
                with ExitStack() as st3:
                    self.st = st3
                    self.HN = self.sb("HN", [128, NCH, S_LEN], BF16)
                    self.HNB = [[Buf() for _ in range(NTC)] for _ in range(NCH)]
                    Yt = self.sb("Yt", [128, 4, S_LEN], BF16)
                    YBt = [Buf() for _ in range(NTC)]
                    self.Y = [Yt, Yt, Yt]
                    self.YB = [YBt, YBt, YBt]
                    if "norwkv" in dbg or "rwkvseq" in dbg:
                        self.rmsnorm_to_hn("mix_norm")
                    if "norwkv" not in dbg:
                        if "rwkvseq" in dbg:
                            self.rwkv_branch_seq(w_rwkv, w2_d, a2_d, g2_d, gng_d, gnb_d)
                        else:
                            self.rwkv_branch(w_rwkv, w2_d, a2_d, g2_d, gng_d, gnb_d, rmk_d)
                    else:
                        S.op("pool", lambda e: e.memset(Yt[:], 0.0), writes=YBt)
                    if "y_rwkv" in dbg:
                        self.dump_feat("dbg_y_rwkv", Yt, 4, YBt)
                    self.M = self.sb("M", [128, NCH, S_LEN], BF16)
                    self.MB = [[Buf() for _ in range(NTC)] for _ in range(NCH)]
                    do_merge = stop_after != "mix"
                    if do_merge:
                        self.fold(0, w_gb, w_br[0], True)
                    if "nomem" not in dbg:
                        self.mem_branch(memT, mem_wk, mem_wv, w_qm)
                    else:
                        S.op("pool", lambda e: e.memset(Yt[:], 0.0), writes=YBt)
                    if "y_mem" in dbg:
                        self.dump_feat("dbg_y_mem", Yt, 4, YBt)
                    if do_merge:
                        self.fold(2, w_gb, w_br[2], False)
                    if "nonsa" not in dbg:
                        self.nsa_branch(nd)
                    else:
                        S.op("pool", lambda e: e.memset(Yt[:], 0.0), writes=YBt)
                    if "y_nsa" in dbg:
                        self.dump_feat("dbg_y_nsa_p", Yt, 4, YBt)
                    if do_merge:
                        self.fold(1, w_gb, w_br[1], False)
                        self.outproj(w_out)
                    S.full_barrier()
                    self.st = st
                if "x2" in dbg:
                    self.dump_x("dbg_x2")
                if stop_after not in ("mix", "merge"):
                    ffn_phase(f2g, f2u, f2d, "ffn2_norm")
            self.final_norm_out(outT)
            S.wait_all_dma("sp")
            S.wait_all_dma("pool")
        return nc


NSA_PERM = np.concatenate([np.concatenate([np.arange(64 * j, 64 * j + 64), np.arange(64 * (4 + j), 64 * (4 + j) + 64)])
                           for j in range(4)])


def _t5_bucket_np(dist):
    n = np.maximum(dist, 0)
    nf = np.maximum(n, 1).astype(np.float32)
    large = 16 + (np.log(nf / np.float32(16)) / np.float32(math.log(128 / 16)) * np.float32(16)).astype(np.int32)
    large = np.minimum(large, 31)
    return np.where(n < 16, n, large)


def _nsa_consts(rel_bias):
    rb = np.asarray(rel_bias, np.float32)
    c = np.arange(128)[:, None]; p = np.arange(128)[None, :]
    out = {}
    hd = np.arange(8).reshape(2, 4)
    dists = [p - c, 128 + p - c, 512 + p - c]
    valid = [p >= c, np.ones((128, 128), bool), c > p]
    for k in range(3):
        bk = _t5_bucket_np(dists[k])
        g = rb[bk[:, None, None, :], hd[None, :, :, None]]
        out["bmg%d" % k] = np.ascontiguousarray(g.reshape(128, 2, 512))
        out["msk%d" % k] = np.where(valid[k], 0.0, -30000.0).astype(np.float32)
    out["t31"] = np.ascontiguousarray(np.broadcast_to(rb[31][hd][None, :, :, None], (128, 2, 4, 128)).reshape(128, 2, 512))
    m = np.arange(32)[:, None]
    dc = p - 16 * (m - 8) - 31
    bk = _t5_bucket_np(dc)
    g = rb[bk[:, None, None, :], hd[None, :, :, None]]
    out["bvcg"] = np.ascontiguousarray(g.reshape(32, 2, 512))
    mk = np.where((dc >= 0) & (m < 16), 0.0, -30000.0).astype(np.float32)
    mk[17:] = 0.0
    out["mskc"] = mk
    shcf = np.zeros((32, 247), np.float32)
    for x in range(247):
        r = x - 112
        if 0 <= r < 16:
            shcf[r, x] = 1.0
        elif r >= 16:
            shcf[16, x] = 1.0
    out["shcf"] = shcf
    ef = np.zeros((32, S_LEN), np.float32)
    ef[np.arange(S_LEN) // 64, np.arange(S_LEN)] = 1.0
    out["efull"] = ef
    ic = np.arange(127)[:, None]; jb = np.arange(32)[None, :]
    out["ov"] = ((ic * 16 <= jb * 64 + 63) & (ic * 16 + 31 >= jb * 64)).astype(np.float32)
    ab = np.zeros((128, 2, 64), np.float32)
    for pp in range(128):
        curr = 1 if pp >= 64 else 0
        for mm in range(64):
            jr = mm - 32
            if jr <= curr - 2:
                ab[pp, 0, mm] = 1.0
            if jr in (curr, curr - 1):
                ab[pp, 1, mm] = 1e6
    out["abf"] = ab
    selg = np.zeros((24, 12, 128), np.float32)
    for br in range(3):
        for j in range(4):
            for mm in range(128):
                selg[br * 8 + (mm // 64) * 4 + j, br * 4 + j, mm] = 1.0
    out["selg"] = selg
    return out


def prep_inputs(inputs, b):
    m = {}
    m["xT"] = np.ascontiguousarray(inputs["x"][b].T)
    cols = np.zeros((128, NCOLS), np.float32)
    for n in ("ffn1_norm", "mix_norm", "ffn2_norm", "final_norm", "mem_norm"):
        c0, k = COLS[n]
        cols[:, c0:c0 + k] = _colpack(np.asarray(inputs[n]).reshape(-1))
    for n, src in (("mu", "rwkv_mu"), ("w0", "rwkv_w0"), ("a0", "rwkv_a0"), ("k_k", "rwkv_k_k"), ("k_a", "rwkv_k_a"), ("r_k", "rwkv_r_k"),
                   ("gn_g", "rwkv_gn_gain"), ("gn_b", "rwkv_gn_bias")):
        c0, k = COLS[n]
        cols[:, c0:c0 + k] = _colpack(np.asarray(inputs[src]).reshape(-1))
    m["cols"] = cols
    m["w_rwkv"] = np.ascontiguousarray(np.asarray(inputs["w_in"])[0][:, 0:1792])
    m["rwkv_w2"] = np.ascontiguousarray(np.asarray(inputs["rwkv_w2"])[0])
    m["rwkv_a2"] = np.ascontiguousarray(np.asarray(inputs["rwkv_a2"])[0])
    m["rwkv_g2"] = np.ascontiguousarray(np.asarray(inputs["rwkv_g2"])[0])
    m["gng_rep"] = np.ascontiguousarray(np.broadcast_to(np.asarray(inputs["rwkv_gn_gain"]).reshape(1, 512), (128, 512)))
    m["gnb_rep"] = np.ascontiguousarray(np.broadcast_to(np.asarray(inputs["rwkv_gn_bias"]).reshape(1, 512), (128, 512)))
    m["ident"] = np.eye(128, dtype=np.float32)
    si = np.arange(128)[:, None]; ti = np.arange(128)[None, :]
    same = (si // 64) == (ti // 64)
    mk = np.zeros((128, 3, 128), np.float32)
    mk[:, 0, :] = np.where(same & (si < ti), -1.0, 0.0)
    mk[:, 1, :] = np.where(same & (ti < si), -1.0, 0.0)
    mk[:, 2, :] = np.where(same & (si <= ti), 1.0, 0.0)
    m["rwkv_masks"] = mk
    w_in_ = np.asarray(inputs["w_in"])[0]
    m["w_qn"] = np.ascontiguousarray(w_in_[:, 1792:2304][:, NSA_PERM])
    m["w_kvn"] = np.ascontiguousarray(w_in_[:, 2304:3072])
    m["w_gn"] = np.ascontiguousarray(w_in_[:, 3072:3096])
    m.update(_nsa_consts(inputs["rel_bias"]))
    for n in ("cmp_k_w1", "cmp_v_w1", "cmp_k_w2", "cmp_v_w2"):
        m[n] = np.ascontiguousarray(np.asarray(inputs[n])[0])
    m["cmp_pe_kT"] = np.ascontiguousarray(np.asarray(inputs["cmp_pe_k"])[0].T)
    m["cmp_pe_vT"] = np.ascontiguousarray(np.asarray(inputs["cmp_pe_v"])[0].T)
    for n in ("ffn1_w_gate", "ffn1_w_up", "ffn1_w_down", "ffn2_w_gate", "ffn2_w_up", "ffn2_w_down",
              "mem_w_k", "mem_w_v", "w_br_rwkv", "w_br_mem", "w_out"):
        m[n] = np.ascontiguousarray(np.asarray(inputs[n])[0])
    m["memT"] = np.ascontiguousarray(inputs["mem"][b].T)
    w_in = np.asarray(inputs["w_in"])[0]
    m["w_qm"] = np.ascontiguousarray(w_in[:, 3096:3608])
    m["w_gb"] = np.ascontiguousarray(w_in[:, 3608:6680])
    m["w_br_nsa_p"] = np.ascontiguousarray(np.asarray(inputs["w_br_nsa"])[0][NSA_PERM, :])
    return m


_CACHE = {}


def kernel(**inputs):
    inputs = {k: np.asarray(v) for k, v in inputs.items()}
    if "nc" not in _CACHE:
        _CACHE["nc"] = Builder().build()
    nc = _CACHE["nc"]
    n = 8
    in_maps = [prep_inputs(inputs, b) for b in range(n)]
    res = run_bass_kernel_spmd(nc, in_maps, core_ids=list(range(n)))
    out = np.stack([np.ascontiguousarray(r["outT"].T) for r in res.results], axis=0)
    return out.astype(np.float32)
```
